# Optimizing a Trainium2 kernel written in Bass

```python
import math
import jax
import jax.numpy as jnp
from jax import lax
import numpy as np


D_MODEL = 1024
BATCH = 16
SEQ = 2048
DEPTH = 2

GRID_W = 64
CTX_LEN = 256
EPS = 1e-6
LOG_FLOOR = 1e-30
N_BRANCH = 4
BRANCH_W = D_MODEL // 4
HEAD_DIM = 64
CHUNK = 64
DN_HEADS = BRANCH_W // HEAD_DIM
DN_CONV = 5
S5_WIDTH = BRANCH_W
S5_GROUP = 16
S5_GROUPS = S5_WIDTH // S5_GROUP
S5_STATE = 64
HG_HEADS = BRANCH_W // HEAD_DIM
ATT_HEADS = BRANCH_W // HEAD_DIM
ATT_KV_HEADS = ATT_HEADS // 2
ATT_GROUP = ATT_HEADS // ATT_KV_HEADS
ATT_BLOCK = 128
ROPE_THETA = 10000.0
D_FF = ((8 * D_MODEL // 3 + 127) // 128) * 128
MACARON_W = 0.5
N_MOD = 9
IN_SIZES = (
    3 * BRANCH_W,
    BRANCH_W,
    2 * DN_HEADS,
    2 * DN_HEADS,
    S5_WIDTH,
    BRANCH_W,
    2 * BRANCH_W,
    BRANCH_W,
    BRANCH_W,
    ATT_HEADS * HEAD_DIM,
    2 * ATT_KV_HEADS * HEAD_DIM,
    N_BRANCH * D_MODEL,
)
IN_COLS = sum(IN_SIZES)

kernel_name = 'hybrid_parallel_flow_block'


def rms_norm(x, g):
    xf = x.astype(jnp.float32)
    y = xf * lax.rsqrt(jnp.mean(xf * xf, axis=-1, keepdims=True) + EPS)
    return (y * g.astype(jnp.float32)).astype(x.dtype)


def modulate(h, shift, scale):
    return h * (1.0 + scale) + shift


def ada_mods(cond, w, b):
    m = jax.nn.silu(cond) @ w + b
    m = m.reshape(m.shape[:-1] + (N_MOD, D_MODEL))
    return [m[..., i, :][..., None, :] for i in range(N_MOD)]


def split_cols(p):
    return jnp.split(p, np.cumsum(IN_SIZES)[:-1].tolist(), axis=-1)


def seq_join(part_c, part_l, reverse):
    if reverse:
        part_c, part_l = jnp.flip(part_c, axis=1), jnp.flip(part_l, axis=1)
    return jnp.concatenate([part_c, part_l], axis=1)


def seq_split(y, n_ctx, reverse):
    y_c, y_l = y[:, :n_ctx], y[:, n_ctx:]
    if reverse:
        y_c, y_l = jnp.flip(y_c, axis=1), jnp.flip(y_l, axis=1)
    return y_c, y_l


def dwconv_centred(x, w):
    k = w.shape[0]
    return lax.conv_general_dilated(x, w[:, None, :].astype(x.dtype), window_strides=(1,), padding=[(k // 2, k // 2)], dimension_numbers=('NWC', 'WIO', 'NWC'), feature_group_count=x.shape[-1])


def l2norm(t):
    return t * lax.rsqrt(jnp.sum(t * t, axis=-1, keepdims=True) + EPS)


def to_chunks(a, n):
    return a.reshape((a.shape[0], n, CHUNK) + a.shape[2:])


def axial_rope_tables(n_tokens):
    rows = n_tokens // GRID_W
    r, col = jnp.meshgrid(jnp.arange(rows), jnp.arange(GRID_W), indexing='ij')
    axis_dim = HEAD_DIM // 2
    inv = ROPE_THETA ** (-jnp.arange(0, axis_dim, 2, dtype=jnp.float32) / axis_dim)
    ang_r = r.reshape(-1, 1).astype(jnp.float32) * inv
    ang_c = col.reshape(-1, 1).astype(jnp.float32) * inv
    ang = jnp.concatenate([ang_r, ang_r, ang_c, ang_c], axis=-1)
    return jnp.cos(ang), jnp.sin(ang)


def apply_axial_rope(x, cos, sin):
    x1, x2, x3, x4 = jnp.split(x, 4, axis=-1)
    rot = jnp.concatenate([-x2, x1, -x4, x3], axis=-1)
    shape = (cos.shape[0],) + (1,) * (x.ndim - 3) + (HEAD_DIM,)
    return x * cos.reshape(shape) + rot * sin.reshape(shape)


def gated_delta_scan(q, k, v, g, beta):
    bsz, t_len, n_heads, dk = q.shape
    dv = v.shape[-1]
    n = t_len // CHUNK
    qc, kc, vc = (jnp.swapaxes(to_chunks(a, n), 2, 3) for a in (q, k, v))
    gc, bc = (jnp.swapaxes(to_chunks(a, n), 2, 3) for a in (g, beta))
    gcum = jnp.cumsum(gc, axis=-1)
    pos = jnp.arange(CHUNK)
    incl = pos[:, None] >= pos[None, :]
    strict = pos[:, None] > pos[None, :]
    decay = jnp.where(incl, jnp.exp(jnp.where(incl, gcum[..., :, None] - gcum[..., None, :], 0.0)), 0.0)
    lower = jnp.where(strict, jnp.einsum('bnhid,bnhjd->bnhij', kc, kc) * decay * bc[..., :, None], 0.0)
    eye = jnp.eye(CHUNK, dtype=q.dtype)
    rhs = jnp.concatenate([vc * bc[..., None], kc * (bc * jnp.exp(gcum))[..., None]], axis=-1)
    sol = lax.linalg.triangular_solve(eye + lower, rhs, left_side=True, lower=True)
    u, w = sol[..., :dv], sol[..., dv:]
    attn = jnp.einsum('bnhid,bnhjd->bnhij', qc, kc) * decay
    q_dec = qc * jnp.exp(gcum)[..., None]
    k_dec = kc * jnp.exp(gcum[..., -1:] - gcum)[..., None]
    g_last = jnp.exp(gcum[..., -1])

    def step(state, xs):
        u_i, w_i, attn_i, qd_i, kd_i, gl_i = xs
        v_new = u_i - jnp.einsum('bhck,bhkv->bhcv', w_i, state)
        o_i = jnp.einsum('bhck,bhkv->bhcv', qd_i, state) + jnp.einsum('bhij,bhjv->bhiv', attn_i, v_new)
        state = state * gl_i[..., None, None] + jnp.einsum('bhck,bhcv->bhkv', kd_i, v_new)
        return state, o_i

    xs = tuple(jnp.moveaxis(a, 1, 0) for a in (u, w, attn, q_dec, k_dec, g_last))
    _, o = lax.scan(step, jnp.zeros((bsz, n_heads, dk, dv), q.dtype), xs)
    return jnp.swapaxes(jnp.moveaxis(o, 0, 1), 2, 3).reshape(bsz, t_len, n_heads, dv)


def hgrn2_scan(q, log_f, k, v):
    bsz, t_len, n_heads, dk = q.shape
    dv = v.shape[-1]
    n = t_len // CHUNK
    bcum = jnp.cumsum(to_chunks(log_f, n), axis=2)
    pos = jnp.arange(CHUNK)
    incl = (pos[:, None] >= pos[None, :])[None, :, :, None, None]

    def step(state, xs):
        q_i, k_i, v_i, b_i = xs
        diff = jnp.where(incl, b_i[:, :, None] - b_i[:, None, :], 0.0)
        w_ts = jnp.where(incl, jnp.exp(diff), 0.0)
        scores = jnp.einsum('bthk,bshk,btshk->bhts', q_i, k_i, w_ts)
        b_last = b_i[:, -1]
        o_i = jnp.einsum('bhts,bshv->bthv', scores, v_i) + jnp.einsum('bthk,bhkv->bthv', q_i * jnp.exp(b_i), state)
        state = state * jnp.exp(b_last)[..., None] + jnp.einsum('bshk,bshv->bhkv', k_i * jnp.exp(b_last[:, None] - b_i), v_i)
        return state, o_i

    xs = tuple(jnp.moveaxis(a, 1, 0) for a in (to_chunks(q, n), to_chunks(k, n), to_chunks(v, n), bcum))
    _, o = lax.scan(step, jnp.zeros((bsz, n_heads, dk, dv), q.dtype), xs)
    return jnp.moveaxis(o, 0, 1).reshape(bsz, t_len, n_heads, dv)


def complex_affine_combine(e1, e2):
    a1r, a1i, b1r, b1i = e1
    a2r, a2i, b2r, b2i = e2
    return (a2r * a1r - a2i * a1i, a2r * a1i + a2i * a1r, a2r * b1r - a2i * b1i + b2r, a2r * b1i + a2i * b1r + b2i)


def s5_scan(u, lam_re, lam_im, log_step, b_re, b_im):
    lam_re, lam_im = lam_re.astype(jnp.float32), lam_im.astype(jnp.float32)
    step = jnp.exp(log_step.astype(jnp.float32))[:, None]
    mag = jnp.exp(lam_re * step)
    a_re, a_im = mag * jnp.cos(lam_im * step), mag * jnp.sin(lam_im * step)
    den = lam_re * lam_re + lam_im * lam_im
    n_re = a_re - 1.0
    coef_re = (n_re * lam_re + a_im * lam_im) / den
    coef_im = (a_im * lam_re - n_re * lam_im) / den
    b_re, b_im = b_re.astype(jnp.float32), b_im.astype(jnp.float32)
    bb_re = coef_re[..., None] * b_re - coef_im[..., None] * b_im
    bb_im = coef_re[..., None] * b_im + coef_im[..., None] * b_re
    x_re = jnp.einsum('btgh,gph->btgp', u, bb_re)
    x_im = jnp.einsum('btgh,gph->btgp', u, bb_im)
    shape = (1, u.shape[1]) + a_re.shape
    _, _, h_re, h_im = lax.associative_scan(complex_affine_combine, (jnp.broadcast_to(a_re, shape), jnp.broadcast_to(a_im, shape), x_re, x_im), axis=1)
    return h_re, h_im


def deltanet_mixer(qkv_c, qkv_l, z_c, z_l, a_c, a_l, b_c, b_l, conv_w, a_log, dt_bias, norm_g):
    dt = qkv_l.dtype
    n_ctx = qkv_c.shape[1]

    def prep(qkv):
        h = jax.nn.silu(dwconv_centred(qkv, conv_w)).astype(jnp.float32)
        h = h.reshape(h.shape[:2] + (3, DN_HEADS, HEAD_DIM))
        return l2norm(h[:, :, 0]) * HEAD_DIM ** -0.5, l2norm(h[:, :, 1]), h[:, :, 2]

    def gates(a, b, d):
        a = a.astype(jnp.float32)[..., d * DN_HEADS:(d + 1) * DN_HEADS]
        b = b.astype(jnp.float32)[..., d * DN_HEADS:(d + 1) * DN_HEADS]
        g = -jnp.exp(a_log[d].astype(jnp.float32)) * jax.nn.softplus(a + dt_bias[d].astype(jnp.float32))
        return g, jax.nn.sigmoid(b)

    qc, kc, vc = prep(qkv_c)
    ql, kl, vl = prep(qkv_l)
    oc, ol = [], []
    for d in range(2):
        rev = d == 1
        g_c, beta_c = gates(a_c, b_c, d)
        g_l, beta_l = gates(a_l, b_l, d)
        o = gated_delta_scan(seq_join(qc, ql, rev), seq_join(kc, kl, rev), seq_join(vc, vl, rev), seq_join(g_c, g_l, rev), seq_join(beta_c, beta_l, rev))
        o_c, o_l = seq_split(o, n_ctx, rev)
        oc.append(o_c)
        ol.append(o_l)

    def out(o, z):
        y = rms_norm(o, norm_g) * jax.nn.silu(z.astype(jnp.float32).reshape(o.shape))
        return y.reshape(o.shape[:2] + (BRANCH_W,)).astype(dt)

    return out(oc[0] + oc[1], z_c), out(ol[0] + ol[1], z_l)


def s5_mixer(u_c, u_l, lam_re, lam_im, log_step, b_re, b_im, c_re, c_im, d_skip, w_glu):
    dt = u_l.dtype
    n_ctx = u_c.shape[1]

    def grp(u):
        return u.astype(jnp.float32).reshape(u.shape[:2] + (S5_GROUPS, S5_GROUP))

    gu_c, gu_l = grp(u_c), grp(u_l)
    c_re, c_im = c_re.astype(jnp.float32), c_im.astype(jnp.float32)
    yc, yl = [], []
    for d in range(2):
        rev = d == 1
        h_re, h_im = s5_scan(seq_join(gu_c, gu_l, rev), lam_re[d], lam_im[d], log_step[d], b_re, b_im)
        y = jnp.einsum('gkp,btgp->btgk', c_re, h_re) - jnp.einsum('gkp,btgp->btgk', c_im, h_im)
        y_c, y_l = seq_split(y, n_ctx, rev)
        yc.append(y_c)
        yl.append(y_l)

    def out(y, u):
        y = (y + d_skip.astype(jnp.float32).reshape(S5_GROUPS, S5_GROUP) * u).reshape(u.shape[:2] + (S5_WIDTH,))
        ab = jax.nn.gelu(y) @ w_glu.astype(jnp.float32)
        return (ab[..., :S5_WIDTH] * jax.nn.sigmoid(ab[..., S5_WIDTH:])).astype(dt)

    return out(yc[0] + yc[1], gu_c), out(yl[0] + yl[1], gu_l)


def hgrn2_mixer(q_c, q_l, f_c, f_l, v_c, v_l, g_c, g_l, lb, norm_g):
    dt = q_l.dtype
    n_ctx = q_c.shape[1]
    lb = lb.astype(jnp.float32).reshape(HG_HEADS, HEAD_DIM)

    def heads(t):
        return t.astype(jnp.float32).reshape(t.shape[:2] + (HG_HEADS, HEAD_DIM))

    def forget(f, d):
        f = heads(f[..., d * BRANCH_W:(d + 1) * BRANCH_W])
        f_gate = lb + (1.0 - lb) * jax.nn.sigmoid(f)
        return jnp.log(jnp.maximum(f_gate, LOG_FLOOR)), (1.0 - lb) * jax.nn.sigmoid(-f)

    qc, ql = jax.nn.silu(heads(q_c)), jax.nn.silu(heads(q_l))
    vc, vl = heads(v_c), heads(v_l)
    oc, ol = [], []
    for d in range(2):
        rev = d == 1
        lf_c, k_c = forget(f_c, d)
        lf_l, k_l = forget(f_l, d)
        o = hgrn2_scan(seq_join(qc, ql, rev), seq_join(lf_c, lf_l, rev), seq_join(k_c, k_l, rev), seq_join(vc, vl, rev))
        o_c, o_l = seq_split(o, n_ctx, rev)
        oc.append(o_c)
        ol.append(o_l)

    def out(o, g):
        y = rms_norm(o, norm_g) * jax.nn.sigmoid(heads(g))
        return y.reshape(o.shape[:2] + (BRANCH_W,)).astype(dt)

    return out(oc[0] + oc[1], g_c), out(ol[0] + ol[1], g_l)


def attention_mixer(q_c, q_l, kv_c, kv_l, qn_g, kn_g, cos, sin):
    dt = q_l.dtype
    scale = HEAD_DIM ** -0.5

    def split_qkv(q, kv):
        bsz, t = q.shape[:2]
        q = rms_norm(q.astype(jnp.float32).reshape(bsz, t, ATT_KV_HEADS, ATT_GROUP, HEAD_DIM), qn_g)
        kv = kv.astype(jnp.float32).reshape(bsz, t, 2, ATT_KV_HEADS, HEAD_DIM)
        return q, rms_norm(kv[:, :, 0], kn_g), kv[:, :, 1]

    qc, kc, vc = split_qkv(q_c, kv_c)
    ql, kl, vl = split_qkv(q_l, kv_l)
    ql, kl = apply_axial_rope(ql, cos, sin), apply_axial_rope(kl, cos, sin)
    p_c = jax.nn.softmax(jnp.einsum('bqkgd,bskd->bkgqs', qc, kc) * scale, axis=-1)
    oc = jnp.einsum('bkgqs,bskd->bqkgd', p_c, vc)
    keys = jnp.concatenate([kc, kl], axis=1)
    vals = jnp.concatenate([vc, vl], axis=1)
    bsz, n_lat = ql.shape[:2]
    qb = jnp.moveaxis(ql.reshape((bsz, n_lat // ATT_BLOCK, ATT_BLOCK) + ql.shape[2:]), 1, 0)

    def attend(q_blk):
        p = jax.nn.softmax(jnp.einsum('bqkgd,bskd->bkgqs', q_blk, keys) * scale, axis=-1)
        return jnp.einsum('bkgqs,bskd->bqkgd', p, vals)

    ol = jnp.moveaxis(lax.map(attend, qb), 0, 1)
    return oc.reshape(oc.shape[:2] + (BRANCH_W,)).astype(dt), ol.reshape(bsz, n_lat, BRANCH_W).astype(dt)


def merge_branches(ys, gates, w_branch, w_out):
    acc = None
    for i, y in enumerate(ys):
        term = jax.nn.sigmoid(gates[..., i * D_MODEL:(i + 1) * D_MODEL]) * (y @ w_branch[i])
        acc = term if acc is None else acc + term
    return acc @ w_out


def swiglu_half_step(x, g, shift, scale, gate, w1, w3, w2):
    h = modulate(rms_norm(x, g), shift, scale)
    return x + MACARON_W * gate * ((jax.nn.silu(h @ w1) * (h @ w3)) @ w2)


def setup_inputs(seed: int = 0) -> dict:
    key = jax.random.key(seed)
    ks = jax.random.split(key, 32)

    def nrm(i, shape, s):
        return s * jax.random.normal(ks[i], shape, jnp.float32)

    def unif(i, shape, lo, hi):
        return jax.random.uniform(ks[i], shape, jnp.float32, lo, hi)

    L = DEPTH
    dt0 = jnp.exp(unif(13, (L, 2, DN_HEADS), math.log(1e-3), math.log(1e-1)))
    return {
        'x': nrm(0, (BATCH, SEQ, D_MODEL), 1.0),
        'c': nrm(1, (BATCH, D_MODEL), 1.0),
        'ctx': nrm(2, (BATCH, CTX_LEN, D_MODEL), 1.0),
        'c_ctx': nrm(3, (D_MODEL,), 1.0),
        'ada_w': nrm(4, (L, D_MODEL, N_MOD * D_MODEL), 0.5 * D_MODEL ** -0.5),
        'ada_b': nrm(5, (L, N_MOD * D_MODEL), 0.02),
        'norm_g': 1.0 + nrm(6, (L, 3, D_MODEL), 0.05),
        'ffn_w1': nrm(7, (L, 2, D_MODEL, D_FF), D_MODEL ** -0.5),
        'ffn_w3': nrm(8, (L, 2, D_MODEL, D_FF), D_MODEL ** -0.5),
        'ffn_w2': nrm(9, (L, 2, D_FF, D_MODEL), D_FF ** -0.5),
        'w_in': nrm(10, (L, D_MODEL, IN_COLS), D_MODEL ** -0.5),
        'dn_conv': nrm(11, (L, DN_CONV, 3 * BRANCH_W), DN_CONV ** -0.5),
        'dn_a_log': jnp.log(unif(12, (L, 2, DN_HEADS), 1.0, 16.0)),
        'dn_dt_bias': dt0 + jnp.log(-jnp.expm1(-dt0)),
        'dn_norm_g': 1.0 + nrm(14, (L, HEAD_DIM), 0.05),
        's5_lam_re': -0.5 + nrm(15, (L, 2, S5_GROUPS, S5_STATE), 0.01),
        's5_lam_im': jnp.pi * jnp.arange(S5_STATE, dtype=jnp.float32) + nrm(16, (L, 2, S5_GROUPS, S5_STATE), 0.01),
        's5_log_step': unif(17, (L, 2, S5_GROUPS), math.log(1e-3), math.log(1e-1)),
        's5_b_re': nrm(18, (L, S5_GROUPS, S5_STATE, S5_GROUP), (2 * S5_GROUP) ** -0.5),
        's5_b_im': nrm(19, (L, S5_GROUPS, S5_STATE, S5_GROUP), (2 * S5_GROUP) ** -0.5),
        's5_c_re': nrm(20, (L, S5_GROUPS, S5_GROUP, S5_STATE), S5_STATE ** -0.5),
        's5_c_im': nrm(21, (L, S5_GROUPS, S5_GROUP, S5_STATE), S5_STATE ** -0.5),
        's5_d': nrm(22, (L, S5_WIDTH), 1.0),
        's5_glu': nrm(23, (L, S5_WIDTH, 2 * S5_WIDTH), S5_WIDTH ** -0.5),
        'hg_lb_logits': nrm(24, (L, HG_HEADS * HEAD_DIM), 0.1),
        'hg_norm_g': 1.0 + nrm(25, (L, HEAD_DIM), 0.05),
        'at_qn_g': 1.0 + nrm(26, (L, HEAD_DIM), 0.05),
        'at_kn_g': 1.0 + nrm(27, (L, HEAD_DIM), 0.05),
        'w_branch': nrm(28, (L, N_BRANCH, BRANCH_W, D_MODEL), BRANCH_W ** -0.5),
        'w_out': nrm(29, (L, D_MODEL, D_MODEL), D_MODEL ** -0.5),
        'final_g': 1.0 + nrm(30, (D_MODEL,), 0.05),
    }


def reference(x, c, ctx, c_ctx, ada_w, ada_b, norm_g, ffn_w1, ffn_w3, ffn_w2, w_in, dn_conv, dn_a_log, dn_dt_bias, dn_norm_g, s5_lam_re, s5_lam_im, s5_log_step, s5_b_re, s5_b_im, s5_c_re, s5_c_im, s5_d, s5_glu, hg_lb_logits, hg_norm_g, at_qn_g, at_kn_g, w_branch, w_out, final_g):
    cos, sin = axial_rope_tables(x.shape[1])
    lb_w = jax.nn.softmax(hg_lb_logits.astype(jnp.float32), axis=0)
    lower_bounds = jnp.cumsum(lb_w, axis=0) - lb_w[0:1]
    x_l, x_c = x, ctx
    for l in range(DEPTH):
        m_l = ada_mods(c, ada_w[l], ada_b[l])
        m_c = ada_mods(c_ctx, ada_w[l], ada_b[l])
        x_l = swiglu_half_step(x_l, norm_g[l, 0], m_l[0], m_l[1], m_l[2], ffn_w1[l, 0], ffn_w3[l, 0], ffn_w2[l, 0])
        x_c = swiglu_half_step(x_c, norm_g[l, 0], m_c[0], m_c[1], m_c[2], ffn_w1[l, 0], ffn_w3[l, 0], ffn_w2[l, 0])
        p_l = split_cols(modulate(rms_norm(x_l, norm_g[l, 1]), m_l[3], m_l[4]) @ w_in[l])
        p_c = split_cols(modulate(rms_norm(x_c, norm_g[l, 1]), m_c[3], m_c[4]) @ w_in[l])
        y_dn = deltanet_mixer(p_c[0], p_l[0], p_c[1], p_l[1], p_c[2], p_l[2], p_c[3], p_l[3], dn_conv[l], dn_a_log[l], dn_dt_bias[l], dn_norm_g[l])
        y_s5 = s5_mixer(p_c[4], p_l[4], s5_lam_re[l], s5_lam_im[l], s5_log_step[l], s5_b_re[l], s5_b_im[l], s5_c_re[l], s5_c_im[l], s5_d[l], s5_glu[l])
        y_hg = hgrn2_mixer(p_c[5], p_l[5], p_c[6], p_l[6], p_c[7], p_l[7], p_c[8], p_l[8], lower_bounds[l], hg_norm_g[l])
        y_at = attention_mixer(p_c[9], p_l[9], p_c[10], p_l[10], at_qn_g[l], at_kn_g[l], cos, sin)
        x_l = x_l + m_l[5] * merge_branches([y_dn[1], y_s5[1], y_hg[1], y_at[1]], p_l[11], w_branch[l], w_out[l])
        x_l = swiglu_half_step(x_l, norm_g[l, 2], m_l[6], m_l[7], m_l[8], ffn_w1[l, 1], ffn_w3[l, 1], ffn_w2[l, 1])
        if l < DEPTH - 1:
            x_c = x_c + m_c[5] * merge_branches([y_dn[0], y_s5[0], y_hg[0], y_at[0]], p_c[11], w_branch[l], w_out[l])
            x_c = swiglu_half_step(x_c, norm_g[l, 2], m_c[6], m_c[7], m_c[8], ffn_w1[l, 1], ffn_w3[l, 1], ffn_w2[l, 1])
    return rms_norm(x_l, final_g)
```

```python
import numpy as np
from contextlib import ExitStack
import concourse.bass as bass
import concourse.mybir as mybir
from concourse.bass_utils import run_bass_kernel_spmd

F32 = mybir.dt.float32
BF16 = mybir.dt.bfloat16
AF = mybir.ActivationFunctionType
ALU = mybir.AluOpType
AX = mybir.AxisListType

D = 1024
SEQ = 2048
CTX = 256
T = SEQ + CTX
DFF = 2816
NFF = DFF // 128
EPS = 1e-6
N_CORES = 8
ENG = ("pe", "act", "dve", "pool", "sp")

FM_GROUPS = [(0, 768), (768, 256), (1040, 256), (1296, 256), (1552, 512), (2320, 256), (2576, 256), (2832, 128)]
FM_W = sum(w for _, w in FM_GROUPS)
FM_Q, FM_K, FM_V, FM_Z, FM_U, FM_HQ, FM_HF, FM_HG, FM_AQ, FM_AK = 0, 256, 512, 768, 1024, 1280, 1536, 2048, 2304, 2560
TOK_GROUPS = [(2960, 128), (2064, 256), (1024, 8), (1032, 8)]
TOK_W = 400
TK_AV, TK_HV, TK_A, TK_B = 0, 128, 384, 392
GATE0 = 3088


class Prog:
    def __init__(self, nc, n_dma_sems=56):
        self.nc = nc
        self.sem = {e: nc.semaphore("sem_" + e).__enter__() for e in ("pe", "act", "dve", "pool")}
        self.dsem = [nc.semaphore("dsem%d" % i).__enter__() for i in range(n_dma_sems)]
        self.duse = [0] * n_dma_sems
        self.dnext = 0
        self.cnt = {e: 0 for e in self.sem}
        self.lastw = {}
        self.readers = {}
        self.ops = {e: [] for e in ENG}
        self.waited = {}
        self.nops = 0
        self.serial = False
        self.last_ev = {}

    def _need(self, eng, ev, waits):
        if ev is None:
            return
        key, val, src = ev
        if self.waited.get((eng, key), 0) >= val:
            return
        self.waited[(eng, key)] = val
        waits.append((key, val))

    def _semh(self, key):
        return self.sem[key] if isinstance(key, str) else self.dsem[key]

    def op(self, eng, fn, reads=(), writes=(), inc=True):
        waits = []
        for r in reads:
            ev = self.lastw.get(r)
            if ev is not None and not (ev[2] == eng and eng == "pe"):
                self._need(eng, ev, waits)
        for w in writes:
            ev = self.lastw.get(w)
            if ev is not None and ev[2] != eng:
                self._need(eng, ev, waits)
            for rv in self.readers.get(w, ()):
                if rv[2] != eng:
                    self._need(eng, rv, waits)
        if self.serial:
            for oe, ev in self.last_ev.items():
                if oe != eng:
                    self._need(eng, ev, waits)
        if inc:
            self.cnt[eng] += 1
            me = (eng, self.cnt[eng], eng)
        else:
            me = (eng, self.cnt[eng] + 1, eng)
        self.last_ev[eng] = me
        for r in reads:
            self.readers.setdefault(r, []).append(me)
        for w in writes:
            self.lastw[w] = me
            self.readers[w] = []
        self.ops[eng].append((waits, fn, ("sem", eng) if inc else ("none", eng)))
        self.nops += 1
        return me

    def dma(self, fn, reads=(), writes=(), q="sp"):
        waits = []
        for r in reads:
            self._need(q, self.lastw.get(r), waits)
        for w in writes:
            self._need(q, self.lastw.get(w), waits)
            for rv in self.readers.get(w, ()):
                self._need(q, rv, waits)
        i = self.dnext
        self.dnext = (self.dnext + 1) % len(self.dsem)
        if self.duse[i] > 0:
            self._need(q, (i, 16 * self.duse[i], "dma"), waits)
        self.duse[i] += 1
        me = (i, 16 * self.duse[i], "dma")
        for r in reads:
            self.readers.setdefault(r, []).append(me)
        for w in writes:
            self.lastw[w] = me
            self.readers[w] = []
        self.ops[q].append((waits, fn, ("dsem", i)))
        self.nops += 1
        return me

    def end_stage(self):
        waits = []
        for tok, ev in self.lastw.items():
            self._need("sp", ev, waits)
        for tok, evs in self.readers.items():
            for ev in evs:
                self._need("sp", ev, waits)
        self.ops["sp"].append((waits, None, None))
        nc = self.nc
        engobj = {"pe": "tensor", "act": "scalar", "dve": "vector", "pool": "gpsimd", "sp": "sync"}
        with nc.Block() as block:
            for e in ENG:
                ops = self.ops[e]
                if not ops:
                    continue

                def body(eo, ops=ops):
                    for waits, fn, inc in ops:
                        for key, val in waits:
                            eo.wait_ge(self._semh(key), val)
                        if fn is None:
                            continue
                        ins = fn(eo)
                        if inc[0] == "sem":
                            ins.then_inc(self.sem[inc[1]], 1)
                        elif inc[0] == "dsem":
                            ins.then_inc(self.dsem[inc[1]], 16)

                getattr(block, engobj[e])(body)
        self.ops = {e: [] for e in ENG}
        self.waited = {}
        self.lastw = {}
        self.readers = {}
        self.last_ev = {}
        self.serial = False


class K:
    pass


class NCProxy:
    def __init__(self, nc):
        object.__setattr__(self, "_nc", nc)
        object.__setattr__(self, "_n", [0])

    def __getattr__(self, name):
        return getattr(self._nc, name)

    def sbuf_tensor(self, name, shape, dtype):
        self._n[0] += 1
        return self._nc.sbuf_tensor("%s_u%d" % (name, self._n[0]), shape, dtype)

    def psum_tensor(self, name, shape, dtype):
        self._n[0] += 1
        return self._nc.psum_tensor("%s_u%d" % (name, self._n[0]), shape, dtype)


def token_tiles(NB, ts=512):
    tl = []
    for b in range(NB):
        tl.append((b, 0, CTX, NB))
        for i in range(SEQ // ts):
            tl.append((b, CTX + i * ts, ts, b))
    return tl


def build(NB, opts=None):
    opts = opts or {}
    NC = NB + 1
    nc = bass.Bass("TRN2", target_bir_lowering=False)
    k = K()
    k.nc, k.NB, k.NC, k.opts = NCProxy(nc), NB, NC, opts
    inp = lambda name, shape: nc.dram_tensor(name, list(shape), F32, kind="ExternalInput").ap()
    k.x = inp("x", [NB, SEQ, D])
    k.ctx = inp("ctx", [NB, CTX, D])
    k.cT = inp("cT", [128, 8, NC])
    k.ada_w = inp("ada_w", [2, D, 9 * D])
    k.ada_b = inp("ada_b", [2, 128, 72])
    k.norm_g = inp("norm_g", [2, 3, 128, 8])
    k.final_g = inp("final_g", [128, 8])
    k.w1 = inp("ffn_w1", [2, 2, D, DFF])
    k.w3 = inp("ffn_w3", [2, 2, D, DFF])
    k.w2 = inp("ffn_w2", [2, 2, DFF, D])
    k.w_fm = inp("w_fm", [2, D, FM_W])
    k.w_tok = inp("w_tok", [2, D, TOK_W])
    k.w_gate = inp("w_gate", [2, D, 4 * D])
    k.w_branch = inp("w_branch", [2, 4, 256, D])
    k.w_out = inp("w_out", [2, D, D])
    k.ident = inp("ident", [128, 128])
    k.rope_cos = inp("rope_cos", [64, SEQ])
    k.rope_sin = inp("rope_sin", [64, SEQ])
    k.rope_rm = inp("rope_rm", [64, 64])
    k.at_qn_g = inp("at_qn_g", [2, 64, 1])
    k.at_kn_g = inp("at_kn_g", [2, 64, 1])
    k.hg_masks = inp("hg_masks", [128, 2, 64])
    k.s5_lam_re = inp("s5_lam_re", [2, 128, 16])
    k.s5_lam_im = inp("s5_lam_im", [2, 128, 16])
    k.s5_log_step = inp("s5_log_step", [2, 128, 16])
    k.s5_b_re = inp("s5_b_re", [2, 128, 8, 16])
    k.s5_b_im = inp("s5_b_im", [2, 128, 8, 16])
    k.s5_c_re = inp("s5_c_re", [2, 128, 8, 16])
    k.s5_c_im = inp("s5_c_im", [2, 128, 8, 16])
    k.s5_d = inp("s5_d", [2, 128, 2])
    k.s5_glu = inp("s5_glu", [2, 256, 512])
    k.S5TAB = nc.dram_tensor("S5TAB", [16, 2, 128, T], F32).ap()
    k.dn_gm64 = inp("dn_gm64", [64, 2, 2, 128])
    k.dn_gmbd = inp("dn_gmbd", [128, 2, 2, 128])
    k.dn_idn2 = inp("dn_idn2", [128, 64])
    k.dn_conv = inp("dn_conv", [2, 128, 6, 5])
    k.dn_a_log = inp("dn_a_log", [2, 128, 8])
    k.dn_dt_bias = inp("dn_dt_bias", [2, 128, 8])
    k.dn_norm_g = inp("dn_norm_g", [2, 128, 64])
    k.bd64 = inp("bd64", [128, 128])
    k.hg_lb = inp("hg_lb", [128, 2, 2])
    k.hg_norm_g = inp("hg_norm_g", [2, 128, 1])
    k.out = nc.dram_tensor("out", [NB, SEQ, D], F32, kind="ExternalOutput").ap()
    k.XT = nc.dram_tensor("XT", [NB, D, T], F32).ap()
    only = opts.get("only")
    k.PT = nc.dram_tensor("PT", [NB, FM_W, T], F32, **({"kind": "ExternalInput"} if only else {})).ap()
    k.TOK = nc.dram_tensor("TOK", [NB, T, TOK_W], F32, **({"kind": "ExternalInput"} if only else {})).ap()
    k.YT = nc.dram_tensor("YT", [NB, 4, 256, T], BF16, **({"kind": "ExternalOutput"} if only else {})).ap()
    if "dump" in opts:
        k.dump = nc.dram_tensor("dump", [16, 128, T], F32, kind="ExternalOutput").ap()
    if "dbg" in opts:
        k.dbg = {nm: nc.dram_tensor("dbg_" + nm, list(shp), F32, kind="ExternalOutput").ap() for nm, shp in opts["dbg"].items()}
    P = Prog(nc)
    k.P = P
    with ExitStack() as es:
        al = lambda name, shape, dt=F32: es.enter_context(nc.sbuf_tensor(name, list(shape), dt))
        k.idn = al("idn", [128, 128])
        k.onesb = al("onesb", [128, 128], BF16)
        k.mods = al("mods", [128, 72, NC])
        k.ng = al("ng", [128, 2, 3, 8])
        k.fg = al("fg", [128, 8])
        P.dma(lambda e: e.dma_start(out=k.idn[:], in_=k.ident), writes=["idn"])
        P.dma(lambda e: e.dma_start(out=k.fg[:], in_=k.final_g), writes=["fg"])
        for l in range(2):
            for j in range(3):
                P.dma(lambda e, l=l, j=j: e.dma_start(out=k.ng[:, l, j, :], in_=k.norm_g[l, j]), writes=["ng"])
        P.op("dve", lambda e: e.memset(k.onesb[:], 1.0 / D), writes=["onesb"])
        P.end_stage()
        if only:
            for l in opts.get("layers", (0,)):
                {"att": stage_attention, "hg": stage_hgrn2, "dn": stage_deltanet, "s5": stage_s5}[only](k, l)
            return nc
        stage_transpose_in(k)
        stop = opts.get("stop", "")
        for l in range(2):
            stage_ada(k, l)
            stage_ffn(k, l, 0)
            if stop == "ffn1_%d" % l:
                break
            stage_inproj(k, l)
            if stop == "inproj_%d" % l:
                break
            stage_mixers(k, l)
            if stop == "mixers_%d" % l:
                break
            stage_merge(k, l)
            if stop == "merge_%d" % l:
                break
            stage_ffn(k, l, 1)
        if "XT" in opts.get("dbg", {}):
            P.dma(lambda e: e.dma_start(out=k.dbg["XT"], in_=k.XT), reads=[], writes=["dbgx"])
            P.end_stage()
        if "PT" in opts.get("dbg", {}):
            P.dma(lambda e: e.dma_start(out=k.dbg["PT"], in_=k.PT), reads=[], writes=["dbgp"])
            P.dma(lambda e: e.dma_start(out=k.dbg["TOK"], in_=k.TOK), reads=[], writes=["dbgt"])
            P.end_stage()
        stage_final(k)
    return nc


def stage_transpose_in(k):
    nc, P = k.nc, k.P
    with ExitStack() as es:
        xin = [es.enter_context(nc.sbuf_tensor("ti_x%d" % i, [128, D], F32)) for i in range(2)]
        xo = [es.enter_context(nc.sbuf_tensor("ti_o%d" % i, [128, 8, 128], F32)) for i in range(2)]
        ps = [es.enter_context(nc.psum_tensor("ti_ps%d" % i, [128, 4, 128], F32)) for i in range(4)]
        it = 0
        for b in range(k.NB):
            for tb in range(T // 128):
                s = it % 2
                src = k.ctx[b, tb * 128:(tb + 1) * 128, :] if tb < 2 else k.x[b, (tb - 2) * 128:(tb - 1) * 128, :]
                P.dma(lambda e, s=s, src=src: e.dma_start(out=xin[s][:], in_=src), writes=[("xin", s)])
                for h in range(2):
                    pi = (it * 2 + h) % 4
                    for c in range(4):
                        ch = h * 4 + c
                        P.op("pe", lambda e, s=s, pi=pi, c=c, ch=ch: e.transpose(ps[pi][:, c, :], xin[s][:, ch * 128:(ch + 1) * 128], k.idn[:]),
                             reads=[("xin", s), "idn"], writes=[("tps", pi)])
                    eng = "act" if h == 0 else "dve"
                    if eng == "act":
                        P.op("act", lambda e, s=s, pi=pi, h=h: e.activation(out=xo[s][:, h * 4:(h + 1) * 4, :], in_=ps[pi][:], func=AF.Identity),
                             reads=[("tps", pi)], writes=[("xo", s, h)])
                    else:
                        P.op("dve", lambda e, s=s, pi=pi, h=h: e.tensor_copy(out=xo[s][:, h * 4:(h + 1) * 4, :], in_=ps[pi][:]),
                             reads=[("tps", pi)], writes=[("xo", s, h)])
                dst = k.XT[b].rearrange("(c p) t -> p c t", p=128)[:, :, tb * 128:(tb + 1) * 128]
                P.dma(lambda e, s=s, dst=dst: e.dma_start(out=dst, in_=xo[s][:]), reads=[("xo", s, 0), ("xo", s, 1)], writes=[("XT", b)])
                it += 1
        P.end_stage()


def stage_ada(k, l):
    nc, P, NC = k.nc, k.P, k.NC
    with ExitStack() as es:
        sc = es.enter_context(nc.sbuf_tensor("ad_sc", [128, 8, NC], F32))
        ab = es.enter_context(nc.sbuf_tensor("ad_b", [128, 72], F32))
        wt = [es.enter_context(nc.sbuf_tensor("ad_w%d" % i, [128, 8, 1024], F32)) for i in range(2)]
        ps = [es.enter_context(nc.psum_tensor("ad_ps%d" % i, [128, 8, NC], F32)) for i in range(2)]
        P.dma(lambda e: e.dma_start(out=sc[:], in_=k.cT), writes=["sc"])
        P.dma(lambda e: e.dma_start(out=ab[:], in_=k.ada_b[l]), writes=["ab"])
        P.op("act", lambda e: e.activation(out=sc[:], in_=sc[:], func=AF.Silu), reads=["sc"], writes=["sc"])
        for j in range(9):
            s = j % 2
            src = k.ada_w[l][:, j * 1024:(j + 1) * 1024].rearrange("(c p) n -> p c n", p=128)
            for hh in range(2):
                P.dma(lambda e, s=s, src=src, hh=hh: e.dma_start(out=wt[s][:, hh * 4:(hh + 1) * 4, :], in_=src[:, hh * 4:(hh + 1) * 4, :]), writes=[("adw", s, hh)])
            for m in range(8):
                for kc in range(8):
                    P.op("pe", lambda e, s=s, m=m, kc=kc: e.matmul(ps[s][:, m, :], lhsT=wt[s][:, kc, m * 128:(m + 1) * 128], rhs=sc[:, kc, :], start=(kc == 0), stop=(kc == 7)),
                         reads=[("adw", s, kc // 4), "sc"], writes=[("adps", s)])
            P.op("dve", lambda e, s=s, j=j: e.tensor_tensor(out=k.mods[:, j * 8:(j + 1) * 8, :], in0=ps[s][:], in1=ab[:, j * 8:(j + 1) * 8].rearrange("p (c o) -> p c o", o=1).to_broadcast([128, 8, NC]), op=ALU.add),
                 reads=[("adps", s), "ab"], writes=["mods"])
        P.end_stage()


def emit_norm_mod(k, es_tiles, x_t, h_t, n, A, Sh, cond, tag, sl):
    P = k.P
    sq, rs, tmp, psms = es_tiles
    P.op("act", lambda e: e.activation(out=sq[:, :, :n], in_=x_t[:, :, :n], func=AF.Square), reads=[("x", sl)], writes=["sq"])
    for c in range(8):
        P.op("pe", lambda e, c=c: e.matmul(psms[:, :n], lhsT=k.onesb[:], rhs=sq[:, c, :n], start=(c == 0), stop=(c == 7)), reads=["sq", "onesb"], writes=["psms"])
    P.op("act", lambda e: e.activation(out=rs[:, :n], in_=psms[:, :n], func=AF.Sqrt, bias=k.epsb[:, 0:1], scale=1.0), reads=["psms", "epsb"], writes=["rs0", "rs"])
    P.op("dve", lambda e: e.reciprocal(out=rs[:, :n], in_=rs[:, :n]), reads=["rs0"], writes=["rs"])
    for c in range(8):
        P.op("dve", lambda e, c=c: e.tensor_tensor(out=tmp[c % 2][:, :n], in0=x_t[:, c, :n], in1=rs[:, :n], op=ALU.mult), reads=[("x", sl), "rs"], writes=[("tmp", c % 2)])
        P.op("pool", lambda e, c=c: e.tensor_scalar(out=h_t[:, c, :n], in0=tmp[c % 2][:, :n], scalar1=A[:, c, cond:cond + 1], scalar2=Sh[:, c, cond:cond + 1], op0=ALU.mult, op1=ALU.add),
             reads=[("tmp", c % 2), tag], writes=["h"])


def alloc_norm_tiles(k, es, pfx, ts=512):
    nc = k.nc
    sq = es.enter_context(nc.sbuf_tensor(pfx + "sq", [128, 8, ts], BF16))
    rs = es.enter_context(nc.sbuf_tensor(pfx + "rs", [128, ts], F32))
    tmp = [es.enter_context(nc.sbuf_tensor(pfx + "tmp%d" % i, [128, ts], F32)) for i in range(2)]
    psms = es.enter_context(nc.psum_tensor(pfx + "psms", [128, 512], F32))
    k.epsb = es.enter_context(nc.sbuf_tensor(pfx + "epsb", [128, 1], F32))
    k.P.op("dve", lambda e: e.memset(k.epsb[:], EPS), writes=["epsb"])
    return sq, rs, tmp, psms


def emit_mod_scalars(k, es, pfx, l, jn, i_shift, i_scale, i_gate, gate_mul):
    nc, P, NC = k.nc, k.P, k.NC
    A = es.enter_context(nc.sbuf_tensor(pfx + "A", [128, 8, NC], F32))
    G = es.enter_context(nc.sbuf_tensor(pfx + "G", [128, 8, NC], F32))
    Sh = k.mods[:, i_shift * 8:(i_shift + 1) * 8, :]
    P.op("dve", lambda e: e.tensor_scalar(out=A[:], in0=k.mods[:, i_scale * 8:(i_scale + 1) * 8, :], scalar1=1.0, scalar2=None, op0=ALU.add), reads=["mods"], writes=["A0"])
    P.op("dve", lambda e: e.tensor_tensor(out=A[:], in0=A[:], in1=k.ng[:, l, jn, :].rearrange("p (c o) -> p c o", o=1).to_broadcast([128, 8, NC]), op=ALU.mult), reads=["A0", "ng"], writes=["modsc"])
    if i_gate is not None:
        P.op("dve", lambda e: e.tensor_scalar(out=G[:], in0=k.mods[:, i_gate * 8:(i_gate + 1) * 8, :], scalar1=gate_mul, scalar2=None, op0=ALU.mult), reads=["mods"], writes=["modsg"])
    return A, Sh, G


def load_w_bf16(k, dst, src_rows, ncols, tok, nk=8):
    P = k.P
    for kc in range(nk):
        P.dma(lambda e, kc=kc: e.dma_start(out=dst[:, kc, :], in_=src_rows[kc * 128:(kc + 1) * 128, :], max_dma_last_dim=4096),
              writes=[(tok, kc)], q="pool")


def stage_ffn(k, l, i):
    nc, P, NB, NC = k.nc, k.P, k.NB, k.NC
    jn = 0 if i == 0 else 2
    mi = (0, 1, 2) if i == 0 else (6, 7, 8)
    last = (l == 1 and i == 1)
    with ExitStack() as es:
        w1 = es.enter_context(nc.sbuf_tensor("f_w1", [128, 8, DFF], BF16))
        w3 = es.enter_context(nc.sbuf_tensor("f_w3", [128, 8, DFF], BF16))
        w2 = es.enter_context(nc.sbuf_tensor("f_w2", [128, NFF, D], BF16))
        xt = [es.enter_context(nc.sbuf_tensor("f_x%d" % s, [128, 8, 256], F32)) for s in range(2)]
        ht = es.enter_context(nc.sbuf_tensor("f_h", [128, 8, 256], BF16))
        hid = es.enter_context(nc.sbuf_tensor("f_hid", [128, NFF, 256], BF16))
        sl_t = [es.enter_context(nc.sbuf_tensor("f_s%d" % s, [128, 256], F32)) for s in range(2)]
        ps1 = [es.enter_context(nc.psum_tensor("f_ps1%d" % s, [128, 512], F32)) for s in range(2)]
        ps3 = [es.enter_context(nc.psum_tensor("f_ps3%d" % s, [128, 512], F32)) for s in range(2)]
        pso = [es.enter_context(nc.psum_tensor("f_pso%d" % s, [128, 512], F32)) for s in range(2)]
        ntiles = alloc_norm_tiles(k, es, "f_", 256)
        A, Sh, G = emit_mod_scalars(k, es, "f_", l, jn, mi[0], mi[1], mi[2], 0.5)
        load_w_bf16(k, w1, k.w1[l, i], DFF, "w1")
        load_w_bf16(k, w3, k.w3[l, i], DFF, "w3")
        load_w_bf16(k, w2, k.w2[l, i], D, "w2", nk=NFF)
        it = 0
        for (b, t0, n, cond) in token_tiles(NB, 256):
            if last and cond == NB:
                continue
            s = it % 2
            it += 1
            xsrc = k.XT[b].rearrange("(c p) t -> p c t", p=128)[:, :, t0:t0 + n]
            P.dma(lambda e, s=s, xsrc=xsrc, n=n: e.dma_start(out=xt[s][:, :, :n], in_=xsrc), reads=[("XT", b)], writes=[("x", s)])
            emit_norm_mod(k, ntiles, xt[s], ht, n, A, Sh, cond, "modsc", s)
            for f in range(NFF):
                q = f % 2
                for kc in range(8):
                    P.op("pe", lambda e, f=f, q=q, kc=kc, n=n: e.matmul(ps1[q][:, :n], lhsT=w1[:, kc, f * 128:(f + 1) * 128], rhs=ht[:, kc, :n], start=(kc == 0), stop=(kc == 7)),
                         reads=[("w1", kc), "h"], writes=[("ps1", q)])
                for kc in range(8):
                    P.op("pe", lambda e, f=f, q=q, kc=kc, n=n: e.matmul(ps3[q][:, :n], lhsT=w3[:, kc, f * 128:(f + 1) * 128], rhs=ht[:, kc, :n], start=(kc == 0), stop=(kc == 7)),
                         reads=[("w3", kc), "h"], writes=[("ps3", q)])
                P.op("act", lambda e, q=q, n=n: e.activation(out=sl_t[q][:, :n], in_=ps1[q][:, :n], func=AF.Silu), reads=[("ps1", q)], writes=[("sl", q)])
                P.op("dve", lambda e, q=q, f=f, n=n: e.tensor_tensor(out=hid[:, f, :n], in0=sl_t[q][:, :n], in1=ps3[q][:, :n], op=ALU.mult), reads=[("sl", q), ("ps3", q)], writes=[("hid", f)])
            for m in range(8):
                q = m % 2
                for f in range(NFF):
                    P.op("pe", lambda e, m=m, q=q, f=f, n=n: e.matmul(pso[q][:, :n], lhsT=w2[:, f, m * 128:(m + 1) * 128], rhs=hid[:, f, :n], start=(f == 0), stop=(f == NFF - 1)),
                         reads=[("w2", f), ("hid", f)], writes=[("pso", q)])
                P.op("dve", lambda e, m=m, q=q, s=s, n=n, cond=cond: e.scalar_tensor_tensor(out=xt[s][:, m, :n], in0=pso[q][:, :n], scalar=G[:, m, cond:cond + 1], in1=xt[s][:, m, :n], op0=ALU.mult, op1=ALU.add),
                     reads=[("pso", q), ("x", s), "modsg"], writes=[("x", s)])
            P.dma(lambda e, s=s, xsrc=xsrc, n=n: e.dma_start(out=xsrc, in_=xt[s][:, :, :n]), reads=[("x", s)], writes=[("XT", b)])
        P.end_stage()


def stage_inproj(k, l):
    nc, P, NB, NC = k.nc, k.P, k.NB, k.NC
    NCH = FM_W // 128
    with ExitStack() as es:
        wf = es.enter_context(nc.sbuf_tensor("p_wf", [128, 8, FM_W], BF16))
        wk = es.enter_context(nc.sbuf_tensor("p_wk", [128, 8, TOK_W], BF16))
        xt = [es.enter_context(nc.sbuf_tensor("p_x%d" % s, [128, 8, 512], F32)) for s in range(2)]
        ht = es.enter_context(nc.sbuf_tensor("p_h", [128, 8, 512], BF16))
        ot = [es.enter_context(nc.sbuf_tensor("p_o%d" % s, [128, 512], F32)) for s in range(4)]
        ps = [es.enter_context(nc.psum_tensor("p_ps%d" % s, [128, 512], F32)) for s in range(4)]
        ntiles = alloc_norm_tiles(k, es, "p_")
        A, Sh, G = emit_mod_scalars(k, es, "p_", l, 1, 3, 4, None, 1.0)
        load_w_bf16(k, wf, k.w_fm[l], FM_W, "wf")
        load_w_bf16(k, wk, k.w_tok[l], TOK_W, "wk")
        it = 0
        oi = 0
        for (b, t0, n, cond) in token_tiles(NB):
            s = it % 2
            it += 1
            xsrc = k.XT[b].rearrange("(c p) t -> p c t", p=128)[:, :, t0:t0 + n]
            P.dma(lambda e, s=s, xsrc=xsrc, n=n: e.dma_start(out=xt[s][:, :, :n], in_=xsrc), reads=[("XT", b)], writes=[("x", s)])
            emit_norm_mod(k, ntiles, xt[s], ht, n, A, Sh, cond, "modsc", s)
            for ch in range(NCH):
                q = oi % 4
                oi += 1
                for kc in range(8):
                    P.op("pe", lambda e, ch=ch, q=q, kc=kc, n=n: e.matmul(ps[q][:, :n], lhsT=wf[:, kc, ch * 128:(ch + 1) * 128], rhs=ht[:, kc, :n], start=(kc == 0), stop=(kc == 7)),
                         reads=[("wf", kc), "h"], writes=[("ps", q)])
                if oi % 2 == 0:
                    P.op("act", lambda e, q=q, n=n: e.activation(out=ot[q][:, :n], in_=ps[q][:, :n], func=AF.Identity), reads=[("ps", q)], writes=[("ot", q)])
                else:
                    P.op("dve", lambda e, q=q, n=n: e.tensor_copy(out=ot[q][:, :n], in_=ps[q][:, :n]), reads=[("ps", q)], writes=[("ot", q)])
                P.dma(lambda e, q=q, ch=ch, b=b, t0=t0, n=n: e.dma_start(out=k.PT[b, ch * 128:(ch + 1) * 128, t0:t0 + n], in_=ot[q][:, :n]), reads=[("ot", q)], writes=[("PT", b)])
            for tb in range(n // 128):
                q = oi % 4
                oi += 1
                for kc in range(8):
                    P.op("pe", lambda e, tb=tb, q=q, kc=kc: e.matmul(ps[q][:, :TOK_W], lhsT=ht[:, kc, tb * 128:(tb + 1) * 128], rhs=wk[:, kc, :], start=(kc == 0), stop=(kc == 7)),
                         reads=[("wk", kc), "h"], writes=[("ps", q)])
                P.op("dve", lambda e, q=q: e.tensor_copy(out=ot[q][:, :TOK_W], in_=ps[q][:, :TOK_W]), reads=[("ps", q)], writes=[("ot", q)])
                P.dma(lambda e, q=q, b=b, tb=tb, t0=t0: e.dma_start(out=k.TOK[b, t0 + tb * 128:t0 + (tb + 1) * 128, :], in_=ot[q][:, :TOK_W]), reads=[("ot", q)], writes=[("TOK", b)])
        P.end_stage()


def stage_attention(k, l):
    nc, P, NB = k.nc, k.P, k.NB
    NKB = T // 128
    with ExitStack() as es:
        al = lambda name, shape, dt=F32: es.enter_context(nc.sbuf_tensor("a_" + name, list(shape), dt))
        cos = al("cos", [64, SEQ]); sin = al("sin", [64, SEQ]); rm = al("rm", [64, 64]); o64 = al("o64", [64, 64])
        gq = al("gq", [64, 1]); gk = al("gk", [64, 1]); epsb = al("eps", [64, 1])
        onesb = al("onesb", [128, 64], BF16)
        raw = [al("raw%d" % i, [64, T]) for i in range(2)]
        kT = [al("kT%d" % i, [64, T], BF16) for i in range(2)]
        qT = [al("qT%d" % i, [64, T], BF16) for i in range(2)]
        vraw = al("vraw", [128, NKB, 128])
        vb = al("vb", [128, NKB, 128], BF16)
        sq = al("sq", [64, 512]); rs = al("rs", [64, 512]); kn = al("kn", [64, 512]); t1 = al("t1", [64, 512]); t2 = al("t2", [64, 512])
        pT = [al("pT%d" % i, [128, 512], BF16) for i in range(3)]
        rden = al("rden", [64, 512])
        ot = [al("ot%d" % i, [64, 512], BF16) for i in range(2)]
        psa = es.enter_context(nc.psum_tensor("a_psa", [64, 512], F32))
        psr = es.enter_context(nc.psum_tensor("a_psr", [64, 512], F32))
        pss = [es.enter_context(nc.psum_tensor("a_pss%d" % i, [128, 512], F32)) for i in range(3)]
        pso = es.enter_context(nc.psum_tensor("a_pso", [64, 512], F32))
        psd = es.enter_context(nc.psum_tensor("a_psd", [64, 512], F32))
        P.dma(lambda e: e.dma_start(out=cos[:], in_=k.rope_cos), writes=["cos"])
        P.dma(lambda e: e.dma_start(out=sin[:], in_=k.rope_sin), writes=["sin"])
        P.dma(lambda e: e.dma_start(out=rm[:], in_=k.rope_rm), writes=["rm"])
        P.dma(lambda e: e.dma_start(out=gq[:], in_=k.at_qn_g[l]), writes=["gq"])
        P.dma(lambda e: e.dma_start(out=gk[:], in_=k.at_kn_g[l]), writes=["gk"])
        P.op("dve", lambda e: e.memset(o64[:], 1.0 / 64), writes=["o64"])
        P.op("dve", lambda e: e.memset(epsb[:], EPS), writes=["epsb"])
        P.op("dve", lambda e: e.memset(onesb[:], 1.0), writes=["onesb"])
        cnt = {"raw": 0, "p": 0, "o": 0}

        def prep(b, row0, g_t, gtok, dst, dtok):
            ri = cnt["raw"] % 2
            cnt["raw"] += 1
            P.dma(lambda e: e.dma_start(out=raw[ri][:], in_=k.PT[b, row0:row0 + 64, :]), reads=[("PT", b)], writes=[("raw", ri)])
            for (t0, n) in [(0, CTX)] + [(CTX + i * 512, 512) for i in range(4)]:
                P.op("act", lambda e, t0=t0, n=n: e.activation(out=sq[:, :n], in_=raw[ri][:, t0:t0 + n], func=AF.Square), reads=[("raw", ri)], writes=["sq"])
                P.op("pe", lambda e, n=n: e.matmul(psa[:, :n], lhsT=o64[:], rhs=sq[:, :n], start=True, stop=True), reads=["sq", "o64"], writes=["psa"])
                P.op("act", lambda e, n=n: e.activation(out=rs[:, :n], in_=psa[:, :n], func=AF.Sqrt, bias=epsb[:, 0:1], scale=1.0), reads=["psa", "epsb"], writes=["rs0", "rs"])
                P.op("dve", lambda e, n=n: e.reciprocal(out=rs[:, :n], in_=rs[:, :n]), reads=["rs0"], writes=["rs"])
                if t0 == 0:
                    P.op("dve", lambda e, t0=t0, n=n: e.scalar_tensor_tensor(out=dst[:, t0:t0 + n], in0=raw[ri][:, t0:t0 + n], scalar=g_t[:, 0:1], in1=rs[:, :n], op0=ALU.mult, op1=ALU.mult),
                         reads=[("raw", ri), "rs", gtok], writes=[dtok])
                    continue
                P.op("dve", lambda e, t0=t0, n=n: e.scalar_tensor_tensor(out=kn[:, :n], in0=raw[ri][:, t0:t0 + n], scalar=g_t[:, 0:1], in1=rs[:, :n], op0=ALU.mult, op1=ALU.mult),
                     reads=[("raw", ri), "rs", gtok], writes=["kn"])
                P.op("pe", lambda e, n=n: e.matmul(psr[:, :n], lhsT=rm[:], rhs=kn[:, :n], start=True, stop=True), reads=["kn", "rm"], writes=["psr"])
                P.op("pool", lambda e, t0=t0, n=n: e.tensor_tensor(out=t1[:, :n], in0=kn[:, :n], in1=cos[:, t0 - CTX:t0 - CTX + n], op=ALU.mult), reads=["kn", "cos"], writes=["t1"])
                P.op("dve", lambda e, t0=t0, n=n: e.tensor_tensor(out=t2[:, :n], in0=psr[:, :n], in1=sin[:, t0 - CTX:t0 - CTX + n], op=ALU.mult), reads=["psr", "sin"], writes=["t2"])
                P.op("dve", lambda e, t0=t0, n=n: e.tensor_tensor(out=dst[:, t0:t0 + n], in0=t1[:, :n], in1=t2[:, :n], op=ALU.add), reads=["t1", "t2"], writes=[dtok])

        def do_tile(b, hq, kvh, qi, q0, nq, kbs):
            def s_mm(kb, pi):
                P.op("pe", lambda e: e.matmul(pss[pi][:, :nq], lhsT=kT[kvh][:, kb * 128:(kb + 1) * 128], rhs=qT[qi][:, q0:q0 + nq], start=True, stop=True),
                     reads=[("kT", kvh), ("qT", qi)], writes=[("pss", pi)])
            pis = []
            for j in range(len(kbs)):
                pis.append(cnt["p"] % 3)
                cnt["p"] += 1
            s_mm(kbs[0], pis[0])
            for j, kb in enumerate(kbs):
                if j + 1 < len(kbs):
                    s_mm(kbs[j + 1], pis[j + 1])
                pi = pis[j]
                P.op("act", lambda e, pi=pi: e.activation(out=pT[pi][:, :nq], in_=pss[pi][:, :nq], func=AF.Exp, scale=0.125), reads=[("pss", pi)], writes=[("pT", pi)])
                P.op("pe", lambda e, pi=pi, kb=kb, j=j: e.matmul(pso[:, :nq], lhsT=vb[:, kb, kvh * 64:(kvh + 1) * 64], rhs=pT[pi][:, :nq], start=(j == 0), stop=(j == len(kbs) - 1)),
                     reads=[("pT", pi), "vb"], writes=["pso"], inc=False)
                P.op("pe", lambda e, pi=pi, j=j: e.matmul(psd[:, :nq], lhsT=onesb[:], rhs=pT[pi][:, :nq], start=(j == 0), stop=(j == len(kbs) - 1)),
                     reads=[("pT", pi), "onesb"], writes=["psd"])
            oi = cnt["o"] % 2
            cnt["o"] += 1
            P.op("dve", lambda e: e.reciprocal(out=rden[:, :nq], in_=psd[:, :nq]), reads=["psd"], writes=["rden"])
            P.op("dve", lambda e: e.tensor_tensor(out=ot[oi][:, :nq], in0=pso[:, :nq], in1=rden[:, :nq], op=ALU.mult), reads=["pso", "rden"], writes=[("ot", oi)])
            P.dma(lambda e: e.dma_start(out=k.YT[b, 3, hq * 64:(hq + 1) * 64, q0:q0 + nq], in_=ot[oi][:, :nq]), reads=[("ot", oi)], writes=[("YT", b)])

        for b in range(NB):
            for kvh in range(2):
                prep(b, FM_AK + kvh * 64, gk, "gk", kT[kvh], ("kT", kvh))
            vsrc = k.TOK[b].rearrange("(blk p) c -> p blk c", p=128)[:, :, TK_AV:TK_AV + 128]
            P.dma(lambda e, vsrc=vsrc: e.dma_start(out=vraw[:], in_=vsrc), reads=[("TOK", b)], writes=["vraw"])
            P.op("pool", lambda e: e.tensor_copy(out=vb[:], in_=vraw[:]), reads=["vraw"], writes=["vb"])
            for hq in range(4):
                kvh = hq // 2
                qi = hq % 2
                prep(b, FM_AQ + hq * 64, gq, "gq", qT[qi], ("qT", qi))
                for (q0, nq, kbs) in [(0, CTX, [0, 1])] + [(CTX + i * 512, 512, list(range(NKB))) for i in range(4)]:
                    do_tile(b, hq, kvh, qi, q0, nq, kbs)
        P.end_stage()


def stage_hgrn2(k, l):
    nc, P, NB = k.nc, k.P, k.NB
    NCK = T // 64
    with ExitStack() as es:
        al = lambda name, shape, dt=F32: es.enter_context(nc.sbuf_tensor("h_" + name, list(shape), dt))
        m01 = al("m01", [128, T]); mk = al("mk", [128, 2, 64]); bd = al("bd", [128, 128])
        lg = al("lg", [128, 2, 2]); lb = al("lb", [128, 2]); oml = al("oml", [128, 2]); gn = al("gn", [128, 1]); epsb = al("eps", [128, 1])
        z = al("z", [128, T]); fgt = al("fgt", [128, T]); bb = al("bb", [128, T]); tmp = al("tmp", [128, T])
        ex = [al("ex%d" % i, [128, T]) for i in range(2)]
        q = al("q", [128, T]); kk = al("kk", [128, T]); kd = al("kd", [128, T]); O = al("O", [128, T])
        qt = al("qt", [128, T], BF16); ktI = [al("kt%d" % i, [128, T], BF16) for i in range(4)]; qd = al("qd", [128, T], BF16)
        dec = al("dec", [128, NCK, 1])
        Vb = al("Vb", [128, NCK, 256], BF16)
        kdT = [al("kdT%d" % i, [64, 128], BF16) for i in range(2)]
        scm = [al("scm%d" % i, [128, 64], BF16) for i in range(2)]
        S32 = al("S32", [128, 64]); S16 = [al("S16%d" % i, [128, 64], BF16) for i in range(2)]
        sq = al("sq", [128, 512]); rs = al("rs", [128, 512]); yb = [al("yb%d" % i, [128, 512], BF16) for i in range(2)]
        pst = [es.enter_context(nc.psum_tensor("h_pst%d" % i, [128, 512], F32)) for i in range(2)]
        pss = [es.enter_context(nc.psum_tensor("h_pss%d" % i, [128, 512], F32)) for i in range(2)]
        pso = [es.enter_context(nc.psum_tensor("h_pso%d" % i, [128, 512], F32)) for i in range(2)]
        pskv = [es.enter_context(nc.psum_tensor("h_pskv%d" % i, [128, 512], F32)) for i in range(2)]
        P.dma(lambda e: e.dma_start(out=mk[:], in_=k.hg_masks), writes=["mk"])
        P.dma(lambda e: e.dma_start(out=bd[:], in_=k.bd64), writes=["bd"])
        P.dma(lambda e: e.dma_start(out=lg[:], in_=k.hg_lb), writes=["lg"])
        P.dma(lambda e: e.dma_start(out=gn[:], in_=k.hg_norm_g[l]), writes=["gn"])
        P.op("dve", lambda e: e.memset(epsb[:], EPS), writes=["epsb"])
        P.op("dve", lambda e: e.memset(m01[:], 1.0), writes=["m01"])
        P.op("dve", lambda e: e.memset(m01[:].rearrange("p (c j) -> p c j", j=64)[:, :, 0:1], 0.0), writes=["m01"])
        for i4 in range(4):
            P.op("pool", lambda e, i4=i4: e.memset(ktI[i4][:], 0.0), writes=[("kt", i4)])
        if l == 0:
            P.op("dve", lambda e: e.memset(lb[:], 0.0), writes=["lb"])
            P.op("dve", lambda e: e.memset(oml[:], 1.0), writes=["oml"])
        else:
            P.op("dve", lambda e: e.tensor_tensor(out=lb[:], in0=lg[:, :, 1], in1=lg[:, :, 0], op=ALU.subtract), reads=["lg"], writes=["lb0"])
            P.op("act", lambda e: e.activation(out=lb[:], in_=lb[:], func=AF.Sigmoid), reads=["lb0"], writes=["lb", "lb0"])
            P.op("dve", lambda e: e.tensor_scalar(out=oml[:], in0=lb[:], scalar1=-1.0, scalar2=1.0, op0=ALU.mult, op1=ALU.add), reads=["lb"], writes=["oml"])
        cnt = {"c": 0, "y": 0}
        bb3 = bb[:].rearrange("p (c j) -> p c j", j=64)
        tmp3 = tmp[:].rearrange("p (c j) -> p c j", j=64)

        def do_chunk(b, hp, d, c, first):
            i = cnt["c"] % 2
            cnt["c"] += 1
            cs = slice(c * 64, (c + 1) * 64)
            P.op("pe", lambda e: e.transpose(pst[i][:64, :128], kd[:, cs], k.idn[:]), reads=["kd", "idn"], writes=[("pst", i)])
            P.op("act", lambda e: e.activation(out=kdT[i][:], in_=pst[i][:64, :128], func=AF.Identity), reads=[("pst", i)], writes=[("kdT", i)])
            for h2 in range(2):
                pb = h2 * 64
                for I in range(4):
                    ts_ = slice(c * 64 + 16 * I, c * 64 + 16 * I + 16)
                    P.op("pe", lambda e, pb=pb, I=I, ts_=ts_: e.matmul(pss[i][pb:pb + 64, 16 * I:16 * I + 16], lhsT=ktI[I][pb:pb + 64, cs], rhs=qt[pb:pb + 64, ts_], start=True, stop=True),
                         reads=[("kt", I), "qt"], writes=[("pss", i)], inc=(h2 == 1 and I == 3))
            for h2 in range(2):
                pb = h2 * 64
                h = hp * 2 + h2
                P.op("pe", lambda e, pb=pb, h=h: e.matmul(pskv[i][pb:pb + 64, :64], lhsT=kdT[i][:, pb:pb + 64], rhs=Vb[0:64, c, h * 64:(h + 1) * 64], start=True, stop=True),
                     reads=[("kdT", i), "Vb"], writes=[("pskv", i)], inc=(h2 == 1))
            P.op("dve", lambda e: e.tensor_tensor(out=scm[i][:], in0=pss[i][:, :64], in1=mk[:, d, :], op=ALU.mult), reads=[("pss", i), "mk"], writes=[("scm", i)])
            for h2 in range(2):
                pb = h2 * 64
                h = hp * 2 + h2
                P.op("pe", lambda e, pb=pb, h=h: e.matmul(pso[i][pb:pb + 64, :64], lhsT=Vb[pb:pb + 64, c, h * 64:(h + 1) * 64], rhs=scm[i][pb:pb + 64, :], start=True, stop=False),
                     reads=[("scm", i), "Vb"], writes=[("pso", i)], inc=False)
                P.op("pe", lambda e, pb=pb: e.matmul(pso[i][pb:pb + 64, :64], lhsT=S16[1 - i][pb:pb + 64, :], rhs=qd[pb:pb + 64, cs], start=False, stop=True),
                     reads=[("S16", 1 - i), "qd"], writes=[("pso", i)], inc=(h2 == 1))
            if d == 0:
                P.op("act", lambda e: e.activation(out=O[:, cs], in_=pso[i][:, :64], func=AF.Identity), reads=[("pso", i)], writes=["O"])
            else:
                P.op("dve", lambda e: e.tensor_tensor(out=O[:, cs], in0=O[:, cs], in1=pso[i][:, :64], op=ALU.add), reads=[("pso", i), "O"], writes=["O"])
            P.op("dve", lambda e: e.scalar_tensor_tensor(out=S32[:], in0=S32[:], scalar=dec[:, c, :], in1=pskv[i][:, :64], op0=ALU.mult, op1=ALU.add),
                 reads=["S32", "dec", ("pskv", i)], writes=["S32"])
            P.op("act", lambda e: e.activation(out=S16[i][:], in_=S32[:], func=AF.Identity), reads=["S32"], writes=[("S16", i)])

        def do_dir(b, hp, d):
            rev = (lambda ap: ap[:, ::-1]) if d == 1 else (lambda ap: ap)
            last = 63 if d == 0 else 0
            r0 = FM_HF + d * 256 + hp * 128
            P.dma(lambda e: e.dma_start(out=z[:], in_=k.PT[b, r0:r0 + 128, :]), reads=[("PT", b)], writes=["z"])
            P.op("act", lambda e: e.activation(out=fgt[:], in_=z[:], func=AF.Sigmoid), reads=["z"], writes=["fgt"])
            P.op("dve", lambda e: e.tensor_scalar(out=fgt[:], in0=fgt[:], scalar1=oml[:, hp:hp + 1], scalar2=lb[:, hp:hp + 1], op0=ALU.mult, op1=ALU.add), reads=["fgt", "oml", "lb"], writes=["fgt"])
            P.op("dve", lambda e: e.tensor_scalar(out=fgt[:], in0=fgt[:], scalar1=1e-30, scalar2=None, op0=ALU.max), reads=["fgt"], writes=["fgt"])
            P.op("act", lambda e: e.activation(out=z[:], in_=fgt[:], func=AF.Ln), reads=["fgt"], writes=["z"])
            P.op("dve", lambda e: e.tensor_scalar(out=kk[:], in0=fgt[:], scalar1=-1.0, scalar2=1.0, op0=ALU.mult, op1=ALU.add), reads=["fgt"], writes=["kk"])
            P.op("dve", lambda e: e.tensor_tensor_scan(out=rev(bb[:]), data0=m01[:], data1=rev(z[:]), initial=0.0, op0=ALU.mult, op1=ALU.add), reads=["z", "m01"], writes=["bb"])
            ref = 0 if d == 0 else 15
            bb4 = bb[:].rearrange("p (c i j) -> p c i j", i=4, j=16)
            tmp4 = tmp[:].rearrange("p (c i j) -> p c i j", i=4, j=16)
            P.op("dve", lambda e: e.tensor_tensor(out=tmp4, in0=bb4, in1=bb4[:, :, :, ref:ref + 1].to_broadcast([128, NCK, 4, 16]), op=ALU.subtract), reads=["bb"], writes=["tmp"])
            P.op("act", lambda e: e.activation(out=ex[0][:], in_=tmp[:], func=AF.Exp), reads=["tmp"], writes=[("ex", 0)])
            P.op("pool", lambda e: e.tensor_tensor(out=qt[:], in0=q[:], in1=ex[0][:], op=ALU.mult), reads=["q", ("ex", 0)], writes=["qt"])
            ex3 = [ex[i][:].rearrange("p (c j) -> p c j", j=64) for i in range(2)]
            kk3 = kk[:].rearrange("p (c j) -> p c j", j=64)
            for I in range(4):
                cs_ = slice(0, 16 * (I + 1)) if d == 0 else slice(16 * I, 64)
                w = cs_.stop - cs_.start
                e_ = ex3[I % 2][:, :, cs_]
                rp = 16 * I + ref
                kt3 = ktI[I][:].rearrange("p (c j) -> p c j", j=64)[:, :, cs_]
                P.op("dve", lambda e, cs_=cs_, w=w, rp=rp: e.scalar_tensor_tensor(out=tmp3[:, :, cs_], in0=bb3[:, :, cs_], scalar=-1.0, in1=bb3[:, :, rp:rp + 1].to_broadcast([128, NCK, w]), op0=ALU.mult, op1=ALU.add),
                     reads=["bb"], writes=["tmp"])
                P.op("dve", lambda e, cs_=cs_: e.tensor_scalar(out=tmp3[:, :, cs_], in0=tmp3[:, :, cs_], scalar1=60.0, scalar2=None, op0=ALU.min), reads=["tmp"], writes=["tmp"])
                P.op("act", lambda e, cs_=cs_, e_=e_: e.activation(out=e_, in_=tmp3[:, :, cs_], func=AF.Exp), reads=["tmp"], writes=[("ex", I % 2)])
                P.op("pool", lambda e, cs_=cs_, e_=e_, kt3=kt3: e.tensor_tensor(out=kt3, in0=kk3[:, :, cs_], in1=e_, op=ALU.mult), reads=["kk", ("ex", I % 2)], writes=[("kt", I)])
            P.op("act", lambda e: e.activation(out=ex[0][:], in_=bb[:], func=AF.Exp), reads=["bb"], writes=[("ex", 0)])
            P.op("dve", lambda e: e.tensor_tensor(out=qd[:], in0=q[:], in1=ex[0][:], op=ALU.mult), reads=["q", ("ex", 0)], writes=["qd"])
            P.op("dve", lambda e: e.tensor_tensor(out=tmp3, in0=bb3, in1=bb3[:, :, last:last + 1].to_broadcast([128, NCK, 64]), op=ALU.subtract), reads=["bb"], writes=["tmp"])
            P.op("act", lambda e: e.activation(out=ex[1][:], in_=tmp[:], func=AF.Exp, scale=-1.0), reads=["tmp"], writes=[("ex", 1)])
            P.op("pool", lambda e: e.tensor_tensor(out=kd[:], in0=kk[:], in1=ex[1][:], op=ALU.mult), reads=["kk", ("ex", 1)], writes=["kd"])
            P.op("act", lambda e: e.activation(out=dec[:], in_=bb3[:, :, last:last + 1], func=AF.Exp), reads=["bb"], writes=["dec"])
            if "dump" in k.opts and (b, hp, d) == k.opts["dump"]:
                for j, (tl, tk) in enumerate([(z, "z"), (bb, "bb"), (kd, "kd"), (kk, "kk"), (fgt, "fgt")]):
                    P.dma(lambda e, j=j, tl=tl: e.dma_start(out=k.dump[j], in_=tl[:]), reads=[tk], writes=[("dump", j)])
            P.op("dve", lambda e: e.memset(S32[:], 0.0), writes=["S32"])
            for i in range(2):
                P.op("dve", lambda e, i=i: e.memset(S16[i][:], 0.0), writes=[("S16", i)])
            order = list(range(NCK)) if d == 0 else [3, 2, 1, 0] + list(range(NCK - 1, 3, -1))
            for idx, c in enumerate(order):
                do_chunk(b, hp, d, c, idx == 0)

        def do_pair(b, hp):
            r0 = FM_HQ + hp * 128
            P.dma(lambda e: e.dma_start(out=q[:], in_=k.PT[b, r0:r0 + 128, :]), reads=[("PT", b)], writes=["q"])
            P.op("act", lambda e: e.activation(out=q[:], in_=q[:], func=AF.Silu), reads=["q"], writes=["q"])
            for d in range(2):
                do_dir(b, hp, d)
            if "dump" in k.opts and (b, hp) == k.opts["dump"][:2]:
                P.dma(lambda e: e.dma_start(out=k.dump[5], in_=O[:]), reads=["O"], writes=[("dump", 5)])
            g0 = FM_HG + hp * 128
            P.dma(lambda e: e.dma_start(out=z[:], in_=k.PT[b, g0:g0 + 128, :]), reads=[("PT", b)], writes=["z"])
            P.op("act", lambda e: e.activation(out=z[:], in_=z[:], func=AF.Sigmoid), reads=["z"], writes=["z"])
            for (t0, n) in [(0, CTX)] + [(CTX + i * 512, 512) for i in range(4)]:
                do_out(b, hp, t0, n)

        def do_out(b, hp, t0, n):
            yi = cnt["y"] % 2
            cnt["y"] += 1
            P.op("act", lambda e: e.activation(out=sq[:, :n], in_=O[:, t0:t0 + n], func=AF.Square), reads=["O"], writes=["sq"])
            P.op("pe", lambda e: e.matmul(pss[0][:, :n], lhsT=bd[:], rhs=sq[:, :n], start=True, stop=True), reads=["sq", "bd"], writes=[("pss", 0)])
            P.op("act", lambda e: e.activation(out=rs[:, :n], in_=pss[0][:, :n], func=AF.Sqrt, bias=epsb[:, 0:1], scale=1.0), reads=[("pss", 0), "epsb"], writes=["rs0", "rs"])
            P.op("dve", lambda e: e.reciprocal(out=rs[:, :n], in_=rs[:, :n]), reads=["rs0"], writes=["rs"])
            P.op("dve", lambda e: e.scalar_tensor_tensor(out=sq[:, :n], in0=O[:, t0:t0 + n], scalar=gn[:, 0:1], in1=rs[:, :n], op0=ALU.mult, op1=ALU.mult), reads=["O", "gn", "rs"], writes=["sq"])
            P.op("pool", lambda e: e.tensor_tensor(out=yb[yi][:, :n], in0=sq[:, :n], in1=z[:, t0:t0 + n], op=ALU.mult), reads=["sq", "z"], writes=[("yb", yi)])
            P.dma(lambda e: e.dma_start(out=k.YT[b, 2, hp * 128:(hp + 1) * 128, t0:t0 + n], in_=yb[yi][:, :n]), reads=[("yb", yi)], writes=[("YT", b)])

        for b in range(NB):
            vsrc = k.TOK[b].rearrange("(c s) v -> s c v", s=64)[:, :, TK_HV:TK_HV + 256]
            for h2 in range(2):
                P.dma(lambda e, h2=h2, vsrc=vsrc: e.dma_start(out=Vb[h2 * 64:(h2 + 1) * 64, :, :], in_=vsrc), reads=[("TOK", b)], writes=["Vb"], q="pool")
            for hp in range(2):
                do_pair(b, hp)
        P.end_stage()


class Slots:
    def __init__(self, aps, name, mod=0):
        self.aps, self.name, self.i, self.mod = aps, name, 0, mod

    def get(self):
        j = self.i % len(self.aps)
        self.i += 1
        return self.aps[j], (self.name, j % self.mod if self.mod else j)


def stage_deltanet(k, l):
    nc, P, NB = k.nc, k.P, k.NB
    NCK = T // 64
    P.serial = k.opts.get("serial", True)
    with ExitStack() as es:
        al = lambda name, shape, dt=F32: es.enter_context(nc.sbuf_tensor("d_" + name, list(shape), dt))
        gm64 = al("gm64", [64, 2, 2, 128]); gmbd = al("gmbd", [128, 2, 2, 128]); idn2 = al("idn2", [128, 64]); bd = al("bd", [128, 128])
        ones64 = al("ones64", [64, 128])
        cw = al("cw", [128, 6, 5]); alog = al("alog", [128, 8]); dtb = al("dtb", [128, 8]); gno = al("gno", [128, 64]); epsb = al("eps", [128, 1])
        qkv = [al("qkv%d" % i, [128, T]) for i in range(6)]
        xin = al("xin", [128, T]); acc = al("acc", [128, T])
        sq = al("sq", [128, 512]); rs = al("rs", [128, 512])
        ab = al("ab", [128, NCK, 16]); gt = al("gt", [128, NCK, 8]); bt = al("bt", [128, NCK, 8]); t8 = al("t8", [128, NCK, 8]); t8b = al("t8b", [128, NCK, 8])
        gcs = al("gcs", [128, NCK, 8]); gts = al("gts", [128, NCK, 8])
        gcBD = al("gcBD", [128, NCK, 4]); gtBD = al("gtBD", [128, NCK, 4]); bBD = al("bBD", [128, NCK, 4])
        eg = al("eg", [128, NCK, 4]); ekd = al("ekd", [128, NCK, 4]); glast = al("glast", [128, NCK, 4]); nbeta = al("nbeta", [128, NCK, 4]); wsc = al("wsc", [128, NCK, 4])
        u_all = al("u_all", [128, NCK, 64]); wT_all = al("wT_all", [128, NCK, 64]); kdec_all = al("kdec_all", [128, NCK, 64]); attnT_all = al("attnT_all", [128, NCK, 128])
        O_tok = al("O_tok", [128, NCK, 64]); ssq = al("ssq", [128, NCK]); S = al("S", [128, 64])
        yb = [al("yb%d" % i, [128, 512], BF16) for i in range(2)]
        wk = Slots([al("wk%d" % i, [128, 128])[:] for i in range(28)], "wk")
        wv = Slots([al("wv%d" % i, [128, 64])[:] for i in range(8)], "wv")
        wr = Slots([al("wr%d" % i, [128, 128])[:] for i in range(6)], "wr")
        banks = [es.enter_context(nc.psum_tensor("d_ps%d" % i, [128, 512], F32)) for i in range(8)]
        ps = Slots([banks[i][:, j * 128:(j + 1) * 128] for j in range(k.opts.get("psj", 4)) for i in range(7)], "ps", mod=7)
        psg = banks[7]
        for i in range(8):
            P.op("dve", lambda e, i=i: e.memset(banks[i][:], 0.0), writes=[("ps", j) for j in range(7)] + ["psg"])
        P.dma(lambda e: e.dma_start(out=gm64[:], in_=k.dn_gm64), writes=["gm64"])
        P.dma(lambda e: e.dma_start(out=gmbd[:], in_=k.dn_gmbd), writes=["gmbd"])
        P.dma(lambda e: e.dma_start(out=idn2[:], in_=k.dn_idn2), writes=["idn2"])
        P.dma(lambda e: e.dma_start(out=bd[:], in_=k.bd64), writes=["bd"])
        P.dma(lambda e: e.dma_start(out=cw[:], in_=k.dn_conv[l]), writes=["cw"])
        P.dma(lambda e: e.dma_start(out=alog[:], in_=k.dn_a_log[l]), writes=["alog"])
        P.dma(lambda e: e.dma_start(out=dtb[:], in_=k.dn_dt_bias[l]), writes=["dtb"])
        P.dma(lambda e: e.dma_start(out=gno[:], in_=k.dn_norm_g[l]), writes=["gno"])
        P.op("dve", lambda e: e.memset(epsb[:], EPS), writes=["epsb"])
        P.op("dve", lambda e: e.memset(ones64[:], 1.0), writes=["ones64"])
        P.op("act", lambda e: e.activation(out=alog[:], in_=alog[:], func=AF.Exp), reads=["alog"], writes=["alog"])
        P.op("dve", lambda e: e.tensor_scalar(out=alog[:], in0=alog[:], scalar1=-1.0, scalar2=None, op0=ALU.mult), reads=["alog"], writes=["alog"])
        cnt = {"y": 0}

        def prep_tile(b, ti):
            r0 = ti * 128
            P.dma(lambda e: e.dma_start(out=xin[:], in_=k.PT[b, r0:r0 + 128, :]), reads=[("PT", b)], writes=["xin"])
            P.op("dve", lambda e: e.tensor_scalar(out=acc[:], in0=xin[:], scalar1=cw[:, ti, 2:3], scalar2=None, op0=ALU.mult), reads=["xin", "cw"], writes=["acc"])
            for (s0, s1) in [(0, CTX), (CTX, T)]:
                for j in (0, 1, 3, 4):
                    sh = j - 2
                    o0, o1 = max(s0, s0 - sh), min(s1, s1 - sh)
                    P.op("dve", lambda e, j=j, sh=sh, o0=o0, o1=o1: e.scalar_tensor_tensor(out=acc[:, o0:o1], in0=xin[:, o0 + sh:o1 + sh], scalar=cw[:, ti, j:j + 1], in1=acc[:, o0:o1], op0=ALU.mult, op1=ALU.add),
                         reads=["xin", "cw", "acc"], writes=["acc"])
            dst = qkv[ti]
            if ti >= 4:
                P.op("act", lambda e: e.activation(out=dst[:], in_=acc[:], func=AF.Silu), reads=["acc"], writes=[("qkv", ti)])
                return
            P.op("act", lambda e: e.activation(out=acc[:], in_=acc[:], func=AF.Silu), reads=["acc"], writes=["acc"])
            qs = 0.125 if ti < 2 else 1.0
            for (t0, n) in [(0, CTX)] + [(CTX + i * 512, 512) for i in range(4)]:
                norm_tile(dst, ti, t0, n, qs)

        def norm_tile(dst, ti, t0, n, qs):
            pp, pt = ps.get()
            bank_ap = banks[0]
            P.op("act", lambda e: e.activation(out=sq[:, :n], in_=acc[:, t0:t0 + n], func=AF.Square), reads=["acc"], writes=["sq"])
            P.op("pe", lambda e: e.matmul(psg[:, :n], lhsT=bd[:], rhs=sq[:, :n], start=True, stop=True), reads=["sq", "bd"], writes=["psg"])
            P.op("act", lambda e: e.activation(out=rs[:, :n], in_=psg[:, :n], func=AF.Sqrt, bias=epsb[:, 0:1], scale=64.0), reads=["psg", "epsb"], writes=["rs0", "rs"])
            P.op("dve", lambda e: e.reciprocal(out=rs[:, :n], in_=rs[:, :n]), reads=["rs0"], writes=["rs"])
            P.op("dve", lambda e: e.scalar_tensor_tensor(out=dst[:, t0:t0 + n], in0=acc[:, t0:t0 + n], scalar=qs, in1=rs[:, :n], op0=ALU.mult, op1=ALU.mult), reads=["acc", "rs"], writes=[("qkv", ti)])

        def prep_gates(b):
            src = k.TOK[b].rearrange("(c s) v -> s c v", s=64)[:, :, TK_A:TK_A + 16]
            for h2 in range(2):
                P.dma(lambda e, h2=h2: e.dma_start(out=ab[h2 * 64:(h2 + 1) * 64, :, :], in_=src), reads=[("TOK", b)], writes=["ab"])
            a3, b3 = ab[:, :, 0:8], ab[:, :, 8:16]
            bc8 = lambda t: t[:, :].rearrange("p (o h) -> p o h", o=1).to_broadcast([128, NCK, 8])
            P.op("dve", lambda e: e.tensor_tensor(out=t8[:], in0=a3, in1=bc8(dtb), op=ALU.add), reads=["ab", "dtb"], writes=["t8"])
            P.op("act", lambda e: e.activation(out=t8b[:], in_=t8[:], func=AF.Abs), reads=["t8"], writes=["t8b"])
            P.op("act", lambda e: e.activation(out=t8b[:], in_=t8b[:], func=AF.Exp, scale=-1.0), reads=["t8b"], writes=["t8b"])
            P.op("dve", lambda e: e.tensor_scalar(out=t8b[:], in0=t8b[:], scalar1=1.0, scalar2=None, op0=ALU.add), reads=["t8b"], writes=["t8b"])
            P.op("act", lambda e: e.activation(out=t8b[:], in_=t8b[:], func=AF.Ln), reads=["t8b"], writes=["t8b"])
            P.op("dve", lambda e: e.tensor_scalar(out=t8[:], in0=t8[:], scalar1=0.0, scalar2=None, op0=ALU.max), reads=["t8"], writes=["t8"])
            P.op("dve", lambda e: e.tensor_tensor(out=t8[:], in0=t8[:], in1=t8b[:], op=ALU.add), reads=["t8", "t8b"], writes=["t8"])
            P.op("dve", lambda e: e.tensor_tensor(out=gt[:], in0=t8[:], in1=bc8(alog), op=ALU.mult), reads=["t8", "alog"], writes=["gt"])
            P.op("act", lambda e: e.activation(out=bt[:], in_=b3, func=AF.Sigmoid), reads=["ab"], writes=["bt"])
            for d in range(2):
                P.op("pe", lambda e, d=d: e.matmul(psg[:, d * 144:(d + 1) * 144], lhsT=gm64[:, d, 0, :], rhs=gt[0:64, :, d * 4:(d + 1) * 4], start=True, stop=True), reads=["gm64", "gt"], writes=["psg"])
            P.op("dve", lambda e: e.tensor_copy(out=gcs[:].rearrange("p c (d h) -> p d c h", d=2), in_=psg[:, 0:288].rearrange("p (d c h) -> p d c h", d=2, h=4)), reads=["psg"], writes=["gcs"])
            for d in range(2):
                P.op("pe", lambda e, d=d: e.matmul(psg[:, d * 144:(d + 1) * 144], lhsT=ones64[:], rhs=gt[0:64, :, d * 4:(d + 1) * 4], start=True, stop=True), reads=["ones64", "gt"], writes=["psg"])
            P.op("dve", lambda e: e.tensor_copy(out=gts[:].rearrange("p c (d h) -> p d c h", d=2), in_=psg[:, 0:288].rearrange("p (d c h) -> p d c h", d=2, h=4)), reads=["psg"], writes=["gts"])
            for m in range(4):
                d, hp = m // 2, m % 2
                for h2 in range(2):
                    col = d * 4 + hp * 2 + h2
                    rr = slice(h2 * 64, (h2 + 1) * 64)
                    P.op("dve", lambda e, m=m, col=col, rr=rr: e.tensor_copy(out=gcBD[rr, :, m:m + 1], in_=gcs[rr, :, col:col + 1]), reads=["gcs"], writes=["gcBD"])
                    P.op("dve", lambda e, m=m, col=col, rr=rr: e.tensor_copy(out=gtBD[rr, :, m:m + 1], in_=gts[rr, :, col:col + 1]), reads=["gts"], writes=["gtBD"])
                    P.op("dve", lambda e, m=m, col=col, rr=rr: e.tensor_copy(out=bBD[rr, :, m:m + 1], in_=bt[rr, :, col:col + 1]), reads=["bt"], writes=["bBD"])
            P.op("act", lambda e: e.activation(out=eg[:], in_=gcBD[:], func=AF.Exp), reads=["gcBD"], writes=["eg"])
            P.op("act", lambda e: e.activation(out=glast[:], in_=gtBD[:], func=AF.Exp), reads=["gtBD"], writes=["glast"])
            P.op("dve", lambda e: e.tensor_tensor(out=ekd[:], in0=gtBD[:], in1=gcBD[:], op=ALU.subtract), reads=["gtBD", "gcBD"], writes=["ekd"])
            P.op("act", lambda e: e.activation(out=ekd[:], in_=ekd[:], func=AF.Exp), reads=["ekd"], writes=["ekd"])
            P.op("dve", lambda e: e.tensor_scalar(out=nbeta[:], in0=bBD[:], scalar1=-1.0, scalar2=None, op0=ALU.mult), reads=["bBD"], writes=["nbeta"])
            P.op("dve", lambda e: e.tensor_tensor(out=wsc[:], in0=bBD[:], in1=eg[:], op=ALU.mult), reads=["bBD", "eg"], writes=["wsc"])

        def phase_a_steps(m, c):
            d, hp = m // 2, m % 2
            qn, kn, vn = qkv[hp], qkv[2 + hp], qkv[4 + hp]
            qtk, ktk, vtk = ("qkv", hp), ("qkv", 2 + hp), ("qkv", 4 + hp)
            cs = slice(c * 64, (c + 1) * 64)
            hd0 = d * 4 + hp * 2
            X = {}

            def s1():
                Gm, Gmt = wk.get()
                Gi, Git = wk.get()
                g2 = gt[0:64, c, hd0:hd0 + 2].rearrange("p (h o) -> p h o", o=1).to_broadcast([64, 2, 64])
                P.op("dve", lambda e: e.tensor_tensor(out=Gm[0:64, :].rearrange("p (h j) -> p h j", h=2), in0=g2, in1=gm64[:, d, 1, :].rearrange("p (h j) -> p h j", h=2), op=ALU.mult), reads=["gt", "gm64"], writes=[Gmt])
                P.op("dve", lambda e: e.tensor_tensor(out=Gi[0:64, :].rearrange("p (h j) -> p h j", h=2), in0=g2, in1=gm64[:, d, 0, :].rearrange("p (h j) -> p h j", h=2), op=ALU.mult), reads=["gt", "gm64"], writes=[Git])
                pD, pDt = ps.get()
                pDT, pDTt = ps.get()
                P.op("pe", lambda e: e.matmul(pD, lhsT=gm64[:, d, 0, :], rhs=Gm[0:64, :], start=True, stop=True), reads=["gm64", Gmt], writes=[pDt], inc=False)
                P.op("pe", lambda e: e.matmul(pDT, lhsT=gm64[:, d, 1, :], rhs=Gi[0:64, :], start=True, stop=True), reads=["gm64", Git], writes=[pDTt])
                pKK, pKKt = ps.get()
                pQK, pQKt = ps.get()
                pTok, pTokt = ps.get()
                for h2 in range(2):
                    pb = h2 * 64
                    P.op("pe", lambda e, pb=pb: e.matmul(pKK[pb:pb + 64, pb:pb + 64], lhsT=kn[pb:pb + 64, cs], rhs=kn[pb:pb + 64, cs], start=True, stop=True), reads=[ktk], writes=[pKKt], inc=False)
                    P.op("pe", lambda e, pb=pb: e.matmul(pQK[pb:pb + 64, pb:pb + 64], lhsT=kn[pb:pb + 64, cs], rhs=qn[pb:pb + 64, cs], start=True, stop=True), reads=[ktk, qtk], writes=[pQKt], inc=False)
                    P.op("pe", lambda e, pb=pb: e.matmul(pTok[pb:pb + 64, 0:64], lhsT=kn[pb:pb + 64, cs], rhs=idn2[pb:pb + 64, :], start=True, stop=True), reads=[ktk, "idn2"], writes=[pTokt], inc=False)
                    P.op("pe", lambda e, pb=pb: e.matmul(pTok[pb:pb + 64, 64:128], lhsT=vn[pb:pb + 64, cs], rhs=idn2[pb:pb + 64, :], start=True, stop=True), reads=[vtk, "idn2"], writes=[pTokt], inc=(h2 == 1))
                X.update(pD=pD, pDt=pDt, pDT=pDT, pDTt=pDTt, pKK=pKK, pKKt=pKKt, pQK=pQK, pQKt=pQKt, pTok=pTok, pTokt=pTokt)

            def s2():
                x = dict(X)
                D, Dt = wk.get()
                DT, DTt = wk.get()
                X["n"] = X.get("n", 0) + 1
                if X["n"] <= k.opts.get("s2n", 99):
                    P.op("act", lambda e: e.activation(out=D, in_=x["pD"], func=AF.Exp), reads=[x["pDt"]], writes=[Dt])
                X["n"] = X.get("n", 0) + 1
                if X["n"] <= k.opts.get("s2n", 99):
                    P.op("act", lambda e: e.activation(out=DT, in_=x["pDT"], func=AF.Exp), reads=[x["pDTt"]], writes=[DTt])
                X["n"] = X.get("n", 0) + 1
                if X["n"] <= k.opts.get("s2n", 99):
                    P.op("pool", lambda e: e.tensor_tensor(out=D, in0=D, in1=gmbd[:, d, 0, :], op=ALU.mult), reads=[Dt, "gmbd"], writes=[Dt])
                X["n"] = X.get("n", 0) + 1
                if X["n"] <= k.opts.get("s2n", 99):
                    P.op("pool", lambda e: e.tensor_tensor(out=DT, in0=DT, in1=gmbd[:, d, 1, :], op=ALU.mult), reads=[DTt, "gmbd"], writes=[DTt])
                N, Nt = wk.get()
                X["n"] = X.get("n", 0) + 1
                if X["n"] <= k.opts.get("s2n", 99):
                    P.op("dve", lambda e: e.scalar_tensor_tensor(out=N, in0=x["pKK"], scalar=nbeta[:, c, m:m + 1], in1=D, op0=ALU.mult, op1=ALU.mult), reads=[x["pKKt"], "nbeta", Dt], writes=[Nt])
                X["n"] = X.get("n", 0) + 1
                if X["n"] <= k.opts.get("s2n", 99):
                    P.op("dve", lambda e: e.tensor_tensor(out=attnT_all[:, c, :], in0=x["pQK"], in1=DT, op=ALU.mult), reads=[x["pQKt"], DTt], writes=[("attnT", c)])
                rhs, rhst = wr.get()
                X["n"] = X.get("n", 0) + 1
                if X["n"] <= k.opts.get("s2n", 99):
                    P.op("dve", lambda e: e.tensor_scalar(out=rhs[:, 0:64], in0=x["pTok"][:, 0:64], scalar1=wsc[:, c, m:m + 1], scalar2=None, op0=ALU.mult), reads=[x["pTokt"], "wsc"], writes=[rhst])
                X["n"] = X.get("n", 0) + 1
                if X["n"] <= k.opts.get("s2n", 99):
                    P.op("dve", lambda e: e.tensor_scalar(out=rhs[:, 64:128], in0=x["pTok"][:, 64:128], scalar1=bBD[:, c, m:m + 1], scalar2=None, op0=ALU.mult), reads=[x["pTokt"], "bBD"], writes=[rhst])
                X["n"] = X.get("n", 0) + 1
                if X["n"] <= k.opts.get("s2n", 99):
                    P.op("dve", lambda e: e.tensor_scalar(out=kdec_all[:, c, :], in0=x["pTok"][:, 0:64], scalar1=ekd[:, c, m:m + 1], scalar2=None, op0=ALU.mult), reads=[x["pTokt"], "ekd"], writes=[("kdec", c)])
                X.update(N=N, Nt=Nt, rhs=rhs, rhst=rhst)

            def s3():
                x = dict(X)
                pNT, pNTt = ps.get()
                P.op("pe", lambda e: e.transpose(pNT, x["N"], k.idn[:]), reads=[x["Nt"], "idn"], writes=[pNTt])
                X.update(pNT=pNT, pNTt=pNTt)

            def s4():
                x = dict(X)
                PT_, PTt = wk.get()
                XT, XTt = wk.get()
                P.op("act", lambda e: e.activation(out=PT_, in_=x["pNT"], func=AF.Identity), reads=[x["pNTt"]], writes=[PTt])
                P.op("dve", lambda e: e.tensor_tensor(out=XT, in0=x["pNT"], in1=k.idn[:], op=ALU.add), reads=[x["pNTt"], "idn"], writes=[XTt])
                X.update(P=x["N"], Pt=x["Nt"], PT=PT_, PTt=PTt, XT=XT, XTt=XTt)

            def lvl_mm(kk):
                def f():
                    x = dict(X)
                    pP, pPt = ps.get()
                    P.op("pe", lambda e: e.matmul(pP, lhsT=x["PT"], rhs=x["P"], start=True, stop=True), reads=[x["PTt"], x["Pt"]], writes=[pPt], inc=(kk == 5))
                    X.update(pP=pP, pPt=pPt)
                    if kk < 5:
                        pPT, pPTt = ps.get()
                        P.op("pe", lambda e: e.matmul(pPT, lhsT=x["P"], rhs=x["PT"], start=True, stop=True), reads=[x["PTt"], x["Pt"]], writes=[pPTt])
                        X.update(pPT=pPT, pPTt=pPTt)
                return f

            def lvl_ev(kk):
                def f():
                    x = dict(X)
                    nP, nPt = wk.get()
                    P.op("act", lambda e: e.activation(out=nP, in_=x["pP"], func=AF.Identity), reads=[x["pPt"]], writes=[nPt])
                    X.update(P=nP, Pt=nPt)
                    if kk < 5:
                        nPT, nPTt = wk.get()
                        P.op("dve", lambda e: e.tensor_copy(out=nPT, in_=x["pPT"]), reads=[x["pPTt"]], writes=[nPTt])
                        X.update(PT=nPT, PTt=nPTt)
                    pX, pXt = ps.get()
                    P.op("pe", lambda e: e.matmul(pX, lhsT=nP, rhs=x["XT"], start=True, stop=True), reads=[nPt, x["XTt"]], writes=[pXt])
                    X.update(pX=pX, pXt=pXt)
                return f

            def lvl_acc(kk):
                def f():
                    x = dict(X)
                    nX, nXt = wk.get()
                    P.op("dve", lambda e: e.tensor_tensor(out=nX, in0=x["XT"], in1=x["pX"], op=ALU.add), reads=[x["XTt"], x["pXt"]], writes=[nXt])
                    X.update(XT=nX, XTt=nXt)
                return f

            def s_sol():
                x = dict(X)
                pU, pUt = ps.get()
                pW, pWt = ps.get()
                P.op("pe", lambda e: e.matmul(pU[:, 0:64], lhsT=x["XT"], rhs=x["rhs"][:, 64:128], start=True, stop=True), reads=[x["XTt"], x["rhst"]], writes=[pUt], inc=False)
                for h2 in range(2):
                    pb = h2 * 64
                    P.op("pe", lambda e, pb=pb: e.matmul(pW[pb:pb + 64, 0:64], lhsT=x["rhs"][pb:pb + 64, 0:64], rhs=x["XT"][pb:pb + 64, pb:pb + 64], start=True, stop=True), reads=[x["XTt"], x["rhst"]], writes=[pWt], inc=(h2 == 1))
                X.update(pU=pU, pUt=pUt, pW=pW, pWt=pWt)

            def s_solev():
                x = dict(X)
                P.op("act", lambda e: e.activation(out=u_all[:, c, :], in_=x["pU"][:, 0:64], func=AF.Identity), reads=[x["pUt"]], writes=[("u", c)])
                P.op("dve", lambda e: e.tensor_copy(out=wT_all[:, c, :], in_=x["pW"][:, 0:64]), reads=[x["pWt"]], writes=[("wT", c)])

            steps = [s1, s2, s3, s4]
            for kk in range(1, 6):
                steps += [lvl_mm(kk), lvl_ev(kk), lvl_acc(kk)]
            steps += [s_sol, s_solev]
            return steps[:k.opts.get("dn_steps", 99)]

        def phase_b_chunk(m, c, first_dir):
            d, hp = m // 2, m % 2
            qn = qkv[hp]
            cs = slice(c * 64, (c + 1) * 64)
            p1, p1t = ps.get()
            p2, p2t = ps.get()
            for h2 in range(2):
                pb = h2 * 64
                P.op("pe", lambda e, pb=pb: e.matmul(p1[pb:pb + 64, 0:64], lhsT=wT_all[pb:pb + 64, c, :], rhs=S[pb:pb + 64, :], start=True, stop=True), reads=[("wT", c), "S"], writes=[p1t], inc=False)
                P.op("pe", lambda e, pb=pb: e.matmul(p2[pb:pb + 64, 0:64], lhsT=qn[pb:pb + 64, cs], rhs=S[pb:pb + 64, :], start=True, stop=True), reads=[("qkv", hp), "S"], writes=[p2t], inc=(h2 == 1))
            vn_, vnt = wv.get()
            P.op("dve", lambda e: e.tensor_tensor(out=vn_, in0=u_all[:, c, :], in1=p1[:, 0:64], op=ALU.subtract), reads=[("u", c), p1t], writes=[vnt])
            p3, p3t = ps.get()
            p4, p4t = ps.get()
            P.op("pe", lambda e: e.matmul(p3[:, 0:64], lhsT=attnT_all[:, c, :], rhs=vn_, start=True, stop=True), reads=[("attnT", c), vnt], writes=[p3t], inc=False)
            for h2 in range(2):
                pb = h2 * 64
                P.op("pe", lambda e, pb=pb: e.matmul(p4[pb:pb + 64, 0:64], lhsT=kdec_all[pb:pb + 64, c, :], rhs=vn_[pb:pb + 64, :], start=True, stop=True), reads=[("kdec", c), vnt], writes=[p4t], inc=(h2 == 1))
            t_, tt_ = wv.get()
            P.op("dve", lambda e: e.tensor_scalar(out=t_, in0=p2[:, 0:64], scalar1=eg[:, c, m:m + 1], scalar2=None, op0=ALU.mult), reads=[p2t, "eg"], writes=[tt_])
            if first_dir:
                P.op("dve", lambda e: e.tensor_tensor(out=O_tok[:, c, :], in0=t_, in1=p3[:, 0:64], op=ALU.add), reads=[tt_, p3t], writes=[("O", c)])
            else:
                P.op("dve", lambda e: e.tensor_tensor(out=t_, in0=t_, in1=p3[:, 0:64], op=ALU.add), reads=[tt_, p3t], writes=[tt_])
                P.op("pool", lambda e: e.tensor_tensor(out=O_tok[:, c, :], in0=O_tok[:, c, :], in1=t_, op=ALU.add), reads=[tt_, ("O", c)], writes=[("O", c)])
            P.op("dve", lambda e: e.scalar_tensor_tensor(out=S[:], in0=S[:], scalar=glast[:, c, m:m + 1], in1=p4[:, 0:64], op0=ALU.mult, op1=ALU.add), reads=["S", "glast", p4t], writes=["S"])

        def out_phase(b, hp):
            z, yfm = xin, acc
            r0 = FM_Z + hp * 128
            P.dma(lambda e: e.dma_start(out=z[:], in_=k.PT[b, r0:r0 + 128, :]), reads=[("PT", b)], writes=["xin"])
            P.op("act", lambda e: e.activation(out=z[:], in_=z[:], func=AF.Silu), reads=["xin"], writes=["xin"])
            allO = [("O", c) for c in range(NCK)]
            P.op("dve", lambda e: e.tensor_tensor(out=u_all[:], in0=O_tok[:], in1=O_tok[:], op=ALU.mult), reads=allO, writes=[("u", c) for c in range(NCK)])
            P.op("dve", lambda e: e.tensor_reduce(out=ssq[:], in_=u_all[:], axis=AX.X, op=ALU.add), reads=[("u", c) for c in range(NCK)], writes=["ssq"])
            P.op("act", lambda e: e.activation(out=ssq[:], in_=ssq[:], func=AF.Sqrt, bias=epsb[:, 0:1], scale=1.0 / 64), reads=["ssq", "epsb"], writes=["ssq"])
            P.op("dve", lambda e: e.reciprocal(out=ssq[:], in_=ssq[:]), reads=["ssq"], writes=["ssq"])
            P.op("dve", lambda e: e.tensor_tensor(out=O_tok[:], in0=O_tok[:], in1=ssq[:].rearrange("p (c o) -> p c o", o=1).to_broadcast([128, NCK, 64]), op=ALU.mult), reads=allO + ["ssq"], writes=allO)
            P.op("dve", lambda e: e.tensor_tensor(out=O_tok[:], in0=O_tok[:], in1=gno[:].rearrange("p (o v) -> p o v", o=1).to_broadcast([128, NCK, 64]), op=ALU.mult), reads=allO + ["gno"], writes=allO)
            for c in range(NCK):
                out_chunk(c)
            for (t0, n) in [(0, CTX)] + [(CTX + i * 512, 512) for i in range(4)]:
                out_tile(b, hp, t0, n)

        def out_chunk(c):
            pp, ppt = ps.get()
            for h2 in range(2):
                pb = h2 * 64
                P.op("pe", lambda e, pb=pb: e.matmul(pp[pb:pb + 64, 0:64], lhsT=O_tok[pb:pb + 64, c, :], rhs=idn2[pb:pb + 64, :], start=True, stop=True), reads=[("O", c), "idn2"], writes=[ppt], inc=(h2 == 1))
            P.op("act", lambda e: e.activation(out=acc[:, c * 64:(c + 1) * 64], in_=pp[:, 0:64], func=AF.Identity), reads=[ppt], writes=["acc"])

        def out_tile(b, hp, t0, n):
            yi = cnt["y"] % 2
            cnt["y"] += 1
            P.op("dve", lambda e: e.tensor_tensor(out=yb[yi][:, :n], in0=acc[:, t0:t0 + n], in1=xin[:, t0:t0 + n], op=ALU.mult), reads=["acc", "xin"], writes=[("yb", yi)])
            P.dma(lambda e: e.dma_start(out=k.YT[b, 0, hp * 128:(hp + 1) * 128, t0:t0 + n], in_=yb[yi][:, :n]), reads=[("yb", yi)], writes=[("YT", b)])

        G = 3
        for b in range(NB):
            for ti in range(6):
                prep_tile(b, ti)
            prep_gates(b)
            lim = k.opts.get("dn_lim", 99)
            if lim == 0:
                continue
            for hp in range(2):
                for d in range(2):
                    m = d * 2 + hp
                    for c0 in range(0, NCK if lim >= 2 else G, G):
                        lists = [phase_a_steps(m, c) for c in range(c0, min(NCK, c0 + G))]
                        for si in range(len(lists[0])):
                            for lst in lists:
                                lst[si]()
                    if lim < 3:
                        continue
                    P.op("dve", lambda e: e.memset(S[:], 0.0), writes=["S"])
                    order = list(range(NCK)) if d == 0 else [3, 2, 1, 0] + list(range(NCK - 1, 3, -1))
                    for c in order:
                        phase_b_chunk(m, c, d == 0)
                    if "dump" in k.opts and (b, m) == k.opts["dump"][:2]:
                        for j in range(6):
                            P.dma(lambda e, j=j: e.dma_start(out=k.dump[j], in_=qkv[j][:]), reads=[("qkv", j)], writes=[("dump", j)])
                        for j, (tl, tk) in enumerate([(u_all, "u"), (wT_all, "wT"), (kdec_all, "kdec"), (O_tok, "O")]):
                            P.dma(lambda e, j=j, tl=tl: e.dma_start(out=k.dump[6 + j], in_=tl[:].rearrange("p c v -> p (c v)")), reads=[(tk, c) for c in range(NCK)], writes=[("dump", 6 + j)])
                        P.dma(lambda e: e.dma_start(out=k.dump[10:12].rearrange("a p t -> p a t"), in_=attnT_all[:].rearrange("p (a c) v -> p a (c v)", a=2)), reads=[("attnT", c) for c in range(NCK)], writes=[("dump", 10)])
                        for j, (tl, tk, w) in enumerate([(gt, "gt", 288), (bt, "bt", 288), (gcBD, "gcBD", 144), (gtBD, "gtBD", 144), (bBD, "bBD", 144)]):
                            P.dma(lambda e, j=j, tl=tl, w=w: e.dma_start(out=k.dump[12, :, j * 300:j * 300 + w], in_=tl[:].rearrange("p c v -> p (c v)")), reads=[tk], writes=[("dump", 12, j)])
                if lim >= 4:
                    out_phase(b, hp)
        P.end_stage()


def stage_s5(k, l):
    nc, P, NB = k.nc, k.P, k.NB
    HALF_PI = float(np.pi / 2)
    P.serial = k.opts.get("serial", True)
    with ExitStack() as es:
        al = lambda name, shape, dt=F32: es.enter_context(nc.sbuf_tensor("s_" + name, list(shape), dt))
        lre = al("lre", [128, 16]); lim = al("lim", [128, 16]); stp = al("stp", [128, 16]); mag = al("mag", [128, 16])
        cth = al("cth", [128, 16]); sth = al("sth", [128, 16]); t1 = al("t1", [128, 16]); t2 = al("t2", [128, 16]); t3 = al("t3", [128, 16])
        are = al("are", [128, 16]); aim = al("aim", [128, 16]); cfr = al("cfr", [128, 16]); cfi = al("cfi", [128, 16]); hpi = al("hpi", [128, 1])
        pwc = al("pwc", [128, 16, 12]); pws = al("pws", [128, 16, 12])
        bre = al("bre", [128, 8, 16]); bim = al("bim", [128, 8, 16]); cre = al("cre", [128, 8, 16]); cim = al("cim", [128, 8, 16])
        bb_all = al("bb_all", [128, 32, 16]); bd_all = al("bd_all", [128, 32, 32]); tb = [al("tb%d" % i, [128, 8, 16]) for i in range(4)]
        W_all = al("W_all", [32, 32, 128]); cw_all = al("cw_all", [128, 8, 2, 128])
        dsk = al("dsk", [128, 2]); wgl = al("wgl", [128, 2, 512], BF16)
        cs = [al("cs%d" % i, [128, T]) for i in range(2)]; sn = [al("sn%d" % i, [128, T]) for i in range(2)]
        xr = al("xr", [128, T]); xi = al("xi", [128, T]); gr = al("gr", [128, T]); gi = al("gi", [128, T])
        u32 = al("u32", [32, T]); Y = [al("Y%d" % i, [128, T]) for i in range(2)]
        mt = Slots([al("mt%d" % i, [128, 512])[:] for i in range(6)], "mt")
        gel = al("gel", [128, 2, 512], BF16); sg = [al("sg%d" % i, [128, 512]) for i in range(2)]; yb = [al("yb%d" % i, [128, 512], BF16) for i in range(2)]
        banks = [es.enter_context(nc.psum_tensor("s_ps%d" % i, [128, 512], F32)) for i in range(8)]
        psl = Slots([banks[i][:] for i in range(8)], "psb")
        ld = lambda dst, src, tok: P.dma(lambda e: e.dma_start(out=dst, in_=src), writes=[tok])
        ld(lre[:], k.s5_lam_re[l], "lre"); ld(lim[:], k.s5_lam_im[l], "lim"); ld(stp[:], k.s5_log_step[l], "stp")
        ld(bre[:], k.s5_b_re[l], "bre"); ld(bim[:], k.s5_b_im[l], "bim"); ld(cre[:], k.s5_c_re[l], "cre"); ld(cim[:], k.s5_c_im[l], "cim")
        ld(dsk[:], k.s5_d[l], "dsk")
        P.dma(lambda e: e.dma_start(out=wgl[:], in_=k.s5_glu[l].rearrange("(c p) n -> p c n", p=128)), writes=["wgl"], q="pool")
        P.op("dve", lambda e: e.memset(hpi[:], HALF_PI), writes=["hpi"])
        P.op("dve", lambda e: e.memset(bd_all[:], 0.0), writes=["bd_all"])
        P.op("dve", lambda e: e.memset(cw_all[:], 0.0), writes=["cw_all"])
        tt = lambda out, a, b_, op, rd, wr, eng="dve": P.op(eng, lambda e: e.tensor_tensor(out=out, in0=a, in1=b_, op=op), reads=rd, writes=wr)
        P.op("act", lambda e: e.activation(out=stp[:], in_=stp[:], func=AF.Exp), reads=["stp"], writes=["stp"])
        tt(t1[:], lre[:], stp[:], ALU.mult, ["lre", "stp"], ["t1"])
        P.op("act", lambda e: e.activation(out=mag[:], in_=t1[:], func=AF.Exp), reads=["t1"], writes=["mag"])
        tt(t2[:], lim[:], stp[:], ALU.mult, ["lim", "stp"], ["t2"])
        P.op("act", lambda e: e.activation(out=sth[:], in_=t2[:], func=AF.Sin, scale=1.0 / 32), reads=["t2"], writes=["sth"])
        P.op("act", lambda e: e.activation(out=cth[:], in_=t2[:], func=AF.Sin, scale=1.0 / 32, bias=hpi[:, 0:1]), reads=["t2", "hpi"], writes=["cth"])
        for it in range(5):
            tt(t1[:], cth[:], cth[:], ALU.mult, ["cth"], ["t1"])
            tt(t3[:], sth[:], sth[:], ALU.mult, ["sth"], ["t3"])
            P.op("dve", lambda e: e.scalar_tensor_tensor(out=sth[:], in0=cth[:], scalar=2.0, in1=sth[:], op0=ALU.mult, op1=ALU.mult), reads=["cth", "sth"], writes=["sth"])
            tt(cth[:], t1[:], t3[:], ALU.subtract, ["t1", "t3"], ["cth"])
        tt(are[:], mag[:], cth[:], ALU.mult, ["mag", "cth"], ["are"])
        tt(aim[:], mag[:], sth[:], ALU.mult, ["mag", "sth"], ["aim"])
        tt(t1[:], lre[:], lre[:], ALU.mult, ["lre"], ["t1"])
        tt(t3[:], lim[:], lim[:], ALU.mult, ["lim"], ["t3"])
        tt(t1[:], t1[:], t3[:], ALU.add, ["t1", "t3"], ["t1"])
        P.op("dve", lambda e: e.reciprocal(out=t1[:], in_=t1[:]), reads=["t1"], writes=["t1"])
        P.op("dve", lambda e: e.tensor_scalar(out=t2[:], in0=are[:], scalar1=-1.0, scalar2=None, op0=ALU.add), reads=["are"], writes=["t2"])
        tt(cfr[:], t2[:], lre[:], ALU.mult, ["t2", "lre"], ["cfr"])
        tt(t3[:], aim[:], lim[:], ALU.mult, ["aim", "lim"], ["t3"])
        tt(cfr[:], cfr[:], t3[:], ALU.add, ["cfr", "t3"], ["cfr"])
        tt(cfr[:], cfr[:], t1[:], ALU.mult, ["cfr", "t1"], ["cfr"])
        tt(cfi[:], aim[:], lre[:], ALU.mult, ["aim", "lre"], ["cfi"])
        tt(t3[:], t2[:], lim[:], ALU.mult, ["t2", "lim"], ["t3"])
        tt(cfi[:], cfi[:], t3[:], ALU.subtract, ["cfi", "t3"], ["cfi"])
        tt(cfi[:], cfi[:], t1[:], ALU.mult, ["cfi", "t1"], ["cfi"])
        bb4 = bb_all[:].rearrange("p (d c r) h -> p d c r h", d=2, r=2)
        for d in range(2):
            bc = lambda t_: t_[:, d * 8:(d + 1) * 8].rearrange("p (c o) -> p c o", o=1).to_broadcast([128, 8, 16])
            tt(tb[0][:], bre[:], bc(cfr), ALU.mult, ["bre", "cfr"], [("tb", 0)])
            tt(tb[1][:], bim[:], bc(cfi), ALU.mult, ["bim", "cfi"], [("tb", 1)])
            tt(bb4[:, d, :, 0, :], tb[0][:], tb[1][:], ALU.subtract, [("tb", 0), ("tb", 1)], ["bb_all"])
            tt(tb[2][:], bim[:], bc(cfr), ALU.mult, ["bim", "cfr"], [("tb", 2)])
            tt(tb[3][:], bre[:], bc(cfi), ALU.mult, ["bre", "cfi"], [("tb", 3)])
            tt(bb4[:, d, :, 1, :], tb[2][:], tb[3][:], ALU.add, [("tb", 2), ("tb", 3)], ["bb_all"])
        for g2 in range(2):
            rr_ = slice(g2 * 64, (g2 + 1) * 64)
            P.op("dve", lambda e, rr_=rr_, g2=g2: e.tensor_copy(out=bd_all[rr_, :, g2 * 16:(g2 + 1) * 16], in_=bb_all[rr_, :, :]), reads=["bb_all", "bd_all"], writes=["bd_all"])

        def w_build(j):
            pw, pwt = psl.get()
            P.op("pe", lambda e: e.transpose(pw[0:32, 0:128], bd_all[:, j, :], k.idn[:]), reads=["bd_all", "idn"], writes=[pwt])
            P.op("act", lambda e: e.activation(out=W_all[:, j, :], in_=pw[0:32, 0:128], func=AF.Identity), reads=[pwt], writes=["W_all"])
        for j in range(32):
            w_build(j)

        def cw_build(ct, g2):
            q = ct % 4
            rr_ = slice(g2 * 64, (g2 + 1) * 64)
            c0 = q * 32 + g2 * 16
            P.op("dve", lambda e: e.tensor_copy(out=cw_all[rr_, ct, 0, c0:c0 + 16], in_=cre[rr_, ct, :]), reads=["cre", "cw_all"], writes=["cw_all"])
            P.op("dve", lambda e: e.tensor_scalar(out=cw_all[rr_, ct, 1, c0:c0 + 16], in0=cim[rr_, ct, :], scalar1=-1.0, scalar2=None, op0=ALU.mult), reads=["cim", "cw_all"], writes=["cw_all"])
        for ct in range(8):
            for g2 in range(2):
                cw_build(ct, g2)
        P.op("dve", lambda e: e.tensor_copy(out=pwc[:, :, 0], in_=cth[:]), reads=["cth"], writes=["pwc"])
        P.op("dve", lambda e: e.tensor_copy(out=pws[:, :, 0], in_=sth[:]), reads=["sth"], writes=["pws"])

        def pw_level(kk):
            c_, s_ = pwc[:, :, kk - 1], pws[:, :, kk - 1]
            tt(t1[:], c_, c_, ALU.mult, ["pwc"], ["t1"])
            tt(t3[:], s_, s_, ALU.mult, ["pws"], ["t3"])
            tt(pwc[:, :, kk], t1[:], t3[:], ALU.subtract, ["t1", "t3", "pwc"], ["pwc"])
            P.op("dve", lambda e: e.scalar_tensor_tensor(out=pws[:, :, kk], in0=c_, scalar=2.0, in1=s_, op0=ALU.mult, op1=ALU.mult), reads=["pwc", "pws"], writes=["pws"])
        for kk in range(1, 12):
            pw_level(kk)

        def table_gen(j):
            i = j % 2
            C, S_ = cs[i], sn[i]
            ctk, stk = ("cs", i), ("sn", i)
            P.op("dve", lambda e: e.memset(C[:, 0:1], 1.0), writes=[ctk])
            P.op("dve", lambda e: e.memset(S_[:, 0:1], 0.0), writes=[stk])
            for kk in range(12):
                ln = 1 << kk
                nn = min(ln, T - ln)
                if nn <= 0:
                    break
                pc, ps_ = pwc[:, j, kk:kk + 1], pws[:, j, kk:kk + 1]
                lvl(C, S_, ctk, stk, ln, nn, pc, ps_)
            P.dma(lambda e: e.dma_start(out=k.S5TAB[j, 0], in_=C[:]), reads=[ctk], writes=[("TAB", j)])
            P.dma(lambda e: e.dma_start(out=k.S5TAB[j, 1], in_=S_[:]), reads=[stk], writes=[("TAB", j)])

        def lvl(C, S_, ctk, stk, ln, nn, pc, ps_):
            m1, m1t = mt.get()
            m2, m2t = mt.get()
            w_ = min(nn, 512)
            for o in range(0, nn, 512):
                w = min(512, nn - o)
                sub(C, S_, ctk, stk, ln, o, w, pc, ps_)

        def sub(C, S_, ctk, stk, ln, o, w, pc, ps_):
            m1, m1t = mt.get()
            m2, m2t = mt.get()
            P.op("dve", lambda e: e.tensor_scalar(out=m1[:, :w], in0=S_[:, o:o + w], scalar1=ps_, scalar2=None, op0=ALU.mult), reads=[stk, "pws"], writes=[m1t])
            P.op("dve", lambda e: e.tensor_scalar(out=m2[:, :w], in0=S_[:, o:o + w], scalar1=pc, scalar2=None, op0=ALU.mult), reads=[stk, "pwc"], writes=[m2t])
            P.op("dve", lambda e: e.scalar_tensor_tensor(out=C[:, ln + o:ln + o + w], in0=C[:, o:o + w], scalar=pc, in1=m1[:, :w], op0=ALU.mult, op1=ALU.subtract), reads=[ctk, m1t, "pwc"], writes=[ctk])
            P.op("dve", lambda e: e.scalar_tensor_tensor(out=S_[:, ln + o:ln + o + w], in0=C[:, o:o + w], scalar=ps_, in1=m2[:, :w], op0=ALU.mult, op1=ALU.add), reads=[ctk, m2t, "pws"], writes=[stk])
        for j in range(16):
            table_gen(j)

        blocks = [(0, CTX)] + [(CTX + i * 512, 512) for i in range(4)]

        def tabview(tab, d, t0, n):
            if d == 0:
                return tab[:, t0:t0 + n]
            if t0 < CTX:
                lo = CTX - 1 - (t0 + n - 1)
            else:
                lo = CTX + (T - 1 - (t0 + n - 1))
            return tab[:, lo:lo + n][:, ::-1]

        def do_block_in(ct, d, i, t0, n):
            j = d * 8 + ct
            pr, prt = psl.get()
            pi_, pit = psl.get()
            P.op("pe", lambda e: e.matmul(pr[:, :n], lhsT=W_all[:, j * 2, :], rhs=u32[:, t0:t0 + n], start=True, stop=True), reads=["W_all", "u32"], writes=[prt])
            P.op("pe", lambda e: e.matmul(pi_[:, :n], lhsT=W_all[:, j * 2 + 1, :], rhs=u32[:, t0:t0 + n], start=True, stop=True), reads=["W_all", "u32"], writes=[pit])
            cv, sv = tabview(cs[i], d, t0, n), tabview(sn[i], d, t0, n)
            ms = [mt.get() for _ in range(4)]
            tt(ms[0][0][:, :n], pr[:, :n], cv, ALU.mult, [prt, ("cs", i)], [ms[0][1]])
            tt(ms[1][0][:, :n], pi_[:, :n], sv, ALU.mult, [pit, ("sn", i)], [ms[1][1]])
            tt(xr[:, t0:t0 + n], ms[0][0][:, :n], ms[1][0][:, :n], ALU.add, [ms[0][1], ms[1][1]], ["xr"], "pool")
            tt(ms[2][0][:, :n], pi_[:, :n], cv, ALU.mult, [pit, ("cs", i)], [ms[2][1]])
            tt(ms[3][0][:, :n], pr[:, :n], sv, ALU.mult, [prt, ("sn", i)], [ms[3][1]])
            tt(xi[:, t0:t0 + n], ms[2][0][:, :n], ms[3][0][:, :n], ALU.subtract, [ms[2][1], ms[3][1]], ["xi"], "pool")

        def do_scan(ct, d):
            j = d * 8 + ct
            for (src, dst, stok, dtok) in ((xr, gr, "xr", "gr"), (xi, gi, "xi", "gi")):
                scan1(j, d, src, dst, stok, dtok)

        def scan1(j, d, src, dst, stok, dtok):
            rb = lambda n: mag[:, j:j + 1].to_broadcast([128, n])
            if d == 0:
                P.op("dve", lambda e: e.tensor_tensor_scan(out=dst[:], data0=rb(T), data1=src[:], initial=0.0, op0=ALU.mult, op1=ALU.add), reads=[stok, "mag"], writes=[dtok])
            else:
                P.op("dve", lambda e: e.tensor_tensor_scan(out=dst[:, 0:CTX][:, ::-1], data0=rb(CTX), data1=src[:, 0:CTX][:, ::-1], initial=0.0, op0=ALU.mult, op1=ALU.add), reads=[stok, "mag"], writes=[dtok])
                P.op("dve", lambda e: e.tensor_tensor_scan(out=dst[:, CTX:T][:, ::-1], data0=rb(SEQ), data1=src[:, CTX:T][:, ::-1], initial=dst[:, 0:1], op0=ALU.mult, op1=ALU.add), reads=[stok, "mag", dtok], writes=[dtok])

        def do_block_out(ct, d, i, t0, n, first):
            cv, sv = tabview(cs[i], d, t0, n), tabview(sn[i], d, t0, n)
            ms = [mt.get() for _ in range(4)]
            tt(ms[0][0][:, :n], gr[:, t0:t0 + n], cv, ALU.mult, ["gr", ("cs", i)], [ms[0][1]])
            tt(ms[1][0][:, :n], gi[:, t0:t0 + n], sv, ALU.mult, ["gi", ("sn", i)], [ms[1][1]], "pool")
            tt(xr[:, t0:t0 + n], ms[0][0][:, :n], ms[1][0][:, :n], ALU.subtract, [ms[0][1], ms[1][1]], ["xr"])
            tt(ms[2][0][:, :n], gr[:, t0:t0 + n], sv, ALU.mult, ["gr", ("sn", i)], [ms[2][1]], "pool")
            tt(ms[3][0][:, :n], gi[:, t0:t0 + n], cv, ALU.mult, ["gi", ("cs", i)], [ms[3][1]])
            tt(xi[:, t0:t0 + n], ms[2][0][:, :n], ms[3][0][:, :n], ALU.add, [ms[2][1], ms[3][1]], ["xi"], "pool")
            py, pyt = psl.get()
            P.op("pe", lambda e: e.matmul(py[:, :n], lhsT=cw_all[:, ct, 0, :], rhs=xr[:, t0:t0 + n], start=True, stop=False), reads=["cw_all", "xr"], writes=[pyt], inc=False)
            P.op("pe", lambda e: e.matmul(py[:, :n], lhsT=cw_all[:, ct, 1, :], rhs=xi[:, t0:t0 + n], start=False, stop=True), reads=["cw_all", "xi"], writes=[pyt])
            yt_ = Y[ct // 4]
            ytk = ("Y", ct // 4)
            if first:
                P.op("act", lambda e: e.activation(out=yt_[:, t0:t0 + n], in_=py[:, :n], func=AF.Identity), reads=[pyt], writes=[ytk])
            else:
                tt(yt_[:, t0:t0 + n], yt_[:, t0:t0 + n], py[:, :n], ALU.add, [pyt, ytk], [ytk])

        def do_ct_dir(b, ct, d):
            j = d * 8 + ct
            i = j % 2
            P.dma(lambda e: e.dma_start(out=cs[i][:], in_=k.S5TAB[j, 0]), reads=[("TAB", j)], writes=[("cs", i)])
            P.dma(lambda e: e.dma_start(out=sn[i][:], in_=k.S5TAB[j, 1]), reads=[("TAB", j)], writes=[("sn", i)])
            for (t0, n) in blocks:
                do_block_in(ct, d, i, t0, n)
            do_scan(ct, d)
            for (t0, n) in blocks:
                do_block_out(ct, d, i, t0, n, (ct % 4 == 0 and d == 0))

        def do_ct(b, ct):
            r0 = FM_U + ct * 32
            P.dma(lambda e: e.dma_start(out=u32[:], in_=k.PT[b, r0:r0 + 32, :]), reads=[("PT", b)], writes=["u32"])
            for d in range(2):
                do_ct_dir(b, ct, d)

        def out_tile(b, t0, n):
            for yt in range(2):
                ub, ubt = mt.get()
                r0 = FM_U + yt * 128
                P.dma(lambda e, ub=ub, r0=r0: e.dma_start(out=ub[:, :n], in_=k.PT[b, r0:r0 + 128, t0:t0 + n]), reads=[("PT", b)], writes=[ubt])
                yv, yvt = mt.get()
                P.op("dve", lambda e, ub=ub, yv=yv, yt=yt: e.scalar_tensor_tensor(out=yv[:, :n], in0=ub[:, :n], scalar=dsk[:, yt:yt + 1], in1=Y[yt][:, t0:t0 + n], op0=ALU.mult, op1=ALU.add), reads=[ubt, "dsk", ("Y", yt)], writes=[yvt])
                x2, x2t = mt.get()
                P.op("act", lambda e, yv=yv, x2=x2: e.activation(out=x2[:, :n], in_=yv[:, :n], func=AF.Square), reads=[yvt], writes=[x2t])
                P.op("dve", lambda e, x2=x2: e.tensor_scalar(out=x2[:, :n], in0=x2[:, :n], scalar1=0.044715, scalar2=1.0, op0=ALU.mult, op1=ALU.add), reads=[x2t], writes=[x2t])
                P.op("dve", lambda e, x2=x2, yv=yv: e.tensor_tensor(out=x2[:, :n], in0=x2[:, :n], in1=yv[:, :n], op=ALU.mult), reads=[x2t, yvt], writes=[x2t])
                P.op("act", lambda e, x2=x2: e.activation(out=x2[:, :n], in_=x2[:, :n], func=AF.Tanh, scale=0.7978845608028654), reads=[x2t], writes=[x2t])
                P.op("dve", lambda e, x2=x2, yv=yv, yt=yt: e.scalar_tensor_tensor(out=gel[:, yt, :n], in0=x2[:, :n], scalar=1.0, in1=yv[:, :n], op0=ALU.add, op1=ALU.mult), reads=[x2t, yvt], writes=[("gel", yt)])
            for oc in range(2):
                pa, pat = psl.get()
                pg, pgt = psl.get()
                for kc in range(2):
                    P.op("pe", lambda e, oc=oc, kc=kc, pa=pa: e.matmul(pa[:, :n], lhsT=wgl[:, kc, oc * 128:(oc + 1) * 128], rhs=gel[:, kc, :n], start=(kc == 0), stop=(kc == 1)), reads=["wgl", ("gel", kc)], writes=[pat])
                for kc in range(2):
                    P.op("pe", lambda e, oc=oc, kc=kc, pg=pg: e.matmul(pg[:, :n], lhsT=wgl[:, kc, 256 + oc * 128:256 + (oc + 1) * 128], rhs=gel[:, kc, :n], start=(kc == 0), stop=(kc == 1)), reads=["wgl", ("gel", kc)], writes=[pgt])
                P.op("act", lambda e, oc=oc, pg=pg: e.activation(out=sg[oc][:, :n], in_=pg[:, :n], func=AF.Sigmoid, scale=0.5), reads=[pgt], writes=[("sg", oc)])
                P.op("dve", lambda e, oc=oc, pa=pa: e.scalar_tensor_tensor(out=yb[oc][:, :n], in0=pa[:, :n], scalar=0.5, in1=sg[oc][:, :n], op0=ALU.mult, op1=ALU.mult), reads=[pat, ("sg", oc)], writes=[("yb", oc)])
                P.dma(lambda e, oc=oc: e.dma_start(out=k.YT[b, 1, oc * 128:(oc + 1) * 128, t0:t0 + n], in_=yb[oc][:, :n]), reads=[("yb", oc)], writes=[("YT", b)])

        for b in range(NB):
            for ct in range(8):
                do_ct(b, ct)
            for (t0, n) in blocks:
                out_tile(b, t0, n)
        P.end_stage()


def stage_mixers(k, l):
    sel = k.opts.get("mixers", ("dn", "s5", "hg", "att"))
    if "dn" in sel:
        stage_deltanet(k, l)
    if "s5" in sel:
        stage_s5(k, l)
    if "hg" in sel:
        stage_hgrn2(k, l)
    if "att" in sel:
        stage_attention(k, l)


def stage_merge(k, l):
    nc, P, NB, NC = k.nc, k.P, k.NB, k.NC
    with ExitStack() as es:
        wg = es.enter_context(nc.sbuf_tensor("m_wg", [128, 8, 4 * D], BF16))
        wb = es.enter_context(nc.sbuf_tensor("m_wb", [128, 8, D], BF16))
        wo = es.enter_context(nc.sbuf_tensor("m_wo", [128, 8, D], BF16))
        xt = [es.enter_context(nc.sbuf_tensor("m_x%d" % s, [128, 8, 256], F32)) for s in range(2)]
        yt = [es.enter_context(nc.sbuf_tensor("m_y%d" % s, [128, 8, 256], BF16)) for s in range(2)]
        ht = es.enter_context(nc.sbuf_tensor("m_h", [128, 8, 256], BF16))
        acc = es.enter_context(nc.sbuf_tensor("m_acc", [128, 8, 256], F32))
        accb = es.enter_context(nc.sbuf_tensor("m_accb", [128, 8, 256], BF16))
        sg = [es.enter_context(nc.sbuf_tensor("m_sg%d" % s, [128, 256], F32)) for s in range(2)]
        tt = [es.enter_context(nc.sbuf_tensor("m_tt%d" % s, [128, 256], F32)) for s in range(2)]
        psg = [es.enter_context(nc.psum_tensor("m_psg%d" % s, [128, 512], F32)) for s in range(2)]
        psy = [es.enter_context(nc.psum_tensor("m_psy%d" % s, [128, 512], F32)) for s in range(2)]
        pso = [es.enter_context(nc.psum_tensor("m_pso%d" % s, [128, 512], F32)) for s in range(2)]
        ntiles = alloc_norm_tiles(k, es, "m_", 256)
        A, Sh, G = emit_mod_scalars(k, es, "m_", l, 1, 3, 4, 5, 1.0)
        load_w_bf16(k, wg, k.w_gate[l], 4 * D, "wg")
        load_w_bf16(k, wb, k.w_branch[l].rearrange("i r n -> (i r) n"), D, "wb")
        load_w_bf16(k, wo, k.w_out[l], D, "wo")
        it = 0
        qi = 0
        for (b, t0, n, cond) in token_tiles(NB, 256):
            if l == 1 and cond == NB:
                continue
            s = it % 2
            it += 1
            xsrc = k.XT[b].rearrange("(c p) t -> p c t", p=128)[:, :, t0:t0 + n]
            P.dma(lambda e, s=s, xsrc=xsrc, n=n: e.dma_start(out=xt[s][:, :, :n], in_=xsrc), reads=[("XT", b)], writes=[("x", s)])
            ysrc = k.YT[b].rearrange("i (c p) t -> p (i c) t", p=128)[:, :, t0:t0 + n]
            P.dma(lambda e, s=s, ysrc=ysrc, n=n: e.dma_start(out=yt[s][:, :, :n], in_=ysrc), reads=[("YT", b)], writes=[("y", s)])
            emit_norm_mod(k, ntiles, xt[s], ht, n, A, Sh, cond, "modsc", s)
            for m in range(8):
                for i in range(4):
                    q = qi % 2
                    qi += 1
                    for kc in range(8):
                        P.op("pe", lambda e, i=i, m=m, q=q, kc=kc, n=n: e.matmul(psg[q][:, :n], lhsT=wg[:, kc, i * D + m * 128:i * D + (m + 1) * 128], rhs=ht[:, kc, :n], start=(kc == 0), stop=(kc == 7)),
                             reads=[("wg", kc), "h"], writes=[("psg", q)])
                    for kk in range(2):
                        P.op("pe", lambda e, i=i, m=m, q=q, kk=kk, n=n, s=s: e.matmul(psy[q][:, :n], lhsT=wb[:, i * 2 + kk, m * 128:(m + 1) * 128], rhs=yt[s][:, i * 2 + kk, :n], start=(kk == 0), stop=(kk == 1)),
                             reads=[("wb", i * 2 + kk), ("y", s)], writes=[("psy", q)])
                    P.op("act", lambda e, q=q, n=n: e.activation(out=sg[q][:, :n], in_=psg[q][:, :n], func=AF.Sigmoid), reads=[("psg", q)], writes=[("sg", q)])
                    if i == 0:
                        P.op("dve", lambda e, q=q, n=n, m=m: e.tensor_tensor(out=acc[:, m, :n], in0=sg[q][:, :n], in1=psy[q][:, :n], op=ALU.mult), reads=[("sg", q), ("psy", q)], writes=[("acc", m)])
                    else:
                        P.op("dve", lambda e, q=q, n=n: e.tensor_tensor(out=tt[q][:, :n], in0=sg[q][:, :n], in1=psy[q][:, :n], op=ALU.mult), reads=[("sg", q), ("psy", q)], writes=[("tt", q)])
                        if i < 3:
                            P.op("pool", lambda e, q=q, n=n, m=m: e.tensor_tensor(out=acc[:, m, :n], in0=acc[:, m, :n], in1=tt[q][:, :n], op=ALU.add), reads=[("tt", q), ("acc", m)], writes=[("acc", m)])
                        else:
                            P.op("pool", lambda e, q=q, n=n, m=m: e.tensor_tensor(out=accb[:, m, :n], in0=acc[:, m, :n], in1=tt[q][:, :n], op=ALU.add), reads=[("tt", q), ("acc", m)], writes=[("accb", m)])
            for m in range(8):
                q = m % 2
                for kc in range(8):
                    P.op("pe", lambda e, m=m, q=q, kc=kc, n=n: e.matmul(pso[q][:, :n], lhsT=wo[:, kc, m * 128:(m + 1) * 128], rhs=accb[:, kc, :n], start=(kc == 0), stop=(kc == 7)),
                         reads=[("wo", kc), ("accb", kc)], writes=[("pso", q)])
                P.op("dve", lambda e, m=m, q=q, s=s, n=n, cond=cond: e.scalar_tensor_tensor(out=xt[s][:, m, :n], in0=pso[q][:, :n], scalar=G[:, m, cond:cond + 1], in1=xt[s][:, m, :n], op0=ALU.mult, op1=ALU.add),
                     reads=[("pso", q), ("x", s), "modsg"], writes=[("x", s)])
            P.dma(lambda e, s=s, xsrc=xsrc, n=n: e.dma_start(out=xsrc, in_=xt[s][:, :, :n]), reads=[("x", s)], writes=[("XT", b)])
        P.end_stage()


def stage_final(k):
    nc, P, NB = k.nc, k.P, k.NB
    with ExitStack() as es:
        xt = [es.enter_context(nc.sbuf_tensor("o_x%d" % s, [128, 8, 512], F32)) for s in range(2)]
        yt = es.enter_context(nc.sbuf_tensor("o_y", [128, 8, 512], F32))
        ot = [es.enter_context(nc.sbuf_tensor("o_o%d" % s, [128, D], F32)) for s in range(2)]
        sq = es.enter_context(nc.sbuf_tensor("o_sq", [128, 8, 512], BF16))
        rs = es.enter_context(nc.sbuf_tensor("o_rs", [128, 512], F32))
        epsb = es.enter_context(nc.sbuf_tensor("o_eps", [128, 1], F32))
        psms = es.enter_context(nc.psum_tensor("o_psms", [128, 512], F32))
        ps = [es.enter_context(nc.psum_tensor("o_ps%d" % s, [128, 4, 128], F32)) for s in range(4)]
        P.op("dve", lambda e: e.memset(epsb[:], EPS), writes=["epsb"])
        it = 0
        oi = 0
        pi = 0
        for (b, t0, n, cond) in token_tiles(NB):
            if cond == NB:
                continue
            s = it % 2
            it += 1
            xsrc = k.XT[b].rearrange("(c p) t -> p c t", p=128)[:, :, t0:t0 + n]
            P.dma(lambda e, s=s, xsrc=xsrc: e.dma_start(out=xt[s][:], in_=xsrc), reads=[("XT", b)], writes=[("x", s)])
            P.op("act", lambda e, s=s: e.activation(out=sq[:], in_=xt[s][:], func=AF.Square), reads=[("x", s)], writes=["sq"])
            for c in range(8):
                P.op("pe", lambda e, c=c: e.matmul(psms[:], lhsT=k.onesb[:], rhs=sq[:, c, :], start=(c == 0), stop=(c == 7)), reads=["sq", "onesb"], writes=["psms"])
            P.op("act", lambda e: e.activation(out=rs[:], in_=psms[:], func=AF.Sqrt, bias=epsb[:, 0:1], scale=1.0), reads=["psms", "epsb"], writes=["rs0", "rs"])
            P.op("dve", lambda e: e.reciprocal(out=rs[:], in_=rs[:]), reads=["rs0"], writes=["rs"])
            for c in range(8):
                P.op("dve", lambda e, c=c, s=s: e.scalar_tensor_tensor(out=yt[:, c, :], in0=xt[s][:, c, :], scalar=k.fg[:, c:c + 1], in1=rs[:], op0=ALU.mult, op1=ALU.mult),
                     reads=[("x", s), "rs", "fg"], writes=[("yt", c)])
            for tb in range(4):
                o = oi % 2
                oi += 1
                for h in range(2):
                    p = pi % 4
                    pi += 1
                    for c in range(4):
                        ch = h * 4 + c
                        P.op("pe", lambda e, p=p, c=c, ch=ch, tb=tb: e.transpose(ps[p][:, c, :], yt[:, ch, tb * 128:(tb + 1) * 128], k.idn[:]),
                             reads=[("yt", ch), "idn"], writes=[("ps", p)])
                    if h == 0:
                        P.op("act", lambda e, p=p, o=o, h=h: e.activation(out=ot[o][:, h * 512:(h + 1) * 512], in_=ps[p][:].rearrange("p a b -> p (a b)"), func=AF.Identity), reads=[("ps", p)], writes=[("ot", o, h)])
                    else:
                        P.op("dve", lambda e, p=p, o=o, h=h: e.tensor_copy(out=ot[o][:, h * 512:(h + 1) * 512], in_=ps[p][:].rearrange("p a b -> p (a b)")), reads=[("ps", p)], writes=[("ot", o, h)])
                r0 = t0 - CTX + tb * 128
                P.dma(lambda e, o=o, b=b, r0=r0: e.dma_start(out=k.out[b, r0:r0 + 128, :], in_=ot[o][:]), reads=[("ot", o, 0), ("ot", o, 1)], writes=[("out", b)])
        P.end_stage()


def make_inputs(inputs, core, NB):
    b0 = core * NB
    f = lambda a: np.ascontiguousarray(a, dtype=np.float32)
    w_in = inputs["w_in"]
    cT = np.concatenate([inputs["c"][b0:b0 + NB], inputs["c_ctx"][None]], 0).reshape(NB + 1, 8, 128).transpose(2, 1, 0)
    return {
        "x": f(inputs["x"][b0:b0 + NB]),
        "ctx": f(inputs["ctx"][b0:b0 + NB]),
        "cT": f(cT),
        "ada_w": f(inputs["ada_w"]),
        "ada_b": f(inputs["ada_b"].reshape(2, 72, 128).transpose(0, 2, 1)),
        "norm_g": f(inputs["norm_g"].reshape(2, 3, 8, 128).transpose(0, 1, 3, 2)),
        "final_g": f(inputs["final_g"].reshape(8, 128).T),
        "ffn_w1": f(inputs["ffn_w1"]), "ffn_w3": f(inputs["ffn_w3"]), "ffn_w2": f(inputs["ffn_w2"]),
        "w_fm": f(np.concatenate([w_in[:, :, a:a + w] for a, w in FM_GROUPS], axis=2)),
        "w_tok": f(np.concatenate([w_in[:, :, a:a + w] for a, w in TOK_GROUPS], axis=2)),
        "w_gate": f(w_in[:, :, GATE0:]),
        "w_branch": f(inputs["w_branch"]),
        "w_out": f(inputs["w_out"]),
        "ident": np.eye(128, dtype=np.float32),
        "rope_cos": ROPE[0], "rope_sin": ROPE[1], "rope_rm": ROPE[2],
        "hg_masks": HG_MASKS, "bd64": BD64,
        "s5_lam_re": f(inputs["s5_lam_re"].reshape(2, 2, 8, 2, 64).transpose(0, 3, 4, 1, 2).reshape(2, 128, 16)),
        "s5_lam_im": f(inputs["s5_lam_im"].reshape(2, 2, 8, 2, 64).transpose(0, 3, 4, 1, 2).reshape(2, 128, 16)),
        "s5_log_step": f(np.broadcast_to(inputs["s5_log_step"].reshape(2, 2, 8, 2, 1), (2, 2, 8, 2, 64)).transpose(0, 3, 4, 1, 2).reshape(2, 128, 16)),
        "s5_b_re": f(inputs["s5_b_re"].reshape(2, 8, 2, 64, 16).transpose(0, 2, 3, 1, 4).reshape(2, 128, 8, 16)),
        "s5_b_im": f(inputs["s5_b_im"].reshape(2, 8, 2, 64, 16).transpose(0, 2, 3, 1, 4).reshape(2, 128, 8, 16)),
        "s5_c_re": f(inputs["s5_c_re"].reshape(2, 8, 2, 16, 64).transpose(0, 2, 4, 1, 3).reshape(2, 128, 8, 16)),
        "s5_c_im": f(inputs["s5_c_im"].reshape(2, 8, 2, 16, 64).transpose(0, 2, 4, 1, 3).reshape(2, 128, 8, 16)),
        "s5_d": f(inputs["s5_d"].reshape(2, 2, 128).transpose(0, 2, 1)),
        "s5_glu": f(inputs["s5_glu"]),
        "dn_gm64": DN_GM64, "dn_gmbd": DN_GMBD, "dn_idn2": DN_IDN2,
        "dn_conv": f(inputs["dn_conv"].reshape(2, 5, 6, 128).transpose(0, 3, 2, 1)),
        "dn_a_log": f(np.broadcast_to(inputs["dn_a_log"].reshape(2, 1, 8), (2, 128, 8))),
        "dn_dt_bias": f(np.broadcast_to(inputs["dn_dt_bias"].reshape(2, 1, 8), (2, 128, 8))),
        "dn_norm_g": f(np.broadcast_to(inputs["dn_norm_g"].reshape(2, 1, 64), (2, 128, 64))),
        "hg_lb": f(inputs["hg_lb_logits"].reshape(2, 2, 128).transpose(2, 1, 0)),
        "hg_norm_g": f(np.tile(inputs["hg_norm_g"], (1, 2)).reshape(2, 128, 1)),
        "at_qn_g": f(inputs["at_qn_g"].reshape(2, 64, 1)), "at_kn_g": f(inputs["at_kn_g"].reshape(2, 64, 1)),
    }


def _rope_consts():
    n = np.arange(SEQ)
    r, c = n // 64, n % 64
    inv = (10000.0 ** (-np.arange(0, 32, 2, dtype=np.float32) / 32)).astype(np.float32)
    ang_r = r[:, None].astype(np.float32) * inv
    ang_c = c[:, None].astype(np.float32) * inv
    ang = np.concatenate([ang_r, ang_r, ang_c, ang_c], -1)
    rm = np.zeros((64, 64), np.float32)
    for i in range(16):
        rm[16 + i, i] = -1.0
        rm[i, 16 + i] = 1.0
        rm[48 + i, 32 + i] = -1.0
        rm[32 + i, 48 + i] = 1.0
    return np.ascontiguousarray(np.cos(ang).T.astype(np.float32)), np.ascontiguousarray(np.sin(ang).T.astype(np.float32)), rm


ROPE = _rope_consts()


def _dn_consts():
    a = np.arange(64)
    le = [(a[:, None] <= a[None, :]), (a[:, None] >= a[None, :])]
    st = [(a[None, :] < a[:, None]), (a[None, :] > a[:, None])]
    gm64 = np.zeros((64, 2, 2, 128), np.float32)
    gmbd = np.zeros((128, 2, 2, 128), np.float32)
    for d in range(2):
        gm64[:, d, 0, :] = np.tile(le[d], (1, 2))
        gm64[:, d, 1, :] = np.tile(st[d], (1, 2))
        for h in range(2):
            gmbd[h * 64:(h + 1) * 64, d, 0, h * 64:(h + 1) * 64] = st[d]
            gmbd[h * 64:(h + 1) * 64, d, 1, h * 64:(h + 1) * 64] = le[d]
    idn2 = np.tile(np.eye(64, dtype=np.float32), (2, 1))
    return gm64, gmbd, idn2


DN_GM64, DN_GMBD, DN_IDN2 = _dn_consts()
_s = np.arange(128)[:, None] % 64
_t = np.arange(64)[None, :]
HG_MASKS = np.ascontiguousarray(np.stack([(_s <= _t), (_s >= _t)], 1).astype(np.float32))
BD64 = np.kron(np.eye(2, dtype=np.float32), np.full((64, 64), 1.0 / 64, np.float32))
_CACHE = {}


def kernel(**inputs):
    NB = inputs["x"].shape[0] // N_CORES
    if "nc" not in _CACHE:
        _CACHE["nc"] = build(NB)
    nc = _CACHE["nc"]
    shared = None
    in_maps = []
    for c in range(N_CORES):
        in_maps.append(make_inputs(inputs, c, NB))
    res = run_bass_kernel_spmd(nc, in_maps, core_ids=list(range(N_CORES)))
    return np.concatenate([r["out"] for r in res.results], axis=0).astype(np.float32)
```

```python
import numpy as np
from contextlib import ExitStack
import concourse.bass as bass
import concourse.mybir as mybir
from concourse.bass_utils import run_bass_kernel_spmd

F32 = mybir.dt.float32
BF16 = mybir.dt.bfloat16
AF = mybir.ActivationFunctionType
ALU = mybir.AluOpType
AX = mybir.AxisListType

D = 1024
SEQ = 2048
CTX = 256
T = SEQ + CTX
DFF = 2816
NFF = DFF // 128
EPS = 1e-6
N_CORES = 8
ENG = ("pe", "act", "dve", "pool", "sp")

FM_GROUPS = [(0, 768), (768, 256), (1040, 256), (1296, 256), (1552, 512), (2320, 256), (2576, 256), (2832, 128)]
FM_W = sum(w for _, w in FM_GROUPS)
FM_Q, FM_K, FM_V, FM_Z, FM_U, FM_HQ, FM_HF, FM_HG, FM_AQ, FM_AK = 0, 256, 512, 768, 1024, 1280, 1536, 2048, 2304, 2560
TOK_GROUPS = [(2960, 128), (2064, 256), (1024, 8), (1032, 8)]
TOK_W = 400
TK_AV, TK_HV, TK_A, TK_B = 0, 128, 384, 392
GATE0 = 3088


class Prog:
    def __init__(self, nc, n_dma_sems=56):
        self.nc = nc
        self.sem = {e: nc.semaphore("sem_" + e).__enter__() for e in ("pe", "act", "dve", "pool")}
        self.dsem = [nc.semaphore("dsem%d" % i).__enter__() for i in range(n_dma_sems)]
        self.duse = [0] * n_dma_sems
        self.dnext = 0
        self.cnt = {e: 0 for e in self.sem}
        self.lastw = {}
        self.readers = {}
        self.ops = {e: [] for e in ENG}
        self.waited = {}
        self.nops = 0
        self.serial = False
        self.last_ev = {}
        self.prev_ev = None

    def _need(self, eng, ev, waits):
        if ev is None:
            return
        key, val, src = ev
        if self.waited.get((eng, key), 0) >= val:
            return
        self.waited[(eng, key)] = val
        waits.append((key, val))

    def _semh(self, key):
        return self.sem[key] if isinstance(key, str) else self.dsem[key]

    def op(self, eng, fn, reads=(), writes=(), inc=True):
        waits = []
        for r in reads:
            ev = self.lastw.get(r)
            if ev is not None and not (ev[2] == eng and eng == "pe"):
                self._need(eng, ev, waits)
        for w in writes:
            ev = self.lastw.get(w)
            if ev is not None and ev[2] != eng:
                self._need(eng, ev, waits)
            for rv in self.readers.get(w, ()):
                if rv[2] != eng:
                    self._need(eng, rv, waits)
        if self.serial:
            waits = [w_ for w_ in waits if not isinstance(w_[0], str) or w_[0] == eng]
            for w_ in waits:
                pass
            if self.prev_ev is not None and self.prev_ev[2] != eng:
                if self.waited.get((eng, self.prev_ev[0]), 0) < self.prev_ev[1] or True:
                    self.waited[(eng, self.prev_ev[0])] = max(self.waited.get((eng, self.prev_ev[0]), 0), self.prev_ev[1])
                    waits.append((self.prev_ev[0], self.prev_ev[1]))
        if inc:
            self.cnt[eng] += 1
            me = (eng, self.cnt[eng], eng)
        else:
            me = (eng, self.cnt[eng] + 1, eng)
        self.last_ev[eng] = me
        self.prev_ev = me if inc else (self.prev_ev if not self.serial else me)
        for r in reads:
            self.readers.setdefault(r, []).append(me)
        for w in writes:
            self.lastw[w] = me
            self.readers[w] = []
        self.ops[eng].append((waits, fn, ("sem", eng) if inc else ("none", eng)))
        self.nops += 1
        return me

    def dma(self, fn, reads=(), writes=(), q="sp"):
        waits = []
        for r in reads:
            self._need(q, self.lastw.get(r), waits)
        for w in writes:
            self._need(q, self.lastw.get(w), waits)
            for rv in self.readers.get(w, ()):
                self._need(q, rv, waits)
        i = self.dnext
        self.dnext = (self.dnext + 1) % len(self.dsem)
        if self.duse[i] > 0:
            self._need(q, (i, 16 * self.duse[i], "dma"), waits)
        self.duse[i] += 1
        me = (i, 16 * self.duse[i], "dma")
        for r in reads:
            self.readers.setdefault(r, []).append(me)
        for w in writes:
            self.lastw[w] = me
            self.readers[w] = []
        self.ops[q].append((waits, fn, ("dsem", i)))
        self.nops += 1
        return me

    def end_stage(self):
        waits = []
        for tok, ev in self.lastw.items():
            self._need("sp", ev, waits)
        for tok, evs in self.readers.items():
            for ev in evs:
                self._need("sp", ev, waits)
        self.ops["sp"].append((waits, None, None))
        nc = self.nc
        engobj = {"pe": "tensor", "act": "scalar", "dve": "vector", "pool": "gpsimd", "sp": "sync"}
        with nc.Block() as block:
            for e in ENG:
                ops = self.ops[e]
                if not ops:
                    continue

                def body(eo, ops=ops):
                    for waits, fn, inc in ops:
                        for key, val in waits:
                            eo.wait_ge(self._semh(key), val)
                        if fn is None:
                            continue
                        ins = fn(eo)
                        if inc[0] == "sem":
                            ins.then_inc(self.sem[inc[1]], 1)
                        elif inc[0] == "dsem":
                            ins.then_inc(self.dsem[inc[1]], 16)

                getattr(block, engobj[e])(body)
        self.ops = {e: [] for e in ENG}
        self.waited = {}
        self.lastw = {}
        self.readers = {}
        self.last_ev = {}
        self.serial = False
        self.prev_ev = None


class K:
    pass


class NCProxy:
    def __init__(self, nc):
        object.__setattr__(self, "_nc", nc)
        object.__setattr__(self, "_n", [0])

    def __getattr__(self, name):
        return getattr(self._nc, name)

    def sbuf_tensor(self, name, shape, dtype):
        self._n[0] += 1
        return self._nc.sbuf_tensor("%s_u%d" % (name, self._n[0]), shape, dtype)

    def psum_tensor(self, name, shape, dtype):
        self._n[0] += 1
        return self._nc.psum_tensor("%s_u%d" % (name, self._n[0]), shape, dtype)


def token_tiles(NB, ts=512):
    tl = []
    for b in range(NB):
        tl.append((b, 0, CTX, NB))
        for i in range(SEQ // ts):
            tl.append((b, CTX + i * ts, ts, b))
    return tl


def build(NB, opts=None):
    opts = opts or {}
    NC = NB + 1
    nc = bass.Bass("TRN2", target_bir_lowering=False)
    k = K()
    k.nc, k.NB, k.NC, k.opts = NCProxy(nc), NB, NC, opts
    inp = lambda name, shape: nc.dram_tensor(name, list(shape), F32, kind="ExternalInput").ap()
    k.x = inp("x", [NB, SEQ, D])
    k.ctx = inp("ctx", [NB, CTX, D])
    k.cT = inp("cT", [128, 8, NC])
    k.ada_w = inp("ada_w", [2, D, 9 * D])
    k.ada_b = inp("ada_b", [2, 128, 72])
    k.norm_g = inp("norm_g", [2, 3, 128, 8])
    k.final_g = inp("final_g", [128, 8])
    k.w1 = inp("ffn_w1", [2, 2, D, DFF])
    k.w3 = inp("ffn_w3", [2, 2, D, DFF])
    k.w2 = inp("ffn_w2", [2, 2, DFF, D])
    k.w_fm = inp("w_fm", [2, D, FM_W])
    k.w_tok = inp("w_tok", [2, D, TOK_W])
    k.w_gate = inp("w_gate", [2, D, 4 * D])
    k.w_branch = inp("w_branch", [2, 4, 256, D])
    k.w_out = inp("w_out", [2, D, D])
    k.ident = inp("ident", [128, 128])
    k.rope_cos = inp("rope_cos", [64, SEQ])
    k.rope_sin = inp("rope_sin", [64, SEQ])
    k.rope_rm = inp("rope_rm", [64, 64])
    k.at_qn_g = inp("at_qn_g", [2, 64, 1])
    k.at_kn_g = inp("at_kn_g", [2, 64, 1])
    k.hg_masks = inp("hg_masks", [128, 2, 64])
    k.s5_lam_re = inp("s5_lam_re", [2, 128, 16])
    k.s5_lam_im = inp("s5_lam_im", [2, 128, 16])
    k.s5_log_step = inp("s5_log_step", [2, 128, 16])
    k.s5_b_re = inp("s5_b_re", [2, 128, 8, 16])
    k.s5_b_im = inp("s5_b_im", [2, 128, 8, 16])
    k.s5_c_re = inp("s5_c_re", [2, 128, 8, 16])
    k.s5_c_im = inp("s5_c_im", [2, 128, 8, 16])
    k.s5_d = inp("s5_d", [2, 128, 2])
    k.s5_glu = inp("s5_glu", [2, 256, 512])
    k.S5TAB = nc.dram_tensor("S5TAB", [16, 2, 128, T], F32).ap()
    k.dn_gm64 = inp("dn_gm64", [64, 2, 2, 128])
    k.dn_gmbd = inp("dn_gmbd", [128, 2, 2, 128])
    k.dn_idn2 = inp("dn_idn2", [128, 64])
    k.dn_conv = inp("dn_conv", [2, 128, 6, 5])
    k.dn_a_log = inp("dn_a_log", [2, 128, 8])
    k.dn_dt_bias = inp("dn_dt_bias", [2, 128, 8])
    k.dn_norm_g = inp("dn_norm_g", [2, 128, 64])
    k.bd64 = inp("bd64", [128, 128])
    k.hg_lb = inp("hg_lb", [128, 2, 2])
    k.hg_norm_g = inp("hg_norm_g", [2, 128, 1])
    k.out = nc.dram_tensor("out", [NB, SEQ, D], F32, kind="ExternalOutput").ap()
    k.XT = nc.dram_tensor("XT", [NB, D, T], F32).ap()
    only = opts.get("only")
    k.PT = nc.dram_tensor("PT", [NB, FM_W, T], F32, **({"kind": "ExternalInput"} if only else {})).ap()
    k.TOK = nc.dram_tensor("TOK", [NB, T, TOK_W], F32, **({"kind": "ExternalInput"} if only else {})).ap()
    k.YT = nc.dram_tensor("YT", [NB, 4, 256, T], BF16, **({"kind": "ExternalOutput"} if only else {})).ap()
    if "dump" in opts:
        k.dump = nc.dram_tensor("dump", [16, 128, T], F32, kind="ExternalOutput").ap()
    if "dbg" in opts:
        k.dbg = {nm: nc.dram_tensor("dbg_" + nm, list(shp), F32, kind="ExternalOutput").ap() for nm, shp in opts["dbg"].items()}
    P = Prog(nc)
    k.P = P
    with ExitStack() as es:
        al = lambda name, shape, dt=F32: es.enter_context(nc.sbuf_tensor(name, list(shape), dt))
        k.idn = al("idn", [128, 128])
        k.onesb = al("onesb", [128, 128], BF16)
        k.mods = al("mods", [128, 72, NC])
        k.ng = al("ng", [128, 2, 3, 8])
        k.fg = al("fg", [128, 8])
        P.dma(lambda e: e.dma_start(out=k.idn[:], in_=k.ident), writes=["idn"])
        P.dma(lambda e: e.dma_start(out=k.fg[:], in_=k.final_g), writes=["fg"])
        for l in range(2):
            for j in range(3):
                P.dma(lambda e, l=l, j=j: e.dma_start(out=k.ng[:, l, j, :], in_=k.norm_g[l, j]), writes=["ng"])
        P.op("dve", lambda e: e.memset(k.onesb[:], 1.0 / D), writes=["onesb"])
        P.end_stage()
        if only:
            for l in opts.get("layers", (0,)):
                {"att": stage_attention, "hg": stage_hgrn2, "dn": stage_deltanet, "s5": stage_s5}[only](k, l)
            return nc
        stage_transpose_in(k)
        stop = opts.get("stop", "")
        for l in range(2):
            stage_ada(k, l)
            stage_ffn(k, l, 0)
            if stop == "ffn1_%d" % l:
                break
            stage_inproj(k, l)
            if stop == "inproj_%d" % l:
                break
            stage_mixers(k, l)
            if stop == "mixers_%d" % l:
                break
            stage_merge(k, l)
            if stop == "merge_%d" % l:
                break
            stage_ffn(k, l, 1)
        if "XT" in opts.get("dbg", {}):
            P.dma(lambda e: e.dma_start(out=k.dbg["XT"], in_=k.XT), reads=[], writes=["dbgx"])
            P.end_stage()
        if "PT" in opts.get("dbg", {}):
            P.dma(lambda e: e.dma_start(out=k.dbg["PT"], in_=k.PT), reads=[], writes=["dbgp"])
            P.dma(lambda e: e.dma_start(out=k.dbg["TOK"], in_=k.TOK), reads=[], writes=["dbgt"])
            P.end_stage()
        stage_final(k)
    return nc


def stage_transpose_in(k):
    nc, P = k.nc, k.P
    with ExitStack() as es:
        xin = [es.enter_context(nc.sbuf_tensor("ti_x%d" % i, [128, D], F32)) for i in range(2)]
        xo = [es.enter_context(nc.sbuf_tensor("ti_o%d" % i, [128, 8, 128], F32)) for i in range(2)]
        ps = [es.enter_context(nc.psum_tensor("ti_ps%d" % i, [128, 4, 128], F32)) for i in range(4)]
        it = 0
        for b in range(k.NB):
            for tb in range(T // 128):
                s = it % 2
                src = k.ctx[b, tb * 128:(tb + 1) * 128, :] if tb < 2 else k.x[b, (tb - 2) * 128:(tb - 1) * 128, :]
                P.dma(lambda e, s=s, src=src: e.dma_start(out=xin[s][:], in_=src), writes=[("xin", s)])
                for h in range(2):
                    pi = (it * 2 + h) % 4
                    for c in range(4):
                        ch = h * 4 + c
                        P.op("pe", lambda e, s=s, pi=pi, c=c, ch=ch: e.transpose(ps[pi][:, c, :], xin[s][:, ch * 128:(ch + 1) * 128], k.idn[:]),
                             reads=[("xin", s), "idn"], writes=[("tps", pi)])
                    eng = "act" if h == 0 else "dve"
                    if eng == "act":
                        P.op("act", lambda e, s=s, pi=pi, h=h: e.activation(out=xo[s][:, h * 4:(h + 1) * 4, :], in_=ps[pi][:], func=AF.Identity),
                             reads=[("tps", pi)], writes=[("xo", s, h)])
                    else:
                        P.op("dve", lambda e, s=s, pi=pi, h=h: e.tensor_copy(out=xo[s][:, h * 4:(h + 1) * 4, :], in_=ps[pi][:]),
                             reads=[("tps", pi)], writes=[("xo", s, h)])
                dst = k.XT[b].rearrange("(c p) t -> p c t", p=128)[:, :, tb * 128:(tb + 1) * 128]
                P.dma(lambda e, s=s, dst=dst: e.dma_start(out=dst, in_=xo[s][:]), reads=[("xo", s, 0), ("xo", s, 1)], writes=[("XT", b)])
                it += 1
        P.end_stage()


def stage_ada(k, l):
    nc, P, NC = k.nc, k.P, k.NC
    with ExitStack() as es:
        sc = es.enter_context(nc.sbuf_tensor("ad_sc", [128, 8, NC], F32))
        ab = es.enter_context(nc.sbuf_tensor("ad_b", [128, 72], F32))
        wt = [es.enter_context(nc.sbuf_tensor("ad_w%d" % i, [128, 8, 1024], F32)) for i in range(2)]
        ps = [es.enter_context(nc.psum_tensor("ad_ps%d" % i, [128, 8, NC], F32)) for i in range(2)]
        P.dma(lambda e: e.dma_start(out=sc[:], in_=k.cT), writes=["sc"])
        P.dma(lambda e: e.dma_start(out=ab[:], in_=k.ada_b[l]), writes=["ab"])
        P.op("act", lambda e: e.activation(out=sc[:], in_=sc[:], func=AF.Silu), reads=["sc"], writes=["sc"])
        for j in range(9):
            s = j % 2
            src = k.ada_w[l][:, j * 1024:(j + 1) * 1024].rearrange("(c p) n -> p c n", p=128)
            for hh in range(2):
                P.dma(lambda e, s=s, src=src, hh=hh: e.dma_start(out=wt[s][:, hh * 4:(hh + 1) * 4, :], in_=src[:, hh * 4:(hh + 1) * 4, :]), writes=[("adw", s, hh)])
            for m in range(8):
                for kc in range(8):
                    P.op("pe", lambda e, s=s, m=m, kc=kc: e.matmul(ps[s][:, m, :], lhsT=wt[s][:, kc, m * 128:(m + 1) * 128], rhs=sc[:, kc, :], start=(kc == 0), stop=(kc == 7)),
                         reads=[("adw", s, kc // 4), "sc"], writes=[("adps", s)])
            P.op("dve", lambda e, s=s, j=j: e.tensor_tensor(out=k.mods[:, j * 8:(j + 1) * 8, :], in0=ps[s][:], in1=ab[:, j * 8:(j + 1) * 8].rearrange("p (c o) -> p c o", o=1).to_broadcast([128, 8, NC]), op=ALU.add),
                 reads=[("adps", s), "ab"], writes=["mods"])
        P.end_stage()


def emit_norm_mod(k, es_tiles, x_t, h_t, n, A, Sh, cond, tag, sl):
    P = k.P
    sq, rs, tmp, psms = es_tiles
    P.op("act", lambda e: e.activation(out=sq[:, :, :n], in_=x_t[:, :, :n], func=AF.Square), reads=[("x", sl)], writes=["sq"])
    for c in range(8):
        P.op("pe", lambda e, c=c: e.matmul(psms[:, :n], lhsT=k.onesb[:], rhs=sq[:, c, :n], start=(c == 0), stop=(c == 7)), reads=["sq", "onesb"], writes=["psms"])
    P.op("act", lambda e: e.activation(out=rs[:, :n], in_=psms[:, :n], func=AF.Sqrt, bias=k.epsb[:, 0:1], scale=1.0), reads=["psms", "epsb"], writes=["rs0", "rs"])
    P.op("dve", lambda e: e.reciprocal(out=rs[:, :n], in_=rs[:, :n]), reads=["rs0"], writes=["rs"])
    for c in range(8):
        P.op("dve", lambda e, c=c: e.tensor_tensor(out=tmp[c % 2][:, :n], in0=x_t[:, c, :n], in1=rs[:, :n], op=ALU.mult), reads=[("x", sl), "rs"], writes=[("tmp", c % 2)])
        P.op("pool", lambda e, c=c: e.tensor_scalar(out=h_t[:, c, :n], in0=tmp[c % 2][:, :n], scalar1=A[:, c, cond:cond + 1], scalar2=Sh[:, c, cond:cond + 1], op0=ALU.mult, op1=ALU.add),
             reads=[("tmp", c % 2), tag], writes=["h"])


def alloc_norm_tiles(k, es, pfx, ts=512):
    nc = k.nc
    sq = es.enter_context(nc.sbuf_tensor(pfx + "sq", [128, 8, ts], BF16))
    rs = es.enter_context(nc.sbuf_tensor(pfx + "rs", [128, ts], F32))
    tmp = [es.enter_context(nc.sbuf_tensor(pfx + "tmp%d" % i, [128, ts], F32)) for i in range(2)]
    psms = es.enter_context(nc.psum_tensor(pfx + "psms", [128, 512], F32))
    k.epsb = es.enter_context(nc.sbuf_tensor(pfx + "epsb", [128, 1], F32))
    k.P.op("dve", lambda e: e.memset(k.epsb[:], EPS), writes=["epsb"])
    return sq, rs, tmp, psms


def emit_mod_scalars(k, es, pfx, l, jn, i_shift, i_scale, i_gate, gate_mul):
    nc, P, NC = k.nc, k.P, k.NC
    A = es.enter_context(nc.sbuf_tensor(pfx + "A", [128, 8, NC], F32))
    G = es.enter_context(nc.sbuf_tensor(pfx + "G", [128, 8, NC], F32))
    Sh = k.mods[:, i_shift * 8:(i_shift + 1) * 8, :]
    P.op("dve", lambda e: e.tensor_scalar(out=A[:], in0=k.mods[:, i_scale * 8:(i_scale + 1) * 8, :], scalar1=1.0, scalar2=None, op0=ALU.add), reads=["mods"], writes=["A0"])
    P.op("dve", lambda e: e.tensor_tensor(out=A[:], in0=A[:], in1=k.ng[:, l, jn, :].rearrange("p (c o) -> p c o", o=1).to_broadcast([128, 8, NC]), op=ALU.mult), reads=["A0", "ng"], writes=["modsc"])
    if i_gate is not None:
        P.op("dve", lambda e: e.tensor_scalar(out=G[:], in0=k.mods[:, i_gate * 8:(i_gate + 1) * 8, :], scalar1=gate_mul, scalar2=None, op0=ALU.mult), reads=["mods"], writes=["modsg"])
    return A, Sh, G


def load_w_bf16(k, dst, src_rows, ncols, tok, nk=8):
    P = k.P
    for kc in range(nk):
        P.dma(lambda e, kc=kc: e.dma_start(out=dst[:, kc, :], in_=src_rows[kc * 128:(kc + 1) * 128, :], max_dma_last_dim=4096),
              writes=[(tok, kc)], q="pool")


def stage_ffn(k, l, i):
    nc, P, NB, NC = k.nc, k.P, k.NB, k.NC
    jn = 0 if i == 0 else 2
    mi = (0, 1, 2) if i == 0 else (6, 7, 8)
    last = (l == 1 and i == 1)
    with ExitStack() as es:
        w1 = es.enter_context(nc.sbuf_tensor("f_w1", [128, 8, DFF], BF16))
        w3 = es.enter_context(nc.sbuf_tensor("f_w3", [128, 8, DFF], BF16))
        w2 = es.enter_context(nc.sbuf_tensor("f_w2", [128, NFF, D], BF16))
        xt = [es.enter_context(nc.sbuf_tensor("f_x%d" % s, [128, 8, 256], F32)) for s in range(2)]
        ht = es.enter_context(nc.sbuf_tensor("f_h", [128, 8, 256], BF16))
        hid = es.enter_context(nc.sbuf_tensor("f_hid", [128, NFF, 256], BF16))
        sl_t = [es.enter_context(nc.sbuf_tensor("f_s%d" % s, [128, 256], F32)) for s in range(2)]
        ps1 = [es.enter_context(nc.psum_tensor("f_ps1%d" % s, [128, 512], F32)) for s in range(2)]
        ps3 = [es.enter_context(nc.psum_tensor("f_ps3%d" % s, [128, 512], F32)) for s in range(2)]
        pso = [es.enter_context(nc.psum_tensor("f_pso%d" % s, [128, 512], F32)) for s in range(2)]
        ntiles = alloc_norm_tiles(k, es, "f_", 256)
        A, Sh, G = emit_mod_scalars(k, es, "f_", l, jn, mi[0], mi[1], mi[2], 0.5)
        load_w_bf16(k, w1, k.w1[l, i], DFF, "w1")
        load_w_bf16(k, w3, k.w3[l, i], DFF, "w3")
        load_w_bf16(k, w2, k.w2[l, i], D, "w2", nk=NFF)
        it = 0
        for (b, t0, n, cond) in token_tiles(NB, 256):
            if last and cond == NB:
                continue
            s = it % 2
            it += 1
            xsrc = k.XT[b].rearrange("(c p) t -> p c t", p=128)[:, :, t0:t0 + n]
            P.dma(lambda e, s=s, xsrc=xsrc, n=n: e.dma_start(out=xt[s][:, :, :n], in_=xsrc), reads=[("XT", b)], writes=[("x", s)])
            emit_norm_mod(k, ntiles, xt[s], ht, n, A, Sh, cond, "modsc", s)
            for f in range(NFF):
                q = f % 2
                for kc in range(8):
                    P.op("pe", lambda e, f=f, q=q, kc=kc, n=n: e.matmul(ps1[q][:, :n], lhsT=w1[:, kc, f * 128:(f + 1) * 128], rhs=ht[:, kc, :n], start=(kc == 0), stop=(kc == 7)),
                         reads=[("w1", kc), "h"], writes=[("ps1", q)])
                for kc in range(8):
                    P.op("pe", lambda e, f=f, q=q, kc=kc, n=n: e.matmul(ps3[q][:, :n], lhsT=w3[:, kc, f * 128:(f + 1) * 128], rhs=ht[:, kc, :n], start=(kc == 0), stop=(kc == 7)),
                         reads=[("w3", kc), "h"], writes=[("ps3", q)])
                P.op("act", lambda e, q=q, n=n: e.activation(out=sl_t[q][:, :n], in_=ps1[q][:, :n], func=AF.Silu), reads=[("ps1", q)], writes=[("sl", q)])
                P.op("dve", lambda e, q=q, f=f, n=n: e.tensor_tensor(out=hid[:, f, :n], in0=sl_t[q][:, :n], in1=ps3[q][:, :n], op=ALU.mult), reads=[("sl", q), ("ps3", q)], writes=[("hid", f)])
            for m in range(8):
                q = m % 2
                for f in range(NFF):
                    P.op("pe", lambda e, m=m, q=q, f=f, n=n: e.matmul(pso[q][:, :n], lhsT=w2[:, f, m * 128:(m + 1) * 128], rhs=hid[:, f, :n], start=(f == 0), stop=(f == NFF - 1)),
                         reads=[("w2", f), ("hid", f)], writes=[("pso", q)])
                P.op("dve", lambda e, m=m, q=q, s=s, n=n, cond=cond: e.scalar_tensor_tensor(out=xt[s][:, m, :n], in0=pso[q][:, :n], scalar=G[:, m, cond:cond + 1], in1=xt[s][:, m, :n], op0=ALU.mult, op1=ALU.add),
                     reads=[("pso", q), ("x", s), "modsg"], writes=[("x", s)])
            P.dma(lambda e, s=s, xsrc=xsrc, n=n: e.dma_start(out=xsrc, in_=xt[s][:, :, :n]), reads=[("x", s)], writes=[("XT", b)])
        P.end_stage()


def stage_inproj(k, l):
    nc, P, NB, NC = k.nc, k.P, k.NB, k.NC
    NCH = FM_W // 128
    with ExitStack() as es:
        wf = es.enter_context(nc.sbuf_tensor("p_wf", [128, 8, FM_W], BF16))
        wk = es.enter_context(nc.sbuf_tensor("p_wk", [128, 8, TOK_W], BF16))
        xt = [es.enter_context(nc.sbuf_tensor("p_x%d" % s, [128, 8, 512], F32)) for s in range(2)]
        ht = es.enter_context(nc.sbuf_tensor("p_h", [128, 8, 512], BF16))
        ot = [es.enter_context(nc.sbuf_tensor("p_o%d" % s, [128, 512], F32)) for s in range(4)]
        ps = [es.enter_context(nc.psum_tensor("p_ps%d" % s, [128, 512], F32)) for s in range(4)]
        ntiles = alloc_norm_tiles(k, es, "p_")
        A, Sh, G = emit_mod_scalars(k, es, "p_", l, 1, 3, 4, None, 1.0)
        load_w_bf16(k, wf, k.w_fm[l], FM_W, "wf")
        load_w_bf16(k, wk, k.w_tok[l], TOK_W, "wk")
        it = 0
        oi = 0
        for (b, t0, n, cond) in token_tiles(NB):
            s = it % 2
            it += 1
            xsrc = k.XT[b].rearrange("(c p) t -> p c t", p=128)[:, :, t0:t0 + n]
            P.dma(lambda e, s=s, xsrc=xsrc, n=n: e.dma_start(out=xt[s][:, :, :n], in_=xsrc), reads=[("XT", b)], writes=[("x", s)])
            emit_norm_mod(k, ntiles, xt[s], ht, n, A, Sh, cond, "modsc", s)
            for ch in range(NCH):
                q = oi % 4
                oi += 1
                for kc in range(8):
                    P.op("pe", lambda e, ch=ch, q=q, kc=kc, n=n: e.matmul(ps[q][:, :n], lhsT=wf[:, kc, ch * 128:(ch + 1) * 128], rhs=ht[:, kc, :n], start=(kc == 0), stop=(kc == 7)),
                         reads=[("wf", kc), "h"], writes=[("ps", q)])
                if oi % 2 == 0:
                    P.op("act", lambda e, q=q, n=n: e.activation(out=ot[q][:, :n], in_=ps[q][:, :n], func=AF.Identity), reads=[("ps", q)], writes=[("ot", q)])
                else:
                    P.op("dve", lambda e, q=q, n=n: e.tensor_copy(out=ot[q][:, :n], in_=ps[q][:, :n]), reads=[("ps", q)], writes=[("ot", q)])
                P.dma(lambda e, q=q, ch=ch, b=b, t0=t0, n=n: e.dma_start(out=k.PT[b, ch * 128:(ch + 1) * 128, t0:t0 + n], in_=ot[q][:, :n]), reads=[("ot", q)], writes=[("PT", b)])
            for tb in range(n // 128):
                q = oi % 4
                oi += 1
                for kc in range(8):
                    P.op("pe", lambda e, tb=tb, q=q, kc=kc: e.matmul(ps[q][:, :TOK_W], lhsT=ht[:, kc, tb * 128:(tb + 1) * 128], rhs=wk[:, kc, :], start=(kc == 0), stop=(kc == 7)),
                         reads=[("wk", kc), "h"], writes=[("ps", q)])
                P.op("dve", lambda e, q=q: e.tensor_copy(out=ot[q][:, :TOK_W], in_=ps[q][:, :TOK_W]), reads=[("ps", q)], writes=[("ot", q)])
                P.dma(lambda e, q=q, b=b, tb=tb, t0=t0: e.dma_start(out=k.TOK[b, t0 + tb * 128:t0 + (tb + 1) * 128, :], in_=ot[q][:, :TOK_W]), reads=[("ot", q)], writes=[("TOK", b)])
        P.end_stage()


def stage_attention(k, l):
    nc, P, NB = k.nc, k.P, k.NB
    NKB = T // 128
    with ExitStack() as es:
        al = lambda name, shape, dt=F32: es.enter_context(nc.sbuf_tensor("a_" + name, list(shape), dt))
        cos = al("cos", [64, SEQ]); sin = al("sin", [64, SEQ]); rm = al("rm", [64, 64]); o64 = al("o64", [64, 64])
        gq = al("gq", [64, 1]); gk = al("gk", [64, 1]); epsb = al("eps", [64, 1])
        onesb = al("onesb", [128, 64], BF16)
        raw = [al("raw%d" % i, [64, T]) for i in range(2)]
        kT = [al("kT%d" % i, [64, T], BF16) for i in range(2)]
        qT = [al("qT%d" % i, [64, T], BF16) for i in range(2)]
        vraw = al("vraw", [128, NKB, 128])
        vb = al("vb", [128, NKB, 128], BF16)
        sq = al("sq", [64, 512]); rs = al("rs", [64, 512]); kn = al("kn", [64, 512]); t1 = al("t1", [64, 512]); t2 = al("t2", [64, 512])
        pT = [al("pT%d" % i, [128, 512], BF16) for i in range(3)]
        rden = al("rden", [64, 512])
        ot = [al("ot%d" % i, [64, 512], BF16) for i in range(2)]
        psa = es.enter_context(nc.psum_tensor("a_psa", [64, 512], F32))
        psr = es.enter_context(nc.psum_tensor("a_psr", [64, 512], F32))
        pss = [es.enter_context(nc.psum_tensor("a_pss%d" % i, [128, 512], F32)) for i in range(3)]
        pso = es.enter_context(nc.psum_tensor("a_pso", [64, 512], F32))
        psd = es.enter_context(nc.psum_tensor("a_psd", [64, 512], F32))
        P.dma(lambda e: e.dma_start(out=cos[:], in_=k.rope_cos), writes=["cos"])
        P.dma(lambda e: e.dma_start(out=sin[:], in_=k.rope_sin), writes=["sin"])
        P.dma(lambda e: e.dma_start(out=rm[:], in_=k.rope_rm), writes=["rm"])
        P.dma(lambda e: e.dma_start(out=gq[:], in_=k.at_qn_g[l]), writes=["gq"])
        P.dma(lambda e: e.dma_start(out=gk[:], in_=k.at_kn_g[l]), writes=["gk"])
        P.op("dve", lambda e: e.memset(o64[:], 1.0 / 64), writes=["o64"])
        P.op("dve", lambda e: e.memset(epsb[:], EPS), writes=["epsb"])
        P.op("dve", lambda e: e.memset(onesb[:], 1.0), writes=["onesb"])
        cnt = {"raw": 0, "p": 0, "o": 0}

        def prep(b, row0, g_t, gtok, dst, dtok):
            ri = cnt["raw"] % 2
            cnt["raw"] += 1
            P.dma(lambda e: e.dma_start(out=raw[ri][:], in_=k.PT[b, row0:row0 + 64, :]), reads=[("PT", b)], writes=[("raw", ri)])
            for (t0, n) in [(0, CTX)] + [(CTX + i * 512, 512) for i in range(4)]:
                P.op("act", lambda e, t0=t0, n=n: e.activation(out=sq[:, :n], in_=raw[ri][:, t0:t0 + n], func=AF.Square), reads=[("raw", ri)], writes=["sq"])
                P.op("pe", lambda e, n=n: e.matmul(psa[:, :n], lhsT=o64[:], rhs=sq[:, :n], start=True, stop=True), reads=["sq", "o64"], writes=["psa"])
                P.op("act", lambda e, n=n: e.activation(out=rs[:, :n], in_=psa[:, :n], func=AF.Sqrt, bias=epsb[:, 0:1], scale=1.0), reads=["psa", "epsb"], writes=["rs0", "rs"])
                P.op("dve", lambda e, n=n: e.reciprocal(out=rs[:, :n], in_=rs[:, :n]), reads=["rs0"], writes=["rs"])
                if t0 == 0:
                    P.op("dve", lambda e, t0=t0, n=n: e.scalar_tensor_tensor(out=dst[:, t0:t0 + n], in0=raw[ri][:, t0:t0 + n], scalar=g_t[:, 0:1], in1=rs[:, :n], op0=ALU.mult, op1=ALU.mult),
                         reads=[("raw", ri), "rs", gtok], writes=[dtok])
                    continue
                P.op("dve", lambda e, t0=t0, n=n: e.scalar_tensor_tensor(out=kn[:, :n], in0=raw[ri][:, t0:t0 + n], scalar=g_t[:, 0:1], in1=rs[:, :n], op0=ALU.mult, op1=ALU.mult),
                     reads=[("raw", ri), "rs", gtok], writes=["kn"])
                P.op("pe", lambda e, n=n: e.matmul(psr[:, :n], lhsT=rm[:], rhs=kn[:, :n], start=True, stop=True), reads=["kn", "rm"], writes=["psr"])
                P.op("pool", lambda e, t0=t0, n=n: e.tensor_tensor(out=t1[:, :n], in0=kn[:, :n], in1=cos[:, t0 - CTX:t0 - CTX + n], op=ALU.mult), reads=["kn", "cos"], writes=["t1"])
                P.op("dve", lambda e, t0=t0, n=n: e.tensor_tensor(out=t2[:, :n], in0=psr[:, :n], in1=sin[:, t0 - CTX:t0 - CTX + n], op=ALU.mult), reads=["psr", "sin"], writes=["t2"])
                P.op("dve", lambda e, t0=t0, n=n: e.tensor_tensor(out=dst[:, t0:t0 + n], in0=t1[:, :n], in1=t2[:, :n], op=ALU.add), reads=["t1", "t2"], writes=[dtok])

        def do_tile(b, hq, kvh, qi, q0, nq, kbs):
            def s_mm(kb, pi):
                P.op("pe", lambda e: e.matmul(pss[pi][:, :nq], lhsT=kT[kvh][:, kb * 128:(kb + 1) * 128], rhs=qT[qi][:, q0:q0 + nq], start=True, stop=True),
                     reads=[("kT", kvh), ("qT", qi)], writes=[("pss", pi)])
            pis = []
            for j in range(len(kbs)):
                pis.append(cnt["p"] % 3)
                cnt["p"] += 1
            s_mm(kbs[0], pis[0])
            for j, kb in enumerate(kbs):
                if j + 1 < len(kbs):
                    s_mm(kbs[j + 1], pis[j + 1])
                pi = pis[j]
                P.op("act", lambda e, pi=pi: e.activation(out=pT[pi][:, :nq], in_=pss[pi][:, :nq], func=AF.Exp, scale=0.125), reads=[("pss", pi)], writes=[("pT", pi)])
                P.op("pe", lambda e, pi=pi, kb=kb, j=j: e.matmul(pso[:, :nq], lhsT=vb[:, kb, kvh * 64:(kvh + 1) * 64], rhs=pT[pi][:, :nq], start=(j == 0), stop=(j == len(kbs) - 1)),
                     reads=[("pT", pi), "vb"], writes=["pso"], inc=False)
                P.op("pe", lambda e, pi=pi, j=j: e.matmul(psd[:, :nq], lhsT=onesb[:], rhs=pT[pi][:, :nq], start=(j == 0), stop=(j == len(kbs) - 1)),
                     reads=[("pT", pi), "onesb"], writes=["psd"])
            oi = cnt["o"] % 2
            cnt["o"] += 1
            P.op("dve", lambda e: e.reciprocal(out=rden[:, :nq], in_=psd[:, :nq]), reads=["psd"], writes=["rden"])
            P.op("dve", lambda e: e.tensor_tensor(out=ot[oi][:, :nq], in0=pso[:, :nq], in1=rden[:, :nq], op=ALU.mult), reads=["pso", "rden"], writes=[("ot", oi)])
            P.dma(lambda e: e.dma_start(out=k.YT[b, 3, hq * 64:(hq + 1) * 64, q0:q0 + nq], in_=ot[oi][:, :nq]), reads=[("ot", oi)], writes=[("YT", b)])

        for b in range(NB):
            for kvh in range(2):
                prep(b, FM_AK + kvh * 64, gk, "gk", kT[kvh], ("kT", kvh))
            vsrc = k.TOK[b].rearrange("(blk p) c -> p blk c", p=128)[:, :, TK_AV:TK_AV + 128]
            P.dma(lambda e, vsrc=vsrc: e.dma_start(out=vraw[:], in_=vsrc), reads=[("TOK", b)], writes=["vraw"])
            P.op("pool", lambda e: e.tensor_copy(out=vb[:], in_=vraw[:]), reads=["vraw"], writes=["vb"])
            for hq in range(4):
                kvh = hq // 2
                qi = hq % 2
                prep(b, FM_AQ + hq * 64, gq, "gq", qT[qi], ("qT", qi))
                for (q0, nq, kbs) in [(0, CTX, [0, 1])] + [(CTX + i * 512, 512, list(range(NKB))) for i in range(4)]:
                    do_tile(b, hq, kvh, qi, q0, nq, kbs)
        P.end_stage()


def stage_hgrn2(k, l):
    nc, P, NB = k.nc, k.P, k.NB
    NCK = T // 64
    with ExitStack() as es:
        al = lambda name, shape, dt=F32: es.enter_context(nc.sbuf_tensor("h_" + name, list(shape), dt))
        m01 = al("m01", [128, T]); mk = al("mk", [128, 2, 64]); bd = al("bd", [128, 128])
        lg = al("lg", [128, 2, 2]); lb = al("lb", [128, 2]); oml = al("oml", [128, 2]); gn = al("gn", [128, 1]); epsb = al("eps", [128, 1])
        z = al("z", [128, T]); fgt = al("fgt", [128, T]); bb = al("bb", [128, T]); tmp = al("tmp", [128, T])
        ex = [al("ex%d" % i, [128, T]) for i in range(2)]
        q = al("q", [128, T]); kk = al("kk", [128, T]); kd = al("kd", [128, T]); O = al("O", [128, T])
        qt = al("qt", [128, T], BF16); ktI = [al("kt%d" % i, [128, T], BF16) for i in range(4)]; qd = al("qd", [128, T], BF16)
        dec = al("dec", [128, NCK, 1])
        Vb = al("Vb", [128, NCK, 256], BF16)
        kdT = [al("kdT%d" % i, [64, 128], BF16) for i in range(2)]
        scm = [al("scm%d" % i, [128, 64], BF16) for i in range(2)]
        S32 = al("S32", [128, 64]); S16 = [al("S16%d" % i, [128, 64], BF16) for i in range(2)]
        sq = al("sq", [128, 512]); rs = al("rs", [128, 512]); yb = [al("yb%d" % i, [128, 512], BF16) for i in range(2)]
        pst = [es.enter_context(nc.psum_tensor("h_pst%d" % i, [128, 512], F32)) for i in range(2)]
        pss = [es.enter_context(nc.psum_tensor("h_pss%d" % i, [128, 512], F32)) for i in range(2)]
        pso = [es.enter_context(nc.psum_tensor("h_pso%d" % i, [128, 512], F32)) for i in range(2)]
        pskv = [es.enter_context(nc.psum_tensor("h_pskv%d" % i, [128, 512], F32)) for i in range(2)]
        P.dma(lambda e: e.dma_start(out=mk[:], in_=k.hg_masks), writes=["mk"])
        P.dma(lambda e: e.dma_start(out=bd[:], in_=k.bd64), writes=["bd"])
        P.dma(lambda e: e.dma_start(out=lg[:], in_=k.hg_lb), writes=["lg"])
        P.dma(lambda e: e.dma_start(out=gn[:], in_=k.hg_norm_g[l]), writes=["gn"])
        P.op("dve", lambda e: e.memset(epsb[:], EPS), writes=["epsb"])
        P.op("dve", lambda e: e.memset(m01[:], 1.0), writes=["m01"])
        P.op("dve", lambda e: e.memset(m01[:].rearrange("p (c j) -> p c j", j=64)[:, :, 0:1], 0.0), writes=["m01"])
        for i4 in range(4):
            P.op("pool", lambda e, i4=i4: e.memset(ktI[i4][:], 0.0), writes=[("kt", i4)])
        if l == 0:
            P.op("dve", lambda e: e.memset(lb[:], 0.0), writes=["lb"])
            P.op("dve", lambda e: e.memset(oml[:], 1.0), writes=["oml"])
        else:
            P.op("dve", lambda e: e.tensor_tensor(out=lb[:], in0=lg[:, :, 1], in1=lg[:, :, 0], op=ALU.subtract), reads=["lg"], writes=["lb0"])
            P.op("act", lambda e: e.activation(out=lb[:], in_=lb[:], func=AF.Sigmoid), reads=["lb0"], writes=["lb", "lb0"])
            P.op("dve", lambda e: e.tensor_scalar(out=oml[:], in0=lb[:], scalar1=-1.0, scalar2=1.0, op0=ALU.mult, op1=ALU.add), reads=["lb"], writes=["oml"])
        cnt = {"c": 0, "y": 0}
        bb3 = bb[:].rearrange("p (c j) -> p c j", j=64)
        tmp3 = tmp[:].rearrange("p (c j) -> p c j", j=64)

        def do_chunk(b, hp, d, c, first):
            i = cnt["c"] % 2
            cnt["c"] += 1
            cs = slice(c * 64, (c + 1) * 64)
            P.op("pe", lambda e: e.transpose(pst[i][:64, :128], kd[:, cs], k.idn[:]), reads=["kd", "idn"], writes=[("pst", i)])
            P.op("act", lambda e: e.activation(out=kdT[i][:], in_=pst[i][:64, :128], func=AF.Identity), reads=[("pst", i)], writes=[("kdT", i)])
            for h2 in range(2):
                pb = h2 * 64
                for I in range(4):
                    ts_ = slice(c * 64 + 16 * I, c * 64 + 16 * I + 16)
                    P.op("pe", lambda e, pb=pb, I=I, ts_=ts_: e.matmul(pss[i][pb:pb + 64, 16 * I:16 * I + 16], lhsT=ktI[I][pb:pb + 64, cs], rhs=qt[pb:pb + 64, ts_], start=True, stop=True),
                         reads=[("kt", I), "qt"], writes=[("pss", i)], inc=(h2 == 1 and I == 3))
            for h2 in range(2):
                pb = h2 * 64
                h = hp * 2 + h2
                P.op("pe", lambda e, pb=pb, h=h: e.matmul(pskv[i][pb:pb + 64, :64], lhsT=kdT[i][:, pb:pb + 64], rhs=Vb[0:64, c, h * 64:(h + 1) * 64], start=True, stop=True),
                     reads=[("kdT", i), "Vb"], writes=[("pskv", i)], inc=(h2 == 1))
            P.op("dve", lambda e: e.tensor_tensor(out=scm[i][:], in0=pss[i][:, :64], in1=mk[:, d, :], op=ALU.mult), reads=[("pss", i), "mk"], writes=[("scm", i)])
            for h2 in range(2):
                pb = h2 * 64
                h = hp * 2 + h2
                P.op("pe", lambda e, pb=pb, h=h: e.matmul(pso[i][pb:pb + 64, :64], lhsT=Vb[pb:pb + 64, c, h * 64:(h + 1) * 64], rhs=scm[i][pb:pb + 64, :], start=True, stop=False),
                     reads=[("scm", i), "Vb"], writes=[("pso", i)], inc=False)
                P.op("pe", lambda e, pb=pb: e.matmul(pso[i][pb:pb + 64, :64], lhsT=S16[1 - i][pb:pb + 64, :], rhs=qd[pb:pb + 64, cs], start=False, stop=True),
                     reads=[("S16", 1 - i), "qd"], writes=[("pso", i)], inc=(h2 == 1))
            if d == 0:
                P.op("act", lambda e: e.activation(out=O[:, cs], in_=pso[i][:, :64], func=AF.Identity), reads=[("pso", i)], writes=["O"])
            else:
                P.op("dve", lambda e: e.tensor_tensor(out=O[:, cs], in0=O[:, cs], in1=pso[i][:, :64], op=ALU.add), reads=[("pso", i), "O"], writes=["O"])
            P.op("dve", lambda e: e.scalar_tensor_tensor(out=S32[:], in0=S32[:], scalar=dec[:, c, :], in1=pskv[i][:, :64], op0=ALU.mult, op1=ALU.add),
                 reads=["S32", "dec", ("pskv", i)], writes=["S32"])
            P.op("act", lambda e: e.activation(out=S16[i][:], in_=S32[:], func=AF.Identity), reads=["S32"], writes=[("S16", i)])

        def do_dir(b, hp, d):
            rev = (lambda ap: ap[:, ::-1]) if d == 1 else (lambda ap: ap)
            last = 63 if d == 0 else 0
            r0 = FM_HF + d * 256 + hp * 128
            P.dma(lambda e: e.dma_start(out=z[:], in_=k.PT[b, r0:r0 + 128, :]), reads=[("PT", b)], writes=["z"])
            P.op("act", lambda e: e.activation(out=fgt[:], in_=z[:], func=AF.Sigmoid), reads=["z"], writes=["fgt"])
            P.op("dve", lambda e: e.tensor_scalar(out=fgt[:], in0=fgt[:], scalar1=oml[:, hp:hp + 1], scalar2=lb[:, hp:hp + 1], op0=ALU.mult, op1=ALU.add), reads=["fgt", "oml", "lb"], writes=["fgt"])
            P.op("dve", lambda e: e.tensor_scalar(out=fgt[:], in0=fgt[:], scalar1=1e-30, scalar2=None, op0=ALU.max), reads=["fgt"], writes=["fgt"])
            P.op("act", lambda e: e.activation(out=z[:], in_=fgt[:], func=AF.Ln), reads=["fgt"], writes=["z"])
            P.op("dve", lambda e: e.tensor_scalar(out=kk[:], in0=fgt[:], scalar1=-1.0, scalar2=1.0, op0=ALU.mult, op1=ALU.add), reads=["fgt"], writes=["kk"])
            P.op("dve", lambda e: e.tensor_tensor_scan(out=rev(bb[:]), data0=m01[:], data1=rev(z[:]), initial=0.0, op0=ALU.mult, op1=ALU.add), reads=["z", "m01"], writes=["bb"])
            ref = 0 if d == 0 else 15
            bb4 = bb[:].rearrange("p (c i j) -> p c i j", i=4, j=16)
            tmp4 = tmp[:].rearrange("p (c i j) -> p c i j", i=4, j=16)
            P.op("dve", lambda e: e.tensor_tensor(out=tmp4, in0=bb4, in1=bb4[:, :, :, ref:ref + 1].to_broadcast([128, NCK, 4, 16]), op=ALU.subtract), reads=["bb"], writes=["tmp"])
            P.op("act", lambda e: e.activation(out=ex[0][:], in_=tmp[:], func=AF.Exp), reads=["tmp"], writes=[("ex", 0)])
            P.op("pool", lambda e: e.tensor_tensor(out=qt[:], in0=q[:], in1=ex[0][:], op=ALU.mult), reads=["q", ("ex", 0)], writes=["qt"])
            ex3 = [ex[i][:].rearrange("p (c j) -> p c j", j=64) for i in range(2)]
            kk3 = kk[:].rearrange("p (c j) -> p c j", j=64)
            for I in range(4):
                cs_ = slice(0, 16 * (I + 1)) if d == 0 else slice(16 * I, 64)
                w = cs_.stop - cs_.start
                e_ = ex3[I % 2][:, :, cs_]
                rp = 16 * I + ref
                kt3 = ktI[I][:].rearrange("p (c j) -> p c j", j=64)[:, :, cs_]
                P.op("dve", lambda e, cs_=cs_, w=w, rp=rp: e.scalar_tensor_tensor(out=tmp3[:, :, cs_], in0=bb3[:, :, cs_], scalar=-1.0, in1=bb3[:, :, rp:rp + 1].to_broadcast([128, NCK, w]), op0=ALU.mult, op1=ALU.add),
                     reads=["bb"], writes=["tmp"])
                P.op("dve", lambda e, cs_=cs_: e.tensor_scalar(out=tmp3[:, :, cs_], in0=tmp3[:, :, cs_], scalar1=60.0, scalar2=None, op0=ALU.min), reads=["tmp"], writes=["tmp"])
                P.op("act", lambda e, cs_=cs_, e_=e_: e.activation(out=e_, in_=tmp3[:, :, cs_], func=AF.Exp), reads=["tmp"], writes=[("ex", I % 2)])
                P.op("pool", lambda e, cs_=cs_, e_=e_, kt3=kt3: e.tensor_tensor(out=kt3, in0=kk3[:, :, cs_], in1=e_, op=ALU.mult), reads=["kk", ("ex", I % 2)], writes=[("kt", I)])
            P.op("act", lambda e: e.activation(out=ex[0][:], in_=bb[:], func=AF.Exp), reads=["bb"], writes=[("ex", 0)])
            P.op("dve", lambda e: e.tensor_tensor(out=qd[:], in0=q[:], in1=ex[0][:], op=ALU.mult), reads=["q", ("ex", 0)], writes=["qd"])
            P.op("dve", lambda e: e.tensor_tensor(out=tmp3, in0=bb3, in1=bb3[:, :, last:last + 1].to_broadcast([128, NCK, 64]), op=ALU.subtract), reads=["bb"], writes=["tmp"])
            P.op("act", lambda e: e.activation(out=ex[1][:], in_=tmp[:], func=AF.Exp, scale=-1.0), reads=["tmp"], writes=[("ex", 1)])
            P.op("pool", lambda e: e.tensor_tensor(out=kd[:], in0=kk[:], in1=ex[1][:], op=ALU.mult), reads=["kk", ("ex", 1)], writes=["kd"])
            P.op("act", lambda e: e.activation(out=dec[:], in_=bb3[:, :, last:last + 1], func=AF.Exp), reads=["bb"], writes=["dec"])
            if "dump" in k.opts and (b, hp, d) == k.opts["dump"]:
                for j, (tl, tk) in enumerate([(z, "z"), (bb, "bb"), (kd, "kd"), (kk, "kk"), (fgt, "fgt")]):
                    P.dma(lambda e, j=j, tl=tl: e.dma_start(out=k.dump[j], in_=tl[:]), reads=[tk], writes=[("dump", j)])
            P.op("dve", lambda e: e.memset(S32[:], 0.0), writes=["S32"])
            for i in range(2):
                P.op("dve", lambda e, i=i: e.memset(S16[i][:], 0.0), writes=[("S16", i)])
            order = list(range(NCK)) if d == 0 else [3, 2, 1, 0] + list(range(NCK - 1, 3, -1))
            for idx, c in enumerate(order):
                do_chunk(b, hp, d, c, idx == 0)

        def do_pair(b, hp):
            r0 = FM_HQ + hp * 128
            P.dma(lambda e: e.dma_start(out=q[:], in_=k.PT[b, r0:r0 + 128, :]), reads=[("PT", b)], writes=["q"])
            P.op("act", lambda e: e.activation(out=q[:], in_=q[:], func=AF.Silu), reads=["q"], writes=["q"])
            for d in range(2):
                do_dir(b, hp, d)
            if "dump" in k.opts and (b, hp) == k.opts["dump"][:2]:
                P.dma(lambda e: e.dma_start(out=k.dump[5], in_=O[:]), reads=["O"], writes=[("dump", 5)])
            g0 = FM_HG + hp * 128
            P.dma(lambda e: e.dma_start(out=z[:], in_=k.PT[b, g0:g0 + 128, :]), reads=[("PT", b)], writes=["z"])
            P.op("act", lambda e: e.activation(out=z[:], in_=z[:], func=AF.Sigmoid), reads=["z"], writes=["z"])
            for (t0, n) in [(0, CTX)] + [(CTX + i * 512, 512) for i in range(4)]:
                do_out(b, hp, t0, n)

        def do_out(b, hp, t0, n):
            yi = cnt["y"] % 2
            cnt["y"] += 1
            P.op("act", lambda e: e.activation(out=sq[:, :n], in_=O[:, t0:t0 + n], func=AF.Square), reads=["O"], writes=["sq"])
            P.op("pe", lambda e: e.matmul(pss[0][:, :n], lhsT=bd[:], rhs=sq[:, :n], start=True, stop=True), reads=["sq", "bd"], writes=[("pss", 0)])
            P.op("act", lambda e: e.activation(out=rs[:, :n], in_=pss[0][:, :n], func=AF.Sqrt, bias=epsb[:, 0:1], scale=1.0), reads=[("pss", 0), "epsb"], writes=["rs0", "rs"])
            P.op("dve", lambda e: e.reciprocal(out=rs[:, :n], in_=rs[:, :n]), reads=["rs0"], writes=["rs"])
            P.op("dve", lambda e: e.scalar_tensor_tensor(out=sq[:, :n], in0=O[:, t0:t0 + n], scalar=gn[:, 0:1], in1=rs[:, :n], op0=ALU.mult, op1=ALU.mult), reads=["O", "gn", "rs"], writes=["sq"])
            P.op("pool", lambda e: e.tensor_tensor(out=yb[yi][:, :n], in0=sq[:, :n], in1=z[:, t0:t0 + n], op=ALU.mult), reads=["sq", "z"], writes=[("yb", yi)])
            P.dma(lambda e: e.dma_start(out=k.YT[b, 2, hp * 128:(hp + 1) * 128, t0:t0 + n], in_=yb[yi][:, :n]), reads=[("yb", yi)], writes=[("YT", b)])

        for b in range(NB):
            vsrc = k.TOK[b].rearrange("(c s) v -> s c v", s=64)[:, :, TK_HV:TK_HV + 256]
            for h2 in range(2):
                P.dma(lambda e, h2=h2, vsrc=vsrc: e.dma_start(out=Vb[h2 * 64:(h2 + 1) * 64, :, :], in_=vsrc), reads=[("TOK", b)], writes=["Vb"], q="pool")
            for hp in range(2):
                do_pair(b, hp)
        P.end_stage()


class Slots:
    def __init__(self, aps, name, mod=0):
        self.aps, self.name, self.i, self.mod = aps, name, 0, mod

    def get(self):
        j = self.i % len(self.aps)
        self.i += 1
        return self.aps[j], (self.name, j % self.mod if self.mod else j)


def stage_deltanet(k, l):
    nc, P, NB = k.nc, k.P, k.NB
    NCK = T // 64
    P.serial = k.opts.get("serial", True)
    with ExitStack() as es:
        al = lambda name, shape, dt=F32: es.enter_context(nc.sbuf_tensor("d_" + name, list(shape), dt))
        gm64 = al("gm64", [64, 2, 2, 128]); gmbd = al("gmbd", [128, 2, 2, 128]); idn2 = al("idn2", [128, 64]); bd = al("bd", [128, 128])
        ones64 = al("ones64", [64, 128])
        cw = al("cw", [128, 6, 5]); alog = al("alog", [128, 8]); dtb = al("dtb", [128, 8]); gno = al("gno", [128, 64]); epsb = al("eps", [128, 1])
        qkv = [al("qkv%d" % i, [128, T]) for i in range(6)]
        xin = al("xin", [128, T]); acc = al("acc", [128, T])
        sq = al("sq", [128, 512]); rs = al("rs", [128, 512])
        ab = al("ab", [128, NCK, 16]); gt = al("gt", [128, NCK, 8]); bt = al("bt", [128, NCK, 8]); t8 = al("t8", [128, NCK, 8]); t8b = al("t8b", [128, NCK, 8])
        gcs = al("gcs", [128, NCK, 8]); gts = al("gts", [128, NCK, 8])
        gcBD = al("gcBD", [128, NCK, 4]); gtBD = al("gtBD", [128, NCK, 4]); bBD = al("bBD", [128, NCK, 4])
        eg = al("eg", [128, NCK, 4]); ekd = al("ekd", [128, NCK, 4]); glast = al("glast", [128, NCK, 4]); nbeta = al("nbeta", [128, NCK, 4]); wsc = al("wsc", [128, NCK, 4])
        u_all = al("u_all", [128, NCK, 64]); wT_all = al("wT_all", [128, NCK, 64]); kdec_all = al("kdec_all", [128, NCK, 64]); attnT_all = al("attnT_all", [128, NCK, 128])
        O_tok = al("O_tok", [128, NCK, 64]); ssq = al("ssq", [128, NCK]); S = al("S", [128, 64])
        yb = [al("yb%d" % i, [128, 512], BF16) for i in range(2)]
        wk = Slots([al("wk%d" % i, [128, 128])[:] for i in range(28)], "wk")
        wv = Slots([al("wv%d" % i, [128, 64])[:] for i in range(8)], "wv")
        wr = Slots([al("wr%d" % i, [128, 128])[:] for i in range(6)], "wr")
        banks = [es.enter_context(nc.psum_tensor("d_ps%d" % i, [128, 512], F32)) for i in range(8)]
        ps = Slots([banks[i][:, j * 128:(j + 1) * 128] for j in range(k.opts.get("psj", 4)) for i in range(7)], "ps", mod=7)
        psg = banks[7]
        for i in range(8):
            P.op("dve", lambda e, i=i: e.memset(banks[i][:], 0.0), writes=[("ps", j) for j in range(7)] + ["psg"])
        P.dma(lambda e: e.dma_start(out=gm64[:], in_=k.dn_gm64), writes=["gm64"])
        P.dma(lambda e: e.dma_start(out=gmbd[:], in_=k.dn_gmbd), writes=["gmbd"])
        P.dma(lambda e: e.dma_start(out=idn2[:], in_=k.dn_idn2), writes=["idn2"])
        P.dma(lambda e: e.dma_start(out=bd[:], in_=k.bd64), writes=["bd"])
        P.dma(lambda e: e.dma_start(out=cw[:], in_=k.dn_conv[l]), writes=["cw"])
        P.dma(lambda e: e.dma_start(out=alog[:], in_=k.dn_a_log[l]), writes=["alog"])
        P.dma(lambda e: e.dma_start(out=dtb[:], in_=k.dn_dt_bias[l]), writes=["dtb"])
        P.dma(lambda e: e.dma_start(out=gno[:], in_=k.dn_norm_g[l]), writes=["gno"])
        P.op("dve", lambda e: e.memset(epsb[:], EPS), writes=["epsb"])
        P.op("dve", lambda e: e.memset(ones64[:], 1.0), writes=["ones64"])
        P.op("act", lambda e: e.activation(out=alog[:], in_=alog[:], func=AF.Exp), reads=["alog"], writes=["alog"])
        P.op("dve", lambda e: e.tensor_scalar(out=alog[:], in0=alog[:], scalar1=-1.0, scalar2=None, op0=ALU.mult), reads=["alog"], writes=["alog"])
        cnt = {"y": 0}

        def prep_tile(b, ti):
            r0 = ti * 128
            P.dma(lambda e: e.dma_start(out=xin[:], in_=k.PT[b, r0:r0 + 128, :]), reads=[("PT", b)], writes=["xin"])
            P.op("dve", lambda e: e.tensor_scalar(out=acc[:], in0=xin[:], scalar1=cw[:, ti, 2:3], scalar2=None, op0=ALU.mult), reads=["xin", "cw"], writes=["acc"])
            for (s0, s1) in [(0, CTX), (CTX, T)]:
                for j in (0, 1, 3, 4):
                    sh = j - 2
                    o0, o1 = max(s0, s0 - sh), min(s1, s1 - sh)
                    P.op("dve", lambda e, j=j, sh=sh, o0=o0, o1=o1: e.scalar_tensor_tensor(out=acc[:, o0:o1], in0=xin[:, o0 + sh:o1 + sh], scalar=cw[:, ti, j:j + 1], in1=acc[:, o0:o1], op0=ALU.mult, op1=ALU.add),
                         reads=["xin", "cw", "acc"], writes=["acc"])
            dst = qkv[ti]
            if ti >= 4:
                P.op("act", lambda e: e.activation(out=dst[:], in_=acc[:], func=AF.Silu), reads=["acc"], writes=[("qkv", ti)])
                return
            P.op("act", lambda e: e.activation(out=acc[:], in_=acc[:], func=AF.Silu), reads=["acc"], writes=["acc"])
            qs = 0.125 if ti < 2 else 1.0
            for (t0, n) in [(0, CTX)] + [(CTX + i * 512, 512) for i in range(4)]:
                norm_tile(dst, ti, t0, n, qs)

        def norm_tile(dst, ti, t0, n, qs):
            pp, pt = ps.get()
            bank_ap = banks[0]
            P.op("act", lambda e: e.activation(out=sq[:, :n], in_=acc[:, t0:t0 + n], func=AF.Square), reads=["acc"], writes=["sq"])
            P.op("pe", lambda e: e.matmul(psg[:, :n], lhsT=bd[:], rhs=sq[:, :n], start=True, stop=True), reads=["sq", "bd"], writes=["psg"])
            P.op("act", lambda e: e.activation(out=rs[:, :n], in_=psg[:, :n], func=AF.Sqrt, bias=epsb[:, 0:1], scale=64.0), reads=["psg", "epsb"], writes=["rs0", "rs"])
            P.op("dve", lambda e: e.reciprocal(out=rs[:, :n], in_=rs[:, :n]), reads=["rs0"], writes=["rs"])
            P.op("dve", lambda e: e.scalar_tensor_tensor(out=dst[:, t0:t0 + n], in0=acc[:, t0:t0 + n], scalar=qs, in1=rs[:, :n], op0=ALU.mult, op1=ALU.mult), reads=["acc", "rs"], writes=[("qkv", ti)])

        def prep_gates(b):
            src = k.TOK[b].rearrange("(c s) v -> s c v", s=64)[:, :, TK_A:TK_A + 16]
            for h2 in range(2):
                P.dma(lambda e, h2=h2: e.dma_start(out=ab[h2 * 64:(h2 + 1) * 64, :, :], in_=src), reads=[("TOK", b)], writes=["ab"])
            a3, b3 = ab[:, :, 0:8], ab[:, :, 8:16]
            bc8 = lambda t: t[:, :].rearrange("p (o h) -> p o h", o=1).to_broadcast([128, NCK, 8])
            P.op("dve", lambda e: e.tensor_tensor(out=t8[:], in0=a3, in1=bc8(dtb), op=ALU.add), reads=["ab", "dtb"], writes=["t8"])
            P.op("act", lambda e: e.activation(out=t8b[:], in_=t8[:], func=AF.Abs), reads=["t8"], writes=["t8b"])
            P.op("act", lambda e: e.activation(out=t8b[:], in_=t8b[:], func=AF.Exp, scale=-1.0), reads=["t8b"], writes=["t8b"])
            P.op("dve", lambda e: e.tensor_scalar(out=t8b[:], in0=t8b[:], scalar1=1.0, scalar2=None, op0=ALU.add), reads=["t8b"], writes=["t8b"])
            P.op("act", lambda e: e.activation(out=t8b[:], in_=t8b[:], func=AF.Ln), reads=["t8b"], writes=["t8b"])
            P.op("dve", lambda e: e.tensor_scalar(out=t8[:], in0=t8[:], scalar1=0.0, scalar2=None, op0=ALU.max), reads=["t8"], writes=["t8"])
            P.op("dve", lambda e: e.tensor_tensor(out=t8[:], in0=t8[:], in1=t8b[:], op=ALU.add), reads=["t8", "t8b"], writes=["t8"])
            P.op("dve", lambda e: e.tensor_tensor(out=gt[:], in0=t8[:], in1=bc8(alog), op=ALU.mult), reads=["t8", "alog"], writes=["gt"])
            P.op("act", lambda e: e.activation(out=bt[:], in_=b3, func=AF.Sigmoid), reads=["ab"], writes=["bt"])
            for d in range(2):
                P.op("pe", lambda e, d=d: e.matmul(psg[:, d * 144:(d + 1) * 144], lhsT=gm64[:, d, 0, :], rhs=gt[0:64, :, d * 4:(d + 1) * 4], start=True, stop=True), reads=["gm64", "gt"], writes=["psg"])
            P.op("dve", lambda e: e.tensor_copy(out=gcs[:].rearrange("p c (d h) -> p d c h", d=2), in_=psg[:, 0:288].rearrange("p (d c h) -> p d c h", d=2, h=4)), reads=["psg"], writes=["gcs"])
            for d in range(2):
                P.op("pe", lambda e, d=d: e.matmul(psg[:, d * 144:(d + 1) * 144], lhsT=ones64[:], rhs=gt[0:64, :, d * 4:(d + 1) * 4], start=True, stop=True), reads=["ones64", "gt"], writes=["psg"])
            P.op("dve", lambda e: e.tensor_copy(out=gts[:].rearrange("p c (d h) -> p d c h", d=2), in_=psg[:, 0:288].rearrange("p (d c h) -> p d c h", d=2, h=4)), reads=["psg"], writes=["gts"])
            for m in range(4):
                d, hp = m // 2, m % 2
                for h2 in range(2):
                    col = d * 4 + hp * 2 + h2
                    rr = slice(h2 * 64, (h2 + 1) * 64)
                    P.op("dve", lambda e, m=m, col=col, rr=rr: e.tensor_copy(out=gcBD[rr, :, m:m + 1], in_=gcs[rr, :, col:col + 1]), reads=["gcs"], writes=["gcBD"])
                    P.op("dve", lambda e, m=m, col=col, rr=rr: e.tensor_copy(out=gtBD[rr, :, m:m + 1], in_=gts[rr, :, col:col + 1]), reads=["gts"], writes=["gtBD"])
                    P.op("dve", lambda e, m=m, col=col, rr=rr: e.tensor_copy(out=bBD[rr, :, m:m + 1], in_=bt[rr, :, col:col + 1]), reads=["bt"], writes=["bBD"])
            P.op("act", lambda e: e.activation(out=eg[:], in_=gcBD[:], func=AF.Exp), reads=["gcBD"], writes=["eg"])
            P.op("act", lambda e: e.activation(out=glast[:], in_=gtBD[:], func=AF.Exp), reads=["gtBD"], writes=["glast"])
            P.op("dve", lambda e: e.tensor_tensor(out=ekd[:], in0=gtBD[:], in1=gcBD[:], op=ALU.subtract), reads=["gtBD", "gcBD"], writes=["ekd"])
            P.op("act", lambda e: e.activation(out=ekd[:], in_=ekd[:], func=AF.Exp), reads=["ekd"], writes=["ekd"])
            P.op("dve", lambda e: e.tensor_scalar(out=nbeta[:], in0=bBD[:], scalar1=-1.0, scalar2=None, op0=ALU.mult), reads=["bBD"], writes=["nbeta"])
            P.op("dve", lambda e: e.tensor_tensor(out=wsc[:], in0=bBD[:], in1=eg[:], op=ALU.mult), reads=["bBD", "eg"], writes=["wsc"])

        def phase_a_steps(m, c):
            d, hp = m // 2, m % 2
            qn, kn, vn = qkv[hp], qkv[2 + hp], qkv[4 + hp]
            qtk, ktk, vtk = ("qkv", hp), ("qkv", 2 + hp), ("qkv", 4 + hp)
            cs = slice(c * 64, (c + 1) * 64)
            hd0 = d * 4 + hp * 2
            X = {}

            def s1():
                Gm, Gmt = wk.get()
                Gi, Git = wk.get()
                g2 = gt[0:64, c, hd0:hd0 + 2].rearrange("p (h o) -> p h o", o=1).to_broadcast([64, 2, 64])
                P.op("dve", lambda e: e.tensor_tensor(out=Gm[0:64, :].rearrange("p (h j) -> p h j", h=2), in0=g2, in1=gm64[:, d, 1, :].rearrange("p (h j) -> p h j", h=2), op=ALU.mult), reads=["gt", "gm64"], writes=[Gmt])
                P.op("dve", lambda e: e.tensor_tensor(out=Gi[0:64, :].rearrange("p (h j) -> p h j", h=2), in0=g2, in1=gm64[:, d, 0, :].rearrange("p (h j) -> p h j", h=2), op=ALU.mult), reads=["gt", "gm64"], writes=[Git])
                pD, pDt = ps.get()
                pDT, pDTt = ps.get()
                P.op("pe", lambda e: e.matmul(pD, lhsT=gm64[:, d, 0, :], rhs=Gm[0:64, :], start=True, stop=True), reads=["gm64", Gmt], writes=[pDt], inc=False)
                P.op("pe", lambda e: e.matmul(pDT, lhsT=gm64[:, d, 1, :], rhs=Gi[0:64, :], start=True, stop=True), reads=["gm64", Git], writes=[pDTt])
                pKK, pKKt = ps.get()
                pQK, pQKt = ps.get()
                pTok, pTokt = ps.get()
                for h2 in range(2):
                    pb = h2 * 64
                    P.op("pe", lambda e, pb=pb: e.matmul(pKK[pb:pb + 64, pb:pb + 64], lhsT=kn[pb:pb + 64, cs], rhs=kn[pb:pb + 64, cs], start=True, stop=True), reads=[ktk], writes=[pKKt], inc=False)
                    P.op("pe", lambda e, pb=pb: e.matmul(pQK[pb:pb + 64, pb:pb + 64], lhsT=kn[pb:pb + 64, cs], rhs=qn[pb:pb + 64, cs], start=True, stop=True), reads=[ktk, qtk], writes=[pQKt], inc=False)
                    P.op("pe", lambda e, pb=pb: e.matmul(pTok[pb:pb + 64, 0:64], lhsT=kn[pb:pb + 64, cs], rhs=idn2[pb:pb + 64, :], start=True, stop=True), reads=[ktk, "idn2"], writes=[pTokt], inc=False)
                    P.op("pe", lambda e, pb=pb: e.matmul(pTok[pb:pb + 64, 64:128], lhsT=vn[pb:pb + 64, cs], rhs=idn2[pb:pb + 64, :], start=True, stop=True), reads=[vtk, "idn2"], writes=[pTokt], inc=(h2 == 1))
                X.update(pD=pD, pDt=pDt, pDT=pDT, pDTt=pDTt, pKK=pKK, pKKt=pKKt, pQK=pQK, pQKt=pQKt, pTok=pTok, pTokt=pTokt)

            def s2():
                x = dict(X)
                D, Dt = wk.get()
                DT, DTt = wk.get()
                X["n"] = X.get("n", 0) + 1
                if X["n"] <= k.opts.get("s2n", 99):
                    P.op("act", lambda e: e.activation(out=D, in_=x["pD"], func=AF.Exp), reads=[x["pDt"]], writes=[Dt])
                X["n"] = X.get("n", 0) + 1
                if X["n"] <= k.opts.get("s2n", 99):
                    P.op("act", lambda e: e.activation(out=DT, in_=x["pDT"], func=AF.Exp), reads=[x["pDTt"]], writes=[DTt])
                X["n"] = X.get("n", 0) + 1
                if X["n"] <= k.opts.get("s2n", 99):
                    P.op("pool", lambda e: e.tensor_tensor(out=D, in0=D, in1=gmbd[:, d, 0, :], op=ALU.mult), reads=[Dt, "gmbd"], writes=[Dt])
                X["n"] = X.get("n", 0) + 1
                if X["n"] <= k.opts.get("s2n", 99):
                    P.op("pool", lambda e: e.tensor_tensor(out=DT, in0=DT, in1=gmbd[:, d, 1, :], op=ALU.mult), reads=[DTt, "gmbd"], writes=[DTt])
                N, Nt = wk.get()
                X["n"] = X.get("n", 0) + 1
                if X["n"] <= k.opts.get("s2n", 99):
                    P.op("dve", lambda e: e.scalar_tensor_tensor(out=N, in0=x["pKK"], scalar=nbeta[:, c, m:m + 1], in1=D, op0=ALU.mult, op1=ALU.mult), reads=[x["pKKt"], "nbeta", Dt], writes=[Nt])
                X["n"] = X.get("n", 0) + 1
                if X["n"] <= k.opts.get("s2n", 99):
                    P.op("dve", lambda e: e.tensor_tensor(out=attnT_all[:, c, :], in0=x["pQK"], in1=DT, op=ALU.mult), reads=[x["pQKt"], DTt], writes=[("attnT", c)])
                rhs, rhst = wr.get()
                X["n"] = X.get("n", 0) + 1
                if X["n"] <= k.opts.get("s2n", 99):
                    P.op("dve", lambda e: e.tensor_scalar(out=rhs[:, 0:64], in0=x["pTok"][:, 0:64], scalar1=wsc[:, c, m:m + 1], scalar2=None, op0=ALU.mult), reads=[x["pTokt"], "wsc"], writes=[rhst])
                X["n"] = X.get("n", 0) + 1
                if X["n"] <= k.opts.get("s2n", 99):
                    P.op("dve", lambda e: e.tensor_scalar(out=rhs[:, 64:128], in0=x["pTok"][:, 64:128], scalar1=bBD[:, c, m:m + 1], scalar2=None, op0=ALU.mult), reads=[x["pTokt"], "bBD"], writes=[rhst])
                X["n"] = X.get("n", 0) + 1
                if X["n"] <= k.opts.get("s2n", 99):
                    P.op("dve", lambda e: e.tensor_scalar(out=kdec_all[:, c, :], in0=x["pTok"][:, 0:64], scalar1=ekd[:, c, m:m + 1], scalar2=None, op0=ALU.mult), reads=[x["pTokt"], "ekd"], writes=[("kdec", c)])
                X.update(N=N, Nt=Nt, rhs=rhs, rhst=rhst)

            def s3():
                x = dict(X)
                pNT, pNTt = ps.get()
                P.op("pe", lambda e: e.transpose(pNT, x["N"], k.idn[:]), reads=[x["Nt"], "idn"], writes=[pNTt])
                X.update(pNT=pNT, pNTt=pNTt)

            def s4():
                x = dict(X)
                PT_, PTt = wk.get()
                XT, XTt = wk.get()
                P.op("act", lambda e: e.activation(out=PT_, in_=x["pNT"], func=AF.Identity), reads=[x["pNTt"]], writes=[PTt])
                P.op("dve", lambda e: e.tensor_tensor(out=XT, in0=x["pNT"], in1=k.idn[:], op=ALU.add), reads=[x["pNTt"], "idn"], writes=[XTt])
                X.update(P=x["N"], Pt=x["Nt"], PT=PT_, PTt=PTt, XT=XT, XTt=XTt)

            def lvl_mm(kk):
                def f():
                    x = dict(X)
                    pP, pPt = ps.get()
                    P.op("pe", lambda e: e.matmul(pP, lhsT=x["PT"], rhs=x["P"], start=True, stop=True), reads=[x["PTt"], x["Pt"]], writes=[pPt], inc=(kk == 5))
                    X.update(pP=pP, pPt=pPt)
                    if kk < 5:
                        pPT, pPTt = ps.get()
                        P.op("pe", lambda e: e.matmul(pPT, lhsT=x["P"], rhs=x["PT"], start=True, stop=True), reads=[x["PTt"], x["Pt"]], writes=[pPTt])
                        X.update(pPT=pPT, pPTt=pPTt)
                return f

            def lvl_ev(kk):
                def f():
                    x = dict(X)
                    nP, nPt = wk.get()
                    P.op("act", lambda e: e.activation(out=nP, in_=x["pP"], func=AF.Identity), reads=[x["pPt"]], writes=[nPt])
                    X.update(P=nP, Pt=nPt)
                    if kk < 5:
                        nPT, nPTt = wk.get()
                        P.op("dve", lambda e: e.tensor_copy(out=nPT, in_=x["pPT"]), reads=[x["pPTt"]], writes=[nPTt])
                        X.update(PT=nPT, PTt=nPTt)
                    pX, pXt = ps.get()
                    P.op("pe", lambda e: e.matmul(pX, lhsT=nP, rhs=x["XT"], start=True, stop=True), reads=[nPt, x["XTt"]], writes=[pXt])
                    X.update(pX=pX, pXt=pXt)
                return f

            def lvl_acc(kk):
                def f():
                    x = dict(X)
                    nX, nXt = wk.get()
                    P.op("dve", lambda e: e.tensor_tensor(out=nX, in0=x["XT"], in1=x["pX"], op=ALU.add), reads=[x["XTt"], x["pXt"]], writes=[nXt])
                    X.update(XT=nX, XTt=nXt)
                return f

            def s_sol():
                x = dict(X)
                pU, pUt = ps.get()
                pW, pWt = ps.get()
                P.op("pe", lambda e: e.matmul(pU[:, 0:64], lhsT=x["XT"], rhs=x["rhs"][:, 64:128], start=True, stop=True), reads=[x["XTt"], x["rhst"]], writes=[pUt], inc=False)
                for h2 in range(2):
                    pb = h2 * 64
                    P.op("pe", lambda e, pb=pb: e.matmul(pW[pb:pb + 64, 0:64], lhsT=x["rhs"][pb:pb + 64, 0:64], rhs=x["XT"][pb:pb + 64, pb:pb + 64], start=True, stop=True), reads=[x["XTt"], x["rhst"]], writes=[pWt], inc=(h2 == 1))
                X.update(pU=pU, pUt=pUt, pW=pW, pWt=pWt)

            def s_solev():
                x = dict(X)
                P.op("act", lambda e: e.activation(out=u_all[:, c, :], in_=x["pU"][:, 0:64], func=AF.Identity), reads=[x["pUt"]], writes=[("u", c)])
                P.op("dve", lambda e: e.tensor_copy(out=wT_all[:, c, :], in_=x["pW"][:, 0:64]), reads=[x["pWt"]], writes=[("wT", c)])

            steps = [s1, s2, s3, s4]
            for kk in range(1, 6):
                steps += [lvl_mm(kk), lvl_ev(kk), lvl_acc(kk)]
            steps += [s_sol, s_solev]
            return steps[:k.opts.get("dn_steps", 99)]

        def phase_b_chunk(m, c, first_dir):
            d, hp = m // 2, m % 2
            qn = qkv[hp]
            cs = slice(c * 64, (c + 1) * 64)
            p1, p1t = ps.get()
            p2, p2t = ps.get()
            for h2 in range(2):
                pb = h2 * 64
                P.op("pe", lambda e, pb=pb: e.matmul(p1[pb:pb + 64, 0:64], lhsT=wT_all[pb:pb + 64, c, :], rhs=S[pb:pb + 64, :], start=True, stop=True), reads=[("wT", c), "S"], writes=[p1t], inc=False)
                P.op("pe", lambda e, pb=pb: e.matmul(p2[pb:pb + 64, 0:64], lhsT=qn[pb:pb + 64, cs], rhs=S[pb:pb + 64, :], start=True, stop=True), reads=[("qkv", hp), "S"], writes=[p2t], inc=(h2 == 1))
            vn_, vnt = wv.get()
            P.op("dve", lambda e: e.tensor_tensor(out=vn_, in0=u_all[:, c, :], in1=p1[:, 0:64], op=ALU.subtract), reads=[("u", c), p1t], writes=[vnt])
            p3, p3t = ps.get()
            p4, p4t = ps.get()
            P.op("pe", lambda e: e.matmul(p3[:, 0:64], lhsT=attnT_all[:, c, :], rhs=vn_, start=True, stop=True), reads=[("attnT", c), vnt], writes=[p3t], inc=False)
            for h2 in range(2):
                pb = h2 * 64
                P.op("pe", lambda e, pb=pb: e.matmul(p4[pb:pb + 64, 0:64], lhsT=kdec_all[pb:pb + 64, c, :], rhs=vn_[pb:pb + 64, :], start=True, stop=True), reads=[("kdec", c), vnt], writes=[p4t], inc=(h2 == 1))
            t_, tt_ = wv.get()
            P.op("dve", lambda e: e.tensor_scalar(out=t_, in0=p2[:, 0:64], scalar1=eg[:, c, m:m + 1], scalar2=None, op0=ALU.mult), reads=[p2t, "eg"], writes=[tt_])
            if first_dir:
                P.op("dve", lambda e: e.tensor_tensor(out=O_tok[:, c, :], in0=t_, in1=p3[:, 0:64], op=ALU.add), reads=[tt_, p3t], writes=[("O", c)])
            else:
                P.op("dve", lambda e: e.tensor_tensor(out=t_, in0=t_, in1=p3[:, 0:64], op=ALU.add), reads=[tt_, p3t], writes=[tt_])
                P.op("pool", lambda e: e.tensor_tensor(out=O_tok[:, c, :], in0=O_tok[:, c, :], in1=t_, op=ALU.add), reads=[tt_, ("O", c)], writes=[("O", c)])
            P.op("dve", lambda e: e.scalar_tensor_tensor(out=S[:], in0=S[:], scalar=glast[:, c, m:m + 1], in1=p4[:, 0:64], op0=ALU.mult, op1=ALU.add), reads=["S", "glast", p4t], writes=["S"])

        def out_phase(b, hp):
            z, yfm = xin, acc
            r0 = FM_Z + hp * 128
            P.dma(lambda e: e.dma_start(out=z[:], in_=k.PT[b, r0:r0 + 128, :]), reads=[("PT", b)], writes=["xin"])
            P.op("act", lambda e: e.activation(out=z[:], in_=z[:], func=AF.Silu), reads=["xin"], writes=["xin"])
            allO = [("O", c) for c in range(NCK)]
            P.op("dve", lambda e: e.tensor_tensor(out=u_all[:], in0=O_tok[:], in1=O_tok[:], op=ALU.mult), reads=allO, writes=[("u", c) for c in range(NCK)])
            P.op("dve", lambda e: e.tensor_reduce(out=ssq[:], in_=u_all[:], axis=AX.X, op=ALU.add), reads=[("u", c) for c in range(NCK)], writes=["ssq"])
            P.op("act", lambda e: e.activation(out=ssq[:], in_=ssq[:], func=AF.Sqrt, bias=epsb[:, 0:1], scale=1.0 / 64), reads=["ssq", "epsb"], writes=["ssq"])
            P.op("dve", lambda e: e.reciprocal(out=ssq[:], in_=ssq[:]), reads=["ssq"], writes=["ssq"])
            P.op("dve", lambda e: e.tensor_tensor(out=O_tok[:], in0=O_tok[:], in1=ssq[:].rearrange("p (c o) -> p c o", o=1).to_broadcast([128, NCK, 64]), op=ALU.mult), reads=allO + ["ssq"], writes=allO)
            P.op("dve", lambda e: e.tensor_tensor(out=O_tok[:], in0=O_tok[:], in1=gno[:].rearrange("p (o v) -> p o v", o=1).to_broadcast([128, NCK, 64]), op=ALU.mult), reads=allO + ["gno"], writes=allO)
            for c in range(NCK):
                out_chunk(c)
            for (t0, n) in [(0, CTX)] + [(CTX + i * 512, 512) for i in range(4)]:
                out_tile(b, hp, t0, n)

        def out_chunk(c):
            pp, ppt = ps.get()
            for h2 in range(2):
                pb = h2 * 64
                P.op("pe", lambda e, pb=pb: e.matmul(pp[pb:pb + 64, 0:64], lhsT=O_tok[pb:pb + 64, c, :], rhs=idn2[pb:pb + 64, :], start=True, stop=True), reads=[("O", c), "idn2"], writes=[ppt], inc=(h2 == 1))
            P.op("act", lambda e: e.activation(out=acc[:, c * 64:(c + 1) * 64], in_=pp[:, 0:64], func=AF.Identity), reads=[ppt], writes=["acc"])

        def out_tile(b, hp, t0, n):
            yi = cnt["y"] % 2
            cnt["y"] += 1
            P.op("dve", lambda e: e.tensor_tensor(out=yb[yi][:, :n], in0=acc[:, t0:t0 + n], in1=xin[:, t0:t0 + n], op=ALU.mult), reads=["acc", "xin"], writes=[("yb", yi)])
            P.dma(lambda e: e.dma_start(out=k.YT[b, 0, hp * 128:(hp + 1) * 128, t0:t0 + n], in_=yb[yi][:, :n]), reads=[("yb", yi)], writes=[("YT", b)])

        G = 3
        for b in range(NB):
            for ti in range(6):
                prep_tile(b, ti)
            prep_gates(b)
            lim = k.opts.get("dn_lim", 99)
            if lim == 0:
                continue
            for hp in range(2):
                for d in range(2):
                    m = d * 2 + hp
                    for c0 in range(0, NCK if lim >= 2 else G, G):
                        lists = [phase_a_steps(m, c) for c in range(c0, min(NCK, c0 + G))]
                        for si in range(len(lists[0])):
                            for lst in lists:
                                lst[si]()
                    if lim < 3:
                        continue
                    P.op("dve", lambda e: e.memset(S[:], 0.0), writes=["S"])
                    order = list(range(NCK)) if d == 0 else [3, 2, 1, 0] + list(range(NCK - 1, 3, -1))
                    for c in order:
                        phase_b_chunk(m, c, d == 0)
                    if "dump" in k.opts and (b, m) == k.opts["dump"][:2]:
                        for j in range(6):
                            P.dma(lambda e, j=j: e.dma_start(out=k.dump[j], in_=qkv[j][:]), reads=[("qkv", j)], writes=[("dump", j)])
                        for j, (tl, tk) in enumerate([(u_all, "u"), (wT_all, "wT"), (kdec_all, "kdec"), (O_tok, "O")]):
                            P.dma(lambda e, j=j, tl=tl: e.dma_start(out=k.dump[6 + j], in_=tl[:].rearrange("p c v -> p (c v)")), reads=[(tk, c) for c in range(NCK)], writes=[("dump", 6 + j)])
                        P.dma(lambda e: e.dma_start(out=k.dump[10:12].rearrange("a p t -> p a t"), in_=attnT_all[:].rearrange("p (a c) v -> p a (c v)", a=2)), reads=[("attnT", c) for c in range(NCK)], writes=[("dump", 10)])
                        for j, (tl, tk, w) in enumerate([(gt, "gt", 288), (bt, "bt", 288), (gcBD, "gcBD", 144), (gtBD, "gtBD", 144), (bBD, "bBD", 144)]):
                            P.dma(lambda e, j=j, tl=tl, w=w: e.dma_start(out=k.dump[12, :, j * 300:j * 300 + w], in_=tl[:].rearrange("p c v -> p (c v)")), reads=[tk], writes=[("dump", 12, j)])
                if lim >= 4:
                    out_phase(b, hp)
        P.end_stage()


def stage_s5(k, l):
    nc, P, NB = k.nc, k.P, k.NB
    HALF_PI = float(np.pi / 2)
    P.serial = k.opts.get("serial", True)
    with ExitStack() as es:
        al = lambda name, shape, dt=F32: es.enter_context(nc.sbuf_tensor("s_" + name, list(shape), dt))
        lre = al("lre", [128, 16]); lim = al("lim", [128, 16]); stp = al("stp", [128, 16]); mag = al("mag", [128, 16])
        cth = al("cth", [128, 16]); sth = al("sth", [128, 16]); t1 = al("t1", [128, 16]); t2 = al("t2", [128, 16]); t3 = al("t3", [128, 16])
        are = al("are", [128, 16]); aim = al("aim", [128, 16]); cfr = al("cfr", [128, 16]); cfi = al("cfi", [128, 16]); hpi = al("hpi", [128, 1])
        pwc = al("pwc", [128, 16, 12]); pws = al("pws", [128, 16, 12])
        bre = al("bre", [128, 8, 16]); bim = al("bim", [128, 8, 16]); cre = al("cre", [128, 8, 16]); cim = al("cim", [128, 8, 16])
        bb_all = al("bb_all", [128, 32, 16]); bd_all = al("bd_all", [128, 32, 32]); tb = [al("tb%d" % i, [128, 8, 16]) for i in range(4)]
        W_all = al("W_all", [32, 32, 128]); cw_all = al("cw_all", [128, 8, 2, 128])
        dsk = al("dsk", [128, 2]); wgl = al("wgl", [128, 2, 512], BF16)
        cs = [al("cs%d" % i, [128, T]) for i in range(2)]; sn = [al("sn%d" % i, [128, T]) for i in range(2)]
        xr = al("xr", [128, T]); xi = al("xi", [128, T]); gr = al("gr", [128, T]); gi = al("gi", [128, T])
        u32 = al("u32", [32, T]); Y = [al("Y%d" % i, [128, T]) for i in range(2)]
        mt = Slots([al("mt%d" % i, [128, 512])[:] for i in range(6)], "mt")
        gel = al("gel", [128, 2, 512], BF16); sg = [al("sg%d" % i, [128, 512]) for i in range(2)]; yb = [al("yb%d" % i, [128, 512], BF16) for i in range(2)]
        banks = [es.enter_context(nc.psum_tensor("s_ps%d" % i, [128, 512], F32)) for i in range(8)]
        psl = Slots([banks[i][:] for i in range(8)], "psb")
        ld = lambda dst, src, tok: P.dma(lambda e: e.dma_start(out=dst, in_=src), writes=[tok])
        ld(lre[:], k.s5_lam_re[l], "lre"); ld(lim[:], k.s5_lam_im[l], "lim"); ld(stp[:], k.s5_log_step[l], "stp")
        ld(bre[:], k.s5_b_re[l], "bre"); ld(bim[:], k.s5_b_im[l], "bim"); ld(cre[:], k.s5_c_re[l], "cre"); ld(cim[:], k.s5_c_im[l], "cim")
        ld(dsk[:], k.s5_d[l], "dsk")
        P.dma(lambda e: e.dma_start(out=wgl[:], in_=k.s5_glu[l].rearrange("(c p) n -> p c n", p=128)), writes=["wgl"], q="pool")
        P.op("dve", lambda e: e.memset(hpi[:], HALF_PI), writes=["hpi"])
        P.op("dve", lambda e: e.memset(bd_all[:], 0.0), writes=["bd_all"])
        P.op("dve", lambda e: e.memset(cw_all[:], 0.0), writes=["cw_all"])
        tt = lambda out, a, b_, op, rd, wr, eng="dve": P.op(eng, lambda e: e.tensor_tensor(out=out, in0=a, in1=b_, op=op), reads=rd, writes=wr)
        P.op("act", lambda e: e.activation(out=stp[:], in_=stp[:], func=AF.Exp), reads=["stp"], writes=["stp"])
        tt(t1[:], lre[:], stp[:], ALU.mult, ["lre", "stp"], ["t1"])
        P.op("act", lambda e: e.activation(out=mag[:], in_=t1[:], func=AF.Exp), reads=["t1"], writes=["mag"])
        tt(t2[:], lim[:], stp[:], ALU.mult, ["lim", "stp"], ["t2"])
        P.op("act", lambda e: e.activation(out=sth[:], in_=t2[:], func=AF.Sin, scale=1.0 / 32), reads=["t2"], writes=["sth"])
        P.op("act", lambda e: e.activation(out=cth[:], in_=t2[:], func=AF.Sin, scale=1.0 / 32, bias=hpi[:, 0:1]), reads=["t2", "hpi"], writes=["cth"])
        for it in range(5):
            tt(t1[:], cth[:], cth[:], ALU.mult, ["cth"], ["t1"])
            tt(t3[:], sth[:], sth[:], ALU.mult, ["sth"], ["t3"])
            P.op("dve", lambda e: e.scalar_tensor_tensor(out=sth[:], in0=cth[:], scalar=2.0, in1=sth[:], op0=ALU.mult, op1=ALU.mult), reads=["cth", "sth"], writes=["sth"])
            tt(cth[:], t1[:], t3[:], ALU.subtract, ["t1", "t3"], ["cth"])
        tt(are[:], mag[:], cth[:], ALU.mult, ["mag", "cth"], ["are"])
        tt(aim[:], mag[:], sth[:], ALU.mult, ["mag", "sth"], ["aim"])
        tt(t1[:], lre[:], lre[:], ALU.mult, ["lre"], ["t1"])
        tt(t3[:], lim[:], lim[:], ALU.mult, ["lim"], ["t3"])
        tt(t1[:], t1[:], t3[:], ALU.add, ["t1", "t3"], ["t1"])
        P.op("dve", lambda e: e.reciprocal(out=t1[:], in_=t1[:]), reads=["t1"], writes=["t1"])
        P.op("dve", lambda e: e.tensor_scalar(out=t2[:], in0=are[:], scalar1=-1.0, scalar2=None, op0=ALU.add), reads=["are"], writes=["t2"])
        tt(cfr[:], t2[:], lre[:], ALU.mult, ["t2", "lre"], ["cfr"])
        tt(t3[:], aim[:], lim[:], ALU.mult, ["aim", "lim"], ["t3"])
        tt(cfr[:], cfr[:], t3[:], ALU.add, ["cfr", "t3"], ["cfr"])
        tt(cfr[:], cfr[:], t1[:], ALU.mult, ["cfr", "t1"], ["cfr"])
        tt(cfi[:], aim[:], lre[:], ALU.mult, ["aim", "lre"], ["cfi"])
        tt(t3[:], t2[:], lim[:], ALU.mult, ["t2", "lim"], ["t3"])
        tt(cfi[:], cfi[:], t3[:], ALU.subtract, ["cfi", "t3"], ["cfi"])
        tt(cfi[:], cfi[:], t1[:], ALU.mult, ["cfi", "t1"], ["cfi"])
        bb4 = bb_all[:].rearrange("p (d c r) h -> p d c r h", d=2, r=2)
        for d in range(2):
            bc = lambda t_: t_[:, d * 8:(d + 1) * 8].rearrange("p (c o) -> p c o", o=1).to_broadcast([128, 8, 16])
            tt(tb[0][:], bre[:], bc(cfr), ALU.mult, ["bre", "cfr"], [("tb", 0)])
            tt(tb[1][:], bim[:], bc(cfi), ALU.mult, ["bim", "cfi"], [("tb", 1)])
            tt(bb4[:, d, :, 0, :], tb[0][:], tb[1][:], ALU.subtract, [("tb", 0), ("tb", 1)], ["bb_all"])
            tt(tb[2][:], bim[:], bc(cfr), ALU.mult, ["bim", "cfr"], [("tb", 2)])
            tt(tb[3][:], bre[:], bc(cfi), ALU.mult, ["bre", "cfi"], [("tb", 3)])
            tt(bb4[:, d, :, 1, :], tb[2][:], tb[3][:], ALU.add, [("tb", 2), ("tb", 3)], ["bb_all"])
        for g2 in range(2):
            rr_ = slice(g2 * 64, (g2 + 1) * 64)
            P.op("dve", lambda e, rr_=rr_, g2=g2: e.tensor_copy(out=bd_all[rr_, :, g2 * 16:(g2 + 1) * 16], in_=bb_all[rr_, :, :]), reads=["bb_all", "bd_all"], writes=["bd_all"])

        def w_build(j):
            pw, pwt = psl.get()
            P.op("pe", lambda e: e.transpose(pw[0:32, 0:128], bd_all[:, j, :], k.idn[:]), reads=["bd_all", "idn"], writes=[pwt])
            P.op("act", lambda e: e.activation(out=W_all[:, j, :], in_=pw[0:32, 0:128], func=AF.Identity), reads=[pwt], writes=["W_all"])
        for j in range(32):
            w_build(j)

        def cw_build(ct, g2):
            q = ct % 4
            rr_ = slice(g2 * 64, (g2 + 1) * 64)
            c0 = q * 32 + g2 * 16
            P.op("dve", lambda e: e.tensor_copy(out=cw_all[rr_, ct, 0, c0:c0 + 16], in_=cre[rr_, ct, :]), reads=["cre", "cw_all"], writes=["cw_all"])
            P.op("dve", lambda e: e.tensor_scalar(out=cw_all[rr_, ct, 1, c0:c0 + 16], in0=cim[rr_, ct, :], scalar1=-1.0, scalar2=None, op0=ALU.mult), reads=["cim", "cw_all"], writes=["cw_all"])
        for ct in range(8):
            for g2 in range(2):
                cw_build(ct, g2)
        P.op("dve", lambda e: e.tensor_copy(out=pwc[:, :, 0], in_=cth[:]), reads=["cth"], writes=["pwc"])
        P.op("dve", lambda e: e.tensor_copy(out=pws[:, :, 0], in_=sth[:]), reads=["sth"], writes=["pws"])

        def pw_level(kk):
            c_, s_ = pwc[:, :, kk - 1], pws[:, :, kk - 1]
            tt(t1[:], c_, c_, ALU.mult, ["pwc"], ["t1"])
            tt(t3[:], s_, s_, ALU.mult, ["pws"], ["t3"])
            tt(pwc[:, :, kk], t1[:], t3[:], ALU.subtract, ["t1", "t3", "pwc"], ["pwc"])
            P.op("dve", lambda e: e.scalar_tensor_tensor(out=pws[:, :, kk], in0=c_, scalar=2.0, in1=s_, op0=ALU.mult, op1=ALU.mult), reads=["pwc", "pws"], writes=["pws"])
        for kk in range(1, 12):
            pw_level(kk)

        def table_gen(j):
            i = j % 2
            C, S_ = cs[i], sn[i]
            ctk, stk = ("cs", i), ("sn", i)
            P.op("dve", lambda e: e.memset(C[:, 0:1], 1.0), writes=[ctk])
            P.op("dve", lambda e: e.memset(S_[:, 0:1], 0.0), writes=[stk])
            for kk in range(12):
                ln = 1 << kk
                nn = min(ln, T - ln)
                if nn <= 0:
                    break
                pc, ps_ = pwc[:, j, kk:kk + 1], pws[:, j, kk:kk + 1]
                lvl(C, S_, ctk, stk, ln, nn, pc, ps_)
            P.dma(lambda e: e.dma_start(out=k.S5TAB[j, 0], in_=C[:]), reads=[ctk], writes=[("TAB", j)])
            P.dma(lambda e: e.dma_start(out=k.S5TAB[j, 1], in_=S_[:]), reads=[stk], writes=[("TAB", j)])

        def lvl(C, S_, ctk, stk, ln, nn, pc, ps_):
            m1, m1t = mt.get()
            m2, m2t = mt.get()
            w_ = min(nn, 512)
            for o in range(0, nn, 512):
                w = min(512, nn - o)
                sub(C, S_, ctk, stk, ln, o, w, pc, ps_)

        def sub(C, S_, ctk, stk, ln, o, w, pc, ps_):
            m1, m1t = mt.get()
            m2, m2t = mt.get()
            P.op("dve", lambda e: e.tensor_scalar(out=m1[:, :w], in0=S_[:, o:o + w], scalar1=ps_, scalar2=None, op0=ALU.mult), reads=[stk, "pws"], writes=[m1t])
            P.op("dve", lambda e: e.tensor_scalar(out=m2[:, :w], in0=S_[:, o:o + w], scalar1=pc, scalar2=None, op0=ALU.mult), reads=[stk, "pwc"], writes=[m2t])
            P.op("dve", lambda e: e.scalar_tensor_tensor(out=C[:, ln + o:ln + o + w], in0=C[:, o:o + w], scalar=pc, in1=m1[:, :w], op0=ALU.mult, op1=ALU.subtract), reads=[ctk, m1t, "pwc"], writes=[ctk])
            P.op("dve", lambda e: e.scalar_tensor_tensor(out=S_[:, ln + o:ln + o + w], in0=C[:, o:o + w], scalar=ps_, in1=m2[:, :w], op0=ALU.mult, op1=ALU.add), reads=[ctk, m2t, "pws"], writes=[stk])
        for j in range(16):
            table_gen(j)

        blocks = [(0, CTX)] + [(CTX + i * 512, 512) for i in range(4)]

        def tabview(tab, d, t0, n):
            if d == 0:
                return tab[:, t0:t0 + n]
            if t0 < CTX:
                lo = CTX - 1 - (t0 + n - 1)
            else:
                lo = CTX + (T - 1 - (t0 + n - 1))
            return tab[:, lo:lo + n][:, ::-1]

        def do_block_in(ct, d, i, t0, n):
            j = d * 8 + ct
            pr, prt = psl.get()
            pi_, pit = psl.get()
            P.op("pe", lambda e: e.matmul(pr[:, :n], lhsT=W_all[:, j * 2, :], rhs=u32[:, t0:t0 + n], start=True, stop=True), reads=["W_all", "u32"], writes=[prt])
            P.op("pe", lambda e: e.matmul(pi_[:, :n], lhsT=W_all[:, j * 2 + 1, :], rhs=u32[:, t0:t0 + n], start=True, stop=True), reads=["W_all", "u32"], writes=[pit])
            cv, sv = tabview(cs[i], d, t0, n), tabview(sn[i], d, t0, n)
            ms = [mt.get() for _ in range(4)]
            tt(ms[0][0][:, :n], pr[:, :n], cv, ALU.mult, [prt, ("cs", i)], [ms[0][1]])
            tt(ms[1][0][:, :n], pi_[:, :n], sv, ALU.mult, [pit, ("sn", i)], [ms[1][1]])
            tt(xr[:, t0:t0 + n], ms[0][0][:, :n], ms[1][0][:, :n], ALU.add, [ms[0][1], ms[1][1]], ["xr"], "pool")
            tt(ms[2][0][:, :n], pi_[:, :n], cv, ALU.mult, [pit, ("cs", i)], [ms[2][1]])
            tt(ms[3][0][:, :n], pr[:, :n], sv, ALU.mult, [prt, ("sn", i)], [ms[3][1]])
            tt(xi[:, t0:t0 + n], ms[2][0][:, :n], ms[3][0][:, :n], ALU.subtract, [ms[2][1], ms[3][1]], ["xi"], "pool")

        def do_scan(ct, d):
            j = d * 8 + ct
            for (src, dst, stok, dtok) in ((xr, gr, "xr", "gr"), (xi, gi, "xi", "gi")):
                scan1(j, d, src, dst, stok, dtok)

        def scan1(j, d, src, dst, stok, dtok):
            rb = lambda n: mag[:, j:j + 1].to_broadcast([128, n])
            if d == 0:
                P.op("dve", lambda e: e.tensor_tensor_scan(out=dst[:], data0=rb(T), data1=src[:], initial=0.0, op0=ALU.mult, op1=ALU.add), reads=[stok, "mag"], writes=[dtok])
            else:
                P.op("dve", lambda e: e.tensor_tensor_scan(out=dst[:, 0:CTX][:, ::-1], data0=rb(CTX), data1=src[:, 0:CTX][:, ::-1], initial=0.0, op0=ALU.mult, op1=ALU.add), reads=[stok, "mag"], writes=[dtok])
                P.op("dve", lambda e: e.tensor_tensor_scan(out=dst[:, CTX:T][:, ::-1], data0=rb(SEQ), data1=src[:, CTX:T][:, ::-1], initial=dst[:, 0:1], op0=ALU.mult, op1=ALU.add), reads=[stok, "mag", dtok], writes=[dtok])

        def do_block_out(ct, d, i, t0, n, first):
            cv, sv = tabview(cs[i], d, t0, n), tabview(sn[i], d, t0, n)
            ms = [mt.get() for _ in range(4)]
            tt(ms[0][0][:, :n], gr[:, t0:t0 + n], cv, ALU.mult, ["gr", ("cs", i)], [ms[0][1]])
            tt(ms[1][0][:, :n], gi[:, t0:t0 + n], sv, ALU.mult, ["gi", ("sn", i)], [ms[1][1]], "pool")
            tt(xr[:, t0:t0 + n], ms[0][0][:, :n], ms[1][0][:, :n], ALU.subtract, [ms[0][1], ms[1][1]], ["xr"])
            tt(ms[2][0][:, :n], gr[:, t0:t0 + n], sv, ALU.mult, ["gr", ("sn", i)], [ms[2][1]], "pool")
            tt(ms[3][0][:, :n], gi[:, t0:t0 + n], cv, ALU.mult, ["gi", ("cs", i)], [ms[3][1]])
            tt(xi[:, t0:t0 + n], ms[2][0][:, :n], ms[3][0][:, :n], ALU.add, [ms[2][1], ms[3][1]], ["xi"], "pool")
            py, pyt = psl.get()
            P.op("pe", lambda e: e.matmul(py[:, :n], lhsT=cw_all[:, ct, 0, :], rhs=xr[:, t0:t0 + n], start=True, stop=False), reads=["cw_all", "xr"], writes=[pyt], inc=False)
            P.op("pe", lambda e: e.matmul(py[:, :n], lhsT=cw_all[:, ct, 1, :], rhs=xi[:, t0:t0 + n], start=False, stop=True), reads=["cw_all", "xi"], writes=[pyt])
            yt_ = Y[ct // 4]
            ytk = ("Y", ct // 4)
            if first:
                P.op("act", lambda e: e.activation(out=yt_[:, t0:t0 + n], in_=py[:, :n], func=AF.Identity), reads=[pyt], writes=[ytk])
            else:
                tt(yt_[:, t0:t0 + n], yt_[:, t0:t0 + n], py[:, :n], ALU.add, [pyt, ytk], [ytk])

        def do_ct_dir(b, ct, d):
            j = d * 8 + ct
            i = j % 2
            P.dma(lambda e: e.dma_start(out=cs[i][:], in_=k.S5TAB[j, 0]), reads=[("TAB", j)], writes=[("cs", i)])
            P.dma(lambda e: e.dma_start(out=sn[i][:], in_=k.S5TAB[j, 1]), reads=[("TAB", j)], writes=[("sn", i)])
            for (t0, n) in blocks:
                do_block_in(ct, d, i, t0, n)
            do_scan(ct, d)
            for (t0, n) in blocks:
                do_block_out(ct, d, i, t0, n, (ct % 4 == 0 and d == 0))

        def do_ct(b, ct):
            r0 = FM_U + ct * 32
            P.dma(lambda e: e.dma_start(out=u32[:], in_=k.PT[b, r0:r0 + 32, :]), reads=[("PT", b)], writes=["u32"])
            for d in range(2):
                do_ct_dir(b, ct, d)

        def out_tile(b, t0, n):
            for yt in range(2):
                ub, ubt = mt.get()
                r0 = FM_U + yt * 128
                P.dma(lambda e, ub=ub, r0=r0: e.dma_start(out=ub[:, :n], in_=k.PT[b, r0:r0 + 128, t0:t0 + n]), reads=[("PT", b)], writes=[ubt])
                yv, yvt = mt.get()
                P.op("dve", lambda e, ub=ub, yv=yv, yt=yt: e.scalar_tensor_tensor(out=yv[:, :n], in0=ub[:, :n], scalar=dsk[:, yt:yt + 1], in1=Y[yt][:, t0:t0 + n], op0=ALU.mult, op1=ALU.add), reads=[ubt, "dsk", ("Y", yt)], writes=[yvt])
                x2, x2t = mt.get()
                P.op("act", lambda e, yv=yv, x2=x2: e.activation(out=x2[:, :n], in_=yv[:, :n], func=AF.Square), reads=[yvt], writes=[x2t])
                P.op("dve", lambda e, x2=x2: e.tensor_scalar(out=x2[:, :n], in0=x2[:, :n], scalar1=0.044715, scalar2=1.0, op0=ALU.mult, op1=ALU.add), reads=[x2t], writes=[x2t])
                P.op("dve", lambda e, x2=x2, yv=yv: e.tensor_tensor(out=x2[:, :n], in0=x2[:, :n], in1=yv[:, :n], op=ALU.mult), reads=[x2t, yvt], writes=[x2t])
                P.op("act", lambda e, x2=x2: e.activation(out=x2[:, :n], in_=x2[:, :n], func=AF.Tanh, scale=0.7978845608028654), reads=[x2t], writes=[x2t])
                P.op("dve", lambda e, x2=x2, yv=yv, yt=yt: e.scalar_tensor_tensor(out=gel[:, yt, :n], in0=x2[:, :n], scalar=1.0, in1=yv[:, :n], op0=ALU.add, op1=ALU.mult), reads=[x2t, yvt], writes=[("gel", yt)])
            for oc in range(2):
                pa, pat = psl.get()
                pg, pgt = psl.get()
                for kc in range(2):
                    P.op("pe", lambda e, oc=oc, kc=kc, pa=pa: e.matmul(pa[:, :n], lhsT=wgl[:, kc, oc * 128:(oc + 1) * 128], rhs=gel[:, kc, :n], start=(kc == 0), stop=(kc == 1)), reads=["wgl", ("gel", kc)], writes=[pat])
                for kc in range(2):
                    P.op("pe", lambda e, oc=oc, kc=kc, pg=pg: e.matmul(pg[:, :n], lhsT=wgl[:, kc, 256 + oc * 128:256 + (oc + 1) * 128], rhs=gel[:, kc, :n], start=(kc == 0), stop=(kc == 1)), reads=["wgl", ("gel", kc)], writes=[pgt])
                P.op("act", lambda e, oc=oc, pg=pg: e.activation(out=sg[oc][:, :n], in_=pg[:, :n], func=AF.Sigmoid, scale=0.5), reads=[pgt], writes=[("sg", oc)])
                P.op("dve", lambda e, oc=oc, pa=pa: e.scalar_tensor_tensor(out=yb[oc][:, :n], in0=pa[:, :n], scalar=0.5, in1=sg[oc][:, :n], op0=ALU.mult, op1=ALU.mult), reads=[pat, ("sg", oc)], writes=[("yb", oc)])
                P.dma(lambda e, oc=oc: e.dma_start(out=k.YT[b, 1, oc * 128:(oc + 1) * 128, t0:t0 + n], in_=yb[oc][:, :n]), reads=[("yb", oc)], writes=[("YT", b)])

        for b in range(NB):
            for ct in range(8):
                do_ct(b, ct)
            for (t0, n) in blocks:
                out_tile(b, t0, n)
        P.end_stage()


def stage_mixers(k, l):
    sel = k.opts.get("mixers", ("dn", "s5", "hg", "att"))
    if "dn" in sel:
        stage_deltanet(k, l)
    if "s5" in sel:
        stage_s5(k, l)
    if "hg" in sel:
        stage_hgrn2(k, l)
    if "att" in sel:
        stage_attention(k, l)


def stage_merge(k, l):
    nc, P, NB, NC = k.nc, k.P, k.NB, k.NC
    with ExitStack() as es:
        wg = es.enter_context(nc.sbuf_tensor("m_wg", [128, 8, 4 * D], BF16))
        wb = es.enter_context(nc.sbuf_tensor("m_wb", [128, 8, D], BF16))
        wo = es.enter_context(nc.sbuf_tensor("m_wo", [128, 8, D], BF16))
        xt = [es.enter_context(nc.sbuf_tensor("m_x%d" % s, [128, 8, 256], F32)) for s in range(2)]
        yt = [es.enter_context(nc.sbuf_tensor("m_y%d" % s, [128, 8, 256], BF16)) for s in range(2)]
        ht = es.enter_context(nc.sbuf_tensor("m_h", [128, 8, 256], BF16))
        acc = es.enter_context(nc.sbuf_tensor("m_acc", [128, 8, 256], F32))
        accb = es.enter_context(nc.sbuf_tensor("m_accb", [128, 8, 256], BF16))
        sg = [es.enter_context(nc.sbuf_tensor("m_sg%d" % s, [128, 256], F32)) for s in range(2)]
        tt = [es.enter_context(nc.sbuf_tensor("m_tt%d" % s, [128, 256], F32)) for s in range(2)]
        psg = [es.enter_context(nc.psum_tensor("m_psg%d" % s, [128, 512], F32)) for s in range(2)]
        psy = [es.enter_context(nc.psum_tensor("m_psy%d" % s, [128, 512], F32)) for s in range(2)]
        pso = [es.enter_context(nc.psum_tensor("m_pso%d" % s, [128, 512], F32)) for s in range(2)]
        ntiles = alloc_norm_tiles(k, es, "m_", 256)
        A, Sh, G = emit_mod_scalars(k, es, "m_", l, 1, 3, 4, 5, 1.0)
        load_w_bf16(k, wg, k.w_gate[l], 4 * D, "wg")
        load_w_bf16(k, wb, k.w_branch[l].rearrange("i r n -> (i r) n"), D, "wb")
        load_w_bf16(k, wo, k.w_out[l], D, "wo")
        it = 0
        qi = 0
        for (b, t0, n, cond) in token_tiles(NB, 256):
            if l == 1 and cond == NB:
                continue
            s = it % 2
            it += 1
            xsrc = k.XT[b].rearrange("(c p) t -> p c t", p=128)[:, :, t0:t0 + n]
            P.dma(lambda e, s=s, xsrc=xsrc, n=n: e.dma_start(out=xt[s][:, :, :n], in_=xsrc), reads=[("XT", b)], writes=[("x", s)])
            ysrc = k.YT[b].rearrange("i (c p) t -> p (i c) t", p=128)[:, :, t0:t0 + n]
            P.dma(lambda e, s=s, ysrc=ysrc, n=n: e.dma_start(out=yt[s][:, :, :n], in_=ysrc), reads=[("YT", b)], writes=[("y", s)])
            emit_norm_mod(k, ntiles, xt[s], ht, n, A, Sh, cond, "modsc", s)
            for m in range(8):
                for i in range(4):
                    q = qi % 2
                    qi += 1
                    for kc in range(8):
                        P.op("pe", lambda e, i=i, m=m, q=q, kc=kc, n=n: e.matmul(psg[q][:, :n], lhsT=wg[:, kc, i * D + m * 128:i * D + (m + 1) * 128], rhs=ht[:, kc, :n], start=(kc == 0), stop=(kc == 7)),
                             reads=[("wg", kc), "h"], writes=[("psg", q)])
                    for kk in range(2):
                        P.op("pe", lambda e, i=i, m=m, q=q, kk=kk, n=n, s=s: e.matmul(psy[q][:, :n], lhsT=wb[:, i * 2 + kk, m * 128:(m + 1) * 128], rhs=yt[s][:, i * 2 + kk, :n], start=(kk == 0), stop=(kk == 1)),
                             reads=[("wb", i * 2 + kk), ("y", s)], writes=[("psy", q)])
                    P.op("act", lambda e, q=q, n=n: e.activation(out=sg[q][:, :n], in_=psg[q][:, :n], func=AF.Sigmoid), reads=[("psg", q)], writes=[("sg", q)])
                    if i == 0:
                        P.op("dve", lambda e, q=q, n=n, m=m: e.tensor_tensor(out=acc[:, m, :n], in0=sg[q][:, :n], in1=psy[q][:, :n], op=ALU.mult), reads=[("sg", q), ("psy", q)], writes=[("acc", m)])
                    else:
                        P.op("dve", lambda e, q=q, n=n: e.tensor_tensor(out=tt[q][:, :n], in0=sg[q][:, :n], in1=psy[q][:, :n], op=ALU.mult), reads=[("sg", q), ("psy", q)], writes=[("tt", q)])
                        if i < 3:
                            P.op("pool", lambda e, q=q, n=n, m=m: e.tensor_tensor(out=acc[:, m, :n], in0=acc[:, m, :n], in1=tt[q][:, :n], op=ALU.add), reads=[("tt", q), ("acc", m)], writes=[("acc", m)])
                        else:
                            P.op("pool", lambda e, q=q, n=n, m=m: e.tensor_tensor(out=accb[:, m, :n], in0=acc[:, m, :n], in1=tt[q][:, :n], op=ALU.add), reads=[("tt", q), ("acc", m)], writes=[("accb", m)])
            for m in range(8):
                q = m % 2
                for kc in range(8):
                    P.op("pe", lambda e, m=m, q=q, kc=kc, n=n: e.matmul(pso[q][:, :n], lhsT=wo[:, kc, m * 128:(m + 1) * 128], rhs=accb[:, kc, :n], start=(kc == 0), stop=(kc == 7)),
                         reads=[("wo", kc), ("accb", kc)], writes=[("pso", q)])
                P.op("dve", lambda e, m=m, q=q, s=s, n=n, cond=cond: e.scalar_tensor_tensor(out=xt[s][:, m, :n], in0=pso[q][:, :n], scalar=G[:, m, cond:cond + 1], in1=xt[s][:, m, :n], op0=ALU.mult, op1=ALU.add),
                     reads=[("pso", q), ("x", s), "modsg"], writes=[("x", s)])
            P.dma(lambda e, s=s, xsrc=xsrc, n=n: e.dma_start(out=xsrc, in_=xt[s][:, :, :n]), reads=[("x", s)], writes=[("XT", b)])
        P.end_stage()


def stage_final(k):
    nc, P, NB = k.nc, k.P, k.NB
    with ExitStack() as es:
        xt = [es.enter_context(nc.sbuf_tensor("o_x%d" % s, [128, 8, 512], F32)) for s in range(2)]
        yt = es.enter_context(nc.sbuf_tensor("o_y", [128, 8, 512], F32))
        ot = [es.enter_context(nc.sbuf_tensor("o_o%d" % s, [128, D], F32)) for s in range(2)]
        sq = es.enter_context(nc.sbuf_tensor("o_sq", [128, 8, 512], BF16))
        rs = es.enter_context(nc.sbuf_tensor("o_rs", [128, 512], F32))
        epsb = es.enter_context(nc.sbuf_tensor("o_eps", [128, 1], F32))
        psms = es.enter_context(nc.psum_tensor("o_psms", [128, 512], F32))
        ps = [es.enter_context(nc.psum_tensor("o_ps%d" % s, [128, 4, 128], F32)) for s in range(4)]
        P.op("dve", lambda e: e.memset(epsb[:], EPS), writes=["epsb"])
        it = 0
        oi = 0
        pi = 0
        for (b, t0, n, cond) in token_tiles(NB):
            if cond == NB:
                continue
            s = it % 2
            it += 1
            xsrc = k.XT[b].rearrange("(c p) t -> p c t", p=128)[:, :, t0:t0 + n]
            P.dma(lambda e, s=s, xsrc=xsrc: e.dma_start(out=xt[s][:], in_=xsrc), reads=[("XT", b)], writes=[("x", s)])
            P.op("act", lambda e, s=s: e.activation(out=sq[:], in_=xt[s][:], func=AF.Square), reads=[("x", s)], writes=["sq"])
            for c in range(8):
                P.op("pe", lambda e, c=c: e.matmul(psms[:], lhsT=k.onesb[:], rhs=sq[:, c, :], start=(c == 0), stop=(c == 7)), reads=["sq", "onesb"], writes=["psms"])
            P.op("act", lambda e: e.activation(out=rs[:], in_=psms[:], func=AF.Sqrt, bias=epsb[:, 0:1], scale=1.0), reads=["psms", "epsb"], writes=["rs0", "rs"])
            P.op("dve", lambda e: e.reciprocal(out=rs[:], in_=rs[:]), reads=["rs0"], writes=["rs"])
            for c in range(8):
                P.op("dve", lambda e, c=c, s=s: e.scalar_tensor_tensor(out=yt[:, c, :], in0=xt[s][:, c, :], scalar=k.fg[:, c:c + 1], in1=rs[:], op0=ALU.mult, op1=ALU.mult),
                     reads=[("x", s), "rs", "fg"], writes=[("yt", c)])
            for tb in range(4):
                o = oi % 2
                oi += 1
                for h in range(2):
                    p = pi % 4
                    pi += 1
                    for c in range(4):
                        ch = h * 4 + c
                        P.op("pe", lambda e, p=p, c=c, ch=ch, tb=tb: e.transpose(ps[p][:, c, :], yt[:, ch, tb * 128:(tb + 1) * 128], k.idn[:]),
                             reads=[("yt", ch), "idn"], writes=[("ps", p)])
                    if h == 0:
                        P.op("act", lambda e, p=p, o=o, h=h: e.activation(out=ot[o][:, h * 512:(h + 1) * 512], in_=ps[p][:].rearrange("p a b -> p (a b)"), func=AF.Identity), reads=[("ps", p)], writes=[("ot", o, h)])
                    else:
                        P.op("dve", lambda e, p=p, o=o, h=h: e.tensor_copy(out=ot[o][:, h * 512:(h + 1) * 512], in_=ps[p][:].rearrange("p a b -> p (a b)")), reads=[("ps", p)], writes=[("ot", o, h)])
                r0 = t0 - CTX + tb * 128
                P.dma(lambda e, o=o, b=b, r0=r0: e.dma_start(out=k.out[b, r0:r0 + 128, :], in_=ot[o][:]), reads=[("ot", o, 0), ("ot", o, 1)], writes=[("out", b)])
        P.end_stage()


def make_inputs(inputs, core, NB):
    b0 = core * NB
    f = lambda a: np.ascontiguousarray(a, dtype=np.float32)
    w_in = inputs["w_in"]
    cT = np.concatenate([inputs["c"][b0:b0 + NB], inputs["c_ctx"][None]], 0).reshape(NB + 1, 8, 128).transpose(2, 1, 0)
    return {
        "x": f(inputs["x"][b0:b0 + NB]),
        "ctx": f(inputs["ctx"][b0:b0 + NB]),
        "cT": f(cT),
        "ada_w": f(inputs["ada_w"]),
        "ada_b": f(inputs["ada_b"].reshape(2, 72, 128).transpose(0, 2, 1)),
        "norm_g": f(inputs["norm_g"].reshape(2, 3, 8, 128).transpose(0, 1, 3, 2)),
        "final_g": f(inputs["final_g"].reshape(8, 128).T),
        "ffn_w1": f(inputs["ffn_w1"]), "ffn_w3": f(inputs["ffn_w3"]), "ffn_w2": f(inputs["ffn_w2"]),
        "w_fm": f(np.concatenate([w_in[:, :, a:a + w] for a, w in FM_GROUPS], axis=2)),
        "w_tok": f(np.concatenate([w_in[:, :, a:a + w] for a, w in TOK_GROUPS], axis=2)),
        "w_gate": f(w_in[:, :, GATE0:]),
        "w_branch": f(inputs["w_branch"]),
        "w_out": f(inputs["w_out"]),
        "ident": np.eye(128, dtype=np.float32),
        "rope_cos": ROPE[0], "rope_sin": ROPE[1], "rope_rm": ROPE[2],
        "hg_masks": HG_MASKS, "bd64": BD64,
        "s5_lam_re": f(inputs["s5_lam_re"].reshape(2, 2, 8, 2, 64).transpose(0, 3, 4, 1, 2).reshape(2, 128, 16)),
        "s5_lam_im": f(inputs["s5_lam_im"].reshape(2, 2, 8, 2, 64).transpose(0, 3, 4, 1, 2).reshape(2, 128, 16)),
        "s5_log_step": f(np.broadcast_to(inputs["s5_log_step"].reshape(2, 2, 8, 2, 1), (2, 2, 8, 2, 64)).transpose(0, 3, 4, 1, 2).reshape(2, 128, 16)),
        "s5_b_re": f(inputs["s5_b_re"].reshape(2, 8, 2, 64, 16).transpose(0, 2, 3, 1, 4).reshape(2, 128, 8, 16)),
        "s5_b_im": f(inputs["s5_b_im"].reshape(2, 8, 2, 64, 16).transpose(0, 2, 3, 1, 4).reshape(2, 128, 8, 16)),
        "s5_c_re": f(inputs["s5_c_re"].reshape(2, 8, 2, 16, 64).transpose(0, 2, 4, 1, 3).reshape(2, 128, 8, 16)),
        "s5_c_im": f(inputs["s5_c_im"].reshape(2, 8, 2, 16, 64).transpose(0, 2, 4, 1, 3).reshape(2, 128, 8, 16)),
        "s5_d": f(inputs["s5_d"].reshape(2, 2, 128).transpose(0, 2, 1)),
        "s5_glu": f(inputs["s5_glu"]),
        "dn_gm64": DN_GM64, "dn_gmbd": DN_GMBD, "dn_idn2": DN_IDN2,
        "dn_conv": f(inputs["dn_conv"].reshape(2, 5, 6, 128).transpose(0, 3, 2, 1)),
        "dn_a_log": f(np.broadcast_to(inputs["dn_a_log"].reshape(2, 1, 8), (2, 128, 8))),
        "dn_dt_bias": f(np.broadcast_to(inputs["dn_dt_bias"].reshape(2, 1, 8), (2, 128, 8))),
        "dn_norm_g": f(np.broadcast_to(inputs["dn_norm_g"].reshape(2, 1, 64), (2, 128, 64))),
        "hg_lb": f(inputs["hg_lb_logits"].reshape(2, 2, 128).transpose(2, 1, 0)),
        "hg_norm_g": f(np.tile(inputs["hg_norm_g"], (1, 2)).reshape(2, 128, 1)),
        "at_qn_g": f(inputs["at_qn_g"].reshape(2, 64, 1)), "at_kn_g": f(inputs["at_kn_g"].reshape(2, 64, 1)),
    }


def _rope_consts():
    n = np.arange(SEQ)
    r, c = n // 64, n % 64
    inv = (10000.0 ** (-np.arange(0, 32, 2, dtype=np.float32) / 32)).astype(np.float32)
    ang_r = r[:, None].astype(np.float32) * inv
    ang_c = c[:, None].astype(np.float32) * inv
    ang = np.concatenate([ang_r, ang_r, ang_c, ang_c], -1)
    rm = np.zeros((64, 64), np.float32)
    for i in range(16):
        rm[16 + i, i] = -1.0
        rm[i, 16 + i] = 1.0
        rm[48 + i, 32 + i] = -1.0
        rm[32 + i, 48 + i] = 1.0
    return np.ascontiguousarray(np.cos(ang).T.astype(np.float32)), np.ascontiguousarray(np.sin(ang).T.astype(np.float32)), rm


ROPE = _rope_consts()


def _dn_consts():
    a = np.arange(64)
    le = [(a[:, None] <= a[None, :]), (a[:, None] >= a[None, :])]
    st = [(a[None, :] < a[:, None]), (a[None, :] > a[:, None])]
    gm64 = np.zeros((64, 2, 2, 128), np.float32)
    gmbd = np.zeros((128, 2, 2, 128), np.float32)
    for d in range(2):
        gm64[:, d, 0, :] = np.tile(le[d], (1, 2))
        gm64[:, d, 1, :] = np.tile(st[d], (1, 2))
        for h in range(2):
            gmbd[h * 64:(h + 1) * 64, d, 0, h * 64:(h + 1) * 64] = st[d]
            gmbd[h * 64:(h + 1) * 64, d, 1, h * 64:(h + 1) * 64] = le[d]
    idn2 = np.tile(np.eye(64, dtype=np.float32), (2, 1))
    return gm64, gmbd, idn2


DN_GM64, DN_GMBD, DN_IDN2 = _dn_consts()
_s = np.arange(128)[:, None] % 64
_t = np.arange(64)[None, :]
HG_MASKS = np.ascontiguousarray(np.stack([(_s <= _t), (_s >= _t)], 1).astype(np.float32))
BD64 = np.kron(np.eye(2, dtype=np.float32), np.full((64, 64), 1.0 / 64, np.float32))
_CACHE = {}


def kernel(**inputs):
    NB = inputs["x"].shape[0] // N_CORES
    if "nc" not in _CACHE:
        _CACHE["nc"] = build(NB)
    nc = _CACHE["nc"]
    shared = None
    in_maps = []
    for c in range(N_CORES):
        in_maps.append(make_inputs(inputs, c, NB))
    res = run_bass_kernel_spmd(nc, in_maps, core_ids=list(range(N_CORES)))
    return np.concatenate([r["out"] for r in res.results], axis=0).astype(np.float32)
```

```python
import numpy as np
from contextlib import ExitStack
import concourse.bass as bass
import concourse.mybir as mybir
from concourse.bass_utils import run_bass_kernel_spmd

F32 = mybir.dt.float32
BF16 = mybir.dt.bfloat16
AF = mybir.ActivationFunctionType
ALU = mybir.AluOpType
AX = mybir.AxisListType

D = 1024
SEQ = 2048
CTX = 256
T = SEQ + CTX
DFF = 2816
NFF = DFF // 128
EPS = 1e-6
N_CORES = 8
ENG = ("pe", "act", "dve", "pool", "sp")

FM_GROUPS = [(0, 768), (768, 256), (1040, 256), (1296, 256), (1552, 512), (2320, 256), (2576, 256), (2832, 128)]
FM_W = sum(w for _, w in FM_GROUPS)
FM_Q, FM_K, FM_V, FM_Z, FM_U, FM_HQ, FM_HF, FM_HG, FM_AQ, FM_AK = 0, 256, 512, 768, 1024, 1280, 1536, 2048, 2304, 2560
TOK_GROUPS = [(2960, 128), (2064, 256), (1024, 8), (1032, 8)]
TOK_W = 400
TK_AV, TK_HV, TK_A, TK_B = 0, 128, 384, 392
GATE0 = 3088


class Prog:
    def __init__(self, nc, n_dma_sems=56):
        self.nc = nc
        self.sem = {e: nc.semaphore("sem_" + e).__enter__() for e in ("pe", "act", "dve", "pool")}
        self.dsem = [nc.semaphore("dsem%d" % i).__enter__() for i in range(n_dma_sems)]
        self.duse = [0] * n_dma_sems
        self.dnext = 0
        self.cnt = {e: 0 for e in self.sem}
        self.lastw = {}
        self.readers = {}
        self.ops = {e: [] for e in ENG}
        self.waited = {}
        self.nops = 0
        self.serial = False
        self.last_ev = {}
        self.prev_ev = None

    def _need(self, eng, ev, waits):
        if ev is None:
            return
        key, val, src = ev
        if self.waited.get((eng, key), 0) >= val:
            return
        self.waited[(eng, key)] = val
        waits.append((key, val))

    def _semh(self, key):
        return self.sem[key] if isinstance(key, str) else self.dsem[key]

    def op(self, eng, fn, reads=(), writes=(), inc=True):
        waits = []
        for r in reads:
            ev = self.lastw.get(r)
            if ev is not None and not (ev[2] == eng and eng == "pe"):
                self._need(eng, ev, waits)
        for w in writes:
            ev = self.lastw.get(w)
            if ev is not None and ev[2] != eng:
                self._need(eng, ev, waits)
            for rv in self.readers.get(w, ()):
                if rv[2] != eng:
                    self._need(eng, rv, waits)
        if self.serial:
            waits = [w_ for w_ in waits if not isinstance(w_[0], str) or w_[0] == eng]
            for w_ in waits:
                pass
            if self.prev_ev is not None and self.prev_ev[2] != eng:
                if self.waited.get((eng, self.prev_ev[0]), 0) < self.prev_ev[1] or True:
                    self.waited[(eng, self.prev_ev[0])] = max(self.waited.get((eng, self.prev_ev[0]), 0), self.prev_ev[1])
                    waits.append((self.prev_ev[0], self.prev_ev[1]))
        if inc:
            self.cnt[eng] += 1
            me = (eng, self.cnt[eng], eng)
        else:
            me = (eng, self.cnt[eng] + 1, eng)
        self.last_ev[eng] = me
        self.prev_ev = me if inc else (self.prev_ev if not self.serial else me)
        for r in reads:
            self.readers.setdefault(r, []).append(me)
        for w in writes:
            self.lastw[w] = me
            self.readers[w] = []
        self.ops[eng].append((waits, fn, ("sem", eng) if inc else ("none", eng)))
        self.nops += 1
        return me

    def dma(self, fn, reads=(), writes=(), q="sp"):
        waits = []
        for r in reads:
            self._need(q, self.lastw.get(r), waits)
        for w in writes:
            self._need(q, self.lastw.get(w), waits)
            for rv in self.readers.get(w, ()):
                self._need(q, rv, waits)
        i = self.dnext
        self.dnext = (self.dnext + 1) % len(self.dsem)
        if self.duse[i] > 0:
            self._need(q, (i, 16 * self.duse[i], "dma"), waits)
        self.duse[i] += 1
        me = (i, 16 * self.duse[i], "dma")
        for r in reads:
            self.readers.setdefault(r, []).append(me)
        for w in writes:
            self.lastw[w] = me
            self.readers[w] = []
        self.ops[q].append((waits, fn, ("dsem", i)))
        self.nops += 1
        return me

    def end_stage(self):
        waits = []
        for tok, ev in self.lastw.items():
            self._need("sp", ev, waits)
        for tok, evs in self.readers.items():
            for ev in evs:
                self._need("sp", ev, waits)
        self.ops["sp"].append((waits, None, None))
        nc = self.nc
        engobj = {"pe": "tensor", "act": "scalar", "dve": "vector", "pool": "gpsimd", "sp": "sync"}
        with nc.Block() as block:
            for e in ENG:
                ops = self.ops[e]
                if not ops:
                    continue

                def body(eo, ops=ops):
                    for waits, fn, inc in ops:
                        for key, val in waits:
                            eo.wait_ge(self._semh(key), val)
                        if fn is None:
                            continue
                        ins = fn(eo)
                        if inc[0] == "sem":
                            ins.then_inc(self.sem[inc[1]], 1)
                        elif inc[0] == "dsem":
                            ins.then_inc(self.dsem[inc[1]], 16)

                getattr(block, engobj[e])(body)
        self.ops = {e: [] for e in ENG}
        self.waited = {}
        self.lastw = {}
        self.readers = {}
        self.last_ev = {}
        self.serial = False
        self.prev_ev = None


class K:
    pass


class NCProxy:
    def __init__(self, nc):
        object.__setattr__(self, "_nc", nc)
        object.__setattr__(self, "_n", [0])

    def __getattr__(self, name):
        return getattr(self._nc, name)

    def sbuf_tensor(self, name, shape, dtype):
        self._n[0] += 1
        return self._nc.sbuf_tensor("%s_u%d" % (name, self._n[0]), shape, dtype)

    def psum_tensor(self, name, shape, dtype):
        self._n[0] += 1
        return self._nc.psum_tensor("%s_u%d" % (name, self._n[0]), shape, dtype)


def token_tiles(NB, ts=512):
    tl = []
    for b in range(NB):
        tl.append((b, 0, CTX, NB))
        for i in range(SEQ // ts):
            tl.append((b, CTX + i * ts, ts, b))
    return tl


def build(NB, opts=None):
    opts = opts or {}
    NC = NB + 1
    nc = bass.Bass("TRN2", target_bir_lowering=False)
    k = K()
    k.nc, k.NB, k.NC, k.opts = NCProxy(nc), NB, NC, opts
    inp = lambda name, shape: nc.dram_tensor(name, list(shape), F32, kind="ExternalInput").ap()
    k.x = inp("x", [NB, SEQ, D])
    k.ctx = inp("ctx", [NB, CTX, D])
    k.cT = inp("cT", [128, 8, NC])
    k.ada_w = inp("ada_w", [2, D, 9 * D])
    k.ada_b = inp("ada_b", [2, 128, 72])
    k.norm_g = inp("norm_g", [2, 3, 128, 8])
    k.final_g = inp("final_g", [128, 8])
    k.w1 = inp("ffn_w1", [2, 2, D, DFF])
    k.w3 = inp("ffn_w3", [2, 2, D, DFF])
    k.w2 = inp("ffn_w2", [2, 2, DFF, D])
    k.w_fm = inp("w_fm", [2, D, FM_W])
    k.w_tok = inp("w_tok", [2, D, TOK_W])
    k.w_gate = inp("w_gate", [2, D, 4 * D])
    k.w_branch = inp("w_branch", [2, 4, 256, D])
    k.w_out = inp("w_out", [2, D, D])
    k.ident = inp("ident", [128, 128])
    k.rope_cos = inp("rope_cos", [64, SEQ])
    k.rope_sin = inp("rope_sin", [64, SEQ])
    k.rope_rm = inp("rope_rm", [64, 64])
    k.at_qn_g = inp("at_qn_g", [2, 64, 1])
    k.at_kn_g = inp("at_kn_g", [2, 64, 1])
    k.hg_masks = inp("hg_masks", [128, 2, 64])
    k.s5_lam_re = inp("s5_lam_re", [2, 128, 16])
    k.s5_lam_im = inp("s5_lam_im", [2, 128, 16])
    k.s5_log_step = inp("s5_log_step", [2, 128, 16])
    k.s5_b_re = inp("s5_b_re", [2, 128, 8, 16])
    k.s5_b_im = inp("s5_b_im", [2, 128, 8, 16])
    k.s5_c_re = inp("s5_c_re", [2, 128, 8, 16])
    k.s5_c_im = inp("s5_c_im", [2, 128, 8, 16])
    k.s5_d = inp("s5_d", [2, 128, 2])
    k.s5_glu = inp("s5_glu", [2, 256, 512])
    k.S5TAB = nc.dram_tensor("S5TAB", [16, 2, 128, T], F32).ap()
    k.dn_gm64 = inp("dn_gm64", [64, 2, 2, 128])
    k.dn_gmbd = inp("dn_gmbd", [128, 2, 2, 128])
    k.dn_idn2 = inp("dn_idn2", [128, 64])
    k.dn_conv = inp("dn_conv", [2, 128, 6, 5])
    k.dn_a_log = inp("dn_a_log", [2, 128, 8])
    k.dn_dt_bias = inp("dn_dt_bias", [2, 128, 8])
    k.dn_norm_g = inp("dn_norm_g", [2, 128, 64])
    k.bd64 = inp("bd64", [128, 128])
    k.hg_lb = inp("hg_lb", [128, 2, 2])
    k.hg_norm_g = inp("hg_norm_g", [2, 128, 1])
    k.out = nc.dram_tensor("out", [NB, SEQ, D], F32, kind="ExternalOutput").ap()
    k.XT = nc.dram_tensor("XT", [NB, D, T], F32).ap()
    only = opts.get("only")
    k.PT = nc.dram_tensor("PT", [NB, FM_W, T], F32, **({"kind": "ExternalInput"} if only else {})).ap()
    k.TOK = nc.dram_tensor("TOK", [NB, T, TOK_W], F32, **({"kind": "ExternalInput"} if only else {})).ap()
    k.YT = nc.dram_tensor("YT", [NB, 4, 256, T], BF16, **({"kind": "ExternalOutput"} if only else {})).ap()
    if "dump" in opts:
        k.dump = nc.dram_tensor("dump", [16, 128, T], F32, kind="ExternalOutput").ap()
    if "dbg" in opts:
        k.dbg = {nm: nc.dram_tensor("dbg_" + nm, list(shp), F32, kind="ExternalOutput").ap() for nm, shp in opts["dbg"].items()}
    P = Prog(nc)
    k.P = P
    with ExitStack() as es:
        al = lambda name, shape, dt=F32: es.enter_context(nc.sbuf_tensor(name, list(shape), dt))
        k.idn = al("idn", [128, 128])
        k.onesb = al("onesb", [128, 128], BF16)
        k.mods = al("mods", [128, 72, NC])
        k.ng = al("ng", [128, 2, 3, 8])
        k.fg = al("fg", [128, 8])
        P.dma(lambda e: e.dma_start(out=k.idn[:], in_=k.ident), writes=["idn"])
        P.dma(lambda e: e.dma_start(out=k.fg[:], in_=k.final_g), writes=["fg"])
        for l in range(2):
            for j in range(3):
                P.dma(lambda e, l=l, j=j: e.dma_start(out=k.ng[:, l, j, :], in_=k.norm_g[l, j]), writes=["ng"])
        P.op("dve", lambda e: e.memset(k.onesb[:], 1.0 / D), writes=["onesb"])
        P.end_stage()
        if only:
            for l in opts.get("layers", (0,)):
                {"att": stage_attention, "hg": stage_hgrn2, "dn": stage_deltanet, "s5": stage_s5}[only](k, l)
            return nc
        stage_transpose_in(k)
        stop = opts.get("stop", "")
        for l in range(2):
            stage_ada(k, l)
            stage_ffn(k, l, 0)
            if stop == "ffn1_%d" % l:
                break
            stage_inproj(k, l)
            if stop == "inproj_%d" % l:
                break
            stage_mixers(k, l)
            if stop == "mixers_%d" % l:
                break
            stage_merge(k, l)
            if stop == "merge_%d" % l:
                break
            stage_ffn(k, l, 1)
        if "XT" in opts.get("dbg", {}):
            P.dma(lambda e: e.dma_start(out=k.dbg["XT"], in_=k.XT), reads=[], writes=["dbgx"])
            P.end_stage()
        if "PT" in opts.get("dbg", {}):
            P.dma(lambda e: e.dma_start(out=k.dbg["PT"], in_=k.PT), reads=[], writes=["dbgp"])
            P.dma(lambda e: e.dma_start(out=k.dbg["TOK"], in_=k.TOK), reads=[], writes=["dbgt"])
            P.end_stage()
        stage_final(k)
    return nc


def stage_transpose_in(k):
    nc, P = k.nc, k.P
    with ExitStack() as es:
        xin = [es.enter_context(nc.sbuf_tensor("ti_x%d" % i, [128, D], F32)) for i in range(2)]
        xo = [es.enter_context(nc.sbuf_tensor("ti_o%d" % i, [128, 8, 128], F32)) for i in range(2)]
        ps = [es.enter_context(nc.psum_tensor("ti_ps%d" % i, [128, 4, 128], F32)) for i in range(4)]
        it = 0
        for b in range(k.NB):
            for tb in range(T // 128):
                s = it % 2
                src = k.ctx[b, tb * 128:(tb + 1) * 128, :] if tb < 2 else k.x[b, (tb - 2) * 128:(tb - 1) * 128, :]
                P.dma(lambda e, s=s, src=src: e.dma_start(out=xin[s][:], in_=src), writes=[("xin", s)])
                for h in range(2):
                    pi = (it * 2 + h) % 4
                    for c in range(4):
                        ch = h * 4 + c
                        P.op("pe", lambda e, s=s, pi=pi, c=c, ch=ch: e.transpose(ps[pi][:, c, :], xin[s][:, ch * 128:(ch + 1) * 128], k.idn[:]),
                             reads=[("xin", s), "idn"], writes=[("tps", pi)], inc=(c == 3))
                    eng = "act" if h == 0 else "dve"
                    if eng == "act":
                        P.op("act", lambda e, s=s, pi=pi, h=h: e.activation(out=xo[s][:, h * 4:(h + 1) * 4, :], in_=ps[pi][:], func=AF.Identity),
                             reads=[("tps", pi)], writes=[("xo", s, h)])
                    else:
                        P.op("dve", lambda e, s=s, pi=pi, h=h: e.tensor_copy(out=xo[s][:, h * 4:(h + 1) * 4, :], in_=ps[pi][:]),
                             reads=[("tps", pi)], writes=[("xo", s, h)])
                dst = k.XT[b].rearrange("(c p) t -> p c t", p=128)[:, :, tb * 128:(tb + 1) * 128]
                P.dma(lambda e, s=s, dst=dst: e.dma_start(out=dst, in_=xo[s][:]), reads=[("xo", s, 0), ("xo", s, 1)], writes=[("XT", b)])
                it += 1
        P.end_stage()


def stage_ada(k, l):
    nc, P, NC = k.nc, k.P, k.NC
    with ExitStack() as es:
        sc = es.enter_context(nc.sbuf_tensor("ad_sc", [128, 8, NC], F32))
        ab = es.enter_context(nc.sbuf_tensor("ad_b", [128, 72], F32))
        wt = [es.enter_context(nc.sbuf_tensor("ad_w%d" % i, [128, 8, 1024], F32)) for i in range(2)]
        ps = [es.enter_context(nc.psum_tensor("ad_ps%d" % i, [128, 8, NC], F32)) for i in range(2)]
        P.dma(lambda e: e.dma_start(out=sc[:], in_=k.cT), writes=["sc"])
        P.dma(lambda e: e.dma_start(out=ab[:], in_=k.ada_b[l]), writes=["ab"])
        P.op("act", lambda e: e.activation(out=sc[:], in_=sc[:], func=AF.Silu), reads=["sc"], writes=["sc"])
        for j in range(9):
            s = j % 2
            src = k.ada_w[l][:, j * 1024:(j + 1) * 1024].rearrange("(c p) n -> p c n", p=128)
            for hh in range(2):
                P.dma(lambda e, s=s, src=src, hh=hh: e.dma_start(out=wt[s][:, hh * 4:(hh + 1) * 4, :], in_=src[:, hh * 4:(hh + 1) * 4, :]), writes=[("adw", s, hh)])
            for m in range(8):
                for kc in range(8):
                    P.op("pe", lambda e, s=s, m=m, kc=kc: e.matmul(ps[s][:, m, :], lhsT=wt[s][:, kc, m * 128:(m + 1) * 128], rhs=sc[:, kc, :], start=(kc == 0), stop=(kc == 7)),
                         reads=[("adw", s, kc // 4), "sc"], writes=[("adps", s)], inc=(kc == 7 and m == 7))
            P.op("dve", lambda e, s=s, j=j: e.tensor_tensor(out=k.mods[:, j * 8:(j + 1) * 8, :], in0=ps[s][:], in1=ab[:, j * 8:(j + 1) * 8].rearrange("p (c o) -> p c o", o=1).to_broadcast([128, 8, NC]), op=ALU.add),
                 reads=[("adps", s), "ab"], writes=["mods"])
        P.end_stage()


def emit_norm_mod(k, es_tiles, x_t, h_t, n, A, Sh, cond, tag, sl):
    P = k.P
    sq, rs, tmp, psms = es_tiles
    P.op("act", lambda e: e.activation(out=sq[:, :, :n], in_=x_t[:, :, :n], func=AF.Square), reads=[("x", sl)], writes=["sq"])
    for c in range(8):
        P.op("pe", lambda e, c=c: e.matmul(psms[:, :n], lhsT=k.onesb[:], rhs=sq[:, c, :n], start=(c == 0), stop=(c == 7)), reads=["sq", "onesb"], writes=["psms"], inc=(c == 7))
    P.op("act", lambda e: e.activation(out=rs[:, :n], in_=psms[:, :n], func=AF.Sqrt, bias=k.epsb[:, 0:1], scale=1.0), reads=["psms", "epsb"], writes=["rs0", "rs"])
    P.op("dve", lambda e: e.reciprocal(out=rs[:, :n], in_=rs[:, :n]), reads=["rs0"], writes=["rs"])
    for c in range(8):
        P.op("dve", lambda e, c=c: e.tensor_tensor(out=tmp[c % 2][:, :n], in0=x_t[:, c, :n], in1=rs[:, :n], op=ALU.mult), reads=[("x", sl), "rs"], writes=[("tmp", c % 2)])
        P.op("pool", lambda e, c=c: e.tensor_scalar(out=h_t[:, c, :n], in0=tmp[c % 2][:, :n], scalar1=A[:, c, cond:cond + 1], scalar2=Sh[:, c, cond:cond + 1], op0=ALU.mult, op1=ALU.add),
             reads=[("tmp", c % 2), tag], writes=["h"])


def alloc_norm_tiles(k, es, pfx, ts=512):
    nc = k.nc
    sq = es.enter_context(nc.sbuf_tensor(pfx + "sq", [128, 8, ts], BF16))
    rs = es.enter_context(nc.sbuf_tensor(pfx + "rs", [128, ts], F32))
    tmp = [es.enter_context(nc.sbuf_tensor(pfx + "tmp%d" % i, [128, ts], F32)) for i in range(2)]
    psms = es.enter_context(nc.psum_tensor(pfx + "psms", [128, 512], F32))
    k.epsb = es.enter_context(nc.sbuf_tensor(pfx + "epsb", [128, 1], F32))
    k.P.op("dve", lambda e: e.memset(k.epsb[:], EPS), writes=["epsb"])
    return sq, rs, tmp, psms


def emit_mod_scalars(k, es, pfx, l, jn, i_shift, i_scale, i_gate, gate_mul):
    nc, P, NC = k.nc, k.P, k.NC
    A = es.enter_context(nc.sbuf_tensor(pfx + "A", [128, 8, NC], F32))
    G = es.enter_context(nc.sbuf_tensor(pfx + "G", [128, 8, NC], F32))
    Sh = k.mods[:, i_shift * 8:(i_shift + 1) * 8, :]
    P.op("dve", lambda e: e.tensor_scalar(out=A[:], in0=k.mods[:, i_scale * 8:(i_scale + 1) * 8, :], scalar1=1.0, scalar2=None, op0=ALU.add), reads=["mods"], writes=["A0"])
    P.op("dve", lambda e: e.tensor_tensor(out=A[:], in0=A[:], in1=k.ng[:, l, jn, :].rearrange("p (c o) -> p c o", o=1).to_broadcast([128, 8, NC]), op=ALU.mult), reads=["A0", "ng"], writes=["modsc"])
    if i_gate is not None:
        P.op("dve", lambda e: e.tensor_scalar(out=G[:], in0=k.mods[:, i_gate * 8:(i_gate + 1) * 8, :], scalar1=gate_mul, scalar2=None, op0=ALU.mult), reads=["mods"], writes=["modsg"])
    return A, Sh, G


def load_w_bf16(k, dst, src_rows, ncols, tok, nk=8):
    P = k.P
    for kc in range(nk):
        P.dma(lambda e, kc=kc: e.dma_start(out=dst[:, kc, :], in_=src_rows[kc * 128:(kc + 1) * 128, :], max_dma_last_dim=4096),
              writes=[(tok, kc)], q="pool")


def stage_ffn(k, l, i):
    nc, P, NB, NC = k.nc, k.P, k.NB, k.NC
    jn = 0 if i == 0 else 2
    mi = (0, 1, 2) if i == 0 else (6, 7, 8)
    last = (l == 1 and i == 1)
    with ExitStack() as es:
        w1 = es.enter_context(nc.sbuf_tensor("f_w1", [128, 8, DFF], BF16))
        w3 = es.enter_context(nc.sbuf_tensor("f_w3", [128, 8, DFF], BF16))
        w2 = es.enter_context(nc.sbuf_tensor("f_w2", [128, NFF, D], BF16))
        xt = [es.enter_context(nc.sbuf_tensor("f_x%d" % s, [128, 8, 256], F32)) for s in range(2)]
        ht = es.enter_context(nc.sbuf_tensor("f_h", [128, 8, 256], BF16))
        hid = es.enter_context(nc.sbuf_tensor("f_hid", [128, NFF, 256], BF16))
        sl_t = [es.enter_context(nc.sbuf_tensor("f_s%d" % s, [128, 256], F32)) for s in range(2)]
        ps1 = [es.enter_context(nc.psum_tensor("f_ps1%d" % s, [128, 512], F32)) for s in range(2)]
        ps3 = [es.enter_context(nc.psum_tensor("f_ps3%d" % s, [128, 512], F32)) for s in range(2)]
        pso = [es.enter_context(nc.psum_tensor("f_pso%d" % s, [128, 512], F32)) for s in range(2)]
        ntiles = alloc_norm_tiles(k, es, "f_", 256)
        A, Sh, G = emit_mod_scalars(k, es, "f_", l, jn, mi[0], mi[1], mi[2], 0.5)
        load_w_bf16(k, w1, k.w1[l, i], DFF, "w1")
        load_w_bf16(k, w3, k.w3[l, i], DFF, "w3")
        load_w_bf16(k, w2, k.w2[l, i], D, "w2", nk=NFF)
        it = 0
        for (b, t0, n, cond) in token_tiles(NB, 256):
            if last and cond == NB:
                continue
            s = it % 2
            it += 1
            xsrc = k.XT[b].rearrange("(c p) t -> p c t", p=128)[:, :, t0:t0 + n]
            P.dma(lambda e, s=s, xsrc=xsrc, n=n: e.dma_start(out=xt[s][:, :, :n], in_=xsrc), reads=[("XT", b)], writes=[("x", s)])
            emit_norm_mod(k, ntiles, xt[s], ht, n, A, Sh, cond, "modsc", s)
            for f in range(NFF):
                q = f % 2
                for kc in range(8):
                    P.op("pe", lambda e, f=f, q=q, kc=kc, n=n: e.matmul(ps1[q][:, :n], lhsT=w1[:, kc, f * 128:(f + 1) * 128], rhs=ht[:, kc, :n], start=(kc == 0), stop=(kc == 7)),
                         reads=[("w1", kc), "h"], writes=[("ps1", q)], inc=(kc == 7))
                for kc in range(8):
                    P.op("pe", lambda e, f=f, q=q, kc=kc, n=n: e.matmul(ps3[q][:, :n], lhsT=w3[:, kc, f * 128:(f + 1) * 128], rhs=ht[:, kc, :n], start=(kc == 0), stop=(kc == 7)),
                         reads=[("w3", kc), "h"], writes=[("ps3", q)], inc=(kc == 7))
                P.op("act", lambda e, q=q, n=n: e.activation(out=sl_t[q][:, :n], in_=ps1[q][:, :n], func=AF.Silu), reads=[("ps1", q)], writes=[("sl", q)])
                P.op("dve", lambda e, q=q, f=f, n=n: e.tensor_tensor(out=hid[:, f, :n], in0=sl_t[q][:, :n], in1=ps3[q][:, :n], op=ALU.mult), reads=[("sl", q), ("ps3", q)], writes=[("hid", f)])
            for m in range(8):
                q = m % 2
                for f in range(NFF):
                    P.op("pe", lambda e, m=m, q=q, f=f, n=n: e.matmul(pso[q][:, :n], lhsT=w2[:, f, m * 128:(m + 1) * 128], rhs=hid[:, f, :n], start=(f == 0), stop=(f == NFF - 1)),
                         reads=[("w2", f), ("hid", f)], writes=[("pso", q)], inc=(f == NFF - 1))
                P.op("dve", lambda e, m=m, q=q, s=s, n=n, cond=cond: e.scalar_tensor_tensor(out=xt[s][:, m, :n], in0=pso[q][:, :n], scalar=G[:, m, cond:cond + 1], in1=xt[s][:, m, :n], op0=ALU.mult, op1=ALU.add),
                     reads=[("pso", q), ("x", s), "modsg"], writes=[("x", s)])
            P.dma(lambda e, s=s, xsrc=xsrc, n=n: e.dma_start(out=xsrc, in_=xt[s][:, :, :n]), reads=[("x", s)], writes=[("XT", b)])
        P.end_stage()


def stage_inproj(k, l):
    nc, P, NB, NC = k.nc, k.P, k.NB, k.NC
    NCH = FM_W // 128
    with ExitStack() as es:
        wf = es.enter_context(nc.sbuf_tensor("p_wf", [128, 8, FM_W], BF16))
        wk = es.enter_context(nc.sbuf_tensor("p_wk", [128, 8, TOK_W], BF16))
        xt = [es.enter_context(nc.sbuf_tensor("p_x%d" % s, [128, 8, 512], F32)) for s in range(2)]
        ht = es.enter_context(nc.sbuf_tensor("p_h", [128, 8, 512], BF16))
        ot = [es.enter_context(nc.sbuf_tensor("p_o%d" % s, [128, 512], F32)) for s in range(4)]
        ps = [es.enter_context(nc.psum_tensor("p_ps%d" % s, [128, 512], F32)) for s in range(4)]
        ntiles = alloc_norm_tiles(k, es, "p_")
        A, Sh, G = emit_mod_scalars(k, es, "p_", l, 1, 3, 4, None, 1.0)
        load_w_bf16(k, wf, k.w_fm[l], FM_W, "wf")
        load_w_bf16(k, wk, k.w_tok[l], TOK_W, "wk")
        it = 0
        oi = 0
        for (b, t0, n, cond) in token_tiles(NB):
            s = it % 2
            it += 1
            xsrc = k.XT[b].rearrange("(c p) t -> p c t", p=128)[:, :, t0:t0 + n]
            P.dma(lambda e, s=s, xsrc=xsrc, n=n: e.dma_start(out=xt[s][:, :, :n], in_=xsrc), reads=[("XT", b)], writes=[("x", s)])
            emit_norm_mod(k, ntiles, xt[s], ht, n, A, Sh, cond, "modsc", s)
            for ch in range(NCH):
                q = oi % 4
                oi += 1
                for kc in range(8):
                    P.op("pe", lambda e, ch=ch, q=q, kc=kc, n=n: e.matmul(ps[q][:, :n], lhsT=wf[:, kc, ch * 128:(ch + 1) * 128], rhs=ht[:, kc, :n], start=(kc == 0), stop=(kc == 7)),
                         reads=[("wf", kc), "h"], writes=[("ps", q)], inc=(kc == 7))
                if oi % 2 == 0:
                    P.op("act", lambda e, q=q, n=n: e.activation(out=ot[q][:, :n], in_=ps[q][:, :n], func=AF.Identity), reads=[("ps", q)], writes=[("ot", q)])
                else:
                    P.op("dve", lambda e, q=q, n=n: e.tensor_copy(out=ot[q][:, :n], in_=ps[q][:, :n]), reads=[("ps", q)], writes=[("ot", q)])
                P.dma(lambda e, q=q, ch=ch, b=b, t0=t0, n=n: e.dma_start(out=k.PT[b, ch * 128:(ch + 1) * 128, t0:t0 + n], in_=ot[q][:, :n]), reads=[("ot", q)], writes=[("PT", b)])
            for tb in range(n // 128):
                q = oi % 4
                oi += 1
                for kc in range(8):
                    P.op("pe", lambda e, tb=tb, q=q, kc=kc: e.matmul(ps[q][:, :TOK_W], lhsT=ht[:, kc, tb * 128:(tb + 1) * 128], rhs=wk[:, kc, :], start=(kc == 0), stop=(kc == 7)),
                         reads=[("wk", kc), "h"], writes=[("ps", q)], inc=(kc == 7))
                P.op("dve", lambda e, q=q: e.tensor_copy(out=ot[q][:, :TOK_W], in_=ps[q][:, :TOK_W]), reads=[("ps", q)], writes=[("ot", q)])
                P.dma(lambda e, q=q, b=b, tb=tb, t0=t0: e.dma_start(out=k.TOK[b, t0 + tb * 128:t0 + (tb + 1) * 128, :], in_=ot[q][:, :TOK_W]), reads=[("ot", q)], writes=[("TOK", b)])
        P.end_stage()


def stage_attention(k, l):
    nc, P, NB = k.nc, k.P, k.NB
    NKB = T // 128
    with ExitStack() as es:
        al = lambda name, shape, dt=F32: es.enter_context(nc.sbuf_tensor("a_" + name, list(shape), dt))
        cos = al("cos", [64, SEQ]); sin = al("sin", [64, SEQ]); rm = al("rm", [64, 64]); o64 = al("o64", [64, 64])
        gq = al("gq", [64, 1]); gk = al("gk", [64, 1]); epsb = al("eps", [64, 1])
        onesb = al("onesb", [128, 64], BF16)
        raw = [al("raw%d" % i, [64, T]) for i in range(2)]
        kT = [al("kT%d" % i, [64, T], BF16) for i in range(2)]
        qT = [al("qT%d" % i, [64, T], BF16) for i in range(2)]
        vraw = al("vraw", [128, NKB, 128])
        vb = al("vb", [128, NKB, 128], BF16)
        sq = al("sq", [64, 512]); rs = al("rs", [64, 512]); kn = al("kn", [64, 512]); t1 = al("t1", [64, 512]); t2 = al("t2", [64, 512])
        pT = [al("pT%d" % i, [128, 512], BF16) for i in range(3)]
        rden = al("rden", [64, 512])
        ot = [al("ot%d" % i, [64, 512], BF16) for i in range(2)]
        psa = es.enter_context(nc.psum_tensor("a_psa", [64, 512], F32))
        psr = es.enter_context(nc.psum_tensor("a_psr", [64, 512], F32))
        pss = [es.enter_context(nc.psum_tensor("a_pss%d" % i, [128, 512], F32)) for i in range(3)]
        pso = es.enter_context(nc.psum_tensor("a_pso", [64, 512], F32))
        psd = es.enter_context(nc.psum_tensor("a_psd", [64, 512], F32))
        P.dma(lambda e: e.dma_start(out=cos[:], in_=k.rope_cos), writes=["cos"])
        P.dma(lambda e: e.dma_start(out=sin[:], in_=k.rope_sin), writes=["sin"])
        P.dma(lambda e: e.dma_start(out=rm[:], in_=k.rope_rm), writes=["rm"])
        P.dma(lambda e: e.dma_start(out=gq[:], in_=k.at_qn_g[l]), writes=["gq"])
        P.dma(lambda e: e.dma_start(out=gk[:], in_=k.at_kn_g[l]), writes=["gk"])
        P.op("dve", lambda e: e.memset(o64[:], 1.0 / 64), writes=["o64"])
        P.op("dve", lambda e: e.memset(epsb[:], EPS), writes=["epsb"])
        P.op("dve", lambda e: e.memset(onesb[:], 1.0), writes=["onesb"])
        cnt = {"raw": 0, "p": 0, "o": 0}

        def prep(b, row0, g_t, gtok, dst, dtok):
            ri = cnt["raw"] % 2
            cnt["raw"] += 1
            P.dma(lambda e: e.dma_start(out=raw[ri][:], in_=k.PT[b, row0:row0 + 64, :]), reads=[("PT", b)], writes=[("raw", ri)])
            for (t0, n) in [(0, CTX)] + [(CTX + i * 512, 512) for i in range(4)]:
                P.op("act", lambda e, t0=t0, n=n: e.activation(out=sq[:, :n], in_=raw[ri][:, t0:t0 + n], func=AF.Square), reads=[("raw", ri)], writes=["sq"])
                P.op("pe", lambda e, n=n: e.matmul(psa[:, :n], lhsT=o64[:], rhs=sq[:, :n], start=True, stop=True), reads=["sq", "o64"], writes=["psa"])
                P.op("act", lambda e, n=n: e.activation(out=rs[:, :n], in_=psa[:, :n], func=AF.Sqrt, bias=epsb[:, 0:1], scale=1.0), reads=["psa", "epsb"], writes=["rs0", "rs"])
                P.op("dve", lambda e, n=n: e.reciprocal(out=rs[:, :n], in_=rs[:, :n]), reads=["rs0"], writes=["rs"])
                if t0 == 0:
                    P.op("dve", lambda e, t0=t0, n=n: e.scalar_tensor_tensor(out=dst[:, t0:t0 + n], in0=raw[ri][:, t0:t0 + n], scalar=g_t[:, 0:1], in1=rs[:, :n], op0=ALU.mult, op1=ALU.mult),
                         reads=[("raw", ri), "rs", gtok], writes=[dtok])
                    continue
                P.op("dve", lambda e, t0=t0, n=n: e.scalar_tensor_tensor(out=kn[:, :n], in0=raw[ri][:, t0:t0 + n], scalar=g_t[:, 0:1], in1=rs[:, :n], op0=ALU.mult, op1=ALU.mult),
                     reads=[("raw", ri), "rs", gtok], writes=["kn"])
                P.op("pe", lambda e, n=n: e.matmul(psr[:, :n], lhsT=rm[:], rhs=kn[:, :n], start=True, stop=True), reads=["kn", "rm"], writes=["psr"])
                P.op("pool", lambda e, t0=t0, n=n: e.tensor_tensor(out=t1[:, :n], in0=kn[:, :n], in1=cos[:, t0 - CTX:t0 - CTX + n], op=ALU.mult), reads=["kn", "cos"], writes=["t1"])
                P.op("dve", lambda e, t0=t0, n=n: e.tensor_tensor(out=t2[:, :n], in0=psr[:, :n], in1=sin[:, t0 - CTX:t0 - CTX + n], op=ALU.mult), reads=["psr", "sin"], writes=["t2"])
                P.op("dve", lambda e, t0=t0, n=n: e.tensor_tensor(out=dst[:, t0:t0 + n], in0=t1[:, :n], in1=t2[:, :n], op=ALU.add), reads=["t1", "t2"], writes=[dtok])

        def do_tile(b, hq, kvh, qi, q0, nq, kbs):
            def s_mm(kb, pi):
                P.op("pe", lambda e: e.matmul(pss[pi][:, :nq], lhsT=kT[kvh][:, kb * 128:(kb + 1) * 128], rhs=qT[qi][:, q0:q0 + nq], start=True, stop=True),
                     reads=[("kT", kvh), ("qT", qi)], writes=[("pss", pi)])
            pis = []
            for j in range(len(kbs)):
                pis.append(cnt["p"] % 3)
                cnt["p"] += 1
            s_mm(kbs[0], pis[0])
            for j, kb in enumerate(kbs):
                if j + 1 < len(kbs):
                    s_mm(kbs[j + 1], pis[j + 1])
                pi = pis[j]
                P.op("act", lambda e, pi=pi: e.activation(out=pT[pi][:, :nq], in_=pss[pi][:, :nq], func=AF.Exp, scale=0.125), reads=[("pss", pi)], writes=[("pT", pi)])
                P.op("pe", lambda e, pi=pi, kb=kb, j=j: e.matmul(pso[:, :nq], lhsT=vb[:, kb, kvh * 64:(kvh + 1) * 64], rhs=pT[pi][:, :nq], start=(j == 0), stop=(j == len(kbs) - 1)),
                     reads=[("pT", pi), "vb"], writes=["pso"], inc=False)
                P.op("pe", lambda e, pi=pi, j=j: e.matmul(psd[:, :nq], lhsT=onesb[:], rhs=pT[pi][:, :nq], start=(j == 0), stop=(j == len(kbs) - 1)),
                     reads=[("pT", pi), "onesb"], writes=["psd"])
            oi = cnt["o"] % 2
            cnt["o"] += 1
            P.op("dve", lambda e: e.reciprocal(out=rden[:, :nq], in_=psd[:, :nq]), reads=["psd"], writes=["rden"])
            P.op("dve", lambda e: e.tensor_tensor(out=ot[oi][:, :nq], in0=pso[:, :nq], in1=rden[:, :nq], op=ALU.mult), reads=["pso", "rden"], writes=[("ot", oi)])
            P.dma(lambda e: e.dma_start(out=k.YT[b, 3, hq * 64:(hq + 1) * 64, q0:q0 + nq], in_=ot[oi][:, :nq]), reads=[("ot", oi)], writes=[("YT", b)])

        for b in range(NB):
            for kvh in range(2):
                prep(b, FM_AK + kvh * 64, gk, "gk", kT[kvh], ("kT", kvh))
            vsrc = k.TOK[b].rearrange("(blk p) c -> p blk c", p=128)[:, :, TK_AV:TK_AV + 128]
            P.dma(lambda e, vsrc=vsrc: e.dma_start(out=vraw[:], in_=vsrc), reads=[("TOK", b)], writes=["vraw"])
            P.op("pool", lambda e: e.tensor_copy(out=vb[:], in_=vraw[:]), reads=["vraw"], writes=["vb"])
            for hq in range(4):
                kvh = hq // 2
                qi = hq % 2
                prep(b, FM_AQ + hq * 64, gq, "gq", qT[qi], ("qT", qi))
                for (q0, nq, kbs) in [(0, CTX, [0, 1])] + [(CTX + i * 512, 512, list(range(NKB))) for i in range(4)]:
                    do_tile(b, hq, kvh, qi, q0, nq, kbs)
        P.end_stage()


def stage_hgrn2(k, l):
    nc, P, NB = k.nc, k.P, k.NB
    NCK = T // 64
    with ExitStack() as es:
        al = lambda name, shape, dt=F32: es.enter_context(nc.sbuf_tensor("h_" + name, list(shape), dt))
        m01 = al("m01", [128, T]); mk = al("mk", [128, 2, 64]); bd = al("bd", [128, 128])
        lg = al("lg", [128, 2, 2]); lb = al("lb", [128, 2]); oml = al("oml", [128, 2]); gn = al("gn", [128, 1]); epsb = al("eps", [128, 1])
        z = al("z", [128, T]); fgt = al("fgt", [128, T]); bb = al("bb", [128, T]); tmp = al("tmp", [128, T])
        ex = [al("ex%d" % i, [128, T]) for i in range(2)]
        q = al("q", [128, T]); kk = al("kk", [128, T]); kd = al("kd", [128, T]); O = al("O", [128, T])
        qt = al("qt", [128, T], BF16); ktI = [al("kt%d" % i, [128, T], BF16) for i in range(4)]; qd = al("qd", [128, T], BF16)
        dec = al("dec", [128, NCK, 1])
        Vb = al("Vb", [128, NCK, 256], BF16)
        kdT = [al("kdT%d" % i, [64, 128], BF16) for i in range(2)]
        scm = [al("scm%d" % i, [128, 64], BF16) for i in range(2)]
        S32 = al("S32", [128, 64]); S16 = [al("S16%d" % i, [128, 64], BF16) for i in range(2)]
        sq = al("sq", [128, 512]); rs = al("rs", [128, 512]); yb = [al("yb%d" % i, [128, 512], BF16) for i in range(2)]
        pst = [es.enter_context(nc.psum_tensor("h_pst%d" % i, [128, 512], F32)) for i in range(2)]
        pss = [es.enter_context(nc.psum_tensor("h_pss%d" % i, [128, 512], F32)) for i in range(2)]
        pso = [es.enter_context(nc.psum_tensor("h_pso%d" % i, [128, 512], F32)) for i in range(2)]
        pskv = [es.enter_context(nc.psum_tensor("h_pskv%d" % i, [128, 512], F32)) for i in range(2)]
        P.dma(lambda e: e.dma_start(out=mk[:], in_=k.hg_masks), writes=["mk"])
        P.dma(lambda e: e.dma_start(out=bd[:], in_=k.bd64), writes=["bd"])
        P.dma(lambda e: e.dma_start(out=lg[:], in_=k.hg_lb), writes=["lg"])
        P.dma(lambda e: e.dma_start(out=gn[:], in_=k.hg_norm_g[l]), writes=["gn"])
        P.op("dve", lambda e: e.memset(epsb[:], EPS), writes=["epsb"])
        P.op("dve", lambda e: e.memset(m01[:], 1.0), writes=["m01"])
        P.op("dve", lambda e: e.memset(m01[:].rearrange("p (c j) -> p c j", j=64)[:, :, 0:1], 0.0), writes=["m01"])
        for i4 in range(4):
            P.op("pool", lambda e, i4=i4: e.memset(ktI[i4][:], 0.0), writes=[("kt", i4)])
        if l == 0:
            P.op("dve", lambda e: e.memset(lb[:], 0.0), writes=["lb"])
            P.op("dve", lambda e: e.memset(oml[:], 1.0), writes=["oml"])
        else:
            P.op("dve", lambda e: e.tensor_tensor(out=lb[:], in0=lg[:, :, 1], in1=lg[:, :, 0], op=ALU.subtract), reads=["lg"], writes=["lb0"])
            P.op("act", lambda e: e.activation(out=lb[:], in_=lb[:], func=AF.Sigmoid), reads=["lb0"], writes=["lb", "lb0"])
            P.op("dve", lambda e: e.tensor_scalar(out=oml[:], in0=lb[:], scalar1=-1.0, scalar2=1.0, op0=ALU.mult, op1=ALU.add), reads=["lb"], writes=["oml"])
        cnt = {"c": 0, "y": 0}
        bb3 = bb[:].rearrange("p (c j) -> p c j", j=64)
        tmp3 = tmp[:].rearrange("p (c j) -> p c j", j=64)

        def do_chunk(b, hp, d, c, first):
            i = cnt["c"] % 2
            cnt["c"] += 1
            cs = slice(c * 64, (c + 1) * 64)
            P.op("pe", lambda e: e.transpose(pst[i][:64, :128], kd[:, cs], k.idn[:]), reads=["kd", "idn"], writes=[("pst", i)])
            P.op("act", lambda e: e.activation(out=kdT[i][:], in_=pst[i][:64, :128], func=AF.Identity), reads=[("pst", i)], writes=[("kdT", i)])
            for h2 in range(2):
                pb = h2 * 64
                for I in range(4):
                    ts_ = slice(c * 64 + 16 * I, c * 64 + 16 * I + 16)
                    P.op("pe", lambda e, pb=pb, I=I, ts_=ts_: e.matmul(pss[i][pb:pb + 64, 16 * I:16 * I + 16], lhsT=ktI[I][pb:pb + 64, cs], rhs=qt[pb:pb + 64, ts_], start=True, stop=True),
                         reads=[("kt", I), "qt"], writes=[("pss", i)], inc=(h2 == 1 and I == 3))
            for h2 in range(2):
                pb = h2 * 64
                h = hp * 2 + h2
                P.op("pe", lambda e, pb=pb, h=h: e.matmul(pskv[i][pb:pb + 64, :64], lhsT=kdT[i][:, pb:pb + 64], rhs=Vb[0:64, c, h * 64:(h + 1) * 64], start=True, stop=True),
                     reads=[("kdT", i), "Vb"], writes=[("pskv", i)], inc=(h2 == 1))
            P.op("dve", lambda e: e.tensor_tensor(out=scm[i][:], in0=pss[i][:, :64], in1=mk[:, d, :], op=ALU.mult), reads=[("pss", i), "mk"], writes=[("scm", i)])
            for h2 in range(2):
                pb = h2 * 64
                h = hp * 2 + h2
                P.op("pe", lambda e, pb=pb, h=h: e.matmul(pso[i][pb:pb + 64, :64], lhsT=Vb[pb:pb + 64, c, h * 64:(h + 1) * 64], rhs=scm[i][pb:pb + 64, :], start=True, stop=False),
                     reads=[("scm", i), "Vb"], writes=[("pso", i)], inc=False)
                P.op("pe", lambda e, pb=pb: e.matmul(pso[i][pb:pb + 64, :64], lhsT=S16[1 - i][pb:pb + 64, :], rhs=qd[pb:pb + 64, cs], start=False, stop=True),
                     reads=[("S16", 1 - i), "qd"], writes=[("pso", i)], inc=(h2 == 1))
            if d == 0:
                P.op("act", lambda e: e.activation(out=O[:, cs], in_=pso[i][:, :64], func=AF.Identity), reads=[("pso", i)], writes=["O"])
            else:
                P.op("dve", lambda e: e.tensor_tensor(out=O[:, cs], in0=O[:, cs], in1=pso[i][:, :64], op=ALU.add), reads=[("pso", i), "O"], writes=["O"])
            P.op("dve", lambda e: e.scalar_tensor_tensor(out=S32[:], in0=S32[:], scalar=dec[:, c, :], in1=pskv[i][:, :64], op0=ALU.mult, op1=ALU.add),
                 reads=["S32", "dec", ("pskv", i)], writes=["S32"])
            P.op("act", lambda e: e.activation(out=S16[i][:], in_=S32[:], func=AF.Identity), reads=["S32"], writes=[("S16", i)])

        def do_dir(b, hp, d):
            rev = (lambda ap: ap[:, ::-1]) if d == 1 else (lambda ap: ap)
            last = 63 if d == 0 else 0
            r0 = FM_HF + d * 256 + hp * 128
            P.dma(lambda e: e.dma_start(out=z[:], in_=k.PT[b, r0:r0 + 128, :]), reads=[("PT", b)], writes=["z"])
            P.op("act", lambda e: e.activation(out=fgt[:], in_=z[:], func=AF.Sigmoid), reads=["z"], writes=["fgt"])
            P.op("dve", lambda e: e.tensor_scalar(out=fgt[:], in0=fgt[:], scalar1=oml[:, hp:hp + 1], scalar2=lb[:, hp:hp + 1], op0=ALU.mult, op1=ALU.add), reads=["fgt", "oml", "lb"], writes=["fgt"])
            P.op("dve", lambda e: e.tensor_scalar(out=fgt[:], in0=fgt[:], scalar1=1e-30, scalar2=None, op0=ALU.max), reads=["fgt"], writes=["fgt"])
            P.op("act", lambda e: e.activation(out=z[:], in_=fgt[:], func=AF.Ln), reads=["fgt"], writes=["z"])
            P.op("dve", lambda e: e.tensor_scalar(out=kk[:], in0=fgt[:], scalar1=-1.0, scalar2=1.0, op0=ALU.mult, op1=ALU.add), reads=["fgt"], writes=["kk"])
            P.op("dve", lambda e: e.tensor_tensor_scan(out=rev(bb[:]), data0=m01[:], data1=rev(z[:]), initial=0.0, op0=ALU.mult, op1=ALU.add), reads=["z", "m01"], writes=["bb"])
            ref = 0 if d == 0 else 15
            bb4 = bb[:].rearrange("p (c i j) -> p c i j", i=4, j=16)
            tmp4 = tmp[:].rearrange("p (c i j) -> p c i j", i=4, j=16)
            P.op("dve", lambda e: e.tensor_tensor(out=tmp4, in0=bb4, in1=bb4[:, :, :, ref:ref + 1].to_broadcast([128, NCK, 4, 16]), op=ALU.subtract), reads=["bb"], writes=["tmp"])
            P.op("act", lambda e: e.activation(out=ex[0][:], in_=tmp[:], func=AF.Exp), reads=["tmp"], writes=[("ex", 0)])
            P.op("pool", lambda e: e.tensor_tensor(out=qt[:], in0=q[:], in1=ex[0][:], op=ALU.mult), reads=["q", ("ex", 0)], writes=["qt"])
            ex3 = [ex[i][:].rearrange("p (c j) -> p c j", j=64) for i in range(2)]
            kk3 = kk[:].rearrange("p (c j) -> p c j", j=64)
            for I in range(4):
                cs_ = slice(0, 16 * (I + 1)) if d == 0 else slice(16 * I, 64)
                w = cs_.stop - cs_.start
                e_ = ex3[I % 2][:, :, cs_]
                rp = 16 * I + ref
                kt3 = ktI[I][:].rearrange("p (c j) -> p c j", j=64)[:, :, cs_]
                P.op("dve", lambda e, cs_=cs_, w=w, rp=rp: e.scalar_tensor_tensor(out=tmp3[:, :, cs_], in0=bb3[:, :, cs_], scalar=-1.0, in1=bb3[:, :, rp:rp + 1].to_broadcast([128, NCK, w]), op0=ALU.mult, op1=ALU.add),
                     reads=["bb"], writes=["tmp"])
                P.op("dve", lambda e, cs_=cs_: e.tensor_scalar(out=tmp3[:, :, cs_], in0=tmp3[:, :, cs_], scalar1=60.0, scalar2=None, op0=ALU.min), reads=["tmp"], writes=["tmp"])
                P.op("act", lambda e, cs_=cs_, e_=e_: e.activation(out=e_, in_=tmp3[:, :, cs_], func=AF.Exp), reads=["tmp"], writes=[("ex", I % 2)])
                P.op("pool", lambda e, cs_=cs_, e_=e_, kt3=kt3: e.tensor_tensor(out=kt3, in0=kk3[:, :, cs_], in1=e_, op=ALU.mult), reads=["kk", ("ex", I % 2)], writes=[("kt", I)])
            P.op("act", lambda e: e.activation(out=ex[0][:], in_=bb[:], func=AF.Exp), reads=["bb"], writes=[("ex", 0)])
            P.op("dve", lambda e: e.tensor_tensor(out=qd[:], in0=q[:], in1=ex[0][:], op=ALU.mult), reads=["q", ("ex", 0)], writes=["qd"])
            P.op("dve", lambda e: e.tensor_tensor(out=tmp3, in0=bb3, in1=bb3[:, :, last:last + 1].to_broadcast([128, NCK, 64]), op=ALU.subtract), reads=["bb"], writes=["tmp"])
            P.op("act", lambda e: e.activation(out=ex[1][:], in_=tmp[:], func=AF.Exp, scale=-1.0), reads=["tmp"], writes=[("ex", 1)])
            P.op("pool", lambda e: e.tensor_tensor(out=kd[:], in0=kk[:], in1=ex[1][:], op=ALU.mult), reads=["kk", ("ex", 1)], writes=["kd"])
            P.op("act", lambda e: e.activation(out=dec[:], in_=bb3[:, :, last:last + 1], func=AF.Exp), reads=["bb"], writes=["dec"])
            if "dump" in k.opts and (b, hp, d) == k.opts["dump"]:
                for j, (tl, tk) in enumerate([(z, "z"), (bb, "bb"), (kd, "kd"), (kk, "kk"), (fgt, "fgt")]):
                    P.dma(lambda e, j=j, tl=tl: e.dma_start(out=k.dump[j], in_=tl[:]), reads=[tk], writes=[("dump", j)])
            P.op("dve", lambda e: e.memset(S32[:], 0.0), writes=["S32"])
            for i in range(2):
                P.op("dve", lambda e, i=i: e.memset(S16[i][:], 0.0), writes=[("S16", i)])
            order = list(range(NCK)) if d == 0 else [3, 2, 1, 0] + list(range(NCK - 1, 3, -1))
            for idx, c in enumerate(order):
                do_chunk(b, hp, d, c, idx == 0)

        def do_pair(b, hp):
            r0 = FM_HQ + hp * 128
            P.dma(lambda e: e.dma_start(out=q[:], in_=k.PT[b, r0:r0 + 128, :]), reads=[("PT", b)], writes=["q"])
            P.op("act", lambda e: e.activation(out=q[:], in_=q[:], func=AF.Silu), reads=["q"], writes=["q"])
            for d in range(2):
                do_dir(b, hp, d)
            if "dump" in k.opts and (b, hp) == k.opts["dump"][:2]:
                P.dma(lambda e: e.dma_start(out=k.dump[5], in_=O[:]), reads=["O"], writes=[("dump", 5)])
            g0 = FM_HG + hp * 128
            P.dma(lambda e: e.dma_start(out=z[:], in_=k.PT[b, g0:g0 + 128, :]), reads=[("PT", b)], writes=["z"])
            P.op("act", lambda e: e.activation(out=z[:], in_=z[:], func=AF.Sigmoid), reads=["z"], writes=["z"])
            for (t0, n) in [(0, CTX)] + [(CTX + i * 512, 512) for i in range(4)]:
                do_out(b, hp, t0, n)

        def do_out(b, hp, t0, n):
            yi = cnt["y"] % 2
            cnt["y"] += 1
            P.op("act", lambda e: e.activation(out=sq[:, :n], in_=O[:, t0:t0 + n], func=AF.Square), reads=["O"], writes=["sq"])
            P.op("pe", lambda e: e.matmul(pss[0][:, :n], lhsT=bd[:], rhs=sq[:, :n], start=True, stop=True), reads=["sq", "bd"], writes=[("pss", 0)])
            P.op("act", lambda e: e.activation(out=rs[:, :n], in_=pss[0][:, :n], func=AF.Sqrt, bias=epsb[:, 0:1], scale=1.0), reads=[("pss", 0), "epsb"], writes=["rs0", "rs"])
            P.op("dve", lambda e: e.reciprocal(out=rs[:, :n], in_=rs[:, :n]), reads=["rs0"], writes=["rs"])
            P.op("dve", lambda e: e.scalar_tensor_tensor(out=sq[:, :n], in0=O[:, t0:t0 + n], scalar=gn[:, 0:1], in1=rs[:, :n], op0=ALU.mult, op1=ALU.mult), reads=["O", "gn", "rs"], writes=["sq"])
            P.op("pool", lambda e: e.tensor_tensor(out=yb[yi][:, :n], in0=sq[:, :n], in1=z[:, t0:t0 + n], op=ALU.mult), reads=["sq", "z"], writes=[("yb", yi)])
            P.dma(lambda e: e.dma_start(out=k.YT[b, 2, hp * 128:(hp + 1) * 128, t0:t0 + n], in_=yb[yi][:, :n]), reads=[("yb", yi)], writes=[("YT", b)])

        for b in range(NB):
            vsrc = k.TOK[b].rearrange("(c s) v -> s c v", s=64)[:, :, TK_HV:TK_HV + 256]
            for h2 in range(2):
                P.dma(lambda e, h2=h2, vsrc=vsrc: e.dma_start(out=Vb[h2 * 64:(h2 + 1) * 64, :, :], in_=vsrc), reads=[("TOK", b)], writes=["Vb"], q="pool")
            for hp in range(2):
                do_pair(b, hp)
        P.end_stage()


class Slots:
    def __init__(self, aps, name, mod=0):
        self.aps, self.name, self.i, self.mod = aps, name, 0, mod

    def get(self):
        j = self.i % len(self.aps)
        self.i += 1
        return self.aps[j], (self.name, j % self.mod if self.mod else j)


def stage_deltanet(k, l):
    nc, P, NB = k.nc, k.P, k.NB
    NCK = T // 64
    P.serial = k.opts.get("serial", True)
    with ExitStack() as es:
        al = lambda name, shape, dt=F32: es.enter_context(nc.sbuf_tensor("d_" + name, list(shape), dt))
        gm64 = al("gm64", [64, 2, 2, 128]); gmbd = al("gmbd", [128, 2, 2, 128]); idn2 = al("idn2", [128, 64]); bd = al("bd", [128, 128])
        ones64 = al("ones64", [64, 128])
        cw = al("cw", [128, 6, 5]); alog = al("alog", [128, 8]); dtb = al("dtb", [128, 8]); gno = al("gno", [128, 64]); epsb = al("eps", [128, 1])
        qkv = [al("qkv%d" % i, [128, T]) for i in range(6)]
        xin = al("xin", [128, T]); acc = al("acc", [128, T])
        sq = al("sq", [128, 512]); rs = al("rs", [128, 512])
        ab = al("ab", [128, NCK, 16]); gt = al("gt", [128, NCK, 8]); bt = al("bt", [128, NCK, 8]); t8 = al("t8", [128, NCK, 8]); t8b = al("t8b", [128, NCK, 8])
        gcs = al("gcs", [128, NCK, 8]); gts = al("gts", [128, NCK, 8])
        gcBD = al("gcBD", [128, NCK, 4]); gtBD = al("gtBD", [128, NCK, 4]); bBD = al("bBD", [128, NCK, 4])
        eg = al("eg", [128, NCK, 4]); ekd = al("ekd", [128, NCK, 4]); glast = al("glast", [128, NCK, 4]); nbeta = al("nbeta", [128, NCK, 4]); wsc = al("wsc", [128, NCK, 4])
        u_all = al("u_all", [128, NCK, 64]); wT_all = al("wT_all", [128, NCK, 64]); kdec_all = al("kdec_all", [128, NCK, 64]); attnT_all = al("attnT_all", [128, NCK, 128])
        O_tok = al("O_tok", [128, NCK, 64]); ssq = al("ssq", [128, NCK]); S = al("S", [128, 64])
        yb = [al("yb%d" % i, [128, 512], BF16) for i in range(2)]
        wk = Slots([al("wk%d" % i, [128, 128])[:] for i in range(28)], "wk")
        wv = Slots([al("wv%d" % i, [128, 64])[:] for i in range(8)], "wv")
        wr = Slots([al("wr%d" % i, [128, 128])[:] for i in range(6)], "wr")
        banks = [es.enter_context(nc.psum_tensor("d_ps%d" % i, [128, 512], F32)) for i in range(8)]
        ps = Slots([banks[i][:, j * 128:(j + 1) * 128] for j in range(k.opts.get("psj", 4)) for i in range(7)], "ps", mod=7)
        psg = banks[7]
        for i in range(8):
            P.op("dve", lambda e, i=i: e.memset(banks[i][:], 0.0), writes=[("ps", j) for j in range(7)] + ["psg"])
        P.dma(lambda e: e.dma_start(out=gm64[:], in_=k.dn_gm64), writes=["gm64"])
        P.dma(lambda e: e.dma_start(out=gmbd[:], in_=k.dn_gmbd), writes=["gmbd"])
        P.dma(lambda e: e.dma_start(out=idn2[:], in_=k.dn_idn2), writes=["idn2"])
        P.dma(lambda e: e.dma_start(out=bd[:], in_=k.bd64), writes=["bd"])
        P.dma(lambda e: e.dma_start(out=cw[:], in_=k.dn_conv[l]), writes=["cw"])
        P.dma(lambda e: e.dma_start(out=alog[:], in_=k.dn_a_log[l]), writes=["alog"])
        P.dma(lambda e: e.dma_start(out=dtb[:], in_=k.dn_dt_bias[l]), writes=["dtb"])
        P.dma(lambda e: e.dma_start(out=gno[:], in_=k.dn_norm_g[l]), writes=["gno"])
        P.op("dve", lambda e: e.memset(epsb[:], EPS), writes=["epsb"])
        P.op("dve", lambda e: e.memset(ones64[:], 1.0), writes=["ones64"])
        P.op("act", lambda e: e.activation(out=alog[:], in_=alog[:], func=AF.Exp), reads=["alog"], writes=["alog"])
        P.op("dve", lambda e: e.tensor_scalar(out=alog[:], in0=alog[:], scalar1=-1.0, scalar2=None, op0=ALU.mult), reads=["alog"], writes=["alog"])
        cnt = {"y": 0}

        def prep_tile(b, ti):
            r0 = ti * 128
            P.dma(lambda e: e.dma_start(out=xin[:], in_=k.PT[b, r0:r0 + 128, :]), reads=[("PT", b)], writes=["xin"])
            P.op("dve", lambda e: e.tensor_scalar(out=acc[:], in0=xin[:], scalar1=cw[:, ti, 2:3], scalar2=None, op0=ALU.mult), reads=["xin", "cw"], writes=["acc"])
            for (s0, s1) in [(0, CTX), (CTX, T)]:
                for j in (0, 1, 3, 4):
                    sh = j - 2
                    o0, o1 = max(s0, s0 - sh), min(s1, s1 - sh)
                    P.op("dve", lambda e, j=j, sh=sh, o0=o0, o1=o1: e.scalar_tensor_tensor(out=acc[:, o0:o1], in0=xin[:, o0 + sh:o1 + sh], scalar=cw[:, ti, j:j + 1], in1=acc[:, o0:o1], op0=ALU.mult, op1=ALU.add),
                         reads=["xin", "cw", "acc"], writes=["acc"])
            dst = qkv[ti]
            if ti >= 4:
                P.op("act", lambda e: e.activation(out=dst[:], in_=acc[:], func=AF.Silu), reads=["acc"], writes=[("qkv", ti)])
                return
            P.op("act", lambda e: e.activation(out=acc[:], in_=acc[:], func=AF.Silu), reads=["acc"], writes=["acc"])
            qs = 0.125 if ti < 2 else 1.0
            for (t0, n) in [(0, CTX)] + [(CTX + i * 512, 512) for i in range(4)]:
                norm_tile(dst, ti, t0, n, qs)

        def norm_tile(dst, ti, t0, n, qs):
            pp, pt = ps.get()
            bank_ap = banks[0]
            P.op("act", lambda e: e.activation(out=sq[:, :n], in_=acc[:, t0:t0 + n], func=AF.Square), reads=["acc"], writes=["sq"])
            P.op("pe", lambda e: e.matmul(psg[:, :n], lhsT=bd[:], rhs=sq[:, :n], start=True, stop=True), reads=["sq", "bd"], writes=["psg"])
            P.op("act", lambda e: e.activation(out=rs[:, :n], in_=psg[:, :n], func=AF.Sqrt, bias=epsb[:, 0:1], scale=64.0), reads=["psg", "epsb"], writes=["rs0", "rs"])
            P.op("dve", lambda e: e.reciprocal(out=rs[:, :n], in_=rs[:, :n]), reads=["rs0"], writes=["rs"])
            P.op("dve", lambda e: e.scalar_tensor_tensor(out=dst[:, t0:t0 + n], in0=acc[:, t0:t0 + n], scalar=qs, in1=rs[:, :n], op0=ALU.mult, op1=ALU.mult), reads=["acc", "rs"], writes=[("qkv", ti)])

        def prep_gates(b):
            src = k.TOK[b].rearrange("(c s) v -> s c v", s=64)[:, :, TK_A:TK_A + 16]
            for h2 in range(2):
                P.dma(lambda e, h2=h2: e.dma_start(out=ab[h2 * 64:(h2 + 1) * 64, :, :], in_=src), reads=[("TOK", b)], writes=["ab"])
            a3, b3 = ab[:, :, 0:8], ab[:, :, 8:16]
            bc8 = lambda t: t[:, :].rearrange("p (o h) -> p o h", o=1).to_broadcast([128, NCK, 8])
            P.op("dve", lambda e: e.tensor_tensor(out=t8[:], in0=a3, in1=bc8(dtb), op=ALU.add), reads=["ab", "dtb"], writes=["t8"])
            P.op("act", lambda e: e.activation(out=t8b[:], in_=t8[:], func=AF.Abs), reads=["t8"], writes=["t8b"])
            P.op("act", lambda e: e.activation(out=t8b[:], in_=t8b[:], func=AF.Exp, scale=-1.0), reads=["t8b"], writes=["t8b"])
            P.op("dve", lambda e: e.tensor_scalar(out=t8b[:], in0=t8b[:], scalar1=1.0, scalar2=None, op0=ALU.add), reads=["t8b"], writes=["t8b"])
            P.op("act", lambda e: e.activation(out=t8b[:], in_=t8b[:], func=AF.Ln), reads=["t8b"], writes=["t8b"])
            P.op("dve", lambda e: e.tensor_scalar(out=t8[:], in0=t8[:], scalar1=0.0, scalar2=None, op0=ALU.max), reads=["t8"], writes=["t8"])
            P.op("dve", lambda e: e.tensor_tensor(out=t8[:], in0=t8[:], in1=t8b[:], op=ALU.add), reads=["t8", "t8b"], writes=["t8"])
            P.op("dve", lambda e: e.tensor_tensor(out=gt[:], in0=t8[:], in1=bc8(alog), op=ALU.mult), reads=["t8", "alog"], writes=["gt"])
            P.op("act", lambda e: e.activation(out=bt[:], in_=b3, func=AF.Sigmoid), reads=["ab"], writes=["bt"])
            for d in range(2):
                P.op("pe", lambda e, d=d: e.matmul(psg[:, d * 144:(d + 1) * 144], lhsT=gm64[:, d, 0, :], rhs=gt[0:64, :, d * 4:(d + 1) * 4], start=True, stop=True), reads=["gm64", "gt"], writes=["psg"])
            P.op("dve", lambda e: e.tensor_copy(out=gcs[:].rearrange("p c (d h) -> p d c h", d=2), in_=psg[:, 0:288].rearrange("p (d c h) -> p d c h", d=2, h=4)), reads=["psg"], writes=["gcs"])
            for d in range(2):
                P.op("pe", lambda e, d=d: e.matmul(psg[:, d * 144:(d + 1) * 144], lhsT=ones64[:], rhs=gt[0:64, :, d * 4:(d + 1) * 4], start=True, stop=True), reads=["ones64", "gt"], writes=["psg"])
            P.op("dve", lambda e: e.tensor_copy(out=gts[:].rearrange("p c (d h) -> p d c h", d=2), in_=psg[:, 0:288].rearrange("p (d c h) -> p d c h", d=2, h=4)), reads=["psg"], writes=["gts"])
            for m in range(4):
                d, hp = m // 2, m % 2
                for h2 in range(2):
                    col = d * 4 + hp * 2 + h2
                    rr = slice(h2 * 64, (h2 + 1) * 64)
                    P.op("dve", lambda e, m=m, col=col, rr=rr: e.tensor_copy(out=gcBD[rr, :, m:m + 1], in_=gcs[rr, :, col:col + 1]), reads=["gcs"], writes=["gcBD"])
                    P.op("dve", lambda e, m=m, col=col, rr=rr: e.tensor_copy(out=gtBD[rr, :, m:m + 1], in_=gts[rr, :, col:col + 1]), reads=["gts"], writes=["gtBD"])
                    P.op("dve", lambda e, m=m, col=col, rr=rr: e.tensor_copy(out=bBD[rr, :, m:m + 1], in_=bt[rr, :, col:col + 1]), reads=["bt"], writes=["bBD"])
            P.op("act", lambda e: e.activation(out=eg[:], in_=gcBD[:], func=AF.Exp), reads=["gcBD"], writes=["eg"])
            P.op("act", lambda e: e.activation(out=glast[:], in_=gtBD[:], func=AF.Exp), reads=["gtBD"], writes=["glast"])
            P.op("dve", lambda e: e.tensor_tensor(out=ekd[:], in0=gtBD[:], in1=gcBD[:], op=ALU.subtract), reads=["gtBD", "gcBD"], writes=["ekd"])
            P.op("act", lambda e: e.activation(out=ekd[:], in_=ekd[:], func=AF.Exp), reads=["ekd"], writes=["ekd"])
            P.op("dve", lambda e: e.tensor_scalar(out=nbeta[:], in0=bBD[:], scalar1=-1.0, scalar2=None, op0=ALU.mult), reads=["bBD"], writes=["nbeta"])
            P.op("dve", lambda e: e.tensor_tensor(out=wsc[:], in0=bBD[:], in1=eg[:], op=ALU.mult), reads=["bBD", "eg"], writes=["wsc"])

        def phase_a_steps(m, c):
            d, hp = m // 2, m % 2
            qn, kn, vn = qkv[hp], qkv[2 + hp], qkv[4 + hp]
            qtk, ktk, vtk = ("qkv", hp), ("qkv", 2 + hp), ("qkv", 4 + hp)
            cs = slice(c * 64, (c + 1) * 64)
            hd0 = d * 4 + hp * 2
            X = {}

            def s1():
                Gm, Gmt = wk.get()
                Gi, Git = wk.get()
                g2 = gt[0:64, c, hd0:hd0 + 2].rearrange("p (h o) -> p h o", o=1).to_broadcast([64, 2, 64])
                P.op("dve", lambda e: e.tensor_tensor(out=Gm[0:64, :].rearrange("p (h j) -> p h j", h=2), in0=g2, in1=gm64[:, d, 1, :].rearrange("p (h j) -> p h j", h=2), op=ALU.mult), reads=["gt", "gm64"], writes=[Gmt])
                P.op("dve", lambda e: e.tensor_tensor(out=Gi[0:64, :].rearrange("p (h j) -> p h j", h=2), in0=g2, in1=gm64[:, d, 0, :].rearrange("p (h j) -> p h j", h=2), op=ALU.mult), reads=["gt", "gm64"], writes=[Git])
                pD, pDt = ps.get()
                pDT, pDTt = ps.get()
                P.op("pe", lambda e: e.matmul(pD, lhsT=gm64[:, d, 0, :], rhs=Gm[0:64, :], start=True, stop=True), reads=["gm64", Gmt], writes=[pDt], inc=False)
                P.op("pe", lambda e: e.matmul(pDT, lhsT=gm64[:, d, 1, :], rhs=Gi[0:64, :], start=True, stop=True), reads=["gm64", Git], writes=[pDTt])
                pKK, pKKt = ps.get()
                pQK, pQKt = ps.get()
                pTok, pTokt = ps.get()
                for h2 in range(2):
                    pb = h2 * 64
                    P.op("pe", lambda e, pb=pb: e.matmul(pKK[pb:pb + 64, pb:pb + 64], lhsT=kn[pb:pb + 64, cs], rhs=kn[pb:pb + 64, cs], start=True, stop=True), reads=[ktk], writes=[pKKt], inc=False)
                    P.op("pe", lambda e, pb=pb: e.matmul(pQK[pb:pb + 64, pb:pb + 64], lhsT=kn[pb:pb + 64, cs], rhs=qn[pb:pb + 64, cs], start=True, stop=True), reads=[ktk, qtk], writes=[pQKt], inc=False)
                    P.op("pe", lambda e, pb=pb: e.matmul(pTok[pb:pb + 64, 0:64], lhsT=kn[pb:pb + 64, cs], rhs=idn2[pb:pb + 64, :], start=True, stop=True), reads=[ktk, "idn2"], writes=[pTokt], inc=False)
                    P.op("pe", lambda e, pb=pb: e.matmul(pTok[pb:pb + 64, 64:128], lhsT=vn[pb:pb + 64, cs], rhs=idn2[pb:pb + 64, :], start=True, stop=True), reads=[vtk, "idn2"], writes=[pTokt], inc=(h2 == 1))
                X.update(pD=pD, pDt=pDt, pDT=pDT, pDTt=pDTt, pKK=pKK, pKKt=pKKt, pQK=pQK, pQKt=pQKt, pTok=pTok, pTokt=pTokt)

            def s2():
                x = dict(X)
                D, Dt = wk.get()
                DT, DTt = wk.get()
                X["n"] = X.get("n", 0) + 1
                if X["n"] <= k.opts.get("s2n", 99):
                    P.op("act", lambda e: e.activation(out=D, in_=x["pD"], func=AF.Exp), reads=[x["pDt"]], writes=[Dt])
                X["n"] = X.get("n", 0) + 1
                if X["n"] <= k.opts.get("s2n", 99):
                    P.op("act", lambda e: e.activation(out=DT, in_=x["pDT"], func=AF.Exp), reads=[x["pDTt"]], writes=[DTt])
                X["n"] = X.get("n", 0) + 1
                if X["n"] <= k.opts.get("s2n", 99):
                    P.op("pool", lambda e: e.tensor_tensor(out=D, in0=D, in1=gmbd[:, d, 0, :], op=ALU.mult), reads=[Dt, "gmbd"], writes=[Dt])
                X["n"] = X.get("n", 0) + 1
                if X["n"] <= k.opts.get("s2n", 99):
                    P.op("pool", lambda e: e.tensor_tensor(out=DT, in0=DT, in1=gmbd[:, d, 1, :], op=ALU.mult), reads=[DTt, "gmbd"], writes=[DTt])
                N, Nt = wk.get()
                X["n"] = X.get("n", 0) + 1
                if X["n"] <= k.opts.get("s2n", 99):
                    P.op("dve", lambda e: e.scalar_tensor_tensor(out=N, in0=x["pKK"], scalar=nbeta[:, c, m:m + 1], in1=D, op0=ALU.mult, op1=ALU.mult), reads=[x["pKKt"], "nbeta", Dt], writes=[Nt])
                X["n"] = X.get("n", 0) + 1
                if X["n"] <= k.opts.get("s2n", 99):
                    P.op("dve", lambda e: e.tensor_tensor(out=attnT_all[:, c, :], in0=x["pQK"], in1=DT, op=ALU.mult), reads=[x["pQKt"], DTt], writes=[("attnT", c)])
                rhs, rhst = wr.get()
                X["n"] = X.get("n", 0) + 1
                if X["n"] <= k.opts.get("s2n", 99):
                    P.op("dve", lambda e: e.tensor_scalar(out=rhs[:, 0:64], in0=x["pTok"][:, 0:64], scalar1=wsc[:, c, m:m + 1], scalar2=None, op0=ALU.mult), reads=[x["pTokt"], "wsc"], writes=[rhst])
                X["n"] = X.get("n", 0) + 1
                if X["n"] <= k.opts.get("s2n", 99):
                    P.op("dve", lambda e: e.tensor_scalar(out=rhs[:, 64:128], in0=x["pTok"][:, 64:128], scalar1=bBD[:, c, m:m + 1], scalar2=None, op0=ALU.mult), reads=[x["pTokt"], "bBD"], writes=[rhst])
                X["n"] = X.get("n", 0) + 1
                if X["n"] <= k.opts.get("s2n", 99):
                    P.op("dve", lambda e: e.tensor_scalar(out=kdec_all[:, c, :], in0=x["pTok"][:, 0:64], scalar1=ekd[:, c, m:m + 1], scalar2=None, op0=ALU.mult), reads=[x["pTokt"], "ekd"], writes=[("kdec", c)])
                X.update(N=N, Nt=Nt, rhs=rhs, rhst=rhst)

            def s3():
                x = dict(X)
                pNT, pNTt = ps.get()
                P.op("pe", lambda e: e.transpose(pNT, x["N"], k.idn[:]), reads=[x["Nt"], "idn"], writes=[pNTt])
                X.update(pNT=pNT, pNTt=pNTt)

            def s4():
                x = dict(X)
                PT_, PTt = wk.get()
                XT, XTt = wk.get()
                P.op("act", lambda e: e.activation(out=PT_, in_=x["pNT"], func=AF.Identity), reads=[x["pNTt"]], writes=[PTt])
                P.op("dve", lambda e: e.tensor_tensor(out=XT, in0=x["pNT"], in1=k.idn[:], op=ALU.add), reads=[x["pNTt"], "idn"], writes=[XTt])
                X.update(P=x["N"], Pt=x["Nt"], PT=PT_, PTt=PTt, XT=XT, XTt=XTt)

            def lvl_mm(kk):
                def f():
                    x = dict(X)
                    pP, pPt = ps.get()
                    P.op("pe", lambda e: e.matmul(pP, lhsT=x["PT"], rhs=x["P"], start=True, stop=True), reads=[x["PTt"], x["Pt"]], writes=[pPt], inc=(kk == 5))
                    X.update(pP=pP, pPt=pPt)
                    if kk < 5:
                        pPT, pPTt = ps.get()
                        P.op("pe", lambda e: e.matmul(pPT, lhsT=x["P"], rhs=x["PT"], start=True, stop=True), reads=[x["PTt"], x["Pt"]], writes=[pPTt])
                        X.update(pPT=pPT, pPTt=pPTt)
                return f

            def lvl_ev(kk):
                def f():
                    x = dict(X)
                    nP, nPt = wk.get()
                    P.op("act", lambda e: e.activation(out=nP, in_=x["pP"], func=AF.Identity), reads=[x["pPt"]], writes=[nPt])
                    X.update(P=nP, Pt=nPt)
                    if kk < 5:
                        nPT, nPTt = wk.get()
                        P.op("dve", lambda e: e.tensor_copy(out=nPT, in_=x["pPT"]), reads=[x["pPTt"]], writes=[nPTt])
                        X.update(PT=nPT, PTt=nPTt)
                    pX, pXt = ps.get()
                    P.op("pe", lambda e: e.matmul(pX, lhsT=nP, rhs=x["XT"], start=True, stop=True), reads=[nPt, x["XTt"]], writes=[pXt])
                    X.update(pX=pX, pXt=pXt)
                return f

            def lvl_acc(kk):
                def f():
                    x = dict(X)
                    nX, nXt = wk.get()
                    P.op("dve", lambda e: e.tensor_tensor(out=nX, in0=x["XT"], in1=x["pX"], op=ALU.add), reads=[x["XTt"], x["pXt"]], writes=[nXt])
                    X.update(XT=nX, XTt=nXt)
                return f

            def s_sol():
                x = dict(X)
                pU, pUt = ps.get()
                pW, pWt = ps.get()
                P.op("pe", lambda e: e.matmul(pU[:, 0:64], lhsT=x["XT"], rhs=x["rhs"][:, 64:128], start=True, stop=True), reads=[x["XTt"], x["rhst"]], writes=[pUt], inc=False)
                for h2 in range(2):
                    pb = h2 * 64
                    P.op("pe", lambda e, pb=pb: e.matmul(pW[pb:pb + 64, 0:64], lhsT=x["rhs"][pb:pb + 64, 0:64], rhs=x["XT"][pb:pb + 64, pb:pb + 64], start=True, stop=True), reads=[x["XTt"], x["rhst"]], writes=[pWt], inc=(h2 == 1))
                X.update(pU=pU, pUt=pUt, pW=pW, pWt=pWt)

            def s_solev():
                x = dict(X)
                P.op("act", lambda e: e.activation(out=u_all[:, c, :], in_=x["pU"][:, 0:64], func=AF.Identity), reads=[x["pUt"]], writes=[("u", c)])
                P.op("dve", lambda e: e.tensor_copy(out=wT_all[:, c, :], in_=x["pW"][:, 0:64]), reads=[x["pWt"]], writes=[("wT", c)])

            steps = [s1, s2, s3, s4]
            for kk in range(1, 6):
                steps += [lvl_mm(kk), lvl_ev(kk), lvl_acc(kk)]
            steps += [s_sol, s_solev]
            return steps[:k.opts.get("dn_steps", 99)]

        def phase_b_chunk(m, c, first_dir):
            d, hp = m // 2, m % 2
            qn = qkv[hp]
            cs = slice(c * 64, (c + 1) * 64)
            p1, p1t = ps.get()
            p2, p2t = ps.get()
            for h2 in range(2):
                pb = h2 * 64
                P.op("pe", lambda e, pb=pb: e.matmul(p1[pb:pb + 64, 0:64], lhsT=wT_all[pb:pb + 64, c, :], rhs=S[pb:pb + 64, :], start=True, stop=True), reads=[("wT", c), "S"], writes=[p1t], inc=False)
                P.op("pe", lambda e, pb=pb: e.matmul(p2[pb:pb + 64, 0:64], lhsT=qn[pb:pb + 64, cs], rhs=S[pb:pb + 64, :], start=True, stop=True), reads=[("qkv", hp), "S"], writes=[p2t], inc=(h2 == 1))
            vn_, vnt = wv.get()
            P.op("dve", lambda e: e.tensor_tensor(out=vn_, in0=u_all[:, c, :], in1=p1[:, 0:64], op=ALU.subtract), reads=[("u", c), p1t], writes=[vnt])
            p3, p3t = ps.get()
            p4, p4t = ps.get()
            P.op("pe", lambda e: e.matmul(p3[:, 0:64], lhsT=attnT_all[:, c, :], rhs=vn_, start=True, stop=True), reads=[("attnT", c), vnt], writes=[p3t], inc=False)
            for h2 in range(2):
                pb = h2 * 64
                P.op("pe", lambda e, pb=pb: e.matmul(p4[pb:pb + 64, 0:64], lhsT=kdec_all[pb:pb + 64, c, :], rhs=vn_[pb:pb + 64, :], start=True, stop=True), reads=[("kdec", c), vnt], writes=[p4t], inc=(h2 == 1))
            t_, tt_ = wv.get()
            P.op("dve", lambda e: e.tensor_scalar(out=t_, in0=p2[:, 0:64], scalar1=eg[:, c, m:m + 1], scalar2=None, op0=ALU.mult), reads=[p2t, "eg"], writes=[tt_])
            if first_dir:
                P.op("dve", lambda e: e.tensor_tensor(out=O_tok[:, c, :], in0=t_, in1=p3[:, 0:64], op=ALU.add), reads=[tt_, p3t], writes=[("O", c)])
            else:
                P.op("dve", lambda e: e.tensor_tensor(out=t_, in0=t_, in1=p3[:, 0:64], op=ALU.add), reads=[tt_, p3t], writes=[tt_])
                P.op("pool", lambda e: e.tensor_tensor(out=O_tok[:, c, :], in0=O_tok[:, c, :], in1=t_, op=ALU.add), reads=[tt_, ("O", c)], writes=[("O", c)])
            P.op("dve", lambda e: e.scalar_tensor_tensor(out=S[:], in0=S[:], scalar=glast[:, c, m:m + 1], in1=p4[:, 0:64], op0=ALU.mult, op1=ALU.add), reads=["S", "glast", p4t], writes=["S"])

        def out_phase(b, hp):
            z, yfm = xin, acc
            r0 = FM_Z + hp * 128
            P.dma(lambda e: e.dma_start(out=z[:], in_=k.PT[b, r0:r0 + 128, :]), reads=[("PT", b)], writes=["xin"])
            P.op("act", lambda e: e.activation(out=z[:], in_=z[:], func=AF.Silu), reads=["xin"], writes=["xin"])
            allO = [("O", c) for c in range(NCK)]
            P.op("dve", lambda e: e.tensor_tensor(out=u_all[:], in0=O_tok[:], in1=O_tok[:], op=ALU.mult), reads=allO, writes=[("u", c) for c in range(NCK)])
            P.op("dve", lambda e: e.tensor_reduce(out=ssq[:], in_=u_all[:], axis=AX.X, op=ALU.add), reads=[("u", c) for c in range(NCK)], writes=["ssq"])
            P.op("act", lambda e: e.activation(out=ssq[:], in_=ssq[:], func=AF.Sqrt, bias=epsb[:, 0:1], scale=1.0 / 64), reads=["ssq", "epsb"], writes=["ssq"])
            P.op("dve", lambda e: e.reciprocal(out=ssq[:], in_=ssq[:]), reads=["ssq"], writes=["ssq"])
            P.op("dve", lambda e: e.tensor_tensor(out=O_tok[:], in0=O_tok[:], in1=ssq[:].rearrange("p (c o) -> p c o", o=1).to_broadcast([128, NCK, 64]), op=ALU.mult), reads=allO + ["ssq"], writes=allO)
            P.op("dve", lambda e: e.tensor_tensor(out=O_tok[:], in0=O_tok[:], in1=gno[:].rearrange("p (o v) -> p o v", o=1).to_broadcast([128, NCK, 64]), op=ALU.mult), reads=allO + ["gno"], writes=allO)
            for c in range(NCK):
                out_chunk(c)
            for (t0, n) in [(0, CTX)] + [(CTX + i * 512, 512) for i in range(4)]:
                out_tile(b, hp, t0, n)

        def out_chunk(c):
            pp, ppt = ps.get()
            for h2 in range(2):
                pb = h2 * 64
                P.op("pe", lambda e, pb=pb: e.matmul(pp[pb:pb + 64, 0:64], lhsT=O_tok[pb:pb + 64, c, :], rhs=idn2[pb:pb + 64, :], start=True, stop=True), reads=[("O", c), "idn2"], writes=[ppt], inc=(h2 == 1))
            P.op("act", lambda e: e.activation(out=acc[:, c * 64:(c + 1) * 64], in_=pp[:, 0:64], func=AF.Identity), reads=[ppt], writes=["acc"])

        def out_tile(b, hp, t0, n):
            yi = cnt["y"] % 2
            cnt["y"] += 1
            P.op("dve", lambda e: e.tensor_tensor(out=yb[yi][:, :n], in0=acc[:, t0:t0 + n], in1=xin[:, t0:t0 + n], op=ALU.mult), reads=["acc", "xin"], writes=[("yb", yi)])
            P.dma(lambda e: e.dma_start(out=k.YT[b, 0, hp * 128:(hp + 1) * 128, t0:t0 + n], in_=yb[yi][:, :n]), reads=[("yb", yi)], writes=[("YT", b)])

        G = 3
        for b in range(NB):
            for ti in range(6):
                prep_tile(b, ti)
            prep_gates(b)
            lim = k.opts.get("dn_lim", 99)
            if lim == 0:
                continue
            for hp in range(2):
                for d in range(2):
                    m = d * 2 + hp
                    for c0 in range(0, NCK if lim >= 2 else G, G):
                        lists = [phase_a_steps(m, c) for c in range(c0, min(NCK, c0 + G))]
                        for si in range(len(lists[0])):
                            for lst in lists:
                                lst[si]()
                    if lim < 3:
                        continue
                    P.op("dve", lambda e: e.memset(S[:], 0.0), writes=["S"])
                    order = list(range(NCK)) if d == 0 else [3, 2, 1, 0] + list(range(NCK - 1, 3, -1))
                    for c in order:
                        phase_b_chunk(m, c, d == 0)
                    if "dump" in k.opts and (b, m) == k.opts["dump"][:2]:
                        for j in range(6):
                            P.dma(lambda e, j=j: e.dma_start(out=k.dump[j], in_=qkv[j][:]), reads=[("qkv", j)], writes=[("dump", j)])
                        for j, (tl, tk) in enumerate([(u_all, "u"), (wT_all, "wT"), (kdec_all, "kdec"), (O_tok, "O")]):
                            P.dma(lambda e, j=j, tl=tl: e.dma_start(out=k.dump[6 + j], in_=tl[:].rearrange("p c v -> p (c v)")), reads=[(tk, c) for c in range(NCK)], writes=[("dump", 6 + j)])
                        P.dma(lambda e: e.dma_start(out=k.dump[10:12].rearrange("a p t -> p a t"), in_=attnT_all[:].rearrange("p (a c) v -> p a (c v)", a=2)), reads=[("attnT", c) for c in range(NCK)], writes=[("dump", 10)])
                        for j, (tl, tk, w) in enumerate([(gt, "gt", 288), (bt, "bt", 288), (gcBD, "gcBD", 144), (gtBD, "gtBD", 144), (bBD, "bBD", 144)]):
                            P.dma(lambda e, j=j, tl=tl, w=w: e.dma_start(out=k.dump[12, :, j * 300:j * 300 + w], in_=tl[:].rearrange("p c v -> p (c v)")), reads=[tk], writes=[("dump", 12, j)])
                if lim >= 4:
                    out_phase(b, hp)
        P.end_stage()


def stage_s5(k, l):
    nc, P, NB = k.nc, k.P, k.NB
    HALF_PI = float(np.pi / 2)
    P.serial = k.opts.get("serial", True)
    with ExitStack() as es:
        al = lambda name, shape, dt=F32: es.enter_context(nc.sbuf_tensor("s_" + name, list(shape), dt))
        lre = al("lre", [128, 16]); lim = al("lim", [128, 16]); stp = al("stp", [128, 16]); mag = al("mag", [128, 16])
        cth = al("cth", [128, 16]); sth = al("sth", [128, 16]); t1 = al("t1", [128, 16]); t2 = al("t2", [128, 16]); t3 = al("t3", [128, 16])
        are = al("are", [128, 16]); aim = al("aim", [128, 16]); cfr = al("cfr", [128, 16]); cfi = al("cfi", [128, 16]); hpi = al("hpi", [128, 1])
        pwc = al("pwc", [128, 16, 12]); pws = al("pws", [128, 16, 12])
        bre = al("bre", [128, 8, 16]); bim = al("bim", [128, 8, 16]); cre = al("cre", [128, 8, 16]); cim = al("cim", [128, 8, 16])
        bb_all = al("bb_all", [128, 32, 16]); bd_all = al("bd_all", [128, 32, 32]); tb = [al("tb%d" % i, [128, 8, 16]) for i in range(4)]
        W_all = al("W_all", [32, 32, 128]); cw_all = al("cw_all", [128, 8, 2, 128])
        dsk = al("dsk", [128, 2]); wgl = al("wgl", [128, 2, 512], BF16)
        cs = [al("cs%d" % i, [128, T]) for i in range(2)]; sn = [al("sn%d" % i, [128, T]) for i in range(2)]
        xr = al("xr", [128, T]); xi = al("xi", [128, T]); gr = al("gr", [128, T]); gi = al("gi", [128, T])
        u32 = al("u32", [32, T]); Y = [al("Y%d" % i, [128, T]) for i in range(2)]
        mt = Slots([al("mt%d" % i, [128, 512])[:] for i in range(6)], "mt")
        gel = al("gel", [128, 2, 512], BF16); sg = [al("sg%d" % i, [128, 512]) for i in range(2)]; yb = [al("yb%d" % i, [128, 512], BF16) for i in range(2)]
        banks = [es.enter_context(nc.psum_tensor("s_ps%d" % i, [128, 512], F32)) for i in range(8)]
        psl = Slots([banks[i][:] for i in range(8)], "psb")
        ld = lambda dst, src, tok: P.dma(lambda e: e.dma_start(out=dst, in_=src), writes=[tok])
        ld(lre[:], k.s5_lam_re[l], "lre"); ld(lim[:], k.s5_lam_im[l], "lim"); ld(stp[:], k.s5_log_step[l], "stp")
        ld(bre[:], k.s5_b_re[l], "bre"); ld(bim[:], k.s5_b_im[l], "bim"); ld(cre[:], k.s5_c_re[l], "cre"); ld(cim[:], k.s5_c_im[l], "cim")
        ld(dsk[:], k.s5_d[l], "dsk")
        P.dma(lambda e: e.dma_start(out=wgl[:], in_=k.s5_glu[l].rearrange("(c p) n -> p c n", p=128)), writes=["wgl"], q="pool")
        P.op("dve", lambda e: e.memset(hpi[:], HALF_PI), writes=["hpi"])
        P.op("dve", lambda e: e.memset(bd_all[:], 0.0), writes=["bd_all"])
        P.op("dve", lambda e: e.memset(cw_all[:], 0.0), writes=["cw_all"])
        tt = lambda out, a, b_, op, rd, wr, eng="dve": P.op(eng, lambda e: e.tensor_tensor(out=out, in0=a, in1=b_, op=op), reads=rd, writes=wr)
        P.op("act", lambda e: e.activation(out=stp[:], in_=stp[:], func=AF.Exp), reads=["stp"], writes=["stp"])
        tt(t1[:], lre[:], stp[:], ALU.mult, ["lre", "stp"], ["t1"])
        P.op("act", lambda e: e.activation(out=mag[:], in_=t1[:], func=AF.Exp), reads=["t1"], writes=["mag"])
        tt(t2[:], lim[:], stp[:], ALU.mult, ["lim", "stp"], ["t2"])
        P.op("act", lambda e: e.activation(out=sth[:], in_=t2[:], func=AF.Sin, scale=1.0 / 32), reads=["t2"], writes=["sth"])
        P.op("act", lambda e: e.activation(out=cth[:], in_=t2[:], func=AF.Sin, scale=1.0 / 32, bias=hpi[:, 0:1]), reads=["t2", "hpi"], writes=["cth"])
        for it in range(5):
            tt(t1[:], cth[:], cth[:], ALU.mult, ["cth"], ["t1"])
            tt(t3[:], sth[:], sth[:], ALU.mult, ["sth"], ["t3"])
            P.op("dve", lambda e: e.scalar_tensor_tensor(out=sth[:], in0=cth[:], scalar=2.0, in1=sth[:], op0=ALU.mult, op1=ALU.mult), reads=["cth", "sth"], writes=["sth"])
            tt(cth[:], t1[:], t3[:], ALU.subtract, ["t1", "t3"], ["cth"])
        tt(are[:], mag[:], cth[:], ALU.mult, ["mag", "cth"], ["are"])
        tt(aim[:], mag[:], sth[:], ALU.mult, ["mag", "sth"], ["aim"])
        tt(t1[:], lre[:], lre[:], ALU.mult, ["lre"], ["t1"])
        tt(t3[:], lim[:], lim[:], ALU.mult, ["lim"], ["t3"])
        tt(t1[:], t1[:], t3[:], ALU.add, ["t1", "t3"], ["t1"])
        P.op("dve", lambda e: e.reciprocal(out=t1[:], in_=t1[:]), reads=["t1"], writes=["t1"])
        P.op("dve", lambda e: e.tensor_scalar(out=t2[:], in0=are[:], scalar1=-1.0, scalar2=None, op0=ALU.add), reads=["are"], writes=["t2"])
        tt(cfr[:], t2[:], lre[:], ALU.mult, ["t2", "lre"], ["cfr"])
        tt(t3[:], aim[:], lim[:], ALU.mult, ["aim", "lim"], ["t3"])
        tt(cfr[:], cfr[:], t3[:], ALU.add, ["cfr", "t3"], ["cfr"])
        tt(cfr[:], cfr[:], t1[:], ALU.mult, ["cfr", "t1"], ["cfr"])
        tt(cfi[:], aim[:], lre[:], ALU.mult, ["aim", "lre"], ["cfi"])
        tt(t3[:], t2[:], lim[:], ALU.mult, ["t2", "lim"], ["t3"])
        tt(cfi[:], cfi[:], t3[:], ALU.subtract, ["cfi", "t3"], ["cfi"])
        tt(cfi[:], cfi[:], t1[:], ALU.mult, ["cfi", "t1"], ["cfi"])
        bb4 = bb_all[:].rearrange("p (d c r) h -> p d c r h", d=2, r=2)
        for d in range(2):
            bc = lambda t_: t_[:, d * 8:(d + 1) * 8].rearrange("p (c o) -> p c o", o=1).to_broadcast([128, 8, 16])
            tt(tb[0][:], bre[:], bc(cfr), ALU.mult, ["bre", "cfr"], [("tb", 0)])
            tt(tb[1][:], bim[:], bc(cfi), ALU.mult, ["bim", "cfi"], [("tb", 1)])
            tt(bb4[:, d, :, 0, :], tb[0][:], tb[1][:], ALU.subtract, [("tb", 0), ("tb", 1)], ["bb_all"])
            tt(tb[2][:], bim[:], bc(cfr), ALU.mult, ["bim", "cfr"], [("tb", 2)])
            tt(tb[3][:], bre[:], bc(cfi), ALU.mult, ["bre", "cfi"], [("tb", 3)])
            tt(bb4[:, d, :, 1, :], tb[2][:], tb[3][:], ALU.add, [("tb", 2), ("tb", 3)], ["bb_all"])
        for g2 in range(2):
            rr_ = slice(g2 * 64, (g2 + 1) * 64)
            P.op("dve", lambda e, rr_=rr_, g2=g2: e.tensor_copy(out=bd_all[rr_, :, g2 * 16:(g2 + 1) * 16], in_=bb_all[rr_, :, :]), reads=["bb_all", "bd_all"], writes=["bd_all"])

        def w_build(j):
            pw, pwt = psl.get()
            P.op("pe", lambda e: e.transpose(pw[0:32, 0:128], bd_all[:, j, :], k.idn[:]), reads=["bd_all", "idn"], writes=[pwt])
            P.op("act", lambda e: e.activation(out=W_all[:, j, :], in_=pw[0:32, 0:128], func=AF.Identity), reads=[pwt], writes=["W_all"])
        for j in range(32):
            w_build(j)

        def cw_build(ct, g2):
            q = ct % 4
            rr_ = slice(g2 * 64, (g2 + 1) * 64)
            c0 = q * 32 + g2 * 16
            P.op("dve", lambda e: e.tensor_copy(out=cw_all[rr_, ct, 0, c0:c0 + 16], in_=cre[rr_, ct, :]), reads=["cre", "cw_all"], writes=["cw_all"])
            P.op("dve", lambda e: e.tensor_scalar(out=cw_all[rr_, ct, 1, c0:c0 + 16], in0=cim[rr_, ct, :], scalar1=-1.0, scalar2=None, op0=ALU.mult), reads=["cim", "cw_all"], writes=["cw_all"])
        for ct in range(8):
            for g2 in range(2):
                cw_build(ct, g2)
        P.op("dve", lambda e: e.tensor_copy(out=pwc[:, :, 0], in_=cth[:]), reads=["cth"], writes=["pwc"])
        P.op("dve", lambda e: e.tensor_copy(out=pws[:, :, 0], in_=sth[:]), reads=["sth"], writes=["pws"])

        def pw_level(kk):
            c_, s_ = pwc[:, :, kk - 1], pws[:, :, kk - 1]
            tt(t1[:], c_, c_, ALU.mult, ["pwc"], ["t1"])
            tt(t3[:], s_, s_, ALU.mult, ["pws"], ["t3"])
            tt(pwc[:, :, kk], t1[:], t3[:], ALU.subtract, ["t1", "t3", "pwc"], ["pwc"])
            P.op("dve", lambda e: e.scalar_tensor_tensor(out=pws[:, :, kk], in0=c_, scalar=2.0, in1=s_, op0=ALU.mult, op1=ALU.mult), reads=["pwc", "pws"], writes=["pws"])
        for kk in range(1, 12):
            pw_level(kk)

        def table_gen(j):
            i = j % 2
            C, S_ = cs[i], sn[i]
            ctk, stk = ("cs", i), ("sn", i)
            P.op("dve", lambda e: e.memset(C[:, 0:1], 1.0), writes=[ctk])
            P.op("dve", lambda e: e.memset(S_[:, 0:1], 0.0), writes=[stk])
            for kk in range(12):
                ln = 1 << kk
                nn = min(ln, T - ln)
                if nn <= 0:
                    break
                pc, ps_ = pwc[:, j, kk:kk + 1], pws[:, j, kk:kk + 1]
                lvl(C, S_, ctk, stk, ln, nn, pc, ps_)
            P.dma(lambda e: e.dma_start(out=k.S5TAB[j, 0], in_=C[:]), reads=[ctk], writes=[("TAB", j)])
            P.dma(lambda e: e.dma_start(out=k.S5TAB[j, 1], in_=S_[:]), reads=[stk], writes=[("TAB", j)])

        def lvl(C, S_, ctk, stk, ln, nn, pc, ps_):
            m1, m1t = mt.get()
            m2, m2t = mt.get()
            w_ = min(nn, 512)
            for o in range(0, nn, 512):
                w = min(512, nn - o)
                sub(C, S_, ctk, stk, ln, o, w, pc, ps_)

        def sub(C, S_, ctk, stk, ln, o, w, pc, ps_):
            m1, m1t = mt.get()
            m2, m2t = mt.get()
            P.op("dve", lambda e: e.tensor_scalar(out=m1[:, :w], in0=S_[:, o:o + w], scalar1=ps_, scalar2=None, op0=ALU.mult), reads=[stk, "pws"], writes=[m1t])
            P.op("dve", lambda e: e.tensor_scalar(out=m2[:, :w], in0=S_[:, o:o + w], scalar1=pc, scalar2=None, op0=ALU.mult), reads=[stk, "pwc"], writes=[m2t])
            P.op("dve", lambda e: e.scalar_tensor_tensor(out=C[:, ln + o:ln + o + w], in0=C[:, o:o + w], scalar=pc, in1=m1[:, :w], op0=ALU.mult, op1=ALU.subtract), reads=[ctk, m1t, "pwc"], writes=[ctk])
            P.op("dve", lambda e: e.scalar_tensor_tensor(out=S_[:, ln + o:ln + o + w], in0=C[:, o:o + w], scalar=ps_, in1=m2[:, :w], op0=ALU.mult, op1=ALU.add), reads=[ctk, m2t, "pws"], writes=[stk])
        for j in range(16):
            table_gen(j)

        blocks = [(0, CTX)] + [(CTX + i * 512, 512) for i in range(4)]

        def tabview(tab, d, t0, n):
            if d == 0:
                return tab[:, t0:t0 + n]
            if t0 < CTX:
                lo = CTX - 1 - (t0 + n - 1)
            else:
                lo = CTX + (T - 1 - (t0 + n - 1))
            return tab[:, lo:lo + n][:, ::-1]

        def do_block_in(ct, d, i, t0, n):
            j = d * 8 + ct
            pr, prt = psl.get()
            pi_, pit = psl.get()
            P.op("pe", lambda e: e.matmul(pr[:, :n], lhsT=W_all[:, j * 2, :], rhs=u32[:, t0:t0 + n], start=True, stop=True), reads=["W_all", "u32"], writes=[prt])
            P.op("pe", lambda e: e.matmul(pi_[:, :n], lhsT=W_all[:, j * 2 + 1, :], rhs=u32[:, t0:t0 + n], start=True, stop=True), reads=["W_all", "u32"], writes=[pit])
            cv, sv = tabview(cs[i], d, t0, n), tabview(sn[i], d, t0, n)
            ms = [mt.get() for _ in range(4)]
            tt(ms[0][0][:, :n], pr[:, :n], cv, ALU.mult, [prt, ("cs", i)], [ms[0][1]])
            tt(ms[1][0][:, :n], pi_[:, :n], sv, ALU.mult, [pit, ("sn", i)], [ms[1][1]])
            tt(xr[:, t0:t0 + n], ms[0][0][:, :n], ms[1][0][:, :n], ALU.add, [ms[0][1], ms[1][1]], ["xr"], "pool")
            tt(ms[2][0][:, :n], pi_[:, :n], cv, ALU.mult, [pit, ("cs", i)], [ms[2][1]])
            tt(ms[3][0][:, :n], pr[:, :n], sv, ALU.mult, [prt, ("sn", i)], [ms[3][1]])
            tt(xi[:, t0:t0 + n], ms[2][0][:, :n], ms[3][0][:, :n], ALU.subtract, [ms[2][1], ms[3][1]], ["xi"], "pool")

        def do_scan(ct, d):
            j = d * 8 + ct
            for (src, dst, stok, dtok) in ((xr, gr, "xr", "gr"), (xi, gi, "xi", "gi")):
                scan1(j, d, src, dst, stok, dtok)

        def scan1(j, d, src, dst, stok, dtok):
            rb = lambda n: mag[:, j:j + 1].to_broadcast([128, n])
            if d == 0:
                P.op("dve", lambda e: e.tensor_tensor_scan(out=dst[:], data0=rb(T), data1=src[:], initial=0.0, op0=ALU.mult, op1=ALU.add), reads=[stok, "mag"], writes=[dtok])
            else:
                P.op("dve", lambda e: e.tensor_tensor_scan(out=dst[:, 0:CTX][:, ::-1], data0=rb(CTX), data1=src[:, 0:CTX][:, ::-1], initial=0.0, op0=ALU.mult, op1=ALU.add), reads=[stok, "mag"], writes=[dtok])
                P.op("dve", lambda e: e.tensor_tensor_scan(out=dst[:, CTX:T][:, ::-1], data0=rb(SEQ), data1=src[:, CTX:T][:, ::-1], initial=dst[:, 0:1], op0=ALU.mult, op1=ALU.add), reads=[stok, "mag", dtok], writes=[dtok])

        def do_block_out(ct, d, i, t0, n, first):
            cv, sv = tabview(cs[i], d, t0, n), tabview(sn[i], d, t0, n)
            ms = [mt.get() for _ in range(4)]
            tt(ms[0][0][:, :n], gr[:, t0:t0 + n], cv, ALU.mult, ["gr", ("cs", i)], [ms[0][1]])
            tt(ms[1][0][:, :n], gi[:, t0:t0 + n], sv, ALU.mult, ["gi", ("sn", i)], [ms[1][1]], "pool")
            tt(xr[:, t0:t0 + n], ms[0][0][:, :n], ms[1][0][:, :n], ALU.subtract, [ms[0][1], ms[1][1]], ["xr"])
            tt(ms[2][0][:, :n], gr[:, t0:t0 + n], sv, ALU.mult, ["gr", ("sn", i)], [ms[2][1]], "pool")
            tt(ms[3][0][:, :n], gi[:, t0:t0 + n], cv, ALU.mult, ["gi", ("cs", i)], [ms[3][1]])
            tt(xi[:, t0:t0 + n], ms[2][0][:, :n], ms[3][0][:, :n], ALU.add, [ms[2][1], ms[3][1]], ["xi"], "pool")
            py, pyt = psl.get()
            P.op("pe", lambda e: e.matmul(py[:, :n], lhsT=cw_all[:, ct, 0, :], rhs=xr[:, t0:t0 + n], start=True, stop=False), reads=["cw_all", "xr"], writes=[pyt], inc=False)
            P.op("pe", lambda e: e.matmul(py[:, :n], lhsT=cw_all[:, ct, 1, :], rhs=xi[:, t0:t0 + n], start=False, stop=True), reads=["cw_all", "xi"], writes=[pyt])
            yt_ = Y[ct // 4]
            ytk = ("Y", ct // 4)
            if first:
                P.op("act", lambda e: e.activation(out=yt_[:, t0:t0 + n], in_=py[:, :n], func=AF.Identity), reads=[pyt], writes=[ytk])
            else:
                tt(yt_[:, t0:t0 + n], yt_[:, t0:t0 + n], py[:, :n], ALU.add, [pyt, ytk], [ytk])

        def do_ct_dir(b, ct, d):
            j = d * 8 + ct
            i = j % 2
            P.dma(lambda e: e.dma_start(out=cs[i][:], in_=k.S5TAB[j, 0]), reads=[("TAB", j)], writes=[("cs", i)])
            P.dma(lambda e: e.dma_start(out=sn[i][:], in_=k.S5TAB[j, 1]), reads=[("TAB", j)], writes=[("sn", i)])
            for (t0, n) in blocks:
                do_block_in(ct, d, i, t0, n)
            do_scan(ct, d)
            for (t0, n) in blocks:
                do_block_out(ct, d, i, t0, n, (ct % 4 == 0 and d == 0))

        def do_ct(b, ct):
            r0 = FM_U + ct * 32
            P.dma(lambda e: e.dma_start(out=u32[:], in_=k.PT[b, r0:r0 + 32, :]), reads=[("PT", b)], writes=["u32"])
            for d in range(2):
                do_ct_dir(b, ct, d)

        def out_tile(b, t0, n):
            for yt in range(2):
                ub, ubt = mt.get()
                r0 = FM_U + yt * 128
                P.dma(lambda e, ub=ub, r0=r0: e.dma_start(out=ub[:, :n], in_=k.PT[b, r0:r0 + 128, t0:t0 + n]), reads=[("PT", b)], writes=[ubt])
                yv, yvt = mt.get()
                P.op("dve", lambda e, ub=ub, yv=yv, yt=yt: e.scalar_tensor_tensor(out=yv[:, :n], in0=ub[:, :n], scalar=dsk[:, yt:yt + 1], in1=Y[yt][:, t0:t0 + n], op0=ALU.mult, op1=ALU.add), reads=[ubt, "dsk", ("Y", yt)], writes=[yvt])
                x2, x2t = mt.get()
                P.op("act", lambda e, yv=yv, x2=x2: e.activation(out=x2[:, :n], in_=yv[:, :n], func=AF.Square), reads=[yvt], writes=[x2t])
                P.op("dve", lambda e, x2=x2: e.tensor_scalar(out=x2[:, :n], in0=x2[:, :n], scalar1=0.044715, scalar2=1.0, op0=ALU.mult, op1=ALU.add), reads=[x2t], writes=[x2t])
                P.op("dve", lambda e, x2=x2, yv=yv: e.tensor_tensor(out=x2[:, :n], in0=x2[:, :n], in1=yv[:, :n], op=ALU.mult), reads=[x2t, yvt], writes=[x2t])
                P.op("act", lambda e, x2=x2: e.activation(out=x2[:, :n], in_=x2[:, :n], func=AF.Tanh, scale=0.7978845608028654), reads=[x2t], writes=[x2t])
                P.op("dve", lambda e, x2=x2, yv=yv, yt=yt: e.scalar_tensor_tensor(out=gel[:, yt, :n], in0=x2[:, :n], scalar=1.0, in1=yv[:, :n], op0=ALU.add, op1=ALU.mult), reads=[x2t, yvt], writes=[("gel", yt)])
            for oc in range(2):
                pa, pat = psl.get()
                pg, pgt = psl.get()
                for kc in range(2):
                    P.op("pe", lambda e, oc=oc, kc=kc, pa=pa: e.matmul(pa[:, :n], lhsT=wgl[:, kc, oc * 128:(oc + 1) * 128], rhs=gel[:, kc, :n], start=(kc == 0), stop=(kc == 1)), reads=["wgl", ("gel", kc)], writes=[pat])
                for kc in range(2):
                    P.op("pe", lambda e, oc=oc, kc=kc, pg=pg: e.matmul(pg[:, :n], lhsT=wgl[:, kc, 256 + oc * 128:256 + (oc + 1) * 128], rhs=gel[:, kc, :n], start=(kc == 0), stop=(kc == 1)), reads=["wgl", ("gel", kc)], writes=[pgt])
                P.op("act", lambda e, oc=oc, pg=pg: e.activation(out=sg[oc][:, :n], in_=pg[:, :n], func=AF.Sigmoid, scale=0.5), reads=[pgt], writes=[("sg", oc)])
                P.op("dve", lambda e, oc=oc, pa=pa: e.scalar_tensor_tensor(out=yb[oc][:, :n], in0=pa[:, :n], scalar=0.5, in1=sg[oc][:, :n], op0=ALU.mult, op1=ALU.mult), reads=[pat, ("sg", oc)], writes=[("yb", oc)])
                P.dma(lambda e, oc=oc: e.dma_start(out=k.YT[b, 1, oc * 128:(oc + 1) * 128, t0:t0 + n], in_=yb[oc][:, :n]), reads=[("yb", oc)], writes=[("YT", b)])

        for b in range(NB):
            for ct in range(8):
                do_ct(b, ct)
            for (t0, n) in blocks:
                out_tile(b, t0, n)
        P.end_stage()


def stage_mixers(k, l):
    sel = k.opts.get("mixers", ("dn", "s5", "hg", "att"))
    if "dn" in sel:
        stage_deltanet(k, l)
    if "s5" in sel:
        stage_s5(k, l)
    if "hg" in sel:
        stage_hgrn2(k, l)
    if "att" in sel:
        stage_attention(k, l)


def stage_merge(k, l):
    nc, P, NB, NC = k.nc, k.P, k.NB, k.NC
    with ExitStack() as es:
        wg = es.enter_context(nc.sbuf_tensor("m_wg", [128, 8, 4 * D], BF16))
        wb = es.enter_context(nc.sbuf_tensor("m_wb", [128, 8, D], BF16))
        wo = es.enter_context(nc.sbuf_tensor("m_wo", [128, 8, D], BF16))
        xt = [es.enter_context(nc.sbuf_tensor("m_x%d" % s, [128, 8, 256], F32)) for s in range(2)]
        yt = [es.enter_context(nc.sbuf_tensor("m_y%d" % s, [128, 8, 256], BF16)) for s in range(2)]
        ht = es.enter_context(nc.sbuf_tensor("m_h", [128, 8, 256], BF16))
        acc = es.enter_context(nc.sbuf_tensor("m_acc", [128, 8, 256], F32))
        accb = es.enter_context(nc.sbuf_tensor("m_accb", [128, 8, 256], BF16))
        sg = [es.enter_context(nc.sbuf_tensor("m_sg%d" % s, [128, 256], F32)) for s in range(2)]
        tt = [es.enter_context(nc.sbuf_tensor("m_tt%d" % s, [128, 256], F32)) for s in range(2)]
        psg = [es.enter_context(nc.psum_tensor("m_psg%d" % s, [128, 512], F32)) for s in range(2)]
        psy = [es.enter_context(nc.psum_tensor("m_psy%d" % s, [128, 512], F32)) for s in range(2)]
        pso = [es.enter_context(nc.psum_tensor("m_pso%d" % s, [128, 512], F32)) for s in range(2)]
        ntiles = alloc_norm_tiles(k, es, "m_", 256)
        A, Sh, G = emit_mod_scalars(k, es, "m_", l, 1, 3, 4, 5, 1.0)
        load_w_bf16(k, wg, k.w_gate[l], 4 * D, "wg")
        load_w_bf16(k, wb, k.w_branch[l].rearrange("i r n -> (i r) n"), D, "wb")
        load_w_bf16(k, wo, k.w_out[l], D, "wo")
        it = 0
        qi = 0
        for (b, t0, n, cond) in token_tiles(NB, 256):
            if l == 1 and cond == NB:
                continue
            s = it % 2
            it += 1
            xsrc = k.XT[b].rearrange("(c p) t -> p c t", p=128)[:, :, t0:t0 + n]
            P.dma(lambda e, s=s, xsrc=xsrc, n=n: e.dma_start(out=xt[s][:, :, :n], in_=xsrc), reads=[("XT", b)], writes=[("x", s)])
            ysrc = k.YT[b].rearrange("i (c p) t -> p (i c) t", p=128)[:, :, t0:t0 + n]
            P.dma(lambda e, s=s, ysrc=ysrc, n=n: e.dma_start(out=yt[s][:, :, :n], in_=ysrc), reads=[("YT", b)], writes=[("y", s)])
            emit_norm_mod(k, ntiles, xt[s], ht, n, A, Sh, cond, "modsc", s)
            for m in range(8):
                for i in range(4):
                    q = qi % 2
                    qi += 1
                    for kc in range(8):
                        P.op("pe", lambda e, i=i, m=m, q=q, kc=kc, n=n: e.matmul(psg[q][:, :n], lhsT=wg[:, kc, i * D + m * 128:i * D + (m + 1) * 128], rhs=ht[:, kc, :n], start=(kc == 0), stop=(kc == 7)),
                             reads=[("wg", kc), "h"], writes=[("psg", q)], inc=(kc == 7))
                    for kk in range(2):
                        P.op("pe", lambda e, i=i, m=m, q=q, kk=kk, n=n, s=s: e.matmul(psy[q][:, :n], lhsT=wb[:, i * 2 + kk, m * 128:(m + 1) * 128], rhs=yt[s][:, i * 2 + kk, :n], start=(kk == 0), stop=(kk == 1)),
                             reads=[("wb", i * 2 + kk), ("y", s)], writes=[("psy", q)], inc=(kk == 1))
                    P.op("act", lambda e, q=q, n=n: e.activation(out=sg[q][:, :n], in_=psg[q][:, :n], func=AF.Sigmoid), reads=[("psg", q)], writes=[("sg", q)])
                    if i == 0:
                        P.op("dve", lambda e, q=q, n=n, m=m: e.tensor_tensor(out=acc[:, m, :n], in0=sg[q][:, :n], in1=psy[q][:, :n], op=ALU.mult), reads=[("sg", q), ("psy", q)], writes=[("acc", m)])
                    else:
                        P.op("dve", lambda e, q=q, n=n: e.tensor_tensor(out=tt[q][:, :n], in0=sg[q][:, :n], in1=psy[q][:, :n], op=ALU.mult), reads=[("sg", q), ("psy", q)], writes=[("tt", q)])
                        if i < 3:
                            P.op("pool", lambda e, q=q, n=n, m=m: e.tensor_tensor(out=acc[:, m, :n], in0=acc[:, m, :n], in1=tt[q][:, :n], op=ALU.add), reads=[("tt", q), ("acc", m)], writes=[("acc", m)])
                        else:
                            P.op("pool", lambda e, q=q, n=n, m=m: e.tensor_tensor(out=accb[:, m, :n], in0=acc[:, m, :n], in1=tt[q][:, :n], op=ALU.add), reads=[("tt", q), ("acc", m)], writes=[("accb", m)])
            for m in range(8):
                q = m % 2
                for kc in range(8):
                    P.op("pe", lambda e, m=m, q=q, kc=kc, n=n: e.matmul(pso[q][:, :n], lhsT=wo[:, kc, m * 128:(m + 1) * 128], rhs=accb[:, kc, :n], start=(kc == 0), stop=(kc == 7)),
                         reads=[("wo", kc), ("accb", kc)], writes=[("pso", q)], inc=(kc == 7))
                P.op("dve", lambda e, m=m, q=q, s=s, n=n, cond=cond: e.scalar_tensor_tensor(out=xt[s][:, m, :n], in0=pso[q][:, :n], scalar=G[:, m, cond:cond + 1], in1=xt[s][:, m, :n], op0=ALU.mult, op1=ALU.add),
                     reads=[("pso", q), ("x", s), "modsg"], writes=[("x", s)])
            P.dma(lambda e, s=s, xsrc=xsrc, n=n: e.dma_start(out=xsrc, in_=xt[s][:, :, :n]), reads=[("x", s)], writes=[("XT", b)])
        P.end_stage()


def stage_final(k):
    nc, P, NB = k.nc, k.P, k.NB
    with ExitStack() as es:
        xt = [es.enter_context(nc.sbuf_tensor("o_x%d" % s, [128, 8, 512], F32)) for s in range(2)]
        yt = es.enter_context(nc.sbuf_tensor("o_y", [128, 8, 512], F32))
        ot = [es.enter_context(nc.sbuf_tensor("o_o%d" % s, [128, D], F32)) for s in range(2)]
        sq = es.enter_context(nc.sbuf_tensor("o_sq", [128, 8, 512], BF16))
        rs = es.enter_context(nc.sbuf_tensor("o_rs", [128, 512], F32))
        epsb = es.enter_context(nc.sbuf_tensor("o_eps", [128, 1], F32))
        psms = es.enter_context(nc.psum_tensor("o_psms", [128, 512], F32))
        ps = [es.enter_context(nc.psum_tensor("o_ps%d" % s, [128, 4, 128], F32)) for s in range(4)]
        P.op("dve", lambda e: e.memset(epsb[:], EPS), writes=["epsb"])
        it = 0
        oi = 0
        pi = 0
        for (b, t0, n, cond) in token_tiles(NB):
            if cond == NB:
                continue
            s = it % 2
            it += 1
            xsrc = k.XT[b].rearrange("(c p) t -> p c t", p=128)[:, :, t0:t0 + n]
            P.dma(lambda e, s=s, xsrc=xsrc: e.dma_start(out=xt[s][:], in_=xsrc), reads=[("XT", b)], writes=[("x", s)])
            P.op("act", lambda e, s=s: e.activation(out=sq[:], in_=xt[s][:], func=AF.Square), reads=[("x", s)], writes=["sq"])
            for c in range(8):
                P.op("pe", lambda e, c=c: e.matmul(psms[:], lhsT=k.onesb[:], rhs=sq[:, c, :], start=(c == 0), stop=(c == 7)), reads=["sq", "onesb"], writes=["psms"], inc=(c == 7))
            P.op("act", lambda e: e.activation(out=rs[:], in_=psms[:], func=AF.Sqrt, bias=epsb[:, 0:1], scale=1.0), reads=["psms", "epsb"], writes=["rs0", "rs"])
            P.op("dve", lambda e: e.reciprocal(out=rs[:], in_=rs[:]), reads=["rs0"], writes=["rs"])
            for c in range(8):
                P.op("dve", lambda e, c=c, s=s: e.scalar_tensor_tensor(out=yt[:, c, :], in0=xt[s][:, c, :], scalar=k.fg[:, c:c + 1], in1=rs[:], op0=ALU.mult, op1=ALU.mult),
                     reads=[("x", s), "rs", "fg"], writes=[("yt", c)])
            for tb in range(4):
                o = oi % 2
                oi += 1
                for h in range(2):
                    p = pi % 4
                    pi += 1
                    for c in range(4):
                        ch = h * 4 + c
                        P.op("pe", lambda e, p=p, c=c, ch=ch, tb=tb: e.transpose(ps[p][:, c, :], yt[:, ch, tb * 128:(tb + 1) * 128], k.idn[:]),
                             reads=[("yt", ch), "idn"], writes=[("ps", p)], inc=(c == 3))
                    if h == 0:
                        P.op("act", lambda e, p=p, o=o, h=h: e.activation(out=ot[o][:, h * 512:(h + 1) * 512], in_=ps[p][:].rearrange("p a b -> p (a b)"), func=AF.Identity), reads=[("ps", p)], writes=[("ot", o, h)])
                    else:
                        P.op("dve", lambda e, p=p, o=o, h=h: e.tensor_copy(out=ot[o][:, h * 512:(h + 1) * 512], in_=ps[p][:].rearrange("p a b -> p (a b)")), reads=[("ps", p)], writes=[("ot", o, h)])
                r0 = t0 - CTX + tb * 128
                P.dma(lambda e, o=o, b=b, r0=r0: e.dma_start(out=k.out[b, r0:r0 + 128, :], in_=ot[o][:]), reads=[("ot", o, 0), ("ot", o, 1)], writes=[("out", b)])
        P.end_stage()


def make_inputs(inputs, core, NB):
    b0 = core * NB
    f = lambda a: np.ascontiguousarray(a, dtype=np.float32)
    w_in = inputs["w_in"]
    cT = np.concatenate([inputs["c"][b0:b0 + NB], inputs["c_ctx"][None]], 0).reshape(NB + 1, 8, 128).transpose(2, 1, 0)
    return {
        "x": f(inputs["x"][b0:b0 + NB]),
        "ctx": f(inputs["ctx"][b0:b0 + NB]),
        "cT": f(cT),
        "ada_w": f(inputs["ada_w"]),
        "ada_b": f(inputs["ada_b"].reshape(2, 72, 128).transpose(0, 2, 1)),
        "norm_g": f(inputs["norm_g"].reshape(2, 3, 8, 128).transpose(0, 1, 3, 2)),
        "final_g": f(inputs["final_g"].reshape(8, 128).T),
        "ffn_w1": f(inputs["ffn_w1"]), "ffn_w3": f(inputs["ffn_w3"]), "ffn_w2": f(inputs["ffn_w2"]),
        "w_fm": f(np.concatenate([w_in[:, :, a:a + w] for a, w in FM_GROUPS], axis=2)),
        "w_tok": f(np.concatenate([w_in[:, :, a:a + w] for a, w in TOK_GROUPS], axis=2)),
        "w_gate": f(w_in[:, :, GATE0:]),
        "w_branch": f(inputs["w_branch"]),
        "w_out": f(inputs["w_out"]),
        "ident": np.eye(128, dtype=np.float32),
        "rope_cos": ROPE[0], "rope_sin": ROPE[1], "rope_rm": ROPE[2],
        "hg_masks": HG_MASKS, "bd64": BD64,
        "s5_lam_re": f(inputs["s5_lam_re"].reshape(2, 2, 8, 2, 64).transpose(0, 3, 4, 1, 2).reshape(2, 128, 16)),
        "s5_lam_im": f(inputs["s5_lam_im"].reshape(2, 2, 8, 2, 64).transpose(0, 3, 4, 1, 2).reshape(2, 128, 16)),
        "s5_log_step": f(np.broadcast_to(inputs["s5_log_step"].reshape(2, 2, 8, 2, 1), (2, 2, 8, 2, 64)).transpose(0, 3, 4, 1, 2).reshape(2, 128, 16)),
        "s5_b_re": f(inputs["s5_b_re"].reshape(2, 8, 2, 64, 16).transpose(0, 2, 3, 1, 4).reshape(2, 128, 8, 16)),
        "s5_b_im": f(inputs["s5_b_im"].reshape(2, 8, 2, 64, 16).transpose(0, 2, 3, 1, 4).reshape(2, 128, 8, 16)),
        "s5_c_re": f(inputs["s5_c_re"].reshape(2, 8, 2, 16, 64).transpose(0, 2, 4, 1, 3).reshape(2, 128, 8, 16)),
        "s5_c_im": f(inputs["s5_c_im"].reshape(2, 8, 2, 16, 64).transpose(0, 2, 4, 1, 3).reshape(2, 128, 8, 16)),
        "s5_d": f(inputs["s5_d"].reshape(2, 2, 128).transpose(0, 2, 1)),
        "s5_glu": f(inputs["s5_glu"]),
        "dn_gm64": DN_GM64, "dn_gmbd": DN_GMBD, "dn_idn2": DN_IDN2,
        "dn_conv": f(inputs["dn_conv"].reshape(2, 5, 6, 128).transpose(0, 3, 2, 1)),
        "dn_a_log": f(np.broadcast_to(inputs["dn_a_log"].reshape(2, 1, 8), (2, 128, 8))),
        "dn_dt_bias": f(np.broadcast_to(inputs["dn_dt_bias"].reshape(2, 1, 8), (2, 128, 8))),
        "dn_norm_g": f(np.broadcast_to(inputs["dn_norm_g"].reshape(2, 1, 64), (2, 128, 64))),
        "hg_lb": f(inputs["hg_lb_logits"].reshape(2, 2, 128).transpose(2, 1, 0)),
        "hg_norm_g": f(np.tile(inputs["hg_norm_g"], (1, 2)).reshape(2, 128, 1)),
        "at_qn_g": f(inputs["at_qn_g"].reshape(2, 64, 1)), "at_kn_g": f(inputs["at_kn_g"].reshape(2, 64, 1)),
    }


def _rope_consts():
    n = np.arange(SEQ)
    r, c = n // 64, n % 64
    inv = (10000.0 ** (-np.arange(0, 32, 2, dtype=np.float32) / 32)).astype(np.float32)
    ang_r = r[:, None].astype(np.float32) * inv
    ang_c = c[:, None].astype(np.float32) * inv
    ang = np.concatenate([ang_r, ang_r, ang_c, ang_c], -1)
    rm = np.zeros((64, 64), np.float32)
    for i in range(16):
        rm[16 + i, i] = -1.0
        rm[i, 16 + i] = 1.0
        rm[48 + i, 32 + i] = -1.0
        rm[32 + i, 48 + i] = 1.0
    return np.ascontiguousarray(np.cos(ang).T.astype(np.float32)), np.ascontiguousarray(np.sin(ang).T.astype(np.float32)), rm


ROPE = _rope_consts()


def _dn_consts():
    a = np.arange(64)
    le = [(a[:, None] <= a[None, :]), (a[:, None] >= a[None, :])]
    st = [(a[None, :] < a[:, None]), (a[None, :] > a[:, None])]
    gm64 = np.zeros((64, 2, 2, 128), np.float32)
    gmbd = np.zeros((128, 2, 2, 128), np.float32)
    for d in range(2):
        gm64[:, d, 0, :] = np.tile(le[d], (1, 2))
        gm64[:, d, 1, :] = np.tile(st[d], (1, 2))
        for h in range(2):
            gmbd[h * 64:(h + 1) * 64, d, 0, h * 64:(h + 1) * 64] = st[d]
            gmbd[h * 64:(h + 1) * 64, d, 1, h * 64:(h + 1) * 64] = le[d]
    idn2 = np.tile(np.eye(64, dtype=np.float32), (2, 1))
    return gm64, gmbd, idn2


DN_GM64, DN_GMBD, DN_IDN2 = _dn_consts()
_s = np.arange(128)[:, None] % 64
_t = np.arange(64)[None, :]
HG_MASKS = np.ascontiguousarray(np.stack([(_s <= _t), (_s >= _t)], 1).astype(np.float32))
BD64 = np.kron(np.eye(2, dtype=np.float32), np.full((64, 64), 1.0 / 64, np.float32))
_CACHE = {}


def kernel(**inputs):
    NB = inputs["x"].shape[0] // N_CORES
    if "nc" not in _CACHE:
        _CACHE["nc"] = build(NB)
    nc = _CACHE["nc"]
    shared = None
    in_maps = []
    for c in range(N_CORES):
        in_maps.append(make_inputs(inputs, c, NB))
    res = run_bass_kernel_spmd(nc, in_maps, core_ids=list(range(N_CORES)))
    return np.concatenate([r["out"] for r in res.results], axis=0).astype(np.float32)
```

```python
import numpy as np
from contextlib import ExitStack
import concourse.bass as bass
import concourse.mybir as mybir
from concourse.bass_utils import run_bass_kernel_spmd

F32 = mybir.dt.float32
BF16 = mybir.dt.bfloat16
AF = mybir.ActivationFunctionType
ALU = mybir.AluOpType
AX = mybir.AxisListType

D = 1024
SEQ = 2048
CTX = 256
T = SEQ + CTX
DFF = 2816
NFF = DFF // 128
EPS = 1e-6
N_CORES = 8
ENG = ("pe", "act", "dve", "pool", "sp")

FM_GROUPS = [(0, 768), (768, 256), (1040, 256), (1296, 256), (1552, 512), (2320, 256), (2576, 256), (2832, 128)]
FM_W = sum(w for _, w in FM_GROUPS)
FM_Q, FM_K, FM_V, FM_Z, FM_U, FM_HQ, FM_HF, FM_HG, FM_AQ, FM_AK = 0, 256, 512, 768, 1024, 1280, 1536, 2048, 2304, 2560
TOK_GROUPS = [(2960, 128), (2064, 256), (1024, 8), (1032, 8)]
TOK_W = 400
TK_AV, TK_HV, TK_A, TK_B = 0, 128, 384, 392
GATE0 = 3088
SERIAL_DN = True
SERIAL_S5 = True


class Prog:
    def __init__(self, nc, n_dma_sems=56):
        self.nc = nc
        self.sem = {e: nc.semaphore("sem_" + e).__enter__() for e in ("pe", "act", "dve", "pool")}
        self.dsem = [nc.semaphore("dsem%d" % i).__enter__() for i in range(n_dma_sems)]
        self.duse = [0] * n_dma_sems
        self.dnext = 0
        self.cnt = {e: 0 for e in self.sem}
        self.lastw = {}
        self.readers = {}
        self.ops = {e: [] for e in ENG}
        self.waited = {}
        self.nops = 0
        self.serial = False
        self.last_ev = {}
        self.prev_ev = None

    def _need(self, eng, ev, waits):
        if ev is None:
            return
        key, val, src = ev
        if self.waited.get((eng, key), 0) >= val:
            return
        self.waited[(eng, key)] = val
        waits.append((key, val))

    def _semh(self, key):
        return self.sem[key] if isinstance(key, str) else self.dsem[key]

    def op(self, eng, fn, reads=(), writes=(), inc=True):
        waits = []
        for r in reads:
            ev = self.lastw.get(r)
            if ev is not None and not (ev[2] == eng and eng == "pe"):
                self._need(eng, ev, waits)
        for w in writes:
            ev = self.lastw.get(w)
            if ev is not None and ev[2] != eng:
                self._need(eng, ev, waits)
            for rv in self.readers.get(w, ()):
                if rv[2] != eng:
                    self._need(eng, rv, waits)
        if self.serial:
            waits = [w_ for w_ in waits if not isinstance(w_[0], str) or w_[0] == eng]
            for w_ in waits:
                pass
            if self.prev_ev is not None and self.prev_ev[2] != eng:
                if self.waited.get((eng, self.prev_ev[0]), 0) < self.prev_ev[1] or True:
                    self.waited[(eng, self.prev_ev[0])] = max(self.waited.get((eng, self.prev_ev[0]), 0), self.prev_ev[1])
                    waits.append((self.prev_ev[0], self.prev_ev[1]))
        if inc:
            self.cnt[eng] += 1
            me = (eng, self.cnt[eng], eng)
        else:
            me = (eng, self.cnt[eng] + 1, eng)
        self.last_ev[eng] = me
        self.prev_ev = me if inc else (self.prev_ev if not self.serial else me)
        for r in reads:
            self.readers.setdefault(r, []).append(me)
        for w in writes:
            self.lastw[w] = me
            self.readers[w] = []
        self.ops[eng].append((waits, fn, ("sem", eng) if inc else ("none", eng)))
        self.nops += 1
        return me

    def dma(self, fn, reads=(), writes=(), q="sp"):
        waits = []
        for r in reads:
            self._need(q, self.lastw.get(r), waits)
        for w in writes:
            self._need(q, self.lastw.get(w), waits)
            for rv in self.readers.get(w, ()):
                self._need(q, rv, waits)
        i = self.dnext
        self.dnext = (self.dnext + 1) % len(self.dsem)
        if self.duse[i] > 0:
            self._need(q, (i, 16 * self.duse[i], "dma"), waits)
        self.duse[i] += 1
        me = (i, 16 * self.duse[i], "dma")
        for r in reads:
            self.readers.setdefault(r, []).append(me)
        for w in writes:
            self.lastw[w] = me
            self.readers[w] = []
        self.ops[q].append((waits, fn, ("dsem", i)))
        self.nops += 1
        return me

    def end_stage(self):
        waits = []
        for tok, ev in self.lastw.items():
            self._need("sp", ev, waits)
        for tok, evs in self.readers.items():
            for ev in evs:
                self._need("sp", ev, waits)
        self.ops["sp"].append((waits, None, None))
        nc = self.nc
        engobj = {"pe": "tensor", "act": "scalar", "dve": "vector", "pool": "gpsimd", "sp": "sync"}
        with nc.Block() as block:
            for e in ENG:
                ops = self.ops[e]
                if not ops:
                    continue

                def body(eo, ops=ops):
                    for waits, fn, inc in ops:
                        for key, val in waits:
                            eo.wait_ge(self._semh(key), val)
                        if fn is None:
                            continue
                        ins = fn(eo)
                        if inc[0] == "sem":
                            ins.then_inc(self.sem[inc[1]], 1)
                        elif inc[0] == "dsem":
                            ins.then_inc(self.dsem[inc[1]], 16)

                getattr(block, engobj[e])(body)
        self.ops = {e: [] for e in ENG}
        self.waited = {}
        self.lastw = {}
        self.readers = {}
        self.last_ev = {}
        self.serial = False
        self.prev_ev = None


class K:
    pass


class NCProxy:
    def __init__(self, nc):
        object.__setattr__(self, "_nc", nc)
        object.__setattr__(self, "_n", [0])

    def __getattr__(self, name):
        return getattr(self._nc, name)

    def sbuf_tensor(self, name, shape, dtype):
        self._n[0] += 1
        return self._nc.sbuf_tensor("%s_u%d" % (name, self._n[0]), shape, dtype)

    def psum_tensor(self, name, shape, dtype):
        self._n[0] += 1
        return self._nc.psum_tensor("%s_u%d" % (name, self._n[0]), shape, dtype)


def token_tiles(NB, ts=512):
    tl = []
    for b in range(NB):
        tl.append((b, 0, CTX, NB))
        for i in range(SEQ // ts):
            tl.append((b, CTX + i * ts, ts, b))
    return tl


def build(NB, opts=None):
    opts = opts or {}
    NC = NB + 1
    nc = bass.Bass("TRN2", target_bir_lowering=False)
    k = K()
    k.nc, k.NB, k.NC, k.opts = NCProxy(nc), NB, NC, opts
    inp = lambda name, shape: nc.dram_tensor(name, list(shape), F32, kind="ExternalInput").ap()
    k.x = inp("x", [NB, SEQ, D])
    k.ctx = inp("ctx", [NB, CTX, D])
    k.cT = inp("cT", [128, 8, NC])
    k.ada_w = inp("ada_w", [2, D, 9 * D])
    k.ada_b = inp("ada_b", [2, 128, 72])
    k.norm_g = inp("norm_g", [2, 3, 128, 8])
    k.final_g = inp("final_g", [128, 8])
    k.w1 = inp("ffn_w1", [2, 2, D, DFF])
    k.w3 = inp("ffn_w3", [2, 2, D, DFF])
    k.w2 = inp("ffn_w2", [2, 2, DFF, D])
    k.w_fm = inp("w_fm", [2, D, FM_W])
    k.w_tok = inp("w_tok", [2, D, TOK_W])
    k.w_gate = inp("w_gate", [2, D, 4 * D])
    k.w_branch = inp("w_branch", [2, 4, 256, D])
    k.w_out = inp("w_out", [2, D, D])
    k.ident = inp("ident", [128, 128])
    k.rope_cos = inp("rope_cos", [64, SEQ])
    k.rope_sin = inp("rope_sin", [64, SEQ])
    k.rope_rm = inp("rope_rm", [64, 64])
    k.at_qn_g = inp("at_qn_g", [2, 64, 1])
    k.at_kn_g = inp("at_kn_g", [2, 64, 1])
    k.hg_masks = inp("hg_masks", [128, 2, 64])
    k.s5_lam_re = inp("s5_lam_re", [2, 128, 16])
    k.s5_lam_im = inp("s5_lam_im", [2, 128, 16])
    k.s5_log_step = inp("s5_log_step", [2, 128, 16])
    k.s5_b_re = inp("s5_b_re", [2, 128, 8, 16])
    k.s5_b_im = inp("s5_b_im", [2, 128, 8, 16])
    k.s5_c_re = inp("s5_c_re", [2, 128, 8, 16])
    k.s5_c_im = inp("s5_c_im", [2, 128, 8, 16])
    k.s5_d = inp("s5_d", [2, 128, 2])
    k.s5_glu = inp("s5_glu", [2, 256, 512])
    k.S5TAB = nc.dram_tensor("S5TAB", [16, 2, 128, T], F32).ap()
    k.dn_gm64 = inp("dn_gm64", [64, 2, 2, 128])
    k.dn_gmbd = inp("dn_gmbd", [128, 2, 2, 128])
    k.dn_idn2 = inp("dn_idn2", [128, 64])
    k.dn_conv = inp("dn_conv", [2, 128, 6, 5])
    k.dn_a_log = inp("dn_a_log", [2, 128, 8])
    k.dn_dt_bias = inp("dn_dt_bias", [2, 128, 8])
    k.dn_norm_g = inp("dn_norm_g", [2, 128, 64])
    k.bd64 = inp("bd64", [128, 128])
    k.hg_lb = inp("hg_lb", [128, 2, 2])
    k.hg_norm_g = inp("hg_norm_g", [2, 128, 1])
    k.out = nc.dram_tensor("out", [NB, SEQ, D], F32, kind="ExternalOutput").ap()
    k.XT = nc.dram_tensor("XT", [NB, D, T], F32).ap()
    only = opts.get("only")
    k.PT = nc.dram_tensor("PT", [NB, FM_W, T], F32, **({"kind": "ExternalInput"} if only else {})).ap()
    k.TOK = nc.dram_tensor("TOK", [NB, T, TOK_W], F32, **({"kind": "ExternalInput"} if only else {})).ap()
    k.YT = nc.dram_tensor("YT", [NB, 4, 256, T], BF16, **({"kind": "ExternalOutput"} if only else {})).ap()
    if "dump" in opts:
        k.dump = nc.dram_tensor("dump", [16, 128, T], F32, kind="ExternalOutput").ap()
    if "dbg" in opts:
        k.dbg = {nm: nc.dram_tensor("dbg_" + nm, list(shp), F32, kind="ExternalOutput").ap() for nm, shp in opts["dbg"].items()}
    P = Prog(nc)
    k.P = P
    with ExitStack() as es:
        al = lambda name, shape, dt=F32: es.enter_context(nc.sbuf_tensor(name, list(shape), dt))
        k.idn = al("idn", [128, 128])
        k.onesb = al("onesb", [128, 128], BF16)
        k.mods = al("mods", [128, 72, NC])
        k.ng = al("ng", [128, 2, 3, 8])
        k.fg = al("fg", [128, 8])
        P.dma(lambda e: e.dma_start(out=k.idn[:], in_=k.ident), writes=["idn"])
        P.dma(lambda e: e.dma_start(out=k.fg[:], in_=k.final_g), writes=["fg"])
        for l in range(2):
            for j in range(3):
                P.dma(lambda e, l=l, j=j: e.dma_start(out=k.ng[:, l, j, :], in_=k.norm_g[l, j]), writes=["ng"])
        P.op("dve", lambda e: e.memset(k.onesb[:], 1.0 / D), writes=["onesb"])
        P.end_stage()
        if only:
            for l in opts.get("layers", (0,)):
                {"att": stage_attention, "hg": stage_hgrn2, "dn": stage_deltanet, "s5": stage_s5}[only](k, l)
            return nc
        stage_transpose_in(k)
        stop = opts.get("stop", "")
        for l in range(2):
            stage_ada(k, l)
            stage_ffn(k, l, 0)
            if stop == "ffn1_%d" % l:
                break
            stage_inproj(k, l)
            if stop == "inproj_%d" % l:
                break
            stage_mixers(k, l)
            if stop == "mixers_%d" % l:
                break
            stage_merge(k, l)
            if stop == "merge_%d" % l:
                break
            stage_ffn(k, l, 1)
        if "XT" in opts.get("dbg", {}):
            P.dma(lambda e: e.dma_start(out=k.dbg["XT"], in_=k.XT), reads=[], writes=["dbgx"])
            P.end_stage()
        if "PT" in opts.get("dbg", {}):
            P.dma(lambda e: e.dma_start(out=k.dbg["PT"], in_=k.PT), reads=[], writes=["dbgp"])
            P.dma(lambda e: e.dma_start(out=k.dbg["TOK"], in_=k.TOK), reads=[], writes=["dbgt"])
            P.end_stage()
        stage_final(k)
    return nc


def stage_transpose_in(k):
    nc, P = k.nc, k.P
    with ExitStack() as es:
        xin = [es.enter_context(nc.sbuf_tensor("ti_x%d" % i, [128, D], F32)) for i in range(2)]
        xo = [es.enter_context(nc.sbuf_tensor("ti_o%d" % i, [128, 8, 128], F32)) for i in range(2)]
        ps = [es.enter_context(nc.psum_tensor("ti_ps%d" % i, [128, 4, 128], F32)) for i in range(4)]
        it = 0
        for b in range(k.NB):
            for tb in range(T // 128):
                s = it % 2
                src = k.ctx[b, tb * 128:(tb + 1) * 128, :] if tb < 2 else k.x[b, (tb - 2) * 128:(tb - 1) * 128, :]
                P.dma(lambda e, s=s, src=src: e.dma_start(out=xin[s][:], in_=src), writes=[("xin", s)])
                for h in range(2):
                    pi = (it * 2 + h) % 4
                    for c in range(4):
                        ch = h * 4 + c
                        P.op("pe", lambda e, s=s, pi=pi, c=c, ch=ch: e.transpose(ps[pi][:, c, :], xin[s][:, ch * 128:(ch + 1) * 128], k.idn[:]),
                             reads=[("xin", s), "idn"], writes=[("tps", pi)], inc=(c == 3))
                    eng = "act" if h == 0 else "dve"
                    if eng == "act":
                        P.op("act", lambda e, s=s, pi=pi, h=h: e.activation(out=xo[s][:, h * 4:(h + 1) * 4, :], in_=ps[pi][:], func=AF.Identity),
                             reads=[("tps", pi)], writes=[("xo", s, h)])
                    else:
                        P.op("dve", lambda e, s=s, pi=pi, h=h: e.tensor_copy(out=xo[s][:, h * 4:(h + 1) * 4, :], in_=ps[pi][:]),
                             reads=[("tps", pi)], writes=[("xo", s, h)])
                dst = k.XT[b].rearrange("(c p) t -> p c t", p=128)[:, :, tb * 128:(tb + 1) * 128]
                P.dma(lambda e, s=s, dst=dst: e.dma_start(out=dst, in_=xo[s][:]), reads=[("xo", s, 0), ("xo", s, 1)], writes=[("XT", b)])
                it += 1
        P.end_stage()


def stage_ada(k, l):
    nc, P, NC = k.nc, k.P, k.NC
    with ExitStack() as es:
        sc = es.enter_context(nc.sbuf_tensor("ad_sc", [128, 8, NC], F32))
        ab = es.enter_context(nc.sbuf_tensor("ad_b", [128, 72], F32))
        wt = [es.enter_context(nc.sbuf_tensor("ad_w%d" % i, [128, 8, 1024], F32)) for i in range(2)]
        ps = [es.enter_context(nc.psum_tensor("ad_ps%d" % i, [128, 8, NC], F32)) for i in range(2)]
        P.dma(lambda e: e.dma_start(out=sc[:], in_=k.cT), writes=["sc"])
        P.dma(lambda e: e.dma_start(out=ab[:], in_=k.ada_b[l]), writes=["ab"])
        P.op("act", lambda e: e.activation(out=sc[:], in_=sc[:], func=AF.Silu), reads=["sc"], writes=["sc"])
        for j in range(9):
            s = j % 2
            src = k.ada_w[l][:, j * 1024:(j + 1) * 1024].rearrange("(c p) n -> p c n", p=128)
            for hh in range(2):
                P.dma(lambda e, s=s, src=src, hh=hh: e.dma_start(out=wt[s][:, hh * 4:(hh + 1) * 4, :], in_=src[:, hh * 4:(hh + 1) * 4, :]), writes=[("adw", s, hh)])
            for m in range(8):
                for kc in range(8):
                    P.op("pe", lambda e, s=s, m=m, kc=kc: e.matmul(ps[s][:, m, :], lhsT=wt[s][:, kc, m * 128:(m + 1) * 128], rhs=sc[:, kc, :], start=(kc == 0), stop=(kc == 7)),
                         reads=[("adw", s, kc // 4), "sc"], writes=[("adps", s)], inc=(kc == 7 and m == 7))
            P.op("dve", lambda e, s=s, j=j: e.tensor_tensor(out=k.mods[:, j * 8:(j + 1) * 8, :], in0=ps[s][:], in1=ab[:, j * 8:(j + 1) * 8].rearrange("p (c o) -> p c o", o=1).to_broadcast([128, 8, NC]), op=ALU.add),
                 reads=[("adps", s), "ab"], writes=["mods"])
        P.end_stage()


def emit_norm_mod(k, es_tiles, x_t, h_t, n, A, Sh, cond, tag, sl):
    P = k.P
    sq, rs, tmp, psms = es_tiles
    P.op("act", lambda e: e.activation(out=sq[:, :, :n], in_=x_t[:, :, :n], func=AF.Square), reads=[("x", sl)], writes=["sq"])
    for c in range(8):
        P.op("pe", lambda e, c=c: e.matmul(psms[:, :n], lhsT=k.onesb[:], rhs=sq[:, c, :n], start=(c == 0), stop=(c == 7)), reads=["sq", "onesb"], writes=["psms"], inc=(c == 7))
    P.op("act", lambda e: e.activation(out=rs[:, :n], in_=psms[:, :n], func=AF.Sqrt, bias=k.epsb[:, 0:1], scale=1.0), reads=["psms", "epsb"], writes=["rs0", "rs"])
    P.op("dve", lambda e: e.reciprocal(out=rs[:, :n], in_=rs[:, :n]), reads=["rs0"], writes=["rs"])
    for c in range(8):
        P.op("dve", lambda e, c=c: e.tensor_tensor(out=tmp[c % 2][:, :n], in0=x_t[:, c, :n], in1=rs[:, :n], op=ALU.mult), reads=[("x", sl), "rs"], writes=[("tmp", c % 2)])
        P.op("pool", lambda e, c=c: e.tensor_scalar(out=h_t[:, c, :n], in0=tmp[c % 2][:, :n], scalar1=A[:, c, cond:cond + 1], scalar2=Sh[:, c, cond:cond + 1], op0=ALU.mult, op1=ALU.add),
             reads=[("tmp", c % 2), tag], writes=["h"])


def alloc_norm_tiles(k, es, pfx, ts=512):
    nc = k.nc
    sq = es.enter_context(nc.sbuf_tensor(pfx + "sq", [128, 8, ts], BF16))
    rs = es.enter_context(nc.sbuf_tensor(pfx + "rs", [128, ts], F32))
    tmp = [es.enter_context(nc.sbuf_tensor(pfx + "tmp%d" % i, [128, ts], F32)) for i in range(2)]
    psms = es.enter_context(nc.psum_tensor(pfx + "psms", [128, 512], F32))
    k.epsb = es.enter_context(nc.sbuf_tensor(pfx + "epsb", [128, 1], F32))
    k.P.op("dve", lambda e: e.memset(k.epsb[:], EPS), writes=["epsb"])
    return sq, rs, tmp, psms


def emit_mod_scalars(k, es, pfx, l, jn, i_shift, i_scale, i_gate, gate_mul):
    nc, P, NC = k.nc, k.P, k.NC
    A = es.enter_context(nc.sbuf_tensor(pfx + "A", [128, 8, NC], F32))
    G = es.enter_context(nc.sbuf_tensor(pfx + "G", [128, 8, NC], F32))
    Sh = k.mods[:, i_shift * 8:(i_shift + 1) * 8, :]
    P.op("dve", lambda e: e.tensor_scalar(out=A[:], in0=k.mods[:, i_scale * 8:(i_scale + 1) * 8, :], scalar1=1.0, scalar2=None, op0=ALU.add), reads=["mods"], writes=["A0"])
    P.op("dve", lambda e: e.tensor_tensor(out=A[:], in0=A[:], in1=k.ng[:, l, jn, :].rearrange("p (c o) -> p c o", o=1).to_broadcast([128, 8, NC]), op=ALU.mult), reads=["A0", "ng"], writes=["modsc"])
    if i_gate is not None:
        P.op("dve", lambda e: e.tensor_scalar(out=G[:], in0=k.mods[:, i_gate * 8:(i_gate + 1) * 8, :], scalar1=gate_mul, scalar2=None, op0=ALU.mult), reads=["mods"], writes=["modsg"])
    return A, Sh, G


def load_w_bf16(k, dst, src_rows, ncols, tok, nk=8):
    P = k.P
    for kc in range(nk):
        P.dma(lambda e, kc=kc: e.dma_start(out=dst[:, kc, :], in_=src_rows[kc * 128:(kc + 1) * 128, :], max_dma_last_dim=4096),
              writes=[(tok, kc)], q="pool")


def stage_ffn(k, l, i):
    nc, P, NB, NC = k.nc, k.P, k.NB, k.NC
    jn = 0 if i == 0 else 2
    mi = (0, 1, 2) if i == 0 else (6, 7, 8)
    last = (l == 1 and i == 1)
    with ExitStack() as es:
        w1 = es.enter_context(nc.sbuf_tensor("f_w1", [128, 8, DFF], BF16))
        w3 = es.enter_context(nc.sbuf_tensor("f_w3", [128, 8, DFF], BF16))
        w2 = es.enter_context(nc.sbuf_tensor("f_w2", [128, NFF, D], BF16))
        xt = [es.enter_context(nc.sbuf_tensor("f_x%d" % s, [128, 8, 256], F32)) for s in range(2)]
        ht = es.enter_context(nc.sbuf_tensor("f_h", [128, 8, 256], BF16))
        hid = es.enter_context(nc.sbuf_tensor("f_hid", [128, NFF, 256], BF16))
        sl_t = [es.enter_context(nc.sbuf_tensor("f_s%d" % s, [128, 256], F32)) for s in range(2)]
        ps1 = [es.enter_context(nc.psum_tensor("f_ps1%d" % s, [128, 512], F32)) for s in range(2)]
        ps3 = [es.enter_context(nc.psum_tensor("f_ps3%d" % s, [128, 512], F32)) for s in range(2)]
        pso = [es.enter_context(nc.psum_tensor("f_pso%d" % s, [128, 512], F32)) for s in range(2)]
        ntiles = alloc_norm_tiles(k, es, "f_", 256)
        A, Sh, G = emit_mod_scalars(k, es, "f_", l, jn, mi[0], mi[1], mi[2], 0.5)
        load_w_bf16(k, w1, k.w1[l, i], DFF, "w1")
        load_w_bf16(k, w3, k.w3[l, i], DFF, "w3")
        load_w_bf16(k, w2, k.w2[l, i], D, "w2", nk=NFF)
        it = 0
        for (b, t0, n, cond) in token_tiles(NB, 256):
            if last and cond == NB:
                continue
            s = it % 2
            it += 1
            xsrc = k.XT[b].rearrange("(c p) t -> p c t", p=128)[:, :, t0:t0 + n]
            P.dma(lambda e, s=s, xsrc=xsrc, n=n: e.dma_start(out=xt[s][:, :, :n], in_=xsrc), reads=[("XT", b)], writes=[("x", s)])
            emit_norm_mod(k, ntiles, xt[s], ht, n, A, Sh, cond, "modsc", s)
            for f in range(NFF):
                q = f % 2
                for kc in range(8):
                    P.op("pe", lambda e, f=f, q=q, kc=kc, n=n: e.matmul(ps1[q][:, :n], lhsT=w1[:, kc, f * 128:(f + 1) * 128], rhs=ht[:, kc, :n], start=(kc == 0), stop=(kc == 7)),
                         reads=[("w1", kc), "h"], writes=[("ps1", q)], inc=(kc == 7))
                for kc in range(8):
                    P.op("pe", lambda e, f=f, q=q, kc=kc, n=n: e.matmul(ps3[q][:, :n], lhsT=w3[:, kc, f * 128:(f + 1) * 128], rhs=ht[:, kc, :n], start=(kc == 0), stop=(kc == 7)),
                         reads=[("w3", kc), "h"], writes=[("ps3", q)], inc=(kc == 7))
                P.op("act", lambda e, q=q, n=n: e.activation(out=sl_t[q][:, :n], in_=ps1[q][:, :n], func=AF.Silu), reads=[("ps1", q)], writes=[("sl", q)])
                P.op("dve", lambda e, q=q, f=f, n=n: e.tensor_tensor(out=hid[:, f, :n], in0=sl_t[q][:, :n], in1=ps3[q][:, :n], op=ALU.mult), reads=[("sl", q), ("ps3", q)], writes=[("hid", f)])
            for m in range(8):
                q = m % 2
                for f in range(NFF):
                    P.op("pe", lambda e, m=m, q=q, f=f, n=n: e.matmul(pso[q][:, :n], lhsT=w2[:, f, m * 128:(m + 1) * 128], rhs=hid[:, f, :n], start=(f == 0), stop=(f == NFF - 1)),
                         reads=[("w2", f), ("hid", f)], writes=[("pso", q)], inc=(f == NFF - 1))
                P.op("dve", lambda e, m=m, q=q, s=s, n=n, cond=cond: e.scalar_tensor_tensor(out=xt[s][:, m, :n], in0=pso[q][:, :n], scalar=G[:, m, cond:cond + 1], in1=xt[s][:, m, :n], op0=ALU.mult, op1=ALU.add),
                     reads=[("pso", q), ("x", s), "modsg"], writes=[("x", s)])
            P.dma(lambda e, s=s, xsrc=xsrc, n=n: e.dma_start(out=xsrc, in_=xt[s][:, :, :n]), reads=[("x", s)], writes=[("XT", b)])
        P.end_stage()


def stage_inproj(k, l):
    nc, P, NB, NC = k.nc, k.P, k.NB, k.NC
    NCH = FM_W // 128
    with ExitStack() as es:
        wf = es.enter_context(nc.sbuf_tensor("p_wf", [128, 8, FM_W], BF16))
        wk = es.enter_context(nc.sbuf_tensor("p_wk", [128, 8, TOK_W], BF16))
        xt = [es.enter_context(nc.sbuf_tensor("p_x%d" % s, [128, 8, 512], F32)) for s in range(2)]
        ht = es.enter_context(nc.sbuf_tensor("p_h", [128, 8, 512], BF16))
        ot = [es.enter_context(nc.sbuf_tensor("p_o%d" % s, [128, 512], F32)) for s in range(4)]
        ps = [es.enter_context(nc.psum_tensor("p_ps%d" % s, [128, 512], F32)) for s in range(4)]
        ntiles = alloc_norm_tiles(k, es, "p_")
        A, Sh, G = emit_mod_scalars(k, es, "p_", l, 1, 3, 4, None, 1.0)
        load_w_bf16(k, wf, k.w_fm[l], FM_W, "wf")
        load_w_bf16(k, wk, k.w_tok[l], TOK_W, "wk")
        it = 0
        oi = 0
        for (b, t0, n, cond) in token_tiles(NB):
            s = it % 2
            it += 1
            xsrc = k.XT[b].rearrange("(c p) t -> p c t", p=128)[:, :, t0:t0 + n]
            P.dma(lambda e, s=s, xsrc=xsrc, n=n: e.dma_start(out=xt[s][:, :, :n], in_=xsrc), reads=[("XT", b)], writes=[("x", s)])
            emit_norm_mod(k, ntiles, xt[s], ht, n, A, Sh, cond, "modsc", s)
            for ch in range(NCH):
                q = oi % 4
                oi += 1
                for kc in range(8):
                    P.op("pe", lambda e, ch=ch, q=q, kc=kc, n=n: e.matmul(ps[q][:, :n], lhsT=wf[:, kc, ch * 128:(ch + 1) * 128], rhs=ht[:, kc, :n], start=(kc == 0), stop=(kc == 7)),
                         reads=[("wf", kc), "h"], writes=[("ps", q)], inc=(kc == 7))
                if oi % 2 == 0:
                    P.op("act", lambda e, q=q, n=n: e.activation(out=ot[q][:, :n], in_=ps[q][:, :n], func=AF.Identity), reads=[("ps", q)], writes=[("ot", q)])
                else:
                    P.op("dve", lambda e, q=q, n=n: e.tensor_copy(out=ot[q][:, :n], in_=ps[q][:, :n]), reads=[("ps", q)], writes=[("ot", q)])
                P.dma(lambda e, q=q, ch=ch, b=b, t0=t0, n=n: e.dma_start(out=k.PT[b, ch * 128:(ch + 1) * 128, t0:t0 + n], in_=ot[q][:, :n]), reads=[("ot", q)], writes=[("PT", b)])
            for tb in range(n // 128):
                q = oi % 4
                oi += 1
                for kc in range(8):
                    P.op("pe", lambda e, tb=tb, q=q, kc=kc: e.matmul(ps[q][:, :TOK_W], lhsT=ht[:, kc, tb * 128:(tb + 1) * 128], rhs=wk[:, kc, :], start=(kc == 0), stop=(kc == 7)),
                         reads=[("wk", kc), "h"], writes=[("ps", q)], inc=(kc == 7))
                P.op("dve", lambda e, q=q: e.tensor_copy(out=ot[q][:, :TOK_W], in_=ps[q][:, :TOK_W]), reads=[("ps", q)], writes=[("ot", q)])
                P.dma(lambda e, q=q, b=b, tb=tb, t0=t0: e.dma_start(out=k.TOK[b, t0 + tb * 128:t0 + (tb + 1) * 128, :], in_=ot[q][:, :TOK_W]), reads=[("ot", q)], writes=[("TOK", b)])
        P.end_stage()


def stage_attention(k, l):
    nc, P, NB = k.nc, k.P, k.NB
    NKB = T // 128
    with ExitStack() as es:
        al = lambda name, shape, dt=F32: es.enter_context(nc.sbuf_tensor("a_" + name, list(shape), dt))
        cos = al("cos", [64, SEQ]); sin = al("sin", [64, SEQ]); rm = al("rm", [64, 64]); o64 = al("o64", [64, 64])
        gq = al("gq", [64, 1]); gk = al("gk", [64, 1]); epsb = al("eps", [64, 1])
        onesb = al("onesb", [128, 64], BF16)
        raw = [al("raw%d" % i, [64, T]) for i in range(2)]
        kT = [al("kT%d" % i, [64, T], BF16) for i in range(2)]
        qT = [al("qT%d" % i, [64, T], BF16) for i in range(2)]
        vraw = al("vraw", [128, NKB, 128])
        vb = al("vb", [128, NKB, 128], BF16)
        sq = al("sq", [64, 512]); rs = al("rs", [64, 512]); kn = al("kn", [64, 512]); t1 = al("t1", [64, 512]); t2 = al("t2", [64, 512])
        pT = [al("pT%d" % i, [128, 512], BF16) for i in range(3)]
        rden = al("rden", [64, 512])
        ot = [al("ot%d" % i, [64, 512], BF16) for i in range(2)]
        psa = es.enter_context(nc.psum_tensor("a_psa", [64, 512], F32))
        psr = es.enter_context(nc.psum_tensor("a_psr", [64, 512], F32))
        pss = [es.enter_context(nc.psum_tensor("a_pss%d" % i, [128, 512], F32)) for i in range(3)]
        pso = es.enter_context(nc.psum_tensor("a_pso", [64, 512], F32))
        psd = es.enter_context(nc.psum_tensor("a_psd", [64, 512], F32))
        P.dma(lambda e: e.dma_start(out=cos[:], in_=k.rope_cos), writes=["cos"])
        P.dma(lambda e: e.dma_start(out=sin[:], in_=k.rope_sin), writes=["sin"])
        P.dma(lambda e: e.dma_start(out=rm[:], in_=k.rope_rm), writes=["rm"])
        P.dma(lambda e: e.dma_start(out=gq[:], in_=k.at_qn_g[l]), writes=["gq"])
        P.dma(lambda e: e.dma_start(out=gk[:], in_=k.at_kn_g[l]), writes=["gk"])
        P.op("dve", lambda e: e.memset(o64[:], 1.0 / 64), writes=["o64"])
        P.op("dve", lambda e: e.memset(epsb[:], EPS), writes=["epsb"])
        P.op("dve", lambda e: e.memset(onesb[:], 1.0), writes=["onesb"])
        cnt = {"raw": 0, "p": 0, "o": 0}

        def prep(b, row0, g_t, gtok, dst, dtok):
            ri = cnt["raw"] % 2
            cnt["raw"] += 1
            P.dma(lambda e: e.dma_start(out=raw[ri][:], in_=k.PT[b, row0:row0 + 64, :]), reads=[("PT", b)], writes=[("raw", ri)])
            for (t0, n) in [(0, CTX)] + [(CTX + i * 512, 512) for i in range(4)]:
                P.op("act", lambda e, t0=t0, n=n: e.activation(out=sq[:, :n], in_=raw[ri][:, t0:t0 + n], func=AF.Square), reads=[("raw", ri)], writes=["sq"])
                P.op("pe", lambda e, n=n: e.matmul(psa[:, :n], lhsT=o64[:], rhs=sq[:, :n], start=True, stop=True), reads=["sq", "o64"], writes=["psa"])
                P.op("act", lambda e, n=n: e.activation(out=rs[:, :n], in_=psa[:, :n], func=AF.Sqrt, bias=epsb[:, 0:1], scale=1.0), reads=["psa", "epsb"], writes=["rs0", "rs"])
                P.op("dve", lambda e, n=n: e.reciprocal(out=rs[:, :n], in_=rs[:, :n]), reads=["rs0"], writes=["rs"])
                if t0 == 0:
                    P.op("dve", lambda e, t0=t0, n=n: e.scalar_tensor_tensor(out=dst[:, t0:t0 + n], in0=raw[ri][:, t0:t0 + n], scalar=g_t[:, 0:1], in1=rs[:, :n], op0=ALU.mult, op1=ALU.mult),
                         reads=[("raw", ri), "rs", gtok], writes=[dtok])
                    continue
                P.op("dve", lambda e, t0=t0, n=n: e.scalar_tensor_tensor(out=kn[:, :n], in0=raw[ri][:, t0:t0 + n], scalar=g_t[:, 0:1], in1=rs[:, :n], op0=ALU.mult, op1=ALU.mult),
                     reads=[("raw", ri), "rs", gtok], writes=["kn"])
                P.op("pe", lambda e, n=n: e.matmul(psr[:, :n], lhsT=rm[:], rhs=kn[:, :n], start=True, stop=True), reads=["kn", "rm"], writes=["psr"])
                P.op("pool", lambda e, t0=t0, n=n: e.tensor_tensor(out=t1[:, :n], in0=kn[:, :n], in1=cos[:, t0 - CTX:t0 - CTX + n], op=ALU.mult), reads=["kn", "cos"], writes=["t1"])
                P.op("dve", lambda e, t0=t0, n=n: e.tensor_tensor(out=t2[:, :n], in0=psr[:, :n], in1=sin[:, t0 - CTX:t0 - CTX + n], op=ALU.mult), reads=["psr", "sin"], writes=["t2"])
                P.op("dve", lambda e, t0=t0, n=n: e.tensor_tensor(out=dst[:, t0:t0 + n], in0=t1[:, :n], in1=t2[:, :n], op=ALU.add), reads=["t1", "t2"], writes=[dtok])

        def do_tile(b, hq, kvh, qi, q0, nq, kbs):
            def s_mm(kb, pi):
                P.op("pe", lambda e: e.matmul(pss[pi][:, :nq], lhsT=kT[kvh][:, kb * 128:(kb + 1) * 128], rhs=qT[qi][:, q0:q0 + nq], start=True, stop=True),
                     reads=[("kT", kvh), ("qT", qi)], writes=[("pss", pi)])
            pis = []
            for j in range(len(kbs)):
                pis.append(cnt["p"] % 3)
                cnt["p"] += 1
            s_mm(kbs[0], pis[0])
            for j, kb in enumerate(kbs):
                if j + 1 < len(kbs):
                    s_mm(kbs[j + 1], pis[j + 1])
                pi = pis[j]
                P.op("act", lambda e, pi=pi: e.activation(out=pT[pi][:, :nq], in_=pss[pi][:, :nq], func=AF.Exp, scale=0.125), reads=[("pss", pi)], writes=[("pT", pi)])
                P.op("pe", lambda e, pi=pi, kb=kb, j=j: e.matmul(pso[:, :nq], lhsT=vb[:, kb, kvh * 64:(kvh + 1) * 64], rhs=pT[pi][:, :nq], start=(j == 0), stop=(j == len(kbs) - 1)),
                     reads=[("pT", pi), "vb"], writes=["pso"], inc=False)
                P.op("pe", lambda e, pi=pi, j=j: e.matmul(psd[:, :nq], lhsT=onesb[:], rhs=pT[pi][:, :nq], start=(j == 0), stop=(j == len(kbs) - 1)),
                     reads=[("pT", pi), "onesb"], writes=["psd"])
            oi = cnt["o"] % 2
            cnt["o"] += 1
            P.op("dve", lambda e: e.reciprocal(out=rden[:, :nq], in_=psd[:, :nq]), reads=["psd"], writes=["rden"])
            P.op("dve", lambda e: e.tensor_tensor(out=ot[oi][:, :nq], in0=pso[:, :nq], in1=rden[:, :nq], op=ALU.mult), reads=["pso", "rden"], writes=[("ot", oi)])
            P.dma(lambda e: e.dma_start(out=k.YT[b, 3, hq * 64:(hq + 1) * 64, q0:q0 + nq], in_=ot[oi][:, :nq]), reads=[("ot", oi)], writes=[("YT", b)])

        for b in range(NB):
            for kvh in range(2):
                prep(b, FM_AK + kvh * 64, gk, "gk", kT[kvh], ("kT", kvh))
            vsrc = k.TOK[b].rearrange("(blk p) c -> p blk c", p=128)[:, :, TK_AV:TK_AV + 128]
            P.dma(lambda e, vsrc=vsrc: e.dma_start(out=vraw[:], in_=vsrc), reads=[("TOK", b)], writes=["vraw"])
            P.op("pool", lambda e: e.tensor_copy(out=vb[:], in_=vraw[:]), reads=["vraw"], writes=["vb"])
            for hq in range(4):
                kvh = hq // 2
                qi = hq % 2
                prep(b, FM_AQ + hq * 64, gq, "gq", qT[qi], ("qT", qi))
                for (q0, nq, kbs) in [(0, CTX, [0, 1])] + [(CTX + i * 512, 512, list(range(NKB))) for i in range(4)]:
                    do_tile(b, hq, kvh, qi, q0, nq, kbs)
        P.end_stage()


def stage_hgrn2(k, l):
    nc, P, NB = k.nc, k.P, k.NB
    NCK = T // 64
    with ExitStack() as es:
        al = lambda name, shape, dt=F32: es.enter_context(nc.sbuf_tensor("h_" + name, list(shape), dt))
        m01 = al("m01", [128, T]); mk = al("mk", [128, 2, 64]); bd = al("bd", [128, 128])
        lg = al("lg", [128, 2, 2]); lb = al("lb", [128, 2]); oml = al("oml", [128, 2]); gn = al("gn", [128, 1]); epsb = al("eps", [128, 1])
        z = al("z", [128, T]); fgt = al("fgt", [128, T]); bb = al("bb", [128, T]); tmp = al("tmp", [128, T])
        ex = [al("ex%d" % i, [128, T]) for i in range(2)]
        q = al("q", [128, T]); kk = al("kk", [128, T]); kd = al("kd", [128, T]); O = al("O", [128, T])
        qt = al("qt", [128, T], BF16); ktI = [al("kt%d" % i, [128, T], BF16) for i in range(4)]; qd = al("qd", [128, T], BF16)
        dec = al("dec", [128, NCK, 1])
        Vb = al("Vb", [128, NCK, 256], BF16)
        kdT = [al("kdT%d" % i, [64, 128], BF16) for i in range(2)]
        scm = [al("scm%d" % i, [128, 64], BF16) for i in range(2)]
        S32 = al("S32", [128, 64]); S16 = [al("S16%d" % i, [128, 64], BF16) for i in range(2)]
        sq = al("sq", [128, 512]); rs = al("rs", [128, 512]); yb = [al("yb%d" % i, [128, 512], BF16) for i in range(2)]
        pst = [es.enter_context(nc.psum_tensor("h_pst%d" % i, [128, 512], F32)) for i in range(2)]
        pss = [es.enter_context(nc.psum_tensor("h_pss%d" % i, [128, 512], F32)) for i in range(2)]
        pso = [es.enter_context(nc.psum_tensor("h_pso%d" % i, [128, 512], F32)) for i in range(2)]
        pskv = [es.enter_context(nc.psum_tensor("h_pskv%d" % i, [128, 512], F32)) for i in range(2)]
        P.dma(lambda e: e.dma_start(out=mk[:], in_=k.hg_masks), writes=["mk"])
        P.dma(lambda e: e.dma_start(out=bd[:], in_=k.bd64), writes=["bd"])
        P.dma(lambda e: e.dma_start(out=lg[:], in_=k.hg_lb), writes=["lg"])
        P.dma(lambda e: e.dma_start(out=gn[:], in_=k.hg_norm_g[l]), writes=["gn"])
        P.op("dve", lambda e: e.memset(epsb[:], EPS), writes=["epsb"])
        P.op("dve", lambda e: e.memset(m01[:], 1.0), writes=["m01"])
        P.op("dve", lambda e: e.memset(m01[:].rearrange("p (c j) -> p c j", j=64)[:, :, 0:1], 0.0), writes=["m01"])
        for i4 in range(4):
            P.op("pool", lambda e, i4=i4: e.memset(ktI[i4][:], 0.0), writes=[("kt", i4)])
        if l == 0:
            P.op("dve", lambda e: e.memset(lb[:], 0.0), writes=["lb"])
            P.op("dve", lambda e: e.memset(oml[:], 1.0), writes=["oml"])
        else:
            P.op("dve", lambda e: e.tensor_tensor(out=lb[:], in0=lg[:, :, 1], in1=lg[:, :, 0], op=ALU.subtract), reads=["lg"], writes=["lb0"])
            P.op("act", lambda e: e.activation(out=lb[:], in_=lb[:], func=AF.Sigmoid), reads=["lb0"], writes=["lb", "lb0"])
            P.op("dve", lambda e: e.tensor_scalar(out=oml[:], in0=lb[:], scalar1=-1.0, scalar2=1.0, op0=ALU.mult, op1=ALU.add), reads=["lb"], writes=["oml"])
        cnt = {"c": 0, "y": 0}
        bb3 = bb[:].rearrange("p (c j) -> p c j", j=64)
        tmp3 = tmp[:].rearrange("p (c j) -> p c j", j=64)

        def do_chunk(b, hp, d, c, first):
            i = cnt["c"] % 2
            cnt["c"] += 1
            cs = slice(c * 64, (c + 1) * 64)
            P.op("pe", lambda e: e.transpose(pst[i][:64, :128], kd[:, cs], k.idn[:]), reads=["kd", "idn"], writes=[("pst", i)])
            P.op("act", lambda e: e.activation(out=kdT[i][:], in_=pst[i][:64, :128], func=AF.Identity), reads=[("pst", i)], writes=[("kdT", i)])
            for h2 in range(2):
                pb = h2 * 64
                for I in range(4):
                    ts_ = slice(c * 64 + 16 * I, c * 64 + 16 * I + 16)
                    P.op("pe", lambda e, pb=pb, I=I, ts_=ts_: e.matmul(pss[i][pb:pb + 64, 16 * I:16 * I + 16], lhsT=ktI[I][pb:pb + 64, cs], rhs=qt[pb:pb + 64, ts_], start=True, stop=True),
                         reads=[("kt", I), "qt"], writes=[("pss", i)], inc=(h2 == 1 and I == 3))
            for h2 in range(2):
                pb = h2 * 64
                h = hp * 2 + h2
                P.op("pe", lambda e, pb=pb, h=h: e.matmul(pskv[i][pb:pb + 64, :64], lhsT=kdT[i][:, pb:pb + 64], rhs=Vb[0:64, c, h * 64:(h + 1) * 64], start=True, stop=True),
                     reads=[("kdT", i), "Vb"], writes=[("pskv", i)], inc=(h2 == 1))
            P.op("dve", lambda e: e.tensor_tensor(out=scm[i][:], in0=pss[i][:, :64], in1=mk[:, d, :], op=ALU.mult), reads=[("pss", i), "mk"], writes=[("scm", i)])
            for h2 in range(2):
                pb = h2 * 64
                h = hp * 2 + h2
                P.op("pe", lambda e, pb=pb, h=h: e.matmul(pso[i][pb:pb + 64, :64], lhsT=Vb[pb:pb + 64, c, h * 64:(h + 1) * 64], rhs=scm[i][pb:pb + 64, :], start=True, stop=False),
                     reads=[("scm", i), "Vb"], writes=[("pso", i)], inc=False)
                P.op("pe", lambda e, pb=pb: e.matmul(pso[i][pb:pb + 64, :64], lhsT=S16[1 - i][pb:pb + 64, :], rhs=qd[pb:pb + 64, cs], start=False, stop=True),
                     reads=[("S16", 1 - i), "qd"], writes=[("pso", i)], inc=(h2 == 1))
            if d == 0:
                P.op("act", lambda e: e.activation(out=O[:, cs], in_=pso[i][:, :64], func=AF.Identity), reads=[("pso", i)], writes=["O"])
            else:
                P.op("dve", lambda e: e.tensor_tensor(out=O[:, cs], in0=O[:, cs], in1=pso[i][:, :64], op=ALU.add), reads=[("pso", i), "O"], writes=["O"])
            P.op("dve", lambda e: e.scalar_tensor_tensor(out=S32[:], in0=S32[:], scalar=dec[:, c, :], in1=pskv[i][:, :64], op0=ALU.mult, op1=ALU.add),
                 reads=["S32", "dec", ("pskv", i)], writes=["S32"])
            P.op("act", lambda e: e.activation(out=S16[i][:], in_=S32[:], func=AF.Identity), reads=["S32"], writes=[("S16", i)])

        def do_dir(b, hp, d):
            rev = (lambda ap: ap[:, ::-1]) if d == 1 else (lambda ap: ap)
            last = 63 if d == 0 else 0
            r0 = FM_HF + d * 256 + hp * 128
            P.dma(lambda e: e.dma_start(out=z[:], in_=k.PT[b, r0:r0 + 128, :]), reads=[("PT", b)], writes=["z"])
            P.op("act", lambda e: e.activation(out=fgt[:], in_=z[:], func=AF.Sigmoid), reads=["z"], writes=["fgt"])
            P.op("dve", lambda e: e.tensor_scalar(out=fgt[:], in0=fgt[:], scalar1=oml[:, hp:hp + 1], scalar2=lb[:, hp:hp + 1], op0=ALU.mult, op1=ALU.add), reads=["fgt", "oml", "lb"], writes=["fgt"])
            P.op("dve", lambda e: e.tensor_scalar(out=fgt[:], in0=fgt[:], scalar1=1e-30, scalar2=None, op0=ALU.max), reads=["fgt"], writes=["fgt"])
            P.op("act", lambda e: e.activation(out=z[:], in_=fgt[:], func=AF.Ln), reads=["fgt"], writes=["z"])
            P.op("dve", lambda e: e.tensor_scalar(out=kk[:], in0=fgt[:], scalar1=-1.0, scalar2=1.0, op0=ALU.mult, op1=ALU.add), reads=["fgt"], writes=["kk"])
            P.op("dve", lambda e: e.tensor_tensor_scan(out=rev(bb[:]), data0=m01[:], data1=rev(z[:]), initial=0.0, op0=ALU.mult, op1=ALU.add), reads=["z", "m01"], writes=["bb"])
            ref = 0 if d == 0 else 15
            bb4 = bb[:].rearrange("p (c i j) -> p c i j", i=4, j=16)
            tmp4 = tmp[:].rearrange("p (c i j) -> p c i j", i=4, j=16)
            P.op("dve", lambda e: e.tensor_tensor(out=tmp4, in0=bb4, in1=bb4[:, :, :, ref:ref + 1].to_broadcast([128, NCK, 4, 16]), op=ALU.subtract), reads=["bb"], writes=["tmp"])
            P.op("act", lambda e: e.activation(out=ex[0][:], in_=tmp[:], func=AF.Exp), reads=["tmp"], writes=[("ex", 0)])
            P.op("pool", lambda e: e.tensor_tensor(out=qt[:], in0=q[:], in1=ex[0][:], op=ALU.mult), reads=["q", ("ex", 0)], writes=["qt"])
            ex3 = [ex[i][:].rearrange("p (c j) -> p c j", j=64) for i in range(2)]
            kk3 = kk[:].rearrange("p (c j) -> p c j", j=64)
            for I in range(4):
                cs_ = slice(0, 16 * (I + 1)) if d == 0 else slice(16 * I, 64)
                w = cs_.stop - cs_.start
                e_ = ex3[I % 2][:, :, cs_]
                rp = 16 * I + ref
                kt3 = ktI[I][:].rearrange("p (c j) -> p c j", j=64)[:, :, cs_]
                P.op("dve", lambda e, cs_=cs_, w=w, rp=rp: e.scalar_tensor_tensor(out=tmp3[:, :, cs_], in0=bb3[:, :, cs_], scalar=-1.0, in1=bb3[:, :, rp:rp + 1].to_broadcast([128, NCK, w]), op0=ALU.mult, op1=ALU.add),
                     reads=["bb"], writes=["tmp"])
                P.op("dve", lambda e, cs_=cs_: e.tensor_scalar(out=tmp3[:, :, cs_], in0=tmp3[:, :, cs_], scalar1=60.0, scalar2=None, op0=ALU.min), reads=["tmp"], writes=["tmp"])
                P.op("act", lambda e, cs_=cs_, e_=e_: e.activation(out=e_, in_=tmp3[:, :, cs_], func=AF.Exp), reads=["tmp"], writes=[("ex", I % 2)])
                P.op("pool", lambda e, cs_=cs_, e_=e_, kt3=kt3: e.tensor_tensor(out=kt3, in0=kk3[:, :, cs_], in1=e_, op=ALU.mult), reads=["kk", ("ex", I % 2)], writes=[("kt", I)])
            P.op("act", lambda e: e.activation(out=ex[0][:], in_=bb[:], func=AF.Exp), reads=["bb"], writes=[("ex", 0)])
            P.op("dve", lambda e: e.tensor_tensor(out=qd[:], in0=q[:], in1=ex[0][:], op=ALU.mult), reads=["q", ("ex", 0)], writes=["qd"])
            P.op("dve", lambda e: e.tensor_tensor(out=tmp3, in0=bb3, in1=bb3[:, :, last:last + 1].to_broadcast([128, NCK, 64]), op=ALU.subtract), reads=["bb"], writes=["tmp"])
            P.op("act", lambda e: e.activation(out=ex[1][:], in_=tmp[:], func=AF.Exp, scale=-1.0), reads=["tmp"], writes=[("ex", 1)])
            P.op("pool", lambda e: e.tensor_tensor(out=kd[:], in0=kk[:], in1=ex[1][:], op=ALU.mult), reads=["kk", ("ex", 1)], writes=["kd"])
            P.op("act", lambda e: e.activation(out=dec[:], in_=bb3[:, :, last:last + 1], func=AF.Exp), reads=["bb"], writes=["dec"])
            if "dump" in k.opts and (b, hp, d) == k.opts["dump"]:
                for j, (tl, tk) in enumerate([(z, "z"), (bb, "bb"), (kd, "kd"), (kk, "kk"), (fgt, "fgt")]):
                    P.dma(lambda e, j=j, tl=tl: e.dma_start(out=k.dump[j], in_=tl[:]), reads=[tk], writes=[("dump", j)])
            P.op("dve", lambda e: e.memset(S32[:], 0.0), writes=["S32"])
            for i in range(2):
                P.op("dve", lambda e, i=i: e.memset(S16[i][:], 0.0), writes=[("S16", i)])
            order = list(range(NCK)) if d == 0 else [3, 2, 1, 0] + list(range(NCK - 1, 3, -1))
            for idx, c in enumerate(order):
                do_chunk(b, hp, d, c, idx == 0)

        def do_pair(b, hp):
            r0 = FM_HQ + hp * 128
            P.dma(lambda e: e.dma_start(out=q[:], in_=k.PT[b, r0:r0 + 128, :]), reads=[("PT", b)], writes=["q"])
            P.op("act", lambda e: e.activation(out=q[:], in_=q[:], func=AF.Silu), reads=["q"], writes=["q"])
            for d in range(2):
                do_dir(b, hp, d)
            if "dump" in k.opts and (b, hp) == k.opts["dump"][:2]:
                P.dma(lambda e: e.dma_start(out=k.dump[5], in_=O[:]), reads=["O"], writes=[("dump", 5)])
            g0 = FM_HG + hp * 128
            P.dma(lambda e: e.dma_start(out=z[:], in_=k.PT[b, g0:g0 + 128, :]), reads=[("PT", b)], writes=["z"])
            P.op("act", lambda e: e.activation(out=z[:], in_=z[:], func=AF.Sigmoid), reads=["z"], writes=["z"])
            for (t0, n) in [(0, CTX)] + [(CTX + i * 512, 512) for i in range(4)]:
                do_out(b, hp, t0, n)

        def do_out(b, hp, t0, n):
            yi = cnt["y"] % 2
            cnt["y"] += 1
            P.op("act", lambda e: e.activation(out=sq[:, :n], in_=O[:, t0:t0 + n], func=AF.Square), reads=["O"], writes=["sq"])
            P.op("pe", lambda e: e.matmul(pss[0][:, :n], lhsT=bd[:], rhs=sq[:, :n], start=True, stop=True), reads=["sq", "bd"], writes=[("pss", 0)])
            P.op("act", lambda e: e.activation(out=rs[:, :n], in_=pss[0][:, :n], func=AF.Sqrt, bias=epsb[:, 0:1], scale=1.0), reads=[("pss", 0), "epsb"], writes=["rs0", "rs"])
            P.op("dve", lambda e: e.reciprocal(out=rs[:, :n], in_=rs[:, :n]), reads=["rs0"], writes=["rs"])
            P.op("dve", lambda e: e.scalar_tensor_tensor(out=sq[:, :n], in0=O[:, t0:t0 + n], scalar=gn[:, 0:1], in1=rs[:, :n], op0=ALU.mult, op1=ALU.mult), reads=["O", "gn", "rs"], writes=["sq"])
            P.op("pool", lambda e: e.tensor_tensor(out=yb[yi][:, :n], in0=sq[:, :n], in1=z[:, t0:t0 + n], op=ALU.mult), reads=["sq", "z"], writes=[("yb", yi)])
            P.dma(lambda e: e.dma_start(out=k.YT[b, 2, hp * 128:(hp + 1) * 128, t0:t0 + n], in_=yb[yi][:, :n]), reads=[("yb", yi)], writes=[("YT", b)])

        for b in range(NB):
            vsrc = k.TOK[b].rearrange("(c s) v -> s c v", s=64)[:, :, TK_HV:TK_HV + 256]
            for h2 in range(2):
                P.dma(lambda e, h2=h2, vsrc=vsrc: e.dma_start(out=Vb[h2 * 64:(h2 + 1) * 64, :, :], in_=vsrc), reads=[("TOK", b)], writes=["Vb"], q="pool")
            for hp in range(2):
                do_pair(b, hp)
        P.end_stage()


class Slots:
    def __init__(self, aps, name, mod=0):
        self.aps, self.name, self.i, self.mod = aps, name, 0, mod

    def get(self):
        j = self.i % len(self.aps)
        self.i += 1
        return self.aps[j], (self.name, j % self.mod if self.mod else j)


def stage_deltanet(k, l):
    nc, P, NB = k.nc, k.P, k.NB
    NCK = T // 64
    P.serial = k.opts.get("serial_dn", SERIAL_DN)
    with ExitStack() as es:
        al = lambda name, shape, dt=F32: es.enter_context(nc.sbuf_tensor("d_" + name, list(shape), dt))
        gm64 = al("gm64", [64, 2, 2, 128]); gmbd = al("gmbd", [128, 2, 2, 128]); idn2 = al("idn2", [128, 64]); bd = al("bd", [128, 128])
        ones64 = al("ones64", [64, 128])
        cw = al("cw", [128, 6, 5]); alog = al("alog", [128, 8]); dtb = al("dtb", [128, 8]); gno = al("gno", [128, 64]); epsb = al("eps", [128, 1])
        qkv = [al("qkv%d" % i, [128, T]) for i in range(6)]
        xin = al("xin", [128, T]); acc = al("acc", [128, T])
        sq = al("sq", [128, 512]); rs = al("rs", [128, 512])
        ab = al("ab", [128, NCK, 16]); gt = al("gt", [128, NCK, 8]); bt = al("bt", [128, NCK, 8]); t8 = al("t8", [128, NCK, 8]); t8b = al("t8b", [128, NCK, 8])
        gcs = al("gcs", [128, NCK, 8]); gts = al("gts", [128, NCK, 8])
        gcBD = al("gcBD", [128, NCK, 4]); gtBD = al("gtBD", [128, NCK, 4]); bBD = al("bBD", [128, NCK, 4])
        eg = al("eg", [128, NCK, 4]); ekd = al("ekd", [128, NCK, 4]); glast = al("glast", [128, NCK, 4]); nbeta = al("nbeta", [128, NCK, 4]); wsc = al("wsc", [128, NCK, 4])
        u_all = al("u_all", [128, NCK, 64]); wT_all = al("wT_all", [128, NCK, 64]); kdec_all = al("kdec_all", [128, NCK, 64]); attnT_all = al("attnT_all", [128, NCK, 128])
        O_tok = al("O_tok", [128, NCK, 64]); ssq = al("ssq", [128, NCK]); S = al("S", [128, 64])
        yb = [al("yb%d" % i, [128, 512], BF16) for i in range(2)]
        wk = Slots([al("wk%d" % i, [128, 128])[:] for i in range(28)], "wk")
        wv = Slots([al("wv%d" % i, [128, 64])[:] for i in range(8)], "wv")
        wr = Slots([al("wr%d" % i, [128, 128])[:] for i in range(6)], "wr")
        banks = [es.enter_context(nc.psum_tensor("d_ps%d" % i, [128, 512], F32)) for i in range(8)]
        ps = Slots([banks[i][:, j * 128:(j + 1) * 128] for j in range(k.opts.get("psj", 4)) for i in range(7)], "ps", mod=7)
        psg = banks[7]
        for i in range(8):
            P.op("dve", lambda e, i=i: e.memset(banks[i][:], 0.0), writes=[("ps", j) for j in range(7)] + ["psg"])
        P.dma(lambda e: e.dma_start(out=gm64[:], in_=k.dn_gm64), writes=["gm64"])
        P.dma(lambda e: e.dma_start(out=gmbd[:], in_=k.dn_gmbd), writes=["gmbd"])
        P.dma(lambda e: e.dma_start(out=idn2[:], in_=k.dn_idn2), writes=["idn2"])
        P.dma(lambda e: e.dma_start(out=bd[:], in_=k.bd64), writes=["bd"])
        P.dma(lambda e: e.dma_start(out=cw[:], in_=k.dn_conv[l]), writes=["cw"])
        P.dma(lambda e: e.dma_start(out=alog[:], in_=k.dn_a_log[l]), writes=["alog"])
        P.dma(lambda e: e.dma_start(out=dtb[:], in_=k.dn_dt_bias[l]), writes=["dtb"])
        P.dma(lambda e: e.dma_start(out=gno[:], in_=k.dn_norm_g[l]), writes=["gno"])
        P.op("dve", lambda e: e.memset(epsb[:], EPS), writes=["epsb"])
        P.op("dve", lambda e: e.memset(ones64[:], 1.0), writes=["ones64"])
        P.op("act", lambda e: e.activation(out=alog[:], in_=alog[:], func=AF.Exp), reads=["alog"], writes=["alog"])
        P.op("dve", lambda e: e.tensor_scalar(out=alog[:], in0=alog[:], scalar1=-1.0, scalar2=None, op0=ALU.mult), reads=["alog"], writes=["alog"])
        cnt = {"y": 0}

        def prep_tile(b, ti):
            r0 = ti * 128
            P.dma(lambda e: e.dma_start(out=xin[:], in_=k.PT[b, r0:r0 + 128, :]), reads=[("PT", b)], writes=["xin"])
            P.op("dve", lambda e: e.tensor_scalar(out=acc[:], in0=xin[:], scalar1=cw[:, ti, 2:3], scalar2=None, op0=ALU.mult), reads=["xin", "cw"], writes=["acc"])
            for (s0, s1) in [(0, CTX), (CTX, T)]:
                for j in (0, 1, 3, 4):
                    sh = j - 2
                    o0, o1 = max(s0, s0 - sh), min(s1, s1 - sh)
                    P.op("dve", lambda e, j=j, sh=sh, o0=o0, o1=o1: e.scalar_tensor_tensor(out=acc[:, o0:o1], in0=xin[:, o0 + sh:o1 + sh], scalar=cw[:, ti, j:j + 1], in1=acc[:, o0:o1], op0=ALU.mult, op1=ALU.add),
                         reads=["xin", "cw", "acc"], writes=["acc"])
            dst = qkv[ti]
            if ti >= 4:
                P.op("act", lambda e: e.activation(out=dst[:], in_=acc[:], func=AF.Silu), reads=["acc"], writes=[("qkv", ti)])
                return
            P.op("act", lambda e: e.activation(out=acc[:], in_=acc[:], func=AF.Silu), reads=["acc"], writes=["acc"])
            qs = 0.125 if ti < 2 else 1.0
            for (t0, n) in [(0, CTX)] + [(CTX + i * 512, 512) for i in range(4)]:
                norm_tile(dst, ti, t0, n, qs)

        def norm_tile(dst, ti, t0, n, qs):
            pp, pt = ps.get()
            bank_ap = banks[0]
            P.op("act", lambda e: e.activation(out=sq[:, :n], in_=acc[:, t0:t0 + n], func=AF.Square), reads=["acc"], writes=["sq"])
            P.op("pe", lambda e: e.matmul(psg[:, :n], lhsT=bd[:], rhs=sq[:, :n], start=True, stop=True), reads=["sq", "bd"], writes=["psg"])
            P.op("act", lambda e: e.activation(out=rs[:, :n], in_=psg[:, :n], func=AF.Sqrt, bias=epsb[:, 0:1], scale=64.0), reads=["psg", "epsb"], writes=["rs0", "rs"])
            P.op("dve", lambda e: e.reciprocal(out=rs[:, :n], in_=rs[:, :n]), reads=["rs0"], writes=["rs"])
            P.op("dve", lambda e: e.scalar_tensor_tensor(out=dst[:, t0:t0 + n], in0=acc[:, t0:t0 + n], scalar=qs, in1=rs[:, :n], op0=ALU.mult, op1=ALU.mult), reads=["acc", "rs"], writes=[("qkv", ti)])

        def prep_gates(b):
            src = k.TOK[b].rearrange("(c s) v -> s c v", s=64)[:, :, TK_A:TK_A + 16]
            for h2 in range(2):
                P.dma(lambda e, h2=h2: e.dma_start(out=ab[h2 * 64:(h2 + 1) * 64, :, :], in_=src), reads=[("TOK", b)], writes=["ab"])
            a3, b3 = ab[:, :, 0:8], ab[:, :, 8:16]
            bc8 = lambda t: t[:, :].rearrange("p (o h) -> p o h", o=1).to_broadcast([128, NCK, 8])
            P.op("dve", lambda e: e.tensor_tensor(out=t8[:], in0=a3, in1=bc8(dtb), op=ALU.add), reads=["ab", "dtb"], writes=["t8"])
            P.op("act", lambda e: e.activation(out=t8b[:], in_=t8[:], func=AF.Abs), reads=["t8"], writes=["t8b"])
            P.op("act", lambda e: e.activation(out=t8b[:], in_=t8b[:], func=AF.Exp, scale=-1.0), reads=["t8b"], writes=["t8b"])
            P.op("dve", lambda e: e.tensor_scalar(out=t8b[:], in0=t8b[:], scalar1=1.0, scalar2=None, op0=ALU.add), reads=["t8b"], writes=["t8b"])
            P.op("act", lambda e: e.activation(out=t8b[:], in_=t8b[:], func=AF.Ln), reads=["t8b"], writes=["t8b"])
            P.op("dve", lambda e: e.tensor_scalar(out=t8[:], in0=t8[:], scalar1=0.0, scalar2=None, op0=ALU.max), reads=["t8"], writes=["t8"])
            P.op("dve", lambda e: e.tensor_tensor(out=t8[:], in0=t8[:], in1=t8b[:], op=ALU.add), reads=["t8", "t8b"], writes=["t8"])
            P.op("dve", lambda e: e.tensor_tensor(out=gt[:], in0=t8[:], in1=bc8(alog), op=ALU.mult), reads=["t8", "alog"], writes=["gt"])
            P.op("act", lambda e: e.activation(out=bt[:], in_=b3, func=AF.Sigmoid), reads=["ab"], writes=["bt"])
            for d in range(2):
                P.op("pe", lambda e, d=d: e.matmul(psg[:, d * 144:(d + 1) * 144], lhsT=gm64[:, d, 0, :], rhs=gt[0:64, :, d * 4:(d + 1) * 4], start=True, stop=True), reads=["gm64", "gt"], writes=["psg"])
            P.op("dve", lambda e: e.tensor_copy(out=gcs[:].rearrange("p c (d h) -> p d c h", d=2), in_=psg[:, 0:288].rearrange("p (d c h) -> p d c h", d=2, h=4)), reads=["psg"], writes=["gcs"])
            for d in range(2):
                P.op("pe", lambda e, d=d: e.matmul(psg[:, d * 144:(d + 1) * 144], lhsT=ones64[:], rhs=gt[0:64, :, d * 4:(d + 1) * 4], start=True, stop=True), reads=["ones64", "gt"], writes=["psg"])
            P.op("dve", lambda e: e.tensor_copy(out=gts[:].rearrange("p c (d h) -> p d c h", d=2), in_=psg[:, 0:288].rearrange("p (d c h) -> p d c h", d=2, h=4)), reads=["psg"], writes=["gts"])
            for m in range(4):
                d, hp = m // 2, m % 2
                for h2 in range(2):
                    col = d * 4 + hp * 2 + h2
                    rr = slice(h2 * 64, (h2 + 1) * 64)
                    P.op("dve", lambda e, m=m, col=col, rr=rr: e.tensor_copy(out=gcBD[rr, :, m:m + 1], in_=gcs[rr, :, col:col + 1]), reads=["gcs"], writes=["gcBD"])
                    P.op("dve", lambda e, m=m, col=col, rr=rr: e.tensor_copy(out=gtBD[rr, :, m:m + 1], in_=gts[rr, :, col:col + 1]), reads=["gts"], writes=["gtBD"])
                    P.op("dve", lambda e, m=m, col=col, rr=rr: e.tensor_copy(out=bBD[rr, :, m:m + 1], in_=bt[rr, :, col:col + 1]), reads=["bt"], writes=["bBD"])
            P.op("act", lambda e: e.activation(out=eg[:], in_=gcBD[:], func=AF.Exp), reads=["gcBD"], writes=["eg"])
            P.op("act", lambda e: e.activation(out=glast[:], in_=gtBD[:], func=AF.Exp), reads=["gtBD"], writes=["glast"])
            P.op("dve", lambda e: e.tensor_tensor(out=ekd[:], in0=gtBD[:], in1=gcBD[:], op=ALU.subtract), reads=["gtBD", "gcBD"], writes=["ekd"])
            P.op("act", lambda e: e.activation(out=ekd[:], in_=ekd[:], func=AF.Exp), reads=["ekd"], writes=["ekd"])
            P.op("dve", lambda e: e.tensor_scalar(out=nbeta[:], in0=bBD[:], scalar1=-1.0, scalar2=None, op0=ALU.mult), reads=["bBD"], writes=["nbeta"])
            P.op("dve", lambda e: e.tensor_tensor(out=wsc[:], in0=bBD[:], in1=eg[:], op=ALU.mult), reads=["bBD", "eg"], writes=["wsc"])

        def phase_a_steps(m, c):
            d, hp = m // 2, m % 2
            qn, kn, vn = qkv[hp], qkv[2 + hp], qkv[4 + hp]
            qtk, ktk, vtk = ("qkv", hp), ("qkv", 2 + hp), ("qkv", 4 + hp)
            cs = slice(c * 64, (c + 1) * 64)
            hd0 = d * 4 + hp * 2
            X = {}

            def s1():
                Gm, Gmt = wk.get()
                Gi, Git = wk.get()
                g2 = gt[0:64, c, hd0:hd0 + 2].rearrange("p (h o) -> p h o", o=1).to_broadcast([64, 2, 64])
                P.op("dve", lambda e: e.tensor_tensor(out=Gm[0:64, :].rearrange("p (h j) -> p h j", h=2), in0=g2, in1=gm64[:, d, 1, :].rearrange("p (h j) -> p h j", h=2), op=ALU.mult), reads=["gt", "gm64"], writes=[Gmt])
                P.op("dve", lambda e: e.tensor_tensor(out=Gi[0:64, :].rearrange("p (h j) -> p h j", h=2), in0=g2, in1=gm64[:, d, 0, :].rearrange("p (h j) -> p h j", h=2), op=ALU.mult), reads=["gt", "gm64"], writes=[Git])
                pD, pDt = ps.get()
                pDT, pDTt = ps.get()
                P.op("pe", lambda e: e.matmul(pD, lhsT=gm64[:, d, 0, :], rhs=Gm[0:64, :], start=True, stop=True), reads=["gm64", Gmt], writes=[pDt], inc=False)
                P.op("pe", lambda e: e.matmul(pDT, lhsT=gm64[:, d, 1, :], rhs=Gi[0:64, :], start=True, stop=True), reads=["gm64", Git], writes=[pDTt])
                pKK, pKKt = ps.get()
                pQK, pQKt = ps.get()
                pTok, pTokt = ps.get()
                for h2 in range(2):
                    pb = h2 * 64
                    P.op("pe", lambda e, pb=pb: e.matmul(pKK[pb:pb + 64, pb:pb + 64], lhsT=kn[pb:pb + 64, cs], rhs=kn[pb:pb + 64, cs], start=True, stop=True), reads=[ktk], writes=[pKKt], inc=False)
                    P.op("pe", lambda e, pb=pb: e.matmul(pQK[pb:pb + 64, pb:pb + 64], lhsT=kn[pb:pb + 64, cs], rhs=qn[pb:pb + 64, cs], start=True, stop=True), reads=[ktk, qtk], writes=[pQKt], inc=False)
                    P.op("pe", lambda e, pb=pb: e.matmul(pTok[pb:pb + 64, 0:64], lhsT=kn[pb:pb + 64, cs], rhs=idn2[pb:pb + 64, :], start=True, stop=True), reads=[ktk, "idn2"], writes=[pTokt], inc=False)
                    P.op("pe", lambda e, pb=pb: e.matmul(pTok[pb:pb + 64, 64:128], lhsT=vn[pb:pb + 64, cs], rhs=idn2[pb:pb + 64, :], start=True, stop=True), reads=[vtk, "idn2"], writes=[pTokt], inc=(h2 == 1))
                X.update(pD=pD, pDt=pDt, pDT=pDT, pDTt=pDTt, pKK=pKK, pKKt=pKKt, pQK=pQK, pQKt=pQKt, pTok=pTok, pTokt=pTokt)

            def s2():
                x = dict(X)
                D, Dt = wk.get()
                DT, DTt = wk.get()
                X["n"] = X.get("n", 0) + 1
                if X["n"] <= k.opts.get("s2n", 99):
                    P.op("act", lambda e: e.activation(out=D, in_=x["pD"], func=AF.Exp), reads=[x["pDt"]], writes=[Dt])
                X["n"] = X.get("n", 0) + 1
                if X["n"] <= k.opts.get("s2n", 99):
                    P.op("act", lambda e: e.activation(out=DT, in_=x["pDT"], func=AF.Exp), reads=[x["pDTt"]], writes=[DTt])
                X["n"] = X.get("n", 0) + 1
                if X["n"] <= k.opts.get("s2n", 99):
                    P.op("dve", lambda e: e.tensor_tensor(out=D, in0=D, in1=gmbd[:, d, 0, :], op=ALU.mult), reads=[Dt, "gmbd"], writes=[Dt])
                X["n"] = X.get("n", 0) + 1
                if X["n"] <= k.opts.get("s2n", 99):
                    P.op("dve", lambda e: e.tensor_tensor(out=DT, in0=DT, in1=gmbd[:, d, 1, :], op=ALU.mult), reads=[DTt, "gmbd"], writes=[DTt])
                N, Nt = wk.get()
                X["n"] = X.get("n", 0) + 1
                if X["n"] <= k.opts.get("s2n", 99):
                    P.op("dve", lambda e: e.scalar_tensor_tensor(out=N, in0=x["pKK"], scalar=nbeta[:, c, m:m + 1], in1=D, op0=ALU.mult, op1=ALU.mult), reads=[x["pKKt"], "nbeta", Dt], writes=[Nt])
                X["n"] = X.get("n", 0) + 1
                if X["n"] <= k.opts.get("s2n", 99):
                    P.op("dve", lambda e: e.tensor_tensor(out=attnT_all[:, c, :], in0=x["pQK"], in1=DT, op=ALU.mult), reads=[x["pQKt"], DTt], writes=[("attnT", c)])
                rhs, rhst = wr.get()
                X["n"] = X.get("n", 0) + 1
                if X["n"] <= k.opts.get("s2n", 99):
                    P.op("dve", lambda e: e.tensor_scalar(out=rhs[:, 0:64], in0=x["pTok"][:, 0:64], scalar1=wsc[:, c, m:m + 1], scalar2=None, op0=ALU.mult), reads=[x["pTokt"], "wsc"], writes=[rhst])
                X["n"] = X.get("n", 0) + 1
                if X["n"] <= k.opts.get("s2n", 99):
                    P.op("dve", lambda e: e.tensor_scalar(out=rhs[:, 64:128], in0=x["pTok"][:, 64:128], scalar1=bBD[:, c, m:m + 1], scalar2=None, op0=ALU.mult), reads=[x["pTokt"], "bBD"], writes=[rhst])
                X["n"] = X.get("n", 0) + 1
                if X["n"] <= k.opts.get("s2n", 99):
                    P.op("dve", lambda e: e.tensor_scalar(out=kdec_all[:, c, :], in0=x["pTok"][:, 0:64], scalar1=ekd[:, c, m:m + 1], scalar2=None, op0=ALU.mult), reads=[x["pTokt"], "ekd"], writes=[("kdec", c)])
                X.update(N=N, Nt=Nt, rhs=rhs, rhst=rhst)

            def s3():
                x = dict(X)
                pNT, pNTt = ps.get()
                P.op("pe", lambda e: e.transpose(pNT, x["N"], k.idn[:]), reads=[x["Nt"], "idn"], writes=[pNTt])
                X.update(pNT=pNT, pNTt=pNTt)

            def s4():
                x = dict(X)
                PT_, PTt = wk.get()
                XT, XTt = wk.get()
                P.op("dve", lambda e: e.tensor_copy(out=PT_, in_=x["pNT"]), reads=[x["pNTt"]], writes=[PTt])
                P.op("dve", lambda e: e.tensor_tensor(out=XT, in0=x["pNT"], in1=k.idn[:], op=ALU.add), reads=[x["pNTt"], "idn"], writes=[XTt])
                X.update(P=x["N"], Pt=x["Nt"], PT=PT_, PTt=PTt, XT=XT, XTt=XTt)

            def lvl_mm(kk):
                def f():
                    x = dict(X)
                    pP, pPt = ps.get()
                    P.op("pe", lambda e: e.matmul(pP, lhsT=x["PT"], rhs=x["P"], start=True, stop=True), reads=[x["PTt"], x["Pt"]], writes=[pPt], inc=(kk == 5))
                    X.update(pP=pP, pPt=pPt)
                    if kk < 5:
                        pPT, pPTt = ps.get()
                        P.op("pe", lambda e: e.matmul(pPT, lhsT=x["P"], rhs=x["PT"], start=True, stop=True), reads=[x["PTt"], x["Pt"]], writes=[pPTt])
                        X.update(pPT=pPT, pPTt=pPTt)
                return f

            def lvl_ev(kk):
                def f():
                    x = dict(X)
                    nP, nPt = wk.get()
                    P.op("dve", lambda e: e.tensor_copy(out=nP, in_=x["pP"]), reads=[x["pPt"]], writes=[nPt])
                    X.update(P=nP, Pt=nPt)
                    if kk < 5:
                        nPT, nPTt = wk.get()
                        P.op("dve", lambda e: e.tensor_copy(out=nPT, in_=x["pPT"]), reads=[x["pPTt"]], writes=[nPTt])
                        X.update(PT=nPT, PTt=nPTt)
                    pX, pXt = ps.get()
                    P.op("pe", lambda e: e.matmul(pX, lhsT=nP, rhs=x["XT"], start=True, stop=True), reads=[nPt, x["XTt"]], writes=[pXt])
                    X.update(pX=pX, pXt=pXt)
                return f

            def lvl_acc(kk):
                def f():
                    x = dict(X)
                    nX, nXt = wk.get()
                    P.op("dve", lambda e: e.tensor_tensor(out=nX, in0=x["XT"], in1=x["pX"], op=ALU.add), reads=[x["XTt"], x["pXt"]], writes=[nXt])
                    X.update(XT=nX, XTt=nXt)
                return f

            def s_sol():
                x = dict(X)
                pU, pUt = ps.get()
                pW, pWt = ps.get()
                P.op("pe", lambda e: e.matmul(pU[:, 0:64], lhsT=x["XT"], rhs=x["rhs"][:, 64:128], start=True, stop=True), reads=[x["XTt"], x["rhst"]], writes=[pUt], inc=False)
                for h2 in range(2):
                    pb = h2 * 64
                    P.op("pe", lambda e, pb=pb: e.matmul(pW[pb:pb + 64, 0:64], lhsT=x["rhs"][pb:pb + 64, 0:64], rhs=x["XT"][pb:pb + 64, pb:pb + 64], start=True, stop=True), reads=[x["XTt"], x["rhst"]], writes=[pWt], inc=(h2 == 1))
                X.update(pU=pU, pUt=pUt, pW=pW, pWt=pWt)

            def s_solev():
                x = dict(X)
                P.op("dve", lambda e: e.tensor_copy(out=u_all[:, c, :], in_=x["pU"][:, 0:64]), reads=[x["pUt"]], writes=[("u", c)])
                P.op("dve", lambda e: e.tensor_copy(out=wT_all[:, c, :], in_=x["pW"][:, 0:64]), reads=[x["pWt"]], writes=[("wT", c)])

            steps = [s1, s2, s3, s4]
            for kk in range(1, 6):
                steps += [lvl_mm(kk), lvl_ev(kk), lvl_acc(kk)]
            steps += [s_sol, s_solev]
            return steps[:k.opts.get("dn_steps", 99)]

        def phase_b_chunk(m, c, first_dir):
            d, hp = m // 2, m % 2
            qn = qkv[hp]
            cs = slice(c * 64, (c + 1) * 64)
            p1, p1t = ps.get()
            p2, p2t = ps.get()
            for h2 in range(2):
                pb = h2 * 64
                P.op("pe", lambda e, pb=pb: e.matmul(p1[pb:pb + 64, 0:64], lhsT=wT_all[pb:pb + 64, c, :], rhs=S[pb:pb + 64, :], start=True, stop=True), reads=[("wT", c), "S"], writes=[p1t], inc=False)
                P.op("pe", lambda e, pb=pb: e.matmul(p2[pb:pb + 64, 0:64], lhsT=qn[pb:pb + 64, cs], rhs=S[pb:pb + 64, :], start=True, stop=True), reads=[("qkv", hp), "S"], writes=[p2t], inc=(h2 == 1))
            vn_, vnt = wv.get()
            P.op("dve", lambda e: e.tensor_tensor(out=vn_, in0=u_all[:, c, :], in1=p1[:, 0:64], op=ALU.subtract), reads=[("u", c), p1t], writes=[vnt])
            p3, p3t = ps.get()
            p4, p4t = ps.get()
            P.op("pe", lambda e: e.matmul(p3[:, 0:64], lhsT=attnT_all[:, c, :], rhs=vn_, start=True, stop=True), reads=[("attnT", c), vnt], writes=[p3t], inc=False)
            for h2 in range(2):
                pb = h2 * 64
                P.op("pe", lambda e, pb=pb: e.matmul(p4[pb:pb + 64, 0:64], lhsT=kdec_all[pb:pb + 64, c, :], rhs=vn_[pb:pb + 64, :], start=True, stop=True), reads=[("kdec", c), vnt], writes=[p4t], inc=(h2 == 1))
            t_, tt_ = wv.get()
            P.op("dve", lambda e: e.tensor_scalar(out=t_, in0=p2[:, 0:64], scalar1=eg[:, c, m:m + 1], scalar2=None, op0=ALU.mult), reads=[p2t, "eg"], writes=[tt_])
            if first_dir:
                P.op("dve", lambda e: e.tensor_tensor(out=O_tok[:, c, :], in0=t_, in1=p3[:, 0:64], op=ALU.add), reads=[tt_, p3t], writes=[("O", c)])
            else:
                P.op("dve", lambda e: e.tensor_tensor(out=t_, in0=t_, in1=p3[:, 0:64], op=ALU.add), reads=[tt_, p3t], writes=[tt_])
                P.op("dve", lambda e: e.tensor_tensor(out=O_tok[:, c, :], in0=O_tok[:, c, :], in1=t_, op=ALU.add), reads=[tt_, ("O", c)], writes=[("O", c)])
            P.op("dve", lambda e: e.scalar_tensor_tensor(out=S[:], in0=S[:], scalar=glast[:, c, m:m + 1], in1=p4[:, 0:64], op0=ALU.mult, op1=ALU.add), reads=["S", "glast", p4t], writes=["S"])

        def out_phase(b, hp):
            z, yfm = xin, acc
            r0 = FM_Z + hp * 128
            P.dma(lambda e: e.dma_start(out=z[:], in_=k.PT[b, r0:r0 + 128, :]), reads=[("PT", b)], writes=["xin"])
            P.op("act", lambda e: e.activation(out=z[:], in_=z[:], func=AF.Silu), reads=["xin"], writes=["xin"])
            allO = [("O", c) for c in range(NCK)]
            P.op("dve", lambda e: e.tensor_tensor(out=u_all[:], in0=O_tok[:], in1=O_tok[:], op=ALU.mult), reads=allO, writes=[("u", c) for c in range(NCK)])
            P.op("dve", lambda e: e.tensor_reduce(out=ssq[:], in_=u_all[:], axis=AX.X, op=ALU.add), reads=[("u", c) for c in range(NCK)], writes=["ssq"])
            P.op("act", lambda e: e.activation(out=ssq[:], in_=ssq[:], func=AF.Sqrt, bias=epsb[:, 0:1], scale=1.0 / 64), reads=["ssq", "epsb"], writes=["ssq"])
            P.op("dve", lambda e: e.reciprocal(out=ssq[:], in_=ssq[:]), reads=["ssq"], writes=["ssq"])
            P.op("dve", lambda e: e.tensor_tensor(out=O_tok[:], in0=O_tok[:], in1=ssq[:].rearrange("p (c o) -> p c o", o=1).to_broadcast([128, NCK, 64]), op=ALU.mult), reads=allO + ["ssq"], writes=allO)
            P.op("dve", lambda e: e.tensor_tensor(out=O_tok[:], in0=O_tok[:], in1=gno[:].rearrange("p (o v) -> p o v", o=1).to_broadcast([128, NCK, 64]), op=ALU.mult), reads=allO + ["gno"], writes=allO)
            for c in range(NCK):
                out_chunk(c)
            for (t0, n) in [(0, CTX)] + [(CTX + i * 512, 512) for i in range(4)]:
                out_tile(b, hp, t0, n)

        def out_chunk(c):
            pp, ppt = ps.get()
            for h2 in range(2):
                pb = h2 * 64
                P.op("pe", lambda e, pb=pb: e.matmul(pp[pb:pb + 64, 0:64], lhsT=O_tok[pb:pb + 64, c, :], rhs=idn2[pb:pb + 64, :], start=True, stop=True), reads=[("O", c), "idn2"], writes=[ppt], inc=(h2 == 1))
            P.op("dve", lambda e: e.tensor_copy(out=acc[:, c * 64:(c + 1) * 64], in_=pp[:, 0:64]), reads=[ppt], writes=["acc"])

        def out_tile(b, hp, t0, n):
            yi = cnt["y"] % 2
            cnt["y"] += 1
            P.op("dve", lambda e: e.tensor_tensor(out=yb[yi][:, :n], in0=acc[:, t0:t0 + n], in1=xin[:, t0:t0 + n], op=ALU.mult), reads=["acc", "xin"], writes=[("yb", yi)])
            P.dma(lambda e: e.dma_start(out=k.YT[b, 0, hp * 128:(hp + 1) * 128, t0:t0 + n], in_=yb[yi][:, :n]), reads=[("yb", yi)], writes=[("YT", b)])

        G = 3
        for b in range(NB):
            for ti in range(6):
                prep_tile(b, ti)
            prep_gates(b)
            lim = k.opts.get("dn_lim", 99)
            if lim == 0:
                continue
            for hp in range(2):
                for d in range(2):
                    m = d * 2 + hp
                    for c0 in range(0, NCK if lim >= 2 else G, G):
                        lists = [phase_a_steps(m, c) for c in range(c0, min(NCK, c0 + G))]
                        for si in range(len(lists[0])):
                            for lst in lists:
                                lst[si]()
                    if lim < 3:
                        continue
                    P.op("dve", lambda e: e.memset(S[:], 0.0), writes=["S"])
                    order = list(range(NCK)) if d == 0 else [3, 2, 1, 0] + list(range(NCK - 1, 3, -1))
                    for c in order:
                        phase_b_chunk(m, c, d == 0)
                    if "dump" in k.opts and (b, m) == k.opts["dump"][:2]:
                        for j in range(6):
                            P.dma(lambda e, j=j: e.dma_start(out=k.dump[j], in_=qkv[j][:]), reads=[("qkv", j)], writes=[("dump", j)])
                        for j, (tl, tk) in enumerate([(u_all, "u"), (wT_all, "wT"), (kdec_all, "kdec"), (O_tok, "O")]):
                            P.dma(lambda e, j=j, tl=tl: e.dma_start(out=k.dump[6 + j], in_=tl[:].rearrange("p c v -> p (c v)")), reads=[(tk, c) for c in range(NCK)], writes=[("dump", 6 + j)])
                        P.dma(lambda e: e.dma_start(out=k.dump[10:12].rearrange("a p t -> p a t"), in_=attnT_all[:].rearrange("p (a c) v -> p a (c v)", a=2)), reads=[("attnT", c) for c in range(NCK)], writes=[("dump", 10)])
                        for j, (tl, tk, w) in enumerate([(gt, "gt", 288), (bt, "bt", 288), (gcBD, "gcBD", 144), (gtBD, "gtBD", 144), (bBD, "bBD", 144)]):
                            P.dma(lambda e, j=j, tl=tl, w=w: e.dma_start(out=k.dump[12, :, j * 300:j * 300 + w], in_=tl[:].rearrange("p c v -> p (c v)")), reads=[tk], writes=[("dump", 12, j)])
                if lim >= 4:
                    out_phase(b, hp)
        P.end_stage()


def stage_s5(k, l):
    nc, P, NB = k.nc, k.P, k.NB
    HALF_PI = float(np.pi / 2)
    P.serial = k.opts.get("serial_s5", SERIAL_S5)
    with ExitStack() as es:
        al = lambda name, shape, dt=F32: es.enter_context(nc.sbuf_tensor("s_" + name, list(shape), dt))
        lre = al("lre", [128, 16]); lim = al("lim", [128, 16]); stp = al("stp", [128, 16]); mag = al("mag", [128, 16])
        cth = al("cth", [128, 16]); sth = al("sth", [128, 16]); t1 = al("t1", [128, 16]); t2 = al("t2", [128, 16]); t3 = al("t3", [128, 16])
        are = al("are", [128, 16]); aim = al("aim", [128, 16]); cfr = al("cfr", [128, 16]); cfi = al("cfi", [128, 16]); hpi = al("hpi", [128, 1])
        pwc = al("pwc", [128, 16, 12]); pws = al("pws", [128, 16, 12])
        bre = al("bre", [128, 8, 16]); bim = al("bim", [128, 8, 16]); cre = al("cre", [128, 8, 16]); cim = al("cim", [128, 8, 16])
        bb_all = al("bb_all", [128, 32, 16]); bd_all = al("bd_all", [128, 32, 32]); tb = [al("tb%d" % i, [128, 8, 16]) for i in range(4)]
        W_all = al("W_all", [32, 32, 128], BF16); cw_all = al("cw_all", [128, 8, 2, 128], BF16)
        dsk = al("dsk", [128, 2]); wgl = al("wgl", [128, 2, 512], BF16)
        cs = [al("cs%d" % i, [128, T]) for i in range(2)]; sn = [al("sn%d" % i, [128, T]) for i in range(2)]
        xr = al("xr", [128, T]); xi = al("xi", [128, T]); gr = al("gr", [128, T]); gi = al("gi", [128, T])
        u32 = al("u32", [32, T]); u32b = al("u32b", [32, T], BF16); hrb = al("hrb", [128, T], BF16); hib = al("hib", [128, T], BF16); Y = [al("Y%d" % i, [128, T]) for i in range(2)]
        mt = Slots([al("mt%d" % i, [128, 512])[:] for i in range(6)], "mt")
        gel = al("gel", [128, 2, 512], BF16); sg = [al("sg%d" % i, [128, 512]) for i in range(2)]; yb = [al("yb%d" % i, [128, 512], BF16) for i in range(2)]
        banks = [es.enter_context(nc.psum_tensor("s_ps%d" % i, [128, 512], F32)) for i in range(8)]
        psl = Slots([banks[i][:] for i in range(8)], "psb")
        ld = lambda dst, src, tok: P.dma(lambda e: e.dma_start(out=dst, in_=src), writes=[tok])
        ld(lre[:], k.s5_lam_re[l], "lre"); ld(lim[:], k.s5_lam_im[l], "lim"); ld(stp[:], k.s5_log_step[l], "stp")
        ld(bre[:], k.s5_b_re[l], "bre"); ld(bim[:], k.s5_b_im[l], "bim"); ld(cre[:], k.s5_c_re[l], "cre"); ld(cim[:], k.s5_c_im[l], "cim")
        ld(dsk[:], k.s5_d[l], "dsk")
        P.dma(lambda e: e.dma_start(out=wgl[:], in_=k.s5_glu[l].rearrange("(c p) n -> p c n", p=128)), writes=["wgl"], q="pool")
        P.op("dve", lambda e: e.memset(hpi[:], HALF_PI), writes=["hpi"])
        P.op("dve", lambda e: e.memset(bd_all[:], 0.0), writes=["bd_all"])
        P.op("dve", lambda e: e.memset(cw_all[:], 0.0), writes=["cw_all"])
        tt = lambda out, a, b_, op, rd, wr, eng="dve": P.op(eng, lambda e: e.tensor_tensor(out=out, in0=a, in1=b_, op=op), reads=rd, writes=wr)
        P.op("act", lambda e: e.activation(out=stp[:], in_=stp[:], func=AF.Exp), reads=["stp"], writes=["stp"])
        tt(t1[:], lre[:], stp[:], ALU.mult, ["lre", "stp"], ["t1"])
        P.op("act", lambda e: e.activation(out=mag[:], in_=t1[:], func=AF.Exp), reads=["t1"], writes=["mag"])
        tt(t2[:], lim[:], stp[:], ALU.mult, ["lim", "stp"], ["t2"])
        P.op("act", lambda e: e.activation(out=sth[:], in_=t2[:], func=AF.Sin, scale=1.0 / 32), reads=["t2"], writes=["sth"])
        P.op("act", lambda e: e.activation(out=cth[:], in_=t2[:], func=AF.Sin, scale=1.0 / 32, bias=hpi[:, 0:1]), reads=["t2", "hpi"], writes=["cth"])
        for it in range(5):
            tt(t1[:], cth[:], cth[:], ALU.mult, ["cth"], ["t1"])
            tt(t3[:], sth[:], sth[:], ALU.mult, ["sth"], ["t3"])
            P.op("dve", lambda e: e.scalar_tensor_tensor(out=sth[:], in0=cth[:], scalar=2.0, in1=sth[:], op0=ALU.mult, op1=ALU.mult), reads=["cth", "sth"], writes=["sth"])
            tt(cth[:], t1[:], t3[:], ALU.subtract, ["t1", "t3"], ["cth"])
        tt(are[:], mag[:], cth[:], ALU.mult, ["mag", "cth"], ["are"])
        tt(aim[:], mag[:], sth[:], ALU.mult, ["mag", "sth"], ["aim"])
        tt(t1[:], lre[:], lre[:], ALU.mult, ["lre"], ["t1"])
        tt(t3[:], lim[:], lim[:], ALU.mult, ["lim"], ["t3"])
        tt(t1[:], t1[:], t3[:], ALU.add, ["t1", "t3"], ["t1"])
        P.op("dve", lambda e: e.reciprocal(out=t1[:], in_=t1[:]), reads=["t1"], writes=["t1"])
        P.op("dve", lambda e: e.tensor_scalar(out=t2[:], in0=are[:], scalar1=-1.0, scalar2=None, op0=ALU.add), reads=["are"], writes=["t2"])
        tt(cfr[:], t2[:], lre[:], ALU.mult, ["t2", "lre"], ["cfr"])
        tt(t3[:], aim[:], lim[:], ALU.mult, ["aim", "lim"], ["t3"])
        tt(cfr[:], cfr[:], t3[:], ALU.add, ["cfr", "t3"], ["cfr"])
        tt(cfr[:], cfr[:], t1[:], ALU.mult, ["cfr", "t1"], ["cfr"])
        tt(cfi[:], aim[:], lre[:], ALU.mult, ["aim", "lre"], ["cfi"])
        tt(t3[:], t2[:], lim[:], ALU.mult, ["t2", "lim"], ["t3"])
        tt(cfi[:], cfi[:], t3[:], ALU.subtract, ["cfi", "t3"], ["cfi"])
        tt(cfi[:], cfi[:], t1[:], ALU.mult, ["cfi", "t1"], ["cfi"])
        bb4 = bb_all[:].rearrange("p (d c r) h -> p d c r h", d=2, r=2)
        for d in range(2):
            bc = lambda t_: t_[:, d * 8:(d + 1) * 8].rearrange("p (c o) -> p c o", o=1).to_broadcast([128, 8, 16])
            tt(tb[0][:], bre[:], bc(cfr), ALU.mult, ["bre", "cfr"], [("tb", 0)])
            tt(tb[1][:], bim[:], bc(cfi), ALU.mult, ["bim", "cfi"], [("tb", 1)])
            tt(bb4[:, d, :, 0, :], tb[0][:], tb[1][:], ALU.subtract, [("tb", 0), ("tb", 1)], ["bb_all"])
            tt(tb[2][:], bim[:], bc(cfr), ALU.mult, ["bim", "cfr"], [("tb", 2)])
            tt(tb[3][:], bre[:], bc(cfi), ALU.mult, ["bre", "cfi"], [("tb", 3)])
            tt(bb4[:, d, :, 1, :], tb[2][:], tb[3][:], ALU.add, [("tb", 2), ("tb", 3)], ["bb_all"])
        for g2 in range(2):
            rr_ = slice(g2 * 64, (g2 + 1) * 64)
            P.op("dve", lambda e, rr_=rr_, g2=g2: e.tensor_copy(out=bd_all[rr_, :, g2 * 16:(g2 + 1) * 16], in_=bb_all[rr_, :, :]), reads=["bb_all", "bd_all"], writes=["bd_all"])

        def w_build(j):
            pw, pwt = psl.get()
            P.op("pe", lambda e: e.transpose(pw[0:32, 0:128], bd_all[:, j, :], k.idn[:]), reads=["bd_all", "idn"], writes=[pwt])
            P.op("act", lambda e: e.activation(out=W_all[:, j, :], in_=pw[0:32, 0:128], func=AF.Identity), reads=[pwt], writes=["W_all"])
        for j in range(32):
            w_build(j)

        def cw_build(ct, g2):
            q = ct % 4
            rr_ = slice(g2 * 64, (g2 + 1) * 64)
            c0 = q * 32 + g2 * 16
            P.op("dve", lambda e: e.tensor_copy(out=cw_all[rr_, ct, 0, c0:c0 + 16], in_=cre[rr_, ct, :]), reads=["cre", "cw_all"], writes=["cw_all"])
            P.op("dve", lambda e: e.tensor_scalar(out=cw_all[rr_, ct, 1, c0:c0 + 16], in0=cim[rr_, ct, :], scalar1=-1.0, scalar2=None, op0=ALU.mult), reads=["cim", "cw_all"], writes=["cw_all"])
        for ct in range(8):
            for g2 in range(2):
                cw_build(ct, g2)
        P.op("dve", lambda e: e.tensor_copy(out=pwc[:, :, 0], in_=cth[:]), reads=["cth"], writes=["pwc"])
        P.op("dve", lambda e: e.tensor_copy(out=pws[:, :, 0], in_=sth[:]), reads=["sth"], writes=["pws"])

        def pw_level(kk):
            c_, s_ = pwc[:, :, kk - 1], pws[:, :, kk - 1]
            tt(t1[:], c_, c_, ALU.mult, ["pwc"], ["t1"])
            tt(t3[:], s_, s_, ALU.mult, ["pws"], ["t3"])
            tt(pwc[:, :, kk], t1[:], t3[:], ALU.subtract, ["t1", "t3", "pwc"], ["pwc"])
            P.op("dve", lambda e: e.scalar_tensor_tensor(out=pws[:, :, kk], in0=c_, scalar=2.0, in1=s_, op0=ALU.mult, op1=ALU.mult), reads=["pwc", "pws"], writes=["pws"])
        for kk in range(1, 12):
            pw_level(kk)

        def table_gen(j):
            i = j % 2
            C, S_ = cs[i], sn[i]
            ctk, stk = ("cs", i), ("sn", i)
            P.op("dve", lambda e: e.memset(C[:, 0:1], 1.0), writes=[ctk])
            P.op("dve", lambda e: e.memset(S_[:, 0:1], 0.0), writes=[stk])
            for kk in range(12):
                ln = 1 << kk
                nn = min(ln, T - ln)
                if nn <= 0:
                    break
                pc, ps_ = pwc[:, j, kk:kk + 1], pws[:, j, kk:kk + 1]
                lvl(C, S_, ctk, stk, ln, nn, pc, ps_)
            P.dma(lambda e: e.dma_start(out=k.S5TAB[j, 0], in_=C[:]), reads=[ctk], writes=[("TAB", j)])
            P.dma(lambda e: e.dma_start(out=k.S5TAB[j, 1], in_=S_[:]), reads=[stk], writes=[("TAB", j)])

        def lvl(C, S_, ctk, stk, ln, nn, pc, ps_):
            m1, m1t = mt.get()
            m2, m2t = mt.get()
            w_ = min(nn, 512)
            for o in range(0, nn, 512):
                w = min(512, nn - o)
                sub(C, S_, ctk, stk, ln, o, w, pc, ps_)

        def sub(C, S_, ctk, stk, ln, o, w, pc, ps_):
            m1, m1t = mt.get()
            m2, m2t = mt.get()
            P.op("dve", lambda e: e.tensor_scalar(out=m1[:, :w], in0=S_[:, o:o + w], scalar1=ps_, scalar2=None, op0=ALU.mult), reads=[stk, "pws"], writes=[m1t])
            P.op("dve", lambda e: e.tensor_scalar(out=m2[:, :w], in0=S_[:, o:o + w], scalar1=pc, scalar2=None, op0=ALU.mult), reads=[stk, "pwc"], writes=[m2t])
            P.op("dve", lambda e: e.scalar_tensor_tensor(out=C[:, ln + o:ln + o + w], in0=C[:, o:o + w], scalar=pc, in1=m1[:, :w], op0=ALU.mult, op1=ALU.subtract), reads=[ctk, m1t, "pwc"], writes=[ctk])
            P.op("dve", lambda e: e.scalar_tensor_tensor(out=S_[:, ln + o:ln + o + w], in0=C[:, o:o + w], scalar=ps_, in1=m2[:, :w], op0=ALU.mult, op1=ALU.add), reads=[ctk, m2t, "pws"], writes=[stk])
        for j in range(16):
            table_gen(j)

        blocks = [(0, CTX)] + [(CTX + i * 512, 512) for i in range(4)]

        def tabview(tab, d, t0, n):
            if d == 0:
                return tab[:, t0:t0 + n]
            if t0 < CTX:
                lo = CTX - 1 - (t0 + n - 1)
            else:
                lo = CTX + (T - 1 - (t0 + n - 1))
            return tab[:, lo:lo + n][:, ::-1]

        def do_block_in(ct, d, i, t0, n):
            j = d * 8 + ct
            pr, prt = psl.get()
            pi_, pit = psl.get()
            P.op("pe", lambda e: e.matmul(pr[:, :n], lhsT=W_all[:, j * 2, :], rhs=u32b[:, t0:t0 + n], start=True, stop=True), reads=["W_all", "u32b"], writes=[prt], inc=False)
            P.op("pe", lambda e: e.matmul(pi_[:, :n], lhsT=W_all[:, j * 2 + 1, :], rhs=u32b[:, t0:t0 + n], start=True, stop=True), reads=["W_all", "u32b"], writes=[pit])
            cv, sv = tabview(cs[i], d, t0, n), tabview(sn[i], d, t0, n)
            ms = [mt.get() for _ in range(4)]
            tt(ms[0][0][:, :n], pr[:, :n], cv, ALU.mult, [prt, ("cs", i)], [ms[0][1]])
            tt(ms[1][0][:, :n], pi_[:, :n], sv, ALU.mult, [pit, ("sn", i)], [ms[1][1]])
            tt(xr[:, t0:t0 + n], ms[0][0][:, :n], ms[1][0][:, :n], ALU.add, [ms[0][1], ms[1][1]], ["xr"])
            tt(ms[2][0][:, :n], pi_[:, :n], cv, ALU.mult, [pit, ("cs", i)], [ms[2][1]])
            tt(ms[3][0][:, :n], pr[:, :n], sv, ALU.mult, [prt, ("sn", i)], [ms[3][1]])
            tt(xi[:, t0:t0 + n], ms[2][0][:, :n], ms[3][0][:, :n], ALU.subtract, [ms[2][1], ms[3][1]], ["xi"])

        def do_scan(ct, d):
            j = d * 8 + ct
            for (src, dst, stok, dtok) in ((xr, gr, "xr", "gr"), (xi, gi, "xi", "gi")):
                scan1(j, d, src, dst, stok, dtok)

        def scan1(j, d, src, dst, stok, dtok):
            rb = lambda n: mag[:, j:j + 1].to_broadcast([128, n])
            if d == 0:
                P.op("dve", lambda e: e.tensor_tensor_scan(out=dst[:], data0=rb(T), data1=src[:], initial=0.0, op0=ALU.mult, op1=ALU.add), reads=[stok, "mag"], writes=[dtok])
            else:
                P.op("dve", lambda e: e.tensor_tensor_scan(out=dst[:, 0:CTX][:, ::-1], data0=rb(CTX), data1=src[:, 0:CTX][:, ::-1], initial=0.0, op0=ALU.mult, op1=ALU.add), reads=[stok, "mag"], writes=[dtok])
                P.op("dve", lambda e: e.tensor_tensor_scan(out=dst[:, CTX:T][:, ::-1], data0=rb(SEQ), data1=src[:, CTX:T][:, ::-1], initial=dst[:, 0:1], op0=ALU.mult, op1=ALU.add), reads=[stok, "mag", dtok], writes=[dtok])

        def do_block_out(ct, d, i, t0, n, first):
            cv, sv = tabview(cs[i], d, t0, n), tabview(sn[i], d, t0, n)
            ms = [mt.get() for _ in range(4)]
            tt(ms[0][0][:, :n], gr[:, t0:t0 + n], cv, ALU.mult, ["gr", ("cs", i)], [ms[0][1]])
            tt(ms[1][0][:, :n], gi[:, t0:t0 + n], sv, ALU.mult, ["gi", ("sn", i)], [ms[1][1]])
            tt(hrb[:, t0:t0 + n], ms[0][0][:, :n], ms[1][0][:, :n], ALU.subtract, [ms[0][1], ms[1][1]], ["hrb"])
            tt(ms[2][0][:, :n], gr[:, t0:t0 + n], sv, ALU.mult, ["gr", ("sn", i)], [ms[2][1]])
            tt(ms[3][0][:, :n], gi[:, t0:t0 + n], cv, ALU.mult, ["gi", ("cs", i)], [ms[3][1]])
            tt(hib[:, t0:t0 + n], ms[2][0][:, :n], ms[3][0][:, :n], ALU.add, [ms[2][1], ms[3][1]], ["hib"])
            py, pyt = psl.get()
            P.op("pe", lambda e: e.matmul(py[:, :n], lhsT=cw_all[:, ct, 0, :], rhs=hrb[:, t0:t0 + n], start=True, stop=False), reads=["cw_all", "hrb"], writes=[pyt], inc=False)
            P.op("pe", lambda e: e.matmul(py[:, :n], lhsT=cw_all[:, ct, 1, :], rhs=hib[:, t0:t0 + n], start=False, stop=True), reads=["cw_all", "hib"], writes=[pyt])
            yt_ = Y[ct // 4]
            ytk = ("Y", ct // 4)
            if first:
                P.op("dve", lambda e: e.tensor_copy(out=yt_[:, t0:t0 + n], in_=py[:, :n]), reads=[pyt], writes=[ytk])
            else:
                tt(yt_[:, t0:t0 + n], yt_[:, t0:t0 + n], py[:, :n], ALU.add, [pyt, ytk], [ytk])

        def do_ct_dir(b, ct, d):
            j = d * 8 + ct
            i = j % 2
            P.dma(lambda e: e.dma_start(out=cs[i][:], in_=k.S5TAB[j, 0]), reads=[("TAB", j)], writes=[("cs", i)])
            P.dma(lambda e: e.dma_start(out=sn[i][:], in_=k.S5TAB[j, 1]), reads=[("TAB", j)], writes=[("sn", i)])
            for (t0, n) in blocks:
                do_block_in(ct, d, i, t0, n)
            do_scan(ct, d)
            for (t0, n) in blocks:
                do_block_out(ct, d, i, t0, n, (ct % 4 == 0 and d == 0))

        def do_ct(b, ct):
            r0 = FM_U + ct * 32
            P.dma(lambda e: e.dma_start(out=u32[:], in_=k.PT[b, r0:r0 + 32, :]), reads=[("PT", b)], writes=["u32"])
            P.op("dve", lambda e: e.tensor_copy(out=u32b[:], in_=u32[:]), reads=["u32"], writes=["u32b"])
            for d in range(2):
                do_ct_dir(b, ct, d)

        def out_tile(b, t0, n):
            for yt in range(2):
                ub, ubt = mt.get()
                r0 = FM_U + yt * 128
                P.dma(lambda e, ub=ub, r0=r0: e.dma_start(out=ub[:, :n], in_=k.PT[b, r0:r0 + 128, t0:t0 + n]), reads=[("PT", b)], writes=[ubt])
                yv, yvt = mt.get()
                P.op("dve", lambda e, ub=ub, yv=yv, yt=yt: e.scalar_tensor_tensor(out=yv[:, :n], in0=ub[:, :n], scalar=dsk[:, yt:yt + 1], in1=Y[yt][:, t0:t0 + n], op0=ALU.mult, op1=ALU.add), reads=[ubt, "dsk", ("Y", yt)], writes=[yvt])
                x2, x2t = mt.get()
                P.op("act", lambda e, yv=yv, x2=x2: e.activation(out=x2[:, :n], in_=yv[:, :n], func=AF.Square), reads=[yvt], writes=[x2t])
                P.op("dve", lambda e, x2=x2: e.tensor_scalar(out=x2[:, :n], in0=x2[:, :n], scalar1=0.044715, scalar2=1.0, op0=ALU.mult, op1=ALU.add), reads=[x2t], writes=[x2t])
                P.op("dve", lambda e, x2=x2, yv=yv: e.tensor_tensor(out=x2[:, :n], in0=x2[:, :n], in1=yv[:, :n], op=ALU.mult), reads=[x2t, yvt], writes=[x2t])
                P.op("act", lambda e, x2=x2: e.activation(out=x2[:, :n], in_=x2[:, :n], func=AF.Tanh, scale=0.7978845608028654), reads=[x2t], writes=[x2t])
                P.op("dve", lambda e, x2=x2, yv=yv, yt=yt: e.scalar_tensor_tensor(out=gel[:, yt, :n], in0=x2[:, :n], scalar=1.0, in1=yv[:, :n], op0=ALU.add, op1=ALU.mult), reads=[x2t, yvt], writes=[("gel", yt)])
            for oc in range(2):
                pa, pat = psl.get()
                pg, pgt = psl.get()
                for kc in range(2):
                    P.op("pe", lambda e, oc=oc, kc=kc, pa=pa: e.matmul(pa[:, :n], lhsT=wgl[:, kc, oc * 128:(oc + 1) * 128], rhs=gel[:, kc, :n], start=(kc == 0), stop=(kc == 1)), reads=["wgl", ("gel", kc)], writes=[pat])
                for kc in range(2):
                    P.op("pe", lambda e, oc=oc, kc=kc, pg=pg: e.matmul(pg[:, :n], lhsT=wgl[:, kc, 256 + oc * 128:256 + (oc + 1) * 128], rhs=gel[:, kc, :n], start=(kc == 0), stop=(kc == 1)), reads=["wgl", ("gel", kc)], writes=[pgt])
                P.op("act", lambda e, oc=oc, pg=pg: e.activation(out=sg[oc][:, :n], in_=pg[:, :n], func=AF.Sigmoid, scale=0.5), reads=[pgt], writes=[("sg", oc)])
                P.op("dve", lambda e, oc=oc, pa=pa: e.scalar_tensor_tensor(out=yb[oc][:, :n], in0=pa[:, :n], scalar=0.5, in1=sg[oc][:, :n], op0=ALU.mult, op1=ALU.mult), reads=[pat, ("sg", oc)], writes=[("yb", oc)])
                P.dma(lambda e, oc=oc: e.dma_start(out=k.YT[b, 1, oc * 128:(oc + 1) * 128, t0:t0 + n], in_=yb[oc][:, :n]), reads=[("yb", oc)], writes=[("YT", b)])

        for b in range(NB):
            for ct in range(8):
                do_ct(b, ct)
            for (t0, n) in blocks:
                out_tile(b, t0, n)
        P.end_stage()


def stage_mixers(k, l):
    sel = k.opts.get("mixers", ("dn", "s5", "hg", "att"))
    if "dn" in sel:
        stage_deltanet(k, l)
    if "s5" in sel:
        stage_s5(k, l)
    if "hg" in sel:
        stage_hgrn2(k, l)
    if "att" in sel:
        stage_attention(k, l)


def stage_merge(k, l):
    nc, P, NB, NC = k.nc, k.P, k.NB, k.NC
    with ExitStack() as es:
        wg = es.enter_context(nc.sbuf_tensor("m_wg", [128, 8, 4 * D], BF16))
        wb = es.enter_context(nc.sbuf_tensor("m_wb", [128, 8, D], BF16))
        wo = es.enter_context(nc.sbuf_tensor("m_wo", [128, 8, D], BF16))
        xt = [es.enter_context(nc.sbuf_tensor("m_x%d" % s, [128, 8, 256], F32)) for s in range(2)]
        yt = [es.enter_context(nc.sbuf_tensor("m_y%d" % s, [128, 8, 256], BF16)) for s in range(2)]
        ht = es.enter_context(nc.sbuf_tensor("m_h", [128, 8, 256], BF16))
        acc = es.enter_context(nc.sbuf_tensor("m_acc", [128, 8, 256], F32))
        accb = es.enter_context(nc.sbuf_tensor("m_accb", [128, 8, 256], BF16))
        sg = [es.enter_context(nc.sbuf_tensor("m_sg%d" % s, [128, 256], F32)) for s in range(2)]
        tt = [es.enter_context(nc.sbuf_tensor("m_tt%d" % s, [128, 256], F32)) for s in range(2)]
        psg = [es.enter_context(nc.psum_tensor("m_psg%d" % s, [128, 512], F32)) for s in range(2)]
        psy = [es.enter_context(nc.psum_tensor("m_psy%d" % s, [128, 512], F32)) for s in range(2)]
        pso = [es.enter_context(nc.psum_tensor("m_pso%d" % s, [128, 512], F32)) for s in range(2)]
        ntiles = alloc_norm_tiles(k, es, "m_", 256)
        A, Sh, G = emit_mod_scalars(k, es, "m_", l, 1, 3, 4, 5, 1.0)
        load_w_bf16(k, wg, k.w_gate[l], 4 * D, "wg")
        load_w_bf16(k, wb, k.w_branch[l].rearrange("i r n -> (i r) n"), D, "wb")
        load_w_bf16(k, wo, k.w_out[l], D, "wo")
        it = 0
        qi = 0
        for (b, t0, n, cond) in token_tiles(NB, 256):
            if l == 1 and cond == NB:
                continue
            s = it % 2
            it += 1
            xsrc = k.XT[b].rearrange("(c p) t -> p c t", p=128)[:, :, t0:t0 + n]
            P.dma(lambda e, s=s, xsrc=xsrc, n=n: e.dma_start(out=xt[s][:, :, :n], in_=xsrc), reads=[("XT", b)], writes=[("x", s)])
            ysrc = k.YT[b].rearrange("i (c p) t -> p (i c) t", p=128)[:, :, t0:t0 + n]
            P.dma(lambda e, s=s, ysrc=ysrc, n=n: e.dma_start(out=yt[s][:, :, :n], in_=ysrc), reads=[("YT", b)], writes=[("y", s)])
            emit_norm_mod(k, ntiles, xt[s], ht, n, A, Sh, cond, "modsc", s)
            for m in range(8):
                for i in range(4):
                    q = qi % 2
                    qi += 1
                    for kc in range(8):
                        P.op("pe", lambda e, i=i, m=m, q=q, kc=kc, n=n: e.matmul(psg[q][:, :n], lhsT=wg[:, kc, i * D + m * 128:i * D + (m + 1) * 128], rhs=ht[:, kc, :n], start=(kc == 0), stop=(kc == 7)),
                             reads=[("wg", kc), "h"], writes=[("psg", q)], inc=(kc == 7))
                    for kk in range(2):
                        P.op("pe", lambda e, i=i, m=m, q=q, kk=kk, n=n, s=s: e.matmul(psy[q][:, :n], lhsT=wb[:, i * 2 + kk, m * 128:(m + 1) * 128], rhs=yt[s][:, i * 2 + kk, :n], start=(kk == 0), stop=(kk == 1)),
                             reads=[("wb", i * 2 + kk), ("y", s)], writes=[("psy", q)], inc=(kk == 1))
                    P.op("act", lambda e, q=q, n=n: e.activation(out=sg[q][:, :n], in_=psg[q][:, :n], func=AF.Sigmoid), reads=[("psg", q)], writes=[("sg", q)])
                    if i == 0:
                        P.op("dve", lambda e, q=q, n=n, m=m: e.tensor_tensor(out=acc[:, m, :n], in0=sg[q][:, :n], in1=psy[q][:, :n], op=ALU.mult), reads=[("sg", q), ("psy", q)], writes=[("acc", m)])
                    else:
                        P.op("dve", lambda e, q=q, n=n: e.tensor_tensor(out=tt[q][:, :n], in0=sg[q][:, :n], in1=psy[q][:, :n], op=ALU.mult), reads=[("sg", q), ("psy", q)], writes=[("tt", q)])
                        if i < 3:
                            P.op("pool", lambda e, q=q, n=n, m=m: e.tensor_tensor(out=acc[:, m, :n], in0=acc[:, m, :n], in1=tt[q][:, :n], op=ALU.add), reads=[("tt", q), ("acc", m)], writes=[("acc", m)])
                        else:
                            P.op("pool", lambda e, q=q, n=n, m=m: e.tensor_tensor(out=accb[:, m, :n], in0=acc[:, m, :n], in1=tt[q][:, :n], op=ALU.add), reads=[("tt", q), ("acc", m)], writes=[("accb", m)])
            for m in range(8):
                q = m % 2
                for kc in range(8):
                    P.op("pe", lambda e, m=m, q=q, kc=kc, n=n: e.matmul(pso[q][:, :n], lhsT=wo[:, kc, m * 128:(m + 1) * 128], rhs=accb[:, kc, :n], start=(kc == 0), stop=(kc == 7)),
                         reads=[("wo", kc), ("accb", kc)], writes=[("pso", q)], inc=(kc == 7))
                P.op("dve", lambda e, m=m, q=q, s=s, n=n, cond=cond: e.scalar_tensor_tensor(out=xt[s][:, m, :n], in0=pso[q][:, :n], scalar=G[:, m, cond:cond + 1], in1=xt[s][:, m, :n], op0=ALU.mult, op1=ALU.add),
                     reads=[("pso", q), ("x", s), "modsg"], writes=[("x", s)])
            P.dma(lambda e, s=s, xsrc=xsrc, n=n: e.dma_start(out=xsrc, in_=xt[s][:, :, :n]), reads=[("x", s)], writes=[("XT", b)])
        P.end_stage()


def stage_final(k):
    nc, P, NB = k.nc, k.P, k.NB
    with ExitStack() as es:
        xt = [es.enter_context(nc.sbuf_tensor("o_x%d" % s, [128, 8, 512], F32)) for s in range(2)]
        yt = es.enter_context(nc.sbuf_tensor("o_y", [128, 8, 512], F32))
        ot = [es.enter_context(nc.sbuf_tensor("o_o%d" % s, [128, D], F32)) for s in range(2)]
        sq = es.enter_context(nc.sbuf_tensor("o_sq", [128, 8, 512], BF16))
        rs = es.enter_context(nc.sbuf_tensor("o_rs", [128, 512], F32))
        epsb = es.enter_context(nc.sbuf_tensor("o_eps", [128, 1], F32))
        psms = es.enter_context(nc.psum_tensor("o_psms", [128, 512], F32))
        ps = [es.enter_context(nc.psum_tensor("o_ps%d" % s, [128, 4, 128], F32)) for s in range(4)]
        P.op("dve", lambda e: e.memset(epsb[:], EPS), writes=["epsb"])
        it = 0
        oi = 0
        pi = 0
        for (b, t0, n, cond) in token_tiles(NB):
            if cond == NB:
                continue
            s = it % 2
            it += 1
            xsrc = k.XT[b].rearrange("(c p) t -> p c t", p=128)[:, :, t0:t0 + n]
            P.dma(lambda e, s=s, xsrc=xsrc: e.dma_start(out=xt[s][:], in_=xsrc), reads=[("XT", b)], writes=[("x", s)])
            P.op("act", lambda e, s=s: e.activation(out=sq[:], in_=xt[s][:], func=AF.Square), reads=[("x", s)], writes=["sq"])
            for c in range(8):
                P.op("pe", lambda e, c=c: e.matmul(psms[:], lhsT=k.onesb[:], rhs=sq[:, c, :], start=(c == 0), stop=(c == 7)), reads=["sq", "onesb"], writes=["psms"], inc=(c == 7))
            P.op("act", lambda e: e.activation(out=rs[:], in_=psms[:], func=AF.Sqrt, bias=epsb[:, 0:1], scale=1.0), reads=["psms", "epsb"], writes=["rs0", "rs"])
            P.op("dve", lambda e: e.reciprocal(out=rs[:], in_=rs[:]), reads=["rs0"], writes=["rs"])
            for c in range(8):
                P.op("dve", lambda e, c=c, s=s: e.scalar_tensor_tensor(out=yt[:, c, :], in0=xt[s][:, c, :], scalar=k.fg[:, c:c + 1], in1=rs[:], op0=ALU.mult, op1=ALU.mult),
                     reads=[("x", s), "rs", "fg"], writes=[("yt", c)])
            for tb in range(4):
                o = oi % 2
                oi += 1
                for h in range(2):
                    p = pi % 4
                    pi += 1
                    for c in range(4):
                        ch = h * 4 + c
                        P.op("pe", lambda e, p=p, c=c, ch=ch, tb=tb: e.transpose(ps[p][:, c, :], yt[:, ch, tb * 128:(tb + 1) * 128], k.idn[:]),
                             reads=[("yt", ch), "idn"], writes=[("ps", p)], inc=(c == 3))
                    if h == 0:
                        P.op("act", lambda e, p=p, o=o, h=h: e.activation(out=ot[o][:, h * 512:(h + 1) * 512], in_=ps[p][:].rearrange("p a b -> p (a b)"), func=AF.Identity), reads=[("ps", p)], writes=[("ot", o, h)])
                    else:
                        P.op("dve", lambda e, p=p, o=o, h=h: e.tensor_copy(out=ot[o][:, h * 512:(h + 1) * 512], in_=ps[p][:].rearrange("p a b -> p (a b)")), reads=[("ps", p)], writes=[("ot", o, h)])
                r0 = t0 - CTX + tb * 128
                P.dma(lambda e, o=o, b=b, r0=r0: e.dma_start(out=k.out[b, r0:r0 + 128, :], in_=ot[o][:]), reads=[("ot", o, 0), ("ot", o, 1)], writes=[("out", b)])
        P.end_stage()


def make_inputs(inputs, core, NB):
    b0 = core * NB
    f = lambda a: np.ascontiguousarray(a, dtype=np.float32)
    w_in = inputs["w_in"]
    cT = np.concatenate([inputs["c"][b0:b0 + NB], inputs["c_ctx"][None]], 0).reshape(NB + 1, 8, 128).transpose(2, 1, 0)
    return {
        "x": f(inputs["x"][b0:b0 + NB]),
        "ctx": f(inputs["ctx"][b0:b0 + NB]),
        "cT": f(cT),
        "ada_w": f(inputs["ada_w"]),
        "ada_b": f(inputs["ada_b"].reshape(2, 72, 128).transpose(0, 2, 1)),
        "norm_g": f(inputs["norm_g"].reshape(2, 3, 8, 128).transpose(0, 1, 3, 2)),
        "final_g": f(inputs["final_g"].reshape(8, 128).T),
        "ffn_w1": f(inputs["ffn_w1"]), "ffn_w3": f(inputs["ffn_w3"]), "ffn_w2": f(inputs["ffn_w2"]),
        "w_fm": f(np.concatenate([w_in[:, :, a:a + w] for a, w in FM_GROUPS], axis=2)),
        "w_tok": f(np.concatenate([w_in[:, :, a:a + w] for a, w in TOK_GROUPS], axis=2)),
        "w_gate": f(w_in[:, :, GATE0:]),
        "w_branch": f(inputs["w_branch"]),
        "w_out": f(inputs["w_out"]),
        "ident": np.eye(128, dtype=np.float32),
        "rope_cos": ROPE[0], "rope_sin": ROPE[1], "rope_rm": ROPE[2],
        "hg_masks": HG_MASKS, "bd64": BD64,
        "s5_lam_re": f(inputs["s5_lam_re"].reshape(2, 2, 8, 2, 64).transpose(0, 3, 4, 1, 2).reshape(2, 128, 16)),
        "s5_lam_im": f(inputs["s5_lam_im"].reshape(2, 2, 8, 2, 64).transpose(0, 3, 4, 1, 2).reshape(2, 128, 16)),
        "s5_log_step": f(np.broadcast_to(inputs["s5_log_step"].reshape(2, 2, 8, 2, 1), (2, 2, 8, 2, 64)).transpose(0, 3, 4, 1, 2).reshape(2, 128, 16)),
        "s5_b_re": f(inputs["s5_b_re"].reshape(2, 8, 2, 64, 16).transpose(0, 2, 3, 1, 4).reshape(2, 128, 8, 16)),
        "s5_b_im": f(inputs["s5_b_im"].reshape(2, 8, 2, 64, 16).transpose(0, 2, 3, 1, 4).reshape(2, 128, 8, 16)),
        "s5_c_re": f(inputs["s5_c_re"].reshape(2, 8, 2, 16, 64).transpose(0, 2, 4, 1, 3).reshape(2, 128, 8, 16)),
        "s5_c_im": f(inputs["s5_c_im"].reshape(2, 8, 2, 16, 64).transpose(0, 2, 4, 1, 3).reshape(2, 128, 8, 16)),
        "s5_d": f(inputs["s5_d"].reshape(2, 2, 128).transpose(0, 2, 1)),
        "s5_glu": f(inputs["s5_glu"]),
        "dn_gm64": DN_GM64, "dn_gmbd": DN_GMBD, "dn_idn2": DN_IDN2,
        "dn_conv": f(inputs["dn_conv"].reshape(2, 5, 6, 128).transpose(0, 3, 2, 1)),
        "dn_a_log": f(np.broadcast_to(inputs["dn_a_log"].reshape(2, 1, 8), (2, 128, 8))),
        "dn_dt_bias": f(np.broadcast_to(inputs["dn_dt_bias"].reshape(2, 1, 8), (2, 128, 8))),
        "dn_norm_g": f(np.broadcast_to(inputs["dn_norm_g"].reshape(2, 1, 64), (2, 128, 64))),
        "hg_lb": f(inputs["hg_lb_logits"].reshape(2, 2, 128).transpose(2, 1, 0)),
        "hg_norm_g": f(np.tile(inputs["hg_norm_g"], (1, 2)).reshape(2, 128, 1)),
        "at_qn_g": f(inputs["at_qn_g"].reshape(2, 64, 1)), "at_kn_g": f(inputs["at_kn_g"].reshape(2, 64, 1)),
    }


def _rope_consts():
    n = np.arange(SEQ)
    r, c = n // 64, n % 64
    inv = (10000.0 ** (-np.arange(0, 32, 2, dtype=np.float32) / 32)).astype(np.float32)
    ang_r = r[:, None].astype(np.float32) * inv
    ang_c = c[:, None].astype(np.float32) * inv
    ang = np.concatenate([ang_r, ang_r, ang_c, ang_c], -1)
    rm = np.zeros((64, 64), np.float32)
    for i in range(16):
        rm[16 + i, i] = -1.0
        rm[i, 16 + i] = 1.0
        rm[48 + i, 32 + i] = -1.0
        rm[32 + i, 48 + i] = 1.0
    return np.ascontiguousarray(np.cos(ang).T.astype(np.float32)), np.ascontiguousarray(np.sin(ang).T.astype(np.float32)), rm


ROPE = _rope_consts()


def _dn_consts():
    a = np.arange(64)
    le = [(a[:, None] <= a[None, :]), (a[:, None] >= a[None, :])]
    st = [(a[None, :] < a[:, None]), (a[None, :] > a[:, None])]
    gm64 = np.zeros((64, 2, 2, 128), np.float32)
    gmbd = np.zeros((128, 2, 2, 128), np.float32)
    for d in range(2):
        gm64[:, d, 0, :] = np.tile(le[d], (1, 2))
        gm64[:, d, 1, :] = np.tile(st[d], (1, 2))
        for h in range(2):
            gmbd[h * 64:(h + 1) * 64, d, 0, h * 64:(h + 1) * 64] = st[d]
            gmbd[h * 64:(h + 1) * 64, d, 1, h * 64:(h + 1) * 64] = le[d]
    idn2 = np.tile(np.eye(64, dtype=np.float32), (2, 1))
    return gm64, gmbd, idn2


DN_GM64, DN_GMBD, DN_IDN2 = _dn_consts()
_s = np.arange(128)[:, None] % 64
_t = np.arange(64)[None, :]
HG_MASKS = np.ascontiguousarray(np.stack([(_s <= _t), (_s >= _t)], 1).astype(np.float32))
BD64 = np.kron(np.eye(2, dtype=np.float32), np.full((64, 64), 1.0 / 64, np.float32))
_CACHE = {}


def kernel(**inputs):
    NB = inputs["x"].shape[0] // N_CORES
    if "nc" not in _CACHE:
        _CACHE["nc"] = build(NB)
    nc = _CACHE["nc"]
    shared = None
    in_maps = []
    for c in range(N_CORES):
        in_maps.append(make_inputs(inputs, c, NB))
    res = run_bass_kernel_spmd(nc, in_maps, core_ids=list(range(N_CORES)))
    return np.concatenate([r["out"] for r in res.results], axis=0).astype(np.float32)
```

```python
import numpy as np
from contextlib import ExitStack
import concourse.bass as bass
import concourse.mybir as mybir
from concourse.bass_utils import run_bass_kernel_spmd

F32 = mybir.dt.float32
BF16 = mybir.dt.bfloat16
AF = mybir.ActivationFunctionType
ALU = mybir.AluOpType
AX = mybir.AxisListType

D = 1024
SEQ = 2048
CTX = 256
T = SEQ + CTX
DFF = 2816
NFF = DFF // 128
EPS = 1e-6
N_CORES = 8
ENG = ("pe", "act", "dve", "pool", "sp")

FM_GROUPS = [(0, 768), (768, 256), (1040, 256), (1296, 256), (1552, 512), (2320, 256), (2576, 256), (2832, 128)]
FM_W = sum(w for _, w in FM_GROUPS)
FM_Q, FM_K, FM_V, FM_Z, FM_U, FM_HQ, FM_HF, FM_HG, FM_AQ, FM_AK = 0, 256, 512, 768, 1024, 1280, 1536, 2048, 2304, 2560
TOK_GROUPS = [(2960, 128), (2064, 256), (1024, 8), (1032, 8)]
TOK_W = 400
TK_AV, TK_HV, TK_A, TK_B = 0, 128, 384, 392
GATE0 = 3088
SERIAL_DN = True
SERIAL_S5 = True


class Prog:
    def __init__(self, nc, n_dma_sems=56):
        self.nc = nc
        self.sem = {e: nc.semaphore("sem_" + e).__enter__() for e in ("pe", "act", "dve", "pool")}
        self.dsem = [nc.semaphore("dsem%d" % i).__enter__() for i in range(n_dma_sems)]
        self.duse = [0] * n_dma_sems
        self.dnext = 0
        self.cnt = {e: 0 for e in self.sem}
        self.lastw = {}
        self.readers = {}
        self.ops = {e: [] for e in ENG}
        self.waited = {}
        self.nops = 0
        self.serial = False
        self.last_ev = {}
        self.prev_ev = None

    def _need(self, eng, ev, waits):
        if ev is None:
            return
        key, val, src = ev
        if self.waited.get((eng, key), 0) >= val:
            return
        self.waited[(eng, key)] = val
        waits.append((key, val))

    def _semh(self, key):
        return self.sem[key] if isinstance(key, str) else self.dsem[key]

    def op(self, eng, fn, reads=(), writes=(), inc=True):
        waits = []
        for r in reads:
            ev = self.lastw.get(r)
            if ev is not None and not (ev[2] == eng and eng == "pe"):
                self._need(eng, ev, waits)
        for w in writes:
            ev = self.lastw.get(w)
            if ev is not None and ev[2] != eng:
                self._need(eng, ev, waits)
            for rv in self.readers.get(w, ()):
                if rv[2] != eng:
                    self._need(eng, rv, waits)
        if self.serial:
            waits = [w_ for w_ in waits if not isinstance(w_[0], str) or w_[0] == eng]
            for w_ in waits:
                pass
            if self.prev_ev is not None and self.prev_ev[2] != eng:
                if self.waited.get((eng, self.prev_ev[0]), 0) < self.prev_ev[1] or True:
                    self.waited[(eng, self.prev_ev[0])] = max(self.waited.get((eng, self.prev_ev[0]), 0), self.prev_ev[1])
                    waits.append((self.prev_ev[0], self.prev_ev[1]))
        if inc:
            self.cnt[eng] += 1
            me = (eng, self.cnt[eng], eng)
        else:
            me = (eng, self.cnt[eng] + 1, eng)
        self.last_ev[eng] = me
        self.prev_ev = me if inc else (self.prev_ev if not self.serial else me)
        for r in reads:
            self.readers.setdefault(r, []).append(me)
        for w in writes:
            self.lastw[w] = me
            self.readers[w] = []
        self.ops[eng].append((waits, fn, ("sem", eng) if inc else ("none", eng)))
        self.nops += 1
        return me

    def dma(self, fn, reads=(), writes=(), q="sp"):
        waits = []
        for r in reads:
            self._need(q, self.lastw.get(r), waits)
        for w in writes:
            self._need(q, self.lastw.get(w), waits)
            for rv in self.readers.get(w, ()):
                self._need(q, rv, waits)
        i = self.dnext
        self.dnext = (self.dnext + 1) % len(self.dsem)
        if self.duse[i] > 0:
            self._need(q, (i, 16 * self.duse[i], "dma"), waits)
        self.duse[i] += 1
        me = (i, 16 * self.duse[i], "dma")
        for r in reads:
            self.readers.setdefault(r, []).append(me)
        for w in writes:
            self.lastw[w] = me
            self.readers[w] = []
        self.ops[q].append((waits, fn, ("dsem", i)))
        self.nops += 1
        return me

    def end_stage(self):
        waits = []
        for tok, ev in self.lastw.items():
            self._need("sp", ev, waits)
        for tok, evs in self.readers.items():
            for ev in evs:
                self._need("sp", ev, waits)
        self.ops["sp"].append((waits, None, None))
        nc = self.nc
        engobj = {"pe": "tensor", "act": "scalar", "dve": "vector", "pool": "gpsimd", "sp": "sync"}
        with nc.Block() as block:
            for e in ENG:
                ops = self.ops[e]
                if not ops:
                    continue

                def body(eo, ops=ops):
                    for waits, fn, inc in ops:
                        for key, val in waits:
                            eo.wait_ge(self._semh(key), val)
                        if fn is None:
                            continue
                        ins = fn(eo)
                        if inc[0] == "sem":
                            ins.then_inc(self.sem[inc[1]], 1)
                        elif inc[0] == "dsem":
                            ins.then_inc(self.dsem[inc[1]], 16)

                getattr(block, engobj[e])(body)
        self.ops = {e: [] for e in ENG}
        self.waited = {}
        self.lastw = {}
        self.readers = {}
        self.last_ev = {}
        self.serial = False
        self.prev_ev = None


class K:
    pass


class NCProxy:
    def __init__(self, nc):
        object.__setattr__(self, "_nc", nc)
        object.__setattr__(self, "_n", [0])

    def __getattr__(self, name):
        return getattr(self._nc, name)

    def sbuf_tensor(self, name, shape, dtype):
        self._n[0] += 1
        return self._nc.sbuf_tensor("%s_u%d" % (name, self._n[0]), shape, dtype)

    def psum_tensor(self, name, shape, dtype):
        self._n[0] += 1
        return self._nc.psum_tensor("%s_u%d" % (name, self._n[0]), shape, dtype)


def token_tiles(NB, ts=512):
    tl = []
    for b in range(NB):
        tl.append((b, 0, CTX, NB))
        for i in range(SEQ // ts):
            tl.append((b, CTX + i * ts, ts, b))
    return tl


def build(NB, opts=None):
    opts = opts or {}
    NC = NB + 1
    nc = bass.Bass("TRN2", target_bir_lowering=False)
    k = K()
    k.nc, k.NB, k.NC, k.opts = NCProxy(nc), NB, NC, opts
    inp = lambda name, shape: nc.dram_tensor(name, list(shape), F32, kind="ExternalInput").ap()
    k.x = inp("x", [NB, SEQ, D])
    k.ctx = inp("ctx", [NB, CTX, D])
    k.cT = inp("cT", [128, 8, NC])
    k.ada_w = inp("ada_w", [2, D, 9 * D])
    k.ada_b = inp("ada_b", [2, 128, 72])
    k.norm_g = inp("norm_g", [2, 3, 128, 8])
    k.final_g = inp("final_g", [128, 8])
    k.w1 = inp("ffn_w1", [2, 2, D, DFF])
    k.w3 = inp("ffn_w3", [2, 2, D, DFF])
    k.w2 = inp("ffn_w2", [2, 2, DFF, D])
    k.w_fm = inp("w_fm", [2, D, FM_W])
    k.w_tok = inp("w_tok", [2, D, TOK_W])
    k.w_gate = inp("w_gate", [2, D, 4 * D])
    k.w_branch = inp("w_branch", [2, 4, 256, D])
    k.w_out = inp("w_out", [2, D, D])
    k.ident = inp("ident", [128, 128])
    k.rope_cos = inp("rope_cos", [64, SEQ])
    k.rope_sin = inp("rope_sin", [64, SEQ])
    k.rope_rm = inp("rope_rm", [64, 64])
    k.at_qn_g = inp("at_qn_g", [2, 64, 1])
    k.at_kn_g = inp("at_kn_g", [2, 64, 1])
    k.hg_masks = inp("hg_masks", [128, 2, 64])
    k.s5_lam_re = inp("s5_lam_re", [2, 128, 16])
    k.s5_lam_im = inp("s5_lam_im", [2, 128, 16])
    k.s5_log_step = inp("s5_log_step", [2, 128, 16])
    k.s5_b_re = inp("s5_b_re", [2, 128, 8, 16])
    k.s5_b_im = inp("s5_b_im", [2, 128, 8, 16])
    k.s5_c_re = inp("s5_c_re", [2, 128, 8, 16])
    k.s5_c_im = inp("s5_c_im", [2, 128, 8, 16])
    k.s5_d = inp("s5_d", [2, 128, 2])
    k.s5_glu = inp("s5_glu", [2, 256, 512])
    k.S5TAB = nc.dram_tensor("S5TAB", [16, 2, 128, T], F32).ap()
    k.dn_gm64 = inp("dn_gm64", [64, 2, 2, 128])
    k.dn_gmbd = inp("dn_gmbd", [128, 2, 2, 128])
    k.dn_idn2 = inp("dn_idn2", [128, 64])
    k.dn_conv = inp("dn_conv", [2, 128, 6, 5])
    k.dn_a_log = inp("dn_a_log", [2, 128, 8])
    k.dn_dt_bias = inp("dn_dt_bias", [2, 128, 8])
    k.dn_norm_g = inp("dn_norm_g", [2, 128, 64])
    k.bd64 = inp("bd64", [128, 128])
    k.hg_lb = inp("hg_lb", [128, 2, 2])
    k.hg_norm_g = inp("hg_norm_g", [2, 128, 1])
    k.out = nc.dram_tensor("out", [NB, SEQ, D], F32, kind="ExternalOutput").ap()
    k.XT = nc.dram_tensor("XT", [NB, D, T], F32).ap()
    only = opts.get("only")
    k.PT = nc.dram_tensor("PT", [NB, FM_W, T], F32, **({"kind": "ExternalInput"} if only else {})).ap()
    k.TOK = nc.dram_tensor("TOK", [NB, T, TOK_W], F32, **({"kind": "ExternalInput"} if only else {})).ap()
    k.YT = nc.dram_tensor("YT", [NB, 4, 256, T], BF16, **({"kind": "ExternalOutput"} if only else {})).ap()
    if "dump" in opts:
        k.dump = nc.dram_tensor("dump", [16, 128, T], F32, kind="ExternalOutput").ap()
    if "dbg" in opts:
        k.dbg = {nm: nc.dram_tensor("dbg_" + nm, list(shp), F32, kind="ExternalOutput").ap() for nm, shp in opts["dbg"].items()}
    P = Prog(nc)
    k.P = P
    with ExitStack() as es:
        al = lambda name, shape, dt=F32: es.enter_context(nc.sbuf_tensor(name, list(shape), dt))
        k.idn = al("idn", [128, 128])
        k.onesb = al("onesb", [128, 128], BF16)
        k.mods = al("mods", [128, 72, NC])
        k.ng = al("ng", [128, 2, 3, 8])
        k.fg = al("fg", [128, 8])
        P.dma(lambda e: e.dma_start(out=k.idn[:], in_=k.ident), writes=["idn"])
        P.dma(lambda e: e.dma_start(out=k.fg[:], in_=k.final_g), writes=["fg"])
        for l in range(2):
            for j in range(3):
                P.dma(lambda e, l=l, j=j: e.dma_start(out=k.ng[:, l, j, :], in_=k.norm_g[l, j]), writes=["ng"])
        P.op("dve", lambda e: e.memset(k.onesb[:], 1.0 / D), writes=["onesb"])
        P.end_stage()
        if only:
            for l in opts.get("layers", (0,)):
                {"att": stage_attention, "hg": stage_hgrn2, "dn": stage_deltanet, "s5": stage_s5}[only](k, l)
            return nc
        stage_transpose_in(k)
        stop = opts.get("stop", "")
        for l in range(2):
            stage_ada(k, l)
            stage_ffn(k, l, 0)
            if stop == "ffn1_%d" % l:
                break
            stage_inproj(k, l)
            if stop == "inproj_%d" % l:
                break
            stage_mixers(k, l)
            if stop == "mixers_%d" % l:
                break
            stage_merge(k, l)
            if stop == "merge_%d" % l:
                break
            stage_ffn(k, l, 1)
        if "XT" in opts.get("dbg", {}):
            P.dma(lambda e: e.dma_start(out=k.dbg["XT"], in_=k.XT), reads=[], writes=["dbgx"])
            P.end_stage()
        if "PT" in opts.get("dbg", {}):
            P.dma(lambda e: e.dma_start(out=k.dbg["PT"], in_=k.PT), reads=[], writes=["dbgp"])
            P.dma(lambda e: e.dma_start(out=k.dbg["TOK"], in_=k.TOK), reads=[], writes=["dbgt"])
            P.end_stage()
        stage_final(k)
    return nc


def stage_transpose_in(k):
    nc, P = k.nc, k.P
    with ExitStack() as es:
        xin = [es.enter_context(nc.sbuf_tensor("ti_x%d" % i, [128, D], F32)) for i in range(2)]
        xo = [es.enter_context(nc.sbuf_tensor("ti_o%d" % i, [128, 8, 128], F32)) for i in range(2)]
        ps = [es.enter_context(nc.psum_tensor("ti_ps%d" % i, [128, 4, 128], F32)) for i in range(4)]
        it = 0
        for b in range(k.NB):
            for tb in range(T // 128):
                s = it % 2
                src = k.ctx[b, tb * 128:(tb + 1) * 128, :] if tb < 2 else k.x[b, (tb - 2) * 128:(tb - 1) * 128, :]
                P.dma(lambda e, s=s, src=src: e.dma_start(out=xin[s][:], in_=src), writes=[("xin", s)])
                for h in range(2):
                    pi = (it * 2 + h) % 4
                    for c in range(4):
                        ch = h * 4 + c
                        P.op("pe", lambda e, s=s, pi=pi, c=c, ch=ch: e.transpose(ps[pi][:, c, :], xin[s][:, ch * 128:(ch + 1) * 128], k.idn[:]),
                             reads=[("xin", s), "idn"], writes=[("tps", pi)], inc=(c == 3))
                    eng = "act" if h == 0 else "dve"
                    if eng == "act":
                        P.op("act", lambda e, s=s, pi=pi, h=h: e.activation(out=xo[s][:, h * 4:(h + 1) * 4, :], in_=ps[pi][:], func=AF.Identity),
                             reads=[("tps", pi)], writes=[("xo", s, h)])
                    else:
                        P.op("dve", lambda e, s=s, pi=pi, h=h: e.tensor_copy(out=xo[s][:, h * 4:(h + 1) * 4, :], in_=ps[pi][:]),
                             reads=[("tps", pi)], writes=[("xo", s, h)])
                dst = k.XT[b].rearrange("(c p) t -> p c t", p=128)[:, :, tb * 128:(tb + 1) * 128]
                P.dma(lambda e, s=s, dst=dst: e.dma_start(out=dst, in_=xo[s][:]), reads=[("xo", s, 0), ("xo", s, 1)], writes=[("XT", b)])
                it += 1
        P.end_stage()


def stage_ada(k, l):
    nc, P, NC = k.nc, k.P, k.NC
    with ExitStack() as es:
        sc = es.enter_context(nc.sbuf_tensor("ad_sc", [128, 8, NC], F32))
        ab = es.enter_context(nc.sbuf_tensor("ad_b", [128, 72], F32))
        wt = [es.enter_context(nc.sbuf_tensor("ad_w%d" % i, [128, 8, 1024], F32)) for i in range(2)]
        ps = [es.enter_context(nc.psum_tensor("ad_ps%d" % i, [128, 8, NC], F32)) for i in range(2)]
        P.dma(lambda e: e.dma_start(out=sc[:], in_=k.cT), writes=["sc"])
        P.dma(lambda e: e.dma_start(out=ab[:], in_=k.ada_b[l]), writes=["ab"])
        P.op("act", lambda e: e.activation(out=sc[:], in_=sc[:], func=AF.Silu), reads=["sc"], writes=["sc"])
        for j in range(9):
            s = j % 2
            src = k.ada_w[l][:, j * 1024:(j + 1) * 1024].rearrange("(c p) n -> p c n", p=128)
            for hh in range(2):
                P.dma(lambda e, s=s, src=src, hh=hh: e.dma_start(out=wt[s][:, hh * 4:(hh + 1) * 4, :], in_=src[:, hh * 4:(hh + 1) * 4, :]), writes=[("adw", s, hh)])
            for m in range(8):
                for kc in range(8):
                    P.op("pe", lambda e, s=s, m=m, kc=kc: e.matmul(ps[s][:, m, :], lhsT=wt[s][:, kc, m * 128:(m + 1) * 128], rhs=sc[:, kc, :], start=(kc == 0), stop=(kc == 7)),
                         reads=[("adw", s, kc // 4), "sc"], writes=[("adps", s)], inc=(kc == 7 and m == 7))
            P.op("dve", lambda e, s=s, j=j: e.tensor_tensor(out=k.mods[:, j * 8:(j + 1) * 8, :], in0=ps[s][:], in1=ab[:, j * 8:(j + 1) * 8].rearrange("p (c o) -> p c o", o=1).to_broadcast([128, 8, NC]), op=ALU.add),
                 reads=[("adps", s), "ab"], writes=["mods"])
        P.end_stage()


def emit_norm_mod(k, es_tiles, x_t, h_t, n, A, Sh, cond, tag, sl):
    P = k.P
    sq, rs, tmp, psms = es_tiles
    P.op("act", lambda e: e.activation(out=sq[:, :, :n], in_=x_t[:, :, :n], func=AF.Square), reads=[("x", sl)], writes=["sq"])
    for c in range(8):
        P.op("pe", lambda e, c=c: e.matmul(psms[:, :n], lhsT=k.onesb[:], rhs=sq[:, c, :n], start=(c == 0), stop=(c == 7)), reads=["sq", "onesb"], writes=["psms"], inc=(c == 7))
    P.op("act", lambda e: e.activation(out=rs[:, :n], in_=psms[:, :n], func=AF.Sqrt, bias=k.epsb[:, 0:1], scale=1.0), reads=["psms", "epsb"], writes=["rs0", "rs"])
    P.op("dve", lambda e: e.reciprocal(out=rs[:, :n], in_=rs[:, :n]), reads=["rs0"], writes=["rs"])
    for c in range(8):
        P.op("dve", lambda e, c=c: e.tensor_tensor(out=tmp[c % 2][:, :n], in0=x_t[:, c, :n], in1=rs[:, :n], op=ALU.mult), reads=[("x", sl), "rs"], writes=[("tmp", c % 2)])
        P.op("pool", lambda e, c=c: e.tensor_scalar(out=h_t[:, c, :n], in0=tmp[c % 2][:, :n], scalar1=A[:, c, cond:cond + 1], scalar2=Sh[:, c, cond:cond + 1], op0=ALU.mult, op1=ALU.add),
             reads=[("tmp", c % 2), tag], writes=["h"])


def alloc_norm_tiles(k, es, pfx, ts=512):
    nc = k.nc
    sq = es.enter_context(nc.sbuf_tensor(pfx + "sq", [128, 8, ts], BF16))
    rs = es.enter_context(nc.sbuf_tensor(pfx + "rs", [128, ts], F32))
    tmp = [es.enter_context(nc.sbuf_tensor(pfx + "tmp%d" % i, [128, ts], F32)) for i in range(2)]
    psms = es.enter_context(nc.psum_tensor(pfx + "psms", [128, 512], F32))
    k.epsb = es.enter_context(nc.sbuf_tensor(pfx + "epsb", [128, 1], F32))
    k.P.op("dve", lambda e: e.memset(k.epsb[:], EPS), writes=["epsb"])
    return sq, rs, tmp, psms


def emit_mod_scalars(k, es, pfx, l, jn, i_shift, i_scale, i_gate, gate_mul):
    nc, P, NC = k.nc, k.P, k.NC
    A = es.enter_context(nc.sbuf_tensor(pfx + "A", [128, 8, NC], F32))
    G = es.enter_context(nc.sbuf_tensor(pfx + "G", [128, 8, NC], F32))
    Sh = k.mods[:, i_shift * 8:(i_shift + 1) * 8, :]
    P.op("dve", lambda e: e.tensor_scalar(out=A[:], in0=k.mods[:, i_scale * 8:(i_scale + 1) * 8, :], scalar1=1.0, scalar2=None, op0=ALU.add), reads=["mods"], writes=["A0"])
    P.op("dve", lambda e: e.tensor_tensor(out=A[:], in0=A[:], in1=k.ng[:, l, jn, :].rearrange("p (c o) -> p c o", o=1).to_broadcast([128, 8, NC]), op=ALU.mult), reads=["A0", "ng"], writes=["modsc"])
    if i_gate is not None:
        P.op("dve", lambda e: e.tensor_scalar(out=G[:], in0=k.mods[:, i_gate * 8:(i_gate + 1) * 8, :], scalar1=gate_mul, scalar2=None, op0=ALU.mult), reads=["mods"], writes=["modsg"])
    return A, Sh, G


def load_w_bf16(k, dst, src_rows, ncols, tok, nk=8):
    P = k.P
    for kc in range(nk):
        P.dma(lambda e, kc=kc: e.dma_start(out=dst[:, kc, :], in_=src_rows[kc * 128:(kc + 1) * 128, :], max_dma_last_dim=4096),
              writes=[(tok, kc)], q="pool")


def stage_ffn(k, l, i):
    nc, P, NB, NC = k.nc, k.P, k.NB, k.NC
    jn = 0 if i == 0 else 2
    mi = (0, 1, 2) if i == 0 else (6, 7, 8)
    last = (l == 1 and i == 1)
    with ExitStack() as es:
        w1 = es.enter_context(nc.sbuf_tensor("f_w1", [128, 8, DFF], BF16))
        w3 = es.enter_context(nc.sbuf_tensor("f_w3", [128, 8, DFF], BF16))
        w2 = es.enter_context(nc.sbuf_tensor("f_w2", [128, NFF, D], BF16))
        xt = [es.enter_context(nc.sbuf_tensor("f_x%d" % s, [128, 8, 256], F32)) for s in range(2)]
        ht = es.enter_context(nc.sbuf_tensor("f_h", [128, 8, 256], BF16))
        hid = es.enter_context(nc.sbuf_tensor("f_hid", [128, NFF, 256], BF16))
        sl_t = [es.enter_context(nc.sbuf_tensor("f_s%d" % s, [128, 256], F32)) for s in range(2)]
        ps1 = [es.enter_context(nc.psum_tensor("f_ps1%d" % s, [128, 512], F32)) for s in range(2)]
        ps3 = [es.enter_context(nc.psum_tensor("f_ps3%d" % s, [128, 512], F32)) for s in range(2)]
        pso = [es.enter_context(nc.psum_tensor("f_pso%d" % s, [128, 512], F32)) for s in range(2)]
        ntiles = alloc_norm_tiles(k, es, "f_", 256)
        A, Sh, G = emit_mod_scalars(k, es, "f_", l, jn, mi[0], mi[1], mi[2], 0.5)
        load_w_bf16(k, w1, k.w1[l, i], DFF, "w1")
        load_w_bf16(k, w3, k.w3[l, i], DFF, "w3")
        load_w_bf16(k, w2, k.w2[l, i], D, "w2", nk=NFF)
        it = 0
        for (b, t0, n, cond) in token_tiles(NB, 256):
            if last and cond == NB:
                continue
            s = it % 2
            it += 1
            xsrc = k.XT[b].rearrange("(c p) t -> p c t", p=128)[:, :, t0:t0 + n]
            P.dma(lambda e, s=s, xsrc=xsrc, n=n: e.dma_start(out=xt[s][:, :, :n], in_=xsrc), reads=[("XT", b)], writes=[("x", s)])
            emit_norm_mod(k, ntiles, xt[s], ht, n, A, Sh, cond, "modsc", s)
            for f in range(NFF):
                q = f % 2
                for kc in range(8):
                    P.op("pe", lambda e, f=f, q=q, kc=kc, n=n: e.matmul(ps1[q][:, :n], lhsT=w1[:, kc, f * 128:(f + 1) * 128], rhs=ht[:, kc, :n], start=(kc == 0), stop=(kc == 7)),
                         reads=[("w1", kc), "h"], writes=[("ps1", q)], inc=(kc == 7))
                for kc in range(8):
                    P.op("pe", lambda e, f=f, q=q, kc=kc, n=n: e.matmul(ps3[q][:, :n], lhsT=w3[:, kc, f * 128:(f + 1) * 128], rhs=ht[:, kc, :n], start=(kc == 0), stop=(kc == 7)),
                         reads=[("w3", kc), "h"], writes=[("ps3", q)], inc=(kc == 7))
                P.op("act", lambda e, q=q, n=n: e.activation(out=sl_t[q][:, :n], in_=ps1[q][:, :n], func=AF.Silu), reads=[("ps1", q)], writes=[("sl", q)])
                P.op("dve", lambda e, q=q, f=f, n=n: e.tensor_tensor(out=hid[:, f, :n], in0=sl_t[q][:, :n], in1=ps3[q][:, :n], op=ALU.mult), reads=[("sl", q), ("ps3", q)], writes=[("hid", f)])
            for m in range(8):
                q = m % 2
                for f in range(NFF):
                    P.op("pe", lambda e, m=m, q=q, f=f, n=n: e.matmul(pso[q][:, :n], lhsT=w2[:, f, m * 128:(m + 1) * 128], rhs=hid[:, f, :n], start=(f == 0), stop=(f == NFF - 1)),
                         reads=[("w2", f), ("hid", f)], writes=[("pso", q)], inc=(f == NFF - 1))
                P.op("dve", lambda e, m=m, q=q, s=s, n=n, cond=cond: e.scalar_tensor_tensor(out=xt[s][:, m, :n], in0=pso[q][:, :n], scalar=G[:, m, cond:cond + 1], in1=xt[s][:, m, :n], op0=ALU.mult, op1=ALU.add),
                     reads=[("pso", q), ("x", s), "modsg"], writes=[("x", s)])
            P.dma(lambda e, s=s, xsrc=xsrc, n=n: e.dma_start(out=xsrc, in_=xt[s][:, :, :n]), reads=[("x", s)], writes=[("XT", b)])
        P.end_stage()


def stage_inproj(k, l):
    nc, P, NB, NC = k.nc, k.P, k.NB, k.NC
    NCH = FM_W // 128
    with ExitStack() as es:
        wf = es.enter_context(nc.sbuf_tensor("p_wf", [128, 8, FM_W], BF16))
        wk = es.enter_context(nc.sbuf_tensor("p_wk", [128, 8, TOK_W], BF16))
        xt = [es.enter_context(nc.sbuf_tensor("p_x%d" % s, [128, 8, 512], F32)) for s in range(2)]
        ht = es.enter_context(nc.sbuf_tensor("p_h", [128, 8, 512], BF16))
        ot = [es.enter_context(nc.sbuf_tensor("p_o%d" % s, [128, 512], F32)) for s in range(4)]
        ps = [es.enter_context(nc.psum_tensor("p_ps%d" % s, [128, 512], F32)) for s in range(4)]
        ntiles = alloc_norm_tiles(k, es, "p_")
        A, Sh, G = emit_mod_scalars(k, es, "p_", l, 1, 3, 4, None, 1.0)
        load_w_bf16(k, wf, k.w_fm[l], FM_W, "wf")
        load_w_bf16(k, wk, k.w_tok[l], TOK_W, "wk")
        it = 0
        oi = 0
        for (b, t0, n, cond) in token_tiles(NB):
            s = it % 2
            it += 1
            xsrc = k.XT[b].rearrange("(c p) t -> p c t", p=128)[:, :, t0:t0 + n]
            P.dma(lambda e, s=s, xsrc=xsrc, n=n: e.dma_start(out=xt[s][:, :, :n], in_=xsrc), reads=[("XT", b)], writes=[("x", s)])
            emit_norm_mod(k, ntiles, xt[s], ht, n, A, Sh, cond, "modsc", s)
            for ch in range(NCH):
                q = oi % 4
                oi += 1
                for kc in range(8):
                    P.op("pe", lambda e, ch=ch, q=q, kc=kc, n=n: e.matmul(ps[q][:, :n], lhsT=wf[:, kc, ch * 128:(ch + 1) * 128], rhs=ht[:, kc, :n], start=(kc == 0), stop=(kc == 7)),
                         reads=[("wf", kc), "h"], writes=[("ps", q)], inc=(kc == 7))
                if oi % 2 == 0:
                    P.op("act", lambda e, q=q, n=n: e.activation(out=ot[q][:, :n], in_=ps[q][:, :n], func=AF.Identity), reads=[("ps", q)], writes=[("ot", q)])
                else:
                    P.op("dve", lambda e, q=q, n=n: e.tensor_copy(out=ot[q][:, :n], in_=ps[q][:, :n]), reads=[("ps", q)], writes=[("ot", q)])
                P.dma(lambda e, q=q, ch=ch, b=b, t0=t0, n=n: e.dma_start(out=k.PT[b, ch * 128:(ch + 1) * 128, t0:t0 + n], in_=ot[q][:, :n]), reads=[("ot", q)], writes=[("PT", b)])
            for tb in range(n // 128):
                q = oi % 4
                oi += 1
                for kc in range(8):
                    P.op("pe", lambda e, tb=tb, q=q, kc=kc: e.matmul(ps[q][:, :TOK_W], lhsT=ht[:, kc, tb * 128:(tb + 1) * 128], rhs=wk[:, kc, :], start=(kc == 0), stop=(kc == 7)),
                         reads=[("wk", kc), "h"], writes=[("ps", q)], inc=(kc == 7))
                P.op("dve", lambda e, q=q: e.tensor_copy(out=ot[q][:, :TOK_W], in_=ps[q][:, :TOK_W]), reads=[("ps", q)], writes=[("ot", q)])
                P.dma(lambda e, q=q, b=b, tb=tb, t0=t0: e.dma_start(out=k.TOK[b, t0 + tb * 128:t0 + (tb + 1) * 128, :], in_=ot[q][:, :TOK_W]), reads=[("ot", q)], writes=[("TOK", b)])
        P.end_stage()


def stage_attention(k, l):
    nc, P, NB = k.nc, k.P, k.NB
    NKB = T // 128
    with ExitStack() as es:
        al = lambda name, shape, dt=F32: es.enter_context(nc.sbuf_tensor("a_" + name, list(shape), dt))
        cos = al("cos", [64, SEQ]); sin = al("sin", [64, SEQ]); rm = al("rm", [64, 64]); o64 = al("o64", [64, 64])
        gq = al("gq", [64, 1]); gk = al("gk", [64, 1]); epsb = al("eps", [64, 1])
        onesb = al("onesb", [128, 64], BF16)
        raw = [al("raw%d" % i, [64, T]) for i in range(2)]
        kT = [al("kT%d" % i, [64, T], BF16) for i in range(2)]
        qT = [al("qT%d" % i, [64, T], BF16) for i in range(2)]
        vraw = al("vraw", [128, NKB, 128])
        vb = al("vb", [128, NKB, 128], BF16)
        sq = al("sq", [64, 512]); rs = al("rs", [64, 512]); kn = al("kn", [64, 512]); t1 = al("t1", [64, 512]); t2 = al("t2", [64, 512])
        pT = [al("pT%d" % i, [128, 512], BF16) for i in range(3)]
        rden = al("rden", [64, 512])
        ot = [al("ot%d" % i, [64, 512], BF16) for i in range(2)]
        psa = es.enter_context(nc.psum_tensor("a_psa", [64, 512], F32))
        psr = es.enter_context(nc.psum_tensor("a_psr", [64, 512], F32))
        pss = [es.enter_context(nc.psum_tensor("a_pss%d" % i, [128, 512], F32)) for i in range(3)]
        pso = es.enter_context(nc.psum_tensor("a_pso", [64, 512], F32))
        psd = es.enter_context(nc.psum_tensor("a_psd", [64, 512], F32))
        P.dma(lambda e: e.dma_start(out=cos[:], in_=k.rope_cos), writes=["cos"])
        P.dma(lambda e: e.dma_start(out=sin[:], in_=k.rope_sin), writes=["sin"])
        P.dma(lambda e: e.dma_start(out=rm[:], in_=k.rope_rm), writes=["rm"])
        P.dma(lambda e: e.dma_start(out=gq[:], in_=k.at_qn_g[l]), writes=["gq"])
        P.dma(lambda e: e.dma_start(out=gk[:], in_=k.at_kn_g[l]), writes=["gk"])
        P.op("dve", lambda e: e.memset(o64[:], 1.0 / 64), writes=["o64"])
        P.op("dve", lambda e: e.memset(epsb[:], EPS), writes=["epsb"])
        P.op("dve", lambda e: e.memset(onesb[:], 1.0), writes=["onesb"])
        cnt = {"raw": 0, "p": 0, "o": 0}

        def prep(b, row0, g_t, gtok, dst, dtok):
            ri = cnt["raw"] % 2
            cnt["raw"] += 1
            P.dma(lambda e: e.dma_start(out=raw[ri][:], in_=k.PT[b, row0:row0 + 64, :]), reads=[("PT", b)], writes=[("raw", ri)])
            for (t0, n) in [(0, CTX)] + [(CTX + i * 512, 512) for i in range(4)]:
                P.op("act", lambda e, t0=t0, n=n: e.activation(out=sq[:, :n], in_=raw[ri][:, t0:t0 + n], func=AF.Square), reads=[("raw", ri)], writes=["sq"])
                P.op("pe", lambda e, n=n: e.matmul(psa[:, :n], lhsT=o64[:], rhs=sq[:, :n], start=True, stop=True), reads=["sq", "o64"], writes=["psa"])
                P.op("act", lambda e, n=n: e.activation(out=rs[:, :n], in_=psa[:, :n], func=AF.Sqrt, bias=epsb[:, 0:1], scale=1.0), reads=["psa", "epsb"], writes=["rs0", "rs"])
                P.op("dve", lambda e, n=n: e.reciprocal(out=rs[:, :n], in_=rs[:, :n]), reads=["rs0"], writes=["rs"])
                if t0 == 0:
                    P.op("dve", lambda e, t0=t0, n=n: e.scalar_tensor_tensor(out=dst[:, t0:t0 + n], in0=raw[ri][:, t0:t0 + n], scalar=g_t[:, 0:1], in1=rs[:, :n], op0=ALU.mult, op1=ALU.mult),
                         reads=[("raw", ri), "rs", gtok], writes=[dtok])
                    continue
                P.op("dve", lambda e, t0=t0, n=n: e.scalar_tensor_tensor(out=kn[:, :n], in0=raw[ri][:, t0:t0 + n], scalar=g_t[:, 0:1], in1=rs[:, :n], op0=ALU.mult, op1=ALU.mult),
                     reads=[("raw", ri), "rs", gtok], writes=["kn"])
                P.op("pe", lambda e, n=n: e.matmul(psr[:, :n], lhsT=rm[:], rhs=kn[:, :n], start=True, stop=True), reads=["kn", "rm"], writes=["psr"])
                P.op("pool", lambda e, t0=t0, n=n: e.tensor_tensor(out=t1[:, :n], in0=kn[:, :n], in1=cos[:, t0 - CTX:t0 - CTX + n], op=ALU.mult), reads=["kn", "cos"], writes=["t1"])
                P.op("dve", lambda e, t0=t0, n=n: e.tensor_tensor(out=t2[:, :n], in0=psr[:, :n], in1=sin[:, t0 - CTX:t0 - CTX + n], op=ALU.mult), reads=["psr", "sin"], writes=["t2"])
                P.op("dve", lambda e, t0=t0, n=n: e.tensor_tensor(out=dst[:, t0:t0 + n], in0=t1[:, :n], in1=t2[:, :n], op=ALU.add), reads=["t1", "t2"], writes=[dtok])

        def do_tile(b, hq, kvh, qi, q0, nq, kbs):
            def s_mm(kb, pi):
                P.op("pe", lambda e: e.matmul(pss[pi][:, :nq], lhsT=kT[kvh][:, kb * 128:(kb + 1) * 128], rhs=qT[qi][:, q0:q0 + nq], start=True, stop=True),
                     reads=[("kT", kvh), ("qT", qi)], writes=[("pss", pi)])
            pis = []
            for j in range(len(kbs)):
                pis.append(cnt["p"] % 3)
                cnt["p"] += 1
            s_mm(kbs[0], pis[0])
            for j, kb in enumerate(kbs):
                if j + 1 < len(kbs):
                    s_mm(kbs[j + 1], pis[j + 1])
                pi = pis[j]
                P.op("act", lambda e, pi=pi: e.activation(out=pT[pi][:, :nq], in_=pss[pi][:, :nq], func=AF.Exp, scale=0.125), reads=[("pss", pi)], writes=[("pT", pi)])
                P.op("pe", lambda e, pi=pi, kb=kb, j=j: e.matmul(pso[:, :nq], lhsT=vb[:, kb, kvh * 64:(kvh + 1) * 64], rhs=pT[pi][:, :nq], start=(j == 0), stop=(j == len(kbs) - 1)),
                     reads=[("pT", pi), "vb"], writes=["pso"], inc=False)
                P.op("pe", lambda e, pi=pi, j=j: e.matmul(psd[:, :nq], lhsT=onesb[:], rhs=pT[pi][:, :nq], start=(j == 0), stop=(j == len(kbs) - 1)),
                     reads=[("pT", pi), "onesb"], writes=["psd"])
            oi = cnt["o"] % 2
            cnt["o"] += 1
            P.op("dve", lambda e: e.reciprocal(out=rden[:, :nq], in_=psd[:, :nq]), reads=["psd"], writes=["rden"])
            P.op("dve", lambda e: e.tensor_tensor(out=ot[oi][:, :nq], in0=pso[:, :nq], in1=rden[:, :nq], op=ALU.mult), reads=["pso", "rden"], writes=[("ot", oi)])
            P.dma(lambda e: e.dma_start(out=k.YT[b, 3, hq * 64:(hq + 1) * 64, q0:q0 + nq], in_=ot[oi][:, :nq]), reads=[("ot", oi)], writes=[("YT", b)])

        for b in range(NB):
            for kvh in range(2):
                prep(b, FM_AK + kvh * 64, gk, "gk", kT[kvh], ("kT", kvh))
            vsrc = k.TOK[b].rearrange("(blk p) c -> p blk c", p=128)[:, :, TK_AV:TK_AV + 128]
            P.dma(lambda e, vsrc=vsrc: e.dma_start(out=vraw[:], in_=vsrc), reads=[("TOK", b)], writes=["vraw"])
            P.op("pool", lambda e: e.tensor_copy(out=vb[:], in_=vraw[:]), reads=["vraw"], writes=["vb"])
            for hq in range(4):
                kvh = hq // 2
                qi = hq % 2
                prep(b, FM_AQ + hq * 64, gq, "gq", qT[qi], ("qT", qi))
                for (q0, nq, kbs) in [(0, CTX, [0, 1])] + [(CTX + i * 512, 512, list(range(NKB))) for i in range(4)]:
                    do_tile(b, hq, kvh, qi, q0, nq, kbs)
        P.end_stage()


def stage_hgrn2(k, l):
    nc, P, NB = k.nc, k.P, k.NB
    NCK = T // 64
    with ExitStack() as es:
        al = lambda name, shape, dt=F32: es.enter_context(nc.sbuf_tensor("h_" + name, list(shape), dt))
        m01 = al("m01", [128, T]); mk = al("mk", [128, 2, 64]); bd = al("bd", [128, 128])
        lg = al("lg", [128, 2, 2]); lb = al("lb", [128, 2]); oml = al("oml", [128, 2]); gn = al("gn", [128, 1]); epsb = al("eps", [128, 1])
        z = al("z", [128, T]); fgt = al("fgt", [128, T]); bb = al("bb", [128, T]); tmp = al("tmp", [128, T])
        ex = [al("ex%d" % i, [128, T]) for i in range(2)]
        q = al("q", [128, T]); kk = al("kk", [128, T]); kd = al("kd", [128, T]); O = al("O", [128, T])
        qt = al("qt", [128, T], BF16); ktI = [al("kt%d" % i, [128, T], BF16) for i in range(4)]; qd = al("qd", [128, T], BF16)
        dec = al("dec", [128, NCK, 1])
        Vb = al("Vb", [128, NCK, 256], BF16)
        kdT = [al("kdT%d" % i, [64, 128], BF16) for i in range(2)]
        scm = [al("scm%d" % i, [128, 64], BF16) for i in range(2)]
        S32 = al("S32", [128, 64]); S16 = [al("S16%d" % i, [128, 64], BF16) for i in range(2)]
        sq = al("sq", [128, 512]); rs = al("rs", [128, 512]); yb = [al("yb%d" % i, [128, 512], BF16) for i in range(2)]
        pst = [es.enter_context(nc.psum_tensor("h_pst%d" % i, [128, 512], F32)) for i in range(2)]
        pss = [es.enter_context(nc.psum_tensor("h_pss%d" % i, [128, 512], F32)) for i in range(2)]
        pso = [es.enter_context(nc.psum_tensor("h_pso%d" % i, [128, 512], F32)) for i in range(2)]
        pskv = [es.enter_context(nc.psum_tensor("h_pskv%d" % i, [128, 512], F32)) for i in range(2)]
        P.dma(lambda e: e.dma_start(out=mk[:], in_=k.hg_masks), writes=["mk"])
        P.dma(lambda e: e.dma_start(out=bd[:], in_=k.bd64), writes=["bd"])
        P.dma(lambda e: e.dma_start(out=lg[:], in_=k.hg_lb), writes=["lg"])
        P.dma(lambda e: e.dma_start(out=gn[:], in_=k.hg_norm_g[l]), writes=["gn"])
        P.op("dve", lambda e: e.memset(epsb[:], EPS), writes=["epsb"])
        P.op("dve", lambda e: e.memset(m01[:], 1.0), writes=["m01"])
        P.op("dve", lambda e: e.memset(m01[:].rearrange("p (c j) -> p c j", j=64)[:, :, 0:1], 0.0), writes=["m01"])
        for i4 in range(4):
            P.op("pool", lambda e, i4=i4: e.memset(ktI[i4][:], 0.0), writes=[("kt", i4)])
        if l == 0:
            P.op("dve", lambda e: e.memset(lb[:], 0.0), writes=["lb"])
            P.op("dve", lambda e: e.memset(oml[:], 1.0), writes=["oml"])
        else:
            P.op("dve", lambda e: e.tensor_tensor(out=lb[:], in0=lg[:, :, 1], in1=lg[:, :, 0], op=ALU.subtract), reads=["lg"], writes=["lb0"])
            P.op("act", lambda e: e.activation(out=lb[:], in_=lb[:], func=AF.Sigmoid), reads=["lb0"], writes=["lb", "lb0"])
            P.op("dve", lambda e: e.tensor_scalar(out=oml[:], in0=lb[:], scalar1=-1.0, scalar2=1.0, op0=ALU.mult, op1=ALU.add), reads=["lb"], writes=["oml"])
        cnt = {"c": 0, "y": 0}
        bb3 = bb[:].rearrange("p (c j) -> p c j", j=64)
        tmp3 = tmp[:].rearrange("p (c j) -> p c j", j=64)

        def do_chunk(b, hp, d, c, first):
            i = cnt["c"] % 2
            cnt["c"] += 1
            cs = slice(c * 64, (c + 1) * 64)
            P.op("pe", lambda e: e.transpose(pst[i][:64, :128], kd[:, cs], k.idn[:]), reads=["kd", "idn"], writes=[("pst", i)])
            P.op("act", lambda e: e.activation(out=kdT[i][:], in_=pst[i][:64, :128], func=AF.Identity), reads=[("pst", i)], writes=[("kdT", i)])
            for h2 in range(2):
                pb = h2 * 64
                for I in range(4):
                    ts_ = slice(c * 64 + 16 * I, c * 64 + 16 * I + 16)
                    P.op("pe", lambda e, pb=pb, I=I, ts_=ts_: e.matmul(pss[i][pb:pb + 64, 16 * I:16 * I + 16], lhsT=ktI[I][pb:pb + 64, cs], rhs=qt[pb:pb + 64, ts_], start=True, stop=True),
                         reads=[("kt", I), "qt"], writes=[("pss", i)], inc=(h2 == 1 and I == 3))
            for h2 in range(2):
                pb = h2 * 64
                h = hp * 2 + h2
                P.op("pe", lambda e, pb=pb, h=h: e.matmul(pskv[i][pb:pb + 64, :64], lhsT=kdT[i][:, pb:pb + 64], rhs=Vb[0:64, c, h * 64:(h + 1) * 64], start=True, stop=True),
                     reads=[("kdT", i), "Vb"], writes=[("pskv", i)], inc=(h2 == 1))
            P.op("dve", lambda e: e.tensor_tensor(out=scm[i][:], in0=pss[i][:, :64], in1=mk[:, d, :], op=ALU.mult), reads=[("pss", i), "mk"], writes=[("scm", i)])
            for h2 in range(2):
                pb = h2 * 64
                h = hp * 2 + h2
                P.op("pe", lambda e, pb=pb, h=h: e.matmul(pso[i][pb:pb + 64, :64], lhsT=Vb[pb:pb + 64, c, h * 64:(h + 1) * 64], rhs=scm[i][pb:pb + 64, :], start=True, stop=False),
                     reads=[("scm", i), "Vb"], writes=[("pso", i)], inc=False)
                P.op("pe", lambda e, pb=pb: e.matmul(pso[i][pb:pb + 64, :64], lhsT=S16[1 - i][pb:pb + 64, :], rhs=qd[pb:pb + 64, cs], start=False, stop=True),
                     reads=[("S16", 1 - i), "qd"], writes=[("pso", i)], inc=(h2 == 1))
            if d == 0:
                P.op("act", lambda e: e.activation(out=O[:, cs], in_=pso[i][:, :64], func=AF.Identity), reads=[("pso", i)], writes=["O"])
            else:
                P.op("dve", lambda e: e.tensor_tensor(out=O[:, cs], in0=O[:, cs], in1=pso[i][:, :64], op=ALU.add), reads=[("pso", i), "O"], writes=["O"])
            P.op("dve", lambda e: e.scalar_tensor_tensor(out=S32[:], in0=S32[:], scalar=dec[:, c, :], in1=pskv[i][:, :64], op0=ALU.mult, op1=ALU.add),
                 reads=["S32", "dec", ("pskv", i)], writes=["S32"])
            P.op("act", lambda e: e.activation(out=S16[i][:], in_=S32[:], func=AF.Identity), reads=["S32"], writes=[("S16", i)])

        def do_dir(b, hp, d):
            rev = (lambda ap: ap[:, ::-1]) if d == 1 else (lambda ap: ap)
            last = 63 if d == 0 else 0
            r0 = FM_HF + d * 256 + hp * 128
            P.dma(lambda e: e.dma_start(out=z[:], in_=k.PT[b, r0:r0 + 128, :]), reads=[("PT", b)], writes=["z"])
            P.op("act", lambda e: e.activation(out=fgt[:], in_=z[:], func=AF.Sigmoid), reads=["z"], writes=["fgt"])
            P.op("dve", lambda e: e.tensor_scalar(out=fgt[:], in0=fgt[:], scalar1=oml[:, hp:hp + 1], scalar2=lb[:, hp:hp + 1], op0=ALU.mult, op1=ALU.add), reads=["fgt", "oml", "lb"], writes=["fgt"])
            P.op("dve", lambda e: e.tensor_scalar(out=fgt[:], in0=fgt[:], scalar1=1e-30, scalar2=None, op0=ALU.max), reads=["fgt"], writes=["fgt"])
            P.op("act", lambda e: e.activation(out=z[:], in_=fgt[:], func=AF.Ln), reads=["fgt"], writes=["z"])
            P.op("dve", lambda e: e.tensor_scalar(out=kk[:], in0=fgt[:], scalar1=-1.0, scalar2=1.0, op0=ALU.mult, op1=ALU.add), reads=["fgt"], writes=["kk"])
            P.op("dve", lambda e: e.tensor_tensor_scan(out=rev(bb[:]), data0=m01[:], data1=rev(z[:]), initial=0.0, op0=ALU.mult, op1=ALU.add), reads=["z", "m01"], writes=["bb"])
            ref = 0 if d == 0 else 15
            bb4 = bb[:].rearrange("p (c i j) -> p c i j", i=4, j=16)
            tmp4 = tmp[:].rearrange("p (c i j) -> p c i j", i=4, j=16)
            P.op("dve", lambda e: e.tensor_tensor(out=tmp4, in0=bb4, in1=bb4[:, :, :, ref:ref + 1].to_broadcast([128, NCK, 4, 16]), op=ALU.subtract), reads=["bb"], writes=["tmp"])
            P.op("act", lambda e: e.activation(out=ex[0][:], in_=tmp[:], func=AF.Exp), reads=["tmp"], writes=[("ex", 0)])
            P.op("pool", lambda e: e.tensor_tensor(out=qt[:], in0=q[:], in1=ex[0][:], op=ALU.mult), reads=["q", ("ex", 0)], writes=["qt"])
            ex3 = [ex[i][:].rearrange("p (c j) -> p c j", j=64) for i in range(2)]
            kk3 = kk[:].rearrange("p (c j) -> p c j", j=64)
            for I in range(4):
                cs_ = slice(0, 16 * (I + 1)) if d == 0 else slice(16 * I, 64)
                w = cs_.stop - cs_.start
                e_ = ex3[I % 2][:, :, cs_]
                rp = 16 * I + ref
                kt3 = ktI[I][:].rearrange("p (c j) -> p c j", j=64)[:, :, cs_]
                P.op("dve", lambda e, cs_=cs_, w=w, rp=rp: e.scalar_tensor_tensor(out=tmp3[:, :, cs_], in0=bb3[:, :, cs_], scalar=-1.0, in1=bb3[:, :, rp:rp + 1].to_broadcast([128, NCK, w]), op0=ALU.mult, op1=ALU.add),
                     reads=["bb"], writes=["tmp"])
                P.op("dve", lambda e, cs_=cs_: e.tensor_scalar(out=tmp3[:, :, cs_], in0=tmp3[:, :, cs_], scalar1=60.0, scalar2=None, op0=ALU.min), reads=["tmp"], writes=["tmp"])
                P.op("act", lambda e, cs_=cs_, e_=e_: e.activation(out=e_, in_=tmp3[:, :, cs_], func=AF.Exp), reads=["tmp"], writes=[("ex", I % 2)])
                P.op("pool", lambda e, cs_=cs_, e_=e_, kt3=kt3: e.tensor_tensor(out=kt3, in0=kk3[:, :, cs_], in1=e_, op=ALU.mult), reads=["kk", ("ex", I % 2)], writes=[("kt", I)])
            P.op("act", lambda e: e.activation(out=ex[0][:], in_=bb[:], func=AF.Exp), reads=["bb"], writes=[("ex", 0)])
            P.op("dve", lambda e: e.tensor_tensor(out=qd[:], in0=q[:], in1=ex[0][:], op=ALU.mult), reads=["q", ("ex", 0)], writes=["qd"])
            P.op("dve", lambda e: e.tensor_tensor(out=tmp3, in0=bb3, in1=bb3[:, :, last:last + 1].to_broadcast([128, NCK, 64]), op=ALU.subtract), reads=["bb"], writes=["tmp"])
            P.op("act", lambda e: e.activation(out=ex[1][:], in_=tmp[:], func=AF.Exp, scale=-1.0), reads=["tmp"], writes=[("ex", 1)])
            P.op("pool", lambda e: e.tensor_tensor(out=kd[:], in0=kk[:], in1=ex[1][:], op=ALU.mult), reads=["kk", ("ex", 1)], writes=["kd"])
            P.op("act", lambda e: e.activation(out=dec[:], in_=bb3[:, :, last:last + 1], func=AF.Exp), reads=["bb"], writes=["dec"])
            if "dump" in k.opts and (b, hp, d) == k.opts["dump"]:
                for j, (tl, tk) in enumerate([(z, "z"), (bb, "bb"), (kd, "kd"), (kk, "kk"), (fgt, "fgt")]):
                    P.dma(lambda e, j=j, tl=tl: e.dma_start(out=k.dump[j], in_=tl[:]), reads=[tk], writes=[("dump", j)])
            P.op("dve", lambda e: e.memset(S32[:], 0.0), writes=["S32"])
            for i in range(2):
                P.op("dve", lambda e, i=i: e.memset(S16[i][:], 0.0), writes=[("S16", i)])
            order = list(range(NCK)) if d == 0 else [3, 2, 1, 0] + list(range(NCK - 1, 3, -1))
            for idx, c in enumerate(order):
                do_chunk(b, hp, d, c, idx == 0)

        def do_pair(b, hp):
            r0 = FM_HQ + hp * 128
            P.dma(lambda e: e.dma_start(out=q[:], in_=k.PT[b, r0:r0 + 128, :]), reads=[("PT", b)], writes=["q"])
            P.op("act", lambda e: e.activation(out=q[:], in_=q[:], func=AF.Silu), reads=["q"], writes=["q"])
            for d in range(2):
                do_dir(b, hp, d)
            if "dump" in k.opts and (b, hp) == k.opts["dump"][:2]:
                P.dma(lambda e: e.dma_start(out=k.dump[5], in_=O[:]), reads=["O"], writes=[("dump", 5)])
            g0 = FM_HG + hp * 128
            P.dma(lambda e: e.dma_start(out=z[:], in_=k.PT[b, g0:g0 + 128, :]), reads=[("PT", b)], writes=["z"])
            P.op("act", lambda e: e.activation(out=z[:], in_=z[:], func=AF.Sigmoid), reads=["z"], writes=["z"])
            for (t0, n) in [(0, CTX)] + [(CTX + i * 512, 512) for i in range(4)]:
                do_out(b, hp, t0, n)

        def do_out(b, hp, t0, n):
            yi = cnt["y"] % 2
            cnt["y"] += 1
            P.op("act", lambda e: e.activation(out=sq[:, :n], in_=O[:, t0:t0 + n], func=AF.Square), reads=["O"], writes=["sq"])
            P.op("pe", lambda e: e.matmul(pss[0][:, :n], lhsT=bd[:], rhs=sq[:, :n], start=True, stop=True), reads=["sq", "bd"], writes=[("pss", 0)])
            P.op("act", lambda e: e.activation(out=rs[:, :n], in_=pss[0][:, :n], func=AF.Sqrt, bias=epsb[:, 0:1], scale=1.0), reads=[("pss", 0), "epsb"], writes=["rs0", "rs"])
            P.op("dve", lambda e: e.reciprocal(out=rs[:, :n], in_=rs[:, :n]), reads=["rs0"], writes=["rs"])
            P.op("dve", lambda e: e.scalar_tensor_tensor(out=sq[:, :n], in0=O[:, t0:t0 + n], scalar=gn[:, 0:1], in1=rs[:, :n], op0=ALU.mult, op1=ALU.mult), reads=["O", "gn", "rs"], writes=["sq"])
            P.op("pool", lambda e: e.tensor_tensor(out=yb[yi][:, :n], in0=sq[:, :n], in1=z[:, t0:t0 + n], op=ALU.mult), reads=["sq", "z"], writes=[("yb", yi)])
            P.dma(lambda e: e.dma_start(out=k.YT[b, 2, hp * 128:(hp + 1) * 128, t0:t0 + n], in_=yb[yi][:, :n]), reads=[("yb", yi)], writes=[("YT", b)])

        for b in range(NB):
            vsrc = k.TOK[b].rearrange("(c s) v -> s c v", s=64)[:, :, TK_HV:TK_HV + 256]
            for h2 in range(2):
                P.dma(lambda e, h2=h2, vsrc=vsrc: e.dma_start(out=Vb[h2 * 64:(h2 + 1) * 64, :, :], in_=vsrc), reads=[("TOK", b)], writes=["Vb"], q="pool")
            for hp in range(2):
                do_pair(b, hp)
        P.end_stage()


class Slots:
    def __init__(self, aps, name, mod=0, off=0):
        self.aps, self.name, self.i, self.mod, self.off = aps, name, 0, mod, off

    def get(self):
        j = self.i % len(self.aps)
        self.i += 1
        return self.aps[j], (self.name, self.off + (j % self.mod if self.mod else j))


def stage_deltanet(k, l):
    nc, P, NB = k.nc, k.P, k.NB
    NCK = T // 64
    P.serial = k.opts.get("serial_dn", SERIAL_DN)
    with ExitStack() as es:
        al = lambda name, shape, dt=F32: es.enter_context(nc.sbuf_tensor("d_" + name, list(shape), dt))
        gm64 = al("gm64", [64, 2, 2, 128]); gmbd = al("gmbd", [128, 2, 2, 128]); idn2 = al("idn2", [128, 64]); bd = al("bd", [128, 128])
        ones64 = al("ones64", [64, 128])
        cw = al("cw", [128, 6, 5]); alog = al("alog", [128, 8]); dtb = al("dtb", [128, 8]); gno = al("gno", [128, 64]); epsb = al("eps", [128, 1])
        qkv = [al("qkv%d" % i, [128, T]) for i in range(6)]
        xin = al("xin", [128, T]); acc = al("acc", [128, T])
        sq = al("sq", [128, 512]); rs = al("rs", [128, 512])
        ab = al("ab", [128, NCK, 16]); gt = al("gt", [128, NCK, 8]); bt = al("bt", [128, NCK, 8]); t8 = al("t8", [128, NCK, 8]); t8b = al("t8b", [128, NCK, 8])
        gcs = al("gcs", [128, NCK, 8]); gts = al("gts", [128, NCK, 8])
        gcBD = al("gcBD", [128, NCK, 4]); gtBD = al("gtBD", [128, NCK, 4]); bBD = al("bBD", [128, NCK, 4])
        eg = al("eg", [128, NCK, 4]); ekd = al("ekd", [128, NCK, 4]); glast = al("glast", [128, NCK, 4]); nbeta = al("nbeta", [128, NCK, 4]); wsc = al("wsc", [128, NCK, 4])
        u_all = al("u_all", [128, NCK, 64]); wT_all = al("wT_all", [128, NCK, 64]); kdec_all = al("kdec_all", [128, NCK, 64]); attnT_all = al("attnT_all", [128, NCK, 128])
        O_tok = al("O_tok", [128, NCK, 64]); ssq = al("ssq", [128, NCK]); S = al("S", [128, 64])
        yb = [al("yb%d" % i, [128, 512], BF16) for i in range(2)]
        wk = Slots([al("wk%d" % i, [128, 128])[:] for i in range(28)], "wk")
        wv = Slots([al("wv%d" % i, [128, 64])[:] for i in range(8)], "wv")
        wr = Slots([al("wr%d" % i, [128, 128])[:] for i in range(6)], "wr")
        banks = [es.enter_context(nc.psum_tensor("d_ps%d" % i, [128, 512], F32)) for i in range(8)]
        ps = Slots([banks[i][:, j * 128:(j + 1) * 128] for j in range(4) for i in range(4)], "ps", mod=4)
        ps2 = Slots([banks[4 + i][:, j * 256:(j + 1) * 256] for j in range(2) for i in range(3)], "ps", mod=3, off=4)
        wk2 = Slots([al("wkp%d" % i, [128, 256])[:] for i in range(10)], "wk2")
        psg = banks[7]
        for i in range(8):
            P.op("dve", lambda e, i=i: e.memset(banks[i][:], 0.0), writes=[("ps", j) for j in range(7)] + ["psg"])
        P.dma(lambda e: e.dma_start(out=gm64[:], in_=k.dn_gm64), writes=["gm64"])
        P.dma(lambda e: e.dma_start(out=gmbd[:], in_=k.dn_gmbd), writes=["gmbd"])
        P.dma(lambda e: e.dma_start(out=idn2[:], in_=k.dn_idn2), writes=["idn2"])
        P.dma(lambda e: e.dma_start(out=bd[:], in_=k.bd64), writes=["bd"])
        P.dma(lambda e: e.dma_start(out=cw[:], in_=k.dn_conv[l]), writes=["cw"])
        P.dma(lambda e: e.dma_start(out=alog[:], in_=k.dn_a_log[l]), writes=["alog"])
        P.dma(lambda e: e.dma_start(out=dtb[:], in_=k.dn_dt_bias[l]), writes=["dtb"])
        P.dma(lambda e: e.dma_start(out=gno[:], in_=k.dn_norm_g[l]), writes=["gno"])
        P.op("dve", lambda e: e.memset(epsb[:], EPS), writes=["epsb"])
        P.op("dve", lambda e: e.memset(ones64[:], 1.0), writes=["ones64"])
        P.op("act", lambda e: e.activation(out=alog[:], in_=alog[:], func=AF.Exp), reads=["alog"], writes=["alog"])
        P.op("dve", lambda e: e.tensor_scalar(out=alog[:], in0=alog[:], scalar1=-1.0, scalar2=None, op0=ALU.mult), reads=["alog"], writes=["alog"])
        cnt = {"y": 0}

        def prep_tile(b, ti):
            r0 = ti * 128
            P.dma(lambda e: e.dma_start(out=xin[:], in_=k.PT[b, r0:r0 + 128, :]), reads=[("PT", b)], writes=["xin"])
            P.op("dve", lambda e: e.tensor_scalar(out=acc[:], in0=xin[:], scalar1=cw[:, ti, 2:3], scalar2=None, op0=ALU.mult), reads=["xin", "cw"], writes=["acc"])
            for (s0, s1) in [(0, CTX), (CTX, T)]:
                for j in (0, 1, 3, 4):
                    sh = j - 2
                    o0, o1 = max(s0, s0 - sh), min(s1, s1 - sh)
                    P.op("dve", lambda e, j=j, sh=sh, o0=o0, o1=o1: e.scalar_tensor_tensor(out=acc[:, o0:o1], in0=xin[:, o0 + sh:o1 + sh], scalar=cw[:, ti, j:j + 1], in1=acc[:, o0:o1], op0=ALU.mult, op1=ALU.add),
                         reads=["xin", "cw", "acc"], writes=["acc"])
            dst = qkv[ti]
            if ti >= 4:
                P.op("act", lambda e: e.activation(out=dst[:], in_=acc[:], func=AF.Silu), reads=["acc"], writes=[("qkv", ti)])
                return
            P.op("act", lambda e: e.activation(out=acc[:], in_=acc[:], func=AF.Silu), reads=["acc"], writes=["acc"])
            qs = 0.125 if ti < 2 else 1.0
            for (t0, n) in [(0, CTX)] + [(CTX + i * 512, 512) for i in range(4)]:
                norm_tile(dst, ti, t0, n, qs)

        def norm_tile(dst, ti, t0, n, qs):
            pp, pt = ps.get()
            bank_ap = banks[0]
            P.op("act", lambda e: e.activation(out=sq[:, :n], in_=acc[:, t0:t0 + n], func=AF.Square), reads=["acc"], writes=["sq"])
            P.op("pe", lambda e: e.matmul(psg[:, :n], lhsT=bd[:], rhs=sq[:, :n], start=True, stop=True), reads=["sq", "bd"], writes=["psg"])
            P.op("act", lambda e: e.activation(out=rs[:, :n], in_=psg[:, :n], func=AF.Sqrt, bias=epsb[:, 0:1], scale=64.0), reads=["psg", "epsb"], writes=["rs0", "rs"])
            P.op("dve", lambda e: e.reciprocal(out=rs[:, :n], in_=rs[:, :n]), reads=["rs0"], writes=["rs"])
            P.op("dve", lambda e: e.scalar_tensor_tensor(out=dst[:, t0:t0 + n], in0=acc[:, t0:t0 + n], scalar=qs, in1=rs[:, :n], op0=ALU.mult, op1=ALU.mult), reads=["acc", "rs"], writes=[("qkv", ti)])

        def prep_gates(b):
            src = k.TOK[b].rearrange("(c s) v -> s c v", s=64)[:, :, TK_A:TK_A + 16]
            for h2 in range(2):
                P.dma(lambda e, h2=h2: e.dma_start(out=ab[h2 * 64:(h2 + 1) * 64, :, :], in_=src), reads=[("TOK", b)], writes=["ab"])
            a3, b3 = ab[:, :, 0:8], ab[:, :, 8:16]
            bc8 = lambda t: t[:, :].rearrange("p (o h) -> p o h", o=1).to_broadcast([128, NCK, 8])
            P.op("dve", lambda e: e.tensor_tensor(out=t8[:], in0=a3, in1=bc8(dtb), op=ALU.add), reads=["ab", "dtb"], writes=["t8"])
            P.op("act", lambda e: e.activation(out=t8b[:], in_=t8[:], func=AF.Abs), reads=["t8"], writes=["t8b"])
            P.op("act", lambda e: e.activation(out=t8b[:], in_=t8b[:], func=AF.Exp, scale=-1.0), reads=["t8b"], writes=["t8b"])
            P.op("dve", lambda e: e.tensor_scalar(out=t8b[:], in0=t8b[:], scalar1=1.0, scalar2=None, op0=ALU.add), reads=["t8b"], writes=["t8b"])
            P.op("act", lambda e: e.activation(out=t8b[:], in_=t8b[:], func=AF.Ln), reads=["t8b"], writes=["t8b"])
            P.op("dve", lambda e: e.tensor_scalar(out=t8[:], in0=t8[:], scalar1=0.0, scalar2=None, op0=ALU.max), reads=["t8"], writes=["t8"])
            P.op("dve", lambda e: e.tensor_tensor(out=t8[:], in0=t8[:], in1=t8b[:], op=ALU.add), reads=["t8", "t8b"], writes=["t8"])
            P.op("dve", lambda e: e.tensor_tensor(out=gt[:], in0=t8[:], in1=bc8(alog), op=ALU.mult), reads=["t8", "alog"], writes=["gt"])
            P.op("act", lambda e: e.activation(out=bt[:], in_=b3, func=AF.Sigmoid), reads=["ab"], writes=["bt"])
            for d in range(2):
                P.op("pe", lambda e, d=d: e.matmul(psg[:, d * 144:(d + 1) * 144], lhsT=gm64[:, d, 0, :], rhs=gt[0:64, :, d * 4:(d + 1) * 4], start=True, stop=True), reads=["gm64", "gt"], writes=["psg"])
            P.op("dve", lambda e: e.tensor_copy(out=gcs[:].rearrange("p c (d h) -> p d c h", d=2), in_=psg[:, 0:288].rearrange("p (d c h) -> p d c h", d=2, h=4)), reads=["psg"], writes=["gcs"])
            for d in range(2):
                P.op("pe", lambda e, d=d: e.matmul(psg[:, d * 144:(d + 1) * 144], lhsT=ones64[:], rhs=gt[0:64, :, d * 4:(d + 1) * 4], start=True, stop=True), reads=["ones64", "gt"], writes=["psg"])
            P.op("dve", lambda e: e.tensor_copy(out=gts[:].rearrange("p c (d h) -> p d c h", d=2), in_=psg[:, 0:288].rearrange("p (d c h) -> p d c h", d=2, h=4)), reads=["psg"], writes=["gts"])
            for m in range(4):
                d, hp = m // 2, m % 2
                for h2 in range(2):
                    col = d * 4 + hp * 2 + h2
                    rr = slice(h2 * 64, (h2 + 1) * 64)
                    P.op("dve", lambda e, m=m, col=col, rr=rr: e.tensor_copy(out=gcBD[rr, :, m:m + 1], in_=gcs[rr, :, col:col + 1]), reads=["gcs"], writes=["gcBD"])
                    P.op("dve", lambda e, m=m, col=col, rr=rr: e.tensor_copy(out=gtBD[rr, :, m:m + 1], in_=gts[rr, :, col:col + 1]), reads=["gts"], writes=["gtBD"])
                    P.op("dve", lambda e, m=m, col=col, rr=rr: e.tensor_copy(out=bBD[rr, :, m:m + 1], in_=bt[rr, :, col:col + 1]), reads=["bt"], writes=["bBD"])
            P.op("act", lambda e: e.activation(out=eg[:], in_=gcBD[:], func=AF.Exp), reads=["gcBD"], writes=["eg"])
            P.op("act", lambda e: e.activation(out=glast[:], in_=gtBD[:], func=AF.Exp), reads=["gtBD"], writes=["glast"])
            P.op("dve", lambda e: e.tensor_tensor(out=ekd[:], in0=gtBD[:], in1=gcBD[:], op=ALU.subtract), reads=["gtBD", "gcBD"], writes=["ekd"])
            P.op("act", lambda e: e.activation(out=ekd[:], in_=ekd[:], func=AF.Exp), reads=["ekd"], writes=["ekd"])
            P.op("dve", lambda e: e.tensor_scalar(out=nbeta[:], in0=bBD[:], scalar1=-1.0, scalar2=None, op0=ALU.mult), reads=["bBD"], writes=["nbeta"])
            P.op("dve", lambda e: e.tensor_tensor(out=wsc[:], in0=bBD[:], in1=eg[:], op=ALU.mult), reads=["bBD", "eg"], writes=["wsc"])

        def phase_a_steps(m, c):
            d, hp = m // 2, m % 2
            qn, kn, vn = qkv[hp], qkv[2 + hp], qkv[4 + hp]
            qtk, ktk, vtk = ("qkv", hp), ("qkv", 2 + hp), ("qkv", 4 + hp)
            cs = slice(c * 64, (c + 1) * 64)
            hd0 = d * 4 + hp * 2
            X = {}

            def s1():
                G2, G2t = wk2.get()
                Gmt = Git = G2t
                Gi, Gm = G2[:, 0:128], G2[:, 128:256]
                g4 = gt[0:64, c, hd0:hd0 + 2].rearrange("p (a h o) -> p a h o", a=1, o=1).to_broadcast([64, 2, 2, 64])
                P.op("dve", lambda e: e.tensor_tensor(out=G2[0:64, :].rearrange("p (a h j) -> p a h j", a=2, h=2), in0=g4, in1=gm64[:, d, :, :].rearrange("p a (h j) -> p a h j", h=2), op=ALU.mult), reads=["gt", "gm64"], writes=[G2t])
                pDD, pDt = ps2.get()
                pDTt = pDt
                pD, pDT = pDD[:, 0:128], pDD[:, 128:256]
                P.op("pe", lambda e: e.matmul(pD, lhsT=gm64[:, d, 0, :], rhs=Gm[0:64, :], start=True, stop=True), reads=["gm64", Gmt], writes=[pDt], inc=False)
                P.op("pe", lambda e: e.matmul(pDT, lhsT=gm64[:, d, 1, :], rhs=Gi[0:64, :], start=True, stop=True), reads=["gm64", Git], writes=[pDTt])
                pKK, pKKt = ps.get()
                pQK, pQKt = ps.get()
                pTok, pTokt = ps.get()
                for h2 in range(2):
                    pb = h2 * 64
                    P.op("pe", lambda e, pb=pb: e.matmul(pKK[pb:pb + 64, pb:pb + 64], lhsT=kn[pb:pb + 64, cs], rhs=kn[pb:pb + 64, cs], start=True, stop=True), reads=[ktk], writes=[pKKt], inc=False)
                    P.op("pe", lambda e, pb=pb: e.matmul(pQK[pb:pb + 64, pb:pb + 64], lhsT=kn[pb:pb + 64, cs], rhs=qn[pb:pb + 64, cs], start=True, stop=True), reads=[ktk, qtk], writes=[pQKt], inc=False)
                    P.op("pe", lambda e, pb=pb: e.matmul(pTok[pb:pb + 64, 0:64], lhsT=kn[pb:pb + 64, cs], rhs=idn2[pb:pb + 64, :], start=True, stop=True), reads=[ktk, "idn2"], writes=[pTokt], inc=False)
                    P.op("pe", lambda e, pb=pb: e.matmul(pTok[pb:pb + 64, 64:128], lhsT=vn[pb:pb + 64, cs], rhs=idn2[pb:pb + 64, :], start=True, stop=True), reads=[vtk, "idn2"], writes=[pTokt], inc=(h2 == 1))
                X.update(pDD=pDD, pD=pD, pDt=pDt, pDT=pDT, pDTt=pDTt, pKK=pKK, pKKt=pKKt, pQK=pQK, pQKt=pQKt, pTok=pTok, pTokt=pTokt)

            def s2():
                x = dict(X)
                DD, Dt = wk2.get()
                DTt = Dt
                D, DT = DD[:, 0:128], DD[:, 128:256]
                P.op("act", lambda e: e.activation(out=DD, in_=x["pDD"], func=AF.Exp), reads=[x["pDt"]], writes=[Dt])
                P.op("dve", lambda e: e.tensor_tensor(out=DD.rearrange("p (a j) -> p a j", a=2), in0=DD.rearrange("p (a j) -> p a j", a=2), in1=gmbd[:, d, :, :], op=ALU.mult), reads=[Dt, "gmbd"], writes=[Dt])
                N, Nt = wk.get()
                X["n"] = X.get("n", 0) + 1
                if X["n"] <= k.opts.get("s2n", 99):
                    P.op("dve", lambda e: e.scalar_tensor_tensor(out=N, in0=x["pKK"], scalar=nbeta[:, c, m:m + 1], in1=D, op0=ALU.mult, op1=ALU.mult), reads=[x["pKKt"], "nbeta", Dt], writes=[Nt])
                X["n"] = X.get("n", 0) + 1
                if X["n"] <= k.opts.get("s2n", 99):
                    P.op("dve", lambda e: e.tensor_tensor(out=attnT_all[:, c, :], in0=x["pQK"], in1=DT, op=ALU.mult), reads=[x["pQKt"], DTt], writes=[("attnT", c)])
                rhs, rhst = wr.get()
                X["n"] = X.get("n", 0) + 1
                if X["n"] <= k.opts.get("s2n", 99):
                    P.op("dve", lambda e: e.tensor_scalar(out=rhs[:, 0:64], in0=x["pTok"][:, 0:64], scalar1=wsc[:, c, m:m + 1], scalar2=None, op0=ALU.mult), reads=[x["pTokt"], "wsc"], writes=[rhst])
                X["n"] = X.get("n", 0) + 1
                if X["n"] <= k.opts.get("s2n", 99):
                    P.op("dve", lambda e: e.tensor_scalar(out=rhs[:, 64:128], in0=x["pTok"][:, 64:128], scalar1=bBD[:, c, m:m + 1], scalar2=None, op0=ALU.mult), reads=[x["pTokt"], "bBD"], writes=[rhst])
                X["n"] = X.get("n", 0) + 1
                if X["n"] <= k.opts.get("s2n", 99):
                    P.op("dve", lambda e: e.tensor_scalar(out=kdec_all[:, c, :], in0=x["pTok"][:, 0:64], scalar1=ekd[:, c, m:m + 1], scalar2=None, op0=ALU.mult), reads=[x["pTokt"], "ekd"], writes=[("kdec", c)])
                X.update(N=N, Nt=Nt, rhs=rhs, rhst=rhst)

            def s3():
                x = dict(X)
                pNT, pNTt = ps.get()
                P.op("pe", lambda e: e.transpose(pNT, x["N"], k.idn[:]), reads=[x["Nt"], "idn"], writes=[pNTt])
                X.update(pNT=pNT, pNTt=pNTt)

            def s4():
                x = dict(X)
                PT_, PTt = wk.get()
                XT, XTt = wk.get()
                P.op("dve", lambda e: e.tensor_copy(out=PT_, in_=x["pNT"]), reads=[x["pNTt"]], writes=[PTt])
                P.op("dve", lambda e: e.tensor_tensor(out=XT, in0=x["pNT"], in1=k.idn[:], op=ALU.add), reads=[x["pNTt"], "idn"], writes=[XTt])
                X.update(P=x["N"], Pt=x["Nt"], PT=PT_, PTt=PTt, XT=XT, XTt=XTt)

            def lvl_mm(kk):
                def f():
                    x = dict(X)
                    pPP, pPt = ps2.get()
                    pP, pPT = pPP[:, 0:128], pPP[:, 128:256]
                    P.op("pe", lambda e: e.matmul(pP, lhsT=x["PT"], rhs=x["P"], start=True, stop=True), reads=[x["PTt"], x["Pt"]], writes=[pPt], inc=(kk == 5))
                    if kk < 5:
                        P.op("pe", lambda e: e.matmul(pPT, lhsT=x["P"], rhs=x["PT"], start=True, stop=True), reads=[x["PTt"], x["Pt"]], writes=[pPt])
                    X.update(pPP=pPP, pPt=pPt)
                return f

            def lvl_ev(kk):
                def f():
                    x = dict(X)
                    nPP, nPt = wk2.get()
                    nP, nPT = nPP[:, 0:128], nPP[:, 128:256]
                    w_ = 256 if kk < 5 else 128
                    P.op("dve", lambda e: e.tensor_copy(out=nPP[:, 0:w_], in_=x["pPP"][:, 0:w_]), reads=[x["pPt"]], writes=[nPt])
                    X.update(P=nP, Pt=nPt, PT=nPT, PTt=nPt)
                    pX, pXt = ps.get()
                    P.op("pe", lambda e: e.matmul(pX, lhsT=nP, rhs=x["XT"], start=True, stop=True), reads=[nPt, x["XTt"]], writes=[pXt])
                    X.update(pX=pX, pXt=pXt)
                return f

            def lvl_acc(kk):
                def f():
                    x = dict(X)
                    nX, nXt = wk.get()
                    P.op("dve", lambda e: e.tensor_tensor(out=nX, in0=x["XT"], in1=x["pX"], op=ALU.add), reads=[x["XTt"], x["pXt"]], writes=[nXt])
                    X.update(XT=nX, XTt=nXt)
                return f

            def s_sol():
                x = dict(X)
                pU, pUt = ps.get()
                pW, pWt = ps.get()
                P.op("pe", lambda e: e.matmul(pU[:, 0:64], lhsT=x["XT"], rhs=x["rhs"][:, 64:128], start=True, stop=True), reads=[x["XTt"], x["rhst"]], writes=[pUt], inc=False)
                for h2 in range(2):
                    pb = h2 * 64
                    P.op("pe", lambda e, pb=pb: e.matmul(pW[pb:pb + 64, 0:64], lhsT=x["rhs"][pb:pb + 64, 0:64], rhs=x["XT"][pb:pb + 64, pb:pb + 64], start=True, stop=True), reads=[x["XTt"], x["rhst"]], writes=[pWt], inc=(h2 == 1))
                X.update(pU=pU, pUt=pUt, pW=pW, pWt=pWt)

            def s_solev():
                x = dict(X)
                P.op("dve", lambda e: e.tensor_copy(out=u_all[:, c, :], in_=x["pU"][:, 0:64]), reads=[x["pUt"]], writes=[("u", c)])
                P.op("dve", lambda e: e.tensor_copy(out=wT_all[:, c, :], in_=x["pW"][:, 0:64]), reads=[x["pWt"]], writes=[("wT", c)])

            steps = [s1, s2, s3, s4]
            for kk in range(1, 6):
                steps += [lvl_mm(kk), lvl_ev(kk), lvl_acc(kk)]
            steps += [s_sol, s_solev]
            return steps[:k.opts.get("dn_steps", 99)]

        def phase_b_chunk(m, c, first_dir):
            d, hp = m // 2, m % 2
            qn = qkv[hp]
            cs = slice(c * 64, (c + 1) * 64)
            p1, p1t = ps.get()
            p2, p2t = ps.get()
            for h2 in range(2):
                pb = h2 * 64
                P.op("pe", lambda e, pb=pb: e.matmul(p1[pb:pb + 64, 0:64], lhsT=wT_all[pb:pb + 64, c, :], rhs=S[pb:pb + 64, :], start=True, stop=True), reads=[("wT", c), "S"], writes=[p1t], inc=False)
                P.op("pe", lambda e, pb=pb: e.matmul(p2[pb:pb + 64, 0:64], lhsT=qn[pb:pb + 64, cs], rhs=S[pb:pb + 64, :], start=True, stop=True), reads=[("qkv", hp), "S"], writes=[p2t], inc=(h2 == 1))
            vn_, vnt = wv.get()
            P.op("dve", lambda e: e.tensor_tensor(out=vn_, in0=u_all[:, c, :], in1=p1[:, 0:64], op=ALU.subtract), reads=[("u", c), p1t], writes=[vnt])
            p3, p3t = ps.get()
            p4, p4t = ps.get()
            P.op("pe", lambda e: e.matmul(p3[:, 0:64], lhsT=attnT_all[:, c, :], rhs=vn_, start=True, stop=True), reads=[("attnT", c), vnt], writes=[p3t], inc=False)
            for h2 in range(2):
                pb = h2 * 64
                P.op("pe", lambda e, pb=pb: e.matmul(p4[pb:pb + 64, 0:64], lhsT=kdec_all[pb:pb + 64, c, :], rhs=vn_[pb:pb + 64, :], start=True, stop=True), reads=[("kdec", c), vnt], writes=[p4t], inc=(h2 == 1))
            t_, tt_ = wv.get()
            P.op("dve", lambda e: e.tensor_scalar(out=t_, in0=p2[:, 0:64], scalar1=eg[:, c, m:m + 1], scalar2=None, op0=ALU.mult), reads=[p2t, "eg"], writes=[tt_])
            if first_dir:
                P.op("dve", lambda e: e.tensor_tensor(out=O_tok[:, c, :], in0=t_, in1=p3[:, 0:64], op=ALU.add), reads=[tt_, p3t], writes=[("O", c)])
            else:
                P.op("dve", lambda e: e.tensor_tensor(out=t_, in0=t_, in1=p3[:, 0:64], op=ALU.add), reads=[tt_, p3t], writes=[tt_])
                P.op("dve", lambda e: e.tensor_tensor(out=O_tok[:, c, :], in0=O_tok[:, c, :], in1=t_, op=ALU.add), reads=[tt_, ("O", c)], writes=[("O", c)])
            P.op("dve", lambda e: e.scalar_tensor_tensor(out=S[:], in0=S[:], scalar=glast[:, c, m:m + 1], in1=p4[:, 0:64], op0=ALU.mult, op1=ALU.add), reads=["S", "glast", p4t], writes=["S"])

        def out_phase(b, hp):
            z, yfm = xin, acc
            r0 = FM_Z + hp * 128
            P.dma(lambda e: e.dma_start(out=z[:], in_=k.PT[b, r0:r0 + 128, :]), reads=[("PT", b)], writes=["xin"])
            P.op("act", lambda e: e.activation(out=z[:], in_=z[:], func=AF.Silu), reads=["xin"], writes=["xin"])
            allO = [("O", c) for c in range(NCK)]
            P.op("dve", lambda e: e.tensor_tensor(out=u_all[:], in0=O_tok[:], in1=O_tok[:], op=ALU.mult), reads=allO, writes=[("u", c) for c in range(NCK)])
            P.op("dve", lambda e: e.tensor_reduce(out=ssq[:], in_=u_all[:], axis=AX.X, op=ALU.add), reads=[("u", c) for c in range(NCK)], writes=["ssq"])
            P.op("act", lambda e: e.activation(out=ssq[:], in_=ssq[:], func=AF.Sqrt, bias=epsb[:, 0:1], scale=1.0 / 64), reads=["ssq", "epsb"], writes=["ssq"])
            P.op("dve", lambda e: e.reciprocal(out=ssq[:], in_=ssq[:]), reads=["ssq"], writes=["ssq"])
            P.op("dve", lambda e: e.tensor_tensor(out=O_tok[:], in0=O_tok[:], in1=ssq[:].rearrange("p (c o) -> p c o", o=1).to_broadcast([128, NCK, 64]), op=ALU.mult), reads=allO + ["ssq"], writes=allO)
            P.op("dve", lambda e: e.tensor_tensor(out=O_tok[:], in0=O_tok[:], in1=gno[:].rearrange("p (o v) -> p o v", o=1).to_broadcast([128, NCK, 64]), op=ALU.mult), reads=allO + ["gno"], writes=allO)
            for c in range(NCK):
                out_chunk(c)
            for (t0, n) in [(0, CTX)] + [(CTX + i * 512, 512) for i in range(4)]:
                out_tile(b, hp, t0, n)

        def out_chunk(c):
            pp, ppt = ps.get()
            for h2 in range(2):
                pb = h2 * 64
                P.op("pe", lambda e, pb=pb: e.matmul(pp[pb:pb + 64, 0:64], lhsT=O_tok[pb:pb + 64, c, :], rhs=idn2[pb:pb + 64, :], start=True, stop=True), reads=[("O", c), "idn2"], writes=[ppt], inc=(h2 == 1))
            P.op("dve", lambda e: e.tensor_copy(out=acc[:, c * 64:(c + 1) * 64], in_=pp[:, 0:64]), reads=[ppt], writes=["acc"])

        def out_tile(b, hp, t0, n):
            yi = cnt["y"] % 2
            cnt["y"] += 1
            P.op("dve", lambda e: e.tensor_tensor(out=yb[yi][:, :n], in0=acc[:, t0:t0 + n], in1=xin[:, t0:t0 + n], op=ALU.mult), reads=["acc", "xin"], writes=[("yb", yi)])
            P.dma(lambda e: e.dma_start(out=k.YT[b, 0, hp * 128:(hp + 1) * 128, t0:t0 + n], in_=yb[yi][:, :n]), reads=[("yb", yi)], writes=[("YT", b)])

        G = 3
        for b in range(NB):
            for ti in range(6):
                prep_tile(b, ti)
            prep_gates(b)
            lim = k.opts.get("dn_lim", 99)
            if lim == 0:
                continue
            for hp in range(2):
                for d in range(2):
                    m = d * 2 + hp
                    for c0 in range(0, NCK if lim >= 2 else G, G):
                        lists = [phase_a_steps(m, c) for c in range(c0, min(NCK, c0 + G))]
                        for si in range(len(lists[0])):
                            for lst in lists:
                                lst[si]()
                    if lim < 3:
                        continue
                    P.op("dve", lambda e: e.memset(S[:], 0.0), writes=["S"])
                    order = list(range(NCK)) if d == 0 else [3, 2, 1, 0] + list(range(NCK - 1, 3, -1))
                    for c in order:
                        phase_b_chunk(m, c, d == 0)
                    if "dump" in k.opts and (b, m) == k.opts["dump"][:2]:
                        for j in range(6):
                            P.dma(lambda e, j=j: e.dma_start(out=k.dump[j], in_=qkv[j][:]), reads=[("qkv", j)], writes=[("dump", j)])
                        for j, (tl, tk) in enumerate([(u_all, "u"), (wT_all, "wT"), (kdec_all, "kdec"), (O_tok, "O")]):
                            P.dma(lambda e, j=j, tl=tl: e.dma_start(out=k.dump[6 + j], in_=tl[:].rearrange("p c v -> p (c v)")), reads=[(tk, c) for c in range(NCK)], writes=[("dump", 6 + j)])
                        P.dma(lambda e: e.dma_start(out=k.dump[10:12].rearrange("a p t -> p a t"), in_=attnT_all[:].rearrange("p (a c) v -> p a (c v)", a=2)), reads=[("attnT", c) for c in range(NCK)], writes=[("dump", 10)])
                        for j, (tl, tk, w) in enumerate([(gt, "gt", 288), (bt, "bt", 288), (gcBD, "gcBD", 144), (gtBD, "gtBD", 144), (bBD, "bBD", 144)]):
                            P.dma(lambda e, j=j, tl=tl, w=w: e.dma_start(out=k.dump[12, :, j * 300:j * 300 + w], in_=tl[:].rearrange("p c v -> p (c v)")), reads=[tk], writes=[("dump", 12, j)])
                if lim >= 4:
                    out_phase(b, hp)
        P.end_stage()


def stage_s5(k, l):
    nc, P, NB = k.nc, k.P, k.NB
    HALF_PI = float(np.pi / 2)
    P.serial = k.opts.get("serial_s5", SERIAL_S5)
    with ExitStack() as es:
        al = lambda name, shape, dt=F32: es.enter_context(nc.sbuf_tensor("s_" + name, list(shape), dt))
        lre = al("lre", [128, 16]); lim = al("lim", [128, 16]); stp = al("stp", [128, 16]); mag = al("mag", [128, 16])
        cth = al("cth", [128, 16]); sth = al("sth", [128, 16]); t1 = al("t1", [128, 16]); t2 = al("t2", [128, 16]); t3 = al("t3", [128, 16])
        are = al("are", [128, 16]); aim = al("aim", [128, 16]); cfr = al("cfr", [128, 16]); cfi = al("cfi", [128, 16]); hpi = al("hpi", [128, 1])
        pwc = al("pwc", [128, 16, 12]); pws = al("pws", [128, 16, 12])
        bre = al("bre", [128, 8, 16]); bim = al("bim", [128, 8, 16]); cre = al("cre", [128, 8, 16]); cim = al("cim", [128, 8, 16])
        bb_all = al("bb_all", [128, 32, 16]); bd_all = al("bd_all", [128, 32, 32]); tb = [al("tb%d" % i, [128, 8, 16]) for i in range(4)]
        W_all = al("W_all", [32, 32, 128], BF16); cw_all = al("cw_all", [128, 8, 2, 128], BF16)
        dsk = al("dsk", [128, 2]); wgl = al("wgl", [128, 2, 512], BF16)
        cs = [al("cs%d" % i, [128, T]) for i in range(2)]; sn = [al("sn%d" % i, [128, T]) for i in range(2)]
        xr = al("xr", [128, T]); xi = al("xi", [128, T]); gr = al("gr", [128, T]); gi = al("gi", [128, T])
        u32 = al("u32", [32, T]); u32b = al("u32b", [32, T], BF16); hrb = al("hrb", [128, T], BF16); hib = al("hib", [128, T], BF16); Y = [al("Y%d" % i, [128, T]) for i in range(2)]
        mt = Slots([al("mt%d" % i, [128, 512])[:] for i in range(6)], "mt")
        gel = al("gel", [128, 2, 512], BF16); sg = [al("sg%d" % i, [128, 512]) for i in range(2)]; yb = [al("yb%d" % i, [128, 512], BF16) for i in range(2)]
        banks = [es.enter_context(nc.psum_tensor("s_ps%d" % i, [128, 512], F32)) for i in range(8)]
        psl = Slots([banks[i][:] for i in range(8)], "psb")
        ld = lambda dst, src, tok: P.dma(lambda e: e.dma_start(out=dst, in_=src), writes=[tok])
        ld(lre[:], k.s5_lam_re[l], "lre"); ld(lim[:], k.s5_lam_im[l], "lim"); ld(stp[:], k.s5_log_step[l], "stp")
        ld(bre[:], k.s5_b_re[l], "bre"); ld(bim[:], k.s5_b_im[l], "bim"); ld(cre[:], k.s5_c_re[l], "cre"); ld(cim[:], k.s5_c_im[l], "cim")
        ld(dsk[:], k.s5_d[l], "dsk")
        P.dma(lambda e: e.dma_start(out=wgl[:], in_=k.s5_glu[l].rearrange("(c p) n -> p c n", p=128)), writes=["wgl"], q="pool")
        P.op("dve", lambda e: e.memset(hpi[:], HALF_PI), writes=["hpi"])
        P.op("dve", lambda e: e.memset(bd_all[:], 0.0), writes=["bd_all"])
        P.op("dve", lambda e: e.memset(cw_all[:], 0.0), writes=["cw_all"])
        tt = lambda out, a, b_, op, rd, wr, eng="dve": P.op(eng, lambda e: e.tensor_tensor(out=out, in0=a, in1=b_, op=op), reads=rd, writes=wr)
        P.op("act", lambda e: e.activation(out=stp[:], in_=stp[:], func=AF.Exp), reads=["stp"], writes=["stp"])
        tt(t1[:], lre[:], stp[:], ALU.mult, ["lre", "stp"], ["t1"])
        P.op("act", lambda e: e.activation(out=mag[:], in_=t1[:], func=AF.Exp), reads=["t1"], writes=["mag"])
        tt(t2[:], lim[:], stp[:], ALU.mult, ["lim", "stp"], ["t2"])
        P.op("act", lambda e: e.activation(out=sth[:], in_=t2[:], func=AF.Sin, scale=1.0 / 32), reads=["t2"], writes=["sth"])
        P.op("act", lambda e: e.activation(out=cth[:], in_=t2[:], func=AF.Sin, scale=1.0 / 32, bias=hpi[:, 0:1]), reads=["t2", "hpi"], writes=["cth"])
        for it in range(5):
            tt(t1[:], cth[:], cth[:], ALU.mult, ["cth"], ["t1"])
            tt(t3[:], sth[:], sth[:], ALU.mult, ["sth"], ["t3"])
            P.op("dve", lambda e: e.scalar_tensor_tensor(out=sth[:], in0=cth[:], scalar=2.0, in1=sth[:], op0=ALU.mult, op1=ALU.mult), reads=["cth", "sth"], writes=["sth"])
            tt(cth[:], t1[:], t3[:], ALU.subtract, ["t1", "t3"], ["cth"])
        tt(are[:], mag[:], cth[:], ALU.mult, ["mag", "cth"], ["are"])
        tt(aim[:], mag[:], sth[:], ALU.mult, ["mag", "sth"], ["aim"])
        tt(t1[:], lre[:], lre[:], ALU.mult, ["lre"], ["t1"])
        tt(t3[:], lim[:], lim[:], ALU.mult, ["lim"], ["t3"])
        tt(t1[:], t1[:], t3[:], ALU.add, ["t1", "t3"], ["t1"])
        P.op("dve", lambda e: e.reciprocal(out=t1[:], in_=t1[:]), reads=["t1"], writes=["t1"])
        P.op("dve", lambda e: e.tensor_scalar(out=t2[:], in0=are[:], scalar1=-1.0, scalar2=None, op0=ALU.add), reads=["are"], writes=["t2"])
        tt(cfr[:], t2[:], lre[:], ALU.mult, ["t2", "lre"], ["cfr"])
        tt(t3[:], aim[:], lim[:], ALU.mult, ["aim", "lim"], ["t3"])
        tt(cfr[:], cfr[:], t3[:], ALU.add, ["cfr", "t3"], ["cfr"])
        tt(cfr[:], cfr[:], t1[:], ALU.mult, ["cfr", "t1"], ["cfr"])
        tt(cfi[:], aim[:], lre[:], ALU.mult, ["aim", "lre"], ["cfi"])
        tt(t3[:], t2[:], lim[:], ALU.mult, ["t2", "lim"], ["t3"])
        tt(cfi[:], cfi[:], t3[:], ALU.subtract, ["cfi", "t3"], ["cfi"])
        tt(cfi[:], cfi[:], t1[:], ALU.mult, ["cfi", "t1"], ["cfi"])
        bb4 = bb_all[:].rearrange("p (d c r) h -> p d c r h", d=2, r=2)
        for d in range(2):
            bc = lambda t_: t_[:, d * 8:(d + 1) * 8].rearrange("p (c o) -> p c o", o=1).to_broadcast([128, 8, 16])
            tt(tb[0][:], bre[:], bc(cfr), ALU.mult, ["bre", "cfr"], [("tb", 0)])
            tt(tb[1][:], bim[:], bc(cfi), ALU.mult, ["bim", "cfi"], [("tb", 1)])
            tt(bb4[:, d, :, 0, :], tb[0][:], tb[1][:], ALU.subtract, [("tb", 0), ("tb", 1)], ["bb_all"])
            tt(tb[2][:], bim[:], bc(cfr), ALU.mult, ["bim", "cfr"], [("tb", 2)])
            tt(tb[3][:], bre[:], bc(cfi), ALU.mult, ["bre", "cfi"], [("tb", 3)])
            tt(bb4[:, d, :, 1, :], tb[2][:], tb[3][:], ALU.add, [("tb", 2), ("tb", 3)], ["bb_all"])
        for g2 in range(2):
            rr_ = slice(g2 * 64, (g2 + 1) * 64)
            P.op("dve", lambda e, rr_=rr_, g2=g2: e.tensor_copy(out=bd_all[rr_, :, g2 * 16:(g2 + 1) * 16], in_=bb_all[rr_, :, :]), reads=["bb_all", "bd_all"], writes=["bd_all"])

        def w_build(j):
            pw, pwt = psl.get()
            P.op("pe", lambda e: e.transpose(pw[0:32, 0:128], bd_all[:, j, :], k.idn[:]), reads=["bd_all", "idn"], writes=[pwt])
            P.op("act", lambda e: e.activation(out=W_all[:, j, :], in_=pw[0:32, 0:128], func=AF.Identity), reads=[pwt], writes=["W_all"])
        for j in range(32):
            w_build(j)

        def cw_build(ct, g2):
            q = ct % 4
            rr_ = slice(g2 * 64, (g2 + 1) * 64)
            c0 = q * 32 + g2 * 16
            P.op("dve", lambda e: e.tensor_copy(out=cw_all[rr_, ct, 0, c0:c0 + 16], in_=cre[rr_, ct, :]), reads=["cre", "cw_all"], writes=["cw_all"])
            P.op("dve", lambda e: e.tensor_scalar(out=cw_all[rr_, ct, 1, c0:c0 + 16], in0=cim[rr_, ct, :], scalar1=-1.0, scalar2=None, op0=ALU.mult), reads=["cim", "cw_all"], writes=["cw_all"])
        for ct in range(8):
            for g2 in range(2):
                cw_build(ct, g2)
        P.op("dve", lambda e: e.tensor_copy(out=pwc[:, :, 0], in_=cth[:]), reads=["cth"], writes=["pwc"])
        P.op("dve", lambda e: e.tensor_copy(out=pws[:, :, 0], in_=sth[:]), reads=["sth"], writes=["pws"])

        def pw_level(kk):
            c_, s_ = pwc[:, :, kk - 1], pws[:, :, kk - 1]
            tt(t1[:], c_, c_, ALU.mult, ["pwc"], ["t1"])
            tt(t3[:], s_, s_, ALU.mult, ["pws"], ["t3"])
            tt(pwc[:, :, kk], t1[:], t3[:], ALU.subtract, ["t1", "t3", "pwc"], ["pwc"])
            P.op("dve", lambda e: e.scalar_tensor_tensor(out=pws[:, :, kk], in0=c_, scalar=2.0, in1=s_, op0=ALU.mult, op1=ALU.mult), reads=["pwc", "pws"], writes=["pws"])
        for kk in range(1, 12):
            pw_level(kk)

        def table_gen(j):
            i = j % 2
            C, S_ = cs[i], sn[i]
            ctk, stk = ("cs", i), ("sn", i)
            P.op("dve", lambda e: e.memset(C[:, 0:1], 1.0), writes=[ctk])
            P.op("dve", lambda e: e.memset(S_[:, 0:1], 0.0), writes=[stk])
            for kk in range(12):
                ln = 1 << kk
                nn = min(ln, T - ln)
                if nn <= 0:
                    break
                pc, ps_ = pwc[:, j, kk:kk + 1], pws[:, j, kk:kk + 1]
                lvl(C, S_, ctk, stk, ln, nn, pc, ps_)
            P.dma(lambda e: e.dma_start(out=k.S5TAB[j, 0], in_=C[:]), reads=[ctk], writes=[("TAB", j)])
            P.dma(lambda e: e.dma_start(out=k.S5TAB[j, 1], in_=S_[:]), reads=[stk], writes=[("TAB", j)])

        def lvl(C, S_, ctk, stk, ln, nn, pc, ps_):
            m1, m1t = mt.get()
            m2, m2t = mt.get()
            w_ = min(nn, 512)
            for o in range(0, nn, 512):
                w = min(512, nn - o)
                sub(C, S_, ctk, stk, ln, o, w, pc, ps_)

        def sub(C, S_, ctk, stk, ln, o, w, pc, ps_):
            m1, m1t = mt.get()
            m2, m2t = mt.get()
            P.op("dve", lambda e: e.tensor_scalar(out=m1[:, :w], in0=S_[:, o:o + w], scalar1=ps_, scalar2=None, op0=ALU.mult), reads=[stk, "pws"], writes=[m1t])
            P.op("dve", lambda e: e.tensor_scalar(out=m2[:, :w], in0=S_[:, o:o + w], scalar1=pc, scalar2=None, op0=ALU.mult), reads=[stk, "pwc"], writes=[m2t])
            P.op("dve", lambda e: e.scalar_tensor_tensor(out=C[:, ln + o:ln + o + w], in0=C[:, o:o + w], scalar=pc, in1=m1[:, :w], op0=ALU.mult, op1=ALU.subtract), reads=[ctk, m1t, "pwc"], writes=[ctk])
            P.op("dve", lambda e: e.scalar_tensor_tensor(out=S_[:, ln + o:ln + o + w], in0=C[:, o:o + w], scalar=ps_, in1=m2[:, :w], op0=ALU.mult, op1=ALU.add), reads=[ctk, m2t, "pws"], writes=[stk])
        for j in range(16):
            table_gen(j)

        blocks = [(0, CTX)] + [(CTX + i * 512, 512) for i in range(4)]

        def tabview(tab, d, t0, n):
            if d == 0:
                return tab[:, t0:t0 + n]
            if t0 < CTX:
                lo = CTX - 1 - (t0 + n - 1)
            else:
                lo = CTX + (T - 1 - (t0 + n - 1))
            return tab[:, lo:lo + n][:, ::-1]

        def do_block_in(ct, d, i, t0, n):
            j = d * 8 + ct
            pr, prt = psl.get()
            pi_, pit = psl.get()
            P.op("pe", lambda e: e.matmul(pr[:, :n], lhsT=W_all[:, j * 2, :], rhs=u32b[:, t0:t0 + n], start=True, stop=True), reads=["W_all", "u32b"], writes=[prt], inc=False)
            P.op("pe", lambda e: e.matmul(pi_[:, :n], lhsT=W_all[:, j * 2 + 1, :], rhs=u32b[:, t0:t0 + n], start=True, stop=True), reads=["W_all", "u32b"], writes=[pit])
            cv, sv = tabview(cs[i], d, t0, n), tabview(sn[i], d, t0, n)
            ms = [mt.get() for _ in range(4)]
            tt(ms[0][0][:, :n], pr[:, :n], cv, ALU.mult, [prt, ("cs", i)], [ms[0][1]])
            tt(ms[1][0][:, :n], pi_[:, :n], sv, ALU.mult, [pit, ("sn", i)], [ms[1][1]])
            tt(xr[:, t0:t0 + n], ms[0][0][:, :n], ms[1][0][:, :n], ALU.add, [ms[0][1], ms[1][1]], ["xr"])
            tt(ms[2][0][:, :n], pi_[:, :n], cv, ALU.mult, [pit, ("cs", i)], [ms[2][1]])
            tt(ms[3][0][:, :n], pr[:, :n], sv, ALU.mult, [prt, ("sn", i)], [ms[3][1]])
            tt(xi[:, t0:t0 + n], ms[2][0][:, :n], ms[3][0][:, :n], ALU.subtract, [ms[2][1], ms[3][1]], ["xi"])

        def do_scan(ct, d):
            j = d * 8 + ct
            for (src, dst, stok, dtok) in ((xr, gr, "xr", "gr"), (xi, gi, "xi", "gi")):
                scan1(j, d, src, dst, stok, dtok)

        def scan1(j, d, src, dst, stok, dtok):
            rb = lambda n: mag[:, j:j + 1].to_broadcast([128, n])
            if d == 0:
                P.op("dve", lambda e: e.tensor_tensor_scan(out=dst[:], data0=rb(T), data1=src[:], initial=0.0, op0=ALU.mult, op1=ALU.add), reads=[stok, "mag"], writes=[dtok])
            else:
                P.op("dve", lambda e: e.tensor_tensor_scan(out=dst[:, 0:CTX][:, ::-1], data0=rb(CTX), data1=src[:, 0:CTX][:, ::-1], initial=0.0, op0=ALU.mult, op1=ALU.add), reads=[stok, "mag"], writes=[dtok])
                P.op("dve", lambda e: e.tensor_tensor_scan(out=dst[:, CTX:T][:, ::-1], data0=rb(SEQ), data1=src[:, CTX:T][:, ::-1], initial=dst[:, 0:1], op0=ALU.mult, op1=ALU.add), reads=[stok, "mag", dtok], writes=[dtok])

        def do_block_out(ct, d, i, t0, n, first):
            cv, sv = tabview(cs[i], d, t0, n), tabview(sn[i], d, t0, n)
            ms = [mt.get() for _ in range(4)]
            tt(ms[0][0][:, :n], gr[:, t0:t0 + n], cv, ALU.mult, ["gr", ("cs", i)], [ms[0][1]])
            tt(ms[1][0][:, :n], gi[:, t0:t0 + n], sv, ALU.mult, ["gi", ("sn", i)], [ms[1][1]])
            tt(hrb[:, t0:t0 + n], ms[0][0][:, :n], ms[1][0][:, :n], ALU.subtract, [ms[0][1], ms[1][1]], ["hrb"])
            tt(ms[2][0][:, :n], gr[:, t0:t0 + n], sv, ALU.mult, ["gr", ("sn", i)], [ms[2][1]])
            tt(ms[3][0][:, :n], gi[:, t0:t0 + n], cv, ALU.mult, ["gi", ("cs", i)], [ms[3][1]])
            tt(hib[:, t0:t0 + n], ms[2][0][:, :n], ms[3][0][:, :n], ALU.add, [ms[2][1], ms[3][1]], ["hib"])
            py, pyt = psl.get()
            P.op("pe", lambda e: e.matmul(py[:, :n], lhsT=cw_all[:, ct, 0, :], rhs=hrb[:, t0:t0 + n], start=True, stop=False), reads=["cw_all", "hrb"], writes=[pyt], inc=False)
            P.op("pe", lambda e: e.matmul(py[:, :n], lhsT=cw_all[:, ct, 1, :], rhs=hib[:, t0:t0 + n], start=False, stop=True), reads=["cw_all", "hib"], writes=[pyt])
            yt_ = Y[ct // 4]
            ytk = ("Y", ct // 4)
            if first:
                P.op("dve", lambda e: e.tensor_copy(out=yt_[:, t0:t0 + n], in_=py[:, :n]), reads=[pyt], writes=[ytk])
            else:
                tt(yt_[:, t0:t0 + n], yt_[:, t0:t0 + n], py[:, :n], ALU.add, [pyt, ytk], [ytk])

        def do_ct_dir(b, ct, d):
            j = d * 8 + ct
            i = j % 2
            P.dma(lambda e: e.dma_start(out=cs[i][:], in_=k.S5TAB[j, 0]), reads=[("TAB", j)], writes=[("cs", i)])
            P.dma(lambda e: e.dma_start(out=sn[i][:], in_=k.S5TAB[j, 1]), reads=[("TAB", j)], writes=[("sn", i)])
            for (t0, n) in blocks:
                do_block_in(ct, d, i, t0, n)
            do_scan(ct, d)
            for (t0, n) in blocks:
                do_block_out(ct, d, i, t0, n, (ct % 4 == 0 and d == 0))

        def do_ct(b, ct):
            r0 = FM_U + ct * 32
            P.dma(lambda e: e.dma_start(out=u32[:], in_=k.PT[b, r0:r0 + 32, :]), reads=[("PT", b)], writes=["u32"])
            P.op("dve", lambda e: e.tensor_copy(out=u32b[:], in_=u32[:]), reads=["u32"], writes=["u32b"])
            for d in range(2):
                do_ct_dir(b, ct, d)

        def out_tile(b, t0, n):
            for yt in range(2):
                ub, ubt = mt.get()
                r0 = FM_U + yt * 128
                P.dma(lambda e, ub=ub, r0=r0: e.dma_start(out=ub[:, :n], in_=k.PT[b, r0:r0 + 128, t0:t0 + n]), reads=[("PT", b)], writes=[ubt])
                yv, yvt = mt.get()
                P.op("dve", lambda e, ub=ub, yv=yv, yt=yt: e.scalar_tensor_tensor(out=yv[:, :n], in0=ub[:, :n], scalar=dsk[:, yt:yt + 1], in1=Y[yt][:, t0:t0 + n], op0=ALU.mult, op1=ALU.add), reads=[ubt, "dsk", ("Y", yt)], writes=[yvt])
                x2, x2t = mt.get()
                P.op("act", lambda e, yv=yv, x2=x2: e.activation(out=x2[:, :n], in_=yv[:, :n], func=AF.Square), reads=[yvt], writes=[x2t])
                P.op("dve", lambda e, x2=x2: e.tensor_scalar(out=x2[:, :n], in0=x2[:, :n], scalar1=0.044715, scalar2=1.0, op0=ALU.mult, op1=ALU.add), reads=[x2t], writes=[x2t])
                P.op("dve", lambda e, x2=x2, yv=yv: e.tensor_tensor(out=x2[:, :n], in0=x2[:, :n], in1=yv[:, :n], op=ALU.mult), reads=[x2t, yvt], writes=[x2t])
                P.op("act", lambda e, x2=x2: e.activation(out=x2[:, :n], in_=x2[:, :n], func=AF.Tanh, scale=0.7978845608028654), reads=[x2t], writes=[x2t])
                P.op("dve", lambda e, x2=x2, yv=yv, yt=yt: e.scalar_tensor_tensor(out=gel[:, yt, :n], in0=x2[:, :n], scalar=1.0, in1=yv[:, :n], op0=ALU.add, op1=ALU.mult), reads=[x2t, yvt], writes=[("gel", yt)])
            for oc in range(2):
                pa, pat = psl.get()
                pg, pgt = psl.get()
                for kc in range(2):
                    P.op("pe", lambda e, oc=oc, kc=kc, pa=pa: e.matmul(pa[:, :n], lhsT=wgl[:, kc, oc * 128:(oc + 1) * 128], rhs=gel[:, kc, :n], start=(kc == 0), stop=(kc == 1)), reads=["wgl", ("gel", kc)], writes=[pat])
                for kc in range(2):
                    P.op("pe", lambda e, oc=oc, kc=kc, pg=pg: e.matmul(pg[:, :n], lhsT=wgl[:, kc, 256 + oc * 128:256 + (oc + 1) * 128], rhs=gel[:, kc, :n], start=(kc == 0), stop=(kc == 1)), reads=["wgl", ("gel", kc)], writes=[pgt])
                P.op("act", lambda e, oc=oc, pg=pg: e.activation(out=sg[oc][:, :n], in_=pg[:, :n], func=AF.Sigmoid, scale=0.5), reads=[pgt], writes=[("sg", oc)])
                P.op("dve", lambda e, oc=oc, pa=pa: e.scalar_tensor_tensor(out=yb[oc][:, :n], in0=pa[:, :n], scalar=0.5, in1=sg[oc][:, :n], op0=ALU.mult, op1=ALU.mult), reads=[pat, ("sg", oc)], writes=[("yb", oc)])
                P.dma(lambda e, oc=oc: e.dma_start(out=k.YT[b, 1, oc * 128:(oc + 1) * 128, t0:t0 + n], in_=yb[oc][:, :n]), reads=[("yb", oc)], writes=[("YT", b)])

        for b in range(NB):
            for ct in range(8):
                do_ct(b, ct)
            for (t0, n) in blocks:
                out_tile(b, t0, n)
        P.end_stage()


def stage_mixers(k, l):
    sel = k.opts.get("mixers", ("dn", "s5", "hg", "att"))
    if "dn" in sel:
        stage_deltanet(k, l)
    if "s5" in sel:
        stage_s5(k, l)
    if "hg" in sel:
        stage_hgrn2(k, l)
    if "att" in sel:
        stage_attention(k, l)


def stage_merge(k, l):
    nc, P, NB, NC = k.nc, k.P, k.NB, k.NC
    with ExitStack() as es:
        wg = es.enter_context(nc.sbuf_tensor("m_wg", [128, 8, 4 * D], BF16))
        wb = es.enter_context(nc.sbuf_tensor("m_wb", [128, 8, D], BF16))
        wo = es.enter_context(nc.sbuf_tensor("m_wo", [128, 8, D], BF16))
        xt = [es.enter_context(nc.sbuf_tensor("m_x%d" % s, [128, 8, 256], F32)) for s in range(2)]
        yt = [es.enter_context(nc.sbuf_tensor("m_y%d" % s, [128, 8, 256], BF16)) for s in range(2)]
        ht = es.enter_context(nc.sbuf_tensor("m_h", [128, 8, 256], BF16))
        acc = es.enter_context(nc.sbuf_tensor("m_acc", [128, 8, 256], F32))
        accb = es.enter_context(nc.sbuf_tensor("m_accb", [128, 8, 256], BF16))
        sg = [es.enter_context(nc.sbuf_tensor("m_sg%d" % s, [128, 256], F32)) for s in range(2)]
        tt = [es.enter_context(nc.sbuf_tensor("m_tt%d" % s, [128, 256], F32)) for s in range(2)]
        psg = [es.enter_context(nc.psum_tensor("m_psg%d" % s, [128, 512], F32)) for s in range(2)]
        psy = [es.enter_context(nc.psum_tensor("m_psy%d" % s, [128, 512], F32)) for s in range(2)]
        pso = [es.enter_context(nc.psum_tensor("m_pso%d" % s, [128, 512], F32)) for s in range(2)]
        ntiles = alloc_norm_tiles(k, es, "m_", 256)
        A, Sh, G = emit_mod_scalars(k, es, "m_", l, 1, 3, 4, 5, 1.0)
        load_w_bf16(k, wg, k.w_gate[l], 4 * D, "wg")
        load_w_bf16(k, wb, k.w_branch[l].rearrange("i r n -> (i r) n"), D, "wb")
        load_w_bf16(k, wo, k.w_out[l], D, "wo")
        it = 0
        qi = 0
        for (b, t0, n, cond) in token_tiles(NB, 256):
            if l == 1 and cond == NB:
                continue
            s = it % 2
            it += 1
            xsrc = k.XT[b].rearrange("(c p) t -> p c t", p=128)[:, :, t0:t0 + n]
            P.dma(lambda e, s=s, xsrc=xsrc, n=n: e.dma_start(out=xt[s][:, :, :n], in_=xsrc), reads=[("XT", b)], writes=[("x", s)])
            ysrc = k.YT[b].rearrange("i (c p) t -> p (i c) t", p=128)[:, :, t0:t0 + n]
            P.dma(lambda e, s=s, ysrc=ysrc, n=n: e.dma_start(out=yt[s][:, :, :n], in_=ysrc), reads=[("YT", b)], writes=[("y", s)])
            emit_norm_mod(k, ntiles, xt[s], ht, n, A, Sh, cond, "modsc", s)
            for m in range(8):
                for i in range(4):
                    q = qi % 2
                    qi += 1
                    for kc in range(8):
                        P.op("pe", lambda e, i=i, m=m, q=q, kc=kc, n=n: e.matmul(psg[q][:, :n], lhsT=wg[:, kc, i * D + m * 128:i * D + (m + 1) * 128], rhs=ht[:, kc, :n], start=(kc == 0), stop=(kc == 7)),
                             reads=[("wg", kc), "h"], writes=[("psg", q)], inc=(kc == 7))
                    for kk in range(2):
                        P.op("pe", lambda e, i=i, m=m, q=q, kk=kk, n=n, s=s: e.matmul(psy[q][:, :n], lhsT=wb[:, i * 2 + kk, m * 128:(m + 1) * 128], rhs=yt[s][:, i * 2 + kk, :n], start=(kk == 0), stop=(kk == 1)),
                             reads=[("wb", i * 2 + kk), ("y", s)], writes=[("psy", q)], inc=(kk == 1))
                    P.op("act", lambda e, q=q, n=n: e.activation(out=sg[q][:, :n], in_=psg[q][:, :n], func=AF.Sigmoid), reads=[("psg", q)], writes=[("sg", q)])
                    if i == 0:
                        P.op("dve", lambda e, q=q, n=n, m=m: e.tensor_tensor(out=acc[:, m, :n], in0=sg[q][:, :n], in1=psy[q][:, :n], op=ALU.mult), reads=[("sg", q), ("psy", q)], writes=[("acc", m)])
                    else:
                        P.op("dve", lambda e, q=q, n=n: e.tensor_tensor(out=tt[q][:, :n], in0=sg[q][:, :n], in1=psy[q][:, :n], op=ALU.mult), reads=[("sg", q), ("psy", q)], writes=[("tt", q)])
                        if i < 3:
                            P.op("pool", lambda e, q=q, n=n, m=m: e.tensor_tensor(out=acc[:, m, :n], in0=acc[:, m, :n], in1=tt[q][:, :n], op=ALU.add), reads=[("tt", q), ("acc", m)], writes=[("acc", m)])
                        else:
                            P.op("pool", lambda e, q=q, n=n, m=m: e.tensor_tensor(out=accb[:, m, :n], in0=acc[:, m, :n], in1=tt[q][:, :n], op=ALU.add), reads=[("tt", q), ("acc", m)], writes=[("accb", m)])
            for m in range(8):
                q = m % 2
                for kc in range(8):
                    P.op("pe", lambda e, m=m, q=q, kc=kc, n=n: e.matmul(pso[q][:, :n], lhsT=wo[:, kc, m * 128:(m + 1) * 128], rhs=accb[:, kc, :n], start=(kc == 0), stop=(kc == 7)),
                         reads=[("wo", kc), ("accb", kc)], writes=[("pso", q)], inc=(kc == 7))
                P.op("dve", lambda e, m=m, q=q, s=s, n=n, cond=cond: e.scalar_tensor_tensor(out=xt[s][:, m, :n], in0=pso[q][:, :n], scalar=G[:, m, cond:cond + 1], in1=xt[s][:, m, :n], op0=ALU.mult, op1=ALU.add),
                     reads=[("pso", q), ("x", s), "modsg"], writes=[("x", s)])
            P.dma(lambda e, s=s, xsrc=xsrc, n=n: e.dma_start(out=xsrc, in_=xt[s][:, :, :n]), reads=[("x", s)], writes=[("XT", b)])
        P.end_stage()


def stage_final(k):
    nc, P, NB = k.nc, k.P, k.NB
    with ExitStack() as es:
        xt = [es.enter_context(nc.sbuf_tensor("o_x%d" % s, [128, 8, 512], F32)) for s in range(2)]
        yt = es.enter_context(nc.sbuf_tensor("o_y", [128, 8, 512], F32))
        ot = [es.enter_context(nc.sbuf_tensor("o_o%d" % s, [128, D], F32)) for s in range(2)]
        sq = es.enter_context(nc.sbuf_tensor("o_sq", [128, 8, 512], BF16))
        rs = es.enter_context(nc.sbuf_tensor("o_rs", [128, 512], F32))
        epsb = es.enter_context(nc.sbuf_tensor("o_eps", [128, 1], F32))
        psms = es.enter_context(nc.psum_tensor("o_psms", [128, 512], F32))
        ps = [es.enter_context(nc.psum_tensor("o_ps%d" % s, [128, 4, 128], F32)) for s in range(4)]
        P.op("dve", lambda e: e.memset(epsb[:], EPS), writes=["epsb"])
        it = 0
        oi = 0
        pi = 0
        for (b, t0, n, cond) in token_tiles(NB):
            if cond == NB:
                continue
            s = it % 2
            it += 1
            xsrc = k.XT[b].rearrange("(c p) t -> p c t", p=128)[:, :, t0:t0 + n]
            P.dma(lambda e, s=s, xsrc=xsrc: e.dma_start(out=xt[s][:], in_=xsrc), reads=[("XT", b)], writes=[("x", s)])
            P.op("act", lambda e, s=s: e.activation(out=sq[:], in_=xt[s][:], func=AF.Square), reads=[("x", s)], writes=["sq"])
            for c in range(8):
                P.op("pe", lambda e, c=c: e.matmul(psms[:], lhsT=k.onesb[:], rhs=sq[:, c, :], start=(c == 0), stop=(c == 7)), reads=["sq", "onesb"], writes=["psms"], inc=(c == 7))
            P.op("act", lambda e: e.activation(out=rs[:], in_=psms[:], func=AF.Sqrt, bias=epsb[:, 0:1], scale=1.0), reads=["psms", "epsb"], writes=["rs0", "rs"])
            P.op("dve", lambda e: e.reciprocal(out=rs[:], in_=rs[:]), reads=["rs0"], writes=["rs"])
            for c in range(8):
                P.op("dve", lambda e, c=c, s=s: e.scalar_tensor_tensor(out=yt[:, c, :], in0=xt[s][:, c, :], scalar=k.fg[:, c:c + 1], in1=rs[:], op0=ALU.mult, op1=ALU.mult),
                     reads=[("x", s), "rs", "fg"], writes=[("yt", c)])
            for tb in range(4):
                o = oi % 2
                oi += 1
                for h in range(2):
                    p = pi % 4
                    pi += 1
                    for c in range(4):
                        ch = h * 4 + c
                        P.op("pe", lambda e, p=p, c=c, ch=ch, tb=tb: e.transpose(ps[p][:, c, :], yt[:, ch, tb * 128:(tb + 1) * 128], k.idn[:]),
                             reads=[("yt", ch), "idn"], writes=[("ps", p)], inc=(c == 3))
                    if h == 0:
                        P.op("act", lambda e, p=p, o=o, h=h: e.activation(out=ot[o][:, h * 512:(h + 1) * 512], in_=ps[p][:].rearrange("p a b -> p (a b)"), func=AF.Identity), reads=[("ps", p)], writes=[("ot", o, h)])
                    else:
                        P.op("dve", lambda e, p=p, o=o, h=h: e.tensor_copy(out=ot[o][:, h * 512:(h + 1) * 512], in_=ps[p][:].rearrange("p a b -> p (a b)")), reads=[("ps", p)], writes=[("ot", o, h)])
                r0 = t0 - CTX + tb * 128
                P.dma(lambda e, o=o, b=b, r0=r0: e.dma_start(out=k.out[b, r0:r0 + 128, :], in_=ot[o][:]), reads=[("ot", o, 0), ("ot", o, 1)], writes=[("out", b)])
        P.end_stage()


def make_inputs(inputs, core, NB):
    b0 = core * NB
    f = lambda a: np.ascontiguousarray(a, dtype=np.float32)
    w_in = inputs["w_in"]
    cT = np.concatenate([inputs["c"][b0:b0 + NB], inputs["c_ctx"][None]], 0).reshape(NB + 1, 8, 128).transpose(2, 1, 0)
    return {
        "x": f(inputs["x"][b0:b0 + NB]),
        "ctx": f(inputs["ctx"][b0:b0 + NB]),
        "cT": f(cT),
        "ada_w": f(inputs["ada_w"]),
        "ada_b": f(inputs["ada_b"].reshape(2, 72, 128).transpose(0, 2, 1)),
        "norm_g": f(inputs["norm_g"].reshape(2, 3, 8, 128).transpose(0, 1, 3, 2)),
        "final_g": f(inputs["final_g"].reshape(8, 128).T),
        "ffn_w1": f(inputs["ffn_w1"]), "ffn_w3": f(inputs["ffn_w3"]), "ffn_w2": f(inputs["ffn_w2"]),
        "w_fm": f(np.concatenate([w_in[:, :, a:a + w] for a, w in FM_GROUPS], axis=2)),
        "w_tok": f(np.concatenate([w_in[:, :, a:a + w] for a, w in TOK_GROUPS], axis=2)),
        "w_gate": f(w_in[:, :, GATE0:]),
        "w_branch": f(inputs["w_branch"]),
        "w_out": f(inputs["w_out"]),
        "ident": np.eye(128, dtype=np.float32),
        "rope_cos": ROPE[0], "rope_sin": ROPE[1], "rope_rm": ROPE[2],
        "hg_masks": HG_MASKS, "bd64": BD64,
        "s5_lam_re": f(inputs["s5_lam_re"].reshape(2, 2, 8, 2, 64).transpose(0, 3, 4, 1, 2).reshape(2, 128, 16)),
        "s5_lam_im": f(inputs["s5_lam_im"].reshape(2, 2, 8, 2, 64).transpose(0, 3, 4, 1, 2).reshape(2, 128, 16)),
        "s5_log_step": f(np.broadcast_to(inputs["s5_log_step"].reshape(2, 2, 8, 2, 1), (2, 2, 8, 2, 64)).transpose(0, 3, 4, 1, 2).reshape(2, 128, 16)),
        "s5_b_re": f(inputs["s5_b_re"].reshape(2, 8, 2, 64, 16).transpose(0, 2, 3, 1, 4).reshape(2, 128, 8, 16)),
        "s5_b_im": f(inputs["s5_b_im"].reshape(2, 8, 2, 64, 16).transpose(0, 2, 3, 1, 4).reshape(2, 128, 8, 16)),
        "s5_c_re": f(inputs["s5_c_re"].reshape(2, 8, 2, 16, 64).transpose(0, 2, 4, 1, 3).reshape(2, 128, 8, 16)),
        "s5_c_im": f(inputs["s5_c_im"].reshape(2, 8, 2, 16, 64).transpose(0, 2, 4, 1, 3).reshape(2, 128, 8, 16)),
        "s5_d": f(inputs["s5_d"].reshape(2, 2, 128).transpose(0, 2, 1)),
        "s5_glu": f(inputs["s5_glu"]),
        "dn_gm64": DN_GM64, "dn_gmbd": DN_GMBD, "dn_idn2": DN_IDN2,
        "dn_conv": f(inputs["dn_conv"].reshape(2, 5, 6, 128).transpose(0, 3, 2, 1)),
        "dn_a_log": f(np.broadcast_to(inputs["dn_a_log"].reshape(2, 1, 8), (2, 128, 8))),
        "dn_dt_bias": f(np.broadcast_to(inputs["dn_dt_bias"].reshape(2, 1, 8), (2, 128, 8))),
        "dn_norm_g": f(np.broadcast_to(inputs["dn_norm_g"].reshape(2, 1, 64), (2, 128, 64))),
        "hg_lb": f(inputs["hg_lb_logits"].reshape(2, 2, 128).transpose(2, 1, 0)),
        "hg_norm_g": f(np.tile(inputs["hg_norm_g"], (1, 2)).reshape(2, 128, 1)),
        "at_qn_g": f(inputs["at_qn_g"].reshape(2, 64, 1)), "at_kn_g": f(inputs["at_kn_g"].reshape(2, 64, 1)),
    }


def _rope_consts():
    n = np.arange(SEQ)
    r, c = n // 64, n % 64
    inv = (10000.0 ** (-np.arange(0, 32, 2, dtype=np.float32) / 32)).astype(np.float32)
    ang_r = r[:, None].astype(np.float32) * inv
    ang_c = c[:, None].astype(np.float32) * inv
    ang = np.concatenate([ang_r, ang_r, ang_c, ang_c], -1)
    rm = np.zeros((64, 64), np.float32)
    for i in range(16):
        rm[16 + i, i] = -1.0
        rm[i, 16 + i] = 1.0
        rm[48 + i, 32 + i] = -1.0
        rm[32 + i, 48 + i] = 1.0
    return np.ascontiguousarray(np.cos(ang).T.astype(np.float32)), np.ascontiguousarray(np.sin(ang).T.astype(np.float32)), rm


ROPE = _rope_consts()


def _dn_consts():
    a = np.arange(64)
    le = [(a[:, None] <= a[None, :]), (a[:, None] >= a[None, :])]
    st = [(a[None, :] < a[:, None]), (a[None, :] > a[:, None])]
    gm64 = np.zeros((64, 2, 2, 128), np.float32)
    gmbd = np.zeros((128, 2, 2, 128), np.float32)
    for d in range(2):
        gm64[:, d, 0, :] = np.tile(le[d], (1, 2))
        gm64[:, d, 1, :] = np.tile(st[d], (1, 2))
        for h in range(2):
            gmbd[h * 64:(h + 1) * 64, d, 0, h * 64:(h + 1) * 64] = st[d]
            gmbd[h * 64:(h + 1) * 64, d, 1, h * 64:(h + 1) * 64] = le[d]
    idn2 = np.tile(np.eye(64, dtype=np.float32), (2, 1))
    return gm64, gmbd, idn2


DN_GM64, DN_GMBD, DN_IDN2 = _dn_consts()
_s = np.arange(128)[:, None] % 64
_t = np.arange(64)[None, :]
HG_MASKS = np.ascontiguousarray(np.stack([(_s <= _t), (_s >= _t)], 1).astype(np.float32))
BD64 = np.kron(np.eye(2, dtype=np.float32), np.full((64, 64), 1.0 / 64, np.float32))
_CACHE = {}


def kernel(**inputs):
    NB = inputs["x"].shape[0] // N_CORES
    if "nc" not in _CACHE:
        _CACHE["nc"] = build(NB)
    nc = _CACHE["nc"]
    shared = None
    in_maps = []
    for c in range(N_CORES):
        in_maps.append(make_inputs(inputs, c, NB))
    res = run_bass_kernel_spmd(nc, in_maps, core_ids=list(range(N_CORES)))
    return np.concatenate([r["out"] for r in res.results], axis=0).astype(np.float32)
```

```python
import numpy as np
from contextlib import ExitStack
import concourse.bass as bass
import concourse.mybir as mybir
from concourse.bass_utils import run_bass_kernel_spmd

F32 = mybir.dt.float32
BF16 = mybir.dt.bfloat16
AF = mybir.ActivationFunctionType
ALU = mybir.AluOpType
AX = mybir.AxisListType

D = 1024
SEQ = 2048
CTX = 256
T = SEQ + CTX
DFF = 2816
NFF = DFF // 128
EPS = 1e-6
N_CORES = 8
ENG = ("pe", "act", "dve", "pool", "sp")

FM_GROUPS = [(0, 768), (768, 256), (1040, 256), (1296, 256), (1552, 512), (2320, 256), (2576, 256), (2832, 128)]
FM_W = sum(w for _, w in FM_GROUPS)
FM_Q, FM_K, FM_V, FM_Z, FM_U, FM_HQ, FM_HF, FM_HG, FM_AQ, FM_AK = 0, 256, 512, 768, 1024, 1280, 1536, 2048, 2304, 2560
TOK_GROUPS = [(2960, 128), (2064, 256), (1024, 8), (1032, 8)]
TOK_W = 400
TK_AV, TK_HV, TK_A, TK_B = 0, 128, 384, 392
GATE0 = 3088
SERIAL_DN = True
SERIAL_S5 = True


class Prog:
    def __init__(self, nc, n_dma_sems=56):
        self.nc = nc
        self.sem = {e: nc.semaphore("sem_" + e).__enter__() for e in ("pe", "act", "dve", "pool")}
        self.dsem = [nc.semaphore("dsem%d" % i).__enter__() for i in range(n_dma_sems)]
        self.duse = [0] * n_dma_sems
        self.dnext = 0
        self.cnt = {e: 0 for e in self.sem}
        self.lastw = {}
        self.readers = {}
        self.ops = {e: [] for e in ENG}
        self.waited = {}
        self.nops = 0
        self.serial = False
        self.last_ev = {}
        self.prev_ev = None

    def _need(self, eng, ev, waits):
        if ev is None:
            return
        key, val, src = ev
        if self.waited.get((eng, key), 0) >= val:
            return
        self.waited[(eng, key)] = val
        waits.append((key, val))

    def _semh(self, key):
        return self.sem[key] if isinstance(key, str) else self.dsem[key]

    def op(self, eng, fn, reads=(), writes=(), inc=True):
        waits = []
        for r in reads:
            ev = self.lastw.get(r)
            if ev is not None and not (ev[2] == eng and eng == "pe"):
                self._need(eng, ev, waits)
        for w in writes:
            ev = self.lastw.get(w)
            if ev is not None and ev[2] != eng:
                self._need(eng, ev, waits)
            for rv in self.readers.get(w, ()):
                if rv[2] != eng:
                    self._need(eng, rv, waits)
        if self.serial:
            waits = [w_ for w_ in waits if not isinstance(w_[0], str) or w_[0] == eng]
            for w_ in waits:
                pass
            if self.prev_ev is not None and self.prev_ev[2] != eng:
                if self.waited.get((eng, self.prev_ev[0]), 0) < self.prev_ev[1] or True:
                    self.waited[(eng, self.prev_ev[0])] = max(self.waited.get((eng, self.prev_ev[0]), 0), self.prev_ev[1])
                    waits.append((self.prev_ev[0], self.prev_ev[1]))
        if inc:
            self.cnt[eng] += 1
            me = (eng, self.cnt[eng], eng)
        else:
            me = (eng, self.cnt[eng] + 1, eng)
        self.last_ev[eng] = me
        self.prev_ev = me if inc else (self.prev_ev if not self.serial else me)
        for r in reads:
            self.readers.setdefault(r, []).append(me)
        for w in writes:
            self.lastw[w] = me
            self.readers[w] = []
        self.ops[eng].append((waits, fn, ("sem", eng) if inc else ("none", eng)))
        self.nops += 1
        return me

    def dma(self, fn, reads=(), writes=(), q="sp"):
        waits = []
        for r in reads:
            self._need(q, self.lastw.get(r), waits)
        for w in writes:
            self._need(q, self.lastw.get(w), waits)
            for rv in self.readers.get(w, ()):
                self._need(q, rv, waits)
        i = self.dnext
        self.dnext = (self.dnext + 1) % len(self.dsem)
        if self.duse[i] > 0:
            self._need(q, (i, 16 * self.duse[i], "dma"), waits)
        self.duse[i] += 1
        me = (i, 16 * self.duse[i], "dma")
        for r in reads:
            self.readers.setdefault(r, []).append(me)
        for w in writes:
            self.lastw[w] = me
            self.readers[w] = []
        self.ops[q].append((waits, fn, ("dsem", i)))
        self.nops += 1
        return me

    def end_stage(self):
        waits = []
        for tok, ev in self.lastw.items():
            self._need("sp", ev, waits)
        for tok, evs in self.readers.items():
            for ev in evs:
                self._need("sp", ev, waits)
        self.ops["sp"].append((waits, None, None))
        nc = self.nc
        engobj = {"pe": "tensor", "act": "scalar", "dve": "vector", "pool": "gpsimd", "sp": "sync"}
        with nc.Block() as block:
            for e in ENG:
                ops = self.ops[e]
                if not ops:
                    continue

                def body(eo, ops=ops):
                    for waits, fn, inc in ops:
                        for key, val in waits:
                            eo.wait_ge(self._semh(key), val)
                        if fn is None:
                            continue
                        ins = fn(eo)
                        if inc[0] == "sem":
                            ins.then_inc(self.sem[inc[1]], 1)
                        elif inc[0] == "dsem":
                            ins.then_inc(self.dsem[inc[1]], 16)

                getattr(block, engobj[e])(body)
        self.ops = {e: [] for e in ENG}
        self.waited = {}
        self.lastw = {}
        self.readers = {}
        self.last_ev = {}
        self.serial = False
        self.prev_ev = None


class K:
    pass


class NCProxy:
    def __init__(self, nc):
        object.__setattr__(self, "_nc", nc)
        object.__setattr__(self, "_n", [0])

    def __getattr__(self, name):
        return getattr(self._nc, name)

    def sbuf_tensor(self, name, shape, dtype):
        self._n[0] += 1
        return self._nc.sbuf_tensor("%s_u%d" % (name, self._n[0]), shape, dtype)

    def psum_tensor(self, name, shape, dtype):
        self._n[0] += 1
        return self._nc.psum_tensor("%s_u%d" % (name, self._n[0]), shape, dtype)


def token_tiles(NB, ts=512):
    tl = []
    for b in range(NB):
        tl.append((b, 0, CTX, NB))
        for i in range(SEQ // ts):
            tl.append((b, CTX + i * ts, ts, b))
    return tl


def build(NB, opts=None):
    opts = opts or {}
    NC = NB + 1
    nc = bass.Bass("TRN2", target_bir_lowering=False)
    k = K()
    k.nc, k.NB, k.NC, k.opts = NCProxy(nc), NB, NC, opts
    inp = lambda name, shape: nc.dram_tensor(name, list(shape), F32, kind="ExternalInput").ap()
    k.x = inp("x", [NB, SEQ, D])
    k.ctx = inp("ctx", [NB, CTX, D])
    k.cT = inp("cT", [128, 8, NC])
    k.ada_w = inp("ada_w", [2, D, 9 * D])
    k.ada_b = inp("ada_b", [2, 128, 72])
    k.norm_g = inp("norm_g", [2, 3, 128, 8])
    k.final_g = inp("final_g", [128, 8])
    k.w1 = inp("ffn_w1", [2, 2, D, DFF])
    k.w3 = inp("ffn_w3", [2, 2, D, DFF])
    k.w2 = inp("ffn_w2", [2, 2, DFF, D])
    k.w_fm = inp("w_fm", [2, D, FM_W])
    k.w_tok = inp("w_tok", [2, D, TOK_W])
    k.w_gate = inp("w_gate", [2, D, 4 * D])
    k.w_branch = inp("w_branch", [2, 4, 256, D])
    k.w_out = inp("w_out", [2, D, D])
    k.ident = inp("ident", [128, 128])
    k.rope_cos = inp("rope_cos", [64, SEQ])
    k.rope_sin = inp("rope_sin", [64, SEQ])
    k.rope_rm = inp("rope_rm", [64, 64])
    k.at_qn_g = inp("at_qn_g", [2, 64, 1])
    k.at_kn_g = inp("at_kn_g", [2, 64, 1])
    k.hg_masks = inp("hg_masks", [128, 2, 64])
    k.s5_lam_re = inp("s5_lam_re", [2, 128, 16])
    k.s5_lam_im = inp("s5_lam_im", [2, 128, 16])
    k.s5_log_step = inp("s5_log_step", [2, 128, 16])
    k.s5_b_re = inp("s5_b_re", [2, 128, 8, 16])
    k.s5_b_im = inp("s5_b_im", [2, 128, 8, 16])
    k.s5_c_re = inp("s5_c_re", [2, 128, 8, 16])
    k.s5_c_im = inp("s5_c_im", [2, 128, 8, 16])
    k.s5_d = inp("s5_d", [2, 128, 2])
    k.s5_glu = inp("s5_glu", [2, 256, 512])
    k.S5TAB = nc.dram_tensor("S5TAB", [16, 2, 128, T], F32).ap()
    k.dn_gm64 = inp("dn_gm64", [64, 2, 2, 128])
    k.dn_gmbd = inp("dn_gmbd", [128, 2, 2, 128])
    k.dn_idn2 = inp("dn_idn2", [128, 64])
    k.dn_conv = inp("dn_conv", [2, 128, 6, 5])
    k.dn_a_log = inp("dn_a_log", [2, 128, 8])
    k.dn_dt_bias = inp("dn_dt_bias", [2, 128, 8])
    k.dn_norm_g = inp("dn_norm_g", [2, 128, 64])
    k.bd64 = inp("bd64", [128, 128])
    k.hg_lb = inp("hg_lb", [128, 2, 2])
    k.hg_norm_g = inp("hg_norm_g", [2, 128, 1])
    k.out = nc.dram_tensor("out", [NB, SEQ, D], F32, kind="ExternalOutput").ap()
    k.XT = nc.dram_tensor("XT", [NB, D, T], F32).ap()
    only = opts.get("only")
    k.PT = nc.dram_tensor("PT", [NB, FM_W, T], F32, **({"kind": "ExternalInput"} if only else {})).ap()
    k.TOK = nc.dram_tensor("TOK", [NB, T, TOK_W], F32, **({"kind": "ExternalInput"} if only else {})).ap()
    k.YT = nc.dram_tensor("YT", [NB, 4, 256, T], BF16, **({"kind": "ExternalOutput"} if only else {})).ap()
    if "dump" in opts:
        k.dump = nc.dram_tensor("dump", [16, 128, T], F32, kind="ExternalOutput").ap()
    if "dbg" in opts:
        k.dbg = {nm: nc.dram_tensor("dbg_" + nm, list(shp), F32, kind="ExternalOutput").ap() for nm, shp in opts["dbg"].items()}
    P = Prog(nc)
    k.P = P
    with ExitStack() as es:
        al = lambda name, shape, dt=F32: es.enter_context(nc.sbuf_tensor(name, list(shape), dt))
        k.idn = al("idn", [128, 128])
        k.onesb = al("onesb", [128, 128], BF16)
        k.mods = al("mods", [128, 72, NC])
        k.ng = al("ng", [128, 2, 3, 8])
        k.fg = al("fg", [128, 8])
        P.dma(lambda e: e.dma_start(out=k.idn[:], in_=k.ident), writes=["idn"])
        P.dma(lambda e: e.dma_start(out=k.fg[:], in_=k.final_g), writes=["fg"])
        for l in range(2):
            for j in range(3):
                P.dma(lambda e, l=l, j=j: e.dma_start(out=k.ng[:, l, j, :], in_=k.norm_g[l, j]), writes=["ng"])
        P.op("dve", lambda e: e.memset(k.onesb[:], 1.0 / D), writes=["onesb"])
        P.end_stage()
        if only:
            for l in opts.get("layers", (0,)):
                {"att": stage_attention, "hg": stage_hgrn2, "dn": stage_deltanet, "s5": stage_s5}[only](k, l)
            return nc
        stage_transpose_in(k)
        stop = opts.get("stop", "")
        for l in range(2):
            stage_ada(k, l)
            stage_ffn(k, l, 0)
            if stop == "ffn1_%d" % l:
                break
            stage_inproj(k, l)
            if stop == "inproj_%d" % l:
                break
            stage_mixers(k, l)
            if stop == "mixers_%d" % l:
                break
            stage_merge(k, l)
            if stop == "merge_%d" % l:
                break
            stage_ffn(k, l, 1)
        if "XT" in opts.get("dbg", {}):
            P.dma(lambda e: e.dma_start(out=k.dbg["XT"], in_=k.XT), reads=[], writes=["dbgx"])
            P.end_stage()
        if "PT" in opts.get("dbg", {}):
            P.dma(lambda e: e.dma_start(out=k.dbg["PT"], in_=k.PT), reads=[], writes=["dbgp"])
            P.dma(lambda e: e.dma_start(out=k.dbg["TOK"], in_=k.TOK), reads=[], writes=["dbgt"])
            P.end_stage()
        stage_final(k)
    return nc


def stage_transpose_in(k):
    nc, P = k.nc, k.P
    with ExitStack() as es:
        xin = [es.enter_context(nc.sbuf_tensor("ti_x%d" % i, [128, D], F32)) for i in range(2)]
        xo = [es.enter_context(nc.sbuf_tensor("ti_o%d" % i, [128, 8, 128], F32)) for i in range(2)]
        ps = [es.enter_context(nc.psum_tensor("ti_ps%d" % i, [128, 4, 128], F32)) for i in range(4)]
        it = 0
        for b in range(k.NB):
            for tb in range(T // 128):
                s = it % 2
                src = k.ctx[b, tb * 128:(tb + 1) * 128, :] if tb < 2 else k.x[b, (tb - 2) * 128:(tb - 1) * 128, :]
                P.dma(lambda e, s=s, src=src: e.dma_start(out=xin[s][:], in_=src), writes=[("xin", s)])
                for h in range(2):
                    pi = (it * 2 + h) % 4
                    for c in range(4):
                        ch = h * 4 + c
                        P.op("pe", lambda e, s=s, pi=pi, c=c, ch=ch: e.transpose(ps[pi][:, c, :], xin[s][:, ch * 128:(ch + 1) * 128], k.idn[:]),
                             reads=[("xin", s), "idn"], writes=[("tps", pi)], inc=(c == 3))
                    eng = "act" if h == 0 else "dve"
                    if eng == "act":
                        P.op("act", lambda e, s=s, pi=pi, h=h: e.activation(out=xo[s][:, h * 4:(h + 1) * 4, :], in_=ps[pi][:], func=AF.Identity),
                             reads=[("tps", pi)], writes=[("xo", s, h)])
                    else:
                        P.op("dve", lambda e, s=s, pi=pi, h=h: e.tensor_copy(out=xo[s][:, h * 4:(h + 1) * 4, :], in_=ps[pi][:]),
                             reads=[("tps", pi)], writes=[("xo", s, h)])
                dst = k.XT[b].rearrange("(c p) t -> p c t", p=128)[:, :, tb * 128:(tb + 1) * 128]
                P.dma(lambda e, s=s, dst=dst: e.dma_start(out=dst, in_=xo[s][:]), reads=[("xo", s, 0), ("xo", s, 1)], writes=[("XT", b)])
                it += 1
        P.end_stage()


def stage_ada(k, l):
    nc, P, NC = k.nc, k.P, k.NC
    with ExitStack() as es:
        sc = es.enter_context(nc.sbuf_tensor("ad_sc", [128, 8, NC], F32))
        ab = es.enter_context(nc.sbuf_tensor("ad_b", [128, 72], F32))
        wt = [es.enter_context(nc.sbuf_tensor("ad_w%d" % i, [128, 8, 1024], F32)) for i in range(2)]
        ps = [es.enter_context(nc.psum_tensor("ad_ps%d" % i, [128, 8, NC], F32)) for i in range(2)]
        P.dma(lambda e: e.dma_start(out=sc[:], in_=k.cT), writes=["sc"])
        P.dma(lambda e: e.dma_start(out=ab[:], in_=k.ada_b[l]), writes=["ab"])
        P.op("act", lambda e: e.activation(out=sc[:], in_=sc[:], func=AF.Silu), reads=["sc"], writes=["sc"])
        for j in range(9):
            s = j % 2
            src = k.ada_w[l][:, j * 1024:(j + 1) * 1024].rearrange("(c p) n -> p c n", p=128)
            for hh in range(2):
                P.dma(lambda e, s=s, src=src, hh=hh: e.dma_start(out=wt[s][:, hh * 4:(hh + 1) * 4, :], in_=src[:, hh * 4:(hh + 1) * 4, :]), writes=[("adw", s, hh)])
            for m in range(8):
                for kc in range(8):
                    P.op("pe", lambda e, s=s, m=m, kc=kc: e.matmul(ps[s][:, m, :], lhsT=wt[s][:, kc, m * 128:(m + 1) * 128], rhs=sc[:, kc, :], start=(kc == 0), stop=(kc == 7)),
                         reads=[("adw", s, kc // 4), "sc"], writes=[("adps", s)], inc=(kc == 7 and m == 7))
            P.op("dve", lambda e, s=s, j=j: e.tensor_tensor(out=k.mods[:, j * 8:(j + 1) * 8, :], in0=ps[s][:], in1=ab[:, j * 8:(j + 1) * 8].rearrange("p (c o) -> p c o", o=1).to_broadcast([128, 8, NC]), op=ALU.add),
                 reads=[("adps", s), "ab"], writes=["mods"])
        P.end_stage()


def emit_norm_mod(k, es_tiles, x_t, h_t, n, A, Sh, cond, tag, sl):
    P = k.P
    sq, rs, tmp, psms = es_tiles
    P.op("act", lambda e: e.activation(out=sq[:, :, :n], in_=x_t[:, :, :n], func=AF.Square), reads=[("x", sl)], writes=["sq"])
    for c in range(8):
        P.op("pe", lambda e, c=c: e.matmul(psms[:, :n], lhsT=k.onesb[:], rhs=sq[:, c, :n], start=(c == 0), stop=(c == 7)), reads=["sq", "onesb"], writes=["psms"], inc=(c == 7))
    P.op("act", lambda e: e.activation(out=rs[:, :n], in_=psms[:, :n], func=AF.Sqrt, bias=k.epsb[:, 0:1], scale=1.0), reads=["psms", "epsb"], writes=["rs0", "rs"])
    P.op("dve", lambda e: e.reciprocal(out=rs[:, :n], in_=rs[:, :n]), reads=["rs0"], writes=["rs"])
    for c in range(8):
        P.op("dve", lambda e, c=c: e.tensor_tensor(out=tmp[c % 2][:, :n], in0=x_t[:, c, :n], in1=rs[:, :n], op=ALU.mult), reads=[("x", sl), "rs"], writes=[("tmp", c % 2)])
        P.op("pool", lambda e, c=c: e.tensor_scalar(out=h_t[:, c, :n], in0=tmp[c % 2][:, :n], scalar1=A[:, c, cond:cond + 1], scalar2=Sh[:, c, cond:cond + 1], op0=ALU.mult, op1=ALU.add),
             reads=[("tmp", c % 2), tag], writes=["h"])


def alloc_norm_tiles(k, es, pfx, ts=512):
    nc = k.nc
    sq = es.enter_context(nc.sbuf_tensor(pfx + "sq", [128, 8, ts], BF16))
    rs = es.enter_context(nc.sbuf_tensor(pfx + "rs", [128, ts], F32))
    tmp = [es.enter_context(nc.sbuf_tensor(pfx + "tmp%d" % i, [128, ts], F32)) for i in range(2)]
    psms = es.enter_context(nc.psum_tensor(pfx + "psms", [128, 512], F32))
    k.epsb = es.enter_context(nc.sbuf_tensor(pfx + "epsb", [128, 1], F32))
    k.P.op("dve", lambda e: e.memset(k.epsb[:], EPS), writes=["epsb"])
    return sq, rs, tmp, psms


def emit_mod_scalars(k, es, pfx, l, jn, i_shift, i_scale, i_gate, gate_mul):
    nc, P, NC = k.nc, k.P, k.NC
    A = es.enter_context(nc.sbuf_tensor(pfx + "A", [128, 8, NC], F32))
    G = es.enter_context(nc.sbuf_tensor(pfx + "G", [128, 8, NC], F32))
    Sh = k.mods[:, i_shift * 8:(i_shift + 1) * 8, :]
    P.op("dve", lambda e: e.tensor_scalar(out=A[:], in0=k.mods[:, i_scale * 8:(i_scale + 1) * 8, :], scalar1=1.0, scalar2=None, op0=ALU.add), reads=["mods"], writes=["A0"])
    P.op("dve", lambda e: e.tensor_tensor(out=A[:], in0=A[:], in1=k.ng[:, l, jn, :].rearrange("p (c o) -> p c o", o=1).to_broadcast([128, 8, NC]), op=ALU.mult), reads=["A0", "ng"], writes=["modsc"])
    if i_gate is not None:
        P.op("dve", lambda e: e.tensor_scalar(out=G[:], in0=k.mods[:, i_gate * 8:(i_gate + 1) * 8, :], scalar1=gate_mul, scalar2=None, op0=ALU.mult), reads=["mods"], writes=["modsg"])
    return A, Sh, G


def load_w_bf16(k, dst, src_rows, ncols, tok, nk=8):
    P = k.P
    for kc in range(nk):
        P.dma(lambda e, kc=kc: e.dma_start(out=dst[:, kc, :], in_=src_rows[kc * 128:(kc + 1) * 128, :], max_dma_last_dim=4096),
              writes=[(tok, kc)], q="pool")


def stage_ffn(k, l, i):
    nc, P, NB, NC = k.nc, k.P, k.NB, k.NC
    jn = 0 if i == 0 else 2
    mi = (0, 1, 2) if i == 0 else (6, 7, 8)
    last = (l == 1 and i == 1)
    with ExitStack() as es:
        w1 = es.enter_context(nc.sbuf_tensor("f_w1", [128, 8, DFF], BF16))
        w3 = es.enter_context(nc.sbuf_tensor("f_w3", [128, 8, DFF], BF16))
        w2 = es.enter_context(nc.sbuf_tensor("f_w2", [128, NFF, D], BF16))
        xt = [es.enter_context(nc.sbuf_tensor("f_x%d" % s, [128, 8, 512], F32)) for s in range(1)]
        ht = es.enter_context(nc.sbuf_tensor("f_h", [128, 8, 512], BF16))
        hid = es.enter_context(nc.sbuf_tensor("f_hid", [128, NFF, 512], BF16))
        sl_t = [es.enter_context(nc.sbuf_tensor("f_s%d" % s, [128, 512], F32)) for s in range(2)]
        ps1 = [es.enter_context(nc.psum_tensor("f_ps1%d" % s, [128, 512], F32)) for s in range(2)]
        ps3 = [es.enter_context(nc.psum_tensor("f_ps3%d" % s, [128, 512], F32)) for s in range(2)]
        pso = [es.enter_context(nc.psum_tensor("f_pso%d" % s, [128, 512], F32)) for s in range(2)]
        ntiles = alloc_norm_tiles(k, es, "f_", 512)
        A, Sh, G = emit_mod_scalars(k, es, "f_", l, jn, mi[0], mi[1], mi[2], 0.5)
        load_w_bf16(k, w1, k.w1[l, i], DFF, "w1")
        load_w_bf16(k, w3, k.w3[l, i], DFF, "w3")
        load_w_bf16(k, w2, k.w2[l, i], D, "w2", nk=NFF)
        it = 0
        for (b, t0, n, cond) in token_tiles(NB, 512):
            if last and cond == NB:
                continue
            s = 0
            it += 1
            xsrc = k.XT[b].rearrange("(c p) t -> p c t", p=128)[:, :, t0:t0 + n]
            P.dma(lambda e, s=s, xsrc=xsrc, n=n: e.dma_start(out=xt[s][:, :, :n], in_=xsrc), reads=[("XT", b)], writes=[("x", s)])
            emit_norm_mod(k, ntiles, xt[s], ht, n, A, Sh, cond, "modsc", s)
            for f in range(NFF):
                q = f % 2
                for kc in range(8):
                    P.op("pe", lambda e, f=f, q=q, kc=kc, n=n: e.matmul(ps1[q][:, :n], lhsT=w1[:, kc, f * 128:(f + 1) * 128], rhs=ht[:, kc, :n], start=(kc == 0), stop=(kc == 7)),
                         reads=[("w1", kc), "h"], writes=[("ps1", q)], inc=(kc == 7))
                for kc in range(8):
                    P.op("pe", lambda e, f=f, q=q, kc=kc, n=n: e.matmul(ps3[q][:, :n], lhsT=w3[:, kc, f * 128:(f + 1) * 128], rhs=ht[:, kc, :n], start=(kc == 0), stop=(kc == 7)),
                         reads=[("w3", kc), "h"], writes=[("ps3", q)], inc=(kc == 7))
                P.op("act", lambda e, q=q, n=n: e.activation(out=sl_t[q][:, :n], in_=ps1[q][:, :n], func=AF.Silu), reads=[("ps1", q)], writes=[("sl", q)])
                P.op("dve", lambda e, q=q, f=f, n=n: e.tensor_tensor(out=hid[:, f, :n], in0=sl_t[q][:, :n], in1=ps3[q][:, :n], op=ALU.mult), reads=[("sl", q), ("ps3", q)], writes=[("hid", f)])
            for m in range(8):
                q = m % 2
                for f in range(NFF):
                    P.op("pe", lambda e, m=m, q=q, f=f, n=n: e.matmul(pso[q][:, :n], lhsT=w2[:, f, m * 128:(m + 1) * 128], rhs=hid[:, f, :n], start=(f == 0), stop=(f == NFF - 1)),
                         reads=[("w2", f), ("hid", f)], writes=[("pso", q)], inc=(f == NFF - 1))
                P.op("dve", lambda e, m=m, q=q, s=s, n=n, cond=cond: e.scalar_tensor_tensor(out=xt[s][:, m, :n], in0=pso[q][:, :n], scalar=G[:, m, cond:cond + 1], in1=xt[s][:, m, :n], op0=ALU.mult, op1=ALU.add),
                     reads=[("pso", q), ("x", s), "modsg"], writes=[("x", s)])
            P.dma(lambda e, s=s, xsrc=xsrc, n=n: e.dma_start(out=xsrc, in_=xt[s][:, :, :n]), reads=[("x", s)], writes=[("XT", b)])
        P.end_stage()


def stage_inproj(k, l):
    nc, P, NB, NC = k.nc, k.P, k.NB, k.NC
    NCH = FM_W // 128
    with ExitStack() as es:
        wf = es.enter_context(nc.sbuf_tensor("p_wf", [128, 8, FM_W], BF16))
        wk = es.enter_context(nc.sbuf_tensor("p_wk", [128, 8, TOK_W], BF16))
        xt = [es.enter_context(nc.sbuf_tensor("p_x%d" % s, [128, 8, 512], F32)) for s in range(2)]
        ht = es.enter_context(nc.sbuf_tensor("p_h", [128, 8, 512], BF16))
        ot = [es.enter_context(nc.sbuf_tensor("p_o%d" % s, [128, 512], F32)) for s in range(4)]
        ps = [es.enter_context(nc.psum_tensor("p_ps%d" % s, [128, 512], F32)) for s in range(4)]
        ntiles = alloc_norm_tiles(k, es, "p_")
        A, Sh, G = emit_mod_scalars(k, es, "p_", l, 1, 3, 4, None, 1.0)
        load_w_bf16(k, wf, k.w_fm[l], FM_W, "wf")
        load_w_bf16(k, wk, k.w_tok[l], TOK_W, "wk")
        it = 0
        oi = 0
        for (b, t0, n, cond) in token_tiles(NB):
            s = it % 2
            it += 1
            xsrc = k.XT[b].rearrange("(c p) t -> p c t", p=128)[:, :, t0:t0 + n]
            P.dma(lambda e, s=s, xsrc=xsrc, n=n: e.dma_start(out=xt[s][:, :, :n], in_=xsrc), reads=[("XT", b)], writes=[("x", s)])
            emit_norm_mod(k, ntiles, xt[s], ht, n, A, Sh, cond, "modsc", s)
            for ch in range(NCH):
                q = oi % 4
                oi += 1
                for kc in range(8):
                    P.op("pe", lambda e, ch=ch, q=q, kc=kc, n=n: e.matmul(ps[q][:, :n], lhsT=wf[:, kc, ch * 128:(ch + 1) * 128], rhs=ht[:, kc, :n], start=(kc == 0), stop=(kc == 7)),
                         reads=[("wf", kc), "h"], writes=[("ps", q)], inc=(kc == 7))
                if oi % 2 == 0:
                    P.op("act", lambda e, q=q, n=n: e.activation(out=ot[q][:, :n], in_=ps[q][:, :n], func=AF.Identity), reads=[("ps", q)], writes=[("ot", q)])
                else:
                    P.op("dve", lambda e, q=q, n=n: e.tensor_copy(out=ot[q][:, :n], in_=ps[q][:, :n]), reads=[("ps", q)], writes=[("ot", q)])
                P.dma(lambda e, q=q, ch=ch, b=b, t0=t0, n=n: e.dma_start(out=k.PT[b, ch * 128:(ch + 1) * 128, t0:t0 + n], in_=ot[q][:, :n]), reads=[("ot", q)], writes=[("PT", b)])
            for tb in range(n // 128):
                q = oi % 4
                oi += 1
                for kc in range(8):
                    P.op("pe", lambda e, tb=tb, q=q, kc=kc: e.matmul(ps[q][:, :TOK_W], lhsT=ht[:, kc, tb * 128:(tb + 1) * 128], rhs=wk[:, kc, :], start=(kc == 0), stop=(kc == 7)),
                         reads=[("wk", kc), "h"], writes=[("ps", q)], inc=(kc == 7))
                P.op("dve", lambda e, q=q: e.tensor_copy(out=ot[q][:, :TOK_W], in_=ps[q][:, :TOK_W]), reads=[("ps", q)], writes=[("ot", q)])
                P.dma(lambda e, q=q, b=b, tb=tb, t0=t0: e.dma_start(out=k.TOK[b, t0 + tb * 128:t0 + (tb + 1) * 128, :], in_=ot[q][:, :TOK_W]), reads=[("ot", q)], writes=[("TOK", b)])
        P.end_stage()


def stage_attention(k, l):
    nc, P, NB = k.nc, k.P, k.NB
    NKB = T // 128
    with ExitStack() as es:
        al = lambda name, shape, dt=F32: es.enter_context(nc.sbuf_tensor("a_" + name, list(shape), dt))
        cos = al("cos", [64, SEQ]); sin = al("sin", [64, SEQ]); rm = al("rm", [64, 64]); o64 = al("o64", [64, 64])
        gq = al("gq", [64, 1]); gk = al("gk", [64, 1]); epsb = al("eps", [64, 1])
        onesb = al("onesb", [128, 64], BF16)
        raw = [al("raw%d" % i, [64, T]) for i in range(2)]
        kT = [al("kT%d" % i, [64, T], BF16) for i in range(2)]
        qT = [al("qT%d" % i, [64, T], BF16) for i in range(2)]
        vraw = al("vraw", [128, NKB, 128])
        vb = al("vb", [128, NKB, 128], BF16)
        sq = al("sq", [64, 512]); rs = al("rs", [64, 512]); kn = al("kn", [64, 512]); t1 = al("t1", [64, 512]); t2 = al("t2", [64, 512])
        pT = [al("pT%d" % i, [128, 512], BF16) for i in range(3)]
        rden = al("rden", [64, 512])
        ot = [al("ot%d" % i, [64, 512], BF16) for i in range(2)]
        psa = es.enter_context(nc.psum_tensor("a_psa", [64, 512], F32))
        psr = es.enter_context(nc.psum_tensor("a_psr", [64, 512], F32))
        pss = [es.enter_context(nc.psum_tensor("a_pss%d" % i, [128, 512], F32)) for i in range(3)]
        pso = es.enter_context(nc.psum_tensor("a_pso", [64, 512], F32))
        psd = es.enter_context(nc.psum_tensor("a_psd", [64, 512], F32))
        P.dma(lambda e: e.dma_start(out=cos[:], in_=k.rope_cos), writes=["cos"])
        P.dma(lambda e: e.dma_start(out=sin[:], in_=k.rope_sin), writes=["sin"])
        P.dma(lambda e: e.dma_start(out=rm[:], in_=k.rope_rm), writes=["rm"])
        P.dma(lambda e: e.dma_start(out=gq[:], in_=k.at_qn_g[l]), writes=["gq"])
        P.dma(lambda e: e.dma_start(out=gk[:], in_=k.at_kn_g[l]), writes=["gk"])
        P.op("dve", lambda e: e.memset(o64[:], 1.0 / 64), writes=["o64"])
        P.op("dve", lambda e: e.memset(epsb[:], EPS), writes=["epsb"])
        P.op("dve", lambda e: e.memset(onesb[:], 1.0), writes=["onesb"])
        cnt = {"raw": 0, "p": 0, "o": 0}

        def prep(b, row0, g_t, gtok, dst, dtok):
            ri = cnt["raw"] % 2
            cnt["raw"] += 1
            P.dma(lambda e: e.dma_start(out=raw[ri][:], in_=k.PT[b, row0:row0 + 64, :]), reads=[("PT", b)], writes=[("raw", ri)])
            for (t0, n) in [(0, CTX)] + [(CTX + i * 512, 512) for i in range(4)]:
                P.op("act", lambda e, t0=t0, n=n: e.activation(out=sq[:, :n], in_=raw[ri][:, t0:t0 + n], func=AF.Square), reads=[("raw", ri)], writes=["sq"])
                P.op("pe", lambda e, n=n: e.matmul(psa[:, :n], lhsT=o64[:], rhs=sq[:, :n], start=True, stop=True), reads=["sq", "o64"], writes=["psa"])
                P.op("act", lambda e, n=n: e.activation(out=rs[:, :n], in_=psa[:, :n], func=AF.Sqrt, bias=epsb[:, 0:1], scale=1.0), reads=["psa", "epsb"], writes=["rs0", "rs"])
                P.op("dve", lambda e, n=n: e.reciprocal(out=rs[:, :n], in_=rs[:, :n]), reads=["rs0"], writes=["rs"])
                if t0 == 0:
                    P.op("dve", lambda e, t0=t0, n=n: e.scalar_tensor_tensor(out=dst[:, t0:t0 + n], in0=raw[ri][:, t0:t0 + n], scalar=g_t[:, 0:1], in1=rs[:, :n], op0=ALU.mult, op1=ALU.mult),
                         reads=[("raw", ri), "rs", gtok], writes=[dtok])
                    continue
                P.op("dve", lambda e, t0=t0, n=n: e.scalar_tensor_tensor(out=kn[:, :n], in0=raw[ri][:, t0:t0 + n], scalar=g_t[:, 0:1], in1=rs[:, :n], op0=ALU.mult, op1=ALU.mult),
                     reads=[("raw", ri), "rs", gtok], writes=["kn"])
                P.op("pe", lambda e, n=n: e.matmul(psr[:, :n], lhsT=rm[:], rhs=kn[:, :n], start=True, stop=True), reads=["kn", "rm"], writes=["psr"])
                P.op("pool", lambda e, t0=t0, n=n: e.tensor_tensor(out=t1[:, :n], in0=kn[:, :n], in1=cos[:, t0 - CTX:t0 - CTX + n], op=ALU.mult), reads=["kn", "cos"], writes=["t1"])
                P.op("dve", lambda e, t0=t0, n=n: e.tensor_tensor(out=t2[:, :n], in0=psr[:, :n], in1=sin[:, t0 - CTX:t0 - CTX + n], op=ALU.mult), reads=["psr", "sin"], writes=["t2"])
                P.op("dve", lambda e, t0=t0, n=n: e.tensor_tensor(out=dst[:, t0:t0 + n], in0=t1[:, :n], in1=t2[:, :n], op=ALU.add), reads=["t1", "t2"], writes=[dtok])

        def do_tile(b, hq, kvh, qi, q0, nq, kbs):
            def s_mm(kb, pi):
                P.op("pe", lambda e: e.matmul(pss[pi][:, :nq], lhsT=kT[kvh][:, kb * 128:(kb + 1) * 128], rhs=qT[qi][:, q0:q0 + nq], start=True, stop=True),
                     reads=[("kT", kvh), ("qT", qi)], writes=[("pss", pi)])
            pis = []
            for j in range(len(kbs)):
                pis.append(cnt["p"] % 3)
                cnt["p"] += 1
            s_mm(kbs[0], pis[0])
            for j, kb in enumerate(kbs):
                if j + 1 < len(kbs):
                    s_mm(kbs[j + 1], pis[j + 1])
                pi = pis[j]
                P.op("act", lambda e, pi=pi: e.activation(out=pT[pi][:, :nq], in_=pss[pi][:, :nq], func=AF.Exp, scale=0.125), reads=[("pss", pi)], writes=[("pT", pi)])
                P.op("pe", lambda e, pi=pi, kb=kb, j=j: e.matmul(pso[:, :nq], lhsT=vb[:, kb, kvh * 64:(kvh + 1) * 64], rhs=pT[pi][:, :nq], start=(j == 0), stop=(j == len(kbs) - 1)),
                     reads=[("pT", pi), "vb"], writes=["pso"], inc=False)
                P.op("pe", lambda e, pi=pi, j=j: e.matmul(psd[:, :nq], lhsT=onesb[:], rhs=pT[pi][:, :nq], start=(j == 0), stop=(j == len(kbs) - 1)),
                     reads=[("pT", pi), "onesb"], writes=["psd"])
            oi = cnt["o"] % 2
            cnt["o"] += 1
            P.op("dve", lambda e: e.reciprocal(out=rden[:, :nq], in_=psd[:, :nq]), reads=["psd"], writes=["rden"])
            P.op("dve", lambda e: e.tensor_tensor(out=ot[oi][:, :nq], in0=pso[:, :nq], in1=rden[:, :nq], op=ALU.mult), reads=["pso", "rden"], writes=[("ot", oi)])
            P.dma(lambda e: e.dma_start(out=k.YT[b, 3, hq * 64:(hq + 1) * 64, q0:q0 + nq], in_=ot[oi][:, :nq]), reads=[("ot", oi)], writes=[("YT", b)])

        for b in range(NB):
            for kvh in range(2):
                prep(b, FM_AK + kvh * 64, gk, "gk", kT[kvh], ("kT", kvh))
            vsrc = k.TOK[b].rearrange("(blk p) c -> p blk c", p=128)[:, :, TK_AV:TK_AV + 128]
            P.dma(lambda e, vsrc=vsrc: e.dma_start(out=vraw[:], in_=vsrc), reads=[("TOK", b)], writes=["vraw"])
            P.op("pool", lambda e: e.tensor_copy(out=vb[:], in_=vraw[:]), reads=["vraw"], writes=["vb"])
            for hq in range(4):
                kvh = hq // 2
                qi = hq % 2
                prep(b, FM_AQ + hq * 64, gq, "gq", qT[qi], ("qT", qi))
                for (q0, nq, kbs) in [(0, CTX, [0, 1])] + [(CTX + i * 512, 512, list(range(NKB))) for i in range(4)]:
                    do_tile(b, hq, kvh, qi, q0, nq, kbs)
        P.end_stage()


def stage_hgrn2(k, l):
    nc, P, NB = k.nc, k.P, k.NB
    NCK = T // 64
    with ExitStack() as es:
        al = lambda name, shape, dt=F32: es.enter_context(nc.sbuf_tensor("h_" + name, list(shape), dt))
        m01 = al("m01", [128, T]); mk = al("mk", [128, 2, 64]); bd = al("bd", [128, 128])
        lg = al("lg", [128, 2, 2]); lb = al("lb", [128, 2]); oml = al("oml", [128, 2]); gn = al("gn", [128, 1]); epsb = al("eps", [128, 1])
        z = al("z", [128, T]); fgt = al("fgt", [128, T]); bb = al("bb", [128, T]); tmp = al("tmp", [128, T])
        ex = [al("ex%d" % i, [128, T]) for i in range(2)]
        q = al("q", [128, T]); kk = al("kk", [128, T]); kd = al("kd", [128, T]); O = al("O", [128, T])
        qt = al("qt", [128, T], BF16); ktI = [al("kt%d" % i, [128, T], BF16) for i in range(4)]; qd = al("qd", [128, T], BF16)
        dec = al("dec", [128, NCK, 1])
        Vb = al("Vb", [128, NCK, 256], BF16)
        kdT = [al("kdT%d" % i, [64, 128], BF16) for i in range(2)]
        scm = [al("scm%d" % i, [128, 64], BF16) for i in range(2)]
        S32 = al("S32", [128, 64]); S16 = [al("S16%d" % i, [128, 64], BF16) for i in range(2)]
        sq = al("sq", [128, 512]); rs = al("rs", [128, 512]); yb = [al("yb%d" % i, [128, 512], BF16) for i in range(2)]
        pst = [es.enter_context(nc.psum_tensor("h_pst%d" % i, [128, 512], F32)) for i in range(2)]
        pss = [es.enter_context(nc.psum_tensor("h_pss%d" % i, [128, 512], F32)) for i in range(2)]
        pso = [es.enter_context(nc.psum_tensor("h_pso%d" % i, [128, 512], F32)) for i in range(2)]
        pskv = [es.enter_context(nc.psum_tensor("h_pskv%d" % i, [128, 512], F32)) for i in range(2)]
        P.dma(lambda e: e.dma_start(out=mk[:], in_=k.hg_masks), writes=["mk"])
        P.dma(lambda e: e.dma_start(out=bd[:], in_=k.bd64), writes=["bd"])
        P.dma(lambda e: e.dma_start(out=lg[:], in_=k.hg_lb), writes=["lg"])
        P.dma(lambda e: e.dma_start(out=gn[:], in_=k.hg_norm_g[l]), writes=["gn"])
        P.op("dve", lambda e: e.memset(epsb[:], EPS), writes=["epsb"])
        P.op("dve", lambda e: e.memset(m01[:], 1.0), writes=["m01"])
        P.op("dve", lambda e: e.memset(m01[:].rearrange("p (c j) -> p c j", j=64)[:, :, 0:1], 0.0), writes=["m01"])
        for i4 in range(4):
            P.op("pool", lambda e, i4=i4: e.memset(ktI[i4][:], 0.0), writes=[("kt", i4)])
        if l == 0:
            P.op("dve", lambda e: e.memset(lb[:], 0.0), writes=["lb"])
            P.op("dve", lambda e: e.memset(oml[:], 1.0), writes=["oml"])
        else:
            P.op("dve", lambda e: e.tensor_tensor(out=lb[:], in0=lg[:, :, 1], in1=lg[:, :, 0], op=ALU.subtract), reads=["lg"], writes=["lb0"])
            P.op("act", lambda e: e.activation(out=lb[:], in_=lb[:], func=AF.Sigmoid), reads=["lb0"], writes=["lb", "lb0"])
            P.op("dve", lambda e: e.tensor_scalar(out=oml[:], in0=lb[:], scalar1=-1.0, scalar2=1.0, op0=ALU.mult, op1=ALU.add), reads=["lb"], writes=["oml"])
        cnt = {"c": 0, "y": 0}
        bb3 = bb[:].rearrange("p (c j) -> p c j", j=64)
        tmp3 = tmp[:].rearrange("p (c j) -> p c j", j=64)

        def do_chunk(b, hp, d, c, first):
            i = cnt["c"] % 2
            cnt["c"] += 1
            cs = slice(c * 64, (c + 1) * 64)
            P.op("pe", lambda e: e.transpose(pst[i][:64, :128], kd[:, cs], k.idn[:]), reads=["kd", "idn"], writes=[("pst", i)])
            P.op("act", lambda e: e.activation(out=kdT[i][:], in_=pst[i][:64, :128], func=AF.Identity), reads=[("pst", i)], writes=[("kdT", i)])
            for h2 in range(2):
                pb = h2 * 64
                for I in range(4):
                    ts_ = slice(c * 64 + 16 * I, c * 64 + 16 * I + 16)
                    P.op("pe", lambda e, pb=pb, I=I, ts_=ts_: e.matmul(pss[i][pb:pb + 64, 16 * I:16 * I + 16], lhsT=ktI[I][pb:pb + 64, cs], rhs=qt[pb:pb + 64, ts_], start=True, stop=True),
                         reads=[("kt", I), "qt"], writes=[("pss", i)], inc=(h2 == 1 and I == 3))
            for h2 in range(2):
                pb = h2 * 64
                h = hp * 2 + h2
                P.op("pe", lambda e, pb=pb, h=h: e.matmul(pskv[i][pb:pb + 64, :64], lhsT=kdT[i][:, pb:pb + 64], rhs=Vb[0:64, c, h * 64:(h + 1) * 64], start=True, stop=True),
                     reads=[("kdT", i), "Vb"], writes=[("pskv", i)], inc=(h2 == 1))
            P.op("dve", lambda e: e.tensor_tensor(out=scm[i][:], in0=pss[i][:, :64], in1=mk[:, d, :], op=ALU.mult), reads=[("pss", i), "mk"], writes=[("scm", i)])
            for h2 in range(2):
                pb = h2 * 64
                h = hp * 2 + h2
                P.op("pe", lambda e, pb=pb, h=h: e.matmul(pso[i][pb:pb + 64, :64], lhsT=Vb[pb:pb + 64, c, h * 64:(h + 1) * 64], rhs=scm[i][pb:pb + 64, :], start=True, stop=False),
                     reads=[("scm", i), "Vb"], writes=[("pso", i)], inc=False)
                P.op("pe", lambda e, pb=pb: e.matmul(pso[i][pb:pb + 64, :64], lhsT=S16[1 - i][pb:pb + 64, :], rhs=qd[pb:pb + 64, cs], start=False, stop=True),
                     reads=[("S16", 1 - i), "qd"], writes=[("pso", i)], inc=(h2 == 1))
            if d == 0:
                P.op("act", lambda e: e.activation(out=O[:, cs], in_=pso[i][:, :64], func=AF.Identity), reads=[("pso", i)], writes=["O"])
            else:
                P.op("dve", lambda e: e.tensor_tensor(out=O[:, cs], in0=O[:, cs], in1=pso[i][:, :64], op=ALU.add), reads=[("pso", i), "O"], writes=["O"])
            P.op("dve", lambda e: e.scalar_tensor_tensor(out=S32[:], in0=S32[:], scalar=dec[:, c, :], in1=pskv[i][:, :64], op0=ALU.mult, op1=ALU.add),
                 reads=["S32", "dec", ("pskv", i)], writes=["S32"])
            P.op("act", lambda e: e.activation(out=S16[i][:], in_=S32[:], func=AF.Identity), reads=["S32"], writes=[("S16", i)])

        def do_dir(b, hp, d):
            rev = (lambda ap: ap[:, ::-1]) if d == 1 else (lambda ap: ap)
            last = 63 if d == 0 else 0
            r0 = FM_HF + d * 256 + hp * 128
            P.dma(lambda e: e.dma_start(out=z[:], in_=k.PT[b, r0:r0 + 128, :]), reads=[("PT", b)], writes=["z"])
            P.op("act", lambda e: e.activation(out=fgt[:], in_=z[:], func=AF.Sigmoid), reads=["z"], writes=["fgt"])
            P.op("dve", lambda e: e.tensor_scalar(out=fgt[:], in0=fgt[:], scalar1=oml[:, hp:hp + 1], scalar2=lb[:, hp:hp + 1], op0=ALU.mult, op1=ALU.add), reads=["fgt", "oml", "lb"], writes=["fgt"])
            P.op("dve", lambda e: e.tensor_scalar(out=fgt[:], in0=fgt[:], scalar1=1e-30, scalar2=None, op0=ALU.max), reads=["fgt"], writes=["fgt"])
            P.op("act", lambda e: e.activation(out=z[:], in_=fgt[:], func=AF.Ln), reads=["fgt"], writes=["z"])
            P.op("dve", lambda e: e.tensor_scalar(out=kk[:], in0=fgt[:], scalar1=-1.0, scalar2=1.0, op0=ALU.mult, op1=ALU.add), reads=["fgt"], writes=["kk"])
            P.op("dve", lambda e: e.tensor_tensor_scan(out=rev(bb[:]), data0=m01[:], data1=rev(z[:]), initial=0.0, op0=ALU.mult, op1=ALU.add), reads=["z", "m01"], writes=["bb"])
            ref = 0 if d == 0 else 15
            bb4 = bb[:].rearrange("p (c i j) -> p c i j", i=4, j=16)
            tmp4 = tmp[:].rearrange("p (c i j) -> p c i j", i=4, j=16)
            P.op("dve", lambda e: e.tensor_tensor(out=tmp4, in0=bb4, in1=bb4[:, :, :, ref:ref + 1].to_broadcast([128, NCK, 4, 16]), op=ALU.subtract), reads=["bb"], writes=["tmp"])
            P.op("act", lambda e: e.activation(out=ex[0][:], in_=tmp[:], func=AF.Exp), reads=["tmp"], writes=[("ex", 0)])
            P.op("pool", lambda e: e.tensor_tensor(out=qt[:], in0=q[:], in1=ex[0][:], op=ALU.mult), reads=["q", ("ex", 0)], writes=["qt"])
            ex3 = [ex[i][:].rearrange("p (c j) -> p c j", j=64) for i in range(2)]
            kk3 = kk[:].rearrange("p (c j) -> p c j", j=64)
            for I in range(4):
                cs_ = slice(0, 16 * (I + 1)) if d == 0 else slice(16 * I, 64)
                w = cs_.stop - cs_.start
                e_ = ex3[I % 2][:, :, cs_]
                rp = 16 * I + ref
                kt3 = ktI[I][:].rearrange("p (c j) -> p c j", j=64)[:, :, cs_]
                P.op("dve", lambda e, cs_=cs_, w=w, rp=rp: e.scalar_tensor_tensor(out=tmp3[:, :, cs_], in0=bb3[:, :, cs_], scalar=-1.0, in1=bb3[:, :, rp:rp + 1].to_broadcast([128, NCK, w]), op0=ALU.mult, op1=ALU.add),
                     reads=["bb"], writes=["tmp"])
                P.op("dve", lambda e, cs_=cs_: e.tensor_scalar(out=tmp3[:, :, cs_], in0=tmp3[:, :, cs_], scalar1=60.0, scalar2=None, op0=ALU.min), reads=["tmp"], writes=["tmp"])
                P.op("act", lambda e, cs_=cs_, e_=e_: e.activation(out=e_, in_=tmp3[:, :, cs_], func=AF.Exp), reads=["tmp"], writes=[("ex", I % 2)])
                P.op("pool", lambda e, cs_=cs_, e_=e_, kt3=kt3: e.tensor_tensor(out=kt3, in0=kk3[:, :, cs_], in1=e_, op=ALU.mult), reads=["kk", ("ex", I % 2)], writes=[("kt", I)])
            P.op("act", lambda e: e.activation(out=ex[0][:], in_=bb[:], func=AF.Exp), reads=["bb"], writes=[("ex", 0)])
            P.op("dve", lambda e: e.tensor_tensor(out=qd[:], in0=q[:], in1=ex[0][:], op=ALU.mult), reads=["q", ("ex", 0)], writes=["qd"])
            P.op("dve", lambda e: e.tensor_tensor(out=tmp3, in0=bb3, in1=bb3[:, :, last:last + 1].to_broadcast([128, NCK, 64]), op=ALU.subtract), reads=["bb"], writes=["tmp"])
            P.op("act", lambda e: e.activation(out=ex[1][:], in_=tmp[:], func=AF.Exp, scale=-1.0), reads=["tmp"], writes=[("ex", 1)])
            P.op("pool", lambda e: e.tensor_tensor(out=kd[:], in0=kk[:], in1=ex[1][:], op=ALU.mult), reads=["kk", ("ex", 1)], writes=["kd"])
            P.op("act", lambda e: e.activation(out=dec[:], in_=bb3[:, :, last:last + 1], func=AF.Exp), reads=["bb"], writes=["dec"])
            if "dump" in k.opts and (b, hp, d) == k.opts["dump"]:
                for j, (tl, tk) in enumerate([(z, "z"), (bb, "bb"), (kd, "kd"), (kk, "kk"), (fgt, "fgt")]):
                    P.dma(lambda e, j=j, tl=tl: e.dma_start(out=k.dump[j], in_=tl[:]), reads=[tk], writes=[("dump", j)])
            P.op("dve", lambda e: e.memset(S32[:], 0.0), writes=["S32"])
            for i in range(2):
                P.op("dve", lambda e, i=i: e.memset(S16[i][:], 0.0), writes=[("S16", i)])
            order = list(range(NCK)) if d == 0 else [3, 2, 1, 0] + list(range(NCK - 1, 3, -1))
            for idx, c in enumerate(order):
                do_chunk(b, hp, d, c, idx == 0)

        def do_pair(b, hp):
            r0 = FM_HQ + hp * 128
            P.dma(lambda e: e.dma_start(out=q[:], in_=k.PT[b, r0:r0 + 128, :]), reads=[("PT", b)], writes=["q"])
            P.op("act", lambda e: e.activation(out=q[:], in_=q[:], func=AF.Silu), reads=["q"], writes=["q"])
            for d in range(2):
                do_dir(b, hp, d)
            if "dump" in k.opts and (b, hp) == k.opts["dump"][:2]:
                P.dma(lambda e: e.dma_start(out=k.dump[5], in_=O[:]), reads=["O"], writes=[("dump", 5)])
            g0 = FM_HG + hp * 128
            P.dma(lambda e: e.dma_start(out=z[:], in_=k.PT[b, g0:g0 + 128, :]), reads=[("PT", b)], writes=["z"])
            P.op("act", lambda e: e.activation(out=z[:], in_=z[:], func=AF.Sigmoid), reads=["z"], writes=["z"])
            for (t0, n) in [(0, CTX)] + [(CTX + i * 512, 512) for i in range(4)]:
                do_out(b, hp, t0, n)

        def do_out(b, hp, t0, n):
            yi = cnt["y"] % 2
            cnt["y"] += 1
            P.op("act", lambda e: e.activation(out=sq[:, :n], in_=O[:, t0:t0 + n], func=AF.Square), reads=["O"], writes=["sq"])
            P.op("pe", lambda e: e.matmul(pss[0][:, :n], lhsT=bd[:], rhs=sq[:, :n], start=True, stop=True), reads=["sq", "bd"], writes=[("pss", 0)])
            P.op("act", lambda e: e.activation(out=rs[:, :n], in_=pss[0][:, :n], func=AF.Sqrt, bias=epsb[:, 0:1], scale=1.0), reads=[("pss", 0), "epsb"], writes=["rs0", "rs"])
            P.op("dve", lambda e: e.reciprocal(out=rs[:, :n], in_=rs[:, :n]), reads=["rs0"], writes=["rs"])
            P.op("dve", lambda e: e.scalar_tensor_tensor(out=sq[:, :n], in0=O[:, t0:t0 + n], scalar=gn[:, 0:1], in1=rs[:, :n], op0=ALU.mult, op1=ALU.mult), reads=["O", "gn", "rs"], writes=["sq"])
            P.op("pool", lambda e: e.tensor_tensor(out=yb[yi][:, :n], in0=sq[:, :n], in1=z[:, t0:t0 + n], op=ALU.mult), reads=["sq", "z"], writes=[("yb", yi)])
            P.dma(lambda e: e.dma_start(out=k.YT[b, 2, hp * 128:(hp + 1) * 128, t0:t0 + n], in_=yb[yi][:, :n]), reads=[("yb", yi)], writes=[("YT", b)])

        for b in range(NB):
            vsrc = k.TOK[b].rearrange("(c s) v -> s c v", s=64)[:, :, TK_HV:TK_HV + 256]
            for h2 in range(2):
                P.dma(lambda e, h2=h2, vsrc=vsrc: e.dma_start(out=Vb[h2 * 64:(h2 + 1) * 64, :, :], in_=vsrc), reads=[("TOK", b)], writes=["Vb"], q="pool")
            for hp in range(2):
                do_pair(b, hp)
        P.end_stage()


class Slots:
    def __init__(self, aps, name, mod=0, off=0):
        self.aps, self.name, self.i, self.mod, self.off = aps, name, 0, mod, off

    def get(self):
        j = self.i % len(self.aps)
        self.i += 1
        return self.aps[j], (self.name, self.off + (j % self.mod if self.mod else j))


def stage_deltanet(k, l):
    nc, P, NB = k.nc, k.P, k.NB
    NCK = T // 64
    P.serial = k.opts.get("serial_dn", SERIAL_DN)
    with ExitStack() as es:
        al = lambda name, shape, dt=F32: es.enter_context(nc.sbuf_tensor("d_" + name, list(shape), dt))
        gm64 = al("gm64", [64, 2, 2, 128]); gmbd = al("gmbd", [128, 2, 2, 128]); idn2 = al("idn2", [128, 64]); bd = al("bd", [128, 128])
        ones64 = al("ones64", [64, 128])
        cw = al("cw", [128, 6, 5]); alog = al("alog", [128, 8]); dtb = al("dtb", [128, 8]); gno = al("gno", [128, 64]); epsb = al("eps", [128, 1])
        qkv = [al("qkv%d" % i, [128, T]) for i in range(6)]
        xin = al("xin", [128, T]); acc = al("acc", [128, T])
        sq = al("sq", [128, 512]); rs = al("rs", [128, 512])
        ab = al("ab", [128, NCK, 16]); gt = al("gt", [128, NCK, 8]); bt = al("bt", [128, NCK, 8]); t8 = al("t8", [128, NCK, 8]); t8b = al("t8b", [128, NCK, 8])
        gcs = al("gcs", [128, NCK, 8]); gts = al("gts", [128, NCK, 8])
        gcBD = al("gcBD", [128, NCK, 4]); gtBD = al("gtBD", [128, NCK, 4]); bBD = al("bBD", [128, NCK, 4])
        eg = al("eg", [128, NCK, 4]); ekd = al("ekd", [128, NCK, 4]); glast = al("glast", [128, NCK, 4]); nbeta = al("nbeta", [128, NCK, 4]); wsc = al("wsc", [128, NCK, 4])
        u_all = al("u_all", [128, NCK, 64]); wT_all = al("wT_all", [128, NCK, 64]); kdec_all = al("kdec_all", [128, NCK, 64]); attnT_all = al("attnT_all", [128, NCK, 128])
        O_tok = al("O_tok", [128, NCK, 64]); ssq = al("ssq", [128, NCK]); S = al("S", [128, 64])
        yb = [al("yb%d" % i, [128, 512], BF16) for i in range(2)]
        wk = Slots([al("wk%d" % i, [128, 128])[:] for i in range(28)], "wk")
        wv = Slots([al("wv%d" % i, [128, 64])[:] for i in range(8)], "wv")
        wr = Slots([al("wr%d" % i, [128, 128])[:] for i in range(6)], "wr")
        banks = [es.enter_context(nc.psum_tensor("d_ps%d" % i, [128, 512], F32)) for i in range(8)]
        ps = Slots([banks[i][:, j * 128:(j + 1) * 128] for j in range(4) for i in range(4)], "ps", mod=4)
        ps2 = Slots([banks[4 + i][:, j * 256:(j + 1) * 256] for j in range(2) for i in range(3)], "ps", mod=3, off=4)
        wk2 = Slots([al("wkp%d" % i, [128, 256])[:] for i in range(10)], "wk2")
        psg = banks[7]
        for i in range(8):
            P.op("dve", lambda e, i=i: e.memset(banks[i][:], 0.0), writes=[("ps", j) for j in range(7)] + ["psg"])
        P.dma(lambda e: e.dma_start(out=gm64[:], in_=k.dn_gm64), writes=["gm64"])
        P.dma(lambda e: e.dma_start(out=gmbd[:], in_=k.dn_gmbd), writes=["gmbd"])
        P.dma(lambda e: e.dma_start(out=idn2[:], in_=k.dn_idn2), writes=["idn2"])
        P.dma(lambda e: e.dma_start(out=bd[:], in_=k.bd64), writes=["bd"])
        P.dma(lambda e: e.dma_start(out=cw[:], in_=k.dn_conv[l]), writes=["cw"])
        P.dma(lambda e: e.dma_start(out=alog[:], in_=k.dn_a_log[l]), writes=["alog"])
        P.dma(lambda e: e.dma_start(out=dtb[:], in_=k.dn_dt_bias[l]), writes=["dtb"])
        P.dma(lambda e: e.dma_start(out=gno[:], in_=k.dn_norm_g[l]), writes=["gno"])
        P.op("dve", lambda e: e.memset(epsb[:], EPS), writes=["epsb"])
        P.op("dve", lambda e: e.memset(ones64[:], 1.0), writes=["ones64"])
        P.op("act", lambda e: e.activation(out=alog[:], in_=alog[:], func=AF.Exp), reads=["alog"], writes=["alog"])
        P.op("dve", lambda e: e.tensor_scalar(out=alog[:], in0=alog[:], scalar1=-1.0, scalar2=None, op0=ALU.mult), reads=["alog"], writes=["alog"])
        cnt = {"y": 0}

        def prep_tile(b, ti):
            r0 = ti * 128
            P.dma(lambda e: e.dma_start(out=xin[:], in_=k.PT[b, r0:r0 + 128, :]), reads=[("PT", b)], writes=["xin"])
            P.op("dve", lambda e: e.tensor_scalar(out=acc[:], in0=xin[:], scalar1=cw[:, ti, 2:3], scalar2=None, op0=ALU.mult), reads=["xin", "cw"], writes=["acc"])
            for (s0, s1) in [(0, CTX), (CTX, T)]:
                for j in (0, 1, 3, 4):
                    sh = j - 2
                    o0, o1 = max(s0, s0 - sh), min(s1, s1 - sh)
                    P.op("dve", lambda e, j=j, sh=sh, o0=o0, o1=o1: e.scalar_tensor_tensor(out=acc[:, o0:o1], in0=xin[:, o0 + sh:o1 + sh], scalar=cw[:, ti, j:j + 1], in1=acc[:, o0:o1], op0=ALU.mult, op1=ALU.add),
                         reads=["xin", "cw", "acc"], writes=["acc"])
            dst = qkv[ti]
            if ti >= 4:
                P.op("act", lambda e: e.activation(out=dst[:], in_=acc[:], func=AF.Silu), reads=["acc"], writes=[("qkv", ti)])
                return
            P.op("act", lambda e: e.activation(out=acc[:], in_=acc[:], func=AF.Silu), reads=["acc"], writes=["acc"])
            qs = 0.125 if ti < 2 else 1.0
            for (t0, n) in [(0, CTX)] + [(CTX + i * 512, 512) for i in range(4)]:
                norm_tile(dst, ti, t0, n, qs)

        def norm_tile(dst, ti, t0, n, qs):
            pp, pt = ps.get()
            bank_ap = banks[0]
            P.op("act", lambda e: e.activation(out=sq[:, :n], in_=acc[:, t0:t0 + n], func=AF.Square), reads=["acc"], writes=["sq"])
            P.op("pe", lambda e: e.matmul(psg[:, :n], lhsT=bd[:], rhs=sq[:, :n], start=True, stop=True), reads=["sq", "bd"], writes=["psg"])
            P.op("act", lambda e: e.activation(out=rs[:, :n], in_=psg[:, :n], func=AF.Sqrt, bias=epsb[:, 0:1], scale=64.0), reads=["psg", "epsb"], writes=["rs0", "rs"])
            P.op("dve", lambda e: e.reciprocal(out=rs[:, :n], in_=rs[:, :n]), reads=["rs0"], writes=["rs"])
            P.op("dve", lambda e: e.scalar_tensor_tensor(out=dst[:, t0:t0 + n], in0=acc[:, t0:t0 + n], scalar=qs, in1=rs[:, :n], op0=ALU.mult, op1=ALU.mult), reads=["acc", "rs"], writes=[("qkv", ti)])

        def prep_gates(b):
            src = k.TOK[b].rearrange("(c s) v -> s c v", s=64)[:, :, TK_A:TK_A + 16]
            for h2 in range(2):
                P.dma(lambda e, h2=h2: e.dma_start(out=ab[h2 * 64:(h2 + 1) * 64, :, :], in_=src), reads=[("TOK", b)], writes=["ab"])
            a3, b3 = ab[:, :, 0:8], ab[:, :, 8:16]
            bc8 = lambda t: t[:, :].rearrange("p (o h) -> p o h", o=1).to_broadcast([128, NCK, 8])
            P.op("dve", lambda e: e.tensor_tensor(out=t8[:], in0=a3, in1=bc8(dtb), op=ALU.add), reads=["ab", "dtb"], writes=["t8"])
            P.op("act", lambda e: e.activation(out=t8b[:], in_=t8[:], func=AF.Abs), reads=["t8"], writes=["t8b"])
            P.op("act", lambda e: e.activation(out=t8b[:], in_=t8b[:], func=AF.Exp, scale=-1.0), reads=["t8b"], writes=["t8b"])
            P.op("dve", lambda e: e.tensor_scalar(out=t8b[:], in0=t8b[:], scalar1=1.0, scalar2=None, op0=ALU.add), reads=["t8b"], writes=["t8b"])
            P.op("act", lambda e: e.activation(out=t8b[:], in_=t8b[:], func=AF.Ln), reads=["t8b"], writes=["t8b"])
            P.op("dve", lambda e: e.tensor_scalar(out=t8[:], in0=t8[:], scalar1=0.0, scalar2=None, op0=ALU.max), reads=["t8"], writes=["t8"])
            P.op("dve", lambda e: e.tensor_tensor(out=t8[:], in0=t8[:], in1=t8b[:], op=ALU.add), reads=["t8", "t8b"], writes=["t8"])
            P.op("dve", lambda e: e.tensor_tensor(out=gt[:], in0=t8[:], in1=bc8(alog), op=ALU.mult), reads=["t8", "alog"], writes=["gt"])
            P.op("act", lambda e: e.activation(out=bt[:], in_=b3, func=AF.Sigmoid), reads=["ab"], writes=["bt"])
            for d in range(2):
                P.op("pe", lambda e, d=d: e.matmul(psg[:, d * 144:(d + 1) * 144], lhsT=gm64[:, d, 0, :], rhs=gt[0:64, :, d * 4:(d + 1) * 4], start=True, stop=True), reads=["gm64", "gt"], writes=["psg"])
            P.op("dve", lambda e: e.tensor_copy(out=gcs[:].rearrange("p c (d h) -> p d c h", d=2), in_=psg[:, 0:288].rearrange("p (d c h) -> p d c h", d=2, h=4)), reads=["psg"], writes=["gcs"])
            for d in range(2):
                P.op("pe", lambda e, d=d: e.matmul(psg[:, d * 144:(d + 1) * 144], lhsT=ones64[:], rhs=gt[0:64, :, d * 4:(d + 1) * 4], start=True, stop=True), reads=["ones64", "gt"], writes=["psg"])
            P.op("dve", lambda e: e.tensor_copy(out=gts[:].rearrange("p c (d h) -> p d c h", d=2), in_=psg[:, 0:288].rearrange("p (d c h) -> p d c h", d=2, h=4)), reads=["psg"], writes=["gts"])
            for m in range(4):
                d, hp = m // 2, m % 2
                for h2 in range(2):
                    col = d * 4 + hp * 2 + h2
                    rr = slice(h2 * 64, (h2 + 1) * 64)
                    P.op("dve", lambda e, m=m, col=col, rr=rr: e.tensor_copy(out=gcBD[rr, :, m:m + 1], in_=gcs[rr, :, col:col + 1]), reads=["gcs"], writes=["gcBD"])
                    P.op("dve", lambda e, m=m, col=col, rr=rr: e.tensor_copy(out=gtBD[rr, :, m:m + 1], in_=gts[rr, :, col:col + 1]), reads=["gts"], writes=["gtBD"])
                    P.op("dve", lambda e, m=m, col=col, rr=rr: e.tensor_copy(out=bBD[rr, :, m:m + 1], in_=bt[rr, :, col:col + 1]), reads=["bt"], writes=["bBD"])
            P.op("act", lambda e: e.activation(out=eg[:], in_=gcBD[:], func=AF.Exp), reads=["gcBD"], writes=["eg"])
            P.op("act", lambda e: e.activation(out=glast[:], in_=gtBD[:], func=AF.Exp), reads=["gtBD"], writes=["glast"])
            P.op("dve", lambda e: e.tensor_tensor(out=ekd[:], in0=gtBD[:], in1=gcBD[:], op=ALU.subtract), reads=["gtBD", "gcBD"], writes=["ekd"])
            P.op("act", lambda e: e.activation(out=ekd[:], in_=ekd[:], func=AF.Exp), reads=["ekd"], writes=["ekd"])
            P.op("dve", lambda e: e.tensor_scalar(out=nbeta[:], in0=bBD[:], scalar1=-1.0, scalar2=None, op0=ALU.mult), reads=["bBD"], writes=["nbeta"])
            P.op("dve", lambda e: e.tensor_tensor(out=wsc[:], in0=bBD[:], in1=eg[:], op=ALU.mult), reads=["bBD", "eg"], writes=["wsc"])

        def phase_a_steps(m, c):
            d, hp = m // 2, m % 2
            qn, kn, vn = qkv[hp], qkv[2 + hp], qkv[4 + hp]
            qtk, ktk, vtk = ("qkv", hp), ("qkv", 2 + hp), ("qkv", 4 + hp)
            cs = slice(c * 64, (c + 1) * 64)
            hd0 = d * 4 + hp * 2
            X = {}

            def s1():
                G2, G2t = wk2.get()
                Gmt = Git = G2t
                Gi, Gm = G2[:, 0:128], G2[:, 128:256]
                g4 = gt[0:64, c, hd0:hd0 + 2].rearrange("p (a h o) -> p a h o", a=1, o=1).to_broadcast([64, 2, 2, 64])
                P.op("dve", lambda e: e.tensor_tensor(out=G2[0:64, :].rearrange("p (a h j) -> p a h j", a=2, h=2), in0=g4, in1=gm64[:, d, :, :].rearrange("p a (h j) -> p a h j", h=2), op=ALU.mult), reads=["gt", "gm64"], writes=[G2t])
                pDD, pDt = ps2.get()
                pDTt = pDt
                pD, pDT = pDD[:, 0:128], pDD[:, 128:256]
                P.op("pe", lambda e: e.matmul(pD, lhsT=gm64[:, d, 0, :], rhs=Gm[0:64, :], start=True, stop=True), reads=["gm64", Gmt], writes=[pDt], inc=False)
                P.op("pe", lambda e: e.matmul(pDT, lhsT=gm64[:, d, 1, :], rhs=Gi[0:64, :], start=True, stop=True), reads=["gm64", Git], writes=[pDTt])
                pKK, pKKt = ps.get()
                pQK, pQKt = ps.get()
                pTok, pTokt = ps.get()
                for h2 in range(2):
                    pb = h2 * 64
                    P.op("pe", lambda e, pb=pb: e.matmul(pKK[pb:pb + 64, pb:pb + 64], lhsT=kn[pb:pb + 64, cs], rhs=kn[pb:pb + 64, cs], start=True, stop=True), reads=[ktk], writes=[pKKt], inc=False)
                    P.op("pe", lambda e, pb=pb: e.matmul(pQK[pb:pb + 64, pb:pb + 64], lhsT=kn[pb:pb + 64, cs], rhs=qn[pb:pb + 64, cs], start=True, stop=True), reads=[ktk, qtk], writes=[pQKt], inc=False)
                    P.op("pe", lambda e, pb=pb: e.matmul(pTok[pb:pb + 64, 0:64], lhsT=kn[pb:pb + 64, cs], rhs=idn2[pb:pb + 64, :], start=True, stop=True), reads=[ktk, "idn2"], writes=[pTokt], inc=False)
                    P.op("pe", lambda e, pb=pb: e.matmul(pTok[pb:pb + 64, 64:128], lhsT=vn[pb:pb + 64, cs], rhs=idn2[pb:pb + 64, :], start=True, stop=True), reads=[vtk, "idn2"], writes=[pTokt], inc=(h2 == 1))
                X.update(pDD=pDD, pD=pD, pDt=pDt, pDT=pDT, pDTt=pDTt, pKK=pKK, pKKt=pKKt, pQK=pQK, pQKt=pQKt, pTok=pTok, pTokt=pTokt)

            def s2():
                x = dict(X)
                DD, Dt = wk2.get()
                DTt = Dt
                D, DT = DD[:, 0:128], DD[:, 128:256]
                P.op("act", lambda e: e.activation(out=DD, in_=x["pDD"], func=AF.Exp), reads=[x["pDt"]], writes=[Dt])
                P.op("dve", lambda e: e.tensor_tensor(out=DD.rearrange("p (a j) -> p a j", a=2), in0=DD.rearrange("p (a j) -> p a j", a=2), in1=gmbd[:, d, :, :], op=ALU.mult), reads=[Dt, "gmbd"], writes=[Dt])
                N, Nt = wk.get()
                X["n"] = X.get("n", 0) + 1
                if X["n"] <= k.opts.get("s2n", 99):
                    P.op("dve", lambda e: e.scalar_tensor_tensor(out=N, in0=x["pKK"], scalar=nbeta[:, c, m:m + 1], in1=D, op0=ALU.mult, op1=ALU.mult), reads=[x["pKKt"], "nbeta", Dt], writes=[Nt])
                X["n"] = X.get("n", 0) + 1
                if X["n"] <= k.opts.get("s2n", 99):
                    P.op("dve", lambda e: e.tensor_tensor(out=attnT_all[:, c, :], in0=x["pQK"], in1=DT, op=ALU.mult), reads=[x["pQKt"], DTt], writes=[("attnT", c)])
                rhs, rhst = wr.get()
                X["n"] = X.get("n", 0) + 1
                if X["n"] <= k.opts.get("s2n", 99):
                    P.op("dve", lambda e: e.tensor_scalar(out=rhs[:, 0:64], in0=x["pTok"][:, 0:64], scalar1=wsc[:, c, m:m + 1], scalar2=None, op0=ALU.mult), reads=[x["pTokt"], "wsc"], writes=[rhst])
                X["n"] = X.get("n", 0) + 1
                if X["n"] <= k.opts.get("s2n", 99):
                    P.op("dve", lambda e: e.tensor_scalar(out=rhs[:, 64:128], in0=x["pTok"][:, 64:128], scalar1=bBD[:, c, m:m + 1], scalar2=None, op0=ALU.mult), reads=[x["pTokt"], "bBD"], writes=[rhst])
                X["n"] = X.get("n", 0) + 1
                if X["n"] <= k.opts.get("s2n", 99):
                    P.op("dve", lambda e: e.tensor_scalar(out=kdec_all[:, c, :], in0=x["pTok"][:, 0:64], scalar1=ekd[:, c, m:m + 1], scalar2=None, op0=ALU.mult), reads=[x["pTokt"], "ekd"], writes=[("kdec", c)])
                X.update(N=N, Nt=Nt, rhs=rhs, rhst=rhst)

            def s3():
                x = dict(X)
                pNT, pNTt = ps.get()
                P.op("pe", lambda e: e.transpose(pNT, x["N"], k.idn[:]), reads=[x["Nt"], "idn"], writes=[pNTt])
                X.update(pNT=pNT, pNTt=pNTt)

            def s4():
                x = dict(X)
                PT_, PTt = wk.get()
                XT, XTt = wk.get()
                P.op("dve", lambda e: e.tensor_copy(out=PT_, in_=x["pNT"]), reads=[x["pNTt"]], writes=[PTt])
                P.op("dve", lambda e: e.tensor_tensor(out=XT, in0=x["pNT"], in1=k.idn[:], op=ALU.add), reads=[x["pNTt"], "idn"], writes=[XTt])
                X.update(P=x["N"], Pt=x["Nt"], PT=PT_, PTt=PTt, XT=XT, XTt=XTt)

            def lvl_mm(kk):
                def f():
                    x = dict(X)
                    pPP, pPt = ps2.get()
                    pP, pPT = pPP[:, 0:128], pPP[:, 128:256]
                    P.op("pe", lambda e: e.matmul(pP, lhsT=x["PT"], rhs=x["P"], start=True, stop=True), reads=[x["PTt"], x["Pt"]], writes=[pPt], inc=(kk == 5))
                    if kk < 5:
                        P.op("pe", lambda e: e.matmul(pPT, lhsT=x["P"], rhs=x["PT"], start=True, stop=True), reads=[x["PTt"], x["Pt"]], writes=[pPt])
                    X.update(pPP=pPP, pPt=pPt)
                return f

            def lvl_ev(kk):
                def f():
                    x = dict(X)
                    nPP, nPt = wk2.get()
                    nP, nPT = nPP[:, 0:128], nPP[:, 128:256]
                    w_ = 256 if kk < 5 else 128
                    P.op("dve", lambda e: e.tensor_copy(out=nPP[:, 0:w_], in_=x["pPP"][:, 0:w_]), reads=[x["pPt"]], writes=[nPt])
                    X.update(P=nP, Pt=nPt, PT=nPT, PTt=nPt)
                    pX, pXt = ps.get()
                    P.op("pe", lambda e: e.matmul(pX, lhsT=nP, rhs=x["XT"], start=True, stop=True), reads=[nPt, x["XTt"]], writes=[pXt])
                    X.update(pX=pX, pXt=pXt)
                return f

            def lvl_acc(kk):
                def f():
                    x = dict(X)
                    nX, nXt = wk.get()
                    P.op("dve", lambda e: e.tensor_tensor(out=nX, in0=x["XT"], in1=x["pX"], op=ALU.add), reads=[x["XTt"], x["pXt"]], writes=[nXt])
                    X.update(XT=nX, XTt=nXt)
                return f

            def s_sol():
                x = dict(X)
                pU, pUt = ps.get()
                pW, pWt = ps.get()
                P.op("pe", lambda e: e.matmul(pU[:, 0:64], lhsT=x["XT"], rhs=x["rhs"][:, 64:128], start=True, stop=True), reads=[x["XTt"], x["rhst"]], writes=[pUt], inc=False)
                for h2 in range(2):
                    pb = h2 * 64
                    P.op("pe", lambda e, pb=pb: e.matmul(pW[pb:pb + 64, 0:64], lhsT=x["rhs"][pb:pb + 64, 0:64], rhs=x["XT"][pb:pb + 64, pb:pb + 64], start=True, stop=True), reads=[x["XTt"], x["rhst"]], writes=[pWt], inc=(h2 == 1))
                X.update(pU=pU, pUt=pUt, pW=pW, pWt=pWt)

            def s_solev():
                x = dict(X)
                P.op("dve", lambda e: e.tensor_copy(out=u_all[:, c, :], in_=x["pU"][:, 0:64]), reads=[x["pUt"]], writes=[("u", c)])
                P.op("dve", lambda e: e.tensor_copy(out=wT_all[:, c, :], in_=x["pW"][:, 0:64]), reads=[x["pWt"]], writes=[("wT", c)])

            steps = [s1, s2, s3, s4]
            for kk in range(1, 6):
                steps += [lvl_mm(kk), lvl_ev(kk), lvl_acc(kk)]
            steps += [s_sol, s_solev]
            return steps[:k.opts.get("dn_steps", 99)]

        def phase_b_chunk(m, c, first_dir):
            d, hp = m // 2, m % 2
            qn = qkv[hp]
            cs = slice(c * 64, (c + 1) * 64)
            p1, p1t = ps.get()
            p2, p2t = ps.get()
            for h2 in range(2):
                pb = h2 * 64
                P.op("pe", lambda e, pb=pb: e.matmul(p1[pb:pb + 64, 0:64], lhsT=wT_all[pb:pb + 64, c, :], rhs=S[pb:pb + 64, :], start=True, stop=True), reads=[("wT", c), "S"], writes=[p1t], inc=False)
                P.op("pe", lambda e, pb=pb: e.matmul(p2[pb:pb + 64, 0:64], lhsT=qn[pb:pb + 64, cs], rhs=S[pb:pb + 64, :], start=True, stop=True), reads=[("qkv", hp), "S"], writes=[p2t], inc=(h2 == 1))
            vn_, vnt = wv.get()
            P.op("dve", lambda e: e.tensor_tensor(out=vn_, in0=u_all[:, c, :], in1=p1[:, 0:64], op=ALU.subtract), reads=[("u", c), p1t], writes=[vnt])
            p3, p3t = ps.get()
            p4, p4t = ps.get()
            P.op("pe", lambda e: e.matmul(p3[:, 0:64], lhsT=attnT_all[:, c, :], rhs=vn_, start=True, stop=True), reads=[("attnT", c), vnt], writes=[p3t], inc=False)
            for h2 in range(2):
                pb = h2 * 64
                P.op("pe", lambda e, pb=pb: e.matmul(p4[pb:pb + 64, 0:64], lhsT=kdec_all[pb:pb + 64, c, :], rhs=vn_[pb:pb + 64, :], start=True, stop=True), reads=[("kdec", c), vnt], writes=[p4t], inc=(h2 == 1))
            t_, tt_ = wv.get()
            P.op("dve", lambda e: e.tensor_scalar(out=t_, in0=p2[:, 0:64], scalar1=eg[:, c, m:m + 1], scalar2=None, op0=ALU.mult), reads=[p2t, "eg"], writes=[tt_])
            if first_dir:
                P.op("dve", lambda e: e.tensor_tensor(out=O_tok[:, c, :], in0=t_, in1=p3[:, 0:64], op=ALU.add), reads=[tt_, p3t], writes=[("O", c)])
            else:
                P.op("dve", lambda e: e.tensor_tensor(out=t_, in0=t_, in1=p3[:, 0:64], op=ALU.add), reads=[tt_, p3t], writes=[tt_])
                P.op("dve", lambda e: e.tensor_tensor(out=O_tok[:, c, :], in0=O_tok[:, c, :], in1=t_, op=ALU.add), reads=[tt_, ("O", c)], writes=[("O", c)])
            P.op("dve", lambda e: e.scalar_tensor_tensor(out=S[:], in0=S[:], scalar=glast[:, c, m:m + 1], in1=p4[:, 0:64], op0=ALU.mult, op1=ALU.add), reads=["S", "glast", p4t], writes=["S"])

        def out_phase(b, hp):
            z, yfm = xin, acc
            r0 = FM_Z + hp * 128
            P.dma(lambda e: e.dma_start(out=z[:], in_=k.PT[b, r0:r0 + 128, :]), reads=[("PT", b)], writes=["xin"])
            P.op("act", lambda e: e.activation(out=z[:], in_=z[:], func=AF.Silu), reads=["xin"], writes=["xin"])
            allO = [("O", c) for c in range(NCK)]
            P.op("dve", lambda e: e.tensor_tensor(out=u_all[:], in0=O_tok[:], in1=O_tok[:], op=ALU.mult), reads=allO, writes=[("u", c) for c in range(NCK)])
            P.op("dve", lambda e: e.tensor_reduce(out=ssq[:], in_=u_all[:], axis=AX.X, op=ALU.add), reads=[("u", c) for c in range(NCK)], writes=["ssq"])
            P.op("act", lambda e: e.activation(out=ssq[:], in_=ssq[:], func=AF.Sqrt, bias=epsb[:, 0:1], scale=1.0 / 64), reads=["ssq", "epsb"], writes=["ssq"])
            P.op("dve", lambda e: e.reciprocal(out=ssq[:], in_=ssq[:]), reads=["ssq"], writes=["ssq"])
            P.op("dve", lambda e: e.tensor_tensor(out=O_tok[:], in0=O_tok[:], in1=ssq[:].rearrange("p (c o) -> p c o", o=1).to_broadcast([128, NCK, 64]), op=ALU.mult), reads=allO + ["ssq"], writes=allO)
            P.op("dve", lambda e: e.tensor_tensor(out=O_tok[:], in0=O_tok[:], in1=gno[:].rearrange("p (o v) -> p o v", o=1).to_broadcast([128, NCK, 64]), op=ALU.mult), reads=allO + ["gno"], writes=allO)
            for c in range(NCK):
                out_chunk(c)
            for (t0, n) in [(0, CTX)] + [(CTX + i * 512, 512) for i in range(4)]:
                out_tile(b, hp, t0, n)

        def out_chunk(c):
            pp, ppt = ps.get()
            for h2 in range(2):
                pb = h2 * 64
                P.op("pe", lambda e, pb=pb: e.matmul(pp[pb:pb + 64, 0:64], lhsT=O_tok[pb:pb + 64, c, :], rhs=idn2[pb:pb + 64, :], start=True, stop=True), reads=[("O", c), "idn2"], writes=[ppt], inc=(h2 == 1))
            P.op("dve", lambda e: e.tensor_copy(out=acc[:, c * 64:(c + 1) * 64], in_=pp[:, 0:64]), reads=[ppt], writes=["acc"])

        def out_tile(b, hp, t0, n):
            yi = cnt["y"] % 2
            cnt["y"] += 1
            P.op("dve", lambda e: e.tensor_tensor(out=yb[yi][:, :n], in0=acc[:, t0:t0 + n], in1=xin[:, t0:t0 + n], op=ALU.mult), reads=["acc", "xin"], writes=[("yb", yi)])
            P.dma(lambda e: e.dma_start(out=k.YT[b, 0, hp * 128:(hp + 1) * 128, t0:t0 + n], in_=yb[yi][:, :n]), reads=[("yb", yi)], writes=[("YT", b)])

        G = 3
        for b in range(NB):
            for ti in range(6):
                prep_tile(b, ti)
            prep_gates(b)
            lim = k.opts.get("dn_lim", 99)
            if lim == 0:
                continue
            for hp in range(2):
                for d in range(2):
                    m = d * 2 + hp
                    for c0 in range(0, NCK if lim >= 2 else G, G):
                        lists = [phase_a_steps(m, c) for c in range(c0, min(NCK, c0 + G))]
                        for si in range(len(lists[0])):
                            for lst in lists:
                                lst[si]()
                    if lim < 3:
                        continue
                    P.op("dve", lambda e: e.memset(S[:], 0.0), writes=["S"])
                    order = list(range(NCK)) if d == 0 else [3, 2, 1, 0] + list(range(NCK - 1, 3, -1))
                    for c in order:
                        phase_b_chunk(m, c, d == 0)
                    if "dump" in k.opts and (b, m) == k.opts["dump"][:2]:
                        for j in range(6):
                            P.dma(lambda e, j=j: e.dma_start(out=k.dump[j], in_=qkv[j][:]), reads=[("qkv", j)], writes=[("dump", j)])
                        for j, (tl, tk) in enumerate([(u_all, "u"), (wT_all, "wT"), (kdec_all, "kdec"), (O_tok, "O")]):
                            P.dma(lambda e, j=j, tl=tl: e.dma_start(out=k.dump[6 + j], in_=tl[:].rearrange("p c v -> p (c v)")), reads=[(tk, c) for c in range(NCK)], writes=[("dump", 6 + j)])
                        P.dma(lambda e: e.dma_start(out=k.dump[10:12].rearrange("a p t -> p a t"), in_=attnT_all[:].rearrange("p (a c) v -> p a (c v)", a=2)), reads=[("attnT", c) for c in range(NCK)], writes=[("dump", 10)])
                        for j, (tl, tk, w) in enumerate([(gt, "gt", 288), (bt, "bt", 288), (gcBD, "gcBD", 144), (gtBD, "gtBD", 144), (bBD, "bBD", 144)]):
                            P.dma(lambda e, j=j, tl=tl, w=w: e.dma_start(out=k.dump[12, :, j * 300:j * 300 + w], in_=tl[:].rearrange("p c v -> p (c v)")), reads=[tk], writes=[("dump", 12, j)])
                if lim >= 4:
                    out_phase(b, hp)
        P.end_stage()


def stage_s5(k, l):
    nc, P, NB = k.nc, k.P, k.NB
    HALF_PI = float(np.pi / 2)
    P.serial = k.opts.get("serial_s5", SERIAL_S5)
    with ExitStack() as es:
        al = lambda name, shape, dt=F32: es.enter_context(nc.sbuf_tensor("s_" + name, list(shape), dt))
        lre = al("lre", [128, 16]); lim = al("lim", [128, 16]); stp = al("stp", [128, 16]); mag = al("mag", [128, 16])
        cth = al("cth", [128, 16]); sth = al("sth", [128, 16]); t1 = al("t1", [128, 16]); t2 = al("t2", [128, 16]); t3 = al("t3", [128, 16])
        are = al("are", [128, 16]); aim = al("aim", [128, 16]); cfr = al("cfr", [128, 16]); cfi = al("cfi", [128, 16]); hpi = al("hpi", [128, 1])
        pwc = al("pwc", [128, 16, 12]); pws = al("pws", [128, 16, 12])
        bre = al("bre", [128, 8, 16]); bim = al("bim", [128, 8, 16]); cre = al("cre", [128, 8, 16]); cim = al("cim", [128, 8, 16])
        bb_all = al("bb_all", [128, 32, 16]); bd_all = al("bd_all", [128, 32, 32]); tb = [al("tb%d" % i, [128, 8, 16]) for i in range(4)]
        W_all = al("W_all", [32, 32, 128], BF16); cw_all = al("cw_all", [128, 8, 2, 128], BF16)
        dsk = al("dsk", [128, 2]); wgl = al("wgl", [128, 2, 512], BF16)
        cs = [al("cs%d" % i, [128, T]) for i in range(2)]; sn = [al("sn%d" % i, [128, T]) for i in range(2)]
        xr = al("xr", [128, T]); xi = al("xi", [128, T]); gr = al("gr", [128, T]); gi = al("gi", [128, T])
        u32 = al("u32", [32, T]); u32b = al("u32b", [32, T], BF16); hrb = al("hrb", [128, T], BF16); hib = al("hib", [128, T], BF16); Y = [al("Y%d" % i, [128, T]) for i in range(2)]
        mt = Slots([al("mt%d" % i, [128, 512])[:] for i in range(6)], "mt")
        gel = al("gel", [128, 2, 512], BF16); sg = [al("sg%d" % i, [128, 512]) for i in range(2)]; yb = [al("yb%d" % i, [128, 512], BF16) for i in range(2)]
        banks = [es.enter_context(nc.psum_tensor("s_ps%d" % i, [128, 512], F32)) for i in range(8)]
        psl = Slots([banks[i][:] for i in range(8)], "psb")
        ld = lambda dst, src, tok: P.dma(lambda e: e.dma_start(out=dst, in_=src), writes=[tok])
        ld(lre[:], k.s5_lam_re[l], "lre"); ld(lim[:], k.s5_lam_im[l], "lim"); ld(stp[:], k.s5_log_step[l], "stp")
        ld(bre[:], k.s5_b_re[l], "bre"); ld(bim[:], k.s5_b_im[l], "bim"); ld(cre[:], k.s5_c_re[l], "cre"); ld(cim[:], k.s5_c_im[l], "cim")
        ld(dsk[:], k.s5_d[l], "dsk")
        P.dma(lambda e: e.dma_start(out=wgl[:], in_=k.s5_glu[l].rearrange("(c p) n -> p c n", p=128)), writes=["wgl"], q="pool")
        P.op("dve", lambda e: e.memset(hpi[:], HALF_PI), writes=["hpi"])
        P.op("dve", lambda e: e.memset(bd_all[:], 0.0), writes=["bd_all"])
        P.op("dve", lambda e: e.memset(cw_all[:], 0.0), writes=["cw_all"])
        tt = lambda out, a, b_, op, rd, wr, eng="dve": P.op(eng, lambda e: e.tensor_tensor(out=out, in0=a, in1=b_, op=op), reads=rd, writes=wr)
        P.op("act", lambda e: e.activation(out=stp[:], in_=stp[:], func=AF.Exp), reads=["stp"], writes=["stp"])
        tt(t1[:], lre[:], stp[:], ALU.mult, ["lre", "stp"], ["t1"])
        P.op("act", lambda e: e.activation(out=mag[:], in_=t1[:], func=AF.Exp), reads=["t1"], writes=["mag"])
        tt(t2[:], lim[:], stp[:], ALU.mult, ["lim", "stp"], ["t2"])
        P.op("act", lambda e: e.activation(out=sth[:], in_=t2[:], func=AF.Sin, scale=1.0 / 32), reads=["t2"], writes=["sth"])
        P.op("act", lambda e: e.activation(out=cth[:], in_=t2[:], func=AF.Sin, scale=1.0 / 32, bias=hpi[:, 0:1]), reads=["t2", "hpi"], writes=["cth"])
        for it in range(5):
            tt(t1[:], cth[:], cth[:], ALU.mult, ["cth"], ["t1"])
            tt(t3[:], sth[:], sth[:], ALU.mult, ["sth"], ["t3"])
            P.op("dve", lambda e: e.scalar_tensor_tensor(out=sth[:], in0=cth[:], scalar=2.0, in1=sth[:], op0=ALU.mult, op1=ALU.mult), reads=["cth", "sth"], writes=["sth"])
            tt(cth[:], t1[:], t3[:], ALU.subtract, ["t1", "t3"], ["cth"])
        tt(are[:], mag[:], cth[:], ALU.mult, ["mag", "cth"], ["are"])
        tt(aim[:], mag[:], sth[:], ALU.mult, ["mag", "sth"], ["aim"])
        tt(t1[:], lre[:], lre[:], ALU.mult, ["lre"], ["t1"])
        tt(t3[:], lim[:], lim[:], ALU.mult, ["lim"], ["t3"])
        tt(t1[:], t1[:], t3[:], ALU.add, ["t1", "t3"], ["t1"])
        P.op("dve", lambda e: e.reciprocal(out=t1[:], in_=t1[:]), reads=["t1"], writes=["t1"])
        P.op("dve", lambda e: e.tensor_scalar(out=t2[:], in0=are[:], scalar1=-1.0, scalar2=None, op0=ALU.add), reads=["are"], writes=["t2"])
        tt(cfr[:], t2[:], lre[:], ALU.mult, ["t2", "lre"], ["cfr"])
        tt(t3[:], aim[:], lim[:], ALU.mult, ["aim", "lim"], ["t3"])
        tt(cfr[:], cfr[:], t3[:], ALU.add, ["cfr", "t3"], ["cfr"])
        tt(cfr[:], cfr[:], t1[:], ALU.mult, ["cfr", "t1"], ["cfr"])
        tt(cfi[:], aim[:], lre[:], ALU.mult, ["aim", "lre"], ["cfi"])
        tt(t3[:], t2[:], lim[:], ALU.mult, ["t2", "lim"], ["t3"])
        tt(cfi[:], cfi[:], t3[:], ALU.subtract, ["cfi", "t3"], ["cfi"])
        tt(cfi[:], cfi[:], t1[:], ALU.mult, ["cfi", "t1"], ["cfi"])
        bb4 = bb_all[:].rearrange("p (d c r) h -> p d c r h", d=2, r=2)
        for d in range(2):
            bc = lambda t_: t_[:, d * 8:(d + 1) * 8].rearrange("p (c o) -> p c o", o=1).to_broadcast([128, 8, 16])
            tt(tb[0][:], bre[:], bc(cfr), ALU.mult, ["bre", "cfr"], [("tb", 0)])
            tt(tb[1][:], bim[:], bc(cfi), ALU.mult, ["bim", "cfi"], [("tb", 1)])
            tt(bb4[:, d, :, 0, :], tb[0][:], tb[1][:], ALU.subtract, [("tb", 0), ("tb", 1)], ["bb_all"])
            tt(tb[2][:], bim[:], bc(cfr), ALU.mult, ["bim", "cfr"], [("tb", 2)])
            tt(tb[3][:], bre[:], bc(cfi), ALU.mult, ["bre", "cfi"], [("tb", 3)])
            tt(bb4[:, d, :, 1, :], tb[2][:], tb[3][:], ALU.add, [("tb", 2), ("tb", 3)], ["bb_all"])
        for g2 in range(2):
            rr_ = slice(g2 * 64, (g2 + 1) * 64)
            P.op("dve", lambda e, rr_=rr_, g2=g2: e.tensor_copy(out=bd_all[rr_, :, g2 * 16:(g2 + 1) * 16], in_=bb_all[rr_, :, :]), reads=["bb_all", "bd_all"], writes=["bd_all"])

        def w_build(j):
            pw, pwt = psl.get()
            P.op("pe", lambda e: e.transpose(pw[0:32, 0:128], bd_all[:, j, :], k.idn[:]), reads=["bd_all", "idn"], writes=[pwt])
            P.op("act", lambda e: e.activation(out=W_all[:, j, :], in_=pw[0:32, 0:128], func=AF.Identity), reads=[pwt], writes=["W_all"])
        for j in range(32):
            w_build(j)

        def cw_build(ct, g2):
            q = ct % 4
            rr_ = slice(g2 * 64, (g2 + 1) * 64)
            c0 = q * 32 + g2 * 16
            P.op("dve", lambda e: e.tensor_copy(out=cw_all[rr_, ct, 0, c0:c0 + 16], in_=cre[rr_, ct, :]), reads=["cre", "cw_all"], writes=["cw_all"])
            P.op("dve", lambda e: e.tensor_scalar(out=cw_all[rr_, ct, 1, c0:c0 + 16], in0=cim[rr_, ct, :], scalar1=-1.0, scalar2=None, op0=ALU.mult), reads=["cim", "cw_all"], writes=["cw_all"])
        for ct in range(8):
            for g2 in range(2):
                cw_build(ct, g2)
        P.op("dve", lambda e: e.tensor_copy(out=pwc[:, :, 0], in_=cth[:]), reads=["cth"], writes=["pwc"])
        P.op("dve", lambda e: e.tensor_copy(out=pws[:, :, 0], in_=sth[:]), reads=["sth"], writes=["pws"])

        def pw_level(kk):
            c_, s_ = pwc[:, :, kk - 1], pws[:, :, kk - 1]
            tt(t1[:], c_, c_, ALU.mult, ["pwc"], ["t1"])
            tt(t3[:], s_, s_, ALU.mult, ["pws"], ["t3"])
            tt(pwc[:, :, kk], t1[:], t3[:], ALU.subtract, ["t1", "t3", "pwc"], ["pwc"])
            P.op("dve", lambda e: e.scalar_tensor_tensor(out=pws[:, :, kk], in0=c_, scalar=2.0, in1=s_, op0=ALU.mult, op1=ALU.mult), reads=["pwc", "pws"], writes=["pws"])
        for kk in range(1, 12):
            pw_level(kk)

        def table_gen(j):
            i = j % 2
            C, S_ = cs[i], sn[i]
            ctk, stk = ("cs", i), ("sn", i)
            P.op("dve", lambda e: e.memset(C[:, 0:1], 1.0), writes=[ctk])
            P.op("dve", lambda e: e.memset(S_[:, 0:1], 0.0), writes=[stk])
            for kk in range(12):
                ln = 1 << kk
                nn = min(ln, T - ln)
                if nn <= 0:
                    break
                pc, ps_ = pwc[:, j, kk:kk + 1], pws[:, j, kk:kk + 1]
                lvl(C, S_, ctk, stk, ln, nn, pc, ps_)
            P.dma(lambda e: e.dma_start(out=k.S5TAB[j, 0], in_=C[:]), reads=[ctk], writes=[("TAB", j)])
            P.dma(lambda e: e.dma_start(out=k.S5TAB[j, 1], in_=S_[:]), reads=[stk], writes=[("TAB", j)])

        def lvl(C, S_, ctk, stk, ln, nn, pc, ps_):
            m1, m1t = mt.get()
            m2, m2t = mt.get()
            w_ = min(nn, 512)
            for o in range(0, nn, 512):
                w = min(512, nn - o)
                sub(C, S_, ctk, stk, ln, o, w, pc, ps_)

        def sub(C, S_, ctk, stk, ln, o, w, pc, ps_):
            m1, m1t = mt.get()
            m2, m2t = mt.get()
            P.op("dve", lambda e: e.tensor_scalar(out=m1[:, :w], in0=S_[:, o:o + w], scalar1=ps_, scalar2=None, op0=ALU.mult), reads=[stk, "pws"], writes=[m1t])
            P.op("dve", lambda e: e.tensor_scalar(out=m2[:, :w], in0=S_[:, o:o + w], scalar1=pc, scalar2=None, op0=ALU.mult), reads=[stk, "pwc"], writes=[m2t])
            P.op("dve", lambda e: e.scalar_tensor_tensor(out=C[:, ln + o:ln + o + w], in0=C[:, o:o + w], scalar=pc, in1=m1[:, :w], op0=ALU.mult, op1=ALU.subtract), reads=[ctk, m1t, "pwc"], writes=[ctk])
            P.op("dve", lambda e: e.scalar_tensor_tensor(out=S_[:, ln + o:ln + o + w], in0=C[:, o:o + w], scalar=ps_, in1=m2[:, :w], op0=ALU.mult, op1=ALU.add), reads=[ctk, m2t, "pws"], writes=[stk])
        for j in range(16):
            table_gen(j)

        blocks = [(0, CTX)] + [(CTX + i * 512, 512) for i in range(4)]

        def tabview(tab, d, t0, n):
            if d == 0:
                return tab[:, t0:t0 + n]
            if t0 < CTX:
                lo = CTX - 1 - (t0 + n - 1)
            else:
                lo = CTX + (T - 1 - (t0 + n - 1))
            return tab[:, lo:lo + n][:, ::-1]

        def do_block_in(ct, d, i, t0, n):
            j = d * 8 + ct
            pr, prt = psl.get()
            pi_, pit = psl.get()
            P.op("pe", lambda e: e.matmul(pr[:, :n], lhsT=W_all[:, j * 2, :], rhs=u32b[:, t0:t0 + n], start=True, stop=True), reads=["W_all", "u32b"], writes=[prt], inc=False)
            P.op("pe", lambda e: e.matmul(pi_[:, :n], lhsT=W_all[:, j * 2 + 1, :], rhs=u32b[:, t0:t0 + n], start=True, stop=True), reads=["W_all", "u32b"], writes=[pit])
            cv, sv = tabview(cs[i], d, t0, n), tabview(sn[i], d, t0, n)
            ms = [mt.get() for _ in range(4)]
            tt(ms[0][0][:, :n], pr[:, :n], cv, ALU.mult, [prt, ("cs", i)], [ms[0][1]])
            tt(ms[1][0][:, :n], pi_[:, :n], sv, ALU.mult, [pit, ("sn", i)], [ms[1][1]])
            tt(xr[:, t0:t0 + n], ms[0][0][:, :n], ms[1][0][:, :n], ALU.add, [ms[0][1], ms[1][1]], ["xr"])
            tt(ms[2][0][:, :n], pi_[:, :n], cv, ALU.mult, [pit, ("cs", i)], [ms[2][1]])
            tt(ms[3][0][:, :n], pr[:, :n], sv, ALU.mult, [prt, ("sn", i)], [ms[3][1]])
            tt(xi[:, t0:t0 + n], ms[2][0][:, :n], ms[3][0][:, :n], ALU.subtract, [ms[2][1], ms[3][1]], ["xi"])

        def do_scan(ct, d):
            j = d * 8 + ct
            for (src, dst, stok, dtok) in ((xr, gr, "xr", "gr"), (xi, gi, "xi", "gi")):
                scan1(j, d, src, dst, stok, dtok)

        def scan1(j, d, src, dst, stok, dtok):
            rb = lambda n: mag[:, j:j + 1].to_broadcast([128, n])
            if d == 0:
                P.op("dve", lambda e: e.tensor_tensor_scan(out=dst[:], data0=rb(T), data1=src[:], initial=0.0, op0=ALU.mult, op1=ALU.add), reads=[stok, "mag"], writes=[dtok])
            else:
                P.op("dve", lambda e: e.tensor_tensor_scan(out=dst[:, 0:CTX][:, ::-1], data0=rb(CTX), data1=src[:, 0:CTX][:, ::-1], initial=0.0, op0=ALU.mult, op1=ALU.add), reads=[stok, "mag"], writes=[dtok])
                P.op("dve", lambda e: e.tensor_tensor_scan(out=dst[:, CTX:T][:, ::-1], data0=rb(SEQ), data1=src[:, CTX:T][:, ::-1], initial=dst[:, 0:1], op0=ALU.mult, op1=ALU.add), reads=[stok, "mag", dtok], writes=[dtok])

        def do_block_out(ct, d, i, t0, n, first):
            cv, sv = tabview(cs[i], d, t0, n), tabview(sn[i], d, t0, n)
            ms = [mt.get() for _ in range(4)]
            tt(ms[0][0][:, :n], gr[:, t0:t0 + n], cv, ALU.mult, ["gr", ("cs", i)], [ms[0][1]])
            tt(ms[1][0][:, :n], gi[:, t0:t0 + n], sv, ALU.mult, ["gi", ("sn", i)], [ms[1][1]])
            tt(hrb[:, t0:t0 + n], ms[0][0][:, :n], ms[1][0][:, :n], ALU.subtract, [ms[0][1], ms[1][1]], ["hrb"])
            tt(ms[2][0][:, :n], gr[:, t0:t0 + n], sv, ALU.mult, ["gr", ("sn", i)], [ms[2][1]])
            tt(ms[3][0][:, :n], gi[:, t0:t0 + n], cv, ALU.mult, ["gi", ("cs", i)], [ms[3][1]])
            tt(hib[:, t0:t0 + n], ms[2][0][:, :n], ms[3][0][:, :n], ALU.add, [ms[2][1], ms[3][1]], ["hib"])
            py, pyt = psl.get()
            P.op("pe", lambda e: e.matmul(py[:, :n], lhsT=cw_all[:, ct, 0, :], rhs=hrb[:, t0:t0 + n], start=True, stop=False), reads=["cw_all", "hrb"], writes=[pyt], inc=False)
            P.op("pe", lambda e: e.matmul(py[:, :n], lhsT=cw_all[:, ct, 1, :], rhs=hib[:, t0:t0 + n], start=False, stop=True), reads=["cw_all", "hib"], writes=[pyt])
            yt_ = Y[ct // 4]
            ytk = ("Y", ct // 4)
            if first:
                P.op("dve", lambda e: e.tensor_copy(out=yt_[:, t0:t0 + n], in_=py[:, :n]), reads=[pyt], writes=[ytk])
            else:
                tt(yt_[:, t0:t0 + n], yt_[:, t0:t0 + n], py[:, :n], ALU.add, [pyt, ytk], [ytk])

        def do_ct_dir(b, ct, d):
            j = d * 8 + ct
            i = j % 2
            P.dma(lambda e: e.dma_start(out=cs[i][:], in_=k.S5TAB[j, 0]), reads=[("TAB", j)], writes=[("cs", i)])
            P.dma(lambda e: e.dma_start(out=sn[i][:], in_=k.S5TAB[j, 1]), reads=[("TAB", j)], writes=[("sn", i)])
            for (t0, n) in blocks:
                do_block_in(ct, d, i, t0, n)
            do_scan(ct, d)
            for (t0, n) in blocks:
                do_block_out(ct, d, i, t0, n, (ct % 4 == 0 and d == 0))

        def do_ct(b, ct):
            r0 = FM_U + ct * 32
            P.dma(lambda e: e.dma_start(out=u32[:], in_=k.PT[b, r0:r0 + 32, :]), reads=[("PT", b)], writes=["u32"])
            P.op("dve", lambda e: e.tensor_copy(out=u32b[:], in_=u32[:]), reads=["u32"], writes=["u32b"])
            for d in range(2):
                do_ct_dir(b, ct, d)

        def out_tile(b, t0, n):
            for yt in range(2):
                ub, ubt = mt.get()
                r0 = FM_U + yt * 128
                P.dma(lambda e, ub=ub, r0=r0: e.dma_start(out=ub[:, :n], in_=k.PT[b, r0:r0 + 128, t0:t0 + n]), reads=[("PT", b)], writes=[ubt])
                yv, yvt = mt.get()
                P.op("dve", lambda e, ub=ub, yv=yv, yt=yt: e.scalar_tensor_tensor(out=yv[:, :n], in0=ub[:, :n], scalar=dsk[:, yt:yt + 1], in1=Y[yt][:, t0:t0 + n], op0=ALU.mult, op1=ALU.add), reads=[ubt, "dsk", ("Y", yt)], writes=[yvt])
                x2, x2t = mt.get()
                P.op("act", lambda e, yv=yv, x2=x2: e.activation(out=x2[:, :n], in_=yv[:, :n], func=AF.Square), reads=[yvt], writes=[x2t])
                P.op("dve", lambda e, x2=x2: e.tensor_scalar(out=x2[:, :n], in0=x2[:, :n], scalar1=0.044715, scalar2=1.0, op0=ALU.mult, op1=ALU.add), reads=[x2t], writes=[x2t])
                P.op("dve", lambda e, x2=x2, yv=yv: e.tensor_tensor(out=x2[:, :n], in0=x2[:, :n], in1=yv[:, :n], op=ALU.mult), reads=[x2t, yvt], writes=[x2t])
                P.op("act", lambda e, x2=x2: e.activation(out=x2[:, :n], in_=x2[:, :n], func=AF.Tanh, scale=0.7978845608028654), reads=[x2t], writes=[x2t])
                P.op("dve", lambda e, x2=x2, yv=yv, yt=yt: e.scalar_tensor_tensor(out=gel[:, yt, :n], in0=x2[:, :n], scalar=1.0, in1=yv[:, :n], op0=ALU.add, op1=ALU.mult), reads=[x2t, yvt], writes=[("gel", yt)])
            for oc in range(2):
                pa, pat = psl.get()
                pg, pgt = psl.get()
                for kc in range(2):
                    P.op("pe", lambda e, oc=oc, kc=kc, pa=pa: e.matmul(pa[:, :n], lhsT=wgl[:, kc, oc * 128:(oc + 1) * 128], rhs=gel[:, kc, :n], start=(kc == 0), stop=(kc == 1)), reads=["wgl", ("gel", kc)], writes=[pat])
                for kc in range(2):
                    P.op("pe", lambda e, oc=oc, kc=kc, pg=pg: e.matmul(pg[:, :n], lhsT=wgl[:, kc, 256 + oc * 128:256 + (oc + 1) * 128], rhs=gel[:, kc, :n], start=(kc == 0), stop=(kc == 1)), reads=["wgl", ("gel", kc)], writes=[pgt])
                P.op("act", lambda e, oc=oc, pg=pg: e.activation(out=sg[oc][:, :n], in_=pg[:, :n], func=AF.Sigmoid, scale=0.5), reads=[pgt], writes=[("sg", oc)])
                P.op("dve", lambda e, oc=oc, pa=pa: e.scalar_tensor_tensor(out=yb[oc][:, :n], in0=pa[:, :n], scalar=0.5, in1=sg[oc][:, :n], op0=ALU.mult, op1=ALU.mult), reads=[pat, ("sg", oc)], writes=[("yb", oc)])
                P.dma(lambda e, oc=oc: e.dma_start(out=k.YT[b, 1, oc * 128:(oc + 1) * 128, t0:t0 + n], in_=yb[oc][:, :n]), reads=[("yb", oc)], writes=[("YT", b)])

        for b in range(NB):
            for ct in range(8):
                do_ct(b, ct)
            for (t0, n) in blocks:
                out_tile(b, t0, n)
        P.end_stage()


def stage_mixers(k, l):
    sel = k.opts.get("mixers", ("dn", "s5", "hg", "att"))
    if "dn" in sel:
        stage_deltanet(k, l)
    if "s5" in sel:
        stage_s5(k, l)
    if "hg" in sel:
        stage_hgrn2(k, l)
    if "att" in sel:
        stage_attention(k, l)


def stage_merge(k, l):
    nc, P, NB, NC = k.nc, k.P, k.NB, k.NC
    with ExitStack() as es:
        wg = es.enter_context(nc.sbuf_tensor("m_wg", [128, 8, 4 * D], BF16))
        wb = es.enter_context(nc.sbuf_tensor("m_wb", [128, 8, D], BF16))
        wo = es.enter_context(nc.sbuf_tensor("m_wo", [128, 8, D], BF16))
        xt = [es.enter_context(nc.sbuf_tensor("m_x%d" % s, [128, 8, 256], F32)) for s in range(2)]
        yt = [es.enter_context(nc.sbuf_tensor("m_y%d" % s, [128, 8, 256], BF16)) for s in range(2)]
        ht = es.enter_context(nc.sbuf_tensor("m_h", [128, 8, 256], BF16))
        acc = es.enter_context(nc.sbuf_tensor("m_acc", [128, 8, 256], F32))
        accb = es.enter_context(nc.sbuf_tensor("m_accb", [128, 8, 256], BF16))
        sg = [es.enter_context(nc.sbuf_tensor("m_sg%d" % s, [128, 256], F32)) for s in range(2)]
        tt = [es.enter_context(nc.sbuf_tensor("m_tt%d" % s, [128, 256], F32)) for s in range(2)]
        psg = [es.enter_context(nc.psum_tensor("m_psg%d" % s, [128, 512], F32)) for s in range(2)]
        psy = [es.enter_context(nc.psum_tensor("m_psy%d" % s, [128, 512], F32)) for s in range(2)]
        pso = [es.enter_context(nc.psum_tensor("m_pso%d" % s, [128, 512], F32)) for s in range(2)]
        ntiles = alloc_norm_tiles(k, es, "m_", 256)
        A, Sh, G = emit_mod_scalars(k, es, "m_", l, 1, 3, 4, 5, 1.0)
        load_w_bf16(k, wg, k.w_gate[l], 4 * D, "wg")
        load_w_bf16(k, wb, k.w_branch[l].rearrange("i r n -> (i r) n"), D, "wb")
        load_w_bf16(k, wo, k.w_out[l], D, "wo")
        it = 0
        qi = 0
        for (b, t0, n, cond) in token_tiles(NB, 256):
            if l == 1 and cond == NB:
                continue
            s = it % 2
            it += 1
            xsrc = k.XT[b].rearrange("(c p) t -> p c t", p=128)[:, :, t0:t0 + n]
            P.dma(lambda e, s=s, xsrc=xsrc, n=n: e.dma_start(out=xt[s][:, :, :n], in_=xsrc), reads=[("XT", b)], writes=[("x", s)])
            ysrc = k.YT[b].rearrange("i (c p) t -> p (i c) t", p=128)[:, :, t0:t0 + n]
            P.dma(lambda e, s=s, ysrc=ysrc, n=n: e.dma_start(out=yt[s][:, :, :n], in_=ysrc), reads=[("YT", b)], writes=[("y", s)])
            emit_norm_mod(k, ntiles, xt[s], ht, n, A, Sh, cond, "modsc", s)
            for m in range(8):
                for i in range(4):
                    q = qi % 2
                    qi += 1
                    for kc in range(8):
                        P.op("pe", lambda e, i=i, m=m, q=q, kc=kc, n=n: e.matmul(psg[q][:, :n], lhsT=wg[:, kc, i * D + m * 128:i * D + (m + 1) * 128], rhs=ht[:, kc, :n], start=(kc == 0), stop=(kc == 7)),
                             reads=[("wg", kc), "h"], writes=[("psg", q)], inc=(kc == 7))
                    for kk in range(2):
                        P.op("pe", lambda e, i=i, m=m, q=q, kk=kk, n=n, s=s: e.matmul(psy[q][:, :n], lhsT=wb[:, i * 2 + kk, m * 128:(m + 1) * 128], rhs=yt[s][:, i * 2 + kk, :n], start=(kk == 0), stop=(kk == 1)),
                             reads=[("wb", i * 2 + kk), ("y", s)], writes=[("psy", q)], inc=(kk == 1))
                    P.op("act", lambda e, q=q, n=n: e.activation(out=sg[q][:, :n], in_=psg[q][:, :n], func=AF.Sigmoid), reads=[("psg", q)], writes=[("sg", q)])
                    if i == 0:
                        P.op("dve", lambda e, q=q, n=n, m=m: e.tensor_tensor(out=acc[:, m, :n], in0=sg[q][:, :n], in1=psy[q][:, :n], op=ALU.mult), reads=[("sg", q), ("psy", q)], writes=[("acc", m)])
                    else:
                        P.op("dve", lambda e, q=q, n=n: e.tensor_tensor(out=tt[q][:, :n], in0=sg[q][:, :n], in1=psy[q][:, :n], op=ALU.mult), reads=[("sg", q), ("psy", q)], writes=[("tt", q)])
                        if i < 3:
                            P.op("pool", lambda e, q=q, n=n, m=m: e.tensor_tensor(out=acc[:, m, :n], in0=acc[:, m, :n], in1=tt[q][:, :n], op=ALU.add), reads=[("tt", q), ("acc", m)], writes=[("acc", m)])
                        else:
                            P.op("pool", lambda e, q=q, n=n, m=m: e.tensor_tensor(out=accb[:, m, :n], in0=acc[:, m, :n], in1=tt[q][:, :n], op=ALU.add), reads=[("tt", q), ("acc", m)], writes=[("accb", m)])
            for m in range(8):
                q = m % 2
                for kc in range(8):
                    P.op("pe", lambda e, m=m, q=q, kc=kc, n=n: e.matmul(pso[q][:, :n], lhsT=wo[:, kc, m * 128:(m + 1) * 128], rhs=accb[:, kc, :n], start=(kc == 0), stop=(kc == 7)),
                         reads=[("wo", kc), ("accb", kc)], writes=[("pso", q)], inc=(kc == 7))
                P.op("dve", lambda e, m=m, q=q, s=s, n=n, cond=cond: e.scalar_tensor_tensor(out=xt[s][:, m, :n], in0=pso[q][:, :n], scalar=G[:, m, cond:cond + 1], in1=xt[s][:, m, :n], op0=ALU.mult, op1=ALU.add),
                     reads=[("pso", q), ("x", s), "modsg"], writes=[("x", s)])
            P.dma(lambda e, s=s, xsrc=xsrc, n=n: e.dma_start(out=xsrc, in_=xt[s][:, :, :n]), reads=[("x", s)], writes=[("XT", b)])
        P.end_stage()


def stage_final(k):
    nc, P, NB = k.nc, k.P, k.NB
    with ExitStack() as es:
        xt = [es.enter_context(nc.sbuf_tensor("o_x%d" % s, [128, 8, 512], F32)) for s in range(2)]
        yt = es.enter_context(nc.sbuf_tensor("o_y", [128, 8, 512], F32))
        ot = [es.enter_context(nc.sbuf_tensor("o_o%d" % s, [128, D], F32)) for s in range(2)]
        sq = es.enter_context(nc.sbuf_tensor("o_sq", [128, 8, 512], BF16))
        rs = es.enter_context(nc.sbuf_tensor("o_rs", [128, 512], F32))
        epsb = es.enter_context(nc.sbuf_tensor("o_eps", [128, 1], F32))
        psms = es.enter_context(nc.psum_tensor("o_psms", [128, 512], F32))
        ps = [es.enter_context(nc.psum_tensor("o_ps%d" % s, [128, 4, 128], F32)) for s in range(4)]
        P.op("dve", lambda e: e.memset(epsb[:], EPS), writes=["epsb"])
        it = 0
        oi = 0
        pi = 0
        for (b, t0, n, cond) in token_tiles(NB):
            if cond == NB:
                continue
            s = it % 2
            it += 1
            xsrc = k.XT[b].rearrange("(c p) t -> p c t", p=128)[:, :, t0:t0 + n]
            P.dma(lambda e, s=s, xsrc=xsrc: e.dma_start(out=xt[s][:], in_=xsrc), reads=[("XT", b)], writes=[("x", s)])
            P.op("act", lambda e, s=s: e.activation(out=sq[:], in_=xt[s][:], func=AF.Square), reads=[("x", s)], writes=["sq"])
            for c in range(8):
                P.op("pe", lambda e, c=c: e.matmul(psms[:], lhsT=k.onesb[:], rhs=sq[:, c, :], start=(c == 0), stop=(c == 7)), reads=["sq", "onesb"], writes=["psms"], inc=(c == 7))
            P.op("act", lambda e: e.activation(out=rs[:], in_=psms[:], func=AF.Sqrt, bias=epsb[:, 0:1], scale=1.0), reads=["psms", "epsb"], writes=["rs0", "rs"])
            P.op("dve", lambda e: e.reciprocal(out=rs[:], in_=rs[:]), reads=["rs0"], writes=["rs"])
            for c in range(8):
                P.op("dve", lambda e, c=c, s=s: e.scalar_tensor_tensor(out=yt[:, c, :], in0=xt[s][:, c, :], scalar=k.fg[:, c:c + 1], in1=rs[:], op0=ALU.mult, op1=ALU.mult),
                     reads=[("x", s), "rs", "fg"], writes=[("yt", c)])
            for tb in range(4):
                o = oi % 2
                oi += 1
                for h in range(2):
                    p = pi % 4
                    pi += 1
                    for c in range(4):
                        ch = h * 4 + c
                        P.op("pe", lambda e, p=p, c=c, ch=ch, tb=tb: e.transpose(ps[p][:, c, :], yt[:, ch, tb * 128:(tb + 1) * 128], k.idn[:]),
                             reads=[("yt", ch), "idn"], writes=[("ps", p)], inc=(c == 3))
                    if h == 0:
                        P.op("act", lambda e, p=p, o=o, h=h: e.activation(out=ot[o][:, h * 512:(h + 1) * 512], in_=ps[p][:].rearrange("p a b -> p (a b)"), func=AF.Identity), reads=[("ps", p)], writes=[("ot", o, h)])
                    else:
                        P.op("dve", lambda e, p=p, o=o, h=h: e.tensor_copy(out=ot[o][:, h * 512:(h + 1) * 512], in_=ps[p][:].rearrange("p a b -> p (a b)")), reads=[("ps", p)], writes=[("ot", o, h)])
                r0 = t0 - CTX + tb * 128
                P.dma(lambda e, o=o, b=b, r0=r0: e.dma_start(out=k.out[b, r0:r0 + 128, :], in_=ot[o][:]), reads=[("ot", o, 0), ("ot", o, 1)], writes=[("out", b)])
        P.end_stage()


def make_inputs(inputs, core, NB):
    b0 = core * NB
    f = lambda a: np.ascontiguousarray(a, dtype=np.float32)
    w_in = inputs["w_in"]
    cT = np.concatenate([inputs["c"][b0:b0 + NB], inputs["c_ctx"][None]], 0).reshape(NB + 1, 8, 128).transpose(2, 1, 0)
    return {
        "x": f(inputs["x"][b0:b0 + NB]),
        "ctx": f(inputs["ctx"][b0:b0 + NB]),
        "cT": f(cT),
        "ada_w": f(inputs["ada_w"]),
        "ada_b": f(inputs["ada_b"].reshape(2, 72, 128).transpose(0, 2, 1)),
        "norm_g": f(inputs["norm_g"].reshape(2, 3, 8, 128).transpose(0, 1, 3, 2)),
        "final_g": f(inputs["final_g"].reshape(8, 128).T),
        "ffn_w1": f(inputs["ffn_w1"]), "ffn_w3": f(inputs["ffn_w3"]), "ffn_w2": f(inputs["ffn_w2"]),
        "w_fm": f(np.concatenate([w_in[:, :, a:a + w] for a, w in FM_GROUPS], axis=2)),
        "w_tok": f(np.concatenate([w_in[:, :, a:a + w] for a, w in TOK_GROUPS], axis=2)),
        "w_gate": f(w_in[:, :, GATE0:]),
        "w_branch": f(inputs["w_branch"]),
        "w_out": f(inputs["w_out"]),
        "ident": np.eye(128, dtype=np.float32),
        "rope_cos": ROPE[0], "rope_sin": ROPE[1], "rope_rm": ROPE[2],
        "hg_masks": HG_MASKS, "bd64": BD64,
        "s5_lam_re": f(inputs["s5_lam_re"].reshape(2, 2, 8, 2, 64).transpose(0, 3, 4, 1, 2).reshape(2, 128, 16)),
        "s5_lam_im": f(inputs["s5_lam_im"].reshape(2, 2, 8, 2, 64).transpose(0, 3, 4, 1, 2).reshape(2, 128, 16)),
        "s5_log_step": f(np.broadcast_to(inputs["s5_log_step"].reshape(2, 2, 8, 2, 1), (2, 2, 8, 2, 64)).transpose(0, 3, 4, 1, 2).reshape(2, 128, 16)),
        "s5_b_re": f(inputs["s5_b_re"].reshape(2, 8, 2, 64, 16).transpose(0, 2, 3, 1, 4).reshape(2, 128, 8, 16)),
        "s5_b_im": f(inputs["s5_b_im"].reshape(2, 8, 2, 64, 16).transpose(0, 2, 3, 1, 4).reshape(2, 128, 8, 16)),
        "s5_c_re": f(inputs["s5_c_re"].reshape(2, 8, 2, 16, 64).transpose(0, 2, 4, 1, 3).reshape(2, 128, 8, 16)),
        "s5_c_im": f(inputs["s5_c_im"].reshape(2, 8, 2, 16, 64).transpose(0, 2, 4, 1, 3).reshape(2, 128, 8, 16)),
        "s5_d": f(inputs["s5_d"].reshape(2, 2, 128).transpose(0, 2, 1)),
        "s5_glu": f(inputs["s5_glu"]),
        "dn_gm64": DN_GM64, "dn_gmbd": DN_GMBD, "dn_idn2": DN_IDN2,
        "dn_conv": f(inputs["dn_conv"].reshape(2, 5, 6, 128).transpose(0, 3, 2, 1)),
        "dn_a_log": f(np.broadcast_to(inputs["dn_a_log"].reshape(2, 1, 8), (2, 128, 8))),
        "dn_dt_bias": f(np.broadcast_to(inputs["dn_dt_bias"].reshape(2, 1, 8), (2, 128, 8))),
        "dn_norm_g": f(np.broadcast_to(inputs["dn_norm_g"].reshape(2, 1, 64), (2, 128, 64))),
        "hg_lb": f(inputs["hg_lb_logits"].reshape(2, 2, 128).transpose(2, 1, 0)),
        "hg_norm_g": f(np.tile(inputs["hg_norm_g"], (1, 2)).reshape(2, 128, 1)),
        "at_qn_g": f(inputs["at_qn_g"].reshape(2, 64, 1)), "at_kn_g": f(inputs["at_kn_g"].reshape(2, 64, 1)),
    }


def _rope_consts():
    n = np.arange(SEQ)
    r, c = n // 64, n % 64
    inv = (10000.0 ** (-np.arange(0, 32, 2, dtype=np.float32) / 32)).astype(np.float32)
    ang_r = r[:, None].astype(np.float32) * inv
    ang_c = c[:, None].astype(np.float32) * inv
    ang = np.concatenate([ang_r, ang_r, ang_c, ang_c], -1)
    rm = np.zeros((64, 64), np.float32)
    for i in range(16):
        rm[16 + i, i] = -1.0
        rm[i, 16 + i] = 1.0
        rm[48 + i, 32 + i] = -1.0
        rm[32 + i, 48 + i] = 1.0
    return np.ascontiguousarray(np.cos(ang).T.astype(np.float32)), np.ascontiguousarray(np.sin(ang).T.astype(np.float32)), rm


ROPE = _rope_consts()


def _dn_consts():
    a = np.arange(64)
    le = [(a[:, None] <= a[None, :]), (a[:, None] >= a[None, :])]
    st = [(a[None, :] < a[:, None]), (a[None, :] > a[:, None])]
    gm64 = np.zeros((64, 2, 2, 128), np.float32)
    gmbd = np.zeros((128, 2, 2, 128), np.float32)
    for d in range(2):
        gm64[:, d, 0, :] = np.tile(le[d], (1, 2))
        gm64[:, d, 1, :] = np.tile(st[d], (1, 2))
        for h in range(2):
            gmbd[h * 64:(h + 1) * 64, d, 0, h * 64:(h + 1) * 64] = st[d]
            gmbd[h * 64:(h + 1) * 64, d, 1, h * 64:(h + 1) * 64] = le[d]
    idn2 = np.tile(np.eye(64, dtype=np.float32), (2, 1))
    return gm64, gmbd, idn2


DN_GM64, DN_GMBD, DN_IDN2 = _dn_consts()
_s = np.arange(128)[:, None] % 64
_t = np.arange(64)[None, :]
HG_MASKS = np.ascontiguousarray(np.stack([(_s <= _t), (_s >= _t)], 1).astype(np.float32))
BD64 = np.kron(np.eye(2, dtype=np.float32), np.full((64, 64), 1.0 / 64, np.float32))
_CACHE = {}


def kernel(**inputs):
    NB = inputs["x"].shape[0] // N_CORES
    if "nc" not in _CACHE:
        _CACHE["nc"] = build(NB)
    nc = _CACHE["nc"]
    shared = None
    in_maps = []
    for c in range(N_CORES):
        in_maps.append(make_inputs(inputs, c, NB))
    res = run_bass_kernel_spmd(nc, in_maps, core_ids=list(range(N_CORES)))
    return np.concatenate([r["out"] for r in res.results], axis=0).astype(np.float32)
```

```python
import numpy as np
from contextlib import ExitStack
import concourse.bass as bass
import concourse.mybir as mybir
from concourse.bass_utils import run_bass_kernel_spmd

F32 = mybir.dt.float32
BF16 = mybir.dt.bfloat16
AF = mybir.ActivationFunctionType
ALU = mybir.AluOpType
AX = mybir.AxisListType

D = 1024
SEQ = 2048
CTX = 256
T = SEQ + CTX
DFF = 2816
NFF = DFF // 128
EPS = 1e-6
N_CORES = 8
ENG = ("pe", "act", "dve", "pool", "sp")

FM_GROUPS = [(0, 768), (768, 256), (1040, 256), (1296, 256), (1552, 512), (2320, 256), (2576, 256), (2832, 128)]
FM_W = sum(w for _, w in FM_GROUPS)
FM_Q, FM_K, FM_V, FM_Z, FM_U, FM_HQ, FM_HF, FM_HG, FM_AQ, FM_AK = 0, 256, 512, 768, 1024, 1280, 1536, 2048, 2304, 2560
TOK_GROUPS = [(2960, 128), (2064, 256), (1024, 8), (1032, 8)]
TOK_W = 400
TK_AV, TK_HV, TK_A, TK_B = 0, 128, 384, 392
GATE0 = 3088
SERIAL_DN = True
SERIAL_S5 = True


class Prog:
    def __init__(self, nc, n_dma_sems=56):
        self.nc = nc
        self.sem = {e: nc.semaphore("sem_" + e).__enter__() for e in ("pe", "act", "dve", "pool")}
        self.dsem = [nc.semaphore("dsem%d" % i).__enter__() for i in range(n_dma_sems)]
        self.duse = [0] * n_dma_sems
        self.dnext = 0
        self.cnt = {e: 0 for e in self.sem}
        self.lastw = {}
        self.readers = {}
        self.ops = {e: [] for e in ENG}
        self.waited = {}
        self.nops = 0
        self.serial = False
        self.last_ev = {}
        self.prev_ev = None

    def _need(self, eng, ev, waits):
        if ev is None:
            return
        key, val, src = ev
        if self.waited.get((eng, key), 0) >= val:
            return
        self.waited[(eng, key)] = val
        waits.append((key, val))

    def _semh(self, key):
        return self.sem[key] if isinstance(key, str) else self.dsem[key]

    def op(self, eng, fn, reads=(), writes=(), inc=True):
        waits = []
        for r in reads:
            ev = self.lastw.get(r)
            if ev is not None and not (ev[2] == eng and eng == "pe"):
                self._need(eng, ev, waits)
        for w in writes:
            ev = self.lastw.get(w)
            if ev is not None and ev[2] != eng:
                self._need(eng, ev, waits)
            for rv in self.readers.get(w, ()):
                if rv[2] != eng:
                    self._need(eng, rv, waits)
        if self.serial:
            waits = [w_ for w_ in waits if not isinstance(w_[0], str) or w_[0] == eng]
            for w_ in waits:
                pass
            if self.prev_ev is not None and self.prev_ev[2] != eng:
                if self.waited.get((eng, self.prev_ev[0]), 0) < self.prev_ev[1] or True:
                    self.waited[(eng, self.prev_ev[0])] = max(self.waited.get((eng, self.prev_ev[0]), 0), self.prev_ev[1])
                    waits.append((self.prev_ev[0], self.prev_ev[1]))
        if inc:
            self.cnt[eng] += 1
            me = (eng, self.cnt[eng], eng)
        else:
            me = (eng, self.cnt[eng] + 1, eng)
        self.last_ev[eng] = me
        self.prev_ev = me if inc else (self.prev_ev if not self.serial else me)
        for r in reads:
            self.readers.setdefault(r, []).append(me)
        for w in writes:
            self.lastw[w] = me
            self.readers[w] = []
        self.ops[eng].append((waits, fn, ("sem", eng) if inc else ("none", eng)))
        self.nops += 1
        return me

    def dma(self, fn, reads=(), writes=(), q="sp"):
        waits = []
        for r in reads:
            self._need(q, self.lastw.get(r), waits)
        for w in writes:
            self._need(q, self.lastw.get(w), waits)
            for rv in self.readers.get(w, ()):
                self._need(q, rv, waits)
        i = self.dnext
        self.dnext = (self.dnext + 1) % len(self.dsem)
        if self.duse[i] > 0:
            self._need(q, (i, 16 * self.duse[i], "dma"), waits)
        self.duse[i] += 1
        me = (i, 16 * self.duse[i], "dma")
        for r in reads:
            self.readers.setdefault(r, []).append(me)
        for w in writes:
            self.lastw[w] = me
            self.readers[w] = []
        self.ops[q].append((waits, fn, ("dsem", i)))
        self.nops += 1
        return me

    def end_stage(self):
        waits = []
        for tok, ev in self.lastw.items():
            self._need("sp", ev, waits)
        for tok, evs in self.readers.items():
            for ev in evs:
                self._need("sp", ev, waits)
        self.ops["sp"].append((waits, None, None))
        nc = self.nc
        engobj = {"pe": "tensor", "act": "scalar", "dve": "vector", "pool": "gpsimd", "sp": "sync"}
        with nc.Block() as block:
            for e in ENG:
                ops = self.ops[e]
                if not ops:
                    continue

                def body(eo, ops=ops):
                    for waits, fn, inc in ops:
                        for key, val in waits:
                            eo.wait_ge(self._semh(key), val)
                        if fn is None:
                            continue
                        ins = fn(eo)
                        if inc[0] == "sem":
                            ins.then_inc(self.sem[inc[1]], 1)
                        elif inc[0] == "dsem":
                            ins.then_inc(self.dsem[inc[1]], 16)

                getattr(block, engobj[e])(body)
        self.ops = {e: [] for e in ENG}
        self.waited = {}
        self.lastw = {}
        self.readers = {}
        self.last_ev = {}
        self.serial = False
        self.prev_ev = None


class K:
    pass


class NCProxy:
    def __init__(self, nc):
        object.__setattr__(self, "_nc", nc)
        object.__setattr__(self, "_n", [0])

    def __getattr__(self, name):
        return getattr(self._nc, name)

    def sbuf_tensor(self, name, shape, dtype):
        self._n[0] += 1
        return self._nc.sbuf_tensor("%s_u%d" % (name, self._n[0]), shape, dtype)

    def psum_tensor(self, name, shape, dtype):
        self._n[0] += 1
        return self._nc.psum_tensor("%s_u%d" % (name, self._n[0]), shape, dtype)


def token_tiles(NB, ts=512):
    tl = []
    for b in range(NB):
        tl.append((b, 0, CTX, NB))
        for i in range(SEQ // ts):
            tl.append((b, CTX + i * ts, ts, b))
    return tl


def build(NB, opts=None):
    opts = opts or {}
    NC = NB + 1
    nc = bass.Bass("TRN2", target_bir_lowering=False)
    k = K()
    k.nc, k.NB, k.NC, k.opts = NCProxy(nc), NB, NC, opts
    inp = lambda name, shape: nc.dram_tensor(name, list(shape), F32, kind="ExternalInput").ap()
    k.x = inp("x", [NB, SEQ, D])
    k.ctx = inp("ctx", [NB, CTX, D])
    k.cT = inp("cT", [128, 8, NC])
    k.ada_w = inp("ada_w", [2, D, 9 * D])
    k.ada_b = inp("ada_b", [2, 128, 72])
    k.norm_g = inp("norm_g", [2, 3, 128, 8])
    k.final_g = inp("final_g", [128, 8])
    k.w1 = inp("ffn_w1", [2, 2, D, DFF])
    k.w3 = inp("ffn_w3", [2, 2, D, DFF])
    k.w2 = inp("ffn_w2", [2, 2, DFF, D])
    k.w_fm = inp("w_fm", [2, D, FM_W])
    k.w_tok = inp("w_tok", [2, D, TOK_W])
    k.w_gate = inp("w_gate", [2, D, 4 * D])
    k.w_branch = inp("w_branch", [2, 4, 256, D])
    k.w_out = inp("w_out", [2, D, D])
    k.ident = inp("ident", [128, 128])
    k.rope_cos = inp("rope_cos", [64, SEQ])
    k.rope_sin = inp("rope_sin", [64, SEQ])
    k.rope_rm = inp("rope_rm", [64, 64])
    k.at_qn_g = inp("at_qn_g", [2, 64, 1])
    k.at_kn_g = inp("at_kn_g", [2, 64, 1])
    k.hg_masks = inp("hg_masks", [128, 2, 64])
    k.s5_lam_re = inp("s5_lam_re", [2, 128, 16])
    k.s5_lam_im = inp("s5_lam_im", [2, 128, 16])
    k.s5_log_step = inp("s5_log_step", [2, 128, 16])
    k.s5_b_re = inp("s5_b_re", [2, 128, 8, 16])
    k.s5_b_im = inp("s5_b_im", [2, 128, 8, 16])
    k.s5_c_re = inp("s5_c_re", [2, 128, 8, 16])
    k.s5_c_im = inp("s5_c_im", [2, 128, 8, 16])
    k.s5_d = inp("s5_d", [2, 128, 2])
    k.s5_glu = inp("s5_glu", [2, 256, 512])
    k.S5TAB = nc.dram_tensor("S5TAB", [16, 2, 128, T], F32).ap()
    k.dn_gm64 = inp("dn_gm64", [64, 2, 2, 128])
    k.dn_gmbd = inp("dn_gmbd", [128, 2, 2, 128])
    k.dn_idn2 = inp("dn_idn2", [128, 64])
    k.dn_conv = inp("dn_conv", [2, 128, 6, 5])
    k.dn_a_log = inp("dn_a_log", [2, 128, 8])
    k.dn_dt_bias = inp("dn_dt_bias", [2, 128, 8])
    k.dn_norm_g = inp("dn_norm_g", [2, 128, 64])
    k.bd64 = inp("bd64", [128, 128])
    k.hg_lb = inp("hg_lb", [128, 2, 2])
    k.hg_norm_g = inp("hg_norm_g", [2, 128, 1])
    k.out = nc.dram_tensor("out", [NB, SEQ, D], F32, kind="ExternalOutput").ap()
    k.XT = nc.dram_tensor("XT", [NB, D, T], F32).ap()
    only = opts.get("only")
    k.PT = nc.dram_tensor("PT", [NB, FM_W, T], F32, **({"kind": "ExternalInput"} if only else {})).ap()
    k.TOK = nc.dram_tensor("TOK", [NB, T, TOK_W], F32, **({"kind": "ExternalInput"} if only else {})).ap()
    k.YT = nc.dram_tensor("YT", [NB, 4, 256, T], BF16, **({"kind": "ExternalOutput"} if only else {})).ap()
    if "dump" in opts:
        k.dump = nc.dram_tensor("dump", [16, 128, T], F32, kind="ExternalOutput").ap()
    if "dbg" in opts:
        k.dbg = {nm: nc.dram_tensor("dbg_" + nm, list(shp), F32, kind="ExternalOutput").ap() for nm, shp in opts["dbg"].items()}
    P = Prog(nc)
    k.P = P
    with ExitStack() as es:
        al = lambda name, shape, dt=F32: es.enter_context(nc.sbuf_tensor(name, list(shape), dt))
        k.idn = al("idn", [128, 128])
        k.onesb = al("onesb", [128, 128], BF16)
        k.mods = al("mods", [128, 72, NC])
        k.ng = al("ng", [128, 2, 3, 8])
        k.fg = al("fg", [128, 8])
        P.dma(lambda e: e.dma_start(out=k.idn[:], in_=k.ident), writes=["idn"])
        P.dma(lambda e: e.dma_start(out=k.fg[:], in_=k.final_g), writes=["fg"])
        for l in range(2):
            for j in range(3):
                P.dma(lambda e, l=l, j=j: e.dma_start(out=k.ng[:, l, j, :], in_=k.norm_g[l, j]), writes=["ng"])
        P.op("dve", lambda e: e.memset(k.onesb[:], 1.0 / D), writes=["onesb"])
        P.end_stage()
        if only:
            for l in opts.get("layers", (0,)):
                {"att": stage_attention, "hg": stage_hgrn2, "dn": stage_deltanet, "s5": stage_s5}[only](k, l)
            return nc
        stage_transpose_in(k)
        stop = opts.get("stop", "")
        for l in range(2):
            stage_ada(k, l)
            stage_ffn(k, l, 0)
            if stop == "ffn1_%d" % l:
                break
            stage_inproj(k, l)
            if stop == "inproj_%d" % l:
                break
            stage_mixers(k, l)
            if stop == "mixers_%d" % l:
                break
            stage_merge(k, l)
            if stop == "merge_%d" % l:
                break
            stage_ffn(k, l, 1)
        if "XT" in opts.get("dbg", {}):
            P.dma(lambda e: e.dma_start(out=k.dbg["XT"], in_=k.XT), reads=[], writes=["dbgx"])
            P.end_stage()
        if "PT" in opts.get("dbg", {}):
            P.dma(lambda e: e.dma_start(out=k.dbg["PT"], in_=k.PT), reads=[], writes=["dbgp"])
            P.dma(lambda e: e.dma_start(out=k.dbg["TOK"], in_=k.TOK), reads=[], writes=["dbgt"])
            P.end_stage()
        stage_final(k)
    return nc


def stage_transpose_in(k):
    nc, P = k.nc, k.P
    with ExitStack() as es:
        xin = [es.enter_context(nc.sbuf_tensor("ti_x%d" % i, [128, D], F32)) for i in range(2)]
        xo = [es.enter_context(nc.sbuf_tensor("ti_o%d" % i, [128, 8, 128], F32)) for i in range(2)]
        ps = [es.enter_context(nc.psum_tensor("ti_ps%d" % i, [128, 4, 128], F32)) for i in range(4)]
        it = 0
        for b in range(k.NB):
            for tb in range(T // 128):
                s = it % 2
                src = k.ctx[b, tb * 128:(tb + 1) * 128, :] if tb < 2 else k.x[b, (tb - 2) * 128:(tb - 1) * 128, :]
                P.dma(lambda e, s=s, src=src: e.dma_start(out=xin[s][:], in_=src), writes=[("xin", s)])
                for h in range(2):
                    pi = (it * 2 + h) % 4
                    for c in range(4):
                        ch = h * 4 + c
                        P.op("pe", lambda e, s=s, pi=pi, c=c, ch=ch: e.transpose(ps[pi][:, c, :], xin[s][:, ch * 128:(ch + 1) * 128], k.idn[:]),
                             reads=[("xin", s), "idn"], writes=[("tps", pi)], inc=(c == 3))
                    eng = "act" if h == 0 else "dve"
                    if eng == "act":
                        P.op("act", lambda e, s=s, pi=pi, h=h: e.activation(out=xo[s][:, h * 4:(h + 1) * 4, :], in_=ps[pi][:], func=AF.Identity),
                             reads=[("tps", pi)], writes=[("xo", s, h)])
                    else:
                        P.op("dve", lambda e, s=s, pi=pi, h=h: e.tensor_copy(out=xo[s][:, h * 4:(h + 1) * 4, :], in_=ps[pi][:]),
                             reads=[("tps", pi)], writes=[("xo", s, h)])
                dst = k.XT[b].rearrange("(c p) t -> p c t", p=128)[:, :, tb * 128:(tb + 1) * 128]
                P.dma(lambda e, s=s, dst=dst: e.dma_start(out=dst, in_=xo[s][:]), reads=[("xo", s, 0), ("xo", s, 1)], writes=[("XT", b)])
                it += 1
        P.end_stage()


def stage_ada(k, l):
    nc, P, NC = k.nc, k.P, k.NC
    with ExitStack() as es:
        sc = es.enter_context(nc.sbuf_tensor("ad_sc", [128, 8, NC], F32))
        ab = es.enter_context(nc.sbuf_tensor("ad_b", [128, 72], F32))
        wt = [es.enter_context(nc.sbuf_tensor("ad_w%d" % i, [128, 8, 1024], F32)) for i in range(2)]
        ps = [es.enter_context(nc.psum_tensor("ad_ps%d" % i, [128, 8, NC], F32)) for i in range(2)]
        P.dma(lambda e: e.dma_start(out=sc[:], in_=k.cT), writes=["sc"])
        P.dma(lambda e: e.dma_start(out=ab[:], in_=k.ada_b[l]), writes=["ab"])
        P.op("act", lambda e: e.activation(out=sc[:], in_=sc[:], func=AF.Silu), reads=["sc"], writes=["sc"])
        for j in range(9):
            s = j % 2
            src = k.ada_w[l][:, j * 1024:(j + 1) * 1024].rearrange("(c p) n -> p c n", p=128)
            for hh in range(2):
                P.dma(lambda e, s=s, src=src, hh=hh: e.dma_start(out=wt[s][:, hh * 4:(hh + 1) * 4, :], in_=src[:, hh * 4:(hh + 1) * 4, :]), writes=[("adw", s, hh)])
            for m in range(8):
                for kc in range(8):
                    P.op("pe", lambda e, s=s, m=m, kc=kc: e.matmul(ps[s][:, m, :], lhsT=wt[s][:, kc, m * 128:(m + 1) * 128], rhs=sc[:, kc, :], start=(kc == 0), stop=(kc == 7)),
                         reads=[("adw", s, kc // 4), "sc"], writes=[("adps", s)], inc=(kc == 7 and m == 7))
            P.op("dve", lambda e, s=s, j=j: e.tensor_tensor(out=k.mods[:, j * 8:(j + 1) * 8, :], in0=ps[s][:], in1=ab[:, j * 8:(j + 1) * 8].rearrange("p (c o) -> p c o", o=1).to_broadcast([128, 8, NC]), op=ALU.add),
                 reads=[("adps", s), "ab"], writes=["mods"])
        P.end_stage()


def emit_norm_mod(k, es_tiles, x_t, h_t, n, A, Sh, cond, tag, sl):
    P = k.P
    sq, rs, tmp, psms = es_tiles
    P.op("act", lambda e: e.activation(out=sq[:, :, :n], in_=x_t[:, :, :n], func=AF.Square), reads=[("x", sl)], writes=["sq"])
    for c in range(8):
        P.op("pe", lambda e, c=c: e.matmul(psms[:, :n], lhsT=k.onesb[:], rhs=sq[:, c, :n], start=(c == 0), stop=(c == 7)), reads=["sq", "onesb"], writes=["psms"], inc=(c == 7))
    P.op("act", lambda e: e.activation(out=rs[:, :n], in_=psms[:, :n], func=AF.Sqrt, bias=k.epsb[:, 0:1], scale=1.0), reads=["psms", "epsb"], writes=["rs0", "rs"])
    P.op("dve", lambda e: e.reciprocal(out=rs[:, :n], in_=rs[:, :n]), reads=["rs0"], writes=["rs"])
    for c in range(8):
        P.op("dve", lambda e, c=c: e.tensor_tensor(out=tmp[c % 2][:, :n], in0=x_t[:, c, :n], in1=rs[:, :n], op=ALU.mult), reads=[("x", sl), "rs"], writes=[("tmp", c % 2)])
        P.op("pool", lambda e, c=c: e.tensor_scalar(out=h_t[:, c, :n], in0=tmp[c % 2][:, :n], scalar1=A[:, c, cond:cond + 1], scalar2=Sh[:, c, cond:cond + 1], op0=ALU.mult, op1=ALU.add),
             reads=[("tmp", c % 2), tag], writes=["h"])


def alloc_norm_tiles(k, es, pfx, ts=512):
    nc = k.nc
    sq = es.enter_context(nc.sbuf_tensor(pfx + "sq", [128, 8, ts], BF16))
    rs = es.enter_context(nc.sbuf_tensor(pfx + "rs", [128, ts], F32))
    tmp = [es.enter_context(nc.sbuf_tensor(pfx + "tmp%d" % i, [128, ts], F32)) for i in range(2)]
    psms = es.enter_context(nc.psum_tensor(pfx + "psms", [128, 512], F32))
    k.epsb = es.enter_context(nc.sbuf_tensor(pfx + "epsb", [128, 1], F32))
    k.P.op("dve", lambda e: e.memset(k.epsb[:], EPS), writes=["epsb"])
    return sq, rs, tmp, psms


def emit_mod_scalars(k, es, pfx, l, jn, i_shift, i_scale, i_gate, gate_mul):
    nc, P, NC = k.nc, k.P, k.NC
    A = es.enter_context(nc.sbuf_tensor(pfx + "A", [128, 8, NC], F32))
    G = es.enter_context(nc.sbuf_tensor(pfx + "G", [128, 8, NC], F32))
    Sh = k.mods[:, i_shift * 8:(i_shift + 1) * 8, :]
    P.op("dve", lambda e: e.tensor_scalar(out=A[:], in0=k.mods[:, i_scale * 8:(i_scale + 1) * 8, :], scalar1=1.0, scalar2=None, op0=ALU.add), reads=["mods"], writes=["A0"])
    P.op("dve", lambda e: e.tensor_tensor(out=A[:], in0=A[:], in1=k.ng[:, l, jn, :].rearrange("p (c o) -> p c o", o=1).to_broadcast([128, 8, NC]), op=ALU.mult), reads=["A0", "ng"], writes=["modsc"])
    if i_gate is not None:
        P.op("dve", lambda e: e.tensor_scalar(out=G[:], in0=k.mods[:, i_gate * 8:(i_gate + 1) * 8, :], scalar1=gate_mul, scalar2=None, op0=ALU.mult), reads=["mods"], writes=["modsg"])
    return A, Sh, G


def load_w_bf16(k, dst, src_rows, ncols, tok, nk=8):
    P = k.P
    for kc in range(nk):
        P.dma(lambda e, kc=kc: e.dma_start(out=dst[:, kc, :], in_=src_rows[kc * 128:(kc + 1) * 128, :], max_dma_last_dim=4096),
              writes=[(tok, kc)], q="pool")


def stage_ffn(k, l, i):
    nc, P, NB, NC = k.nc, k.P, k.NB, k.NC
    jn = 0 if i == 0 else 2
    mi = (0, 1, 2) if i == 0 else (6, 7, 8)
    last = (l == 1 and i == 1)
    with ExitStack() as es:
        w1 = es.enter_context(nc.sbuf_tensor("f_w1", [128, 8, DFF], BF16))
        w3 = es.enter_context(nc.sbuf_tensor("f_w3", [128, 8, DFF], BF16))
        w2 = es.enter_context(nc.sbuf_tensor("f_w2", [128, NFF, D], BF16))
        xt = [es.enter_context(nc.sbuf_tensor("f_x%d" % s, [128, 8, 512], F32)) for s in range(1)]
        ht = es.enter_context(nc.sbuf_tensor("f_h", [128, 8, 512], BF16))
        hid = es.enter_context(nc.sbuf_tensor("f_hid", [128, NFF, 512], BF16))
        sl_t = [es.enter_context(nc.sbuf_tensor("f_s%d" % s, [128, 512], F32)) for s in range(2)]
        ps1 = [es.enter_context(nc.psum_tensor("f_ps1%d" % s, [128, 512], F32)) for s in range(2)]
        ps3 = [es.enter_context(nc.psum_tensor("f_ps3%d" % s, [128, 512], F32)) for s in range(2)]
        pso = [es.enter_context(nc.psum_tensor("f_pso%d" % s, [128, 512], F32)) for s in range(2)]
        ntiles = alloc_norm_tiles(k, es, "f_", 512)
        A, Sh, G = emit_mod_scalars(k, es, "f_", l, jn, mi[0], mi[1], mi[2], 0.5)
        load_w_bf16(k, w1, k.w1[l, i], DFF, "w1")
        load_w_bf16(k, w3, k.w3[l, i], DFF, "w3")
        load_w_bf16(k, w2, k.w2[l, i], D, "w2", nk=NFF)
        it = 0
        for (b, t0, n, cond) in token_tiles(NB, 512):
            if last and cond == NB:
                continue
            s = 0
            it += 1
            xsrc = k.XT[b].rearrange("(c p) t -> p c t", p=128)[:, :, t0:t0 + n]
            P.dma(lambda e, s=s, xsrc=xsrc, n=n: e.dma_start(out=xt[s][:, :, :n], in_=xsrc), reads=[("XT", b)], writes=[("x", s)])
            emit_norm_mod(k, ntiles, xt[s], ht, n, A, Sh, cond, "modsc", s)
            for f in range(NFF):
                q = f % 2
                for kc in range(8):
                    P.op("pe", lambda e, f=f, q=q, kc=kc, n=n: e.matmul(ps1[q][:, :n], lhsT=w1[:, kc, f * 128:(f + 1) * 128], rhs=ht[:, kc, :n], start=(kc == 0), stop=(kc == 7)),
                         reads=[("w1", kc), "h"], writes=[("ps1", q)], inc=(kc == 7))
                for kc in range(8):
                    P.op("pe", lambda e, f=f, q=q, kc=kc, n=n: e.matmul(ps3[q][:, :n], lhsT=w3[:, kc, f * 128:(f + 1) * 128], rhs=ht[:, kc, :n], start=(kc == 0), stop=(kc == 7)),
                         reads=[("w3", kc), "h"], writes=[("ps3", q)], inc=(kc == 7))
                P.op("act", lambda e, q=q, n=n: e.activation(out=sl_t[q][:, :n], in_=ps1[q][:, :n], func=AF.Silu), reads=[("ps1", q)], writes=[("sl", q)])
                P.op("dve", lambda e, q=q, f=f, n=n: e.tensor_tensor(out=hid[:, f, :n], in0=sl_t[q][:, :n], in1=ps3[q][:, :n], op=ALU.mult), reads=[("sl", q), ("ps3", q)], writes=[("hid", f)])
            for m in range(8):
                q = m % 2
                for f in range(NFF):
                    P.op("pe", lambda e, m=m, q=q, f=f, n=n: e.matmul(pso[q][:, :n], lhsT=w2[:, f, m * 128:(m + 1) * 128], rhs=hid[:, f, :n], start=(f == 0), stop=(f == NFF - 1)),
                         reads=[("w2", f), ("hid", f)], writes=[("pso", q)], inc=(f == NFF - 1))
                P.op("dve", lambda e, m=m, q=q, s=s, n=n, cond=cond: e.scalar_tensor_tensor(out=xt[s][:, m, :n], in0=pso[q][:, :n], scalar=G[:, m, cond:cond + 1], in1=xt[s][:, m, :n], op0=ALU.mult, op1=ALU.add),
                     reads=[("pso", q), ("x", s), "modsg"], writes=[("x", s)])
            P.dma(lambda e, s=s, xsrc=xsrc, n=n: e.dma_start(out=xsrc, in_=xt[s][:, :, :n]), reads=[("x", s)], writes=[("XT", b)])
        P.end_stage()


def stage_inproj(k, l):
    nc, P, NB, NC = k.nc, k.P, k.NB, k.NC
    NCH = FM_W // 128
    with ExitStack() as es:
        wf = es.enter_context(nc.sbuf_tensor("p_wf", [128, 8, FM_W], BF16))
        wk = es.enter_context(nc.sbuf_tensor("p_wk", [128, 8, TOK_W], BF16))
        xt = [es.enter_context(nc.sbuf_tensor("p_x%d" % s, [128, 8, 512], F32)) for s in range(2)]
        ht = es.enter_context(nc.sbuf_tensor("p_h", [128, 8, 512], BF16))
        ot = [es.enter_context(nc.sbuf_tensor("p_o%d" % s, [128, 512], F32)) for s in range(4)]
        ps = [es.enter_context(nc.psum_tensor("p_ps%d" % s, [128, 512], F32)) for s in range(4)]
        ntiles = alloc_norm_tiles(k, es, "p_")
        A, Sh, G = emit_mod_scalars(k, es, "p_", l, 1, 3, 4, None, 1.0)
        load_w_bf16(k, wf, k.w_fm[l], FM_W, "wf")
        load_w_bf16(k, wk, k.w_tok[l], TOK_W, "wk")
        it = 0
        oi = 0
        for (b, t0, n, cond) in token_tiles(NB):
            s = it % 2
            it += 1
            xsrc = k.XT[b].rearrange("(c p) t -> p c t", p=128)[:, :, t0:t0 + n]
            P.dma(lambda e, s=s, xsrc=xsrc, n=n: e.dma_start(out=xt[s][:, :, :n], in_=xsrc), reads=[("XT", b)], writes=[("x", s)])
            emit_norm_mod(k, ntiles, xt[s], ht, n, A, Sh, cond, "modsc", s)
            for ch in range(NCH):
                q = oi % 4
                oi += 1
                for kc in range(8):
                    P.op("pe", lambda e, ch=ch, q=q, kc=kc, n=n: e.matmul(ps[q][:, :n], lhsT=wf[:, kc, ch * 128:(ch + 1) * 128], rhs=ht[:, kc, :n], start=(kc == 0), stop=(kc == 7)),
                         reads=[("wf", kc), "h"], writes=[("ps", q)], inc=(kc == 7))
                if oi % 2 == 0:
                    P.op("act", lambda e, q=q, n=n: e.activation(out=ot[q][:, :n], in_=ps[q][:, :n], func=AF.Identity), reads=[("ps", q)], writes=[("ot", q)])
                else:
                    P.op("dve", lambda e, q=q, n=n: e.tensor_copy(out=ot[q][:, :n], in_=ps[q][:, :n]), reads=[("ps", q)], writes=[("ot", q)])
                P.dma(lambda e, q=q, ch=ch, b=b, t0=t0, n=n: e.dma_start(out=k.PT[b, ch * 128:(ch + 1) * 128, t0:t0 + n], in_=ot[q][:, :n]), reads=[("ot", q)], writes=[("PT", b)])
            for tb in range(n // 128):
                q = oi % 4
                oi += 1
                for kc in range(8):
                    P.op("pe", lambda e, tb=tb, q=q, kc=kc: e.matmul(ps[q][:, :TOK_W], lhsT=ht[:, kc, tb * 128:(tb + 1) * 128], rhs=wk[:, kc, :], start=(kc == 0), stop=(kc == 7)),
                         reads=[("wk", kc), "h"], writes=[("ps", q)], inc=(kc == 7))
                P.op("dve", lambda e, q=q: e.tensor_copy(out=ot[q][:, :TOK_W], in_=ps[q][:, :TOK_W]), reads=[("ps", q)], writes=[("ot", q)])
                P.dma(lambda e, q=q, b=b, tb=tb, t0=t0: e.dma_start(out=k.TOK[b, t0 + tb * 128:t0 + (tb + 1) * 128, :], in_=ot[q][:, :TOK_W]), reads=[("ot", q)], writes=[("TOK", b)])
        P.end_stage()


def stage_attention(k, l):
    nc, P, NB = k.nc, k.P, k.NB
    NKB = T // 128
    with ExitStack() as es:
        al = lambda name, shape, dt=F32: es.enter_context(nc.sbuf_tensor("a_" + name, list(shape), dt))
        cos = al("cos", [64, SEQ]); sin = al("sin", [64, SEQ]); rm = al("rm", [64, 64]); o64 = al("o64", [64, 64])
        gq = al("gq", [64, 1]); gk = al("gk", [64, 1]); epsb = al("eps", [64, 1])
        onesb = al("onesb", [128, 64], BF16)
        raw = [al("raw%d" % i, [64, T]) for i in range(2)]
        kT = [al("kT%d" % i, [64, T], BF16) for i in range(2)]
        qT = [al("qT%d" % i, [64, T], BF16) for i in range(2)]
        vraw = al("vraw", [128, NKB, 128])
        vb = al("vb", [128, NKB, 128], BF16)
        sq = al("sq", [64, 512]); rs = al("rs", [64, 512]); kn = al("kn", [64, 512]); t1 = al("t1", [64, 512]); t2 = al("t2", [64, 512])
        pT = [al("pT%d" % i, [128, 512], BF16) for i in range(3)]
        rden = al("rden", [64, 512])
        ot = [al("ot%d" % i, [64, 512], BF16) for i in range(2)]
        psa = es.enter_context(nc.psum_tensor("a_psa", [64, 512], F32))
        psr = es.enter_context(nc.psum_tensor("a_psr", [64, 512], F32))
        pss = [es.enter_context(nc.psum_tensor("a_pss%d" % i, [128, 512], F32)) for i in range(3)]
        pso = es.enter_context(nc.psum_tensor("a_pso", [64, 512], F32))
        psd = es.enter_context(nc.psum_tensor("a_psd", [64, 512], F32))
        P.dma(lambda e: e.dma_start(out=cos[:], in_=k.rope_cos), writes=["cos"])
        P.dma(lambda e: e.dma_start(out=sin[:], in_=k.rope_sin), writes=["sin"])
        P.dma(lambda e: e.dma_start(out=rm[:], in_=k.rope_rm), writes=["rm"])
        P.dma(lambda e: e.dma_start(out=gq[:], in_=k.at_qn_g[l]), writes=["gq"])
        P.dma(lambda e: e.dma_start(out=gk[:], in_=k.at_kn_g[l]), writes=["gk"])
        P.op("dve", lambda e: e.memset(o64[:], 1.0 / 64), writes=["o64"])
        P.op("dve", lambda e: e.memset(epsb[:], EPS), writes=["epsb"])
        P.op("dve", lambda e: e.memset(onesb[:], 1.0), writes=["onesb"])
        cnt = {"raw": 0, "p": 0, "o": 0}

        def prep(b, row0, g_t, gtok, dst, dtok):
            ri = cnt["raw"] % 2
            cnt["raw"] += 1
            P.dma(lambda e: e.dma_start(out=raw[ri][:], in_=k.PT[b, row0:row0 + 64, :]), reads=[("PT", b)], writes=[("raw", ri)])
            for (t0, n) in [(0, CTX)] + [(CTX + i * 512, 512) for i in range(4)]:
                P.op("act", lambda e, t0=t0, n=n: e.activation(out=sq[:, :n], in_=raw[ri][:, t0:t0 + n], func=AF.Square), reads=[("raw", ri)], writes=["sq"])
                P.op("pe", lambda e, n=n: e.matmul(psa[:, :n], lhsT=o64[:], rhs=sq[:, :n], start=True, stop=True), reads=["sq", "o64"], writes=["psa"])
                P.op("act", lambda e, n=n: e.activation(out=rs[:, :n], in_=psa[:, :n], func=AF.Sqrt, bias=epsb[:, 0:1], scale=1.0), reads=["psa", "epsb"], writes=["rs0", "rs"])
                P.op("dve", lambda e, n=n: e.reciprocal(out=rs[:, :n], in_=rs[:, :n]), reads=["rs0"], writes=["rs"])
                if t0 == 0:
                    P.op("dve", lambda e, t0=t0, n=n: e.scalar_tensor_tensor(out=dst[:, t0:t0 + n], in0=raw[ri][:, t0:t0 + n], scalar=g_t[:, 0:1], in1=rs[:, :n], op0=ALU.mult, op1=ALU.mult),
                         reads=[("raw", ri), "rs", gtok], writes=[dtok])
                    continue
                P.op("dve", lambda e, t0=t0, n=n: e.scalar_tensor_tensor(out=kn[:, :n], in0=raw[ri][:, t0:t0 + n], scalar=g_t[:, 0:1], in1=rs[:, :n], op0=ALU.mult, op1=ALU.mult),
                     reads=[("raw", ri), "rs", gtok], writes=["kn"])
                P.op("pe", lambda e, n=n: e.matmul(psr[:, :n], lhsT=rm[:], rhs=kn[:, :n], start=True, stop=True), reads=["kn", "rm"], writes=["psr"])
                P.op("pool", lambda e, t0=t0, n=n: e.tensor_tensor(out=t1[:, :n], in0=kn[:, :n], in1=cos[:, t0 - CTX:t0 - CTX + n], op=ALU.mult), reads=["kn", "cos"], writes=["t1"])
                P.op("dve", lambda e, t0=t0, n=n: e.tensor_tensor(out=t2[:, :n], in0=psr[:, :n], in1=sin[:, t0 - CTX:t0 - CTX + n], op=ALU.mult), reads=["psr", "sin"], writes=["t2"])
                P.op("dve", lambda e, t0=t0, n=n: e.tensor_tensor(out=dst[:, t0:t0 + n], in0=t1[:, :n], in1=t2[:, :n], op=ALU.add), reads=["t1", "t2"], writes=[dtok])

        def do_tile(b, hq, kvh, qi, q0, nq, kbs):
            def s_mm(kb, pi):
                P.op("pe", lambda e: e.matmul(pss[pi][:, :nq], lhsT=kT[kvh][:, kb * 128:(kb + 1) * 128], rhs=qT[qi][:, q0:q0 + nq], start=True, stop=True),
                     reads=[("kT", kvh), ("qT", qi)], writes=[("pss", pi)])
            pis = []
            for j in range(len(kbs)):
                pis.append(cnt["p"] % 3)
                cnt["p"] += 1
            s_mm(kbs[0], pis[0])
            for j, kb in enumerate(kbs):
                if j + 1 < len(kbs):
                    s_mm(kbs[j + 1], pis[j + 1])
                pi = pis[j]
                P.op("act", lambda e, pi=pi: e.activation(out=pT[pi][:, :nq], in_=pss[pi][:, :nq], func=AF.Exp, scale=0.125), reads=[("pss", pi)], writes=[("pT", pi)])
                P.op("pe", lambda e, pi=pi, kb=kb, j=j: e.matmul(pso[:, :nq], lhsT=vb[:, kb, kvh * 64:(kvh + 1) * 64], rhs=pT[pi][:, :nq], start=(j == 0), stop=(j == len(kbs) - 1)),
                     reads=[("pT", pi), "vb"], writes=["pso"], inc=False)
                P.op("pe", lambda e, pi=pi, j=j: e.matmul(psd[:, :nq], lhsT=onesb[:], rhs=pT[pi][:, :nq], start=(j == 0), stop=(j == len(kbs) - 1)),
                     reads=[("pT", pi), "onesb"], writes=["psd"])
            oi = cnt["o"] % 2
            cnt["o"] += 1
            P.op("dve", lambda e: e.reciprocal(out=rden[:, :nq], in_=psd[:, :nq]), reads=["psd"], writes=["rden"])
            P.op("dve", lambda e: e.tensor_tensor(out=ot[oi][:, :nq], in0=pso[:, :nq], in1=rden[:, :nq], op=ALU.mult), reads=["pso", "rden"], writes=[("ot", oi)])
            P.dma(lambda e: e.dma_start(out=k.YT[b, 3, hq * 64:(hq + 1) * 64, q0:q0 + nq], in_=ot[oi][:, :nq]), reads=[("ot", oi)], writes=[("YT", b)])

        for b in range(NB):
            for kvh in range(2):
                prep(b, FM_AK + kvh * 64, gk, "gk", kT[kvh], ("kT", kvh))
            vsrc = k.TOK[b].rearrange("(blk p) c -> p blk c", p=128)[:, :, TK_AV:TK_AV + 128]
            P.dma(lambda e, vsrc=vsrc: e.dma_start(out=vraw[:], in_=vsrc), reads=[("TOK", b)], writes=["vraw"])
            P.op("pool", lambda e: e.tensor_copy(out=vb[:], in_=vraw[:]), reads=["vraw"], writes=["vb"])
            for hq in range(4):
                kvh = hq // 2
                qi = hq % 2
                prep(b, FM_AQ + hq * 64, gq, "gq", qT[qi], ("qT", qi))
                for (q0, nq, kbs) in [(0, CTX, [0, 1])] + [(CTX + i * 512, 512, list(range(NKB))) for i in range(4)]:
                    do_tile(b, hq, kvh, qi, q0, nq, kbs)
        P.end_stage()


def stage_hgrn2(k, l):
    nc, P, NB = k.nc, k.P, k.NB
    NCK = T // 64
    with ExitStack() as es:
        al = lambda name, shape, dt=F32: es.enter_context(nc.sbuf_tensor("h_" + name, list(shape), dt))
        m01 = al("m01", [128, T]); mk = al("mk", [128, 2, 64]); bd = al("bd", [128, 128])
        lg = al("lg", [128, 2, 2]); lb = al("lb", [128, 2]); oml = al("oml", [128, 2]); gn = al("gn", [128, 1]); epsb = al("eps", [128, 1])
        z = al("z", [128, T]); fgt = al("fgt", [128, T]); bb = al("bb", [128, T]); tmp = al("tmp", [128, T])
        ex = [al("ex%d" % i, [128, T]) for i in range(2)]
        q = al("q", [128, T]); kk = al("kk", [128, T]); kd = al("kd", [128, T]); O = al("O", [128, T])
        qt = al("qt", [128, T], BF16); ktI = [al("kt%d" % i, [128, T], BF16) for i in range(4)]; qd = al("qd", [128, T], BF16)
        dec = al("dec", [128, NCK, 1])
        Vb = al("Vb", [128, NCK, 256], BF16)
        kdT = [al("kdT%d" % i, [64, 128], BF16) for i in range(2)]
        scm = [al("scm%d" % i, [128, 64], BF16) for i in range(2)]
        S32 = al("S32", [128, 64]); S16 = [al("S16%d" % i, [128, 64], BF16) for i in range(2)]
        sq = al("sq", [128, 512]); rs = al("rs", [128, 512]); yb = [al("yb%d" % i, [128, 512], BF16) for i in range(2)]
        pst = [es.enter_context(nc.psum_tensor("h_pst%d" % i, [128, 512], F32)) for i in range(2)]
        pss = [es.enter_context(nc.psum_tensor("h_pss%d" % i, [128, 512], F32)) for i in range(2)]
        pso = [es.enter_context(nc.psum_tensor("h_pso%d" % i, [128, 512], F32)) for i in range(2)]
        pskv = [es.enter_context(nc.psum_tensor("h_pskv%d" % i, [128, 512], F32)) for i in range(2)]
        P.dma(lambda e: e.dma_start(out=mk[:], in_=k.hg_masks), writes=["mk"])
        P.dma(lambda e: e.dma_start(out=bd[:], in_=k.bd64), writes=["bd"])
        P.dma(lambda e: e.dma_start(out=lg[:], in_=k.hg_lb), writes=["lg"])
        P.dma(lambda e: e.dma_start(out=gn[:], in_=k.hg_norm_g[l]), writes=["gn"])
        P.op("dve", lambda e: e.memset(epsb[:], EPS), writes=["epsb"])
        P.op("dve", lambda e: e.memset(m01[:], 1.0), writes=["m01"])
        P.op("dve", lambda e: e.memset(m01[:].rearrange("p (c j) -> p c j", j=64)[:, :, 0:1], 0.0), writes=["m01"])
        for i4 in range(4):
            P.op("pool", lambda e, i4=i4: e.memset(ktI[i4][:], 0.0), writes=[("kt", i4)])
        if l == 0:
            P.op("dve", lambda e: e.memset(lb[:], 0.0), writes=["lb"])
            P.op("dve", lambda e: e.memset(oml[:], 1.0), writes=["oml"])
        else:
            P.op("dve", lambda e: e.tensor_tensor(out=lb[:], in0=lg[:, :, 1], in1=lg[:, :, 0], op=ALU.subtract), reads=["lg"], writes=["lb0"])
            P.op("act", lambda e: e.activation(out=lb[:], in_=lb[:], func=AF.Sigmoid), reads=["lb0"], writes=["lb", "lb0"])
            P.op("dve", lambda e: e.tensor_scalar(out=oml[:], in0=lb[:], scalar1=-1.0, scalar2=1.0, op0=ALU.mult, op1=ALU.add), reads=["lb"], writes=["oml"])
        cnt = {"c": 0, "y": 0}
        bb3 = bb[:].rearrange("p (c j) -> p c j", j=64)
        tmp3 = tmp[:].rearrange("p (c j) -> p c j", j=64)

        def do_chunk(b, hp, d, c, first):
            i = cnt["c"] % 2
            cnt["c"] += 1
            cs = slice(c * 64, (c + 1) * 64)
            P.op("pe", lambda e: e.transpose(pst[i][:64, :128], kd[:, cs], k.idn[:]), reads=["kd", "idn"], writes=[("pst", i)])
            P.op("act", lambda e: e.activation(out=kdT[i][:], in_=pst[i][:64, :128], func=AF.Identity), reads=[("pst", i)], writes=[("kdT", i)])
            for h2 in range(2):
                pb = h2 * 64
                for I in range(4):
                    ts_ = slice(c * 64 + 16 * I, c * 64 + 16 * I + 16)
                    P.op("pe", lambda e, pb=pb, I=I, ts_=ts_: e.matmul(pss[i][pb:pb + 64, 16 * I:16 * I + 16], lhsT=ktI[I][pb:pb + 64, cs], rhs=qt[pb:pb + 64, ts_], start=True, stop=True),
                         reads=[("kt", I), "qt"], writes=[("pss", i)], inc=(h2 == 1 and I == 3))
            for h2 in range(2):
                pb = h2 * 64
                h = hp * 2 + h2
                P.op("pe", lambda e, pb=pb, h=h: e.matmul(pskv[i][pb:pb + 64, :64], lhsT=kdT[i][:, pb:pb + 64], rhs=Vb[0:64, c, h * 64:(h + 1) * 64], start=True, stop=True),
                     reads=[("kdT", i), "Vb"], writes=[("pskv", i)], inc=(h2 == 1))
            P.op("dve", lambda e: e.tensor_tensor(out=scm[i][:], in0=pss[i][:, :64], in1=mk[:, d, :], op=ALU.mult), reads=[("pss", i), "mk"], writes=[("scm", i)])
            for h2 in range(2):
                pb = h2 * 64
                h = hp * 2 + h2
                P.op("pe", lambda e, pb=pb, h=h: e.matmul(pso[i][pb:pb + 64, :64], lhsT=Vb[pb:pb + 64, c, h * 64:(h + 1) * 64], rhs=scm[i][pb:pb + 64, :], start=True, stop=False),
                     reads=[("scm", i), "Vb"], writes=[("pso", i)], inc=False)
                P.op("pe", lambda e, pb=pb: e.matmul(pso[i][pb:pb + 64, :64], lhsT=S16[1 - i][pb:pb + 64, :], rhs=qd[pb:pb + 64, cs], start=False, stop=True),
                     reads=[("S16", 1 - i), "qd"], writes=[("pso", i)], inc=(h2 == 1))
            if d == 0:
                P.op("act", lambda e: e.activation(out=O[:, cs], in_=pso[i][:, :64], func=AF.Identity), reads=[("pso", i)], writes=["O"])
            else:
                P.op("dve", lambda e: e.tensor_tensor(out=O[:, cs], in0=O[:, cs], in1=pso[i][:, :64], op=ALU.add), reads=[("pso", i), "O"], writes=["O"])
            P.op("dve", lambda e: e.scalar_tensor_tensor(out=S32[:], in0=S32[:], scalar=dec[:, c, :], in1=pskv[i][:, :64], op0=ALU.mult, op1=ALU.add),
                 reads=["S32", "dec", ("pskv", i)], writes=["S32"])
            P.op("act", lambda e: e.activation(out=S16[i][:], in_=S32[:], func=AF.Identity), reads=["S32"], writes=[("S16", i)])

        def do_dir(b, hp, d):
            rev = (lambda ap: ap[:, ::-1]) if d == 1 else (lambda ap: ap)
            last = 63 if d == 0 else 0
            r0 = FM_HF + d * 256 + hp * 128
            P.dma(lambda e: e.dma_start(out=z[:], in_=k.PT[b, r0:r0 + 128, :]), reads=[("PT", b)], writes=["z"])
            P.op("act", lambda e: e.activation(out=fgt[:], in_=z[:], func=AF.Sigmoid), reads=["z"], writes=["fgt"])
            P.op("dve", lambda e: e.tensor_scalar(out=fgt[:], in0=fgt[:], scalar1=oml[:, hp:hp + 1], scalar2=lb[:, hp:hp + 1], op0=ALU.mult, op1=ALU.add), reads=["fgt", "oml", "lb"], writes=["fgt"])
            P.op("dve", lambda e: e.tensor_scalar(out=fgt[:], in0=fgt[:], scalar1=1e-30, scalar2=None, op0=ALU.max), reads=["fgt"], writes=["fgt"])
            P.op("act", lambda e: e.activation(out=z[:], in_=fgt[:], func=AF.Ln), reads=["fgt"], writes=["z"])
            P.op("dve", lambda e: e.tensor_scalar(out=kk[:], in0=fgt[:], scalar1=-1.0, scalar2=1.0, op0=ALU.mult, op1=ALU.add), reads=["fgt"], writes=["kk"])
            P.op("dve", lambda e: e.tensor_tensor_scan(out=rev(bb[:]), data0=m01[:], data1=rev(z[:]), initial=0.0, op0=ALU.mult, op1=ALU.add), reads=["z", "m01"], writes=["bb"])
            ref = 0 if d == 0 else 15
            bb4 = bb[:].rearrange("p (c i j) -> p c i j", i=4, j=16)
            tmp4 = tmp[:].rearrange("p (c i j) -> p c i j", i=4, j=16)
            P.op("dve", lambda e: e.tensor_tensor(out=tmp4, in0=bb4, in1=bb4[:, :, :, ref:ref + 1].to_broadcast([128, NCK, 4, 16]), op=ALU.subtract), reads=["bb"], writes=["tmp"])
            P.op("act", lambda e: e.activation(out=ex[0][:], in_=tmp[:], func=AF.Exp), reads=["tmp"], writes=[("ex", 0)])
            P.op("pool", lambda e: e.tensor_tensor(out=qt[:], in0=q[:], in1=ex[0][:], op=ALU.mult), reads=["q", ("ex", 0)], writes=["qt"])
            ex3 = [ex[i][:].rearrange("p (c j) -> p c j", j=64) for i in range(2)]
            kk3 = kk[:].rearrange("p (c j) -> p c j", j=64)
            for I in range(4):
                cs_ = slice(0, 16 * (I + 1)) if d == 0 else slice(16 * I, 64)
                w = cs_.stop - cs_.start
                e_ = ex3[I % 2][:, :, cs_]
                rp = 16 * I + ref
                kt3 = ktI[I][:].rearrange("p (c j) -> p c j", j=64)[:, :, cs_]
                P.op("dve", lambda e, cs_=cs_, w=w, rp=rp: e.scalar_tensor_tensor(out=tmp3[:, :, cs_], in0=bb3[:, :, cs_], scalar=-1.0, in1=bb3[:, :, rp:rp + 1].to_broadcast([128, NCK, w]), op0=ALU.mult, op1=ALU.add),
                     reads=["bb"], writes=["tmp"])
                P.op("dve", lambda e, cs_=cs_: e.tensor_scalar(out=tmp3[:, :, cs_], in0=tmp3[:, :, cs_], scalar1=60.0, scalar2=None, op0=ALU.min), reads=["tmp"], writes=["tmp"])
                P.op("act", lambda e, cs_=cs_, e_=e_: e.activation(out=e_, in_=tmp3[:, :, cs_], func=AF.Exp), reads=["tmp"], writes=[("ex", I % 2)])
                P.op("pool", lambda e, cs_=cs_, e_=e_, kt3=kt3: e.tensor_tensor(out=kt3, in0=kk3[:, :, cs_], in1=e_, op=ALU.mult), reads=["kk", ("ex", I % 2)], writes=[("kt", I)])
            P.op("act", lambda e: e.activation(out=ex[0][:], in_=bb[:], func=AF.Exp), reads=["bb"], writes=[("ex", 0)])
            P.op("dve", lambda e: e.tensor_tensor(out=qd[:], in0=q[:], in1=ex[0][:], op=ALU.mult), reads=["q", ("ex", 0)], writes=["qd"])
            P.op("dve", lambda e: e.tensor_tensor(out=tmp3, in0=bb3, in1=bb3[:, :, last:last + 1].to_broadcast([128, NCK, 64]), op=ALU.subtract), reads=["bb"], writes=["tmp"])
            P.op("act", lambda e: e.activation(out=ex[1][:], in_=tmp[:], func=AF.Exp, scale=-1.0), reads=["tmp"], writes=[("ex", 1)])
            P.op("pool", lambda e: e.tensor_tensor(out=kd[:], in0=kk[:], in1=ex[1][:], op=ALU.mult), reads=["kk", ("ex", 1)], writes=["kd"])
            P.op("act", lambda e: e.activation(out=dec[:], in_=bb3[:, :, last:last + 1], func=AF.Exp), reads=["bb"], writes=["dec"])
            if "dump" in k.opts and (b, hp, d) == k.opts["dump"]:
                for j, (tl, tk) in enumerate([(z, "z"), (bb, "bb"), (kd, "kd"), (kk, "kk"), (fgt, "fgt")]):
                    P.dma(lambda e, j=j, tl=tl: e.dma_start(out=k.dump[j], in_=tl[:]), reads=[tk], writes=[("dump", j)])
            P.op("dve", lambda e: e.memset(S32[:], 0.0), writes=["S32"])
            for i in range(2):
                P.op("dve", lambda e, i=i: e.memset(S16[i][:], 0.0), writes=[("S16", i)])
            order = list(range(NCK)) if d == 0 else [3, 2, 1, 0] + list(range(NCK - 1, 3, -1))
            for idx, c in enumerate(order):
                do_chunk(b, hp, d, c, idx == 0)

        def do_pair(b, hp):
            r0 = FM_HQ + hp * 128
            P.dma(lambda e: e.dma_start(out=q[:], in_=k.PT[b, r0:r0 + 128, :]), reads=[("PT", b)], writes=["q"])
            P.op("act", lambda e: e.activation(out=q[:], in_=q[:], func=AF.Silu), reads=["q"], writes=["q"])
            for d in range(2):
                do_dir(b, hp, d)
            if "dump" in k.opts and (b, hp) == k.opts["dump"][:2]:
                P.dma(lambda e: e.dma_start(out=k.dump[5], in_=O[:]), reads=["O"], writes=[("dump", 5)])
            g0 = FM_HG + hp * 128
            P.dma(lambda e: e.dma_start(out=z[:], in_=k.PT[b, g0:g0 + 128, :]), reads=[("PT", b)], writes=["z"])
            P.op("act", lambda e: e.activation(out=z[:], in_=z[:], func=AF.Sigmoid), reads=["z"], writes=["z"])
            for (t0, n) in [(0, CTX)] + [(CTX + i * 512, 512) for i in range(4)]:
                do_out(b, hp, t0, n)

        def do_out(b, hp, t0, n):
            yi = cnt["y"] % 2
            cnt["y"] += 1
            P.op("act", lambda e: e.activation(out=sq[:, :n], in_=O[:, t0:t0 + n], func=AF.Square), reads=["O"], writes=["sq"])
            P.op("pe", lambda e: e.matmul(pss[0][:, :n], lhsT=bd[:], rhs=sq[:, :n], start=True, stop=True), reads=["sq", "bd"], writes=[("pss", 0)])
            P.op("act", lambda e: e.activation(out=rs[:, :n], in_=pss[0][:, :n], func=AF.Sqrt, bias=epsb[:, 0:1], scale=1.0), reads=[("pss", 0), "epsb"], writes=["rs0", "rs"])
            P.op("dve", lambda e: e.reciprocal(out=rs[:, :n], in_=rs[:, :n]), reads=["rs0"], writes=["rs"])
            P.op("dve", lambda e: e.scalar_tensor_tensor(out=sq[:, :n], in0=O[:, t0:t0 + n], scalar=gn[:, 0:1], in1=rs[:, :n], op0=ALU.mult, op1=ALU.mult), reads=["O", "gn", "rs"], writes=["sq"])
            P.op("pool", lambda e: e.tensor_tensor(out=yb[yi][:, :n], in0=sq[:, :n], in1=z[:, t0:t0 + n], op=ALU.mult), reads=["sq", "z"], writes=[("yb", yi)])
            P.dma(lambda e: e.dma_start(out=k.YT[b, 2, hp * 128:(hp + 1) * 128, t0:t0 + n], in_=yb[yi][:, :n]), reads=[("yb", yi)], writes=[("YT", b)])

        for b in range(NB):
            vsrc = k.TOK[b].rearrange("(c s) v -> s c v", s=64)[:, :, TK_HV:TK_HV + 256]
            for h2 in range(2):
                P.dma(lambda e, h2=h2, vsrc=vsrc: e.dma_start(out=Vb[h2 * 64:(h2 + 1) * 64, :, :], in_=vsrc), reads=[("TOK", b)], writes=["Vb"], q="pool")
            for hp in range(2):
                do_pair(b, hp)
        P.end_stage()


class Slots:
    def __init__(self, aps, name, mod=0, off=0):
        self.aps, self.name, self.i, self.mod, self.off = aps, name, 0, mod, off

    def get(self):
        j = self.i % len(self.aps)
        self.i += 1
        return self.aps[j], (self.name, self.off + (j % self.mod if self.mod else j))


def stage_deltanet(k, l):
    nc, P, NB = k.nc, k.P, k.NB
    NCK = T // 64
    P.serial = k.opts.get("serial_dn", SERIAL_DN)
    with ExitStack() as es:
        al = lambda name, shape, dt=F32: es.enter_context(nc.sbuf_tensor("d_" + name, list(shape), dt))
        gm64 = al("gm64", [64, 2, 2, 128]); gmbd = al("gmbd", [128, 2, 2, 128]); idn2 = al("idn2", [128, 64]); bd = al("bd", [128, 128])
        ones64 = al("ones64", [64, 128])
        cw = al("cw", [128, 6, 5]); alog = al("alog", [128, 8]); dtb = al("dtb", [128, 8]); gno = al("gno", [128, 64]); epsb = al("eps", [128, 1])
        qkv = [al("qkv%d" % i, [128, T]) for i in range(6)]
        xin = al("xin", [128, T]); acc = al("acc", [128, T])
        sq = al("sq", [128, 512]); rs = al("rs", [128, 512])
        ab = al("ab", [128, NCK, 16]); gt = al("gt", [128, NCK, 8]); bt = al("bt", [128, NCK, 8]); t8 = al("t8", [128, NCK, 8]); t8b = al("t8b", [128, NCK, 8])
        gcs = al("gcs", [128, NCK, 8]); gts = al("gts", [128, NCK, 8])
        gcBD = al("gcBD", [128, NCK, 4]); gtBD = al("gtBD", [128, NCK, 4]); bBD = al("bBD", [128, NCK, 4])
        eg = al("eg", [128, NCK, 4]); ekd = al("ekd", [128, NCK, 4]); glast = al("glast", [128, NCK, 4]); nbeta = al("nbeta", [128, NCK, 4]); wsc = al("wsc", [128, NCK, 4])
        u_all = al("u_all", [128, NCK, 64]); wT_all = al("wT_all", [128, NCK, 64]); kdec_all = al("kdec_all", [128, NCK, 64]); attnT_all = al("attnT_all", [128, NCK, 128])
        O_tok = al("O_tok", [128, NCK, 64]); ssq = al("ssq", [128, NCK]); S = al("S", [128, 64])
        yb = [al("yb%d" % i, [128, 512], BF16) for i in range(2)]
        wk = Slots([al("wk%d" % i, [128, 128])[:] for i in range(28)], "wk")
        wv = Slots([al("wv%d" % i, [128, 64])[:] for i in range(8)], "wv")
        wr = Slots([al("wr%d" % i, [128, 128])[:] for i in range(6)], "wr")
        banks = [es.enter_context(nc.psum_tensor("d_ps%d" % i, [128, 512], F32)) for i in range(8)]
        ps = Slots([banks[i][:, j * 128:(j + 1) * 128] for j in range(4) for i in range(4)], "ps", mod=4)
        ps2 = Slots([banks[4 + i][:, j * 256:(j + 1) * 256] for j in range(2) for i in range(3)], "ps", mod=3, off=4)
        wk2 = Slots([al("wkp%d" % i, [128, 256])[:] for i in range(10)], "wk2")
        psg = banks[7]
        for i in range(8):
            P.op("dve", lambda e, i=i: e.memset(banks[i][:], 0.0), writes=[("ps", j) for j in range(7)] + ["psg"])
        P.dma(lambda e: e.dma_start(out=gm64[:], in_=k.dn_gm64), writes=["gm64"])
        P.dma(lambda e: e.dma_start(out=gmbd[:], in_=k.dn_gmbd), writes=["gmbd"])
        P.dma(lambda e: e.dma_start(out=idn2[:], in_=k.dn_idn2), writes=["idn2"])
        P.dma(lambda e: e.dma_start(out=bd[:], in_=k.bd64), writes=["bd"])
        P.dma(lambda e: e.dma_start(out=cw[:], in_=k.dn_conv[l]), writes=["cw"])
        P.dma(lambda e: e.dma_start(out=alog[:], in_=k.dn_a_log[l]), writes=["alog"])
        P.dma(lambda e: e.dma_start(out=dtb[:], in_=k.dn_dt_bias[l]), writes=["dtb"])
        P.dma(lambda e: e.dma_start(out=gno[:], in_=k.dn_norm_g[l]), writes=["gno"])
        P.op("dve", lambda e: e.memset(epsb[:], EPS), writes=["epsb"])
        P.op("dve", lambda e: e.memset(ones64[:], 1.0), writes=["ones64"])
        P.op("act", lambda e: e.activation(out=alog[:], in_=alog[:], func=AF.Exp), reads=["alog"], writes=["alog"])
        P.op("dve", lambda e: e.tensor_scalar(out=alog[:], in0=alog[:], scalar1=-1.0, scalar2=None, op0=ALU.mult), reads=["alog"], writes=["alog"])
        cnt = {"y": 0}

        def prep_tile(b, ti):
            r0 = ti * 128
            P.dma(lambda e: e.dma_start(out=xin[:], in_=k.PT[b, r0:r0 + 128, :]), reads=[("PT", b)], writes=["xin"])
            P.op("dve", lambda e: e.tensor_scalar(out=acc[:], in0=xin[:], scalar1=cw[:, ti, 2:3], scalar2=None, op0=ALU.mult), reads=["xin", "cw"], writes=["acc"])
            for (s0, s1) in [(0, CTX), (CTX, T)]:
                for j in (0, 1, 3, 4):
                    sh = j - 2
                    o0, o1 = max(s0, s0 - sh), min(s1, s1 - sh)
                    P.op("dve", lambda e, j=j, sh=sh, o0=o0, o1=o1: e.scalar_tensor_tensor(out=acc[:, o0:o1], in0=xin[:, o0 + sh:o1 + sh], scalar=cw[:, ti, j:j + 1], in1=acc[:, o0:o1], op0=ALU.mult, op1=ALU.add),
                         reads=["xin", "cw", "acc"], writes=["acc"])
            dst = qkv[ti]
            if ti >= 4:
                P.op("act", lambda e: e.activation(out=dst[:], in_=acc[:], func=AF.Silu), reads=["acc"], writes=[("qkv", ti)])
                return
            P.op("act", lambda e: e.activation(out=acc[:], in_=acc[:], func=AF.Silu), reads=["acc"], writes=["acc"])
            qs = 0.125 if ti < 2 else 1.0
            for (t0, n) in [(0, CTX)] + [(CTX + i * 512, 512) for i in range(4)]:
                norm_tile(dst, ti, t0, n, qs)

        def norm_tile(dst, ti, t0, n, qs):
            pp, pt = ps.get()
            bank_ap = banks[0]
            P.op("act", lambda e: e.activation(out=sq[:, :n], in_=acc[:, t0:t0 + n], func=AF.Square), reads=["acc"], writes=["sq"])
            P.op("pe", lambda e: e.matmul(psg[:, :n], lhsT=bd[:], rhs=sq[:, :n], start=True, stop=True), reads=["sq", "bd"], writes=["psg"])
            P.op("act", lambda e: e.activation(out=rs[:, :n], in_=psg[:, :n], func=AF.Sqrt, bias=epsb[:, 0:1], scale=64.0), reads=["psg", "epsb"], writes=["rs0", "rs"])
            P.op("dve", lambda e: e.reciprocal(out=rs[:, :n], in_=rs[:, :n]), reads=["rs0"], writes=["rs"])
            P.op("dve", lambda e: e.scalar_tensor_tensor(out=dst[:, t0:t0 + n], in0=acc[:, t0:t0 + n], scalar=qs, in1=rs[:, :n], op0=ALU.mult, op1=ALU.mult), reads=["acc", "rs"], writes=[("qkv", ti)])

        def prep_gates(b):
            src = k.TOK[b].rearrange("(c s) v -> s c v", s=64)[:, :, TK_A:TK_A + 16]
            for h2 in range(2):
                P.dma(lambda e, h2=h2: e.dma_start(out=ab[h2 * 64:(h2 + 1) * 64, :, :], in_=src), reads=[("TOK", b)], writes=["ab"])
            a3, b3 = ab[:, :, 0:8], ab[:, :, 8:16]
            bc8 = lambda t: t[:, :].rearrange("p (o h) -> p o h", o=1).to_broadcast([128, NCK, 8])
            P.op("dve", lambda e: e.tensor_tensor(out=t8[:], in0=a3, in1=bc8(dtb), op=ALU.add), reads=["ab", "dtb"], writes=["t8"])
            P.op("act", lambda e: e.activation(out=t8b[:], in_=t8[:], func=AF.Abs), reads=["t8"], writes=["t8b"])
            P.op("act", lambda e: e.activation(out=t8b[:], in_=t8b[:], func=AF.Exp, scale=-1.0), reads=["t8b"], writes=["t8b"])
            P.op("dve", lambda e: e.tensor_scalar(out=t8b[:], in0=t8b[:], scalar1=1.0, scalar2=None, op0=ALU.add), reads=["t8b"], writes=["t8b"])
            P.op("act", lambda e: e.activation(out=t8b[:], in_=t8b[:], func=AF.Ln), reads=["t8b"], writes=["t8b"])
            P.op("dve", lambda e: e.tensor_scalar(out=t8[:], in0=t8[:], scalar1=0.0, scalar2=None, op0=ALU.max), reads=["t8"], writes=["t8"])
            P.op("dve", lambda e: e.tensor_tensor(out=t8[:], in0=t8[:], in1=t8b[:], op=ALU.add), reads=["t8", "t8b"], writes=["t8"])
            P.op("dve", lambda e: e.tensor_tensor(out=gt[:], in0=t8[:], in1=bc8(alog), op=ALU.mult), reads=["t8", "alog"], writes=["gt"])
            P.op("act", lambda e: e.activation(out=bt[:], in_=b3, func=AF.Sigmoid), reads=["ab"], writes=["bt"])
            for d in range(2):
                P.op("pe", lambda e, d=d: e.matmul(psg[:, d * 144:(d + 1) * 144], lhsT=gm64[:, d, 0, :], rhs=gt[0:64, :, d * 4:(d + 1) * 4], start=True, stop=True), reads=["gm64", "gt"], writes=["psg"])
            P.op("dve", lambda e: e.tensor_copy(out=gcs[:].rearrange("p c (d h) -> p d c h", d=2), in_=psg[:, 0:288].rearrange("p (d c h) -> p d c h", d=2, h=4)), reads=["psg"], writes=["gcs"])
            for d in range(2):
                P.op("pe", lambda e, d=d: e.matmul(psg[:, d * 144:(d + 1) * 144], lhsT=ones64[:], rhs=gt[0:64, :, d * 4:(d + 1) * 4], start=True, stop=True), reads=["ones64", "gt"], writes=["psg"])
            P.op("dve", lambda e: e.tensor_copy(out=gts[:].rearrange("p c (d h) -> p d c h", d=2), in_=psg[:, 0:288].rearrange("p (d c h) -> p d c h", d=2, h=4)), reads=["psg"], writes=["gts"])
            for m in range(4):
                d, hp = m // 2, m % 2
                for h2 in range(2):
                    col = d * 4 + hp * 2 + h2
                    rr = slice(h2 * 64, (h2 + 1) * 64)
                    P.op("dve", lambda e, m=m, col=col, rr=rr: e.tensor_copy(out=gcBD[rr, :, m:m + 1], in_=gcs[rr, :, col:col + 1]), reads=["gcs"], writes=["gcBD"])
                    P.op("dve", lambda e, m=m, col=col, rr=rr: e.tensor_copy(out=gtBD[rr, :, m:m + 1], in_=gts[rr, :, col:col + 1]), reads=["gts"], writes=["gtBD"])
                    P.op("dve", lambda e, m=m, col=col, rr=rr: e.tensor_copy(out=bBD[rr, :, m:m + 1], in_=bt[rr, :, col:col + 1]), reads=["bt"], writes=["bBD"])
            P.op("act", lambda e: e.activation(out=eg[:], in_=gcBD[:], func=AF.Exp), reads=["gcBD"], writes=["eg"])
            P.op("act", lambda e: e.activation(out=glast[:], in_=gtBD[:], func=AF.Exp), reads=["gtBD"], writes=["glast"])
            P.op("dve", lambda e: e.tensor_tensor(out=ekd[:], in0=gtBD[:], in1=gcBD[:], op=ALU.subtract), reads=["gtBD", "gcBD"], writes=["ekd"])
            P.op("act", lambda e: e.activation(out=ekd[:], in_=ekd[:], func=AF.Exp), reads=["ekd"], writes=["ekd"])
            P.op("dve", lambda e: e.tensor_scalar(out=nbeta[:], in0=bBD[:], scalar1=-1.0, scalar2=None, op0=ALU.mult), reads=["bBD"], writes=["nbeta"])
            P.op("dve", lambda e: e.tensor_tensor(out=wsc[:], in0=bBD[:], in1=eg[:], op=ALU.mult), reads=["bBD", "eg"], writes=["wsc"])

        def phase_a_steps(m, c):
            d, hp = m // 2, m % 2
            qn, kn, vn = qkv[hp], qkv[2 + hp], qkv[4 + hp]
            qtk, ktk, vtk = ("qkv", hp), ("qkv", 2 + hp), ("qkv", 4 + hp)
            cs = slice(c * 64, (c + 1) * 64)
            hd0 = d * 4 + hp * 2
            X = {}

            def s1():
                G2, G2t = wk2.get()
                Gmt = Git = G2t
                Gi, Gm = G2[:, 0:128], G2[:, 128:256]
                g4 = gt[0:64, c, hd0:hd0 + 2].rearrange("p (a h o) -> p a h o", a=1, o=1).to_broadcast([64, 2, 2, 64])
                P.op("dve", lambda e: e.tensor_tensor(out=G2[0:64, :].rearrange("p (a h j) -> p a h j", a=2, h=2), in0=g4, in1=gm64[:, d, :, :].rearrange("p a (h j) -> p a h j", h=2), op=ALU.mult), reads=["gt", "gm64"], writes=[G2t])
                pDD, pDt = ps2.get()
                pDTt = pDt
                pD, pDT = pDD[:, 0:128], pDD[:, 128:256]
                P.op("pe", lambda e: e.matmul(pD, lhsT=gm64[:, d, 0, :], rhs=Gm[0:64, :], start=True, stop=True), reads=["gm64", Gmt], writes=[pDt], inc=False)
                P.op("pe", lambda e: e.matmul(pDT, lhsT=gm64[:, d, 1, :], rhs=Gi[0:64, :], start=True, stop=True), reads=["gm64", Git], writes=[pDTt])
                pKK, pKKt = ps.get()
                pQK, pQKt = ps.get()
                pTok, pTokt = ps.get()
                for h2 in range(2):
                    pb = h2 * 64
                    P.op("pe", lambda e, pb=pb: e.matmul(pKK[pb:pb + 64, pb:pb + 64], lhsT=kn[pb:pb + 64, cs], rhs=kn[pb:pb + 64, cs], start=True, stop=True), reads=[ktk], writes=[pKKt], inc=False)
                    P.op("pe", lambda e, pb=pb: e.matmul(pQK[pb:pb + 64, pb:pb + 64], lhsT=kn[pb:pb + 64, cs], rhs=qn[pb:pb + 64, cs], start=True, stop=True), reads=[ktk, qtk], writes=[pQKt], inc=False)
                    P.op("pe", lambda e, pb=pb: e.matmul(pTok[pb:pb + 64, 0:64], lhsT=kn[pb:pb + 64, cs], rhs=idn2[pb:pb + 64, :], start=True, stop=True), reads=[ktk, "idn2"], writes=[pTokt], inc=False)
                    P.op("pe", lambda e, pb=pb: e.matmul(pTok[pb:pb + 64, 64:128], lhsT=vn[pb:pb + 64, cs], rhs=idn2[pb:pb + 64, :], start=True, stop=True), reads=[vtk, "idn2"], writes=[pTokt], inc=(h2 == 1))
                X.update(pDD=pDD, pD=pD, pDt=pDt, pDT=pDT, pDTt=pDTt, pKK=pKK, pKKt=pKKt, pQK=pQK, pQKt=pQKt, pTok=pTok, pTokt=pTokt)

            def s2a():
                x = dict(X)
                DD, Dt = wk2.get()
                P.op("act", lambda e: e.activation(out=DD, in_=x["pDD"], func=AF.Exp), reads=[x["pDt"]], writes=[Dt])
                X.update(DD=DD, DDt=Dt)

            def s2():
                x = dict(X)
                DD, Dt = x["DD"], x["DDt"]
                DTt = Dt
                D, DT = DD[:, 0:128], DD[:, 128:256]
                P.op("dve", lambda e: e.tensor_tensor(out=DD.rearrange("p (a j) -> p a j", a=2), in0=DD.rearrange("p (a j) -> p a j", a=2), in1=gmbd[:, d, :, :], op=ALU.mult), reads=[Dt, "gmbd"], writes=[Dt])
                N, Nt = wk.get()
                X["n"] = X.get("n", 0) + 1
                if X["n"] <= k.opts.get("s2n", 99):
                    P.op("dve", lambda e: e.scalar_tensor_tensor(out=N, in0=x["pKK"], scalar=nbeta[:, c, m:m + 1], in1=D, op0=ALU.mult, op1=ALU.mult), reads=[x["pKKt"], "nbeta", Dt], writes=[Nt])
                X["n"] = X.get("n", 0) + 1
                if X["n"] <= k.opts.get("s2n", 99):
                    P.op("dve", lambda e: e.tensor_tensor(out=attnT_all[:, c, :], in0=x["pQK"], in1=DT, op=ALU.mult), reads=[x["pQKt"], DTt], writes=[("attnT", c)])
                rhs, rhst = wr.get()
                X["n"] = X.get("n", 0) + 1
                if X["n"] <= k.opts.get("s2n", 99):
                    P.op("dve", lambda e: e.tensor_scalar(out=rhs[:, 0:64], in0=x["pTok"][:, 0:64], scalar1=wsc[:, c, m:m + 1], scalar2=None, op0=ALU.mult), reads=[x["pTokt"], "wsc"], writes=[rhst])
                X["n"] = X.get("n", 0) + 1
                if X["n"] <= k.opts.get("s2n", 99):
                    P.op("dve", lambda e: e.tensor_scalar(out=rhs[:, 64:128], in0=x["pTok"][:, 64:128], scalar1=bBD[:, c, m:m + 1], scalar2=None, op0=ALU.mult), reads=[x["pTokt"], "bBD"], writes=[rhst])
                X["n"] = X.get("n", 0) + 1
                if X["n"] <= k.opts.get("s2n", 99):
                    P.op("dve", lambda e: e.tensor_scalar(out=kdec_all[:, c, :], in0=x["pTok"][:, 0:64], scalar1=ekd[:, c, m:m + 1], scalar2=None, op0=ALU.mult), reads=[x["pTokt"], "ekd"], writes=[("kdec", c)])
                X.update(N=N, Nt=Nt, rhs=rhs, rhst=rhst)

            def s3():
                x = dict(X)
                pNT, pNTt = ps.get()
                P.op("pe", lambda e: e.transpose(pNT, x["N"], k.idn[:]), reads=[x["Nt"], "idn"], writes=[pNTt])
                X.update(pNT=pNT, pNTt=pNTt)

            def s4():
                x = dict(X)
                PT_, PTt = wk.get()
                XT, XTt = wk.get()
                P.op("dve", lambda e: e.tensor_copy(out=PT_, in_=x["pNT"]), reads=[x["pNTt"]], writes=[PTt])
                P.op("dve", lambda e: e.tensor_tensor(out=XT, in0=x["pNT"], in1=k.idn[:], op=ALU.add), reads=[x["pNTt"], "idn"], writes=[XTt])
                X.update(P=x["N"], Pt=x["Nt"], PT=PT_, PTt=PTt, XT=XT, XTt=XTt)

            def lvl_mm(kk):
                def f():
                    x = dict(X)
                    pPP, pPt = ps2.get()
                    pP, pPT = pPP[:, 0:128], pPP[:, 128:256]
                    P.op("pe", lambda e: e.matmul(pP, lhsT=x["PT"], rhs=x["P"], start=True, stop=True), reads=[x["PTt"], x["Pt"]], writes=[pPt], inc=(kk == 5))
                    if kk < 5:
                        P.op("pe", lambda e: e.matmul(pPT, lhsT=x["P"], rhs=x["PT"], start=True, stop=True), reads=[x["PTt"], x["Pt"]], writes=[pPt])
                    X.update(pPP=pPP, pPt=pPt)
                return f

            def lvl_ev(kk):
                def f():
                    x = dict(X)
                    nPP, nPt = wk2.get()
                    nP, nPT = nPP[:, 0:128], nPP[:, 128:256]
                    w_ = 256 if kk < 5 else 128
                    P.op("dve", lambda e: e.tensor_copy(out=nPP[:, 0:w_], in_=x["pPP"][:, 0:w_]), reads=[x["pPt"]], writes=[nPt])
                    X.update(P=nP, Pt=nPt, PT=nPT, PTt=nPt)
                return f

            def lvl_px(kk):
                def f():
                    x = dict(X)
                    pX, pXt = ps.get()
                    P.op("pe", lambda e: e.matmul(pX, lhsT=x["P"], rhs=x["XT"], start=True, stop=True), reads=[x["Pt"], x["XTt"]], writes=[pXt])
                    X.update(pX=pX, pXt=pXt)
                return f

            def lvl_acc(kk):
                def f():
                    x = dict(X)
                    nX, nXt = wk.get()
                    P.op("dve", lambda e: e.tensor_tensor(out=nX, in0=x["XT"], in1=x["pX"], op=ALU.add), reads=[x["XTt"], x["pXt"]], writes=[nXt])
                    X.update(XT=nX, XTt=nXt)
                return f

            def s_sol():
                x = dict(X)
                pU, pUt = ps.get()
                pW, pWt = ps.get()
                P.op("pe", lambda e: e.matmul(pU[:, 0:64], lhsT=x["XT"], rhs=x["rhs"][:, 64:128], start=True, stop=True), reads=[x["XTt"], x["rhst"]], writes=[pUt], inc=False)
                for h2 in range(2):
                    pb = h2 * 64
                    P.op("pe", lambda e, pb=pb: e.matmul(pW[pb:pb + 64, 0:64], lhsT=x["rhs"][pb:pb + 64, 0:64], rhs=x["XT"][pb:pb + 64, pb:pb + 64], start=True, stop=True), reads=[x["XTt"], x["rhst"]], writes=[pWt], inc=(h2 == 1))
                X.update(pU=pU, pUt=pUt, pW=pW, pWt=pWt)

            def s_solev():
                x = dict(X)
                P.op("dve", lambda e: e.tensor_copy(out=u_all[:, c, :], in_=x["pU"][:, 0:64]), reads=[x["pUt"]], writes=[("u", c)])
                P.op("dve", lambda e: e.tensor_copy(out=wT_all[:, c, :], in_=x["pW"][:, 0:64]), reads=[x["pWt"]], writes=[("wT", c)])

            steps = [s1, s2a, s2, s3, s4]
            for kk in range(1, 6):
                steps += [lvl_mm(kk), lvl_ev(kk), lvl_px(kk), lvl_acc(kk)]
            steps += [s_sol, s_solev]
            return steps[:k.opts.get("dn_steps", 99)]

        def phase_b_chunk(m, c, first_dir):
            d, hp = m // 2, m % 2
            qn = qkv[hp]
            cs = slice(c * 64, (c + 1) * 64)
            p1, p1t = ps.get()
            p2, p2t = ps.get()
            for h2 in range(2):
                pb = h2 * 64
                P.op("pe", lambda e, pb=pb: e.matmul(p1[pb:pb + 64, 0:64], lhsT=wT_all[pb:pb + 64, c, :], rhs=S[pb:pb + 64, :], start=True, stop=True), reads=[("wT", c), "S"], writes=[p1t], inc=False)
                P.op("pe", lambda e, pb=pb: e.matmul(p2[pb:pb + 64, 0:64], lhsT=qn[pb:pb + 64, cs], rhs=S[pb:pb + 64, :], start=True, stop=True), reads=[("qkv", hp), "S"], writes=[p2t], inc=(h2 == 1))
            vn_, vnt = wv.get()
            P.op("dve", lambda e: e.tensor_tensor(out=vn_, in0=u_all[:, c, :], in1=p1[:, 0:64], op=ALU.subtract), reads=[("u", c), p1t], writes=[vnt])
            p3, p3t = ps.get()
            p4, p4t = ps.get()
            P.op("pe", lambda e: e.matmul(p3[:, 0:64], lhsT=attnT_all[:, c, :], rhs=vn_, start=True, stop=True), reads=[("attnT", c), vnt], writes=[p3t], inc=False)
            for h2 in range(2):
                pb = h2 * 64
                P.op("pe", lambda e, pb=pb: e.matmul(p4[pb:pb + 64, 0:64], lhsT=kdec_all[pb:pb + 64, c, :], rhs=vn_[pb:pb + 64, :], start=True, stop=True), reads=[("kdec", c), vnt], writes=[p4t], inc=(h2 == 1))
            t_, tt_ = wv.get()
            P.op("dve", lambda e: e.tensor_scalar(out=t_, in0=p2[:, 0:64], scalar1=eg[:, c, m:m + 1], scalar2=None, op0=ALU.mult), reads=[p2t, "eg"], writes=[tt_])
            if first_dir:
                P.op("dve", lambda e: e.tensor_tensor(out=O_tok[:, c, :], in0=t_, in1=p3[:, 0:64], op=ALU.add), reads=[tt_, p3t], writes=[("O", c)])
            else:
                P.op("dve", lambda e: e.tensor_tensor(out=t_, in0=t_, in1=p3[:, 0:64], op=ALU.add), reads=[tt_, p3t], writes=[tt_])
                P.op("dve", lambda e: e.tensor_tensor(out=O_tok[:, c, :], in0=O_tok[:, c, :], in1=t_, op=ALU.add), reads=[tt_, ("O", c)], writes=[("O", c)])
            P.op("dve", lambda e: e.scalar_tensor_tensor(out=S[:], in0=S[:], scalar=glast[:, c, m:m + 1], in1=p4[:, 0:64], op0=ALU.mult, op1=ALU.add), reads=["S", "glast", p4t], writes=["S"])

        def out_phase(b, hp):
            z, yfm = xin, acc
            r0 = FM_Z + hp * 128
            P.dma(lambda e: e.dma_start(out=z[:], in_=k.PT[b, r0:r0 + 128, :]), reads=[("PT", b)], writes=["xin"])
            P.op("act", lambda e: e.activation(out=z[:], in_=z[:], func=AF.Silu), reads=["xin"], writes=["xin"])
            allO = [("O", c) for c in range(NCK)]
            P.op("dve", lambda e: e.tensor_tensor(out=u_all[:], in0=O_tok[:], in1=O_tok[:], op=ALU.mult), reads=allO, writes=[("u", c) for c in range(NCK)])
            P.op("dve", lambda e: e.tensor_reduce(out=ssq[:], in_=u_all[:], axis=AX.X, op=ALU.add), reads=[("u", c) for c in range(NCK)], writes=["ssq"])
            P.op("act", lambda e: e.activation(out=ssq[:], in_=ssq[:], func=AF.Sqrt, bias=epsb[:, 0:1], scale=1.0 / 64), reads=["ssq", "epsb"], writes=["ssq"])
            P.op("dve", lambda e: e.reciprocal(out=ssq[:], in_=ssq[:]), reads=["ssq"], writes=["ssq"])
            P.op("dve", lambda e: e.tensor_tensor(out=O_tok[:], in0=O_tok[:], in1=ssq[:].rearrange("p (c o) -> p c o", o=1).to_broadcast([128, NCK, 64]), op=ALU.mult), reads=allO + ["ssq"], writes=allO)
            P.op("dve", lambda e: e.tensor_tensor(out=O_tok[:], in0=O_tok[:], in1=gno[:].rearrange("p (o v) -> p o v", o=1).to_broadcast([128, NCK, 64]), op=ALU.mult), reads=allO + ["gno"], writes=allO)
            for c in range(NCK):
                out_chunk(c)
            for (t0, n) in [(0, CTX)] + [(CTX + i * 512, 512) for i in range(4)]:
                out_tile(b, hp, t0, n)

        def out_chunk(c):
            pp, ppt = ps.get()
            for h2 in range(2):
                pb = h2 * 64
                P.op("pe", lambda e, pb=pb: e.matmul(pp[pb:pb + 64, 0:64], lhsT=O_tok[pb:pb + 64, c, :], rhs=idn2[pb:pb + 64, :], start=True, stop=True), reads=[("O", c), "idn2"], writes=[ppt], inc=(h2 == 1))
            P.op("dve", lambda e: e.tensor_copy(out=acc[:, c * 64:(c + 1) * 64], in_=pp[:, 0:64]), reads=[ppt], writes=["acc"])

        def out_tile(b, hp, t0, n):
            yi = cnt["y"] % 2
            cnt["y"] += 1
            P.op("dve", lambda e: e.tensor_tensor(out=yb[yi][:, :n], in0=acc[:, t0:t0 + n], in1=xin[:, t0:t0 + n], op=ALU.mult), reads=["acc", "xin"], writes=[("yb", yi)])
            P.dma(lambda e: e.dma_start(out=k.YT[b, 0, hp * 128:(hp + 1) * 128, t0:t0 + n], in_=yb[yi][:, :n]), reads=[("yb", yi)], writes=[("YT", b)])

        G = 4
        for b in range(NB):
            for ti in range(6):
                prep_tile(b, ti)
            prep_gates(b)
            lim = k.opts.get("dn_lim", 99)
            if lim == 0:
                continue
            for hp in range(2):
                for d in range(2):
                    m = d * 2 + hp
                    for c0 in range(0, NCK if lim >= 2 else G, G):
                        lists = [phase_a_steps(m, c) for c in range(c0, min(NCK, c0 + G))]
                        for si in range(len(lists[0])):
                            for lst in lists:
                                lst[si]()
                    if lim < 3:
                        continue
                    P.op("dve", lambda e: e.memset(S[:], 0.0), writes=["S"])
                    order = list(range(NCK)) if d == 0 else [3, 2, 1, 0] + list(range(NCK - 1, 3, -1))
                    for c in order:
                        phase_b_chunk(m, c, d == 0)
                    if "dump" in k.opts and (b, m) == k.opts["dump"][:2]:
                        for j in range(6):
                            P.dma(lambda e, j=j: e.dma_start(out=k.dump[j], in_=qkv[j][:]), reads=[("qkv", j)], writes=[("dump", j)])
                        for j, (tl, tk) in enumerate([(u_all, "u"), (wT_all, "wT"), (kdec_all, "kdec"), (O_tok, "O")]):
                            P.dma(lambda e, j=j, tl=tl: e.dma_start(out=k.dump[6 + j], in_=tl[:].rearrange("p c v -> p (c v)")), reads=[(tk, c) for c in range(NCK)], writes=[("dump", 6 + j)])
                        P.dma(lambda e: e.dma_start(out=k.dump[10:12].rearrange("a p t -> p a t"), in_=attnT_all[:].rearrange("p (a c) v -> p a (c v)", a=2)), reads=[("attnT", c) for c in range(NCK)], writes=[("dump", 10)])
                        for j, (tl, tk, w) in enumerate([(gt, "gt", 288), (bt, "bt", 288), (gcBD, "gcBD", 144), (gtBD, "gtBD", 144), (bBD, "bBD", 144)]):
                            P.dma(lambda e, j=j, tl=tl, w=w: e.dma_start(out=k.dump[12, :, j * 300:j * 300 + w], in_=tl[:].rearrange("p c v -> p (c v)")), reads=[tk], writes=[("dump", 12, j)])
                if lim >= 4:
                    out_phase(b, hp)
        P.end_stage()


def stage_s5(k, l):
    nc, P, NB = k.nc, k.P, k.NB
    HALF_PI = float(np.pi / 2)
    P.serial = k.opts.get("serial_s5", SERIAL_S5)
    with ExitStack() as es:
        al = lambda name, shape, dt=F32: es.enter_context(nc.sbuf_tensor("s_" + name, list(shape), dt))
        lre = al("lre", [128, 16]); lim = al("lim", [128, 16]); stp = al("stp", [128, 16]); mag = al("mag", [128, 16])
        cth = al("cth", [128, 16]); sth = al("sth", [128, 16]); t1 = al("t1", [128, 16]); t2 = al("t2", [128, 16]); t3 = al("t3", [128, 16])
        are = al("are", [128, 16]); aim = al("aim", [128, 16]); cfr = al("cfr", [128, 16]); cfi = al("cfi", [128, 16]); hpi = al("hpi", [128, 1])
        pwc = al("pwc", [128, 16, 12]); pws = al("pws", [128, 16, 12])
        bre = al("bre", [128, 8, 16]); bim = al("bim", [128, 8, 16]); cre = al("cre", [128, 8, 16]); cim = al("cim", [128, 8, 16])
        bb_all = al("bb_all", [128, 32, 16]); bd_all = al("bd_all", [128, 32, 32]); tb = [al("tb%d" % i, [128, 8, 16]) for i in range(4)]
        W_all = al("W_all", [32, 32, 128], BF16); cw_all = al("cw_all", [128, 8, 2, 128], BF16)
        dsk = al("dsk", [128, 2]); wgl = al("wgl", [128, 2, 512], BF16)
        cs = [al("cs%d" % i, [128, T]) for i in range(2)]; sn = [al("sn%d" % i, [128, T]) for i in range(2)]
        xr = al("xr", [128, T]); xi = al("xi", [128, T]); gr = al("gr", [128, T]); gi = al("gi", [128, T])
        u32 = al("u32", [32, T]); u32b = al("u32b", [32, T], BF16); hrb = al("hrb", [128, T], BF16); hib = al("hib", [128, T], BF16); Y = [al("Y%d" % i, [128, T]) for i in range(2)]
        mt = Slots([al("mt%d" % i, [128, 512])[:] for i in range(6)], "mt")
        gel = al("gel", [128, 2, 512], BF16); sg = [al("sg%d" % i, [128, 512]) for i in range(2)]; yb = [al("yb%d" % i, [128, 512], BF16) for i in range(2)]
        banks = [es.enter_context(nc.psum_tensor("s_ps%d" % i, [128, 512], F32)) for i in range(8)]
        psl = Slots([banks[i][:] for i in range(8)], "psb")
        ld = lambda dst, src, tok: P.dma(lambda e: e.dma_start(out=dst, in_=src), writes=[tok])
        ld(lre[:], k.s5_lam_re[l], "lre"); ld(lim[:], k.s5_lam_im[l], "lim"); ld(stp[:], k.s5_log_step[l], "stp")
        ld(bre[:], k.s5_b_re[l], "bre"); ld(bim[:], k.s5_b_im[l], "bim"); ld(cre[:], k.s5_c_re[l], "cre"); ld(cim[:], k.s5_c_im[l], "cim")
        ld(dsk[:], k.s5_d[l], "dsk")
        P.dma(lambda e: e.dma_start(out=wgl[:], in_=k.s5_glu[l].rearrange("(c p) n -> p c n", p=128)), writes=["wgl"], q="pool")
        P.op("dve", lambda e: e.memset(hpi[:], HALF_PI), writes=["hpi"])
        P.op("dve", lambda e: e.memset(bd_all[:], 0.0), writes=["bd_all"])
        P.op("dve", lambda e: e.memset(cw_all[:], 0.0), writes=["cw_all"])
        tt = lambda out, a, b_, op, rd, wr, eng="dve": P.op(eng, lambda e: e.tensor_tensor(out=out, in0=a, in1=b_, op=op), reads=rd, writes=wr)
        P.op("act", lambda e: e.activation(out=stp[:], in_=stp[:], func=AF.Exp), reads=["stp"], writes=["stp"])
        tt(t1[:], lre[:], stp[:], ALU.mult, ["lre", "stp"], ["t1"])
        P.op("act", lambda e: e.activation(out=mag[:], in_=t1[:], func=AF.Exp), reads=["t1"], writes=["mag"])
        tt(t2[:], lim[:], stp[:], ALU.mult, ["lim", "stp"], ["t2"])
        P.op("act", lambda e: e.activation(out=sth[:], in_=t2[:], func=AF.Sin, scale=1.0 / 32), reads=["t2"], writes=["sth"])
        P.op("act", lambda e: e.activation(out=cth[:], in_=t2[:], func=AF.Sin, scale=1.0 / 32, bias=hpi[:, 0:1]), reads=["t2", "hpi"], writes=["cth"])
        for it in range(5):
            tt(t1[:], cth[:], cth[:], ALU.mult, ["cth"], ["t1"])
            tt(t3[:], sth[:], sth[:], ALU.mult, ["sth"], ["t3"])
            P.op("dve", lambda e: e.scalar_tensor_tensor(out=sth[:], in0=cth[:], scalar=2.0, in1=sth[:], op0=ALU.mult, op1=ALU.mult), reads=["cth", "sth"], writes=["sth"])
            tt(cth[:], t1[:], t3[:], ALU.subtract, ["t1", "t3"], ["cth"])
        tt(are[:], mag[:], cth[:], ALU.mult, ["mag", "cth"], ["are"])
        tt(aim[:], mag[:], sth[:], ALU.mult, ["mag", "sth"], ["aim"])
        tt(t1[:], lre[:], lre[:], ALU.mult, ["lre"], ["t1"])
        tt(t3[:], lim[:], lim[:], ALU.mult, ["lim"], ["t3"])
        tt(t1[:], t1[:], t3[:], ALU.add, ["t1", "t3"], ["t1"])
        P.op("dve", lambda e: e.reciprocal(out=t1[:], in_=t1[:]), reads=["t1"], writes=["t1"])
        P.op("dve", lambda e: e.tensor_scalar(out=t2[:], in0=are[:], scalar1=-1.0, scalar2=None, op0=ALU.add), reads=["are"], writes=["t2"])
        tt(cfr[:], t2[:], lre[:], ALU.mult, ["t2", "lre"], ["cfr"])
        tt(t3[:], aim[:], lim[:], ALU.mult, ["aim", "lim"], ["t3"])
        tt(cfr[:], cfr[:], t3[:], ALU.add, ["cfr", "t3"], ["cfr"])
        tt(cfr[:], cfr[:], t1[:], ALU.mult, ["cfr", "t1"], ["cfr"])
        tt(cfi[:], aim[:], lre[:], ALU.mult, ["aim", "lre"], ["cfi"])
        tt(t3[:], t2[:], lim[:], ALU.mult, ["t2", "lim"], ["t3"])
        tt(cfi[:], cfi[:], t3[:], ALU.subtract, ["cfi", "t3"], ["cfi"])
        tt(cfi[:], cfi[:], t1[:], ALU.mult, ["cfi", "t1"], ["cfi"])
        bb4 = bb_all[:].rearrange("p (d c r) h -> p d c r h", d=2, r=2)
        for d in range(2):
            bc = lambda t_: t_[:, d * 8:(d + 1) * 8].rearrange("p (c o) -> p c o", o=1).to_broadcast([128, 8, 16])
            tt(tb[0][:], bre[:], bc(cfr), ALU.mult, ["bre", "cfr"], [("tb", 0)])
            tt(tb[1][:], bim[:], bc(cfi), ALU.mult, ["bim", "cfi"], [("tb", 1)])
            tt(bb4[:, d, :, 0, :], tb[0][:], tb[1][:], ALU.subtract, [("tb", 0), ("tb", 1)], ["bb_all"])
            tt(tb[2][:], bim[:], bc(cfr), ALU.mult, ["bim", "cfr"], [("tb", 2)])
            tt(tb[3][:], bre[:], bc(cfi), ALU.mult, ["bre", "cfi"], [("tb", 3)])
            tt(bb4[:, d, :, 1, :], tb[2][:], tb[3][:], ALU.add, [("tb", 2), ("tb", 3)], ["bb_all"])
        for g2 in range(2):
            rr_ = slice(g2 * 64, (g2 + 1) * 64)
            P.op("dve", lambda e, rr_=rr_, g2=g2: e.tensor_copy(out=bd_all[rr_, :, g2 * 16:(g2 + 1) * 16], in_=bb_all[rr_, :, :]), reads=["bb_all", "bd_all"], writes=["bd_all"])

        def w_build(j):
            pw, pwt = psl.get()
            P.op("pe", lambda e: e.transpose(pw[0:32, 0:128], bd_all[:, j, :], k.idn[:]), reads=["bd_all", "idn"], writes=[pwt])
            P.op("act", lambda e: e.activation(out=W_all[:, j, :], in_=pw[0:32, 0:128], func=AF.Identity), reads=[pwt], writes=["W_all"])
        for j in range(32):
            w_build(j)

        def cw_build(ct, g2):
            q = ct % 4
            rr_ = slice(g2 * 64, (g2 + 1) * 64)
            c0 = q * 32 + g2 * 16
            P.op("dve", lambda e: e.tensor_copy(out=cw_all[rr_, ct, 0, c0:c0 + 16], in_=cre[rr_, ct, :]), reads=["cre", "cw_all"], writes=["cw_all"])
            P.op("dve", lambda e: e.tensor_scalar(out=cw_all[rr_, ct, 1, c0:c0 + 16], in0=cim[rr_, ct, :], scalar1=-1.0, scalar2=None, op0=ALU.mult), reads=["cim", "cw_all"], writes=["cw_all"])
        for ct in range(8):
            for g2 in range(2):
                cw_build(ct, g2)
        P.op("dve", lambda e: e.tensor_copy(out=pwc[:, :, 0], in_=cth[:]), reads=["cth"], writes=["pwc"])
        P.op("dve", lambda e: e.tensor_copy(out=pws[:, :, 0], in_=sth[:]), reads=["sth"], writes=["pws"])

        def pw_level(kk):
            c_, s_ = pwc[:, :, kk - 1], pws[:, :, kk - 1]
            tt(t1[:], c_, c_, ALU.mult, ["pwc"], ["t1"])
            tt(t3[:], s_, s_, ALU.mult, ["pws"], ["t3"])
            tt(pwc[:, :, kk], t1[:], t3[:], ALU.subtract, ["t1", "t3", "pwc"], ["pwc"])
            P.op("dve", lambda e: e.scalar_tensor_tensor(out=pws[:, :, kk], in0=c_, scalar=2.0, in1=s_, op0=ALU.mult, op1=ALU.mult), reads=["pwc", "pws"], writes=["pws"])
        for kk in range(1, 12):
            pw_level(kk)

        def table_gen(j):
            i = j % 2
            C, S_ = cs[i], sn[i]
            ctk, stk = ("cs", i), ("sn", i)
            P.op("dve", lambda e: e.memset(C[:, 0:1], 1.0), writes=[ctk])
            P.op("dve", lambda e: e.memset(S_[:, 0:1], 0.0), writes=[stk])
            for kk in range(12):
                ln = 1 << kk
                nn = min(ln, T - ln)
                if nn <= 0:
                    break
                pc, ps_ = pwc[:, j, kk:kk + 1], pws[:, j, kk:kk + 1]
                lvl(C, S_, ctk, stk, ln, nn, pc, ps_)
            P.dma(lambda e: e.dma_start(out=k.S5TAB[j, 0], in_=C[:]), reads=[ctk], writes=[("TAB", j)])
            P.dma(lambda e: e.dma_start(out=k.S5TAB[j, 1], in_=S_[:]), reads=[stk], writes=[("TAB", j)])

        def lvl(C, S_, ctk, stk, ln, nn, pc, ps_):
            m1, m1t = mt.get()
            m2, m2t = mt.get()
            w_ = min(nn, 512)
            for o in range(0, nn, 512):
                w = min(512, nn - o)
                sub(C, S_, ctk, stk, ln, o, w, pc, ps_)

        def sub(C, S_, ctk, stk, ln, o, w, pc, ps_):
            m1, m1t = mt.get()
            m2, m2t = mt.get()
            P.op("dve", lambda e: e.tensor_scalar(out=m1[:, :w], in0=S_[:, o:o + w], scalar1=ps_, scalar2=None, op0=ALU.mult), reads=[stk, "pws"], writes=[m1t])
            P.op("dve", lambda e: e.tensor_scalar(out=m2[:, :w], in0=S_[:, o:o + w], scalar1=pc, scalar2=None, op0=ALU.mult), reads=[stk, "pwc"], writes=[m2t])
            P.op("dve", lambda e: e.scalar_tensor_tensor(out=C[:, ln + o:ln + o + w], in0=C[:, o:o + w], scalar=pc, in1=m1[:, :w], op0=ALU.mult, op1=ALU.subtract), reads=[ctk, m1t, "pwc"], writes=[ctk])
            P.op("dve", lambda e: e.scalar_tensor_tensor(out=S_[:, ln + o:ln + o + w], in0=C[:, o:o + w], scalar=ps_, in1=m2[:, :w], op0=ALU.mult, op1=ALU.add), reads=[ctk, m2t, "pws"], writes=[stk])
        for j in range(16):
            table_gen(j)

        blocks = [(0, CTX)] + [(CTX + i * 512, 512) for i in range(4)]

        def tabview(tab, d, t0, n):
            if d == 0:
                return tab[:, t0:t0 + n]
            if t0 < CTX:
                lo = CTX - 1 - (t0 + n - 1)
            else:
                lo = CTX + (T - 1 - (t0 + n - 1))
            return tab[:, lo:lo + n][:, ::-1]

        def do_block_in(ct, d, i, t0, n):
            j = d * 8 + ct
            pr, prt = psl.get()
            pi_, pit = psl.get()
            P.op("pe", lambda e: e.matmul(pr[:, :n], lhsT=W_all[:, j * 2, :], rhs=u32b[:, t0:t0 + n], start=True, stop=True), reads=["W_all", "u32b"], writes=[prt], inc=False)
            P.op("pe", lambda e: e.matmul(pi_[:, :n], lhsT=W_all[:, j * 2 + 1, :], rhs=u32b[:, t0:t0 + n], start=True, stop=True), reads=["W_all", "u32b"], writes=[pit])
            cv, sv = tabview(cs[i], d, t0, n), tabview(sn[i], d, t0, n)
            ms = [mt.get() for _ in range(4)]
            tt(ms[0][0][:, :n], pr[:, :n], cv, ALU.mult, [prt, ("cs", i)], [ms[0][1]])
            tt(ms[1][0][:, :n], pi_[:, :n], sv, ALU.mult, [pit, ("sn", i)], [ms[1][1]])
            tt(xr[:, t0:t0 + n], ms[0][0][:, :n], ms[1][0][:, :n], ALU.add, [ms[0][1], ms[1][1]], ["xr"])
            tt(ms[2][0][:, :n], pi_[:, :n], cv, ALU.mult, [pit, ("cs", i)], [ms[2][1]])
            tt(ms[3][0][:, :n], pr[:, :n], sv, ALU.mult, [prt, ("sn", i)], [ms[3][1]])
            tt(xi[:, t0:t0 + n], ms[2][0][:, :n], ms[3][0][:, :n], ALU.subtract, [ms[2][1], ms[3][1]], ["xi"])

        def do_scan(ct, d):
            j = d * 8 + ct
            for (src, dst, stok, dtok) in ((xr, gr, "xr", "gr"), (xi, gi, "xi", "gi")):
                scan1(j, d, src, dst, stok, dtok)

        def scan1(j, d, src, dst, stok, dtok):
            rb = lambda n: mag[:, j:j + 1].to_broadcast([128, n])
            if d == 0:
                P.op("dve", lambda e: e.tensor_tensor_scan(out=dst[:], data0=rb(T), data1=src[:], initial=0.0, op0=ALU.mult, op1=ALU.add), reads=[stok, "mag"], writes=[dtok])
            else:
                P.op("dve", lambda e: e.tensor_tensor_scan(out=dst[:, 0:CTX][:, ::-1], data0=rb(CTX), data1=src[:, 0:CTX][:, ::-1], initial=0.0, op0=ALU.mult, op1=ALU.add), reads=[stok, "mag"], writes=[dtok])
                P.op("dve", lambda e: e.tensor_tensor_scan(out=dst[:, CTX:T][:, ::-1], data0=rb(SEQ), data1=src[:, CTX:T][:, ::-1], initial=dst[:, 0:1], op0=ALU.mult, op1=ALU.add), reads=[stok, "mag", dtok], writes=[dtok])

        def do_block_out(ct, d, i, t0, n, first):
            cv, sv = tabview(cs[i], d, t0, n), tabview(sn[i], d, t0, n)
            ms = [mt.get() for _ in range(4)]
            tt(ms[0][0][:, :n], gr[:, t0:t0 + n], cv, ALU.mult, ["gr", ("cs", i)], [ms[0][1]])
            tt(ms[1][0][:, :n], gi[:, t0:t0 + n], sv, ALU.mult, ["gi", ("sn", i)], [ms[1][1]])
            tt(hrb[:, t0:t0 + n], ms[0][0][:, :n], ms[1][0][:, :n], ALU.subtract, [ms[0][1], ms[1][1]], ["hrb"])
            tt(ms[2][0][:, :n], gr[:, t0:t0 + n], sv, ALU.mult, ["gr", ("sn", i)], [ms[2][1]])
            tt(ms[3][0][:, :n], gi[:, t0:t0 + n], cv, ALU.mult, ["gi", ("cs", i)], [ms[3][1]])
            tt(hib[:, t0:t0 + n], ms[2][0][:, :n], ms[3][0][:, :n], ALU.add, [ms[2][1], ms[3][1]], ["hib"])
            py, pyt = psl.get()
            P.op("pe", lambda e: e.matmul(py[:, :n], lhsT=cw_all[:, ct, 0, :], rhs=hrb[:, t0:t0 + n], start=True, stop=False), reads=["cw_all", "hrb"], writes=[pyt], inc=False)
            P.op("pe", lambda e: e.matmul(py[:, :n], lhsT=cw_all[:, ct, 1, :], rhs=hib[:, t0:t0 + n], start=False, stop=True), reads=["cw_all", "hib"], writes=[pyt])
            yt_ = Y[ct // 4]
            ytk = ("Y", ct // 4)
            if first:
                P.op("dve", lambda e: e.tensor_copy(out=yt_[:, t0:t0 + n], in_=py[:, :n]), reads=[pyt], writes=[ytk])
            else:
                tt(yt_[:, t0:t0 + n], yt_[:, t0:t0 + n], py[:, :n], ALU.add, [pyt, ytk], [ytk])

        def do_ct_dir(b, ct, d):
            j = d * 8 + ct
            i = j % 2
            P.dma(lambda e: e.dma_start(out=cs[i][:], in_=k.S5TAB[j, 0]), reads=[("TAB", j)], writes=[("cs", i)])
            P.dma(lambda e: e.dma_start(out=sn[i][:], in_=k.S5TAB[j, 1]), reads=[("TAB", j)], writes=[("sn", i)])
            for (t0, n) in blocks:
                do_block_in(ct, d, i, t0, n)
            do_scan(ct, d)
            for (t0, n) in blocks:
                do_block_out(ct, d, i, t0, n, (ct % 4 == 0 and d == 0))

        def do_ct(b, ct):
            r0 = FM_U + ct * 32
            P.dma(lambda e: e.dma_start(out=u32[:], in_=k.PT[b, r0:r0 + 32, :]), reads=[("PT", b)], writes=["u32"])
            P.op("dve", lambda e: e.tensor_copy(out=u32b[:], in_=u32[:]), reads=["u32"], writes=["u32b"])
            for d in range(2):
                do_ct_dir(b, ct, d)

        def out_tile(b, t0, n):
            for yt in range(2):
                ub, ubt = mt.get()
                r0 = FM_U + yt * 128
                P.dma(lambda e, ub=ub, r0=r0: e.dma_start(out=ub[:, :n], in_=k.PT[b, r0:r0 + 128, t0:t0 + n]), reads=[("PT", b)], writes=[ubt])
                yv, yvt = mt.get()
                P.op("dve", lambda e, ub=ub, yv=yv, yt=yt: e.scalar_tensor_tensor(out=yv[:, :n], in0=ub[:, :n], scalar=dsk[:, yt:yt + 1], in1=Y[yt][:, t0:t0 + n], op0=ALU.mult, op1=ALU.add), reads=[ubt, "dsk", ("Y", yt)], writes=[yvt])
                x2, x2t = mt.get()
                P.op("act", lambda e, yv=yv, x2=x2: e.activation(out=x2[:, :n], in_=yv[:, :n], func=AF.Square), reads=[yvt], writes=[x2t])
                P.op("dve", lambda e, x2=x2: e.tensor_scalar(out=x2[:, :n], in0=x2[:, :n], scalar1=0.044715, scalar2=1.0, op0=ALU.mult, op1=ALU.add), reads=[x2t], writes=[x2t])
                P.op("dve", lambda e, x2=x2, yv=yv: e.tensor_tensor(out=x2[:, :n], in0=x2[:, :n], in1=yv[:, :n], op=ALU.mult), reads=[x2t, yvt], writes=[x2t])
                P.op("act", lambda e, x2=x2: e.activation(out=x2[:, :n], in_=x2[:, :n], func=AF.Tanh, scale=0.7978845608028654), reads=[x2t], writes=[x2t])
                P.op("dve", lambda e, x2=x2, yv=yv, yt=yt: e.scalar_tensor_tensor(out=gel[:, yt, :n], in0=x2[:, :n], scalar=1.0, in1=yv[:, :n], op0=ALU.add, op1=ALU.mult), reads=[x2t, yvt], writes=[("gel", yt)])
            for oc in range(2):
                pa, pat = psl.get()
                pg, pgt = psl.get()
                for kc in range(2):
                    P.op("pe", lambda e, oc=oc, kc=kc, pa=pa: e.matmul(pa[:, :n], lhsT=wgl[:, kc, oc * 128:(oc + 1) * 128], rhs=gel[:, kc, :n], start=(kc == 0), stop=(kc == 1)), reads=["wgl", ("gel", kc)], writes=[pat])
                for kc in range(2):
                    P.op("pe", lambda e, oc=oc, kc=kc, pg=pg: e.matmul(pg[:, :n], lhsT=wgl[:, kc, 256 + oc * 128:256 + (oc + 1) * 128], rhs=gel[:, kc, :n], start=(kc == 0), stop=(kc == 1)), reads=["wgl", ("gel", kc)], writes=[pgt])
                P.op("act", lambda e, oc=oc, pg=pg: e.activation(out=sg[oc][:, :n], in_=pg[:, :n], func=AF.Sigmoid, scale=0.5), reads=[pgt], writes=[("sg", oc)])
                P.op("dve", lambda e, oc=oc, pa=pa: e.scalar_tensor_tensor(out=yb[oc][:, :n], in0=pa[:, :n], scalar=0.5, in1=sg[oc][:, :n], op0=ALU.mult, op1=ALU.mult), reads=[pat, ("sg", oc)], writes=[("yb", oc)])
                P.dma(lambda e, oc=oc: e.dma_start(out=k.YT[b, 1, oc * 128:(oc + 1) * 128, t0:t0 + n], in_=yb[oc][:, :n]), reads=[("yb", oc)], writes=[("YT", b)])

        for b in range(NB):
            for ct in range(8):
                do_ct(b, ct)
            for (t0, n) in blocks:
                out_tile(b, t0, n)
        P.end_stage()


def stage_mixers(k, l):
    sel = k.opts.get("mixers", ("dn", "s5", "hg", "att"))
    if "dn" in sel:
        stage_deltanet(k, l)
    if "s5" in sel:
        stage_s5(k, l)
    if "hg" in sel:
        stage_hgrn2(k, l)
    if "att" in sel:
        stage_attention(k, l)


def stage_merge(k, l):
    nc, P, NB, NC = k.nc, k.P, k.NB, k.NC
    with ExitStack() as es:
        wg = es.enter_context(nc.sbuf_tensor("m_wg", [128, 8, 4 * D], BF16))
        wb = es.enter_context(nc.sbuf_tensor("m_wb", [128, 8, D], BF16))
        wo = es.enter_context(nc.sbuf_tensor("m_wo", [128, 8, D], BF16))
        xt = [es.enter_context(nc.sbuf_tensor("m_x%d" % s, [128, 8, 256], F32)) for s in range(2)]
        yt = [es.enter_context(nc.sbuf_tensor("m_y%d" % s, [128, 8, 256], BF16)) for s in range(2)]
        ht = es.enter_context(nc.sbuf_tensor("m_h", [128, 8, 256], BF16))
        acc = es.enter_context(nc.sbuf_tensor("m_acc", [128, 8, 256], F32))
        accb = es.enter_context(nc.sbuf_tensor("m_accb", [128, 8, 256], BF16))
        sg = [es.enter_context(nc.sbuf_tensor("m_sg%d" % s, [128, 256], F32)) for s in range(2)]
        tt = [es.enter_context(nc.sbuf_tensor("m_tt%d" % s, [128, 256], F32)) for s in range(2)]
        psg = [es.enter_context(nc.psum_tensor("m_psg%d" % s, [128, 512], F32)) for s in range(2)]
        psy = [es.enter_context(nc.psum_tensor("m_psy%d" % s, [128, 512], F32)) for s in range(2)]
        pso = [es.enter_context(nc.psum_tensor("m_pso%d" % s, [128, 512], F32)) for s in range(2)]
        ntiles = alloc_norm_tiles(k, es, "m_", 256)
        A, Sh, G = emit_mod_scalars(k, es, "m_", l, 1, 3, 4, 5, 1.0)
        load_w_bf16(k, wg, k.w_gate[l], 4 * D, "wg")
        load_w_bf16(k, wb, k.w_branch[l].rearrange("i r n -> (i r) n"), D, "wb")
        load_w_bf16(k, wo, k.w_out[l], D, "wo")
        it = 0
        qi = 0
        for (b, t0, n, cond) in token_tiles(NB, 256):
            if l == 1 and cond == NB:
                continue
            s = it % 2
            it += 1
            xsrc = k.XT[b].rearrange("(c p) t -> p c t", p=128)[:, :, t0:t0 + n]
            P.dma(lambda e, s=s, xsrc=xsrc, n=n: e.dma_start(out=xt[s][:, :, :n], in_=xsrc), reads=[("XT", b)], writes=[("x", s)])
            ysrc = k.YT[b].rearrange("i (c p) t -> p (i c) t", p=128)[:, :, t0:t0 + n]
            P.dma(lambda e, s=s, ysrc=ysrc, n=n: e.dma_start(out=yt[s][:, :, :n], in_=ysrc), reads=[("YT", b)], writes=[("y", s)])
            emit_norm_mod(k, ntiles, xt[s], ht, n, A, Sh, cond, "modsc", s)
            for m in range(8):
                for i in range(4):
                    q = qi % 2
                    qi += 1
                    for kc in range(8):
                        P.op("pe", lambda e, i=i, m=m, q=q, kc=kc, n=n: e.matmul(psg[q][:, :n], lhsT=wg[:, kc, i * D + m * 128:i * D + (m + 1) * 128], rhs=ht[:, kc, :n], start=(kc == 0), stop=(kc == 7)),
                             reads=[("wg", kc), "h"], writes=[("psg", q)], inc=(kc == 7))
                    for kk in range(2):
                        P.op("pe", lambda e, i=i, m=m, q=q, kk=kk, n=n, s=s: e.matmul(psy[q][:, :n], lhsT=wb[:, i * 2 + kk, m * 128:(m + 1) * 128], rhs=yt[s][:, i * 2 + kk, :n], start=(kk == 0), stop=(kk == 1)),
                             reads=[("wb", i * 2 + kk), ("y", s)], writes=[("psy", q)], inc=(kk == 1))
                    P.op("act", lambda e, q=q, n=n: e.activation(out=sg[q][:, :n], in_=psg[q][:, :n], func=AF.Sigmoid), reads=[("psg", q)], writes=[("sg", q)])
                    if i == 0:
                        P.op("dve", lambda e, q=q, n=n, m=m: e.tensor_tensor(out=acc[:, m, :n], in0=sg[q][:, :n], in1=psy[q][:, :n], op=ALU.mult), reads=[("sg", q), ("psy", q)], writes=[("acc", m)])
                    else:
                        P.op("dve", lambda e, q=q, n=n: e.tensor_tensor(out=tt[q][:, :n], in0=sg[q][:, :n], in1=psy[q][:, :n], op=ALU.mult), reads=[("sg", q), ("psy", q)], writes=[("tt", q)])
                        if i < 3:
                            P.op("pool", lambda e, q=q, n=n, m=m: e.tensor_tensor(out=acc[:, m, :n], in0=acc[:, m, :n], in1=tt[q][:, :n], op=ALU.add), reads=[("tt", q), ("acc", m)], writes=[("acc", m)])
                        else:
                            P.op("pool", lambda e, q=q, n=n, m=m: e.tensor_tensor(out=accb[:, m, :n], in0=acc[:, m, :n], in1=tt[q][:, :n], op=ALU.add), reads=[("tt", q), ("acc", m)], writes=[("accb", m)])
            for m in range(8):
                q = m % 2
                for kc in range(8):
                    P.op("pe", lambda e, m=m, q=q, kc=kc, n=n: e.matmul(pso[q][:, :n], lhsT=wo[:, kc, m * 128:(m + 1) * 128], rhs=accb[:, kc, :n], start=(kc == 0), stop=(kc == 7)),
                         reads=[("wo", kc), ("accb", kc)], writes=[("pso", q)], inc=(kc == 7))
                P.op("dve", lambda e, m=m, q=q, s=s, n=n, cond=cond: e.scalar_tensor_tensor(out=xt[s][:, m, :n], in0=pso[q][:, :n], scalar=G[:, m, cond:cond + 1], in1=xt[s][:, m, :n], op0=ALU.mult, op1=ALU.add),
                     reads=[("pso", q), ("x", s), "modsg"], writes=[("x", s)])
            P.dma(lambda e, s=s, xsrc=xsrc, n=n: e.dma_start(out=xsrc, in_=xt[s][:, :, :n]), reads=[("x", s)], writes=[("XT", b)])
        P.end_stage()


def stage_final(k):
    nc, P, NB = k.nc, k.P, k.NB
    with ExitStack() as es:
        xt = [es.enter_context(nc.sbuf_tensor("o_x%d" % s, [128, 8, 512], F32)) for s in range(2)]
        yt = es.enter_context(nc.sbuf_tensor("o_y", [128, 8, 512], F32))
        ot = [es.enter_context(nc.sbuf_tensor("o_o%d" % s, [128, D], F32)) for s in range(2)]
        sq = es.enter_context(nc.sbuf_tensor("o_sq", [128, 8, 512], BF16))
        rs = es.enter_context(nc.sbuf_tensor("o_rs", [128, 512], F32))
        epsb = es.enter_context(nc.sbuf_tensor("o_eps", [128, 1], F32))
        psms = es.enter_context(nc.psum_tensor("o_psms", [128, 512], F32))
        ps = [es.enter_context(nc.psum_tensor("o_ps%d" % s, [128, 4, 128], F32)) for s in range(4)]
        P.op("dve", lambda e: e.memset(epsb[:], EPS), writes=["epsb"])
        it = 0
        oi = 0
        pi = 0
        for (b, t0, n, cond) in token_tiles(NB):
            if cond == NB:
                continue
            s = it % 2
            it += 1
            xsrc = k.XT[b].rearrange("(c p) t -> p c t", p=128)[:, :, t0:t0 + n]
            P.dma(lambda e, s=s, xsrc=xsrc: e.dma_start(out=xt[s][:], in_=xsrc), reads=[("XT", b)], writes=[("x", s)])
            P.op("act", lambda e, s=s: e.activation(out=sq[:], in_=xt[s][:], func=AF.Square), reads=[("x", s)], writes=["sq"])
            for c in range(8):
                P.op("pe", lambda e, c=c: e.matmul(psms[:], lhsT=k.onesb[:], rhs=sq[:, c, :], start=(c == 0), stop=(c == 7)), reads=["sq", "onesb"], writes=["psms"], inc=(c == 7))
            P.op("act", lambda e: e.activation(out=rs[:], in_=psms[:], func=AF.Sqrt, bias=epsb[:, 0:1], scale=1.0), reads=["psms", "epsb"], writes=["rs0", "rs"])
            P.op("dve", lambda e: e.reciprocal(out=rs[:], in_=rs[:]), reads=["rs0"], writes=["rs"])
            for c in range(8):
                P.op("dve", lambda e, c=c, s=s: e.scalar_tensor_tensor(out=yt[:, c, :], in0=xt[s][:, c, :], scalar=k.fg[:, c:c + 1], in1=rs[:], op0=ALU.mult, op1=ALU.mult),
                     reads=[("x", s), "rs", "fg"], writes=[("yt", c)])
            for tb in range(4):
                o = oi % 2
                oi += 1
                for h in range(2):
                    p = pi % 4
                    pi += 1
                    for c in range(4):
                        ch = h * 4 + c
                        P.op("pe", lambda e, p=p, c=c, ch=ch, tb=tb: e.transpose(ps[p][:, c, :], yt[:, ch, tb * 128:(tb + 1) * 128], k.idn[:]),
                             reads=[("yt", ch), "idn"], writes=[("ps", p)], inc=(c == 3))
                    if h == 0:
                        P.op("act", lambda e, p=p, o=o, h=h: e.activation(out=ot[o][:, h * 512:(h + 1) * 512], in_=ps[p][:].rearrange("p a b -> p (a b)"), func=AF.Identity), reads=[("ps", p)], writes=[("ot", o, h)])
                    else:
                        P.op("dve", lambda e, p=p, o=o, h=h: e.tensor_copy(out=ot[o][:, h * 512:(h + 1) * 512], in_=ps[p][:].rearrange("p a b -> p (a b)")), reads=[("ps", p)], writes=[("ot", o, h)])
                r0 = t0 - CTX + tb * 128
                P.dma(lambda e, o=o, b=b, r0=r0: e.dma_start(out=k.out[b, r0:r0 + 128, :], in_=ot[o][:]), reads=[("ot", o, 0), ("ot", o, 1)], writes=[("out", b)])
        P.end_stage()


def make_inputs(inputs, core, NB):
    b0 = core * NB
    f = lambda a: np.ascontiguousarray(a, dtype=np.float32)
    w_in = inputs["w_in"]
    cT = np.concatenate([inputs["c"][b0:b0 + NB], inputs["c_ctx"][None]], 0).reshape(NB + 1, 8, 128).transpose(2, 1, 0)
    return {
        "x": f(inputs["x"][b0:b0 + NB]),
        "ctx": f(inputs["ctx"][b0:b0 + NB]),
        "cT": f(cT),
        "ada_w": f(inputs["ada_w"]),
        "ada_b": f(inputs["ada_b"].reshape(2, 72, 128).transpose(0, 2, 1)),
        "norm_g": f(inputs["norm_g"].reshape(2, 3, 8, 128).transpose(0, 1, 3, 2)),
        "final_g": f(inputs["final_g"].reshape(8, 128).T),
        "ffn_w1": f(inputs["ffn_w1"]), "ffn_w3": f(inputs["ffn_w3"]), "ffn_w2": f(inputs["ffn_w2"]),
        "w_fm": f(np.concatenate([w_in[:, :, a:a + w] for a, w in FM_GROUPS], axis=2)),
        "w_tok": f(np.concatenate([w_in[:, :, a:a + w] for a, w in TOK_GROUPS], axis=2)),
        "w_gate": f(w_in[:, :, GATE0:]),
        "w_branch": f(inputs["w_branch"]),
        "w_out": f(inputs["w_out"]),
        "ident": np.eye(128, dtype=np.float32),
        "rope_cos": ROPE[0], "rope_sin": ROPE[1], "rope_rm": ROPE[2],
        "hg_masks": HG_MASKS, "bd64": BD64,
        "s5_lam_re": f(inputs["s5_lam_re"].reshape(2, 2, 8, 2, 64).transpose(0, 3, 4, 1, 2).reshape(2, 128, 16)),
        "s5_lam_im": f(inputs["s5_lam_im"].reshape(2, 2, 8, 2, 64).transpose(0, 3, 4, 1, 2).reshape(2, 128, 16)),
        "s5_log_step": f(np.broadcast_to(inputs["s5_log_step"].reshape(2, 2, 8, 2, 1), (2, 2, 8, 2, 64)).transpose(0, 3, 4, 1, 2).reshape(2, 128, 16)),
        "s5_b_re": f(inputs["s5_b_re"].reshape(2, 8, 2, 64, 16).transpose(0, 2, 3, 1, 4).reshape(2, 128, 8, 16)),
        "s5_b_im": f(inputs["s5_b_im"].reshape(2, 8, 2, 64, 16).transpose(0, 2, 3, 1, 4).reshape(2, 128, 8, 16)),
        "s5_c_re": f(inputs["s5_c_re"].reshape(2, 8, 2, 16, 64).transpose(0, 2, 4, 1, 3).reshape(2, 128, 8, 16)),
        "s5_c_im": f(inputs["s5_c_im"].reshape(2, 8, 2, 16, 64).transpose(0, 2, 4, 1, 3).reshape(2, 128, 8, 16)),
        "s5_d": f(inputs["s5_d"].reshape(2, 2, 128).transpose(0, 2, 1)),
        "s5_glu": f(inputs["s5_glu"]),
        "dn_gm64": DN_GM64, "dn_gmbd": DN_GMBD, "dn_idn2": DN_IDN2,
        "dn_conv": f(inputs["dn_conv"].reshape(2, 5, 6, 128).transpose(0, 3, 2, 1)),
        "dn_a_log": f(np.broadcast_to(inputs["dn_a_log"].reshape(2, 1, 8), (2, 128, 8))),
        "dn_dt_bias": f(np.broadcast_to(inputs["dn_dt_bias"].reshape(2, 1, 8), (2, 128, 8))),
        "dn_norm_g": f(np.broadcast_to(inputs["dn_norm_g"].reshape(2, 1, 64), (2, 128, 64))),
        "hg_lb": f(inputs["hg_lb_logits"].reshape(2, 2, 128).transpose(2, 1, 0)),
        "hg_norm_g": f(np.tile(inputs["hg_norm_g"], (1, 2)).reshape(2, 128, 1)),
        "at_qn_g": f(inputs["at_qn_g"].reshape(2, 64, 1)), "at_kn_g": f(inputs["at_kn_g"].reshape(2, 64, 1)),
    }


def _rope_consts():
    n = np.arange(SEQ)
    r, c = n // 64, n % 64
    inv = (10000.0 ** (-np.arange(0, 32, 2, dtype=np.float32) / 32)).astype(np.float32)
    ang_r = r[:, None].astype(np.float32) * inv
    ang_c = c[:, None].astype(np.float32) * inv
    ang = np.concatenate([ang_r, ang_r, ang_c, ang_c], -1)
    rm = np.zeros((64, 64), np.float32)
    for i in range(16):
        rm[16 + i, i] = -1.0
        rm[i, 16 + i] = 1.0
        rm[48 + i, 32 + i] = -1.0
        rm[32 + i, 48 + i] = 1.0
    return np.ascontiguousarray(np.cos(ang).T.astype(np.float32)), np.ascontiguousarray(np.sin(ang).T.astype(np.float32)), rm


ROPE = _rope_consts()


def _dn_consts():
    a = np.arange(64)
    le = [(a[:, None] <= a[None, :]), (a[:, None] >= a[None, :])]
    st = [(a[None, :] < a[:, None]), (a[None, :] > a[:, None])]
    gm64 = np.zeros((64, 2, 2, 128), np.float32)
    gmbd = np.zeros((128, 2, 2, 128), np.float32)
    for d in range(2):
        gm64[:, d, 0, :] = np.tile(le[d], (1, 2))
        gm64[:, d, 1, :] = np.tile(st[d], (1, 2))
        for h in range(2):
            gmbd[h * 64:(h + 1) * 64, d, 0, h * 64:(h + 1) * 64] = st[d]
            gmbd[h * 64:(h + 1) * 64, d, 1, h * 64:(h + 1) * 64] = le[d]
    idn2 = np.tile(np.eye(64, dtype=np.float32), (2, 1))
    return gm64, gmbd, idn2


DN_GM64, DN_GMBD, DN_IDN2 = _dn_consts()
_s = np.arange(128)[:, None] % 64
_t = np.arange(64)[None, :]
HG_MASKS = np.ascontiguousarray(np.stack([(_s <= _t), (_s >= _t)], 1).astype(np.float32))
BD64 = np.kron(np.eye(2, dtype=np.float32), np.full((64, 64), 1.0 / 64, np.float32))
_CACHE = {}


def kernel(**inputs):
    NB = inputs["x"].shape[0] // N_CORES
    if "nc" not in _CACHE:
        _CACHE["nc"] = build(NB)
    nc = _CACHE["nc"]
    shared = None
    in_maps = []
    for c in range(N_CORES):
        in_maps.append(make_inputs(inputs, c, NB))
    res = run_bass_kernel_spmd(nc, in_maps, core_ids=list(range(N_CORES)))
    return np.concatenate([r["out"] for r in res.results], axis=0).astype(np.float32)
```

```python
import numpy as np
from contextlib import ExitStack
import concourse.bass as bass
import concourse.mybir as mybir
from concourse.bass_utils import run_bass_kernel_spmd

F32 = mybir.dt.float32
BF16 = mybir.dt.bfloat16
AF = mybir.ActivationFunctionType
ALU = mybir.AluOpType
AX = mybir.AxisListType

D = 1024
SEQ = 2048
CTX = 256
T = SEQ + CTX
DFF = 2816
NFF = DFF // 128
EPS = 1e-6
N_CORES = 8
ENG = ("pe", "act", "dve", "pool", "sp")

FM_GROUPS = [(0, 768), (768, 256), (1040, 256), (1296, 256), (1552, 512), (2320, 256), (2576, 256), (2832, 128)]
FM_W = sum(w for _, w in FM_GROUPS)
FM_Q, FM_K, FM_V, FM_Z, FM_U, FM_HQ, FM_HF, FM_HG, FM_AQ, FM_AK = 0, 256, 512, 768, 1024, 1280, 1536, 2048, 2304, 2560
TOK_GROUPS = [(2960, 128), (2064, 256), (1024, 8), (1032, 8)]
TOK_W = 400
TK_AV, TK_HV, TK_A, TK_B = 0, 128, 384, 392
GATE0 = 3088
SERIAL_DN = True
SERIAL_S5 = True


class Prog:
    def __init__(self, nc, n_dma_sems=56):
        self.nc = nc
        self.sem = {e: nc.semaphore("sem_" + e).__enter__() for e in ("pe", "act", "dve", "pool")}
        self.dsem = [nc.semaphore("dsem%d" % i).__enter__() for i in range(n_dma_sems)]
        self.duse = [0] * n_dma_sems
        self.dnext = 0
        self.cnt = {e: 0 for e in self.sem}
        self.lastw = {}
        self.readers = {}
        self.ops = {e: [] for e in ENG}
        self.waited = {}
        self.nops = 0
        self.serial = False
        self.last_ev = {}
        self.prev_ev = None

    def _need(self, eng, ev, waits):
        if ev is None:
            return
        key, val, src = ev
        if self.waited.get((eng, key), 0) >= val:
            return
        self.waited[(eng, key)] = val
        waits.append((key, val))

    def _semh(self, key):
        return self.sem[key] if isinstance(key, str) else self.dsem[key]

    def op(self, eng, fn, reads=(), writes=(), inc=True):
        waits = []
        for r in reads:
            ev = self.lastw.get(r)
            if ev is not None and not (ev[2] == eng and eng == "pe"):
                self._need(eng, ev, waits)
        for w in writes:
            ev = self.lastw.get(w)
            if ev is not None and ev[2] != eng:
                self._need(eng, ev, waits)
            for rv in self.readers.get(w, ()):
                if rv[2] != eng:
                    self._need(eng, rv, waits)
        if self.serial:
            waits = [w_ for w_ in waits if not isinstance(w_[0], str) or w_[0] == eng]
            for w_ in waits:
                pass
            if self.prev_ev is not None and self.prev_ev[2] != eng:
                if self.waited.get((eng, self.prev_ev[0]), 0) < self.prev_ev[1] or True:
                    self.waited[(eng, self.prev_ev[0])] = max(self.waited.get((eng, self.prev_ev[0]), 0), self.prev_ev[1])
                    waits.append((self.prev_ev[0], self.prev_ev[1]))
        if inc:
            self.cnt[eng] += 1
            me = (eng, self.cnt[eng], eng)
        else:
            me = (eng, self.cnt[eng] + 1, eng)
        self.last_ev[eng] = me
        self.prev_ev = me if inc else (self.prev_ev if not self.serial else me)
        for r in reads:
            self.readers.setdefault(r, []).append(me)
        for w in writes:
            self.lastw[w] = me
            self.readers[w] = []
        self.ops[eng].append((waits, fn, ("sem", eng) if inc else ("none", eng)))
        self.nops += 1
        return me

    def dma(self, fn, reads=(), writes=(), q="sp"):
        waits = []
        for r in reads:
            self._need(q, self.lastw.get(r), waits)
        for w in writes:
            self._need(q, self.lastw.get(w), waits)
            for rv in self.readers.get(w, ()):
                self._need(q, rv, waits)
        i = self.dnext
        self.dnext = (self.dnext + 1) % len(self.dsem)
        if self.duse[i] > 0:
            self._need(q, (i, 16 * self.duse[i], "dma"), waits)
        self.duse[i] += 1
        me = (i, 16 * self.duse[i], "dma")
        for r in reads:
            self.readers.setdefault(r, []).append(me)
        for w in writes:
            self.lastw[w] = me
            self.readers[w] = []
        self.ops[q].append((waits, fn, ("dsem", i)))
        self.nops += 1
        return me

    def end_stage(self):
        waits = []
        for tok, ev in self.lastw.items():
            self._need("sp", ev, waits)
        for tok, evs in self.readers.items():
            for ev in evs:
                self._need("sp", ev, waits)
        self.ops["sp"].append((waits, None, None))
        nc = self.nc
        engobj = {"pe": "tensor", "act": "scalar", "dve": "vector", "pool": "gpsimd", "sp": "sync"}
        with nc.Block() as block:
            for e in ENG:
                ops = self.ops[e]
                if not ops:
                    continue

                def body(eo, ops=ops):
                    for waits, fn, inc in ops:
                        for key, val in waits:
                            eo.wait_ge(self._semh(key), val)
                        if fn is None:
                            continue
                        ins = fn(eo)
                        if inc[0] == "sem":
                            ins.then_inc(self.sem[inc[1]], 1)
                        elif inc[0] == "dsem":
                            ins.then_inc(self.dsem[inc[1]], 16)

                getattr(block, engobj[e])(body)
        self.ops = {e: [] for e in ENG}
        self.waited = {}
        self.lastw = {}
        self.readers = {}
        self.last_ev = {}
        self.serial = False
        self.prev_ev = None


class K:
    pass


class NCProxy:
    def __init__(self, nc):
        object.__setattr__(self, "_nc", nc)
        object.__setattr__(self, "_n", [0])

    def __getattr__(self, name):
        return getattr(self._nc, name)

    def sbuf_tensor(self, name, shape, dtype):
        self._n[0] += 1
        return self._nc.sbuf_tensor("%s_u%d" % (name, self._n[0]), shape, dtype)

    def psum_tensor(self, name, shape, dtype):
        self._n[0] += 1
        return self._nc.psum_tensor("%s_u%d" % (name, self._n[0]), shape, dtype)


def token_tiles(NB, ts=512):
    tl = []
    for b in range(NB):
        tl.append((b, 0, CTX, NB))
        for i in range(SEQ // ts):
            tl.append((b, CTX + i * ts, ts, b))
    return tl


def build(NB, opts=None):
    opts = opts or {}
    NC = NB + 1
    nc = bass.Bass("TRN2", target_bir_lowering=False)
    k = K()
    k.nc, k.NB, k.NC, k.opts = NCProxy(nc), NB, NC, opts
    inp = lambda name, shape: nc.dram_tensor(name, list(shape), F32, kind="ExternalInput").ap()
    k.x = inp("x", [NB, SEQ, D])
    k.ctx = inp("ctx", [NB, CTX, D])
    k.cT = inp("cT", [128, 8, NC])
    k.ada_w = inp("ada_w", [2, D, 9 * D])
    k.ada_b = inp("ada_b", [2, 128, 72])
    k.norm_g = inp("norm_g", [2, 3, 128, 8])
    k.final_g = inp("final_g", [128, 8])
    k.w1 = inp("ffn_w1", [2, 2, D, DFF])
    k.w3 = inp("ffn_w3", [2, 2, D, DFF])
    k.w2 = inp("ffn_w2", [2, 2, DFF, D])
    k.w_fm = inp("w_fm", [2, D, FM_W])
    k.w_tok = inp("w_tok", [2, D, TOK_W])
    k.w_gate = inp("w_gate", [2, D, 4 * D])
    k.w_branch = inp("w_branch", [2, 4, 256, D])
    k.w_out = inp("w_out", [2, D, D])
    k.ident = inp("ident", [128, 128])
    k.rope_cos = inp("rope_cos", [64, SEQ])
    k.rope_sin = inp("rope_sin", [64, SEQ])
    k.rope_rm = inp("rope_rm", [64, 64])
    k.at_qn_g = inp("at_qn_g", [2, 64, 1])
    k.at_kn_g = inp("at_kn_g", [2, 64, 1])
    k.hg_masks = inp("hg_masks", [128, 2, 64])
    k.s5_lam_re = inp("s5_lam_re", [2, 128, 16])
    k.s5_lam_im = inp("s5_lam_im", [2, 128, 16])
    k.s5_log_step = inp("s5_log_step", [2, 128, 16])
    k.s5_b_re = inp("s5_b_re", [2, 128, 8, 16])
    k.s5_b_im = inp("s5_b_im", [2, 128, 8, 16])
    k.s5_c_re = inp("s5_c_re", [2, 128, 8, 16])
    k.s5_c_im = inp("s5_c_im", [2, 128, 8, 16])
    k.s5_d = inp("s5_d", [2, 128, 2])
    k.s5_glu = inp("s5_glu", [2, 256, 512])
    k.S5TAB = nc.dram_tensor("S5TAB", [16, 2, 128, T], F32).ap()
    k.dn_gm64 = inp("dn_gm64", [64, 2, 2, 128])
    k.dn_gmbd = inp("dn_gmbd", [128, 2, 2, 128])
    k.dn_idn2 = inp("dn_idn2", [128, 64])
    k.dn_conv = inp("dn_conv", [2, 128, 6, 5])
    k.dn_a_log = inp("dn_a_log", [2, 128, 8])
    k.dn_dt_bias = inp("dn_dt_bias", [2, 128, 8])
    k.dn_norm_g = inp("dn_norm_g", [2, 128, 64])
    k.bd64 = inp("bd64", [128, 128])
    k.hg_lb = inp("hg_lb", [128, 2, 2])
    k.hg_norm_g = inp("hg_norm_g", [2, 128, 1])
    k.out = nc.dram_tensor("out", [NB, SEQ, D], F32, kind="ExternalOutput").ap()
    k.XT = nc.dram_tensor("XT", [NB, D, T], F32).ap()
    only = opts.get("only")
    k.PT = nc.dram_tensor("PT", [NB, FM_W, T], F32, **({"kind": "ExternalInput"} if only else {})).ap()
    k.TOK = nc.dram_tensor("TOK", [NB, T, TOK_W], F32, **({"kind": "ExternalInput"} if only else {})).ap()
    k.YT = nc.dram_tensor("YT", [NB, 4, 256, T], BF16, **({"kind": "ExternalOutput"} if only else {})).ap()
    if "dump" in opts:
        k.dump = nc.dram_tensor("dump", [16, 128, T], F32, kind="ExternalOutput").ap()
    if "dbg" in opts:
        k.dbg = {nm: nc.dram_tensor("dbg_" + nm, list(shp), F32, kind="ExternalOutput").ap() for nm, shp in opts["dbg"].items()}
    P = Prog(nc)
    k.P = P
    with ExitStack() as es:
        al = lambda name, shape, dt=F32: es.enter_context(nc.sbuf_tensor(name, list(shape), dt))
        k.idn = al("idn", [128, 128])
        k.onesb = al("onesb", [128, 128], BF16)
        k.mods = al("mods", [128, 72, NC])
        k.ng = al("ng", [128, 2, 3, 8])
        k.fg = al("fg", [128, 8])
        P.dma(lambda e: e.dma_start(out=k.idn[:], in_=k.ident), writes=["idn"])
        P.dma(lambda e: e.dma_start(out=k.fg[:], in_=k.final_g), writes=["fg"])
        for l in range(2):
            for j in range(3):
                P.dma(lambda e, l=l, j=j: e.dma_start(out=k.ng[:, l, j, :], in_=k.norm_g[l, j]), writes=["ng"])
        P.op("dve", lambda e: e.memset(k.onesb[:], 1.0 / D), writes=["onesb"])
        P.end_stage()
        if only:
            for l in opts.get("layers", (0,)):
                {"att": stage_attention, "hg": stage_hgrn2, "dn": stage_deltanet, "s5": stage_s5}[only](k, l)
            return nc
        stage_transpose_in(k)
        stop = opts.get("stop", "")
        for l in range(2):
            stage_ada(k, l)
            stage_ffn(k, l, 0)
            if stop == "ffn1_%d" % l:
                break
            stage_inproj(k, l)
            if stop == "inproj_%d" % l:
                break
            stage_mixers(k, l)
            if stop == "mixers_%d" % l:
                break
            stage_merge(k, l)
            if stop == "merge_%d" % l:
                break
            stage_ffn(k, l, 1)
        if "XT" in opts.get("dbg", {}):
            P.dma(lambda e: e.dma_start(out=k.dbg["XT"], in_=k.XT), reads=[], writes=["dbgx"])
            P.end_stage()
        if "PT" in opts.get("dbg", {}):
            P.dma(lambda e: e.dma_start(out=k.dbg["PT"], in_=k.PT), reads=[], writes=["dbgp"])
            P.dma(lambda e: e.dma_start(out=k.dbg["TOK"], in_=k.TOK), reads=[], writes=["dbgt"])
            P.end_stage()
        stage_final(k)
    return nc


def stage_transpose_in(k):
    nc, P = k.nc, k.P
    with ExitStack() as es:
        xin = [es.enter_context(nc.sbuf_tensor("ti_x%d" % i, [128, D], F32)) for i in range(2)]
        xo = [es.enter_context(nc.sbuf_tensor("ti_o%d" % i, [128, 8, 128], F32)) for i in range(2)]
        ps = [es.enter_context(nc.psum_tensor("ti_ps%d" % i, [128, 4, 128], F32)) for i in range(4)]
        it = 0
        for b in range(k.NB):
            for tb in range(T // 128):
                s = it % 2
                src = k.ctx[b, tb * 128:(tb + 1) * 128, :] if tb < 2 else k.x[b, (tb - 2) * 128:(tb - 1) * 128, :]
                P.dma(lambda e, s=s, src=src: e.dma_start(out=xin[s][:], in_=src), writes=[("xin", s)])
                for h in range(2):
                    pi = (it * 2 + h) % 4
                    for c in range(4):
                        ch = h * 4 + c
                        P.op("pe", lambda e, s=s, pi=pi, c=c, ch=ch: e.transpose(ps[pi][:, c, :], xin[s][:, ch * 128:(ch + 1) * 128], k.idn[:]),
                             reads=[("xin", s), "idn"], writes=[("tps", pi)], inc=(c == 3))
                    eng = "act" if h == 0 else "dve"
                    if eng == "act":
                        P.op("act", lambda e, s=s, pi=pi, h=h: e.activation(out=xo[s][:, h * 4:(h + 1) * 4, :], in_=ps[pi][:], func=AF.Identity),
                             reads=[("tps", pi)], writes=[("xo", s, h)])
                    else:
                        P.op("dve", lambda e, s=s, pi=pi, h=h: e.tensor_copy(out=xo[s][:, h * 4:(h + 1) * 4, :], in_=ps[pi][:]),
                             reads=[("tps", pi)], writes=[("xo", s, h)])
                dst = k.XT[b].rearrange("(c p) t -> p c t", p=128)[:, :, tb * 128:(tb + 1) * 128]
                P.dma(lambda e, s=s, dst=dst: e.dma_start(out=dst, in_=xo[s][:]), reads=[("xo", s, 0), ("xo", s, 1)], writes=[("XT", b)])
                it += 1
        P.end_stage()


def stage_ada(k, l):
    nc, P, NC = k.nc, k.P, k.NC
    with ExitStack() as es:
        sc = es.enter_context(nc.sbuf_tensor("ad_sc", [128, 8, NC], F32))
        ab = es.enter_context(nc.sbuf_tensor("ad_b", [128, 72], F32))
        wt = [es.enter_context(nc.sbuf_tensor("ad_w%d" % i, [128, 8, 1024], F32)) for i in range(2)]
        ps = [es.enter_context(nc.psum_tensor("ad_ps%d" % i, [128, 8, NC], F32)) for i in range(2)]
        P.dma(lambda e: e.dma_start(out=sc[:], in_=k.cT), writes=["sc"])
        P.dma(lambda e: e.dma_start(out=ab[:], in_=k.ada_b[l]), writes=["ab"])
        P.op("act", lambda e: e.activation(out=sc[:], in_=sc[:], func=AF.Silu), reads=["sc"], writes=["sc"])
        for j in range(9):
            s = j % 2
            src = k.ada_w[l][:, j * 1024:(j + 1) * 1024].rearrange("(c p) n -> p c n", p=128)
            for hh in range(2):
                P.dma(lambda e, s=s, src=src, hh=hh: e.dma_start(out=wt[s][:, hh * 4:(hh + 1) * 4, :], in_=src[:, hh * 4:(hh + 1) * 4, :]), writes=[("adw", s, hh)])
            for m in range(8):
                for kc in range(8):
                    P.op("pe", lambda e, s=s, m=m, kc=kc: e.matmul(ps[s][:, m, :], lhsT=wt[s][:, kc, m * 128:(m + 1) * 128], rhs=sc[:, kc, :], start=(kc == 0), stop=(kc == 7)),
                         reads=[("adw", s, kc // 4), "sc"], writes=[("adps", s)], inc=(kc == 7 and m == 7))
            P.op("dve", lambda e, s=s, j=j: e.tensor_tensor(out=k.mods[:, j * 8:(j + 1) * 8, :], in0=ps[s][:], in1=ab[:, j * 8:(j + 1) * 8].rearrange("p (c o) -> p c o", o=1).to_broadcast([128, 8, NC]), op=ALU.add),
                 reads=[("adps", s), "ab"], writes=["mods"])
        P.end_stage()


def emit_norm_mod(k, es_tiles, x_t, h_t, n, A, Sh, cond, tag, sl):
    P = k.P
    sq, rs, tmp, psms = es_tiles
    P.op("act", lambda e: e.activation(out=sq[:, :, :n], in_=x_t[:, :, :n], func=AF.Square), reads=[("x", sl)], writes=["sq"])
    for c in range(8):
        P.op("pe", lambda e, c=c: e.matmul(psms[:, :n], lhsT=k.onesb[:], rhs=sq[:, c, :n], start=(c == 0), stop=(c == 7)), reads=["sq", "onesb"], writes=["psms"], inc=(c == 7))
    P.op("act", lambda e: e.activation(out=rs[:, :n], in_=psms[:, :n], func=AF.Sqrt, bias=k.epsb[:, 0:1], scale=1.0), reads=["psms", "epsb"], writes=["rs0", "rs"])
    P.op("dve", lambda e: e.reciprocal(out=rs[:, :n], in_=rs[:, :n]), reads=["rs0"], writes=["rs"])
    for c in range(8):
        P.op("dve", lambda e, c=c: e.tensor_tensor(out=tmp[c % 2][:, :n], in0=x_t[:, c, :n], in1=rs[:, :n], op=ALU.mult), reads=[("x", sl), "rs"], writes=[("tmp", c % 2)])
        P.op("pool", lambda e, c=c: e.tensor_scalar(out=h_t[:, c, :n], in0=tmp[c % 2][:, :n], scalar1=A[:, c, cond:cond + 1], scalar2=Sh[:, c, cond:cond + 1], op0=ALU.mult, op1=ALU.add),
             reads=[("tmp", c % 2), tag], writes=["h"])


def alloc_norm_tiles(k, es, pfx, ts=512):
    nc = k.nc
    sq = es.enter_context(nc.sbuf_tensor(pfx + "sq", [128, 8, ts], BF16))
    rs = es.enter_context(nc.sbuf_tensor(pfx + "rs", [128, ts], F32))
    tmp = [es.enter_context(nc.sbuf_tensor(pfx + "tmp%d" % i, [128, ts], F32)) for i in range(2)]
    psms = es.enter_context(nc.psum_tensor(pfx + "psms", [128, 512], F32))
    k.epsb = es.enter_context(nc.sbuf_tensor(pfx + "epsb", [128, 1], F32))
    k.P.op("dve", lambda e: e.memset(k.epsb[:], EPS), writes=["epsb"])
    return sq, rs, tmp, psms


def emit_mod_scalars(k, es, pfx, l, jn, i_shift, i_scale, i_gate, gate_mul):
    nc, P, NC = k.nc, k.P, k.NC
    A = es.enter_context(nc.sbuf_tensor(pfx + "A", [128, 8, NC], F32))
    G = es.enter_context(nc.sbuf_tensor(pfx + "G", [128, 8, NC], F32))
    Sh = k.mods[:, i_shift * 8:(i_shift + 1) * 8, :]
    P.op("dve", lambda e: e.tensor_scalar(out=A[:], in0=k.mods[:, i_scale * 8:(i_scale + 1) * 8, :], scalar1=1.0, scalar2=None, op0=ALU.add), reads=["mods"], writes=["A0"])
    P.op("dve", lambda e: e.tensor_tensor(out=A[:], in0=A[:], in1=k.ng[:, l, jn, :].rearrange("p (c o) -> p c o", o=1).to_broadcast([128, 8, NC]), op=ALU.mult), reads=["A0", "ng"], writes=["modsc"])
    if i_gate is not None:
        P.op("dve", lambda e: e.tensor_scalar(out=G[:], in0=k.mods[:, i_gate * 8:(i_gate + 1) * 8, :], scalar1=gate_mul, scalar2=None, op0=ALU.mult), reads=["mods"], writes=["modsg"])
    return A, Sh, G


def load_w_bf16(k, dst, src_rows, ncols, tok, nk=8):
    P = k.P
    for kc in range(nk):
        P.dma(lambda e, kc=kc: e.dma_start(out=dst[:, kc, :], in_=src_rows[kc * 128:(kc + 1) * 128, :], max_dma_last_dim=4096),
              writes=[(tok, kc)], q="pool")


def stage_ffn(k, l, i):
    nc, P, NB, NC = k.nc, k.P, k.NB, k.NC
    jn = 0 if i == 0 else 2
    mi = (0, 1, 2) if i == 0 else (6, 7, 8)
    last = (l == 1 and i == 1)
    with ExitStack() as es:
        w1 = es.enter_context(nc.sbuf_tensor("f_w1", [128, 8, DFF], BF16))
        w3 = es.enter_context(nc.sbuf_tensor("f_w3", [128, 8, DFF], BF16))
        w2 = es.enter_context(nc.sbuf_tensor("f_w2", [128, NFF, D], BF16))
        xt = [es.enter_context(nc.sbuf_tensor("f_x%d" % s, [128, 8, 512], F32)) for s in range(1)]
        ht = es.enter_context(nc.sbuf_tensor("f_h", [128, 8, 512], BF16))
        hid = es.enter_context(nc.sbuf_tensor("f_hid", [128, NFF, 512], BF16))
        sl_t = [es.enter_context(nc.sbuf_tensor("f_s%d" % s, [128, 512], F32)) for s in range(2)]
        ps1 = [es.enter_context(nc.psum_tensor("f_ps1%d" % s, [128, 512], F32)) for s in range(2)]
        ps3 = [es.enter_context(nc.psum_tensor("f_ps3%d" % s, [128, 512], F32)) for s in range(2)]
        pso = [es.enter_context(nc.psum_tensor("f_pso%d" % s, [128, 512], F32)) for s in range(2)]
        ntiles = alloc_norm_tiles(k, es, "f_", 512)
        A, Sh, G = emit_mod_scalars(k, es, "f_", l, jn, mi[0], mi[1], mi[2], 0.5)
        load_w_bf16(k, w1, k.w1[l, i], DFF, "w1")
        load_w_bf16(k, w3, k.w3[l, i], DFF, "w3")
        load_w_bf16(k, w2, k.w2[l, i], D, "w2", nk=NFF)
        it = 0
        for (b, t0, n, cond) in token_tiles(NB, 512):
            if last and cond == NB:
                continue
            s = 0
            it += 1
            xsrc = k.XT[b].rearrange("(c p) t -> p c t", p=128)[:, :, t0:t0 + n]
            P.dma(lambda e, s=s, xsrc=xsrc, n=n: e.dma_start(out=xt[s][:, :, :n], in_=xsrc), reads=[("XT", b)], writes=[("x", s)])
            emit_norm_mod(k, ntiles, xt[s], ht, n, A, Sh, cond, "modsc", s)
            for f in range(NFF):
                q = f % 2
                for kc in range(8):
                    P.op("pe", lambda e, f=f, q=q, kc=kc, n=n: e.matmul(ps1[q][:, :n], lhsT=w1[:, kc, f * 128:(f + 1) * 128], rhs=ht[:, kc, :n], start=(kc == 0), stop=(kc == 7)),
                         reads=[("w1", kc), "h"], writes=[("ps1", q)], inc=(kc == 7))
                for kc in range(8):
                    P.op("pe", lambda e, f=f, q=q, kc=kc, n=n: e.matmul(ps3[q][:, :n], lhsT=w3[:, kc, f * 128:(f + 1) * 128], rhs=ht[:, kc, :n], start=(kc == 0), stop=(kc == 7)),
                         reads=[("w3", kc), "h"], writes=[("ps3", q)], inc=(kc == 7))
                P.op("act", lambda e, q=q, n=n: e.activation(out=sl_t[q][:, :n], in_=ps1[q][:, :n], func=AF.Silu), reads=[("ps1", q)], writes=[("sl", q)])
                P.op("dve", lambda e, q=q, f=f, n=n: e.tensor_tensor(out=hid[:, f, :n], in0=sl_t[q][:, :n], in1=ps3[q][:, :n], op=ALU.mult), reads=[("sl", q), ("ps3", q)], writes=[("hid", f)])
            for m in range(8):
                q = m % 2
                for f in range(NFF):
                    P.op("pe", lambda e, m=m, q=q, f=f, n=n: e.matmul(pso[q][:, :n], lhsT=w2[:, f, m * 128:(m + 1) * 128], rhs=hid[:, f, :n], start=(f == 0), stop=(f == NFF - 1)),
                         reads=[("w2", f), ("hid", f)], writes=[("pso", q)], inc=(f == NFF - 1))
                P.op("dve", lambda e, m=m, q=q, s=s, n=n, cond=cond: e.scalar_tensor_tensor(out=xt[s][:, m, :n], in0=pso[q][:, :n], scalar=G[:, m, cond:cond + 1], in1=xt[s][:, m, :n], op0=ALU.mult, op1=ALU.add),
                     reads=[("pso", q), ("x", s), "modsg"], writes=[("x", s)])
            P.dma(lambda e, s=s, xsrc=xsrc, n=n: e.dma_start(out=xsrc, in_=xt[s][:, :, :n]), reads=[("x", s)], writes=[("XT", b)])
        P.end_stage()


def stage_inproj(k, l):
    nc, P, NB, NC = k.nc, k.P, k.NB, k.NC
    NCH = FM_W // 128
    with ExitStack() as es:
        wf = es.enter_context(nc.sbuf_tensor("p_wf", [128, 8, FM_W], BF16))
        wk = es.enter_context(nc.sbuf_tensor("p_wk", [128, 8, TOK_W], BF16))
        xt = [es.enter_context(nc.sbuf_tensor("p_x%d" % s, [128, 8, 512], F32)) for s in range(2)]
        ht = es.enter_context(nc.sbuf_tensor("p_h", [128, 8, 512], BF16))
        ot = [es.enter_context(nc.sbuf_tensor("p_o%d" % s, [128, 512], F32)) for s in range(4)]
        ps = [es.enter_context(nc.psum_tensor("p_ps%d" % s, [128, 512], F32)) for s in range(4)]
        ntiles = alloc_norm_tiles(k, es, "p_")
        A, Sh, G = emit_mod_scalars(k, es, "p_", l, 1, 3, 4, None, 1.0)
        load_w_bf16(k, wf, k.w_fm[l], FM_W, "wf")
        load_w_bf16(k, wk, k.w_tok[l], TOK_W, "wk")
        it = 0
        oi = 0
        for (b, t0, n, cond) in token_tiles(NB):
            s = it % 2
            it += 1
            xsrc = k.XT[b].rearrange("(c p) t -> p c t", p=128)[:, :, t0:t0 + n]
            P.dma(lambda e, s=s, xsrc=xsrc, n=n: e.dma_start(out=xt[s][:, :, :n], in_=xsrc), reads=[("XT", b)], writes=[("x", s)])
            emit_norm_mod(k, ntiles, xt[s], ht, n, A, Sh, cond, "modsc", s)
            for ch in range(NCH):
                q = oi % 4
                oi += 1
                for kc in range(8):
                    P.op("pe", lambda e, ch=ch, q=q, kc=kc, n=n: e.matmul(ps[q][:, :n], lhsT=wf[:, kc, ch * 128:(ch + 1) * 128], rhs=ht[:, kc, :n], start=(kc == 0), stop=(kc == 7)),
                         reads=[("wf", kc), "h"], writes=[("ps", q)], inc=(kc == 7))
                if oi % 2 == 0:
                    P.op("act", lambda e, q=q, n=n: e.activation(out=ot[q][:, :n], in_=ps[q][:, :n], func=AF.Identity), reads=[("ps", q)], writes=[("ot", q)])
                else:
                    P.op("dve", lambda e, q=q, n=n: e.tensor_copy(out=ot[q][:, :n], in_=ps[q][:, :n]), reads=[("ps", q)], writes=[("ot", q)])
                P.dma(lambda e, q=q, ch=ch, b=b, t0=t0, n=n: e.dma_start(out=k.PT[b, ch * 128:(ch + 1) * 128, t0:t0 + n], in_=ot[q][:, :n]), reads=[("ot", q)], writes=[("PT", b)])
            for tb in range(n // 128):
                q = oi % 4
                oi += 1
                for kc in range(8):
                    P.op("pe", lambda e, tb=tb, q=q, kc=kc: e.matmul(ps[q][:, :TOK_W], lhsT=ht[:, kc, tb * 128:(tb + 1) * 128], rhs=wk[:, kc, :], start=(kc == 0), stop=(kc == 7)),
                         reads=[("wk", kc), "h"], writes=[("ps", q)], inc=(kc == 7))
                P.op("dve", lambda e, q=q: e.tensor_copy(out=ot[q][:, :TOK_W], in_=ps[q][:, :TOK_W]), reads=[("ps", q)], writes=[("ot", q)])
                P.dma(lambda e, q=q, b=b, tb=tb, t0=t0: e.dma_start(out=k.TOK[b, t0 + tb * 128:t0 + (tb + 1) * 128, :], in_=ot[q][:, :TOK_W]), reads=[("ot", q)], writes=[("TOK", b)])
        P.end_stage()


def stage_attention(k, l):
    nc, P, NB = k.nc, k.P, k.NB
    NKB = T // 128
    with ExitStack() as es:
        al = lambda name, shape, dt=F32: es.enter_context(nc.sbuf_tensor("a_" + name, list(shape), dt))
        cos = al("cos", [64, SEQ]); sin = al("sin", [64, SEQ]); rm = al("rm", [64, 64]); o64 = al("o64", [64, 64])
        gq = al("gq", [64, 1]); gk = al("gk", [64, 1]); epsb = al("eps", [64, 1])
        onesb = al("onesb", [128, 64], BF16)
        raw = [al("raw%d" % i, [64, T]) for i in range(2)]
        kT = [al("kT%d" % i, [64, T], BF16) for i in range(2)]
        qT = [al("qT%d" % i, [64, T], BF16) for i in range(2)]
        vraw = al("vraw", [128, NKB, 128])
        vb = al("vb", [128, NKB, 128], BF16)
        sq = al("sq", [64, 512]); rs = al("rs", [64, 512]); kn = al("kn", [64, 512]); t1 = al("t1", [64, 512]); t2 = al("t2", [64, 512])
        pT = [al("pT%d" % i, [128, 512], BF16) for i in range(3)]
        rden = al("rden", [64, 512])
        ot = [al("ot%d" % i, [64, 512], BF16) for i in range(2)]
        psa = es.enter_context(nc.psum_tensor("a_psa", [64, 512], F32))
        psr = es.enter_context(nc.psum_tensor("a_psr", [64, 512], F32))
        pss = [es.enter_context(nc.psum_tensor("a_pss%d" % i, [128, 512], F32)) for i in range(3)]
        pso = es.enter_context(nc.psum_tensor("a_pso", [64, 512], F32))
        psd = es.enter_context(nc.psum_tensor("a_psd", [64, 512], F32))
        P.dma(lambda e: e.dma_start(out=cos[:], in_=k.rope_cos), writes=["cos"])
        P.dma(lambda e: e.dma_start(out=sin[:], in_=k.rope_sin), writes=["sin"])
        P.dma(lambda e: e.dma_start(out=rm[:], in_=k.rope_rm), writes=["rm"])
        P.dma(lambda e: e.dma_start(out=gq[:], in_=k.at_qn_g[l]), writes=["gq"])
        P.dma(lambda e: e.dma_start(out=gk[:], in_=k.at_kn_g[l]), writes=["gk"])
        P.op("dve", lambda e: e.memset(o64[:], 1.0 / 64), writes=["o64"])
        P.op("dve", lambda e: e.memset(epsb[:], EPS), writes=["epsb"])
        P.op("dve", lambda e: e.memset(onesb[:], 1.0), writes=["onesb"])
        cnt = {"raw": 0, "p": 0, "o": 0}

        def prep(b, row0, g_t, gtok, dst, dtok):
            ri = cnt["raw"] % 2
            cnt["raw"] += 1
            P.dma(lambda e: e.dma_start(out=raw[ri][:], in_=k.PT[b, row0:row0 + 64, :]), reads=[("PT", b)], writes=[("raw", ri)])
            for (t0, n) in [(0, CTX)] + [(CTX + i * 512, 512) for i in range(4)]:
                P.op("act", lambda e, t0=t0, n=n: e.activation(out=sq[:, :n], in_=raw[ri][:, t0:t0 + n], func=AF.Square), reads=[("raw", ri)], writes=["sq"])
                P.op("pe", lambda e, n=n: e.matmul(psa[:, :n], lhsT=o64[:], rhs=sq[:, :n], start=True, stop=True), reads=["sq", "o64"], writes=["psa"])
                P.op("act", lambda e, n=n: e.activation(out=rs[:, :n], in_=psa[:, :n], func=AF.Sqrt, bias=epsb[:, 0:1], scale=1.0), reads=["psa", "epsb"], writes=["rs0", "rs"])
                P.op("dve", lambda e, n=n: e.reciprocal(out=rs[:, :n], in_=rs[:, :n]), reads=["rs0"], writes=["rs"])
                if t0 == 0:
                    P.op("dve", lambda e, t0=t0, n=n: e.scalar_tensor_tensor(out=dst[:, t0:t0 + n], in0=raw[ri][:, t0:t0 + n], scalar=g_t[:, 0:1], in1=rs[:, :n], op0=ALU.mult, op1=ALU.mult),
                         reads=[("raw", ri), "rs", gtok], writes=[dtok])
                    continue
                P.op("dve", lambda e, t0=t0, n=n: e.scalar_tensor_tensor(out=kn[:, :n], in0=raw[ri][:, t0:t0 + n], scalar=g_t[:, 0:1], in1=rs[:, :n], op0=ALU.mult, op1=ALU.mult),
                     reads=[("raw", ri), "rs", gtok], writes=["kn"])
                P.op("pe", lambda e, n=n: e.matmul(psr[:, :n], lhsT=rm[:], rhs=kn[:, :n], start=True, stop=True), reads=["kn", "rm"], writes=["psr"])
                P.op("pool", lambda e, t0=t0, n=n: e.tensor_tensor(out=t1[:, :n], in0=kn[:, :n], in1=cos[:, t0 - CTX:t0 - CTX + n], op=ALU.mult), reads=["kn", "cos"], writes=["t1"])
                P.op("dve", lambda e, t0=t0, n=n: e.tensor_tensor(out=t2[:, :n], in0=psr[:, :n], in1=sin[:, t0 - CTX:t0 - CTX + n], op=ALU.mult), reads=["psr", "sin"], writes=["t2"])
                P.op("dve", lambda e, t0=t0, n=n: e.tensor_tensor(out=dst[:, t0:t0 + n], in0=t1[:, :n], in1=t2[:, :n], op=ALU.add), reads=["t1", "t2"], writes=[dtok])

        def do_tile(b, hq, kvh, qi, q0, nq, kbs):
            def s_mm(kb, pi):
                P.op("pe", lambda e: e.matmul(pss[pi][:, :nq], lhsT=kT[kvh][:, kb * 128:(kb + 1) * 128], rhs=qT[qi][:, q0:q0 + nq], start=True, stop=True),
                     reads=[("kT", kvh), ("qT", qi)], writes=[("pss", pi)])
            pis = []
            for j in range(len(kbs)):
                pis.append(cnt["p"] % 3)
                cnt["p"] += 1
            s_mm(kbs[0], pis[0])
            for j, kb in enumerate(kbs):
                if j + 1 < len(kbs):
                    s_mm(kbs[j + 1], pis[j + 1])
                pi = pis[j]
                P.op("act", lambda e, pi=pi: e.activation(out=pT[pi][:, :nq], in_=pss[pi][:, :nq], func=AF.Exp, scale=0.125), reads=[("pss", pi)], writes=[("pT", pi)])
                P.op("pe", lambda e, pi=pi, kb=kb, j=j: e.matmul(pso[:, :nq], lhsT=vb[:, kb, kvh * 64:(kvh + 1) * 64], rhs=pT[pi][:, :nq], start=(j == 0), stop=(j == len(kbs) - 1)),
                     reads=[("pT", pi), "vb"], writes=["pso"], inc=False)
                P.op("pe", lambda e, pi=pi, j=j: e.matmul(psd[:, :nq], lhsT=onesb[:], rhs=pT[pi][:, :nq], start=(j == 0), stop=(j == len(kbs) - 1)),
                     reads=[("pT", pi), "onesb"], writes=["psd"])
            oi = cnt["o"] % 2
            cnt["o"] += 1
            P.op("dve", lambda e: e.reciprocal(out=rden[:, :nq], in_=psd[:, :nq]), reads=["psd"], writes=["rden"])
            P.op("dve", lambda e: e.tensor_tensor(out=ot[oi][:, :nq], in0=pso[:, :nq], in1=rden[:, :nq], op=ALU.mult), reads=["pso", "rden"], writes=[("ot", oi)])
            P.dma(lambda e: e.dma_start(out=k.YT[b, 3, hq * 64:(hq + 1) * 64, q0:q0 + nq], in_=ot[oi][:, :nq]), reads=[("ot", oi)], writes=[("YT", b)])

        for b in range(NB):
            for kvh in range(2):
                prep(b, FM_AK + kvh * 64, gk, "gk", kT[kvh], ("kT", kvh))
            vsrc = k.TOK[b].rearrange("(blk p) c -> p blk c", p=128)[:, :, TK_AV:TK_AV + 128]
            P.dma(lambda e, vsrc=vsrc: e.dma_start(out=vraw[:], in_=vsrc), reads=[("TOK", b)], writes=["vraw"])
            P.op("pool", lambda e: e.tensor_copy(out=vb[:], in_=vraw[:]), reads=["vraw"], writes=["vb"])
            for hq in range(4):
                kvh = hq // 2
                qi = hq % 2
                prep(b, FM_AQ + hq * 64, gq, "gq", qT[qi], ("qT", qi))
                for (q0, nq, kbs) in [(0, CTX, [0, 1])] + [(CTX + i * 512, 512, list(range(NKB))) for i in range(4)]:
                    do_tile(b, hq, kvh, qi, q0, nq, kbs)
        P.end_stage()


def stage_hgrn2(k, l):
    nc, P, NB = k.nc, k.P, k.NB
    NCK = T // 64
    with ExitStack() as es:
        al = lambda name, shape, dt=F32: es.enter_context(nc.sbuf_tensor("h_" + name, list(shape), dt))
        m01 = al("m01", [128, T]); mk = al("mk", [128, 2, 64]); bd = al("bd", [128, 128])
        lg = al("lg", [128, 2, 2]); lb = al("lb", [128, 2]); oml = al("oml", [128, 2]); gn = al("gn", [128, 1]); epsb = al("eps", [128, 1])
        z = al("z", [128, T]); fgt = al("fgt", [128, T]); bb = al("bb", [128, T]); tmp = al("tmp", [128, T])
        ex = [al("ex%d" % i, [128, T]) for i in range(2)]
        q = al("q", [128, T]); kk = al("kk", [128, T]); kd = al("kd", [128, T]); O = al("O", [128, T])
        qt = al("qt", [128, T], BF16); ktI = [al("kt%d" % i, [128, T], BF16) for i in range(4)]; qd = al("qd", [128, T], BF16)
        dec = al("dec", [128, NCK, 1])
        Vb = al("Vb", [128, NCK, 256], BF16)
        kdT = [al("kdT%d" % i, [64, 128], BF16) for i in range(2)]
        scm = [al("scm%d" % i, [128, 64], BF16) for i in range(2)]
        S32 = al("S32", [128, 64]); S16 = [al("S16%d" % i, [128, 64], BF16) for i in range(2)]
        sq = al("sq", [128, 512]); rs = al("rs", [128, 512]); yb = [al("yb%d" % i, [128, 512], BF16) for i in range(2)]
        pst = [es.enter_context(nc.psum_tensor("h_pst%d" % i, [128, 512], F32)) for i in range(2)]
        pss = [es.enter_context(nc.psum_tensor("h_pss%d" % i, [128, 512], F32)) for i in range(2)]
        pso = [es.enter_context(nc.psum_tensor("h_pso%d" % i, [128, 512], F32)) for i in range(2)]
        pskv = [es.enter_context(nc.psum_tensor("h_pskv%d" % i, [128, 512], F32)) for i in range(2)]
        P.dma(lambda e: e.dma_start(out=mk[:], in_=k.hg_masks), writes=["mk"])
        P.dma(lambda e: e.dma_start(out=bd[:], in_=k.bd64), writes=["bd"])
        P.dma(lambda e: e.dma_start(out=lg[:], in_=k.hg_lb), writes=["lg"])
        P.dma(lambda e: e.dma_start(out=gn[:], in_=k.hg_norm_g[l]), writes=["gn"])
        P.op("dve", lambda e: e.memset(epsb[:], EPS), writes=["epsb"])
        P.op("dve", lambda e: e.memset(m01[:], 1.0), writes=["m01"])
        P.op("dve", lambda e: e.memset(m01[:].rearrange("p (c j) -> p c j", j=64)[:, :, 0:1], 0.0), writes=["m01"])
        for i4 in range(4):
            P.op("pool", lambda e, i4=i4: e.memset(ktI[i4][:], 0.0), writes=[("kt", i4)])
        if l == 0:
            P.op("dve", lambda e: e.memset(lb[:], 0.0), writes=["lb"])
            P.op("dve", lambda e: e.memset(oml[:], 1.0), writes=["oml"])
        else:
            P.op("dve", lambda e: e.tensor_tensor(out=lb[:], in0=lg[:, :, 1], in1=lg[:, :, 0], op=ALU.subtract), reads=["lg"], writes=["lb0"])
            P.op("act", lambda e: e.activation(out=lb[:], in_=lb[:], func=AF.Sigmoid), reads=["lb0"], writes=["lb", "lb0"])
            P.op("dve", lambda e: e.tensor_scalar(out=oml[:], in0=lb[:], scalar1=-1.0, scalar2=1.0, op0=ALU.mult, op1=ALU.add), reads=["lb"], writes=["oml"])
        cnt = {"c": 0, "y": 0}
        bb3 = bb[:].rearrange("p (c j) -> p c j", j=64)
        tmp3 = tmp[:].rearrange("p (c j) -> p c j", j=64)

        def do_chunk(b, hp, d, c, first):
            i = cnt["c"] % 2
            cnt["c"] += 1
            cs = slice(c * 64, (c + 1) * 64)
            P.op("pe", lambda e: e.transpose(pst[i][:64, :128], kd[:, cs], k.idn[:]), reads=["kd", "idn"], writes=[("pst", i)])
            P.op("act", lambda e: e.activation(out=kdT[i][:], in_=pst[i][:64, :128], func=AF.Identity), reads=[("pst", i)], writes=[("kdT", i)])
            for h2 in range(2):
                pb = h2 * 64
                for I in range(4):
                    ts_ = slice(c * 64 + 16 * I, c * 64 + 16 * I + 16)
                    P.op("pe", lambda e, pb=pb, I=I, ts_=ts_: e.matmul(pss[i][pb:pb + 64, 16 * I:16 * I + 16], lhsT=ktI[I][pb:pb + 64, cs], rhs=qt[pb:pb + 64, ts_], start=True, stop=True),
                         reads=[("kt", I), "qt"], writes=[("pss", i)], inc=(h2 == 1 and I == 3))
            for h2 in range(2):
                pb = h2 * 64
                h = hp * 2 + h2
                P.op("pe", lambda e, pb=pb, h=h: e.matmul(pskv[i][pb:pb + 64, :64], lhsT=kdT[i][:, pb:pb + 64], rhs=Vb[0:64, c, h * 64:(h + 1) * 64], start=True, stop=True),
                     reads=[("kdT", i), "Vb"], writes=[("pskv", i)], inc=(h2 == 1))
            P.op("dve", lambda e: e.tensor_tensor(out=scm[i][:], in0=pss[i][:, :64], in1=mk[:, d, :], op=ALU.mult), reads=[("pss", i), "mk"], writes=[("scm", i)])
            for h2 in range(2):
                pb = h2 * 64
                h = hp * 2 + h2
                P.op("pe", lambda e, pb=pb, h=h: e.matmul(pso[i][pb:pb + 64, :64], lhsT=Vb[pb:pb + 64, c, h * 64:(h + 1) * 64], rhs=scm[i][pb:pb + 64, :], start=True, stop=False),
                     reads=[("scm", i), "Vb"], writes=[("pso", i)], inc=False)
                P.op("pe", lambda e, pb=pb: e.matmul(pso[i][pb:pb + 64, :64], lhsT=S16[1 - i][pb:pb + 64, :], rhs=qd[pb:pb + 64, cs], start=False, stop=True),
                     reads=[("S16", 1 - i), "qd"], writes=[("pso", i)], inc=(h2 == 1))
            if d == 0:
                P.op("act", lambda e: e.activation(out=O[:, cs], in_=pso[i][:, :64], func=AF.Identity), reads=[("pso", i)], writes=["O"])
            else:
                P.op("dve", lambda e: e.tensor_tensor(out=O[:, cs], in0=O[:, cs], in1=pso[i][:, :64], op=ALU.add), reads=[("pso", i), "O"], writes=["O"])
            P.op("dve", lambda e: e.scalar_tensor_tensor(out=S32[:], in0=S32[:], scalar=dec[:, c, :], in1=pskv[i][:, :64], op0=ALU.mult, op1=ALU.add),
                 reads=["S32", "dec", ("pskv", i)], writes=["S32"])
            P.op("act", lambda e: e.activation(out=S16[i][:], in_=S32[:], func=AF.Identity), reads=["S32"], writes=[("S16", i)])

        def do_dir(b, hp, d):
            rev = (lambda ap: ap[:, ::-1]) if d == 1 else (lambda ap: ap)
            last = 63 if d == 0 else 0
            r0 = FM_HF + d * 256 + hp * 128
            P.dma(lambda e: e.dma_start(out=z[:], in_=k.PT[b, r0:r0 + 128, :]), reads=[("PT", b)], writes=["z"])
            P.op("act", lambda e: e.activation(out=fgt[:], in_=z[:], func=AF.Sigmoid), reads=["z"], writes=["fgt"])
            P.op("dve", lambda e: e.tensor_scalar(out=fgt[:], in0=fgt[:], scalar1=oml[:, hp:hp + 1], scalar2=lb[:, hp:hp + 1], op0=ALU.mult, op1=ALU.add), reads=["fgt", "oml", "lb"], writes=["fgt"])
            P.op("dve", lambda e: e.tensor_scalar(out=fgt[:], in0=fgt[:], scalar1=1e-30, scalar2=None, op0=ALU.max), reads=["fgt"], writes=["fgt"])
            P.op("act", lambda e: e.activation(out=z[:], in_=fgt[:], func=AF.Ln), reads=["fgt"], writes=["z"])
            P.op("dve", lambda e: e.tensor_scalar(out=kk[:], in0=fgt[:], scalar1=-1.0, scalar2=1.0, op0=ALU.mult, op1=ALU.add), reads=["fgt"], writes=["kk"])
            P.op("dve", lambda e: e.tensor_tensor_scan(out=rev(bb[:]), data0=m01[:], data1=rev(z[:]), initial=0.0, op0=ALU.mult, op1=ALU.add), reads=["z", "m01"], writes=["bb"])
            ref = 0 if d == 0 else 15
            bb4 = bb[:].rearrange("p (c i j) -> p c i j", i=4, j=16)
            tmp4 = tmp[:].rearrange("p (c i j) -> p c i j", i=4, j=16)
            P.op("dve", lambda e: e.tensor_tensor(out=tmp4, in0=bb4, in1=bb4[:, :, :, ref:ref + 1].to_broadcast([128, NCK, 4, 16]), op=ALU.subtract), reads=["bb"], writes=["tmp"])
            P.op("act", lambda e: e.activation(out=ex[0][:], in_=tmp[:], func=AF.Exp), reads=["tmp"], writes=[("ex", 0)])
            P.op("pool", lambda e: e.tensor_tensor(out=qt[:], in0=q[:], in1=ex[0][:], op=ALU.mult), reads=["q", ("ex", 0)], writes=["qt"])
            ex3 = [ex[i][:].rearrange("p (c j) -> p c j", j=64) for i in range(2)]
            kk3 = kk[:].rearrange("p (c j) -> p c j", j=64)
            for I in range(4):
                cs_ = slice(0, 16 * (I + 1)) if d == 0 else slice(16 * I, 64)
                w = cs_.stop - cs_.start
                e_ = ex3[I % 2][:, :, cs_]
                rp = 16 * I + ref
                kt3 = ktI[I][:].rearrange("p (c j) -> p c j", j=64)[:, :, cs_]
                P.op("dve", lambda e, cs_=cs_, w=w, rp=rp: e.scalar_tensor_tensor(out=tmp3[:, :, cs_], in0=bb3[:, :, cs_], scalar=-1.0, in1=bb3[:, :, rp:rp + 1].to_broadcast([128, NCK, w]), op0=ALU.mult, op1=ALU.add),
                     reads=["bb"], writes=["tmp"])
                P.op("dve", lambda e, cs_=cs_: e.tensor_scalar(out=tmp3[:, :, cs_], in0=tmp3[:, :, cs_], scalar1=60.0, scalar2=None, op0=ALU.min), reads=["tmp"], writes=["tmp"])
                P.op("act", lambda e, cs_=cs_, e_=e_: e.activation(out=e_, in_=tmp3[:, :, cs_], func=AF.Exp), reads=["tmp"], writes=[("ex", I % 2)])
                P.op("pool", lambda e, cs_=cs_, e_=e_, kt3=kt3: e.tensor_tensor(out=kt3, in0=kk3[:, :, cs_], in1=e_, op=ALU.mult), reads=["kk", ("ex", I % 2)], writes=[("kt", I)])
            P.op("act", lambda e: e.activation(out=ex[0][:], in_=bb[:], func=AF.Exp), reads=["bb"], writes=[("ex", 0)])
            P.op("dve", lambda e: e.tensor_tensor(out=qd[:], in0=q[:], in1=ex[0][:], op=ALU.mult), reads=["q", ("ex", 0)], writes=["qd"])
            P.op("dve", lambda e: e.tensor_tensor(out=tmp3, in0=bb3, in1=bb3[:, :, last:last + 1].to_broadcast([128, NCK, 64]), op=ALU.subtract), reads=["bb"], writes=["tmp"])
            P.op("act", lambda e: e.activation(out=ex[1][:], in_=tmp[:], func=AF.Exp, scale=-1.0), reads=["tmp"], writes=[("ex", 1)])
            P.op("pool", lambda e: e.tensor_tensor(out=kd[:], in0=kk[:], in1=ex[1][:], op=ALU.mult), reads=["kk", ("ex", 1)], writes=["kd"])
            P.op("act", lambda e: e.activation(out=dec[:], in_=bb3[:, :, last:last + 1], func=AF.Exp), reads=["bb"], writes=["dec"])
            if "dump" in k.opts and (b, hp, d) == k.opts["dump"]:
                for j, (tl, tk) in enumerate([(z, "z"), (bb, "bb"), (kd, "kd"), (kk, "kk"), (fgt, "fgt")]):
                    P.dma(lambda e, j=j, tl=tl: e.dma_start(out=k.dump[j], in_=tl[:]), reads=[tk], writes=[("dump", j)])
            P.op("dve", lambda e: e.memset(S32[:], 0.0), writes=["S32"])
            for i in range(2):
                P.op("dve", lambda e, i=i: e.memset(S16[i][:], 0.0), writes=[("S16", i)])
            order = list(range(NCK)) if d == 0 else [3, 2, 1, 0] + list(range(NCK - 1, 3, -1))
            for idx, c in enumerate(order):
                do_chunk(b, hp, d, c, idx == 0)

        def do_pair(b, hp):
            r0 = FM_HQ + hp * 128
            P.dma(lambda e: e.dma_start(out=q[:], in_=k.PT[b, r0:r0 + 128, :]), reads=[("PT", b)], writes=["q"])
            P.op("act", lambda e: e.activation(out=q[:], in_=q[:], func=AF.Silu), reads=["q"], writes=["q"])
            for d in range(2):
                do_dir(b, hp, d)
            if "dump" in k.opts and (b, hp) == k.opts["dump"][:2]:
                P.dma(lambda e: e.dma_start(out=k.dump[5], in_=O[:]), reads=["O"], writes=[("dump", 5)])
            g0 = FM_HG + hp * 128
            P.dma(lambda e: e.dma_start(out=z[:], in_=k.PT[b, g0:g0 + 128, :]), reads=[("PT", b)], writes=["z"])
            P.op("act", lambda e: e.activation(out=z[:], in_=z[:], func=AF.Sigmoid), reads=["z"], writes=["z"])
            for (t0, n) in [(0, CTX)] + [(CTX + i * 512, 512) for i in range(4)]:
                do_out(b, hp, t0, n)

        def do_out(b, hp, t0, n):
            yi = cnt["y"] % 2
            cnt["y"] += 1
            P.op("act", lambda e: e.activation(out=sq[:, :n], in_=O[:, t0:t0 + n], func=AF.Square), reads=["O"], writes=["sq"])
            P.op("pe", lambda e: e.matmul(pss[0][:, :n], lhsT=bd[:], rhs=sq[:, :n], start=True, stop=True), reads=["sq", "bd"], writes=[("pss", 0)])
            P.op("act", lambda e: e.activation(out=rs[:, :n], in_=pss[0][:, :n], func=AF.Sqrt, bias=epsb[:, 0:1], scale=1.0), reads=[("pss", 0), "epsb"], writes=["rs0", "rs"])
            P.op("dve", lambda e: e.reciprocal(out=rs[:, :n], in_=rs[:, :n]), reads=["rs0"], writes=["rs"])
            P.op("dve", lambda e: e.scalar_tensor_tensor(out=sq[:, :n], in0=O[:, t0:t0 + n], scalar=gn[:, 0:1], in1=rs[:, :n], op0=ALU.mult, op1=ALU.mult), reads=["O", "gn", "rs"], writes=["sq"])
            P.op("pool", lambda e: e.tensor_tensor(out=yb[yi][:, :n], in0=sq[:, :n], in1=z[:, t0:t0 + n], op=ALU.mult), reads=["sq", "z"], writes=[("yb", yi)])
            P.dma(lambda e: e.dma_start(out=k.YT[b, 2, hp * 128:(hp + 1) * 128, t0:t0 + n], in_=yb[yi][:, :n]), reads=[("yb", yi)], writes=[("YT", b)])

        for b in range(NB):
            vsrc = k.TOK[b].rearrange("(c s) v -> s c v", s=64)[:, :, TK_HV:TK_HV + 256]
            for h2 in range(2):
                P.dma(lambda e, h2=h2, vsrc=vsrc: e.dma_start(out=Vb[h2 * 64:(h2 + 1) * 64, :, :], in_=vsrc), reads=[("TOK", b)], writes=["Vb"], q="pool")
            for hp in range(2):
                do_pair(b, hp)
        P.end_stage()


class Slots:
    def __init__(self, aps, name, mod=0, off=0):
        self.aps, self.name, self.i, self.mod, self.off = aps, name, 0, mod, off

    def get(self):
        j = self.i % len(self.aps)
        self.i += 1
        return self.aps[j], (self.name, self.off + (j % self.mod if self.mod else j))


def stage_deltanet(k, l):
    nc, P, NB = k.nc, k.P, k.NB
    NCK = T // 64
    P.serial = k.opts.get("serial_dn", SERIAL_DN)
    with ExitStack() as es:
        al = lambda name, shape, dt=F32: es.enter_context(nc.sbuf_tensor("d_" + name, list(shape), dt))
        gm64 = al("gm64", [64, 2, 2, 128]); gmbd = al("gmbd", [128, 2, 2, 128]); idn2 = al("idn2", [128, 64]); bd = al("bd", [128, 128])
        ones64 = al("ones64", [64, 128])
        cw = al("cw", [128, 6, 5]); alog = al("alog", [128, 8]); dtb = al("dtb", [128, 8]); gno = al("gno", [128, 64]); epsb = al("eps", [128, 1])
        qkv = [al("qkv%d" % i, [128, T]) for i in range(6)]
        xin = al("xin", [128, T]); acc = al("acc", [128, T])
        sq = al("sq", [128, 512]); rs = al("rs", [128, 512])
        ab = al("ab", [128, NCK, 16]); gt = al("gt", [128, NCK, 8]); bt = al("bt", [128, NCK, 8]); t8 = al("t8", [128, NCK, 8]); t8b = al("t8b", [128, NCK, 8])
        gcs = al("gcs", [128, NCK, 8]); gts = al("gts", [128, NCK, 8])
        gcBD = al("gcBD", [128, NCK, 4]); gtBD = al("gtBD", [128, NCK, 4]); bBD = al("bBD", [128, NCK, 4])
        eg = al("eg", [128, NCK, 4]); ekd = al("ekd", [128, NCK, 4]); glast = al("glast", [128, NCK, 4]); nbeta = al("nbeta", [128, NCK, 4]); wsc = al("wsc", [128, NCK, 4])
        u_all = al("u_all", [128, NCK, 64]); wT_all = al("wT_all", [128, NCK, 64]); kdec_all = al("kdec_all", [128, NCK, 64]); attnT_all = al("attnT_all", [128, NCK, 128])
        O_tok = al("O_tok", [128, NCK, 64]); ssq = al("ssq", [128, NCK]); S = al("S", [128, 64])
        yb = [al("yb%d" % i, [128, 512], BF16) for i in range(2)]
        wk = Slots([al("wk%d" % i, [128, 128])[:] for i in range(28)], "wk")
        wv = Slots([al("wv%d" % i, [128, 64])[:] for i in range(8)], "wv")
        wr = Slots([al("wr%d" % i, [128, 128])[:] for i in range(6)], "wr")
        banks = [es.enter_context(nc.psum_tensor("d_ps%d" % i, [128, 512], F32)) for i in range(8)]
        ps = Slots([banks[i][:, j * 128:(j + 1) * 128] for j in range(4) for i in range(4)], "ps", mod=4)
        ps2 = Slots([banks[4 + i][:, j * 256:(j + 1) * 256] for j in range(2) for i in range(3)], "ps", mod=3, off=4)
        wk2 = Slots([al("wkp%d" % i, [128, 256])[:] for i in range(10)], "wk2")
        psg = banks[7]
        for i in range(8):
            P.op("dve", lambda e, i=i: e.memset(banks[i][:], 0.0), writes=[("ps", j) for j in range(7)] + ["psg"])
        P.dma(lambda e: e.dma_start(out=gm64[:], in_=k.dn_gm64), writes=["gm64"])
        P.dma(lambda e: e.dma_start(out=gmbd[:], in_=k.dn_gmbd), writes=["gmbd"])
        P.dma(lambda e: e.dma_start(out=idn2[:], in_=k.dn_idn2), writes=["idn2"])
        P.dma(lambda e: e.dma_start(out=bd[:], in_=k.bd64), writes=["bd"])
        P.dma(lambda e: e.dma_start(out=cw[:], in_=k.dn_conv[l]), writes=["cw"])
        P.dma(lambda e: e.dma_start(out=alog[:], in_=k.dn_a_log[l]), writes=["alog"])
        P.dma(lambda e: e.dma_start(out=dtb[:], in_=k.dn_dt_bias[l]), writes=["dtb"])
        P.dma(lambda e: e.dma_start(out=gno[:], in_=k.dn_norm_g[l]), writes=["gno"])
        P.op("dve", lambda e: e.memset(epsb[:], EPS), writes=["epsb"])
        P.op("dve", lambda e: e.memset(ones64[:], 1.0), writes=["ones64"])
        P.op("act", lambda e: e.activation(out=alog[:], in_=alog[:], func=AF.Exp), reads=["alog"], writes=["alog"])
        P.op("dve", lambda e: e.tensor_scalar(out=alog[:], in0=alog[:], scalar1=-1.0, scalar2=None, op0=ALU.mult), reads=["alog"], writes=["alog"])
        cnt = {"y": 0}

        def prep_tile(b, ti):
            r0 = ti * 128
            P.dma(lambda e: e.dma_start(out=xin[:], in_=k.PT[b, r0:r0 + 128, :]), reads=[("PT", b)], writes=["xin"])
            P.op("dve", lambda e: e.tensor_scalar(out=acc[:], in0=xin[:], scalar1=cw[:, ti, 2:3], scalar2=None, op0=ALU.mult), reads=["xin", "cw"], writes=["acc"])
            for (s0, s1) in [(0, CTX), (CTX, T)]:
                for j in (0, 1, 3, 4):
                    sh = j - 2
                    o0, o1 = max(s0, s0 - sh), min(s1, s1 - sh)
                    P.op("dve", lambda e, j=j, sh=sh, o0=o0, o1=o1: e.scalar_tensor_tensor(out=acc[:, o0:o1], in0=xin[:, o0 + sh:o1 + sh], scalar=cw[:, ti, j:j + 1], in1=acc[:, o0:o1], op0=ALU.mult, op1=ALU.add),
                         reads=["xin", "cw", "acc"], writes=["acc"])
            dst = qkv[ti]
            if ti >= 4:
                P.op("act", lambda e: e.activation(out=dst[:], in_=acc[:], func=AF.Silu), reads=["acc"], writes=[("qkv", ti)])
                return
            P.op("act", lambda e: e.activation(out=acc[:], in_=acc[:], func=AF.Silu), reads=["acc"], writes=["acc"])
            qs = 0.125 if ti < 2 else 1.0
            for (t0, n) in [(0, CTX)] + [(CTX + i * 512, 512) for i in range(4)]:
                norm_tile(dst, ti, t0, n, qs)

        def norm_tile(dst, ti, t0, n, qs):
            pp, pt = ps.get()
            bank_ap = banks[0]
            P.op("act", lambda e: e.activation(out=sq[:, :n], in_=acc[:, t0:t0 + n], func=AF.Square), reads=["acc"], writes=["sq"])
            P.op("pe", lambda e: e.matmul(psg[:, :n], lhsT=bd[:], rhs=sq[:, :n], start=True, stop=True), reads=["sq", "bd"], writes=["psg"])
            P.op("act", lambda e: e.activation(out=rs[:, :n], in_=psg[:, :n], func=AF.Sqrt, bias=epsb[:, 0:1], scale=64.0), reads=["psg", "epsb"], writes=["rs0", "rs"])
            P.op("dve", lambda e: e.reciprocal(out=rs[:, :n], in_=rs[:, :n]), reads=["rs0"], writes=["rs"])
            P.op("dve", lambda e: e.scalar_tensor_tensor(out=dst[:, t0:t0 + n], in0=acc[:, t0:t0 + n], scalar=qs, in1=rs[:, :n], op0=ALU.mult, op1=ALU.mult), reads=["acc", "rs"], writes=[("qkv", ti)])

        def prep_gates(b):
            src = k.TOK[b].rearrange("(c s) v -> s c v", s=64)[:, :, TK_A:TK_A + 16]
            for h2 in range(2):
                P.dma(lambda e, h2=h2: e.dma_start(out=ab[h2 * 64:(h2 + 1) * 64, :, :], in_=src), reads=[("TOK", b)], writes=["ab"])
            a3, b3 = ab[:, :, 0:8], ab[:, :, 8:16]
            bc8 = lambda t: t[:, :].rearrange("p (o h) -> p o h", o=1).to_broadcast([128, NCK, 8])
            P.op("dve", lambda e: e.tensor_tensor(out=t8[:], in0=a3, in1=bc8(dtb), op=ALU.add), reads=["ab", "dtb"], writes=["t8"])
            P.op("act", lambda e: e.activation(out=t8b[:], in_=t8[:], func=AF.Abs), reads=["t8"], writes=["t8b"])
            P.op("act", lambda e: e.activation(out=t8b[:], in_=t8b[:], func=AF.Exp, scale=-1.0), reads=["t8b"], writes=["t8b"])
            P.op("dve", lambda e: e.tensor_scalar(out=t8b[:], in0=t8b[:], scalar1=1.0, scalar2=None, op0=ALU.add), reads=["t8b"], writes=["t8b"])
            P.op("act", lambda e: e.activation(out=t8b[:], in_=t8b[:], func=AF.Ln), reads=["t8b"], writes=["t8b"])
            P.op("dve", lambda e: e.tensor_scalar(out=t8[:], in0=t8[:], scalar1=0.0, scalar2=None, op0=ALU.max), reads=["t8"], writes=["t8"])
            P.op("dve", lambda e: e.tensor_tensor(out=t8[:], in0=t8[:], in1=t8b[:], op=ALU.add), reads=["t8", "t8b"], writes=["t8"])
            P.op("dve", lambda e: e.tensor_tensor(out=gt[:], in0=t8[:], in1=bc8(alog), op=ALU.mult), reads=["t8", "alog"], writes=["gt"])
            P.op("act", lambda e: e.activation(out=bt[:], in_=b3, func=AF.Sigmoid), reads=["ab"], writes=["bt"])
            for d in range(2):
                P.op("pe", lambda e, d=d: e.matmul(psg[:, d * 144:(d + 1) * 144], lhsT=gm64[:, d, 0, :], rhs=gt[0:64, :, d * 4:(d + 1) * 4], start=True, stop=True), reads=["gm64", "gt"], writes=["psg"])
            P.op("dve", lambda e: e.tensor_copy(out=gcs[:].rearrange("p c (d h) -> p d c h", d=2), in_=psg[:, 0:288].rearrange("p (d c h) -> p d c h", d=2, h=4)), reads=["psg"], writes=["gcs"])
            for d in range(2):
                P.op("pe", lambda e, d=d: e.matmul(psg[:, d * 144:(d + 1) * 144], lhsT=ones64[:], rhs=gt[0:64, :, d * 4:(d + 1) * 4], start=True, stop=True), reads=["ones64", "gt"], writes=["psg"])
            P.op("dve", lambda e: e.tensor_copy(out=gts[:].rearrange("p c (d h) -> p d c h", d=2), in_=psg[:, 0:288].rearrange("p (d c h) -> p d c h", d=2, h=4)), reads=["psg"], writes=["gts"])
            for m in range(4):
                d, hp = m // 2, m % 2
                for h2 in range(2):
                    col = d * 4 + hp * 2 + h2
                    rr = slice(h2 * 64, (h2 + 1) * 64)
                    P.op("dve", lambda e, m=m, col=col, rr=rr: e.tensor_copy(out=gcBD[rr, :, m:m + 1], in_=gcs[rr, :, col:col + 1]), reads=["gcs"], writes=["gcBD"])
                    P.op("dve", lambda e, m=m, col=col, rr=rr: e.tensor_copy(out=gtBD[rr, :, m:m + 1], in_=gts[rr, :, col:col + 1]), reads=["gts"], writes=["gtBD"])
                    P.op("dve", lambda e, m=m, col=col, rr=rr: e.tensor_copy(out=bBD[rr, :, m:m + 1], in_=bt[rr, :, col:col + 1]), reads=["bt"], writes=["bBD"])
            P.op("act", lambda e: e.activation(out=eg[:], in_=gcBD[:], func=AF.Exp), reads=["gcBD"], writes=["eg"])
            P.op("act", lambda e: e.activation(out=glast[:], in_=gtBD[:], func=AF.Exp), reads=["gtBD"], writes=["glast"])
            P.op("dve", lambda e: e.tensor_tensor(out=ekd[:], in0=gtBD[:], in1=gcBD[:], op=ALU.subtract), reads=["gtBD", "gcBD"], writes=["ekd"])
            P.op("act", lambda e: e.activation(out=ekd[:], in_=ekd[:], func=AF.Exp), reads=["ekd"], writes=["ekd"])
            P.op("dve", lambda e: e.tensor_scalar(out=nbeta[:], in0=bBD[:], scalar1=-1.0, scalar2=None, op0=ALU.mult), reads=["bBD"], writes=["nbeta"])
            P.op("dve", lambda e: e.tensor_tensor(out=wsc[:], in0=bBD[:], in1=eg[:], op=ALU.mult), reads=["bBD", "eg"], writes=["wsc"])

        def phase_a_steps(m, c):
            d, hp = m // 2, m % 2
            qn, kn, vn = qkv[hp], qkv[2 + hp], qkv[4 + hp]
            qtk, ktk, vtk = ("qkv", hp), ("qkv", 2 + hp), ("qkv", 4 + hp)
            cs = slice(c * 64, (c + 1) * 64)
            hd0 = d * 4 + hp * 2
            X = {}

            def s1a():
                G2, G2t = wk2.get()
                Gmt = Git = G2t
                Gi, Gm = G2[:, 0:128], G2[:, 128:256]
                g4 = gt[0:64, c, hd0:hd0 + 2].rearrange("p (a h o) -> p a h o", a=1, o=1).to_broadcast([64, 2, 2, 64])
                P.op("dve", lambda e: e.tensor_tensor(out=G2[0:64, :].rearrange("p (a h j) -> p a h j", a=2, h=2), in0=g4, in1=gm64[:, d, :, :].rearrange("p a (h j) -> p a h j", h=2), op=ALU.mult), reads=["gt", "gm64"], writes=[G2t])
                X.update(G2=G2, G2t=G2t)

            def s1():
                G2, G2t = X["G2"], X["G2t"]
                Gmt = Git = G2t
                Gi, Gm = G2[:, 0:128], G2[:, 128:256]
                pDD, pDt = ps2.get()
                pDTt = pDt
                pD, pDT = pDD[:, 0:128], pDD[:, 128:256]
                P.op("pe", lambda e: e.matmul(pD, lhsT=gm64[:, d, 0, :], rhs=Gm[0:64, :], start=True, stop=True), reads=["gm64", Gmt], writes=[pDt], inc=False)
                P.op("pe", lambda e: e.matmul(pDT, lhsT=gm64[:, d, 1, :], rhs=Gi[0:64, :], start=True, stop=True), reads=["gm64", Git], writes=[pDTt])
                pKK, pKKt = ps.get()
                pQK, pQKt = ps.get()
                pTok, pTokt = ps.get()
                for h2 in range(2):
                    pb = h2 * 64
                    P.op("pe", lambda e, pb=pb: e.matmul(pKK[pb:pb + 64, pb:pb + 64], lhsT=kn[pb:pb + 64, cs], rhs=kn[pb:pb + 64, cs], start=True, stop=True), reads=[ktk], writes=[pKKt], inc=False)
                    P.op("pe", lambda e, pb=pb: e.matmul(pQK[pb:pb + 64, pb:pb + 64], lhsT=kn[pb:pb + 64, cs], rhs=qn[pb:pb + 64, cs], start=True, stop=True), reads=[ktk, qtk], writes=[pQKt], inc=False)
                    P.op("pe", lambda e, pb=pb: e.matmul(pTok[pb:pb + 64, 0:64], lhsT=kn[pb:pb + 64, cs], rhs=idn2[pb:pb + 64, :], start=True, stop=True), reads=[ktk, "idn2"], writes=[pTokt], inc=False)
                    P.op("pe", lambda e, pb=pb: e.matmul(pTok[pb:pb + 64, 64:128], lhsT=vn[pb:pb + 64, cs], rhs=idn2[pb:pb + 64, :], start=True, stop=True), reads=[vtk, "idn2"], writes=[pTokt], inc=(h2 == 1))
                X.update(pDD=pDD, pD=pD, pDt=pDt, pDT=pDT, pDTt=pDTt, pKK=pKK, pKKt=pKKt, pQK=pQK, pQKt=pQKt, pTok=pTok, pTokt=pTokt)

            def s2a():
                x = dict(X)
                DD, Dt = wk2.get()
                P.op("act", lambda e: e.activation(out=DD, in_=x["pDD"], func=AF.Exp), reads=[x["pDt"]], writes=[Dt])
                X.update(DD=DD, DDt=Dt)

            def s2():
                x = dict(X)
                DD, Dt = x["DD"], x["DDt"]
                DTt = Dt
                D, DT = DD[:, 0:128], DD[:, 128:256]
                P.op("dve", lambda e: e.tensor_tensor(out=DD.rearrange("p (a j) -> p a j", a=2), in0=DD.rearrange("p (a j) -> p a j", a=2), in1=gmbd[:, d, :, :], op=ALU.mult), reads=[Dt, "gmbd"], writes=[Dt])
                N, Nt = wk.get()
                X["n"] = X.get("n", 0) + 1
                if X["n"] <= k.opts.get("s2n", 99):
                    P.op("dve", lambda e: e.scalar_tensor_tensor(out=N, in0=x["pKK"], scalar=nbeta[:, c, m:m + 1], in1=D, op0=ALU.mult, op1=ALU.mult), reads=[x["pKKt"], "nbeta", Dt], writes=[Nt])
                X["n"] = X.get("n", 0) + 1
                if X["n"] <= k.opts.get("s2n", 99):
                    P.op("dve", lambda e: e.tensor_tensor(out=attnT_all[:, c, :], in0=x["pQK"], in1=DT, op=ALU.mult), reads=[x["pQKt"], DTt], writes=[("attnT", c)])
                rhs, rhst = wr.get()
                X["n"] = X.get("n", 0) + 1
                if X["n"] <= k.opts.get("s2n", 99):
                    P.op("dve", lambda e: e.tensor_scalar(out=rhs[:, 0:64], in0=x["pTok"][:, 0:64], scalar1=wsc[:, c, m:m + 1], scalar2=None, op0=ALU.mult), reads=[x["pTokt"], "wsc"], writes=[rhst])
                X["n"] = X.get("n", 0) + 1
                if X["n"] <= k.opts.get("s2n", 99):
                    P.op("dve", lambda e: e.tensor_scalar(out=rhs[:, 64:128], in0=x["pTok"][:, 64:128], scalar1=bBD[:, c, m:m + 1], scalar2=None, op0=ALU.mult), reads=[x["pTokt"], "bBD"], writes=[rhst])
                X["n"] = X.get("n", 0) + 1
                if X["n"] <= k.opts.get("s2n", 99):
                    P.op("dve", lambda e: e.tensor_scalar(out=kdec_all[:, c, :], in0=x["pTok"][:, 0:64], scalar1=ekd[:, c, m:m + 1], scalar2=None, op0=ALU.mult), reads=[x["pTokt"], "ekd"], writes=[("kdec", c)])
                X.update(N=N, Nt=Nt, rhs=rhs, rhst=rhst)

            def s3():
                x = dict(X)
                pNT, pNTt = ps.get()
                P.op("pe", lambda e: e.transpose(pNT, x["N"], k.idn[:]), reads=[x["Nt"], "idn"], writes=[pNTt])
                X.update(pNT=pNT, pNTt=pNTt)

            def s4():
                x = dict(X)
                PT_, PTt = wk.get()
                XT, XTt = wk.get()
                P.op("dve", lambda e: e.tensor_copy(out=PT_, in_=x["pNT"]), reads=[x["pNTt"]], writes=[PTt])
                P.op("dve", lambda e: e.tensor_tensor(out=XT, in0=x["pNT"], in1=k.idn[:], op=ALU.add), reads=[x["pNTt"], "idn"], writes=[XTt])
                X.update(P=x["N"], Pt=x["Nt"], PT=PT_, PTt=PTt, XT=XT, XTt=XTt)

            def lvl_mm(kk):
                def f():
                    x = dict(X)
                    pPP, pPt = ps2.get()
                    pP, pPT = pPP[:, 0:128], pPP[:, 128:256]
                    P.op("pe", lambda e: e.matmul(pP, lhsT=x["PT"], rhs=x["P"], start=True, stop=True), reads=[x["PTt"], x["Pt"]], writes=[pPt], inc=(kk == 5))
                    if kk < 5:
                        P.op("pe", lambda e: e.matmul(pPT, lhsT=x["P"], rhs=x["PT"], start=True, stop=True), reads=[x["PTt"], x["Pt"]], writes=[pPt])
                    X.update(pPP=pPP, pPt=pPt)
                return f

            def lvl_ev(kk):
                def f():
                    x = dict(X)
                    nPP, nPt = wk2.get()
                    nP, nPT = nPP[:, 0:128], nPP[:, 128:256]
                    w_ = 256 if kk < 5 else 128
                    P.op("dve", lambda e: e.tensor_copy(out=nPP[:, 0:w_], in_=x["pPP"][:, 0:w_]), reads=[x["pPt"]], writes=[nPt])
                    X.update(P=nP, Pt=nPt, PT=nPT, PTt=nPt)
                return f

            def lvl_px(kk):
                def f():
                    x = dict(X)
                    pX, pXt = ps.get()
                    P.op("pe", lambda e: e.matmul(pX, lhsT=x["P"], rhs=x["XT"], start=True, stop=True), reads=[x["Pt"], x["XTt"]], writes=[pXt])
                    X.update(pX=pX, pXt=pXt)
                return f

            def lvl_acc(kk):
                def f():
                    x = dict(X)
                    nX, nXt = wk.get()
                    P.op("dve", lambda e: e.tensor_tensor(out=nX, in0=x["XT"], in1=x["pX"], op=ALU.add), reads=[x["XTt"], x["pXt"]], writes=[nXt])
                    X.update(XT=nX, XTt=nXt)
                return f

            def s_sol():
                x = dict(X)
                pU, pUt = ps.get()
                pW, pWt = ps.get()
                P.op("pe", lambda e: e.matmul(pU[:, 0:64], lhsT=x["XT"], rhs=x["rhs"][:, 64:128], start=True, stop=True), reads=[x["XTt"], x["rhst"]], writes=[pUt], inc=False)
                for h2 in range(2):
                    pb = h2 * 64
                    P.op("pe", lambda e, pb=pb: e.matmul(pW[pb:pb + 64, 0:64], lhsT=x["rhs"][pb:pb + 64, 0:64], rhs=x["XT"][pb:pb + 64, pb:pb + 64], start=True, stop=True), reads=[x["XTt"], x["rhst"]], writes=[pWt], inc=(h2 == 1))
                X.update(pU=pU, pUt=pUt, pW=pW, pWt=pWt)

            def s_solev():
                x = dict(X)
                P.op("dve", lambda e: e.tensor_copy(out=u_all[:, c, :], in_=x["pU"][:, 0:64]), reads=[x["pUt"]], writes=[("u", c)])
                P.op("dve", lambda e: e.tensor_copy(out=wT_all[:, c, :], in_=x["pW"][:, 0:64]), reads=[x["pWt"]], writes=[("wT", c)])

            steps = [s1a, s1, s2a, s2, s3, s4]
            for kk in range(1, 6):
                steps += [lvl_mm(kk), lvl_ev(kk), lvl_px(kk), lvl_acc(kk)]
            steps += [s_sol, s_solev]
            return steps[:k.opts.get("dn_steps", 99)]

        def phase_b_chunk(m, c, first_dir):
            d, hp = m // 2, m % 2
            qn = qkv[hp]
            cs = slice(c * 64, (c + 1) * 64)
            p1, p1t = ps.get()
            p2, p2t = ps.get()
            for h2 in range(2):
                pb = h2 * 64
                P.op("pe", lambda e, pb=pb: e.matmul(p1[pb:pb + 64, 0:64], lhsT=wT_all[pb:pb + 64, c, :], rhs=S[pb:pb + 64, :], start=True, stop=True), reads=[("wT", c), "S"], writes=[p1t], inc=False)
                P.op("pe", lambda e, pb=pb: e.matmul(p2[pb:pb + 64, 0:64], lhsT=qn[pb:pb + 64, cs], rhs=S[pb:pb + 64, :], start=True, stop=True), reads=[("qkv", hp), "S"], writes=[p2t], inc=(h2 == 1))
            vn_, vnt = wv.get()
            P.op("dve", lambda e: e.tensor_tensor(out=vn_, in0=u_all[:, c, :], in1=p1[:, 0:64], op=ALU.subtract), reads=[("u", c), p1t], writes=[vnt])
            p3, p3t = ps.get()
            p4, p4t = ps.get()
            P.op("pe", lambda e: e.matmul(p3[:, 0:64], lhsT=attnT_all[:, c, :], rhs=vn_, start=True, stop=True), reads=[("attnT", c), vnt], writes=[p3t], inc=False)
            for h2 in range(2):
                pb = h2 * 64
                P.op("pe", lambda e, pb=pb: e.matmul(p4[pb:pb + 64, 0:64], lhsT=kdec_all[pb:pb + 64, c, :], rhs=vn_[pb:pb + 64, :], start=True, stop=True), reads=[("kdec", c), vnt], writes=[p4t], inc=(h2 == 1))
            t_, tt_ = wv.get()
            P.op("dve", lambda e: e.tensor_scalar(out=t_, in0=p2[:, 0:64], scalar1=eg[:, c, m:m + 1], scalar2=None, op0=ALU.mult), reads=[p2t, "eg"], writes=[tt_])
            if first_dir:
                P.op("dve", lambda e: e.tensor_tensor(out=O_tok[:, c, :], in0=t_, in1=p3[:, 0:64], op=ALU.add), reads=[tt_, p3t], writes=[("O", c)])
            else:
                P.op("dve", lambda e: e.tensor_tensor(out=t_, in0=t_, in1=p3[:, 0:64], op=ALU.add), reads=[tt_, p3t], writes=[tt_])
                P.op("dve", lambda e: e.tensor_tensor(out=O_tok[:, c, :], in0=O_tok[:, c, :], in1=t_, op=ALU.add), reads=[tt_, ("O", c)], writes=[("O", c)])
            P.op("dve", lambda e: e.scalar_tensor_tensor(out=S[:], in0=S[:], scalar=glast[:, c, m:m + 1], in1=p4[:, 0:64], op0=ALU.mult, op1=ALU.add), reads=["S", "glast", p4t], writes=["S"])

        def out_phase(b, hp):
            z, yfm = xin, acc
            r0 = FM_Z + hp * 128
            P.dma(lambda e: e.dma_start(out=z[:], in_=k.PT[b, r0:r0 + 128, :]), reads=[("PT", b)], writes=["xin"])
            P.op("act", lambda e: e.activation(out=z[:], in_=z[:], func=AF.Silu), reads=["xin"], writes=["xin"])
            allO = [("O", c) for c in range(NCK)]
            P.op("dve", lambda e: e.tensor_tensor(out=u_all[:], in0=O_tok[:], in1=O_tok[:], op=ALU.mult), reads=allO, writes=[("u", c) for c in range(NCK)])
            P.op("dve", lambda e: e.tensor_reduce(out=ssq[:], in_=u_all[:], axis=AX.X, op=ALU.add), reads=[("u", c) for c in range(NCK)], writes=["ssq"])
            P.op("act", lambda e: e.activation(out=ssq[:], in_=ssq[:], func=AF.Sqrt, bias=epsb[:, 0:1], scale=1.0 / 64), reads=["ssq", "epsb"], writes=["ssq"])
            P.op("dve", lambda e: e.reciprocal(out=ssq[:], in_=ssq[:]), reads=["ssq"], writes=["ssq"])
            P.op("dve", lambda e: e.tensor_tensor(out=O_tok[:], in0=O_tok[:], in1=ssq[:].rearrange("p (c o) -> p c o", o=1).to_broadcast([128, NCK, 64]), op=ALU.mult), reads=allO + ["ssq"], writes=allO)
            P.op("dve", lambda e: e.tensor_tensor(out=O_tok[:], in0=O_tok[:], in1=gno[:].rearrange("p (o v) -> p o v", o=1).to_broadcast([128, NCK, 64]), op=ALU.mult), reads=allO + ["gno"], writes=allO)
            for c in range(NCK):
                out_chunk(c)
            for (t0, n) in [(0, CTX)] + [(CTX + i * 512, 512) for i in range(4)]:
                out_tile(b, hp, t0, n)

        def out_chunk(c):
            pp, ppt = ps.get()
            for h2 in range(2):
                pb = h2 * 64
                P.op("pe", lambda e, pb=pb: e.matmul(pp[pb:pb + 64, 0:64], lhsT=O_tok[pb:pb + 64, c, :], rhs=idn2[pb:pb + 64, :], start=True, stop=True), reads=[("O", c), "idn2"], writes=[ppt], inc=(h2 == 1))
            P.op("dve", lambda e: e.tensor_copy(out=acc[:, c * 64:(c + 1) * 64], in_=pp[:, 0:64]), reads=[ppt], writes=["acc"])

        def out_tile(b, hp, t0, n):
            yi = cnt["y"] % 2
            cnt["y"] += 1
            P.op("dve", lambda e: e.tensor_tensor(out=yb[yi][:, :n], in0=acc[:, t0:t0 + n], in1=xin[:, t0:t0 + n], op=ALU.mult), reads=["acc", "xin"], writes=[("yb", yi)])
            P.dma(lambda e: e.dma_start(out=k.YT[b, 0, hp * 128:(hp + 1) * 128, t0:t0 + n], in_=yb[yi][:, :n]), reads=[("yb", yi)], writes=[("YT", b)])

        G = 4
        for b in range(NB):
            for ti in range(6):
                prep_tile(b, ti)
            prep_gates(b)
            lim = k.opts.get("dn_lim", 99)
            if lim == 0:
                continue
            for hp in range(2):
                for d in range(2):
                    m = d * 2 + hp
                    for c0 in range(0, NCK if lim >= 2 else G, G):
                        lists = [phase_a_steps(m, c) for c in range(c0, min(NCK, c0 + G))]
                        for si in range(len(lists[0])):
                            for lst in lists:
                                lst[si]()
                    if lim < 3:
                        continue
                    P.op("dve", lambda e: e.memset(S[:], 0.0), writes=["S"])
                    order = list(range(NCK)) if d == 0 else [3, 2, 1, 0] + list(range(NCK - 1, 3, -1))
                    for c in order:
                        phase_b_chunk(m, c, d == 0)
                    if "dump" in k.opts and (b, m) == k.opts["dump"][:2]:
                        for j in range(6):
                            P.dma(lambda e, j=j: e.dma_start(out=k.dump[j], in_=qkv[j][:]), reads=[("qkv", j)], writes=[("dump", j)])
                        for j, (tl, tk) in enumerate([(u_all, "u"), (wT_all, "wT"), (kdec_all, "kdec"), (O_tok, "O")]):
                            P.dma(lambda e, j=j, tl=tl: e.dma_start(out=k.dump[6 + j], in_=tl[:].rearrange("p c v -> p (c v)")), reads=[(tk, c) for c in range(NCK)], writes=[("dump", 6 + j)])
                        P.dma(lambda e: e.dma_start(out=k.dump[10:12].rearrange("a p t -> p a t"), in_=attnT_all[:].rearrange("p (a c) v -> p a (c v)", a=2)), reads=[("attnT", c) for c in range(NCK)], writes=[("dump", 10)])
                        for j, (tl, tk, w) in enumerate([(gt, "gt", 288), (bt, "bt", 288), (gcBD, "gcBD", 144), (gtBD, "gtBD", 144), (bBD, "bBD", 144)]):
                            P.dma(lambda e, j=j, tl=tl, w=w: e.dma_start(out=k.dump[12, :, j * 300:j * 300 + w], in_=tl[:].rearrange("p c v -> p (c v)")), reads=[tk], writes=[("dump", 12, j)])
                if lim >= 4:
                    out_phase(b, hp)
        P.end_stage()


def stage_s5(k, l):
    nc, P, NB = k.nc, k.P, k.NB
    HALF_PI = float(np.pi / 2)
    P.serial = k.opts.get("serial_s5", SERIAL_S5)
    with ExitStack() as es:
        al = lambda name, shape, dt=F32: es.enter_context(nc.sbuf_tensor("s_" + name, list(shape), dt))
        lre = al("lre", [128, 16]); lim = al("lim", [128, 16]); stp = al("stp", [128, 16]); mag = al("mag", [128, 16])
        cth = al("cth", [128, 16]); sth = al("sth", [128, 16]); t1 = al("t1", [128, 16]); t2 = al("t2", [128, 16]); t3 = al("t3", [128, 16])
        are = al("are", [128, 16]); aim = al("aim", [128, 16]); cfr = al("cfr", [128, 16]); cfi = al("cfi", [128, 16]); hpi = al("hpi", [128, 1])
        pwc = al("pwc", [128, 16, 12]); pws = al("pws", [128, 16, 12])
        bre = al("bre", [128, 8, 16]); bim = al("bim", [128, 8, 16]); cre = al("cre", [128, 8, 16]); cim = al("cim", [128, 8, 16])
        bb_all = al("bb_all", [128, 32, 16]); bd_all = al("bd_all", [128, 32, 32]); tb = [al("tb%d" % i, [128, 8, 16]) for i in range(4)]
        W_all = al("W_all", [32, 32, 128], BF16); cw_all = al("cw_all", [128, 8, 2, 128], BF16)
        dsk = al("dsk", [128, 2]); wgl = al("wgl", [128, 2, 512], BF16)
        cs = [al("cs%d" % i, [128, T]) for i in range(2)]; sn = [al("sn%d" % i, [128, T]) for i in range(2)]
        xr = al("xr", [128, T]); xi = al("xi", [128, T]); gr = al("gr", [128, T]); gi = al("gi", [128, T])
        u32 = al("u32", [32, T]); u32b = al("u32b", [32, T], BF16); hrb = al("hrb", [128, T], BF16); hib = al("hib", [128, T], BF16); Y = [al("Y%d" % i, [128, T]) for i in range(2)]
        mt = Slots([al("mt%d" % i, [128, 512])[:] for i in range(6)], "mt")
        gel = al("gel", [128, 2, 512], BF16); sg = [al("sg%d" % i, [128, 512]) for i in range(2)]; yb = [al("yb%d" % i, [128, 512], BF16) for i in range(2)]
        banks = [es.enter_context(nc.psum_tensor("s_ps%d" % i, [128, 512], F32)) for i in range(8)]
        psl = Slots([banks[i][:] for i in range(8)], "psb")
        ld = lambda dst, src, tok: P.dma(lambda e: e.dma_start(out=dst, in_=src), writes=[tok])
        ld(lre[:], k.s5_lam_re[l], "lre"); ld(lim[:], k.s5_lam_im[l], "lim"); ld(stp[:], k.s5_log_step[l], "stp")
        ld(bre[:], k.s5_b_re[l], "bre"); ld(bim[:], k.s5_b_im[l], "bim"); ld(cre[:], k.s5_c_re[l], "cre"); ld(cim[:], k.s5_c_im[l], "cim")
        ld(dsk[:], k.s5_d[l], "dsk")
        P.dma(lambda e: e.dma_start(out=wgl[:], in_=k.s5_glu[l].rearrange("(c p) n -> p c n", p=128)), writes=["wgl"], q="pool")
        P.op("dve", lambda e: e.memset(hpi[:], HALF_PI), writes=["hpi"])
        P.op("dve", lambda e: e.memset(bd_all[:], 0.0), writes=["bd_all"])
        P.op("dve", lambda e: e.memset(cw_all[:], 0.0), writes=["cw_all"])
        tt = lambda out, a, b_, op, rd, wr, eng="dve": P.op(eng, lambda e: e.tensor_tensor(out=out, in0=a, in1=b_, op=op), reads=rd, writes=wr)
        P.op("act", lambda e: e.activation(out=stp[:], in_=stp[:], func=AF.Exp), reads=["stp"], writes=["stp"])
        tt(t1[:], lre[:], stp[:], ALU.mult, ["lre", "stp"], ["t1"])
        P.op("act", lambda e: e.activation(out=mag[:], in_=t1[:], func=AF.Exp), reads=["t1"], writes=["mag"])
        tt(t2[:], lim[:], stp[:], ALU.mult, ["lim", "stp"], ["t2"])
        P.op("act", lambda e: e.activation(out=sth[:], in_=t2[:], func=AF.Sin, scale=1.0 / 32), reads=["t2"], writes=["sth"])
        P.op("act", lambda e: e.activation(out=cth[:], in_=t2[:], func=AF.Sin, scale=1.0 / 32, bias=hpi[:, 0:1]), reads=["t2", "hpi"], writes=["cth"])
        for it in range(5):
            tt(t1[:], cth[:], cth[:], ALU.mult, ["cth"], ["t1"])
            tt(t3[:], sth[:], sth[:], ALU.mult, ["sth"], ["t3"])
            P.op("dve", lambda e: e.scalar_tensor_tensor(out=sth[:], in0=cth[:], scalar=2.0, in1=sth[:], op0=ALU.mult, op1=ALU.mult), reads=["cth", "sth"], writes=["sth"])
            tt(cth[:], t1[:], t3[:], ALU.subtract, ["t1", "t3"], ["cth"])
        tt(are[:], mag[:], cth[:], ALU.mult, ["mag", "cth"], ["are"])
        tt(aim[:], mag[:], sth[:], ALU.mult, ["mag", "sth"], ["aim"])
        tt(t1[:], lre[:], lre[:], ALU.mult, ["lre"], ["t1"])
        tt(t3[:], lim[:], lim[:], ALU.mult, ["lim"], ["t3"])
        tt(t1[:], t1[:], t3[:], ALU.add, ["t1", "t3"], ["t1"])
        P.op("dve", lambda e: e.reciprocal(out=t1[:], in_=t1[:]), reads=["t1"], writes=["t1"])
        P.op("dve", lambda e: e.tensor_scalar(out=t2[:], in0=are[:], scalar1=-1.0, scalar2=None, op0=ALU.add), reads=["are"], writes=["t2"])
        tt(cfr[:], t2[:], lre[:], ALU.mult, ["t2", "lre"], ["cfr"])
        tt(t3[:], aim[:], lim[:], ALU.mult, ["aim", "lim"], ["t3"])
        tt(cfr[:], cfr[:], t3[:], ALU.add, ["cfr", "t3"], ["cfr"])
        tt(cfr[:], cfr[:], t1[:], ALU.mult, ["cfr", "t1"], ["cfr"])
        tt(cfi[:], aim[:], lre[:], ALU.mult, ["aim", "lre"], ["cfi"])
        tt(t3[:], t2[:], lim[:], ALU.mult, ["t2", "lim"], ["t3"])
        tt(cfi[:], cfi[:], t3[:], ALU.subtract, ["cfi", "t3"], ["cfi"])
        tt(cfi[:], cfi[:], t1[:], ALU.mult, ["cfi", "t1"], ["cfi"])
        bb4 = bb_all[:].rearrange("p (d c r) h -> p d c r h", d=2, r=2)
        for d in range(2):
            bc = lambda t_: t_[:, d * 8:(d + 1) * 8].rearrange("p (c o) -> p c o", o=1).to_broadcast([128, 8, 16])
            tt(tb[0][:], bre[:], bc(cfr), ALU.mult, ["bre", "cfr"], [("tb", 0)])
            tt(tb[1][:], bim[:], bc(cfi), ALU.mult, ["bim", "cfi"], [("tb", 1)])
            tt(bb4[:, d, :, 0, :], tb[0][:], tb[1][:], ALU.subtract, [("tb", 0), ("tb", 1)], ["bb_all"])
            tt(tb[2][:], bim[:], bc(cfr), ALU.mult, ["bim", "cfr"], [("tb", 2)])
            tt(tb[3][:], bre[:], bc(cfi), ALU.mult, ["bre", "cfi"], [("tb", 3)])
            tt(bb4[:, d, :, 1, :], tb[2][:], tb[3][:], ALU.add, [("tb", 2), ("tb", 3)], ["bb_all"])
        for g2 in range(2):
            rr_ = slice(g2 * 64, (g2 + 1) * 64)
            P.op("dve", lambda e, rr_=rr_, g2=g2: e.tensor_copy(out=bd_all[rr_, :, g2 * 16:(g2 + 1) * 16], in_=bb_all[rr_, :, :]), reads=["bb_all", "bd_all"], writes=["bd_all"])

        def w_build(j):
            pw, pwt = psl.get()
            P.op("pe", lambda e: e.transpose(pw[0:32, 0:128], bd_all[:, j, :], k.idn[:]), reads=["bd_all", "idn"], writes=[pwt])
            P.op("act", lambda e: e.activation(out=W_all[:, j, :], in_=pw[0:32, 0:128], func=AF.Identity), reads=[pwt], writes=["W_all"])
        for j in range(32):
            w_build(j)

        def cw_build(ct, g2):
            q = ct % 4
            rr_ = slice(g2 * 64, (g2 + 1) * 64)
            c0 = q * 32 + g2 * 16
            P.op("dve", lambda e: e.tensor_copy(out=cw_all[rr_, ct, 0, c0:c0 + 16], in_=cre[rr_, ct, :]), reads=["cre", "cw_all"], writes=["cw_all"])
            P.op("dve", lambda e: e.tensor_scalar(out=cw_all[rr_, ct, 1, c0:c0 + 16], in0=cim[rr_, ct, :], scalar1=-1.0, scalar2=None, op0=ALU.mult), reads=["cim", "cw_all"], writes=["cw_all"])
        for ct in range(8):
            for g2 in range(2):
                cw_build(ct, g2)
        P.op("dve", lambda e: e.tensor_copy(out=pwc[:, :, 0], in_=cth[:]), reads=["cth"], writes=["pwc"])
        P.op("dve", lambda e: e.tensor_copy(out=pws[:, :, 0], in_=sth[:]), reads=["sth"], writes=["pws"])

        def pw_level(kk):
            c_, s_ = pwc[:, :, kk - 1], pws[:, :, kk - 1]
            tt(t1[:], c_, c_, ALU.mult, ["pwc"], ["t1"])
            tt(t3[:], s_, s_, ALU.mult, ["pws"], ["t3"])
            tt(pwc[:, :, kk], t1[:], t3[:], ALU.subtract, ["t1", "t3", "pwc"], ["pwc"])
            P.op("dve", lambda e: e.scalar_tensor_tensor(out=pws[:, :, kk], in0=c_, scalar=2.0, in1=s_, op0=ALU.mult, op1=ALU.mult), reads=["pwc", "pws"], writes=["pws"])
        for kk in range(1, 12):
            pw_level(kk)

        def table_gen(j):
            i = j % 2
            C, S_ = cs[i], sn[i]
            ctk, stk = ("cs", i), ("sn", i)
            P.op("dve", lambda e: e.memset(C[:, 0:1], 1.0), writes=[ctk])
            P.op("dve", lambda e: e.memset(S_[:, 0:1], 0.0), writes=[stk])
            for kk in range(12):
                ln = 1 << kk
                nn = min(ln, T - ln)
                if nn <= 0:
                    break
                pc, ps_ = pwc[:, j, kk:kk + 1], pws[:, j, kk:kk + 1]
                lvl(C, S_, ctk, stk, ln, nn, pc, ps_)
            P.dma(lambda e: e.dma_start(out=k.S5TAB[j, 0], in_=C[:]), reads=[ctk], writes=[("TAB", j)])
            P.dma(lambda e: e.dma_start(out=k.S5TAB[j, 1], in_=S_[:]), reads=[stk], writes=[("TAB", j)])

        def lvl(C, S_, ctk, stk, ln, nn, pc, ps_):
            m1, m1t = mt.get()
            m2, m2t = mt.get()
            w_ = min(nn, 512)
            for o in range(0, nn, 512):
                w = min(512, nn - o)
                sub(C, S_, ctk, stk, ln, o, w, pc, ps_)

        def sub(C, S_, ctk, stk, ln, o, w, pc, ps_):
            m1, m1t = mt.get()
            m2, m2t = mt.get()
            P.op("dve", lambda e: e.tensor_scalar(out=m1[:, :w], in0=S_[:, o:o + w], scalar1=ps_, scalar2=None, op0=ALU.mult), reads=[stk, "pws"], writes=[m1t])
            P.op("dve", lambda e: e.tensor_scalar(out=m2[:, :w], in0=S_[:, o:o + w], scalar1=pc, scalar2=None, op0=ALU.mult), reads=[stk, "pwc"], writes=[m2t])
            P.op("dve", lambda e: e.scalar_tensor_tensor(out=C[:, ln + o:ln + o + w], in0=C[:, o:o + w], scalar=pc, in1=m1[:, :w], op0=ALU.mult, op1=ALU.subtract), reads=[ctk, m1t, "pwc"], writes=[ctk])
            P.op("dve", lambda e: e.scalar_tensor_tensor(out=S_[:, ln + o:ln + o + w], in0=C[:, o:o + w], scalar=ps_, in1=m2[:, :w], op0=ALU.mult, op1=ALU.add), reads=[ctk, m2t, "pws"], writes=[stk])
        for j in range(16):
            table_gen(j)

        blocks = [(0, CTX)] + [(CTX + i * 512, 512) for i in range(4)]

        def tabview(tab, d, t0, n):
            if d == 0:
                return tab[:, t0:t0 + n]
            if t0 < CTX:
                lo = CTX - 1 - (t0 + n - 1)
            else:
                lo = CTX + (T - 1 - (t0 + n - 1))
            return tab[:, lo:lo + n][:, ::-1]

        def do_block_in(ct, d, i, t0, n):
            j = d * 8 + ct
            pr, prt = psl.get()
            pi_, pit = psl.get()
            P.op("pe", lambda e: e.matmul(pr[:, :n], lhsT=W_all[:, j * 2, :], rhs=u32b[:, t0:t0 + n], start=True, stop=True), reads=["W_all", "u32b"], writes=[prt], inc=False)
            P.op("pe", lambda e: e.matmul(pi_[:, :n], lhsT=W_all[:, j * 2 + 1, :], rhs=u32b[:, t0:t0 + n], start=True, stop=True), reads=["W_all", "u32b"], writes=[pit])
            cv, sv = tabview(cs[i], d, t0, n), tabview(sn[i], d, t0, n)
            ms = [mt.get() for _ in range(4)]
            tt(ms[0][0][:, :n], pr[:, :n], cv, ALU.mult, [prt, ("cs", i)], [ms[0][1]])
            tt(ms[1][0][:, :n], pi_[:, :n], sv, ALU.mult, [pit, ("sn", i)], [ms[1][1]])
            tt(xr[:, t0:t0 + n], ms[0][0][:, :n], ms[1][0][:, :n], ALU.add, [ms[0][1], ms[1][1]], ["xr"])
            tt(ms[2][0][:, :n], pi_[:, :n], cv, ALU.mult, [pit, ("cs", i)], [ms[2][1]])
            tt(ms[3][0][:, :n], pr[:, :n], sv, ALU.mult, [prt, ("sn", i)], [ms[3][1]])
            tt(xi[:, t0:t0 + n], ms[2][0][:, :n], ms[3][0][:, :n], ALU.subtract, [ms[2][1], ms[3][1]], ["xi"])

        def do_scan(ct, d):
            j = d * 8 + ct
            for (src, dst, stok, dtok) in ((xr, gr, "xr", "gr"), (xi, gi, "xi", "gi")):
                scan1(j, d, src, dst, stok, dtok)

        def scan1(j, d, src, dst, stok, dtok):
            rb = lambda n: mag[:, j:j + 1].to_broadcast([128, n])
            if d == 0:
                P.op("dve", lambda e: e.tensor_tensor_scan(out=dst[:], data0=rb(T), data1=src[:], initial=0.0, op0=ALU.mult, op1=ALU.add), reads=[stok, "mag"], writes=[dtok])
            else:
                P.op("dve", lambda e: e.tensor_tensor_scan(out=dst[:, 0:CTX][:, ::-1], data0=rb(CTX), data1=src[:, 0:CTX][:, ::-1], initial=0.0, op0=ALU.mult, op1=ALU.add), reads=[stok, "mag"], writes=[dtok])
                P.op("dve", lambda e: e.tensor_tensor_scan(out=dst[:, CTX:T][:, ::-1], data0=rb(SEQ), data1=src[:, CTX:T][:, ::-1], initial=dst[:, 0:1], op0=ALU.mult, op1=ALU.add), reads=[stok, "mag", dtok], writes=[dtok])

        def do_block_out(ct, d, i, t0, n, first):
            cv, sv = tabview(cs[i], d, t0, n), tabview(sn[i], d, t0, n)
            ms = [mt.get() for _ in range(4)]
            tt(ms[0][0][:, :n], gr[:, t0:t0 + n], cv, ALU.mult, ["gr", ("cs", i)], [ms[0][1]])
            tt(ms[1][0][:, :n], gi[:, t0:t0 + n], sv, ALU.mult, ["gi", ("sn", i)], [ms[1][1]])
            tt(hrb[:, t0:t0 + n], ms[0][0][:, :n], ms[1][0][:, :n], ALU.subtract, [ms[0][1], ms[1][1]], ["hrb"])
            tt(ms[2][0][:, :n], gr[:, t0:t0 + n], sv, ALU.mult, ["gr", ("sn", i)], [ms[2][1]])
            tt(ms[3][0][:, :n], gi[:, t0:t0 + n], cv, ALU.mult, ["gi", ("cs", i)], [ms[3][1]])
            tt(hib[:, t0:t0 + n], ms[2][0][:, :n], ms[3][0][:, :n], ALU.add, [ms[2][1], ms[3][1]], ["hib"])
            py, pyt = psl.get()
            P.op("pe", lambda e: e.matmul(py[:, :n], lhsT=cw_all[:, ct, 0, :], rhs=hrb[:, t0:t0 + n], start=True, stop=False), reads=["cw_all", "hrb"], writes=[pyt], inc=False)
            P.op("pe", lambda e: e.matmul(py[:, :n], lhsT=cw_all[:, ct, 1, :], rhs=hib[:, t0:t0 + n], start=False, stop=True), reads=["cw_all", "hib"], writes=[pyt])
            yt_ = Y[ct // 4]
            ytk = ("Y", ct // 4)
            if first:
                P.op("dve", lambda e: e.tensor_copy(out=yt_[:, t0:t0 + n], in_=py[:, :n]), reads=[pyt], writes=[ytk])
            else:
                tt(yt_[:, t0:t0 + n], yt_[:, t0:t0 + n], py[:, :n], ALU.add, [pyt, ytk], [ytk])

        def do_ct_dir(b, ct, d):
            j = d * 8 + ct
            i = j % 2
            P.dma(lambda e: e.dma_start(out=cs[i][:], in_=k.S5TAB[j, 0]), reads=[("TAB", j)], writes=[("cs", i)])
            P.dma(lambda e: e.dma_start(out=sn[i][:], in_=k.S5TAB[j, 1]), reads=[("TAB", j)], writes=[("sn", i)])
            for (t0, n) in blocks:
                do_block_in(ct, d, i, t0, n)
            do_scan(ct, d)
            for (t0, n) in blocks:
                do_block_out(ct, d, i, t0, n, (ct % 4 == 0 and d == 0))

        def do_ct(b, ct):
            r0 = FM_U + ct * 32
            P.dma(lambda e: e.dma_start(out=u32[:], in_=k.PT[b, r0:r0 + 32, :]), reads=[("PT", b)], writes=["u32"])
            P.op("dve", lambda e: e.tensor_copy(out=u32b[:], in_=u32[:]), reads=["u32"], writes=["u32b"])
            for d in range(2):
                do_ct_dir(b, ct, d)

        def out_tile(b, t0, n):
            for yt in range(2):
                ub, ubt = mt.get()
                r0 = FM_U + yt * 128
                P.dma(lambda e, ub=ub, r0=r0: e.dma_start(out=ub[:, :n], in_=k.PT[b, r0:r0 + 128, t0:t0 + n]), reads=[("PT", b)], writes=[ubt])
                yv, yvt = mt.get()
                P.op("dve", lambda e, ub=ub, yv=yv, yt=yt: e.scalar_tensor_tensor(out=yv[:, :n], in0=ub[:, :n], scalar=dsk[:, yt:yt + 1], in1=Y[yt][:, t0:t0 + n], op0=ALU.mult, op1=ALU.add), reads=[ubt, "dsk", ("Y", yt)], writes=[yvt])
                x2, x2t = mt.get()
                P.op("act", lambda e, yv=yv, x2=x2: e.activation(out=x2[:, :n], in_=yv[:, :n], func=AF.Square), reads=[yvt], writes=[x2t])
                P.op("dve", lambda e, x2=x2: e.tensor_scalar(out=x2[:, :n], in0=x2[:, :n], scalar1=0.044715, scalar2=1.0, op0=ALU.mult, op1=ALU.add), reads=[x2t], writes=[x2t])
                P.op("dve", lambda e, x2=x2, yv=yv: e.tensor_tensor(out=x2[:, :n], in0=x2[:, :n], in1=yv[:, :n], op=ALU.mult), reads=[x2t, yvt], writes=[x2t])
                P.op("act", lambda e, x2=x2: e.activation(out=x2[:, :n], in_=x2[:, :n], func=AF.Tanh, scale=0.7978845608028654), reads=[x2t], writes=[x2t])
                P.op("dve", lambda e, x2=x2, yv=yv, yt=yt: e.scalar_tensor_tensor(out=gel[:, yt, :n], in0=x2[:, :n], scalar=1.0, in1=yv[:, :n], op0=ALU.add, op1=ALU.mult), reads=[x2t, yvt], writes=[("gel", yt)])
            for oc in range(2):
                pa, pat = psl.get()
                pg, pgt = psl.get()
                for kc in range(2):
                    P.op("pe", lambda e, oc=oc, kc=kc, pa=pa: e.matmul(pa[:, :n], lhsT=wgl[:, kc, oc * 128:(oc + 1) * 128], rhs=gel[:, kc, :n], start=(kc == 0), stop=(kc == 1)), reads=["wgl", ("gel", kc)], writes=[pat])
                for kc in range(2):
                    P.op("pe", lambda e, oc=oc, kc=kc, pg=pg: e.matmul(pg[:, :n], lhsT=wgl[:, kc, 256 + oc * 128:256 + (oc + 1) * 128], rhs=gel[:, kc, :n], start=(kc == 0), stop=(kc == 1)), reads=["wgl", ("gel", kc)], writes=[pgt])
                P.op("act", lambda e, oc=oc, pg=pg: e.activation(out=sg[oc][:, :n], in_=pg[:, :n], func=AF.Sigmoid, scale=0.5), reads=[pgt], writes=[("sg", oc)])
                P.op("dve", lambda e, oc=oc, pa=pa: e.scalar_tensor_tensor(out=yb[oc][:, :n], in0=pa[:, :n], scalar=0.5, in1=sg[oc][:, :n], op0=ALU.mult, op1=ALU.mult), reads=[pat, ("sg", oc)], writes=[("yb", oc)])
                P.dma(lambda e, oc=oc: e.dma_start(out=k.YT[b, 1, oc * 128:(oc + 1) * 128, t0:t0 + n], in_=yb[oc][:, :n]), reads=[("yb", oc)], writes=[("YT", b)])

        for b in range(NB):
            for ct in range(8):
                do_ct(b, ct)
            for (t0, n) in blocks:
                out_tile(b, t0, n)
        P.end_stage()


def stage_mixers(k, l):
    sel = k.opts.get("mixers", ("dn", "s5", "hg", "att"))
    if "dn" in sel:
        stage_deltanet(k, l)
    if "s5" in sel:
        stage_s5(k, l)
    if "hg" in sel:
        stage_hgrn2(k, l)
    if "att" in sel:
        stage_attention(k, l)


def stage_merge(k, l):
    nc, P, NB, NC = k.nc, k.P, k.NB, k.NC
    with ExitStack() as es:
        wg = es.enter_context(nc.sbuf_tensor("m_wg", [128, 8, 4 * D], BF16))
        wb = es.enter_context(nc.sbuf_tensor("m_wb", [128, 8, D], BF16))
        wo = es.enter_context(nc.sbuf_tensor("m_wo", [128, 8, D], BF16))
        xt = [es.enter_context(nc.sbuf_tensor("m_x%d" % s, [128, 8, 256], F32)) for s in range(2)]
        yt = [es.enter_context(nc.sbuf_tensor("m_y%d" % s, [128, 8, 256], BF16)) for s in range(2)]
        ht = es.enter_context(nc.sbuf_tensor("m_h", [128, 8, 256], BF16))
        acc = es.enter_context(nc.sbuf_tensor("m_acc", [128, 8, 256], F32))
        accb = es.enter_context(nc.sbuf_tensor("m_accb", [128, 8, 256], BF16))
        sg = [es.enter_context(nc.sbuf_tensor("m_sg%d" % s, [128, 256], F32)) for s in range(2)]
        tt = [es.enter_context(nc.sbuf_tensor("m_tt%d" % s, [128, 256], F32)) for s in range(2)]
        psg = [es.enter_context(nc.psum_tensor("m_psg%d" % s, [128, 512], F32)) for s in range(2)]
        psy = [es.enter_context(nc.psum_tensor("m_psy%d" % s, [128, 512], F32)) for s in range(2)]
        pso = [es.enter_context(nc.psum_tensor("m_pso%d" % s, [128, 512], F32)) for s in range(2)]
        ntiles = alloc_norm_tiles(k, es, "m_", 256)
        A, Sh, G = emit_mod_scalars(k, es, "m_", l, 1, 3, 4, 5, 1.0)
        load_w_bf16(k, wg, k.w_gate[l], 4 * D, "wg")
        load_w_bf16(k, wb, k.w_branch[l].rearrange("i r n -> (i r) n"), D, "wb")
        load_w_bf16(k, wo, k.w_out[l], D, "wo")
        it = 0
        qi = 0
        for (b, t0, n, cond) in token_tiles(NB, 256):
            if l == 1 and cond == NB:
                continue
            s = it % 2
            it += 1
            xsrc = k.XT[b].rearrange("(c p) t -> p c t", p=128)[:, :, t0:t0 + n]
            P.dma(lambda e, s=s, xsrc=xsrc, n=n: e.dma_start(out=xt[s][:, :, :n], in_=xsrc), reads=[("XT", b)], writes=[("x", s)])
            ysrc = k.YT[b].rearrange("i (c p) t -> p (i c) t", p=128)[:, :, t0:t0 + n]
            P.dma(lambda e, s=s, ysrc=ysrc, n=n: e.dma_start(out=yt[s][:, :, :n], in_=ysrc), reads=[("YT", b)], writes=[("y", s)])
            emit_norm_mod(k, ntiles, xt[s], ht, n, A, Sh, cond, "modsc", s)
            for m in range(8):
                for i in range(4):
                    q = qi % 2
                    qi += 1
                    for kc in range(8):
                        P.op("pe", lambda e, i=i, m=m, q=q, kc=kc, n=n: e.matmul(psg[q][:, :n], lhsT=wg[:, kc, i * D + m * 128:i * D + (m + 1) * 128], rhs=ht[:, kc, :n], start=(kc == 0), stop=(kc == 7)),
                             reads=[("wg", kc), "h"], writes=[("psg", q)], inc=(kc == 7))
                    for kk in range(2):
                        P.op("pe", lambda e, i=i, m=m, q=q, kk=kk, n=n, s=s: e.matmul(psy[q][:, :n], lhsT=wb[:, i * 2 + kk, m * 128:(m + 1) * 128], rhs=yt[s][:, i * 2 + kk, :n], start=(kk == 0), stop=(kk == 1)),
                             reads=[("wb", i * 2 + kk), ("y", s)], writes=[("psy", q)], inc=(kk == 1))
                    P.op("act", lambda e, q=q, n=n: e.activation(out=sg[q][:, :n], in_=psg[q][:, :n], func=AF.Sigmoid), reads=[("psg", q)], writes=[("sg", q)])
                    if i == 0:
                        P.op("dve", lambda e, q=q, n=n, m=m: e.tensor_tensor(out=acc[:, m, :n], in0=sg[q][:, :n], in1=psy[q][:, :n], op=ALU.mult), reads=[("sg", q), ("psy", q)], writes=[("acc", m)])
                    else:
                        P.op("dve", lambda e, q=q, n=n: e.tensor_tensor(out=tt[q][:, :n], in0=sg[q][:, :n], in1=psy[q][:, :n], op=ALU.mult), reads=[("sg", q), ("psy", q)], writes=[("tt", q)])
                        if i < 3:
                            P.op("pool", lambda e, q=q, n=n, m=m: e.tensor_tensor(out=acc[:, m, :n], in0=acc[:, m, :n], in1=tt[q][:, :n], op=ALU.add), reads=[("tt", q), ("acc", m)], writes=[("acc", m)])
                        else:
                            P.op("pool", lambda e, q=q, n=n, m=m: e.tensor_tensor(out=accb[:, m, :n], in0=acc[:, m, :n], in1=tt[q][:, :n], op=ALU.add), reads=[("tt", q), ("acc", m)], writes=[("accb", m)])
            for m in range(8):
                q = m % 2
                for kc in range(8):
                    P.op("pe", lambda e, m=m, q=q, kc=kc, n=n: e.matmul(pso[q][:, :n], lhsT=wo[:, kc, m * 128:(m + 1) * 128], rhs=accb[:, kc, :n], start=(kc == 0), stop=(kc == 7)),
                         reads=[("wo", kc), ("accb", kc)], writes=[("pso", q)], inc=(kc == 7))
                P.op("dve", lambda e, m=m, q=q, s=s, n=n, cond=cond: e.scalar_tensor_tensor(out=xt[s][:, m, :n], in0=pso[q][:, :n], scalar=G[:, m, cond:cond + 1], in1=xt[s][:, m, :n], op0=ALU.mult, op1=ALU.add),
                     reads=[("pso", q), ("x", s), "modsg"], writes=[("x", s)])
            P.dma(lambda e, s=s, xsrc=xsrc, n=n: e.dma_start(out=xsrc, in_=xt[s][:, :, :n]), reads=[("x", s)], writes=[("XT", b)])
        P.end_stage()


def stage_final(k):
    nc, P, NB = k.nc, k.P, k.NB
    with ExitStack() as es:
        xt = [es.enter_context(nc.sbuf_tensor("o_x%d" % s, [128, 8, 512], F32)) for s in range(2)]
        yt = es.enter_context(nc.sbuf_tensor("o_y", [128, 8, 512], F32))
        ot = [es.enter_context(nc.sbuf_tensor("o_o%d" % s, [128, D], F32)) for s in range(2)]
        sq = es.enter_context(nc.sbuf_tensor("o_sq", [128, 8, 512], BF16))
        rs = es.enter_context(nc.sbuf_tensor("o_rs", [128, 512], F32))
        epsb = es.enter_context(nc.sbuf_tensor("o_eps", [128, 1], F32))
        psms = es.enter_context(nc.psum_tensor("o_psms", [128, 512], F32))
        ps = [es.enter_context(nc.psum_tensor("o_ps%d" % s, [128, 4, 128], F32)) for s in range(4)]
        P.op("dve", lambda e: e.memset(epsb[:], EPS), writes=["epsb"])
        it = 0
        oi = 0
        pi = 0
        for (b, t0, n, cond) in token_tiles(NB):
            if cond == NB:
                continue
            s = it % 2
            it += 1
            xsrc = k.XT[b].rearrange("(c p) t -> p c t", p=128)[:, :, t0:t0 + n]
            P.dma(lambda e, s=s, xsrc=xsrc: e.dma_start(out=xt[s][:], in_=xsrc), reads=[("XT", b)], writes=[("x", s)])
            P.op("act", lambda e, s=s: e.activation(out=sq[:], in_=xt[s][:], func=AF.Square), reads=[("x", s)], writes=["sq"])
            for c in range(8):
                P.op("pe", lambda e, c=c: e.matmul(psms[:], lhsT=k.onesb[:], rhs=sq[:, c, :], start=(c == 0), stop=(c == 7)), reads=["sq", "onesb"], writes=["psms"], inc=(c == 7))
            P.op("act", lambda e: e.activation(out=rs[:], in_=psms[:], func=AF.Sqrt, bias=epsb[:, 0:1], scale=1.0), reads=["psms", "epsb"], writes=["rs0", "rs"])
            P.op("dve", lambda e: e.reciprocal(out=rs[:], in_=rs[:]), reads=["rs0"], writes=["rs"])
            for c in range(8):
                P.op("dve", lambda e, c=c, s=s: e.scalar_tensor_tensor(out=yt[:, c, :], in0=xt[s][:, c, :], scalar=k.fg[:, c:c + 1], in1=rs[:], op0=ALU.mult, op1=ALU.mult),
                     reads=[("x", s), "rs", "fg"], writes=[("yt", c)])
            for tb in range(4):
                o = oi % 2
                oi += 1
                for h in range(2):
                    p = pi % 4
                    pi += 1
                    for c in range(4):
                        ch = h * 4 + c
                        P.op("pe", lambda e, p=p, c=c, ch=ch, tb=tb: e.transpose(ps[p][:, c, :], yt[:, ch, tb * 128:(tb + 1) * 128], k.idn[:]),
                             reads=[("yt", ch), "idn"], writes=[("ps", p)], inc=(c == 3))
                    if h == 0:
                        P.op("act", lambda e, p=p, o=o, h=h: e.activation(out=ot[o][:, h * 512:(h + 1) * 512], in_=ps[p][:].rearrange("p a b -> p (a b)"), func=AF.Identity), reads=[("ps", p)], writes=[("ot", o, h)])
                    else:
                        P.op("dve", lambda e, p=p, o=o, h=h: e.tensor_copy(out=ot[o][:, h * 512:(h + 1) * 512], in_=ps[p][:].rearrange("p a b -> p (a b)")), reads=[("ps", p)], writes=[("ot", o, h)])
                r0 = t0 - CTX + tb * 128
                P.dma(lambda e, o=o, b=b, r0=r0: e.dma_start(out=k.out[b, r0:r0 + 128, :], in_=ot[o][:]), reads=[("ot", o, 0), ("ot", o, 1)], writes=[("out", b)])
        P.end_stage()


def make_inputs(inputs, core, NB):
    b0 = core * NB
    f = lambda a: np.ascontiguousarray(a, dtype=np.float32)
    w_in = inputs["w_in"]
    cT = np.concatenate([inputs["c"][b0:b0 + NB], inputs["c_ctx"][None]], 0).reshape(NB + 1, 8, 128).transpose(2, 1, 0)
    return {
        "x": f(inputs["x"][b0:b0 + NB]),
        "ctx": f(inputs["ctx"][b0:b0 + NB]),
        "cT": f(cT),
        "ada_w": f(inputs["ada_w"]),
        "ada_b": f(inputs["ada_b"].reshape(2, 72, 128).transpose(0, 2, 1)),
        "norm_g": f(inputs["norm_g"].reshape(2, 3, 8, 128).transpose(0, 1, 3, 2)),
        "final_g": f(inputs["final_g"].reshape(8, 128).T),
        "ffn_w1": f(inputs["ffn_w1"]), "ffn_w3": f(inputs["ffn_w3"]), "ffn_w2": f(inputs["ffn_w2"]),
        "w_fm": f(np.concatenate([w_in[:, :, a:a + w] for a, w in FM_GROUPS], axis=2)),
        "w_tok": f(np.concatenate([w_in[:, :, a:a + w] for a, w in TOK_GROUPS], axis=2)),
        "w_gate": f(w_in[:, :, GATE0:]),
        "w_branch": f(inputs["w_branch"]),
        "w_out": f(inputs["w_out"]),
        "ident": np.eye(128, dtype=np.float32),
        "rope_cos": ROPE[0], "rope_sin": ROPE[1], "rope_rm": ROPE[2],
        "hg_masks": HG_MASKS, "bd64": BD64,
        "s5_lam_re": f(inputs["s5_lam_re"].reshape(2, 2, 8, 2, 64).transpose(0, 3, 4, 1, 2).reshape(2, 128, 16)),
        "s5_lam_im": f(inputs["s5_lam_im"].reshape(2, 2, 8, 2, 64).transpose(0, 3, 4, 1, 2).reshape(2, 128, 16)),
        "s5_log_step": f(np.broadcast_to(inputs["s5_log_step"].reshape(2, 2, 8, 2, 1), (2, 2, 8, 2, 64)).transpose(0, 3, 4, 1, 2).reshape(2, 128, 16)),
        "s5_b_re": f(inputs["s5_b_re"].reshape(2, 8, 2, 64, 16).transpose(0, 2, 3, 1, 4).reshape(2, 128, 8, 16)),
        "s5_b_im": f(inputs["s5_b_im"].reshape(2, 8, 2, 64, 16).transpose(0, 2, 3, 1, 4).reshape(2, 128, 8, 16)),
        "s5_c_re": f(inputs["s5_c_re"].reshape(2, 8, 2, 16, 64).transpose(0, 2, 4, 1, 3).reshape(2, 128, 8, 16)),
        "s5_c_im": f(inputs["s5_c_im"].reshape(2, 8, 2, 16, 64).transpose(0, 2, 4, 1, 3).reshape(2, 128, 8, 16)),
        "s5_d": f(inputs["s5_d"].reshape(2, 2, 128).transpose(0, 2, 1)),
        "s5_glu": f(inputs["s5_glu"]),
        "dn_gm64": DN_GM64, "dn_gmbd": DN_GMBD, "dn_idn2": DN_IDN2,
        "dn_conv": f(inputs["dn_conv"].reshape(2, 5, 6, 128).transpose(0, 3, 2, 1)),
        "dn_a_log": f(np.broadcast_to(inputs["dn_a_log"].reshape(2, 1, 8), (2, 128, 8))),
        "dn_dt_bias": f(np.broadcast_to(inputs["dn_dt_bias"].reshape(2, 1, 8), (2, 128, 8))),
        "dn_norm_g": f(np.broadcast_to(inputs["dn_norm_g"].reshape(2, 1, 64), (2, 128, 64))),
        "hg_lb": f(inputs["hg_lb_logits"].reshape(2, 2, 128).transpose(2, 1, 0)),
        "hg_norm_g": f(np.tile(inputs["hg_norm_g"], (1, 2)).reshape(2, 128, 1)),
        "at_qn_g": f(inputs["at_qn_g"].reshape(2, 64, 1)), "at_kn_g": f(inputs["at_kn_g"].reshape(2, 64, 1)),
    }


def _rope_consts():
    n = np.arange(SEQ)
    r, c = n // 64, n % 64
    inv = (10000.0 ** (-np.arange(0, 32, 2, dtype=np.float32) / 32)).astype(np.float32)
    ang_r = r[:, None].astype(np.float32) * inv
    ang_c = c[:, None].astype(np.float32) * inv
    ang = np.concatenate([ang_r, ang_r, ang_c, ang_c], -1)
    rm = np.zeros((64, 64), np.float32)
    for i in range(16):
        rm[16 + i, i] = -1.0
        rm[i, 16 + i] = 1.0
        rm[48 + i, 32 + i] = -1.0
        rm[32 + i, 48 + i] = 1.0
    return np.ascontiguousarray(np.cos(ang).T.astype(np.float32)), np.ascontiguousarray(np.sin(ang).T.astype(np.float32)), rm


ROPE = _rope_consts()


def _dn_consts():
    a = np.arange(64)
    le = [(a[:, None] <= a[None, :]), (a[:, None] >= a[None, :])]
    st = [(a[None, :] < a[:, None]), (a[None, :] > a[:, None])]
    gm64 = np.zeros((64, 2, 2, 128), np.float32)
    gmbd = np.zeros((128, 2, 2, 128), np.float32)
    for d in range(2):
        gm64[:, d, 0, :] = np.tile(le[d], (1, 2))
        gm64[:, d, 1, :] = np.tile(st[d], (1, 2))
        for h in range(2):
            gmbd[h * 64:(h + 1) * 64, d, 0, h * 64:(h + 1) * 64] = st[d]
            gmbd[h * 64:(h + 1) * 64, d, 1, h * 64:(h + 1) * 64] = le[d]
    idn2 = np.tile(np.eye(64, dtype=np.float32), (2, 1))
    return gm64, gmbd, idn2


DN_GM64, DN_GMBD, DN_IDN2 = _dn_consts()
_s = np.arange(128)[:, None] % 64
_t = np.arange(64)[None, :]
HG_MASKS = np.ascontiguousarray(np.stack([(_s <= _t), (_s >= _t)], 1).astype(np.float32))
BD64 = np.kron(np.eye(2, dtype=np.float32), np.full((64, 64), 1.0 / 64, np.float32))
_CACHE = {}


def kernel(**inputs):
    NB = inputs["x"].shape[0] // N_CORES
    if "nc" not in _CACHE:
        _CACHE["nc"] = build(NB)
    nc = _CACHE["nc"]
    shared = None
    in_maps = []
    for c in range(N_CORES):
        in_maps.append(make_inputs(inputs, c, NB))
    res = run_bass_kernel_spmd(nc, in_maps, core_ids=list(range(N_CORES)))
    return np.concatenate([r["out"] for r in res.results], axis=0).astype(np.float32)
```
